# Optimizing a Trainium2 kernel written in Bass

```python
import jax, jax.numpy as jnp
from jax import lax
import numpy as np

D_MODEL = 1024
BATCH = 16
SEQ = 256
DEPTH = 2
DEC_BATCH = 2
DEC_SEQ = 2048
PAST_LEN = 512

GRID_W = 64
EXPAND = 2
D_INNER = EXPAND * D_MODEL
D_A = D_INNER // 2
DK_A = 128
H_A = D_A // DK_A
DV_A = D_A // H_A
D_B = D_INNER // 2
HS_B = 64
H_B = D_B // HS_B
R_DECAY = 64
R_ICL = 64
D_C = D_INNER
H_C = 4
DH_C = D_C // H_C
CHUNK_A = 32
CHUNK_C = 64
N_EVEN = (DEPTH + 1) // 2
N_ODD = DEPTH // 2
SHIFT_W_B = 3 * D_B + 2 * R_DECAY + 2 * R_ICL
IN_EVEN = 5 * D_A + SHIFT_W_B + D_B
IN_ODD = 5 * D_C + 4 * H_C
EPS = 1e-6
GN_EPS_B = 64e-5
F32 = jnp.float32

kernel_name = "bidir_hgrn2_rwkv7_mlstm_ctx_prefix_step"


def _split(x, sizes):
    return jnp.split(x, np.cumsum(sizes)[:-1].tolist(), axis=-1)


def _rms_norm(x, g):
    xf = x.astype(F32)
    y = xf * lax.rsqrt(jnp.mean(xf * xf, axis=-1, keepdims=True) + EPS)
    return (y * g.astype(F32)).astype(x.dtype)


def _head_rms(y, g):
    b, t = y.shape[:2]
    y = y * lax.rsqrt(jnp.mean(y * y, axis=-1, keepdims=True) + EPS)
    return y.reshape(b, t, -1) * g.astype(F32)


def _head_group_norm(y, g, bias):
    b, t = y.shape[:2]
    yc = y - jnp.mean(y, axis=-1, keepdims=True)
    yn = yc * lax.rsqrt(jnp.mean(yc * yc, axis=-1, keepdims=True) + GN_EPS_B)
    return yn.reshape(b, t, -1) * g.astype(F32) + bias.astype(F32)


def _ada(cond, w, b):
    m = jax.nn.silu(cond) @ w + b
    shift, scale, gate = jnp.split(m, 3, axis=-1)
    return shift[:, None], scale[:, None], gate[:, None]


def _centred_shift(p, mu):
    prev = jnp.pad(p[:, :-1], ((0, 0), (1, 0), (0, 0)))
    nxt = jnp.pad(p[:, 1:], ((0, 0), (0, 1), (0, 0)))
    return p + mu[0] * (prev - p) + mu[1] * (nxt - p)


def _flip(a):
    return jnp.flip(a, axis=1)


def _to_chunks(a, L):
    b, t = a.shape[:2]
    a = a.reshape((b, t // L, L) + a.shape[2:])
    return jnp.moveaxis(jnp.moveaxis(a, 1, 0), 2, 3)


def _from_chunks(a):
    a = jnp.moveaxis(jnp.moveaxis(a, 3, 2), 0, 1)
    return a.reshape((a.shape[0], a.shape[1] * a.shape[2]) + a.shape[3:])


def _hgrn2_chunked(q, k, v, log_f, s0):
    L = CHUNK_A
    causal = jnp.tril(jnp.ones((L, L), dtype=bool))[:, :, None]

    def step(S, blk):
        qc, kc, vc, gc = blk
        b = jnp.cumsum(gc, axis=2)
        o = jnp.einsum("bhtd,bhde->bhte", qc * jnp.exp(b), S)
        rel = jnp.where(causal, b[:, :, :, None, :] - b[:, :, None, :, :], -jnp.inf)
        att = jnp.einsum("bhtd,bhsd,bhtsd->bhts", qc, kc, jnp.exp(rel))
        o = o + jnp.einsum("bhts,bhse->bhte", att, vc)
        b_end = b[:, :, -1:, :]
        S = jnp.exp(b_end[:, :, 0, :, None]) * S + jnp.einsum("bhsd,bhse->bhde", kc * jnp.exp(b_end - b), vc)
        return S, o

    S, o = lax.scan(step, s0, tuple(_to_chunks(a, L) for a in (q, k, v, log_f)))
    return _from_chunks(o), S


def _rwkv7_scan(r, log_w, k, v, kk, a, s0):
    def step(S, inp):
        rt, lwt, kt, vt, kkt, at = inp
        sa = jnp.einsum("bhvk,bhk->bhv", S, kkt)
        S = (S * jnp.exp(lwt)[:, :, None, :] - sa[..., None] * (kkt * at)[:, :, None, :]
             + vt[..., None] * kt[:, :, None, :])
        return S, jnp.einsum("bhvk,bhk->bhv", S, rt)

    S, y = lax.scan(step, s0, tuple(jnp.moveaxis(t, 1, 0) for t in (r, log_w, k, v, kk, a)))
    return jnp.moveaxis(y, 0, 1), S


def _mlstm_chunked(q, k, v, log_i, log_f, C0, n0, m0):
    L = CHUNK_C
    causal = jnp.tril(jnp.ones((L, L), dtype=bool))

    def step(carry, blk):
        C, n, m = carry
        qc, kc, vc, ic, fc = blk
        b = jnp.cumsum(fc, axis=-1)
        log_d = jnp.where(causal, b[..., :, None] - b[..., None, :] + ic[..., None, :], -jnp.inf)
        log_prev = b + m[..., None]
        m_t = jnp.maximum(log_prev, jnp.max(log_d, axis=-1))
        w_prev = jnp.exp(log_prev - m_t)
        s = jnp.einsum("bhtd,bhsd->bhts", qc, kc) * jnp.exp(log_d - m_t[..., None])
        num = w_prev[..., None] * jnp.einsum("bhtd,bhde->bhte", qc, C) + jnp.einsum("bhts,bhse->bhte", s, vc)
        den = w_prev * jnp.einsum("bhtd,bhd->bht", qc, n) + jnp.sum(s, axis=-1)
        h = num / jnp.maximum(jnp.abs(den), jnp.exp(-m_t))[..., None]
        log_s = b[..., -1:] - b + ic
        m_new = jnp.maximum(b[..., -1] + m, jnp.max(log_s, axis=-1))
        w_s = jnp.exp(log_s - m_new[..., None])
        w_old = jnp.exp(b[..., -1] + m - m_new)
        C = w_old[..., None, None] * C + jnp.einsum("bhs,bhsd,bhse->bhde", w_s, kc, vc)
        n = w_old[..., None] * n + jnp.einsum("bhs,bhsd->bhd", w_s, kc)
        return (C, n, m_new), h

    (C, n, m), h = lax.scan(step, (C0, n0, m0), tuple(_to_chunks(a, L) for a in (q, k, v, log_i, log_f)))
    return _from_chunks(h), C, n, m


def _short_conv(x, w, b, rows):
    bsz, t, ch = x.shape
    w = w.astype(x.dtype)
    if rows is None:
        y = lax.conv_general_dilated(x, w[1][:, None, :], (1,), "SAME",
                                     dimension_numbers=("NWC", "WIO", "NWC"), feature_group_count=ch)
    else:
        y = lax.conv_general_dilated(x.reshape(bsz, rows, GRID_W, ch), w[:, :, None, :], (1, 1), "SAME",
                                     dimension_numbers=("NHWC", "HWIO", "NHWC"),
                                     feature_group_count=ch).reshape(bsz, t, ch)
    return y + b.astype(x.dtype)


def _hgrn_rwkv_mixer(h, w_in, w_out, lb, hg_g, mu, w0, w2, a0, a2, k_k, k_a, r_k, gn_g, gn_b, s_hgrn, s_rwkv):
    bsz, t, _ = h.shape
    q_a, i_a, ff_a, fb_a, z_a, sh_in, z_b = _split(h @ w_in, [D_A] * 5 + [SHIFT_W_B, D_B])
    heads = lambda a, nh: a.astype(F32).reshape(bsz, t, nh, -1)
    lb = lb.astype(F32)
    q = heads(q_a, H_A)
    v_a = heads(i_a, H_A)

    def hgrn_gate(f_pre):
        f = lb + (1.0 - lb) * jax.nn.sigmoid(f_pre.astype(F32))
        return heads(jnp.log(f), H_A), heads(1.0 - f, H_A)

    lf_f, k_f = hgrn_gate(ff_a)
    lf_b, k_b = hgrn_gate(fb_a)
    s_hgrn = s_hgrn.astype(F32)
    o_f, sa_f = _hgrn2_chunked(q, k_f, v_a, lf_f, s_hgrn[:, 0])
    o_b, sa_b = _hgrn2_chunked(_flip(q), _flip(k_b), _flip(v_a), _flip(lf_b), s_hgrn[:, 1])
    out_a = _head_rms(o_f + _flip(o_b), hg_g) * jax.nn.silu(z_a.astype(F32))
    sh = _centred_shift(sh_in.astype(F32), mu.astype(F32))
    r, k, v, wl_f, wl_b, al_f, al_b = _split(sh, [D_B] * 3 + [R_DECAY] * 2 + [R_ICL] * 2)
    kk = heads(k * k_k, H_B)
    kk = kk / jnp.maximum(jnp.sqrt(jnp.sum(kk * kk, axis=-1, keepdims=True)), 1e-12)
    r_h = heads(r, H_B)
    v_h = heads(v, H_B)
    s_rwkv = s_rwkv.astype(F32)

    def rwkv_dir(wl, al, d, rev):
        u = w0[d] + jnp.tanh(wl) @ w2[d]
        log_w = heads(-jnp.exp(-jax.nn.softplus(-u) - 0.5), H_B)
        a = jax.nn.sigmoid(a0[d] + al @ a2[d])
        kd = k * (1.0 + (a - 1.0) * k_a)
        a, kd = heads(a, H_B), heads(kd, H_B)
        bonus = jnp.sum(r_h * kd * r_k, axis=-1, keepdims=True) * v_h
        seq = (r_h, log_w, kd, v_h, kk, a)
        if rev:
            seq = tuple(_flip(x) for x in seq)
        y, s = _rwkv7_scan(*seq, s_rwkv[:, d])
        return (_flip(y) if rev else y), bonus, s

    y_f, bo_f, sr_f = rwkv_dir(wl_f, al_f, 0, False)
    y_b, bo_b, sr_b = rwkv_dir(wl_b, al_b, 1, True)
    out_b = (_head_group_norm(y_f + y_b, gn_g, gn_b) + (bo_f + bo_b).reshape(bsz, t, -1)) * jax.nn.silu(z_b.astype(F32))
    out = jnp.concatenate([out_a, out_b], axis=-1).astype(h.dtype) @ w_out
    return out, jnp.stack([sa_f, sa_b], axis=1), jnp.stack([sr_f, sr_b], axis=1)


def _mlstm_mixer(h, w_in, w_out, conv_w, conv_b, gate_b, norm_g, s_c, s_n, s_m, rows):
    bsz, t, _ = h.shape
    qk, v, o, z, gates = _split(h @ w_in, [2 * D_C, D_C, D_C, D_C, 4 * H_C])
    qk = jax.nn.silu(_short_conv(qk, conv_w, conv_b, rows)).astype(F32)
    heads = lambda a: a.astype(F32).reshape(bsz, t, H_C, DH_C)
    q = heads(qk[..., :D_C])
    k = heads(qk[..., D_C:]) * (DH_C ** -0.5)
    v = heads(v)
    g = gates.astype(F32).reshape(bsz, t, 4, H_C) + gate_b.astype(F32)
    s_c, s_n, s_m = s_c.astype(F32), s_n.astype(F32), s_m.astype(F32)

    def run(d, rev):
        seq = (q, k, v, g[:, :, d], jax.nn.log_sigmoid(g[:, :, 2 + d]))
        if rev:
            seq = tuple(_flip(a) for a in seq)
        hd, C, n, m = _mlstm_chunked(*seq, s_c[:, d], s_n[:, d], s_m[:, d])
        return (_flip(hd) if rev else hd), C, n, m

    h_f, c_f, n_f, m_f = run(0, False)
    h_b, c_b, n_b, m_b = run(1, True)
    y = jax.nn.sigmoid(heads(o)) * (h_f + h_b)
    y = _head_rms(y, norm_g) * jax.nn.silu(z.astype(F32))
    return (y.astype(h.dtype) @ w_out, jnp.stack([c_f, c_b], axis=1),
            jnp.stack([n_f, n_b], axis=1), jnp.stack([m_f, m_b], axis=1))


def setup_inputs(seed: int = 0) -> dict:
    key = jax.random.key(seed)
    ks = iter(jax.random.split(key, 48))
    nrm = lambda shape, s: jax.random.normal(next(ks), shape, F32) * s
    inp = {}
    inp["x_prompt"] = nrm((BATCH, SEQ, D_MODEL), 1.0)
    inp["x_sample"] = nrm((DEC_BATCH, DEC_SEQ, D_MODEL), 1.0)
    inp["c"] = nrm((DEC_BATCH, D_MODEL), 1.0)
    inp["state_hgrn"] = nrm((DEC_BATCH, N_EVEN, 2, H_A, DK_A, DV_A), 0.3)
    inp["state_rwkv"] = nrm((DEC_BATCH, N_EVEN, 2, H_B, HS_B, HS_B), 0.1)
    inp["state_mlstm_C"] = nrm((DEC_BATCH, N_ODD, 2, H_C, DH_C, DH_C), 0.05)
    inp["state_mlstm_n"] = nrm((DEC_BATCH, N_ODD, 2, H_C, DH_C), 0.1)
    inp["state_mlstm_m"] = nrm((DEC_BATCH, N_ODD, 2, H_C), 0.5)
    inp["c_ctx"] = nrm((D_MODEL,), 1.0)
    inp["w_mod"] = nrm((DEPTH, D_MODEL, 3 * D_MODEL), D_MODEL ** -0.5)
    inp["b_mod"] = nrm((DEPTH, 3 * D_MODEL), 0.02)
    inp["norm_g"] = 1.0 + nrm((DEPTH, D_MODEL), 0.02)
    inp["final_norm_g"] = 1.0 + nrm((D_MODEL,), 0.02)
    inp["w_in_even"] = nrm((N_EVEN, D_MODEL, IN_EVEN), D_MODEL ** -0.5)
    inp["w_out_even"] = nrm((N_EVEN, D_INNER, D_MODEL), D_INNER ** -0.5)
    inp["hgrn_lb_logits"] = nrm((N_EVEN + 1, D_A), 0.5)
    inp["hgrn_norm_g"] = 1.0 + nrm((N_EVEN, D_A), 0.02)
    inp["rwkv_shift_mu"] = jax.random.uniform(next(ks), (N_EVEN, 2, SHIFT_W_B), F32, 0.0, 0.5)
    inp["rwkv_w0"] = nrm((N_EVEN, 2, D_B), 0.5)
    inp["rwkv_w2"] = nrm((N_EVEN, 2, R_DECAY, D_B), 0.5 * R_DECAY ** -0.5)
    inp["rwkv_a0"] = nrm((N_EVEN, 2, D_B), 0.1)
    inp["rwkv_a2"] = nrm((N_EVEN, 2, R_ICL, D_B), 0.5 * R_ICL ** -0.5)
    inp["rwkv_k_k"] = 0.85 + nrm((N_EVEN, D_B), 0.05)
    inp["rwkv_k_a"] = 1.0 + nrm((N_EVEN, D_B), 0.05)
    inp["rwkv_r_k"] = nrm((N_EVEN, H_B, HS_B), 0.1)
    inp["rwkv_gn_g"] = 1.0 + nrm((N_EVEN, D_B), 0.02)
    inp["rwkv_gn_b"] = nrm((N_EVEN, D_B), 0.02)
    inp["w_in_odd"] = nrm((N_ODD, D_MODEL, IN_ODD), D_MODEL ** -0.5)
    inp["w_out_odd"] = nrm((N_ODD, D_C, D_MODEL), D_C ** -0.5)
    inp["mlstm_conv_w"] = nrm((N_ODD, 3, 3, 2 * D_C), 1.0 / 3.0)
    inp["mlstm_conv_b"] = nrm((N_ODD, 2 * D_C), 0.02)
    inp["mlstm_gate_b"] = nrm((N_ODD, 4, H_C), 0.1) + jnp.array([0.0, 0.0, 3.0, 3.0], F32)[None, :, None]
    inp["mlstm_norm_g"] = 1.0 + nrm((N_ODD, D_C), 0.02)
    return inp


def reference(x_prompt, x_sample, c, state_hgrn, state_rwkv, state_mlstm_C, state_mlstm_n, state_mlstm_m,
              c_ctx, w_mod, b_mod, norm_g, final_norm_g, w_in_even, w_out_even, hgrn_lb_logits, hgrn_norm_g,
              rwkv_shift_mu, rwkv_w0, rwkv_w2, rwkv_a0, rwkv_a2, rwkv_k_k, rwkv_k_a, rwkv_r_k, rwkv_gn_g,
              rwkv_gn_b, w_in_odd, w_out_odd, mlstm_conv_w, mlstm_conv_b, mlstm_gate_b, mlstm_norm_g):
    rows = x_sample.shape[1] // GRID_W
    n_p = x_prompt.shape[0]
    lb_all = jnp.cumsum(jax.nn.softmax(hgrn_lb_logits.astype(F32), axis=0), axis=0)
    xp, xs = x_prompt, x_sample
    new_hgrn, new_rwkv, new_c, new_n, new_m = [], [], [], [], []
    for layer in range(DEPTH):
        j = layer // 2
        sh_p, sc_p, g_p = _ada(c_ctx[None], w_mod[layer], b_mod[layer])
        sh_s, sc_s, g_s = _ada(c, w_mod[layer], b_mod[layer])
        h_p = _rms_norm(xp, norm_g[layer]) * (1.0 + sc_p) + sh_p
        h_s = _rms_norm(xs, norm_g[layer]) * (1.0 + sc_s) + sh_s
        if layer % 2 == 0:
            p = (w_in_even[j], w_out_even[j], lb_all[j], hgrn_norm_g[j], rwkv_shift_mu[j], rwkv_w0[j], rwkv_w2[j],
                 rwkv_a0[j], rwkv_a2[j], rwkv_k_k[j], rwkv_k_a[j], rwkv_r_k[j], rwkv_gn_g[j], rwkv_gn_b[j])
            o_p, s_h, s_r = _hgrn_rwkv_mixer(h_p, *p, jnp.zeros((n_p, 2, H_A, DK_A, DV_A), F32),
                                             jnp.zeros((n_p, 2, H_B, HS_B, HS_B), F32))
            o_s, _, _ = _hgrn_rwkv_mixer(h_s, *p, state_hgrn[:, j], state_rwkv[:, j])
            new_hgrn.append(s_h)
            new_rwkv.append(s_r)
        else:
            p = (w_in_odd[j], w_out_odd[j], mlstm_conv_w[j], mlstm_conv_b[j], mlstm_gate_b[j], mlstm_norm_g[j])
            o_p, s_c, s_n, s_m = _mlstm_mixer(h_p, *p, jnp.zeros((n_p, 2, H_C, DH_C, DH_C), F32),
                                              jnp.zeros((n_p, 2, H_C, DH_C), F32),
                                              jnp.zeros((n_p, 2, H_C), F32), None)
            o_s, _, _, _ = _mlstm_mixer(h_s, *p, state_mlstm_C[:, j], state_mlstm_n[:, j], state_mlstm_m[:, j], rows)
            new_c.append(s_c)
            new_n.append(s_n)
            new_m.append(s_m)
        xp = xp + g_p * o_p
        xs = xs + g_s * o_s
    y_prompt = _rms_norm(xp, final_norm_g)
    y_sample = _rms_norm(xs, final_norm_g)
    dt = x_prompt.dtype
    new_hgrn = jnp.stack(new_hgrn, axis=1).astype(dt)
    new_rwkv = jnp.stack(new_rwkv, axis=1).astype(dt)
    new_mlstm_C = jnp.stack(new_c, axis=1).astype(dt)
    new_mlstm_n = jnp.stack(new_n, axis=1).astype(dt)
    new_mlstm_m = jnp.stack(new_m, axis=1).astype(dt)
    return (y_prompt, y_sample, new_hgrn, new_rwkv, new_mlstm_C, new_mlstm_n, new_mlstm_m)
```

```python
import contextlib
import numpy as np
import concourse.bass as bass
import concourse.mybir as mybir
from concourse.bass_utils import run_bass_kernel_spmd

F32 = mybir.dt.float32
BF16 = mybir.dt.bfloat16
ALU = mybir.AluOpType
AF = mybir.ActivationFunctionType
AX = mybir.AxisListType

D = 1024
TS = 2048
TP = 256
TT = TS + 2 * TP
NCORES = 8


class _Rec:
    def __init__(self):
        self.calls = []

    def __getattr__(self, name):
        def f(*a, **k):
            self.calls.append((name, a, k))
            return self
        return f


class Prog:
    ENGS = ['pe', 'dve', 'act', 'pool', 'sp']
    NDMA = 16

    def __init__(self, nc):
        self.nc = nc
        self.ops = {e: [] for e in self.ENGS}
        self.cnt = {}
        self.waited = {e: {} for e in self.ENGS}
        self.last_write = {}
        self.readers = {}
        self.dma_rr = 0
        self.sem_names = list(self.ENGS) + ['d%d' % i for i in range(self.NDMA)]
        for s in self.sem_names:
            self.cnt[s] = 0
        self.n_ops = 0

    def _deps(self, eng, reads, writes):
        deps = {}

        def add(p):
            if p is None:
                return
            f, n = p
            if f == 'pe' and eng == 'pe':
                return
            if n > deps.get(f, 0):
                deps[f] = n
        for k in reads:
            add(self.last_write.get(k))
        for k in writes:
            add(self.last_write.get(k))
            for p in self.readers.get(k, ()):
                add(p)
        waits = []
        for f, n in deps.items():
            if n > self.waited[eng].get(f, 0):
                waits.append((f, n))
                self.waited[eng][f] = n
        return waits

    def _commit(self, tag, reads, writes):
        for k in reads:
            lst = self.readers.setdefault(k, [])
            lst[:] = [p for p in lst if p[0] != tag[0]]
            lst.append(tag)
        for k in writes:
            self.last_write[k] = tag
            self.readers[k] = []

    def op(self, eng, fn, reads=(), writes=()):
        rec = _Rec()
        fn(rec)
        name, a, k = rec.calls[0]
        fn = (lambda e, name=name, a=a, k=k: getattr(e, name)(*a, **k))
        waits = self._deps(eng, reads, writes)
        self.cnt[eng] += 1
        tag = (eng, self.cnt[eng])
        self.ops[eng].append((waits, fn, eng, 1))
        self._commit(tag, reads, writes)
        self.n_ops += 1

    def dma(self, out, in_, reads=(), writes=(), q='sp', **kw):
        d = 'd%d' % self.dma_rr
        self.dma_rr = (self.dma_rr + 1) % self.NDMA
        waits = self._deps(q, reads, writes)
        prev = self.cnt[d]
        if prev > self.waited[q].get(d, 0):
            waits.append((d, prev))
            self.waited[q][d] = prev
        self.cnt[d] += 16
        tag = (d, self.cnt[d])
        self.ops[q].append((waits, (lambda e: e.dma_start(out=out, in_=in_, **kw)), d, 16))
        self._commit(tag, reads, writes)
        self.n_ops += 1

    def barrier(self):
        allsems = list(self.sem_names)
        for e in self.ENGS:
            waits = []
            for f in allsems:
                if self.cnt[f] > self.waited[e].get(f, 0):
                    waits.append((f, self.cnt[f]))
                    self.waited[e][f] = self.cnt[f]
            self.ops[e].append((waits, None, None, 0))

    def finish(self, q='sp'):
        waits = []
        for i in range(self.NDMA):
            d = 'd%d' % i
            if self.cnt[d] > self.waited[q].get(d, 0):
                waits.append((d, self.cnt[d]))
                self.waited[q][d] = self.cnt[d]
        self.ops[q].append((waits, None, None, 0))

    def emit(self, sems):
        ops = self.ops

        def run(e, lst):
            for waits, fn, semname, inc in lst:
                for f, n in waits:
                    e.wait_ge(sems[f], n)
                if fn is not None:
                    fn(e).then_inc(sems[semname], inc)
        with self.nc.Block() as block:
            @block.tensor
            def _(e):
                run(e, ops['pe'])

            @block.vector
            def _(e):
                run(e, ops['dve'])

            @block.scalar
            def _(e):
                run(e, ops['act'])

            @block.gpsimd
            def _(e):
                run(e, ops['pool'])

            @block.sync
            def _(e):
                run(e, ops['sp'])


SEQS = [(0, TS, 0, True), (TS, TP, 1, False), (TS + TP, TP, 1, False)]


def build(debug=None):
    nc = bass.Bass('TRN2', target_bir_lowering=False)
    P = Prog(nc)
    es = contextlib.ExitStack()

    def din(name, shape):
        return nc.dram_tensor(name, list(shape), F32, kind="ExternalInput").ap()

    def dout(name, shape):
        return nc.dram_tensor(name, list(shape), F32, kind="ExternalOutput").ap()

    xin = din("xin", [TT, D])
    condT = din("condT", [128, 8, 2])
    s_hgrn = din("s_hgrn", [2, 8, 128, 128])
    s_rwkv = din("s_rwkv", [2, 16, 64, 64])
    s_C = din("s_C", [2, 4, 512, 512])
    s_n = din("s_n", [2, 4, 512])
    s_m = din("s_m", [2, 4])
    w_mod = din("w_mod", [2, D, 3 * D])
    b_modT = din("b_modT", [128, 2, 24])
    norm_gT = din("norm_gT", [128, 2, 8])
    fnorm_gT = din("fnorm_gT", [128, 8])
    wA = din("wA", [8, D, 640])
    wB = din("wB", [16, D, 256])
    wLR = din("wLR", [D, 256])
    w_out_even = din("w_out_even", [2 * D, D])
    lbT = din("lbT", [128, 2, 8])
    hg_gT = din("hg_gT", [128, 8])
    mu_rkv = din("mu_rkv", [64, 2, 4, 16])
    mu_lr = din("mu_lr", [64, 2, 4])
    w0T = din("w0T", [64, 2, 16])
    a0T = din("a0T", [64, 2, 16])
    w2 = din("w2", [2, 64, D])
    a2 = din("a2", [2, 64, D])
    kkT = din("kkT", [64, 16])
    kaT = din("kaT", [64, 16])
    rkT = din("rkT", [64, 16])
    gngT = din("gngT", [64, 16])
    gnbT = din("gnbT", [64, 16])
    maskR = din("maskR", [2, 3, 64, 128])
    maskH = din("maskH", [2, 128, 128])
    ident_d = din("ident_in", [128, 128])
    w_in_odd = din("w_in_odd", [D, 10256])
    w_out_odd = din("w_out_odd", [2 * D, D])
    maskC = din("maskC", [2, 128, 128])
    sel_d = din("sel_d", [36, 4, 128])
    gbT_d = din("gbT_d", [36, 4])
    cw_d = din("cw_d", [128, 32, 9])
    cb_d = din("cb_d", [128, 32])
    mng_d = din("mng_d", [128, 16])

    yout = dout("yout", [TT, D])
    o_hgrn = dout("o_hgrn", [2, 2, 8, 128, 128])
    o_rwkv = dout("o_rwkv", [2, 2, 16, 64, 64])
    o_C = dout("o_C", [2, 2, 4, 512, 512])
    o_n = dout("o_n", [2, 2, 4, 512])
    o_m = dout("o_m", [2, 2, 4])
    dbg = dout("dbg", [40, 128, TT]) if debug else None
    dslot = {}
    dumpt = {}
    x1 = dout("x1", [TT, D]) if debug else nc.dram_tensor("x1", [TT, D], F32, kind="Internal").ap()

    def sb(name, shape, dt=F32):
        return es.enter_context(nc.sbuf_tensor(name, list(shape), dt))

    pstiles = [es.enter_context(nc.psum_tensor("ps%d" % i, [128, 512], F32)) for i in range(8)]
    psrr = [0]

    def nps():
        i = psrr[0]
        psrr[0] = (i + 1) % 8
        return pstiles[i], 'ps%d' % i

    def dump(name, ap, keys, n, col0=0, parts=128):
        if not debug:
            return
        slot = dslot.setdefault(name, len(dslot))
        dt_ = dumpt['tile']
        for c0 in range(0, n, 512):
            w_ = min(512, n - c0)
            P.op('pool', (lambda e, c0=c0, w_=w_: e.tensor_copy(out=dt_[0:parts, 0:w_], in_=ap[:, c0:c0 + w_])),
                 reads=keys, writes=['dumpt'])
            P.dma(dbg[slot, 0:parts, col0 + c0:col0 + c0 + w_], dt_[0:parts, 0:w_], reads=['dumpt'])

    if debug:
        dumpt['tile'] = sb("dumpt", [128, 512])

    ident = sb("ident", [128, 128])
    ones = sb("ones", [128, 128])
    P.dma(ident[:], ident_d[:], writes=['ident'])
    P.op('dve', lambda e: e.memset(ones[:], 1.0), writes=['ones'])

    condT_sb = sb("condT_sb", [128, 8, 2])
    bmod_sb = sb("bmod_sb", [128, 2, 24])
    ng_sb = sb("ng_sb", [128, 2, 8])
    fng_sb = sb("fng_sb", [128, 8])
    lb_sb = sb("lb_sb", [128, 2, 8])
    hgg_sb = sb("hgg_sb", [128, 8])
    for t_, d_, k_ in [(condT_sb, condT, 'condT'), (bmod_sb, b_modT, 'bmod'), (ng_sb, norm_gT, 'ng'),
                       (fng_sb, fnorm_gT, 'fng'), (lb_sb, lbT, 'lb'), (hgg_sb, hg_gT, 'hgg')]:
        P.dma(t_[:], d_[:], writes=[k_])

    scT = sb("scT", [128, 8, 2])
    P.op('act', lambda e: e.activation(out=scT[:], in_=condT_sb[:], func=AF.Silu), reads=['condT'], writes=['scT'])
    mT = sb("mT", [128, 2, 24, 2])
    sc1 = sb("sc1", [128, 2, 8, 2])
    gate_bc = sb("gate_bc", [128, 2, D])
    dg = sb("dg", [128, 128])

    def make_gate(l):
        for c in range(2):
            for half in range(2):
                pz, pk = nps()
                for kq in range(4):
                    kc = half * 4 + kq
                    P.op('dve', lambda e: e.tensor_scalar(
                        out=dg[:], in0=ident[:], scalar1=mT[:, l, 16 + kc, c:c + 1], scalar2=None, op0=ALU.mult),
                        reads=['ident', 'mT'], writes=['dg'])
                    P.op('pe', lambda e: e.matmul(pz[:, kq * 128:(kq + 1) * 128], ones[:], dg[:], start=True, stop=True),
                         reads=['ones', 'dg'], writes=[pk])
                P.op('act', lambda e: e.copy(out=gate_bc[:, c, half * 512:(half + 1) * 512], in_=pz[:]),
                     reads=[pk], writes=['gate_bc'])

    with contextlib.ExitStack() as es2:
        wm = [es2.enter_context(nc.sbuf_tensor("wm%d" % i, [128, 8, 512], F32)) for i in range(2)]
        for l in range(2):
            for cbk in range(6):
                i = (l * 6 + cbk) % 2
                P.dma(wm[i][:], w_mod[l].rearrange("(kc p) n -> p kc n", p=128)[:, :, cbk * 512:(cbk + 1) * 512],
                      writes=['wm%d' % i])
                pz, pk = nps()
                for j in range(4):
                    for kc in range(8):
                        P.op('pe', lambda e: e.matmul(pz[:, j * 2:(j + 1) * 2], wm[i][:, kc, j * 128:(j + 1) * 128],
                                                      scT[:, kc, :], start=(kc == 0), stop=(kc == 7)),
                             reads=['wm%d' % i, 'scT'], writes=[pk])
                P.op('dve', lambda e: e.tensor_tensor(out=mT[:, l, cbk * 4:(cbk + 1) * 4, :],
                                                      in0=pz[:, 0:8].rearrange("p (j c) -> p j c", c=2),
                                                      in1=bmod_sb[:, l, cbk * 4:(cbk + 1) * 4].unsqueeze(2).to_broadcast([128, 4, 2]),
                                                      op=ALU.add),
                     reads=[pk, 'bmod'], writes=['mT'])
    P.barrier()
    for l in range(2):
        P.op('dve', lambda e: e.scalar_tensor_tensor(
            out=sc1[:, l], in0=mT[:, l, 8:16, :], scalar=1.0,
            in1=ng_sb[:, l, :].unsqueeze(2).to_broadcast([128, 8, 2]), op0=ALU.add, op1=ALU.mult),
            reads=['mT', 'ng'], writes=['sc1'])
    fng_holder = {}

    def make_fng():
        fng_bc = sb("fng_bc", [128, D])
        fng_holder['t'] = fng_bc
        for half in range(2):
            pz, pk = nps()
            for kq in range(4):
                kc = half * 4 + kq
                P.op('dve', (lambda e, kc=kc: e.tensor_scalar(
                    out=dg[:], in0=ident[:], scalar1=fng_sb[:, kc:kc + 1], scalar2=None, op0=ALU.mult)),
                    reads=['ident', 'fng'], writes=['dg'])
                P.op('pe', (lambda e, pz=pz, kq=kq: e.matmul(pz[:, kq * 128:(kq + 1) * 128], ones[:], dg[:],
                                                             start=True, stop=True)),
                     reads=['ones', 'dg'], writes=[pk])
            P.op('act', (lambda e, half=half, pz=pz: e.copy(out=fng_bc[:, half * 512:(half + 1) * 512], in_=pz[:])),
                 reads=[pk], writes=['fng_bc'])

    lbv = sb("lbv", [128, 8])
    oml = sb("oml", [128, 8])
    P.op('dve', lambda e: e.tensor_tensor(out=lbv[:], in0=lb_sb[:, 0, :], in1=lb_sb[:, 1, :], op=ALU.subtract),
         reads=['lb'], writes=['lbv'])
    P.op('act', lambda e: e.activation(out=lbv[:], in_=lbv[:], func=AF.Sigmoid), reads=['lbv'], writes=['lbv'])
    P.op('act', lambda e: e.activation(out=oml[:], in_=lbv[:], func=AF.Identity, bias=1.0, scale=-1.0),
         reads=['lbv'], writes=['oml'])

    hT = sb("hT", [128, 8, TS], BF16)
    yT = sb("yT", [128, 8, TS], BF16)
    st4 = sb("st4", [128, 4])
    wst = sb("wst", [128, 8, 384])
    FT = [sb("FT%d" % i, [128, TS + 32]) for i in range(6)]
    xt = [FT[0][:, 0:D], FT[0][:, D:2 * D]]
    xn = FT[1][:, 0:D]
    junk = FT[1][:, D:2 * D]
    wo_v = [FT[2][:, 0:TS].bitcast(BF16).rearrange("p (s n) -> p s n", n=D),
            FT[3][:, 0:TS].bitcast(BF16).rearrange("p (s n) -> p s n", n=D)]

    def load_wo(src, parts, nk=8):
        v = src.rearrange("(kc p) n -> p kc n", p=parts)
        for c0 in range(0, D, 384):
            w_ = min(384, D - c0)
            P.dma(wst[0:parts, 0:nk, 0:w_], v[:, :, c0:c0 + w_], writes=['wst'])
            for hf in range(nk // 4):
                P.op('pool', lambda e: e.tensor_copy(out=wo_v[hf][0:parts, :, c0:c0 + w_], in_=wst[0:parts, hf * 4:hf * 4 + 4, 0:w_]),
                     reads=['wst'], writes=['wo_bf'])

    def make_hT(layer, xsrc, xkey, off, T, cidx):
        for tt in range(T // 128):
            i = tt % 2
            P.dma(xt[i], xsrc[off + tt * 128: off + (tt + 1) * 128, :], reads=[(xkey, off // 128 + tt)], writes=['xt%d' % i])
            P.op('act', lambda e: e.activation(out=junk, in_=xt[i], func=AF.Square, accum_out=st4[:, 0:1]),
                 reads=['xt%d' % i], writes=['junk', 'st4'])
            P.op('dve', lambda e: e.tensor_scalar(out=st4[:, 1:2], in0=st4[:, 0:1], scalar1=1.0 / D, scalar2=1e-6,
                                                  op0=ALU.mult, op1=ALU.add), reads=['st4'], writes=['st4'])
            P.op('act', lambda e: e.activation(out=st4[:, 2:3], in_=st4[:, 1:2], func=AF.Sqrt), reads=['st4'], writes=['st4'])
            P.op('dve', lambda e: e.reciprocal(out=st4[:, 3:4], in_=st4[:, 2:3]), reads=['st4'], writes=['st4'])
            P.op('dve', lambda e: e.tensor_scalar(out=xn, in0=xt[i], scalar1=st4[:, 3:4], scalar2=None, op0=ALU.mult),
                 reads=['xt%d' % i, 'st4'], writes=['xn'])
            for half in range(2):
                pz, pk = nps()
                for kq in range(4):
                    kc = half * 4 + kq
                    P.op('pe', lambda e: e.transpose(out=pz[:, kq * 128:(kq + 1) * 128], in_=xn[:, kc * 128:(kc + 1) * 128],
                                                     identity=ident[:]), reads=['xn', 'ident'], writes=[pk])
                for kq in range(4):
                    kc = half * 4 + kq
                    P.op('act', lambda e: e.activation(
                        out=hT[:, kc, tt * 128:(tt + 1) * 128], in_=pz[:, kq * 128:(kq + 1) * 128], func=AF.Identity,
                        bias=mT[:, layer, kc, cidx:cidx + 1], scale=sc1[:, layer, kc, cidx:cidx + 1]),
                        reads=[pk, 'mT', 'sc1'], writes=[('hT', tt)])

    def hT_keys(t0, t1):
        return [('hT', tt) for tt in range(t0 // 128, (t1 + 127) // 128)]

    def load_w(src, ncols, dst=None, dkey='wbf', parts=128, nk=8):
        dst = wbf if dst is None else dst
        v = src.rearrange("(kc p) n -> p kc n", p=parts)
        for c0 in range(0, ncols, 384):
            w_ = min(384, ncols - c0)
            P.dma(wst[0:parts, 0:nk, 0:w_], v[:, :, c0:c0 + w_], writes=['wst'])
            P.op('pool', lambda e: e.tensor_copy(out=dst[0:parts, 0:nk, c0:c0 + w_], in_=wst[0:parts, 0:nk, 0:w_]),
                 reads=['wst'], writes=[dkey])

    def proj(c0, M, t0, n, evac):
        pz, pk = nps()
        for kc in range(8):
            P.op('pe', lambda e: e.matmul(pz[0:M, 0:n], wbf[:, kc, c0:c0 + M], hT[:, kc, t0:t0 + n],
                                          start=(kc == 0), stop=(kc == 7)),
                 reads=['wbf'] + hT_keys(t0, t0 + n), writes=[pk])
        evac(pz, pk)

    def outproj(layer, groups, xsrc, skey, xdst, dkey, off, T, cidx, final=False):
        for tt in range(T // 128):
            i = tt % 2
            P.dma(xt[i], xsrc[off + tt * 128: off + (tt + 1) * 128, :], reads=[(skey, off // 128 + tt)], writes=['xt%d' % i])
            for half in range(2):
                pz, pk = nps()
                for gi, (K, slot) in enumerate(groups):
                    P.op('pe', lambda e: e.matmul(pz[:, 0:512], yT[0:K, slot, tt * 128:(tt + 1) * 128],
                                                  wo_v[slot // 4][0:K, slot % 4, half * 512:(half + 1) * 512],
                                                  start=(gi == 0), stop=(gi == len(groups) - 1)),
                         reads=[('yT', slot), 'wo_bf'], writes=[pk])
                P.op('dve', lambda e: e.tensor_tensor(out=xn[:, half * 512:(half + 1) * 512], in0=pz[:, 0:512],
                                                      in1=gate_bc[:, cidx, half * 512:(half + 1) * 512], op=ALU.mult),
                     reads=[pk, 'gate_bc'], writes=['xn'])
                P.op('pool', lambda e: e.tensor_tensor(out=xt[i][:, half * 512:(half + 1) * 512],
                                                       in0=xt[i][:, half * 512:(half + 1) * 512],
                                                       in1=xn[:, half * 512:(half + 1) * 512], op=ALU.add),
                     reads=['xn', 'xt%d' % i], writes=['xt%d' % i])
            if final:
                P.op('act', lambda e: e.activation(out=junk, in_=xt[i], func=AF.Square, accum_out=st4[:, 0:1]),
                     reads=['xt%d' % i], writes=['junk', 'st4'])
                P.op('dve', lambda e: e.tensor_scalar(out=st4[:, 1:2], in0=st4[:, 0:1], scalar1=1.0 / D, scalar2=1e-6,
                                                      op0=ALU.mult, op1=ALU.add), reads=['st4'], writes=['st4'])
                P.op('act', lambda e: e.activation(out=st4[:, 2:3], in_=st4[:, 1:2], func=AF.Sqrt), reads=['st4'], writes=['st4'])
                P.op('dve', lambda e: e.reciprocal(out=st4[:, 3:4], in_=st4[:, 2:3]), reads=['st4'], writes=['st4'])
                P.op('dve', lambda e: e.scalar_tensor_tensor(out=xt[i], in0=xt[i], scalar=st4[:, 3:4], in1=fng_holder['t'][:],
                                                             op0=ALU.mult, op1=ALU.mult),
                     reads=['xt%d' % i, 'st4', 'fng_bc'], writes=['xt%d' % i])
            P.dma(xdst[off + tt * 128: off + (tt + 1) * 128, :], xt[i], reads=['xt%d' % i], writes=[(dkey, off // 128 + tt)], q='pool')

    L0 = contextlib.ExitStack()

    def sb0(name, shape, dt=F32):
        return L0.enter_context(nc.sbuf_tensor(name, list(shape), dt))

    wbf = sb0("wbf", [128, 8, 384], BF16)
    TB = 256
    Fq, Fsz, Fvr, For = FT[0][:, 0:TS], FT[1][:, 0:TS], FT[2][:, 0:TS], FT[3][:, 0:TS]
    Fv = Fvr.rearrange("p (j c) -> p j c", c=128)
    Fo = For.rearrange("p (j c) -> p j c", c=128)
    BT = [sb0("BT%d" % i, [128, 256]) for i in range(18)]
    bt = {n_: BT[i] for i, n_ in enumerate(['sg', 'lf', 'kg', 'G', 'br', 'E', 'Ei', 'qt', 'kt', 'kh', 'vT'])}
    khtok = sb0("khtok", [128, TB // 128, 128])
    gam = sb0("gam", [128, TB // 32])
    gref = sb0("gref", [128, TB // 32])
    Sst = [sb0("Sst%d" % i, [128, 128]) for i in range(2)]
    attT = sb0("attT", [128, 128])
    ostat = sb0("ostat", [128, TS // 128, 4])
    mH = sb0("mH", [128, 2, 128])
    P.dma(mH[:], maskH.rearrange("d s t -> s d t"), writes=['mH'])

    def hgrn_head(h, off, T, is_sample, pidx):
        load_w(wA[h][:, 0:384], 384)
        tb = min(TB, T)
        nblk = T // tb
        for b in range(nblk):
            t0 = b * tb
            proj(0, 128, t0, tb, lambda pz, pk: P.op(
                'act', lambda e: e.copy(out=Fq[:, t0:t0 + tb], in_=pz[:, 0:tb]), reads=[pk], writes=['Fq']))
            proj(256, 128, t0, tb, lambda pz, pk: P.op(
                'act', lambda e: e.activation(out=Fsz[:, t0:t0 + tb], in_=pz[:, 0:tb], func=AF.Silu), reads=[pk], writes=['Fsz']))
            proj(128, 128, t0, tb, lambda pz, pk: P.op(
                'dve', lambda e: e.tensor_copy(out=bt['vT'][:, 0:tb], in_=pz[:, 0:tb]), reads=[pk], writes=['b_vT']))
            pz, pk = nps()
            for j in range(tb // 128):
                P.op('pe', lambda e: e.transpose(out=pz[:, j * 128:(j + 1) * 128], in_=bt['vT'][:, j * 128:(j + 1) * 128],
                                                 identity=ident[:]), reads=['b_vT', 'ident'], writes=[pk])
            P.op('dve', lambda e: e.tensor_copy(out=Fv[:, t0 // 128:(t0 + tb) // 128, :],
                                                in_=pz[:, 0:tb].rearrange("p (j c) -> p j c", c=128)),
                 reads=[pk], writes=['Fv'])
        load_w(wA[h][:, 384:640], 256)
        for d in range(2):
            rev = (d == 1)
            cur = 0
            if is_sample:
                P.dma(Sst[0][:], s_hgrn[d, h], writes=['Sst0'])
            else:
                P.op('pool', lambda e: e.memset(Sst[0][:], 0.0), writes=['Sst0'])
            blks = list(range(nblk))
            if rev:
                blks = blks[::-1]
            for b in blks:
                t0 = b * tb
                nch = tb // 32
                sg, lf, kg, G, br, E, Ei, qt, kt, kh = [bt[n_] for n_ in ['sg', 'lf', 'kg', 'G', 'br', 'E', 'Ei', 'qt', 'kt', 'kh']]
                proj(128 * d, 128, t0, tb, lambda pz, pk: P.op(
                    'act', lambda e: e.activation(out=sg[:, 0:tb], in_=pz[:, 0:tb], func=AF.Sigmoid), reads=[pk], writes=['b_sg']))
                P.op('dve', lambda e: e.tensor_scalar(out=sg[:, 0:tb], in0=sg[:, 0:tb], scalar1=oml[:, h:h + 1],
                                                      scalar2=lbv[:, h:h + 1], op0=ALU.mult, op1=ALU.add),
                     reads=['b_sg', 'oml', 'lbv'], writes=['b_sg'])
                P.op('act', lambda e: e.activation(out=lf[:, 0:tb], in_=sg[:, 0:tb], func=AF.Ln), reads=['b_sg'], writes=['b_lf'])
                P.op('pool', lambda e: e.tensor_scalar(out=kg[:, 0:tb], in0=sg[:, 0:tb], scalar1=-1.0, scalar2=1.0,
                                                       op0=ALU.mult, op1=ALU.add), reads=['b_sg'], writes=['b_kg'])
                P.op('dve', lambda e: e.memset(E[:, 0:tb], 0.0), writes=['b_E'])
                if not rev:
                    P.op('dve', lambda e: e.tensor_tensor_scan(out=G[:, 0:tb], data0=lf[:, 0:tb], data1=E[:, 0:tb],
                                                               initial=0.0, op0=ALU.add, op1=ALU.add),
                         reads=['b_lf', 'b_E'], writes=['b_G'])
                    ci_ = 0
                else:
                    P.op('dve', lambda e: e.tensor_tensor_scan(out=G[:, 0:tb][:, ::-1], data0=lf[:, 0:tb][:, ::-1],
                                                               data1=E[:, 0:tb], initial=0.0, op0=ALU.add, op1=ALU.add),
                         reads=['b_lf', 'b_E'], writes=['b_G'])
                    ci_ = 31
                G3 = G[:, 0:tb].rearrange("p (c l) -> p c l", l=32)
                lf3 = lf[:, 0:tb].rearrange("p (c l) -> p c l", l=32)
                P.op('dve', lambda e: e.tensor_tensor(out=gref[:, 0:nch], in0=G3[:, :, ci_], in1=lf3[:, :, ci_], op=ALU.subtract),
                     reads=['b_G', 'b_lf'], writes=['gref'])
                P.op('dve', lambda e: e.tensor_tensor(out=br[:, 0:tb].rearrange("p (c l) -> p c l", l=32), in0=G3,
                                                      in1=gref[:, 0:nch].unsqueeze(2).to_broadcast([128, nch, 32]), op=ALU.subtract),
                     reads=['b_G', 'gref'], writes=['b_br'])
                bend = br[:, 0:tb].rearrange("p (c l) -> p c l", l=32)[:, :, (0 if rev else 31)]
                P.op('act', lambda e: e.activation(out=gam[:, 0:nch], in_=bend, func=AF.Exp), reads=['b_br'], writes=['gam'])
                P.op('act', lambda e: e.activation(out=E[:, 0:tb], in_=br[:, 0:tb], func=AF.Exp), reads=['b_br'], writes=['b_E'])
                P.op('act', lambda e: e.activation(out=Ei[:, 0:tb], in_=br[:, 0:tb], func=AF.Exp, scale=-1.0),
                     reads=['b_br'], writes=['b_Ei'])
                P.op('dve', lambda e: e.tensor_tensor(out=qt[:, 0:tb], in0=Fq[:, t0:t0 + tb], in1=E[:, 0:tb], op=ALU.mult),
                     reads=['Fq', 'b_E'], writes=['b_qt'])
                P.op('pool', lambda e: e.tensor_tensor(out=kt[:, 0:tb], in0=kg[:, 0:tb], in1=Ei[:, 0:tb], op=ALU.mult),
                     reads=['b_kg', 'b_Ei'], writes=['b_kt'])
                P.op('dve', lambda e: e.tensor_tensor(out=kh[:, 0:tb].rearrange("p (c l) -> p c l", l=32),
                                                      in0=kt[:, 0:tb].rearrange("p (c l) -> p c l", l=32),
                                                      in1=gam[:, 0:nch].unsqueeze(2).to_broadcast([128, nch, 32]), op=ALU.mult),
                     reads=['b_kt', 'gam'], writes=['b_kh'])
                if debug and debug.get('inner') and h == head_ids[0]:
                    for n_ in ['lf', 'kg', 'br', 'E', 'qt', 'kt', 'kh']:
                        dump("%s_d%d" % (n_, d), bt[n_][:, 0:tb], ['b_' + n_], tb, col0=off + t0)
                pz, pk = nps()
                for j in range(tb // 128):
                    P.op('pe', lambda e: e.transpose(out=pz[:, j * 128:(j + 1) * 128], in_=kh[:, j * 128:(j + 1) * 128],
                                                     identity=ident[:]), reads=['b_kh', 'ident'], writes=[pk])
                P.op('act', lambda e: e.copy(out=khtok[:, 0:tb // 128, :], in_=pz[:, 0:tb].rearrange("p (j c) -> p j c", c=128)),
                     reads=[pk], writes=['khtok'])
                tiles = list(range(tb // 128))
                if rev:
                    tiles = tiles[::-1]
                for j in tiles:
                    tg = t0 // 128 + j
                    pa, pak = nps()
                    P.op('pe', lambda e: e.matmul(pa[:, 0:128], kt[:, j * 128:(j + 1) * 128], qt[:, j * 128:(j + 1) * 128],
                                                  start=True, stop=True), reads=['b_kt', 'b_qt'], writes=[pak])
                    P.op('dve', lambda e: e.tensor_tensor(out=attT[:], in0=pa[:, 0:128], in1=mH[:, d, :], op=ALU.mult),
                         reads=[pak, 'mH'], writes=['attT'])
                    po, pok = nps()
                    P.op('pe', lambda e: e.matmul(po[:, 0:128], attT[:], Fv[:, tg, :], start=True, stop=False),
                         reads=['attT', 'Fv'], writes=[pok])
                    chs = [0, 1, 2, 3]
                    if rev:
                        chs = chs[::-1]
                    for ci, c in enumerate(chs):
                        Scur = Sst[cur]
                        Snew = Sst[1 - cur]
                        P.op('pe', lambda e: e.matmul(
                            po[32 * c:32 * c + 32, 0:128], qt[:, j * 128 + 32 * c: j * 128 + 32 * c + 32], Scur[:],
                            start=False, stop=(ci == 3), tile_position=(0, 32 * c)),
                            reads=['b_qt', 'Sst%d' % cur], writes=[pok])
                        pd, pdk = nps()
                        P.op('pe', lambda e: e.matmul(
                            pd[:, 0:128], khtok[32 * c:32 * c + 32, j, :], Fv[32 * c:32 * c + 32, tg, :],
                            start=True, stop=True, tile_position=(32 * c, 0)),
                            reads=['khtok', 'Fv'], writes=[pdk])
                        gidx = j * 4 + c
                        P.op('dve', lambda e: e.scalar_tensor_tensor(
                            out=Snew[:], in0=Scur[:], scalar=gam[:, gidx:gidx + 1], in1=pd[:, 0:128],
                            op0=ALU.mult, op1=ALU.add),
                            reads=['Sst%d' % cur, 'gam', pdk], writes=['Sst%d' % (1 - cur)])
                        cur = 1 - cur
                    if d == 0:
                        P.op('act', lambda e: e.copy(out=Fo[:, tg, :], in_=po[:, 0:128]), reads=[pok], writes=[('Fo', tg)])
                    else:
                        P.op('dve', lambda e: e.tensor_tensor(out=Fo[:, tg, :], in0=Fo[:, tg, :], in1=po[:, 0:128], op=ALU.add),
                             reads=[pok, ('Fo', tg)], writes=[('Fo', tg)])
            if not is_sample:
                P.dma(o_hgrn[pidx, d, h], Sst[cur][:], reads=['Sst%d' % cur], q='pool')
        for tg in range(T // 128):
            P.op('act', lambda e: e.activation(out=attT[:], in_=Fo[:, tg, :], func=AF.Square, accum_out=ostat[:, tg, 0:1]),
                 reads=[('Fo', tg)], writes=['attT', ('ostat', tg)])
            P.op('dve', lambda e: e.tensor_scalar(out=ostat[:, tg, 1:2], in0=ostat[:, tg, 0:1], scalar1=1.0 / 128,
                                                  scalar2=1e-6, op0=ALU.mult, op1=ALU.add),
                 reads=[('ostat', tg)], writes=[('ostat', tg)])
            P.op('act', lambda e: e.activation(out=ostat[:, tg, 2:3], in_=ostat[:, tg, 1:2], func=AF.Sqrt),
                 reads=[('ostat', tg)], writes=[('ostat', tg)])
            P.op('dve', lambda e: e.reciprocal(out=ostat[:, tg, 3:4], in_=ostat[:, tg, 2:3]),
                 reads=[('ostat', tg)], writes=[('ostat', tg)])
            P.op('dve', lambda e: e.tensor_scalar(out=Fo[:, tg, :], in0=Fo[:, tg, :], scalar1=ostat[:, tg, 3:4],
                                                  scalar2=None, op0=ALU.mult),
                 reads=[('Fo', tg), ('ostat', tg)], writes=[('Fo', tg)])
        n4 = min(4, T // 128)
        for g4 in range(T // (128 * n4)):
            pz, pk = nps()
            for j in range(n4):
                tg = g4 * n4 + j
                P.op('pe', lambda e: e.transpose(out=pz[:, j * 128:(j + 1) * 128], in_=Fo[:, tg, :], identity=ident[:]),
                     reads=[('Fo', tg), 'ident'], writes=[pk])
            w_ = n4 * 128
            P.op('dve', lambda e: e.scalar_tensor_tensor(
                out=yT[:, h, g4 * w_:(g4 + 1) * w_], in0=pz[:, 0:w_], scalar=hgg_sb[:, h:h + 1],
                in1=Fsz[:, g4 * w_:(g4 + 1) * w_], op0=ALU.mult, op1=ALU.mult),
                reads=[pk, 'hgg', 'Fsz'], writes=[('yT', h)])

    TR = 256
    LR = [sb0("LR%d" % g, [64, TS], BF16) for g in range(4)]
    rb = {n_: BT[i][0:64, :] for i, n_ in enumerate(
          ['lw', 'a', 'kk', 'kq', 'kap', 'kd', 'b', 'rk', 'G', 'br', 'E', 'Ei', 'Em', 'bh', 'kh', 'Kb', 'Bb', 't1'])}
    KR = sb0("r_KR", [64, 2, TR])
    rsm = {n_: sb0("rs_" + n_, [64, 128]) for n_ in ['AB', 'BB']}
    rsq = {n_: sb0("rq_" + n_, [64, 64]) for n_ in ['XT0', 'XT1', 'X1', 'Pm0', 'Pm1', 'Vt', 'Kt', 'Bt', 'W', 'U', 'Z0', 'Z1', 'zt']}
    rgam = sb0("rgam", [64, 8])
    rgref = sb0("rgref", [64, 4])
    mR = sb0("mR", [64, 2, 3, 128])
    P.dma(mR[:], maskR.rearrange("d m s t -> s d m t"), writes=['mR'])
    prm = {}
    for n_, src_, shp in [('mu_rkv', mu_rkv, [64, 2, 4, 16]), ('mu_lr', mu_lr, [64, 2, 4]), ('w0', w0T, [64, 2, 16]),
                          ('a0', a0T, [64, 2, 16]), ('kk', kkT, [64, 16]), ('ka', kaT, [64, 16]), ('rk', rkT, [64, 16]),
                          ('gng', gngT, [64, 16]), ('gnb', gnbT, [64, 16])]:
        prm[n_] = sb0("p_" + n_, shp)
        P.dma(prm[n_][:], src_[:], writes=['p_' + n_])
    c0_rkv = sb0("c0_rkv", [64, 4, 16])
    c0_lr = sb0("c0_lr", [64, 4])
    omka = sb0("omka", [64, 16])
    P.op('dve', lambda e: e.tensor_tensor(out=c0_rkv[:], in0=prm['mu_rkv'][:, 0], in1=prm['mu_rkv'][:, 1], op=ALU.add),
         reads=['p_mu_rkv'], writes=['c0_rkv'])
    P.op('dve', lambda e: e.tensor_scalar(out=c0_rkv[:], in0=c0_rkv[:], scalar1=-1.0, scalar2=1.0, op0=ALU.mult, op1=ALU.add),
         reads=['c0_rkv'], writes=['c0_rkv'])
    P.op('dve', lambda e: e.tensor_tensor(out=c0_lr[:], in0=prm['mu_lr'][:, 0], in1=prm['mu_lr'][:, 1], op=ALU.add),
         reads=['p_mu_lr'], writes=['c0_lr'])
    P.op('dve', lambda e: e.tensor_scalar(out=c0_lr[:], in0=c0_lr[:], scalar1=-1.0, scalar2=1.0, op0=ALU.mult, op1=ALU.add),
         reads=['c0_lr'], writes=['c0_lr'])
    P.op('dve', lambda e: e.tensor_scalar(out=omka[:], in0=prm['ka'][:], scalar1=-1.0, scalar2=1.0, op0=ALU.mult, op1=ALU.add),
         reads=['p_ka'], writes=['omka'])
    w2a2 = sb0("w2a2", [64, 4, D], BF16)
    for g, src_ in enumerate([w2[0], w2[1], a2[0], a2[1]]):
        for c0 in range(0, D, 256):
            P.dma(wst[0:64, 0, 0:256], src_[:, c0:c0 + 256], writes=['wst'])
            P.op('pool', lambda e: e.tensor_copy(out=w2a2[:, g, c0:c0 + 256], in_=wst[0:64, 0, 0:256]), reads=['wst'], writes=['w2a2'])

    def shift_into(dst, dkey, raw, rkey, T, c0ap, m0ap, m1ap, t1tile, eng='dve'):
        for s0 in range(0, T, 512):
            n = min(512, T - s0)
            P.op(eng, lambda e: e.tensor_scalar(out=t1tile[:, 0:n], in0=raw[:, 16 + s0:16 + s0 + n], scalar1=c0ap, scalar2=None,
                                                op0=ALU.mult), reads=[rkey], writes=['shift_t'])
            P.op('dve', lambda e: e.scalar_tensor_tensor(out=t1tile[:, 0:n], in0=raw[:, 15 + s0:15 + s0 + n], scalar=m0ap,
                                                       in1=t1tile[:, 0:n], op0=ALU.mult, op1=ALU.add),
                 reads=[rkey, 'shift_t'], writes=['shift_t'])
            P.op('dve', lambda e: e.scalar_tensor_tensor(out=dst[:, s0:s0 + n], in0=raw[:, 17 + s0:17 + s0 + n], scalar=m1ap,
                                                       in1=t1tile[:, 0:n], op0=ALU.mult, op1=ALU.add),
                 reads=[rkey, 'shift_t'], writes=[dkey])

    shiftt = sb0("shiftt", [64, 512])

    def rwkv_seq_setup(off, T):
        load_w(wLR, 256)
        pb = min(512, T)
        for g in range(4):
            raw = FT[g]
            P.op('pool', lambda e: e.memset(raw[0:64, 15:16], 0.0), writes=['FT%d' % g])
            P.op('pool', lambda e: e.memset(raw[0:64, T + 16:T + 17], 0.0), writes=['FT%d' % g])
            for b in range(T // pb):
                t0 = b * pb
                proj(64 * g, 64, t0, pb, lambda pz, pk: P.op(
                    'act', lambda e: e.copy(out=raw[0:64, 16 + t0:16 + t0 + pb], in_=pz[0:64, 0:pb]), reads=[pk], writes=['FT%d' % g]))
            shift_into(FT[4][0:64, :], 'FT4', raw[0:64, :], 'FT%d' % g, T, c0_lr[:, g:g + 1], prm['mu_lr'][:, 0, g:g + 1],
                       prm['mu_lr'][:, 1, g:g + 1], shiftt)
            if g < 2:
                P.op('act', lambda e: e.activation(out=LR[g][:, 0:T], in_=FT[4][0:64, 0:T], func=AF.Tanh), reads=['FT4'], writes=['LR%d' % g])
            else:
                P.op('act', lambda e: e.copy(out=LR[g][:, 0:T], in_=FT[4][0:64, 0:T]), reads=['FT4'], writes=['LR%d' % g])

    def rwkv_head(h, slot, off, T, is_sample, pidx):
        P.barrier()
        load_w(wB[h], 256)
        pb = min(512, T)
        nchT = T // 64
        for g in range(3):
            raw = FT[g]
            P.op('pool', lambda e: e.memset(raw[0:64, 15:16], 0.0), writes=['FT%d' % g])
            P.op('pool', lambda e: e.memset(raw[0:64, T + 16:T + 17], 0.0), writes=['FT%d' % g])
            for b in range(T // pb):
                t0 = b * pb
                proj(64 * g, 64, t0, pb, lambda pz, pk: P.op(
                    'act', lambda e: e.copy(out=raw[0:64, 16 + t0:16 + t0 + pb], in_=pz[0:64, 0:pb]), reads=[pk], writes=['FT%d' % g]))
            shift_into(FT[3 + g][0:64, :], 'FT%d' % (3 + g), raw[0:64, :], 'FT%d' % g, T, c0_rkv[:, g, h:h + 1],
                       prm['mu_rkv'][:, 0, g, h:h + 1], prm['mu_rkv'][:, 1, g, h:h + 1], shiftt, eng=('dve' if g != 1 else 'pool'))
        rS, kS, vS = FT[3][0:64, :], FT[4][0:64, :], FT[5][0:64, :]
        szb, yaccr, bonus = FT[0][0:64, :], FT[1][0:64, 0:T], FT[2][0:64, :]
        yacc = yaccr.rearrange("p (c v) -> p c v", v=64)
        for b in range(T // pb):
            t0 = b * pb
            proj(192, 64, t0, pb, lambda pz, pk: P.op(
                'act', lambda e: e.activation(out=szb[:, t0:t0 + pb], in_=pz[0:64, 0:pb], func=AF.Silu), reads=[pk], writes=['FT0']))
        tb = min(TR, T)
        nblk = T // tb
        P.barrier()
        for d in range(2):
            rev = (d == 1)
            cur = 0
            Zt = [rsq['Z0'], rsq['Z1']]
            if is_sample:
                P.dma(rsq['zt'][:], s_rwkv[d, h], writes=['rq_zt'])
                pz, pk = nps()
                P.op('pe', lambda e: e.transpose(out=pz[0:64, 0:64], in_=rsq['zt'][:], identity=ident[0:64, 0:64]),
                     reads=['rq_zt', 'ident'], writes=[pk])
                P.op('act', lambda e: e.copy(out=Zt[0][:], in_=pz[0:64, 0:64]), reads=[pk], writes=['rq_Z0'])
            else:
                P.op('pool', lambda e: e.memset(Zt[0][:], 0.0), writes=['rq_Z0'])
            blks = list(range(nblk))
            if rev:
                blks = blks[::-1]
            for b in blks:
                t0 = b * tb
                sl = slice(t0, t0 + tb)
                nch = tb // 64
                R_ = rb
                pz, pk = nps()
                P.op('pe', lambda e: e.matmul(pz[0:64, 0:tb], w2a2[:, d, h * 64:(h + 1) * 64], LR[d][:, sl], start=True, stop=True),
                     reads=['w2a2', 'LR%d' % d], writes=[pk])
                P.op('act', lambda e: e.activation(out=R_['lw'][:, 0:tb], in_=pz[0:64, 0:tb], func=AF.Sigmoid,
                                                   bias=prm['w0'][:, d, h:h + 1], scale=1.0), reads=[pk, 'p_w0'], writes=['r_lw'])
                P.op('pool', lambda e: e.tensor_scalar(out=R_['lw'][:, 0:tb], in0=R_['lw'][:, 0:tb], scalar1=-0.6065306597126334,
                                                       scalar2=None, op0=ALU.mult), reads=['r_lw'], writes=['r_lw'])
                pz, pk = nps()
                P.op('pe', lambda e: e.matmul(pz[0:64, 0:tb], w2a2[:, 2 + d, h * 64:(h + 1) * 64], LR[2 + d][:, sl], start=True, stop=True),
                     reads=['w2a2', 'LR%d' % (2 + d)], writes=[pk])
                P.op('act', lambda e: e.activation(out=R_['a'][:, 0:tb], in_=pz[0:64, 0:tb], func=AF.Sigmoid,
                                                   bias=prm['a0'][:, d, h:h + 1], scale=1.0), reads=[pk, 'p_a0'], writes=['r_a'])
                P.op('dve', lambda e: e.tensor_scalar(out=R_['kk'][:, 0:tb], in0=kS[:, sl], scalar1=prm['kk'][:, h:h + 1],
                                                      scalar2=None, op0=ALU.mult), reads=['FT4', 'p_kk'], writes=['r_kk'])
                P.op('pool', lambda e: e.tensor_tensor(out=R_['kq'][:, 0:tb], in0=R_['kk'][:, 0:tb], in1=R_['kk'][:, 0:tb], op=ALU.mult),
                     reads=['r_kk'], writes=['r_kq'])
                pz, pk = nps()
                P.op('pe', lambda e: e.matmul(pz[0:64, 0:tb], ones[0:64, 0:64], R_['kq'][:, 0:tb], start=True, stop=True),
                     reads=['ones', 'r_kq'], writes=[pk])
                P.op('act', lambda e: e.activation(out=R_['kq'][:, 0:tb], in_=pz[0:64, 0:tb], func=AF.Sqrt), reads=[pk], writes=['r_kq'])
                P.op('dve', lambda e: e.tensor_scalar(out=R_['kq'][:, 0:tb], in0=R_['kq'][:, 0:tb], scalar1=1e-12, scalar2=None,
                                                      op0=ALU.max), reads=['r_kq'], writes=['r_kq'])
                P.op('dve', lambda e: e.reciprocal(out=R_['kq'][:, 0:tb], in_=R_['kq'][:, 0:tb]), reads=['r_kq'], writes=['r_kq'])
                P.op('dve', lambda e: e.tensor_tensor(out=R_['kap'][:, 0:tb], in0=R_['kk'][:, 0:tb], in1=R_['kq'][:, 0:tb], op=ALU.mult),
                     reads=['r_kk', 'r_kq'], writes=['r_kap'])
                P.op('pool', lambda e: e.tensor_scalar(out=R_['t1'][:, 0:tb], in0=R_['a'][:, 0:tb], scalar1=prm['ka'][:, h:h + 1],
                                                       scalar2=omka[:, h:h + 1], op0=ALU.mult, op1=ALU.add),
                     reads=['r_a', 'p_ka', 'omka'], writes=['r_t1'])
                P.op('pool', lambda e: e.tensor_tensor(out=R_['kd'][:, 0:tb], in0=kS[:, sl], in1=R_['t1'][:, 0:tb], op=ALU.mult),
                     reads=['FT4', 'r_t1'], writes=['r_kd'])
                P.op('dve', lambda e: e.tensor_tensor(out=R_['b'][:, 0:tb], in0=R_['a'][:, 0:tb], in1=R_['kap'][:, 0:tb], op=ALU.mult),
                     reads=['r_a', 'r_kap'], writes=['r_b'])
                P.op('dve', lambda e: e.scalar_tensor_tensor(out=R_['rk'][:, 0:tb], in0=rS[:, sl], scalar=prm['rk'][:, h:h + 1],
                                                             in1=R_['kd'][:, 0:tb], op0=ALU.mult, op1=ALU.mult),
                     reads=['FT3', 'p_rk', 'r_kd'], writes=['r_rk'])
                pz, pk = nps()
                P.op('pe', lambda e: e.matmul(pz[0:64, 0:tb], ones[0:64, 0:64], R_['rk'][:, 0:tb], start=True, stop=True),
                     reads=['ones', 'r_rk'], writes=[pk])
                if d == 0:
                    P.op('dve', lambda e: e.tensor_tensor(out=bonus[:, sl], in0=pz[0:64, 0:tb], in1=vS[:, sl], op=ALU.mult),
                         reads=[pk, 'FT5'], writes=[('bonus', b)])
                else:
                    P.op('dve', lambda e: e.tensor_tensor(out=R_['rk'][:, 0:tb], in0=pz[0:64, 0:tb], in1=vS[:, sl], op=ALU.mult),
                         reads=[pk, 'FT5'], writes=['r_rk'])
                    P.op('pool', lambda e: e.tensor_tensor(out=bonus[:, sl], in0=bonus[:, sl], in1=R_['rk'][:, 0:tb], op=ALU.add),
                         reads=['r_rk', ('bonus', b)], writes=[('bonus', b)])
                G, br, E, Ei, Em = R_['G'], R_['br'], R_['E'], R_['Ei'], R_['Em']
                P.op('dve', lambda e: e.memset(E[:, 0:tb], 0.0), writes=['r_E'])
                if not rev:
                    P.op('dve', lambda e: e.tensor_tensor_scan(out=G[:, 0:tb], data0=R_['lw'][:, 0:tb], data1=E[:, 0:tb],
                                                               initial=0.0, op0=ALU.add, op1=ALU.add),
                         reads=['r_lw', 'r_E'], writes=['r_G'])
                    ci_ = 0
                else:
                    P.op('dve', lambda e: e.tensor_tensor_scan(out=G[:, 0:tb][:, ::-1], data0=R_['lw'][:, 0:tb][:, ::-1],
                                                               data1=E[:, 0:tb], initial=0.0, op0=ALU.add, op1=ALU.add),
                         reads=['r_lw', 'r_E'], writes=['r_G'])
                    ci_ = 63
                G3 = G[:, 0:tb].rearrange("p (c l) -> p c l", l=64)
                lw3 = R_['lw'][:, 0:tb].rearrange("p (c l) -> p c l", l=64)
                P.op('dve', lambda e: e.tensor_tensor(out=rgref[:, 0:nch], in0=G3[:, :, ci_], in1=lw3[:, :, ci_], op=ALU.subtract),
                     reads=['r_G', 'r_lw'], writes=['rgref'])
                P.op('dve', lambda e: e.tensor_tensor(out=br[:, 0:tb].rearrange("p (c l) -> p c l", l=64), in0=G3,
                                                      in1=rgref[:, 0:nch].unsqueeze(2).to_broadcast([64, nch, 64]), op=ALU.subtract),
                     reads=['r_G', 'rgref'], writes=['r_br'])
                bend = br[:, 0:tb].rearrange("p (c l) -> p c l", l=64)[:, :, (0 if rev else 63)]
                P.op('act', lambda e: e.activation(out=rgam[:, 0:nch], in_=bend, func=AF.Exp), reads=['r_br'], writes=['rgam'])
                P.op('pool', lambda e: e.tensor_scalar(out=rgam[:, 4:4 + nch], in0=rgam[:, 0:nch], scalar1=-1.0, scalar2=None,
                                                       op0=ALU.mult), reads=['rgam'], writes=['rgam'])
                P.op('act', lambda e: e.activation(out=E[:, 0:tb], in_=br[:, 0:tb], func=AF.Exp), reads=['r_br'], writes=['r_E'])
                P.op('act', lambda e: e.activation(out=Ei[:, 0:tb], in_=br[:, 0:tb], func=AF.Exp, scale=-1.0),
                     reads=['r_br'], writes=['r_Ei'])
                P.op('pool', lambda e: e.tensor_tensor(out=R_['t1'][:, 0:tb], in0=br[:, 0:tb], in1=R_['lw'][:, 0:tb], op=ALU.subtract),
                     reads=['r_br', 'r_lw'], writes=['r_t1'])
                P.op('act', lambda e: e.activation(out=Em[:, 0:tb], in_=R_['t1'][:, 0:tb], func=AF.Exp), reads=['r_t1'], writes=['r_Em'])
                P.op('dve', lambda e: e.tensor_tensor(out=KR[:, 0, 0:tb], in0=R_['kap'][:, 0:tb], in1=Em[:, 0:tb], op=ALU.mult),
                     reads=['r_kap', 'r_Em'], writes=['r_KR'])
                P.op('pool', lambda e: e.tensor_tensor(out=KR[:, 1, 0:tb], in0=rS[:, sl], in1=E[:, 0:tb], op=ALU.mult),
                     reads=['FT3', 'r_E', 'r_KR'], writes=['r_KR'])
                P.op('dve', lambda e: e.tensor_tensor(out=R_['bh'][:, 0:tb], in0=R_['b'][:, 0:tb], in1=Ei[:, 0:tb], op=ALU.mult),
                     reads=['r_b', 'r_Ei'], writes=['r_bh'])
                P.op('pool', lambda e: e.tensor_tensor(out=R_['kh'][:, 0:tb], in0=R_['kd'][:, 0:tb], in1=Ei[:, 0:tb], op=ALU.mult),
                     reads=['r_kd', 'r_Ei'], writes=['r_kh'])
                P.op('dve', lambda e: e.tensor_tensor(out=R_['Kb'][:, 0:tb].rearrange("p (c l) -> p c l", l=64),
                                                      in0=R_['kh'][:, 0:tb].rearrange("p (c l) -> p c l", l=64),
                                                      in1=rgam[:, 0:nch].unsqueeze(2).to_broadcast([64, nch, 64]), op=ALU.mult),
                     reads=['r_kh', 'rgam'], writes=['r_Kb'])
                P.op('dve', lambda e: e.tensor_tensor(out=R_['Bb'][:, 0:tb].rearrange("p (c l) -> p c l", l=64),
                                                      in0=R_['bh'][:, 0:tb].rearrange("p (c l) -> p c l", l=64),
                                                      in1=rgam[:, 4:4 + nch].unsqueeze(2).to_broadcast([64, nch, 64]), op=ALU.mult),
                     reads=['r_bh', 'rgam'], writes=['r_Bb'])
                chs = list(range(nch))
                if rev:
                    chs = chs[::-1]
                for c in chs:
                    cs = slice(c * 64, (c + 1) * 64)
                    gsl = slice(t0 + c * 64, t0 + (c + 1) * 64)
                    cg = (t0 // 64) + c
                    Zc, Zn = Zt[cur], Zt[1 - cur]
                    zck, znk = 'rq_Z%d' % cur, 'rq_Z%d' % (1 - cur)
                    pA, pAk = nps()
                    P.op('pe', lambda e: e.matmul(pA[0:64, 0:128], R_['bh'][:, cs], KR[:, :, cs], start=True, stop=True),
                         reads=['r_bh', 'r_KR'], writes=[pAk])
                    P.op('pe', lambda e: e.matmul(pA[0:64, 128:256], R_['kh'][:, cs], KR[:, :, cs], start=True, stop=True),
                         reads=['r_kh', 'r_KR'], writes=[pAk])
                    P.op('pe', lambda e: e.matmul(pA[0:64, 256:320], KR[:, 0, cs], R_['bh'][:, cs], start=True, stop=True),
                         reads=['r_bh', 'r_KR'], writes=[pAk])
                    AB, BB = rsm['AB'], rsm['BB']
                    P.op('dve', lambda e: e.tensor_tensor(out=AB[:], in0=pA[0:64, 0:128], in1=mR[:, d, 0, :], op=ALU.mult),
                         reads=[pAk, 'mR'], writes=['rs_AB'])
                    P.op('dve', lambda e: e.tensor_tensor(out=BB[:], in0=pA[0:64, 128:256], in1=mR[:, d, 1, :], op=ALU.mult),
                         reads=[pAk, 'mR'], writes=['rs_BB'])
                    P.op('dve', lambda e: e.tensor_tensor(out=rsq['XT0'][:], in0=pA[0:64, 256:320], in1=mR[:, d, 2, 0:64], op=ALU.mult),
                         reads=[pAk, 'mR'], writes=['rq_XT0'])
                    P.op('dve', lambda e: e.tensor_tensor(out=rsq['Pm0'][:], in0=AB[:, 0:64], in1=ident[0:64, 0:64], op=ALU.add),
                         reads=['rs_AB', 'ident'], writes=['rq_Pm0'])
                    X, Xk = AB[:, 0:64], 'rs_AB'
                    XT, XTk = rsq['XT0'], 'rq_XT0'
                    pmi = 0
                    for lev in range(5):
                        pq, pqk = nps()
                        nXT = rsq['XT1'] if XT is rsq['XT0'] else rsq['XT0']
                        nXTk = 'rq_XT1' if XTk == 'rq_XT0' else 'rq_XT0'
                        P.op('pe', lambda e: e.matmul(pq[0:64, 64:128], X, XT[:], start=True, stop=True), reads=[Xk, XTk], writes=[pqk])
                        if lev < 4:
                            P.op('pe', lambda e: e.matmul(pq[0:64, 0:64], XT[:], X, start=True, stop=True), reads=[Xk, XTk], writes=[pqk])
                        P.op('act', lambda e: e.copy(out=nXT[:], in_=pq[0:64, 64:128]), reads=[pqk], writes=[nXTk])
                        Pc, Pn = rsq['Pm%d' % pmi], rsq['Pm%d' % (1 - pmi)]
                        pp, ppk = nps()
                        P.op('pe', lambda e: e.matmul(pp[0:64, 0:64], nXT[:], Pc[:], start=True, stop=True),
                             reads=[nXTk, 'rq_Pm%d' % pmi], writes=[ppk])
                        P.op('dve', lambda e: e.tensor_tensor(out=Pn[:], in0=pp[0:64, 0:64], in1=Pc[:], op=ALU.add),
                             reads=[ppk, 'rq_Pm%d' % pmi], writes=['rq_Pm%d' % (1 - pmi)])
                        pmi = 1 - pmi
                        if lev < 4:
                            tgt, tgtk = (rsq['X1'], 'rq_X1') if lev % 2 == 0 else (rsq['W'], 'rq_W')
                            P.op('dve', lambda e: e.tensor_copy(out=tgt[:], in_=pq[0:64, 0:64]), reads=[pqk], writes=[tgtk])
                            X, Xk = tgt[:], tgtk
                        XT, XTk = nXT, nXTk
                    Pm, Pmk = rsq['Pm%d' % pmi], 'rq_Pm%d' % pmi
                    pt, ptk = nps()
                    P.op('pe', lambda e: e.transpose(out=pt[0:64, 0:64], in_=vS[:, gsl], identity=ident[0:64, 0:64]),
                         reads=['FT5', 'ident'], writes=[ptk])
                    P.op('pe', lambda e: e.transpose(out=pt[0:64, 64:128], in_=R_['Kb'][:, cs], identity=ident[0:64, 0:64]),
                         reads=['r_Kb', 'ident'], writes=[ptk])
                    P.op('pe', lambda e: e.transpose(out=pt[0:64, 128:192], in_=R_['Bb'][:, cs], identity=ident[0:64, 0:64]),
                         reads=['r_Bb', 'ident'], writes=[ptk])
                    P.op('act', lambda e: e.copy(out=rsq['Vt'][:], in_=pt[0:64, 0:64]), reads=[ptk], writes=['rq_Vt'])
                    P.op('act', lambda e: e.copy(out=rsq['Kt'][:], in_=pt[0:64, 64:128]), reads=[ptk], writes=['rq_Kt'])
                    P.op('act', lambda e: e.copy(out=rsq['Bt'][:], in_=pt[0:64, 128:192]), reads=[ptk], writes=['rq_Bt'])
                    pw, pwk = nps()
                    P.op('pe', lambda e: e.matmul(pw[0:64, 0:64], KR[:, 0, cs], Zc[:], start=True, stop=False),
                         reads=['r_KR', zck], writes=[pwk])
                    P.op('pe', lambda e: e.matmul(pw[0:64, 0:64], BB[:, 0:64], rsq['Vt'][:], start=False, stop=True),
                         reads=['rs_BB', 'rq_Vt'], writes=[pwk])
                    P.op('act', lambda e: e.copy(out=rsq['zt'][:], in_=pw[0:64, 0:64]), reads=[pwk], writes=['rq_zt'])
                    pu, puk = nps()
                    P.op('pe', lambda e: e.matmul(pu[0:64, 0:64], Pm[:], rsq['zt'][:], start=True, stop=True),
                         reads=[Pmk, 'rq_zt'], writes=[puk])
                    P.op('act', lambda e: e.copy(out=rsq['U'][:], in_=pu[0:64, 0:64]), reads=[puk], writes=['rq_U'])
                    py, pyk = nps()
                    P.op('pe', lambda e: e.matmul(py[0:64, 0:64], KR[:, 1, cs], Zc[:], start=True, stop=False),
                         reads=['r_KR', zck], writes=[pyk])
                    P.op('pe', lambda e: e.matmul(py[0:64, 0:64], BB[:, 64:128], rsq['Vt'][:], start=False, stop=False),
                         reads=['rs_BB', 'rq_Vt'], writes=[pyk])
                    P.op('pe', lambda e: e.matmul(py[0:64, 0:64], AB[:, 64:128], rsq['U'][:], start=False, stop=True),
                         reads=['rs_AB', 'rq_U'], writes=[pyk])
                    if d == 0:
                        P.op('act', lambda e: e.copy(out=yacc[:, cg, :], in_=py[0:64, 0:64]), reads=[pyk], writes=[('yacc', cg)])
                    else:
                        P.op('dve', lambda e: e.tensor_tensor(out=yacc[:, cg, :], in0=yacc[:, cg, :], in1=py[0:64, 0:64], op=ALU.add),
                             reads=[pyk, ('yacc', cg)], writes=[('yacc', cg)])
                    pzz, pzk = nps()
                    P.op('pe', lambda e: e.matmul(pzz[0:64, 0:64], rsq['Kt'][:], rsq['Vt'][:], start=True, stop=False),
                         reads=['rq_Kt', 'rq_Vt'], writes=[pzk])
                    P.op('pe', lambda e: e.matmul(pzz[0:64, 0:64], rsq['Bt'][:], rsq['U'][:], start=False, stop=True),
                         reads=['rq_Bt', 'rq_U'], writes=[pzk])
                    P.op('dve', lambda e: e.scalar_tensor_tensor(out=Zn[:], in0=Zc[:], scalar=rgam[:, c:c + 1], in1=pzz[0:64, 0:64],
                                                                 op0=ALU.mult, op1=ALU.add), reads=[zck, 'rgam', pzk], writes=[znk])
                    cur = 1 - cur
            if not is_sample:
                pz, pk = nps()
                P.op('pe', lambda e: e.transpose(out=pz[0:64, 0:64], in_=Zt[cur][:], identity=ident[0:64, 0:64]),
                     reads=['rq_Z%d' % cur, 'ident'], writes=[pk])
                P.op('act', lambda e: e.copy(out=rsq['zt'][:], in_=pz[0:64, 0:64]), reads=[pk], writes=['rq_zt'])
                P.dma(o_rwkv[pidx, d, h], rsq['zt'][:], reads=['rq_zt'], q='pool')
        ykeys = [('yacc', c) for c in range(nchT)]
        gst = ostat[0:64, :, :].rearrange("p a b -> p (a b)")
        P.op('dve', lambda e: e.tensor_reduce(out=gst[:, 0:nchT], in_=yacc, axis=AX.X, op=ALU.add), reads=ykeys, writes=['gst'])
        P.op('dve', lambda e: e.tensor_scalar(out=gst[:, 0:nchT], in0=gst[:, 0:nchT], scalar1=-1.0 / 64, scalar2=None, op0=ALU.mult),
             reads=['gst'], writes=['gst'])
        P.op('dve', lambda e: e.tensor_tensor(out=yacc, in0=yacc, in1=gst[:, 0:nchT].unsqueeze(2).to_broadcast([64, nchT, 64]), op=ALU.add),
             reads=ykeys + ['gst'], writes=ykeys)
        sq = FT[3][0:64, 0:T].rearrange("p (c v) -> p c v", v=64)
        P.op('pool', lambda e: e.tensor_tensor(out=sq, in0=yacc, in1=yacc, op=ALU.mult), reads=ykeys, writes=['FT3'])
        P.op('dve', lambda e: e.tensor_reduce(out=gst[:, 32:32 + nchT], in_=sq, axis=AX.X, op=ALU.add), reads=['FT3'], writes=['gst'])
        P.op('dve', lambda e: e.tensor_scalar(out=gst[:, 32:32 + nchT], in0=gst[:, 32:32 + nchT], scalar1=1.0 / 64, scalar2=64e-5,
                                              op0=ALU.mult, op1=ALU.add), reads=['gst'], writes=['gst'])
        P.op('act', lambda e: e.activation(out=gst[:, 32:32 + nchT], in_=gst[:, 32:32 + nchT], func=AF.Sqrt), reads=['gst'], writes=['gst'])
        P.op('dve', lambda e: e.reciprocal(out=gst[:, 32:32 + nchT], in_=gst[:, 32:32 + nchT]), reads=['gst'], writes=['gst'])
        P.op('dve', lambda e: e.tensor_tensor(out=yacc, in0=yacc, in1=gst[:, 32:32 + nchT].unsqueeze(2).to_broadcast([64, nchT, 64]),
                                              op=ALU.mult), reads=ykeys + ['gst'], writes=ykeys)
        n8 = min(8, nchT)
        for g8 in range(nchT // n8):
            pz, pk = nps()
            for j in range(n8):
                cg = g8 * n8 + j
                P.op('pe', lambda e: e.transpose(out=pz[0:64, j * 64:(j + 1) * 64], in_=yacc[:, cg, :], identity=ident[0:64, 0:64]),
                     reads=[('yacc', cg), 'ident'], writes=[pk])
            w_ = n8 * 64
            gs = slice(g8 * w_, (g8 + 1) * w_)
            P.op('dve', lambda e: e.tensor_scalar(out=shiftt[:, 0:w_], in0=pz[0:64, 0:w_], scalar1=prm['gng'][:, h:h + 1],
                                                  scalar2=prm['gnb'][:, h:h + 1], op0=ALU.mult, op1=ALU.add),
                 reads=[pk, 'p_gng', 'p_gnb'], writes=['shift_t'])
            P.op('pool', lambda e: e.tensor_tensor(out=shiftt[:, 0:w_], in0=shiftt[:, 0:w_], in1=bonus[:, gs], op=ALU.add),
                 reads=['shift_t'] + [('bonus', b) for b in range(nblk)], writes=['shift_t'])
            P.op('dve', lambda e: e.tensor_tensor(out=yT[0:64, slot, gs], in0=shiftt[:, 0:w_], in1=szb[:, gs], op=ALU.mult),
                 reads=['shift_t', 'FT0'], writes=[('yT', slot)])

    seq_ids = debug.get('seqs', [0, 1, 2]) if debug else [0, 1, 2]
    head_ids = debug.get('heads', list(range(8))) if debug else list(range(8))
    rheads = debug.get('rheads', list(range(16))) if debug else list(range(16))
    make_gate(0)
    for si in seq_ids:
        off, T, cidx, is_sample = SEQS[si]
        P.barrier()
        make_hT(0, xin, 'xin', off, T, cidx)
        P.barrier()
        if debug and debug.get('inner'):
            dump("mT", mT[:, 0].rearrange("p a b -> p (a b)"), ['mT'], 48)
            dump("sc1", sc1[:, 0].rearrange("p a b -> p (a b)"), ['sc1'], 16)
            dump("scT", scT[:].rearrange("p a b -> p (a b)"), ['scT'], 16)
            dump("xn", xn, ['xn'], 1024)
            dump("xt0", xt[0], ['xt0'], 1024)
            for kc in range(8):
                dump("hT%d" % kc, hT[:, kc, 0:T], hT_keys(0, T), T, col0=off)
        for h in head_ids:
            hgrn_head(h, off, T, is_sample, si - 1)
            if debug and debug.get('dump_y'):
                dump("yT%d" % h, yT[:, h, 0:T], [('yT', h)], T, col0=off)
        P.barrier()
        load_wo(w_out_even[0:D, :], 128)
        outproj(0, [(128, s_) for s_ in range(8)], xin, 'xin', x1, 'x1', off, T, cidx)
        P.barrier()
        rwkv_seq_setup(off, T)
        for half in range(2):
            P.barrier()
            for slot in range(8):
                h = half * 8 + slot
                if h in rheads:
                    rwkv_head(h, slot, off, T, is_sample, si - 1)
                    if debug and debug.get('dump_y'):
                        dump("yR%d" % h, yT[0:64, slot, 0:T], [('yT', slot)], T, col0=off, parts=64)
            P.barrier()
            load_wo(w_out_even[D + half * 512: D + (half + 1) * 512, :], 64)
            outproj(0, [(64, s_) for s_ in range(8)], x1, 'x1', x1, 'x1', off, T, cidx)
    P.barrier()
    L0.close()

    if not (debug and debug.get('l0only')):
        L1 = contextlib.ExitStack()

        def sb1(name, shape, dt=F32):
            return L1.enter_context(nc.sbuf_tensor(name, list(shape), dt))

        make_fng()
        LC = 128
        DH = 512
        qT = yT[:, 4:8, :]
        kT = sb1("kT", [128, 4, TS], BF16)
        vch = sb1("vch", [128, DH], BF16)
        Cst = sb1("Cst", [128, 4, DH])
        Cbf = sb1("Cbf", [128, 4, DH], BF16)
        nst = sb1("nst", [128, 8])
        nbf = sb1("nbf", [128, 4], BF16)
        ktok = sb1("ktok", [128, DH], BF16)
        vw = sb1("vw", [128, DH], BF16)
        sTs = sb1("sTs", [128, 128], BF16)
        onesb = sb1("onesb", [128, 1], BF16)
        identb = sb1("identb", [128, 128], BF16)
        mC = sb1("mC", [128, 2, 128])
        SEL = sb1("SEL", [36, 4, 128])
        XA = sb1("XA", [36, TS])
        XB = sb1("XB", [36, TS])
        zrow = sb1("zrow", [36, 512])
        sm = {n_: sb1("sm_" + n_, [36, 16]) for n_ in ['ac', 'bl', 'M', 'MP', 'mu', 'al', 'gref', 'm0']}
        Wtok = sb1("Wtok", [128, 2, 16, 8])
        Wtokb = sb1("Wtokb", [128, 2, 16, 4], BF16)
        ALb = sb1("ALb", [128, 2, 4, 16])
        dstat = sb1("dstat", [128, 8])
        wGb = sb1("wGb", [128, 8, 16], BF16)
        gbT = sb1("gbT", [36, 4])
        ngbT = sb1("ngbT", [36, 4])
        cw = sb1("cw", [128, 32, 9])
        cb = sb1("cb", [128, 32])
        mng = sb1("mng", [128, 16])
        wbf1 = sb1("wbf1", [128, 8, DH], BF16)
        P.dma(mC[:], maskC.rearrange("d s t -> s d t"), writes=['mC'])
        P.dma(SEL[:], sel_d[:], writes=['SEL'])
        P.dma(gbT[:], gbT_d[:], writes=['gbT'])
        P.dma(cw[:], cw_d[:], writes=['cw'])
        P.dma(cb[:], cb_d[:], writes=['cb'])
        P.dma(mng[:], mng_d[:], writes=['mng'])
        P.op('dve', lambda e: e.memset(onesb[:], 1.0), writes=['onesb'])
        P.op('dve', lambda e: e.memset(zrow[:], 0.0), writes=['zrow'])
        P.op('dve', lambda e: e.tensor_copy(out=identb[:], in_=ident[:]), reads=['ident'], writes=['identb'])
        P.op('dve', lambda e: e.tensor_scalar(out=ngbT[:], in0=gbT[:], scalar1=-1.0, scalar2=None, op0=ALU.mult), reads=['gbT'], writes=['ngbT'])
        P.dma(wst[:, :, 0:16], w_in_odd[:, 10240:10256].rearrange("(kc p) n -> p kc n", p=128), writes=['wst'])
        P.op('pool', lambda e: e.tensor_copy(out=wGb[:], in_=wst[:, :, 0:16]), reads=['wst'], writes=['wGb'])
        LNK = float(np.log(DH ** -0.5))

        def load_w1(c0, ncols):
            v = w_in_odd[:, c0:c0 + ncols].rearrange("(kc p) n -> p kc n", p=128)
            for q0 in range(0, ncols, 384):
                w_ = min(384, ncols - q0)
                P.dma(wst[:, :, 0:w_], v[:, :, q0:q0 + w_], writes=['wst'])
                P.op('pool', lambda e: e.tensor_copy(out=wbf1[:, :, q0:q0 + w_], in_=wst[:, :, 0:w_]), reads=['wst'], writes=['wbf1'])

        def gates_seq(T, is_sample):
            NC = T // LC
            pbk = min(512, T)
            for d in range(2):
                pb = 32 * d
                rows = slice(pb, pb + 4)
                for b in range(T // pbk):
                    t0 = b * pbk
                    pz, pk = nps()
                    for kc in range(8):
                        P.op('pe', lambda e: e.matmul(pz[pb:pb + 4, 0:pbk], wGb[:, kc, (2 + d) * 4:(3 + d) * 4], hT[:, kc, t0:t0 + pbk],
                                                      start=(kc == 0), stop=(kc == 7)), reads=['wGb'] + hT_keys(t0, t0 + pbk), writes=[pk])
                    P.op('act', lambda e: e.activation(out=XA[rows, t0:t0 + pbk], in_=pz[pb:pb + 4, 0:pbk], func=AF.Exp,
                                                       bias=ngbT[rows, 2 + d:3 + d], scale=-1.0), reads=[pk, 'ngbT'], writes=['XA'])
                P.op('act', lambda e: e.activation(out=XA[rows, 0:T], in_=XA[rows, 0:T], func=AF.Ln, bias=1.0, scale=1.0),
                     reads=['XA'], writes=['XA'])
                for b in range(T // pbk):
                    bs = slice(b * pbk, (b + 1) * pbk)
                    if d == 0:
                        P.op('dve', lambda e: e.tensor_tensor_scan(out=XB[rows, bs], data0=XA[rows, bs], data1=zrow[rows, 0:pbk],
                                                                   initial=0.0, op0=ALU.add, op1=ALU.add), reads=['XA', 'zrow'], writes=['XB'])
                    else:
                        P.op('dve', lambda e: e.tensor_tensor_scan(out=XB[rows, bs][:, ::-1], data0=XA[rows, bs][:, ::-1],
                                                                   data1=zrow[rows, 0:pbk], initial=0.0, op0=ALU.add, op1=ALU.add),
                             reads=['XA', 'zrow'], writes=['XB'])
                if d == 0:
                    ci_, ce_ = 0, LC - 1
                else:
                    ci_, ce_ = LC - 1, 0
                B3 = XB[rows, 0:T].rearrange("p (c l) -> p c l", l=LC)
                A3 = XA[rows, 0:T].rearrange("p (c l) -> p c l", l=LC)
                S = {k_: v_[rows, :] for k_, v_ in sm.items()}
                P.op('dve', lambda e: e.tensor_tensor(out=S['gref'][:, 0:NC], in0=B3[:, :, ci_], in1=A3[:, :, ci_], op=ALU.subtract),
                     reads=['XA', 'XB'], writes=['sm_gref'])
                P.op('dve', lambda e: e.tensor_tensor(out=B3, in0=B3, in1=S['gref'][:, 0:NC].unsqueeze(2).to_broadcast([4, NC, LC]),
                                                      op=ALU.subtract), reads=['XB', 'sm_gref'], writes=['XB'])
                for b in range(T // pbk):
                    t0 = b * pbk
                    pz, pk = nps()
                    for kc in range(8):
                        P.op('pe', lambda e: e.matmul(pz[pb:pb + 4, 0:pbk], wGb[:, kc, d * 4:(d + 1) * 4], hT[:, kc, t0:t0 + pbk],
                                                      start=(kc == 0), stop=(kc == 7)), reads=['wGb'] + hT_keys(t0, t0 + pbk), writes=[pk])
                    P.op('dve', lambda e: e.scalar_tensor_tensor(out=XA[rows, t0:t0 + pbk], in0=pz[pb:pb + 4, 0:pbk], scalar=gbT[rows, d:d + 1],
                                                                 in1=XB[rows, t0:t0 + pbk], op0=ALU.add, op1=ALU.add),
                         reads=[pk, 'gbT', 'XB', 'XA'], writes=['XA'])
                P.op('dve', lambda e: e.tensor_reduce(out=S['ac'][:, 0:NC], in_=A3, axis=AX.X, op=ALU.max), reads=['XA'], writes=['sm_ac'])
                P.op('dve', lambda e: e.tensor_scalar(out=S['bl'][:, 0:NC], in0=B3[:, :, ce_], scalar1=-1.0, scalar2=None, op0=ALU.mult),
                     reads=['XB'], writes=['sm_bl'])
                if is_sample:
                    P.dma(S['m0'][:, 0:1], s_m[d, :].rearrange("(h o) -> h o", o=1), writes=['sm_m0'])
                else:
                    P.op('dve', lambda e: e.memset(S['m0'][:, 0:1], 0.0), writes=['sm_m0'])
                if d == 0:
                    P.op('dve', lambda e: e.tensor_tensor_scan(out=S['M'][:, 0:NC], data0=S['ac'][:, 0:NC], data1=S['bl'][:, 0:NC],
                                                               initial=S['m0'][:, 0:1], op0=ALU.max, op1=ALU.add),
                         reads=['sm_ac', 'sm_bl', 'sm_m0'], writes=['sm_M'])
                    P.op('dve', lambda e: e.tensor_copy(out=S['MP'][:, 0:1], in_=S['m0'][:, 0:1]), reads=['sm_m0'], writes=['sm_MP'])
                    if NC > 1:
                        P.op('dve', lambda e: e.tensor_copy(out=S['MP'][:, 1:NC], in_=S['M'][:, 0:NC - 1]), reads=['sm_M', 'sm_MP'], writes=['sm_MP'])
                else:
                    P.op('dve', lambda e: e.tensor_tensor_scan(out=S['M'][:, 0:NC][:, ::-1], data0=S['ac'][:, 0:NC][:, ::-1],
                                                               data1=S['bl'][:, 0:NC][:, ::-1], initial=S['m0'][:, 0:1],
                                                               op0=ALU.max, op1=ALU.add),
                         reads=['sm_ac', 'sm_bl', 'sm_m0'], writes=['sm_M'])
                    P.op('dve', lambda e: e.tensor_copy(out=S['MP'][:, NC - 1:NC], in_=S['m0'][:, 0:1]), reads=['sm_m0'], writes=['sm_MP'])
                    if NC > 1:
                        P.op('dve', lambda e: e.tensor_copy(out=S['MP'][:, 0:NC - 1], in_=S['M'][:, 1:NC]), reads=['sm_M', 'sm_MP'], writes=['sm_MP'])
                P.op('dve', lambda e: e.tensor_tensor(out=S['mu'][:, 0:NC], in0=S['MP'][:, 0:NC], in1=S['ac'][:, 0:NC], op=ALU.max),
                     reads=['sm_MP', 'sm_ac'], writes=['sm_mu'])
                P.op('dve', lambda e: e.tensor_tensor(out=S['al'][:, 0:NC], in0=S['MP'][:, 0:NC], in1=S['mu'][:, 0:NC], op=ALU.subtract),
                     reads=['sm_MP', 'sm_mu'], writes=['sm_al'])
                P.op('act', lambda e: e.activation(out=S['al'][:, 0:NC], in_=S['al'][:, 0:NC], func=AF.Exp), reads=['sm_al'], writes=['sm_al'])
                mub = S['mu'][:, 0:NC].unsqueeze(2).to_broadcast([4, NC, LC])
                P.op('dve', lambda e: e.tensor_tensor(out=A3, in0=A3, in1=mub, op=ALU.subtract), reads=['XA', 'sm_mu'], writes=['XA'])
                P.op('dve', lambda e: e.tensor_tensor(out=B3, in0=B3, in1=mub, op=ALU.subtract), reads=['XB', 'sm_mu'], writes=['XB'])
                P.op('dve', lambda e: e.tensor_scalar(out=XA[rows, 0:T], in0=XA[rows, 0:T], scalar1=LNK, scalar2=None, op0=ALU.add),
                     reads=['XA'], writes=['XA'])
                P.op('act', lambda e: e.activation(out=XA[rows, 0:T], in_=XA[rows, 0:T], func=AF.Exp), reads=['XA'], writes=['XA'])
                P.op('act', lambda e: e.activation(out=XB[rows, 0:T], in_=XB[rows, 0:T], func=AF.Exp), reads=['XB'], writes=['XB'])
                pz, pk = nps()
                for c in range(NC):
                    P.op('pe', lambda e: e.transpose(out=pz[:, c * 8:c * 8 + 4], in_=XA[rows, c * LC:(c + 1) * LC],
                                                     identity=ident[rows, pb:pb + 4]), reads=['XA', 'ident'], writes=[pk])
                    P.op('pe', lambda e: e.transpose(out=pz[:, c * 8 + 4:c * 8 + 8], in_=XB[rows, c * LC:(c + 1) * LC],
                                                     identity=ident[rows, pb:pb + 4]), reads=['XB', 'ident'], writes=[pk])
                P.op('dve', lambda e: e.tensor_copy(out=Wtok[:, d, 0:NC, :], in_=pz[:, 0:NC * 8].rearrange("p (c k) -> p c k", k=8)),
                     reads=[pk], writes=['Wtok'])
                P.op('dve', lambda e: e.tensor_copy(out=Wtokb[:, d, 0:NC, :], in_=Wtok[:, d, 0:NC, 0:4]), reads=['Wtok'], writes=['Wtokb'])
                pz, pk = nps()
                for hd in range(4):
                    P.op('pe', lambda e: e.matmul(pz[:, hd * 16:hd * 16 + NC], SEL[rows, hd, :], S['al'][:, 0:NC], start=True, stop=True),
                         reads=['SEL', 'sm_al'], writes=[pk])
                P.op('dve', lambda e: e.tensor_copy(out=ALb[:, d, :, 0:NC], in_=pz[:, 0:64].rearrange("p (h c) -> p h c", c=16)[:, :, 0:NC]),
                     reads=[pk], writes=['ALb'])

        def conv_tile(dst, dkey, slot_j, widx, t0src, T, is_sample):
            X = FT[1][:, 0:T]
            A = FT[0][:, 0:T]
            if is_sample:
                R_, Cw = T // 64, 64
                taps = [(dr, dc) for dr in (-1, 0, 1) for dc in (-1, 0, 1)]
            else:
                R_, Cw = 1, T
                taps = [(0, dc) for dc in (-1, 0, 1)]
            X3 = X.rearrange("p (r c) -> p r c", c=Cw)
            A3 = A.rearrange("p (r c) -> p r c", c=Cw)
            P.op('dve', lambda e: e.tensor_scalar(out=A, in0=X, scalar1=cw[:, widx, 4:5], scalar2=None, op0=ALU.mult),
                 reads=['FT1', 'cw'], writes=['FT0'])
            for (dr, dc) in taps:
                if dr == 0 and dc == 0:
                    continue
                r0, r1 = max(0, -dr), R_ - max(0, dr)
                c0, c1 = max(0, -dc), Cw - max(0, dc)
                ti = (dr + 1) * 3 + (dc + 1)
                P.op('dve', lambda e: e.scalar_tensor_tensor(out=A3[:, r0:r1, c0:c1], in0=X3[:, r0 + dr:r1 + dr, c0 + dc:c1 + dc],
                                                             scalar=cw[:, widx, ti:ti + 1], in1=A3[:, r0:r1, c0:c1],
                                                             op0=ALU.mult, op1=ALU.add), reads=['FT1', 'FT0', 'cw'], writes=['FT0'])
            P.op('act', lambda e: e.activation(out=dst[:, slot_j, 0:T], in_=A, func=AF.Silu, bias=cb[:, widx:widx + 1], scale=1.0),
                 reads=['FT0', 'cb'], writes=[dkey])

        hacc = [FT[2 + i][:, 0:TS].rearrange("p (j e) -> p j e", e=DH) for i in range(4)]

        def mlstm_head(hd, off, T, is_sample, pidx):
            NC = T // LC
            NTt = T // 128
            pbk = min(512, T)
            for qk in range(2):
                load_w1(qk * 2048 + hd * DH, DH)
                for j in range(4):
                    for b in range(T // pbk):
                        t0 = b * pbk
                        pz, pk = nps()
                        for kc in range(8):
                            P.op('pe', lambda e: e.matmul(pz[:, 0:pbk], wbf1[:, kc, j * 128:(j + 1) * 128], hT[:, kc, t0:t0 + pbk],
                                                          start=(kc == 0), stop=(kc == 7)), reads=['wbf1'] + hT_keys(t0, t0 + pbk), writes=[pk])
                        P.op('act', lambda e: e.copy(out=FT[1][:, t0:t0 + pbk], in_=pz[:, 0:pbk]), reads=[pk], writes=['FT1'])
                    widx = (qk * 4 + hd) * 4 + j
                    if qk == 0:
                        conv_tile(qT, ('yT', 4 + j), j, widx, 0, T, is_sample)
                    else:
                        conv_tile(kT, 'kT', j, widx, 0, T, is_sample)
            load_w1(4096 + hd * DH, DH)
            qkeys = [('yT', 4 + j) for j in range(4)]
            for d in range(2):
                rev = (d == 1)
                if is_sample:
                    P.dma(Cst[:], s_C[d, hd].rearrange("(j p) e -> p j e", p=128), writes=['Cst'])
                    P.dma(nst[:, 0:4], s_n[d, hd].rearrange("(j p) -> p j", p=128), writes=['nst'], allow_slow_non_contiguous=True)
                else:
                    P.op('pool', lambda e: e.memset(Cst[:], 0.0), writes=['Cst'])
                    P.op('pool', lambda e: e.memset(nst[:, 0:4], 0.0), writes=['nst'])
                chunks = list(range(NC))
                if rev:
                    chunks = chunks[::-1]
                for c in chunks:
                    cs = slice(c * LC, (c + 1) * LC)
                    wcol = Wtok[:, d, c, hd:hd + 1]
                    thcol = Wtok[:, d, c, 4 + hd:5 + hd]
                    alcol = ALb[:, d, hd, c:c + 1]
                    pv, pvk = nps()
                    for kc in range(8):
                        P.op('pe', lambda e: e.matmul(pv[:, 0:DH], hT[:, kc, cs], wbf1[:, kc, :], start=(kc == 0), stop=(kc == 7)),
                             reads=['wbf1', ('hT', c)], writes=[pvk])
                    P.op('act', lambda e: e.copy(out=vch[:], in_=pv[:, 0:DH]), reads=[pvk], writes=['vch'])
                    pt, ptk = nps()
                    ptb = pt[:].bitcast(BF16)
                    for j in range(4):
                        P.op('pe', lambda e: e.transpose(out=ptb[:, j * 128:(j + 1) * 128], in_=kT[:, j, cs], identity=identb[:]),
                             reads=['kT', 'identb'], writes=[ptk])
                    P.op('act', lambda e: e.copy(out=ktok[:], in_=ptb[:, 0:DH]), reads=[ptk], writes=['ktok'])
                    ps_, psk = nps()
                    for j in range(4):
                        P.op('pe', lambda e: e.matmul(ps_[:, 0:128], kT[:, j, cs], qT[:, j, cs], start=(j == 0), stop=(j == 3)),
                             reads=['kT'] + qkeys, writes=[psk])
                    P.op('dve', lambda e: e.scalar_tensor_tensor(out=sTs[:], in0=ps_[:, 0:128], scalar=wcol, in1=mC[:, d, :],
                                                                 op0=ALU.mult, op1=ALU.mult), reads=[psk, 'Wtok', 'mC'], writes=['sTs'])
                    P.op('dve', lambda e: e.tensor_scalar(out=Cst[:], in0=Cst[:], scalar1=alcol, scalar2=None, op0=ALU.mult),
                         reads=['Cst', 'ALb'], writes=['Cst'])
                    P.op('act', lambda e: e.copy(out=Cbf[:], in_=Cst[:]), reads=['Cst'], writes=['Cbf'])
                    P.op('dve', lambda e: e.tensor_scalar(out=nst[:, 0:4], in0=nst[:, 0:4], scalar1=alcol, scalar2=None, op0=ALU.mult),
                         reads=['nst', 'ALb'], writes=['nst'])
                    P.op('dve', lambda e: e.tensor_copy(out=nbf[:], in_=nst[:, 0:4]), reads=['nst'], writes=['nbf'])
                    pn, pnk = nps()
                    for j in range(4):
                        P.op('pe', lambda e: e.matmul(pn[:, 0:DH], qT[:, j, cs], Cbf[:, j, :], start=(j == 0), stop=False),
                             reads=qkeys + ['Cbf'], writes=[pnk])
                    P.op('pe', lambda e: e.matmul(pn[:, 0:DH], sTs[:], vch[:], start=False, stop=True),
                         reads=['sTs', 'vch'], writes=[pnk])
                    pd_, pdk = nps()
                    for j in range(4):
                        P.op('pe', lambda e: e.matmul(pd_[:, 0:1], qT[:, j, cs], nbf[:, j:j + 1], start=(j == 0), stop=False),
                             reads=qkeys + ['nbf'], writes=[pdk])
                    P.op('pe', lambda e: e.matmul(pd_[:, 0:1], sTs[:], onesb[:], start=False, stop=True), reads=['sTs', 'onesb'], writes=[pdk])
                    P.op('act', lambda e: e.activation(out=dstat[:, 2:3], in_=pd_[:, 0:1], func=AF.Abs), reads=[pdk], writes=['dstat'])
                    P.op('dve', lambda e: e.tensor_tensor(out=dstat[:, 0:1], in0=dstat[:, 2:3], in1=thcol, op=ALU.max),
                         reads=['dstat', 'Wtok'], writes=['dstat'])
                    P.op('dve', lambda e: e.reciprocal(out=dstat[:, 1:2], in_=dstat[:, 0:1]), reads=['dstat'], writes=['dstat'])
                    hdst = hacc[c // 4][:, c % 4, :]
                    if d == 0:
                        P.op('act', lambda e: e.activation(out=hdst, in_=pn[:, 0:DH], func=AF.Identity, scale=dstat[:, 1:2]),
                             reads=[pnk, 'dstat'], writes=[('hacc', c)])
                    else:
                        P.op('dve', lambda e: e.scalar_tensor_tensor(out=hdst, in0=pn[:, 0:DH], scalar=dstat[:, 1:2], in1=hdst,
                                                                     op0=ALU.mult, op1=ALU.add), reads=[pnk, 'dstat', ('hacc', c)], writes=[('hacc', c)])
                    P.op('pool', lambda e: e.tensor_scalar(out=vw[:], in0=vch[:], scalar1=wcol, scalar2=None, op0=ALU.mult),
                         reads=['vch', 'Wtok'], writes=['vw'])
                    for j in range(4):
                        pc, pck = nps()
                        P.op('pe', lambda e: e.matmul(pc[:, 0:DH], ktok[:, j * 128:(j + 1) * 128], vw[:], start=True, stop=True),
                             reads=['ktok', 'vw'], writes=[pck])
                        P.op('dve', lambda e: e.tensor_tensor(out=Cst[:, j, :], in0=Cst[:, j, :], in1=pc[:, 0:DH], op=ALU.add),
                             reads=[pck, 'Cst'], writes=['Cst'])
                    pq_, pqk = nps()
                    for j in range(4):
                        P.op('pe', lambda e: e.matmul(pq_[:, j:j + 1], ktok[:, j * 128:(j + 1) * 128], Wtokb[:, d, c, hd:hd + 1],
                                                      start=True, stop=True), reads=['ktok', 'Wtokb'], writes=[pqk])
                    P.op('dve', lambda e: e.tensor_tensor(out=nst[:, 0:4], in0=nst[:, 0:4], in1=pq_[:, 0:4], op=ALU.add),
                         reads=[pqk, 'nst'], writes=['nst'])
                if not is_sample:
                    P.dma(o_C[pidx, d, hd].rearrange("(j p) e -> p j e", p=128), Cst[:], reads=['Cst'], q='pool')
                    P.dma(o_n[pidx, d, hd].rearrange("(j p) -> p j", p=128), nst[:, 0:4], reads=['nst'], q='pool', allow_slow_non_contiguous=True)
            load_w1(6144 + hd * DH, DH)
            for tt in range(NTt):
                hdst = hacc[tt // 4][:, tt % 4, :]
                pz, pk = nps()
                for kc in range(8):
                    P.op('pe', lambda e: e.matmul(pz[:, 0:DH], hT[:, kc, tt * 128:(tt + 1) * 128], wbf1[:, kc, :],
                                                  start=(kc == 0), stop=(kc == 7)), reads=['wbf1', ('hT', tt)], writes=[pk])
                P.op('act', lambda e: e.activation(out=FT[0][:, 0:DH], in_=pz[:, 0:DH], func=AF.Sigmoid), reads=[pk], writes=['FT0'])
                P.op('dve', lambda e: e.tensor_tensor(out=hdst, in0=hdst, in1=FT[0][:, 0:DH], op=ALU.mult),
                     reads=['FT0', ('hacc', tt)], writes=[('hacc', tt)])
                P.op('act', lambda e: e.activation(out=FT[0][:, 0:DH], in_=hdst, func=AF.Square, accum_out=dstat[:, 4:5]),
                     reads=[('hacc', tt), 'FT0'], writes=['FT0', 'dstat'])
                P.op('dve', lambda e: e.tensor_scalar(out=dstat[:, 5:6], in0=dstat[:, 4:5], scalar1=1.0 / DH, scalar2=1e-6,
                                                      op0=ALU.mult, op1=ALU.add), reads=['dstat'], writes=['dstat'])
                P.op('act', lambda e: e.activation(out=dstat[:, 6:7], in_=dstat[:, 5:6], func=AF.Sqrt), reads=['dstat'], writes=['dstat'])
                P.op('dve', lambda e: e.reciprocal(out=dstat[:, 7:8], in_=dstat[:, 6:7]), reads=['dstat'], writes=['dstat'])
                P.op('dve', lambda e: e.tensor_scalar(out=hdst, in0=hdst, scalar1=dstat[:, 7:8], scalar2=None, op0=ALU.mult),
                     reads=[('hacc', tt), 'dstat'], writes=[('hacc', tt)])
            load_w1(8192 + hd * DH, DH)
            for tt in range(NTt):
                hdst = hacc[tt // 4][:, tt % 4, :]
                pz, pk = nps()
                for kc in range(8):
                    P.op('pe', lambda e: e.matmul(pz[:, 0:DH], hT[:, kc, tt * 128:(tt + 1) * 128], wbf1[:, kc, :],
                                                  start=(kc == 0), stop=(kc == 7)), reads=['wbf1', ('hT', tt)], writes=[pk])
                P.op('act', lambda e: e.activation(out=FT[0][:, 0:DH], in_=pz[:, 0:DH], func=AF.Silu), reads=[pk], writes=['FT0'])
                P.op('dve', lambda e: e.tensor_tensor(out=hdst, in0=hdst, in1=FT[0][:, 0:DH], op=ALU.mult),
                     reads=['FT0', ('hacc', tt)], writes=[('hacc', tt)])
                pz, pk = nps()
                for j in range(4):
                    P.op('pe', lambda e: e.transpose(out=pz[:, j * 128:(j + 1) * 128], in_=hdst[:, j * 128:(j + 1) * 128], identity=ident[:]),
                         reads=[('hacc', tt), 'ident'], writes=[pk])
                for j in range(4):
                    P.op('act', lambda e: e.activation(out=yT[:, j, tt * 128:(tt + 1) * 128], in_=pz[:, j * 128:(j + 1) * 128],
                                                       func=AF.Identity, scale=mng[:, hd * 4 + j:hd * 4 + j + 1]),
                         reads=[pk, 'mng'], writes=[('yT', j)])

        make_gate(1)
        for si in seq_ids:
            off, T, cidx, is_sample = SEQS[si]
            P.barrier()
            make_hT(1, x1, 'x1', off, T, cidx)
            P.barrier()
            gates_seq(T, is_sample)
            if not is_sample:
                for d in range(2):
                    lastc = (T // LC - 1) if d == 0 else 0
                    P.dma(o_m[si - 1, d, :].rearrange("(h o) -> h o", o=1), sm['M'][32 * d:32 * d + 4, lastc:lastc + 1],
                          reads=['sm_M'], q='pool')
            for hd in range(4):
                P.barrier()
                mlstm_head(hd, off, T, is_sample, si - 1)
                if debug and debug.get('dump_y'):
                    for j in range(4):
                        dump("yM%d_%d" % (hd, j), yT[:, j, 0:T], [('yT', j)], T, col0=off)
                P.barrier()
                load_wo(w_out_odd[hd * DH:(hd + 1) * DH, :], 128, nk=4)
                last = (hd == 3)
                outproj(1, [(128, s_) for s_ in range(4)], x1, 'x1', (yout if last else x1), ('yout' if last else 'x1'),
                        off, T, cidx, final=last)
        P.barrier()
        L1.close()
    P.finish()
    sems = {s: es.enter_context(nc.semaphore(s)) for s in P.sem_names}
    P.emit(sems)
    es.close()
    global _last_dslot
    _last_dslot = dslot if debug else {}
    return nc, P


def host_inputs(inp, core):
    f = lambda a: np.ascontiguousarray(a, dtype=np.float32)
    b = core % 2
    m = {}
    m["xin"] = f(np.concatenate([inp["x_sample"][b], inp["x_prompt"][2 * core], inp["x_prompt"][2 * core + 1]], axis=0))
    cond = np.stack([inp["c"][b], inp["c_ctx"]], axis=0)
    m["condT"] = f(cond.reshape(2, 8, 128).transpose(2, 1, 0))
    m["s_hgrn"] = f(inp["state_hgrn"][b, 0])
    m["s_rwkv"] = f(inp["state_rwkv"][b, 0])
    m["s_C"] = f(inp["state_mlstm_C"][b, 0])
    m["s_n"] = f(inp["state_mlstm_n"][b, 0])
    m["s_m"] = f(inp["state_mlstm_m"][b, 0])
    m["w_mod"] = f(inp["w_mod"])
    m["b_modT"] = f(inp["b_mod"].reshape(2, 24, 128).transpose(2, 0, 1))
    m["norm_gT"] = f(inp["norm_g"].reshape(2, 8, 128).transpose(2, 0, 1))
    m["fnorm_gT"] = f(inp["final_norm_g"].reshape(8, 128).T)
    w = inp["w_in_even"][0]
    DA = 1024
    wA = np.stack([np.concatenate([w[:, g * DA + h * 128: g * DA + (h + 1) * 128] for g in (0, 1, 4, 2, 3)], axis=1)
                   for h in range(8)], axis=0)
    m["wA"] = f(wA)
    o = 5 * DA
    zb0 = o + 3328
    wB = np.stack([np.concatenate([w[:, o + g * 1024 + h * 64: o + g * 1024 + (h + 1) * 64] for g in (0, 1, 2)]
                                  + [w[:, zb0 + h * 64: zb0 + (h + 1) * 64]], axis=1) for h in range(16)], axis=0)
    m["wB"] = f(wB)
    m["wLR"] = f(w[:, o + 3072: o + 3328])
    m["w_out_even"] = f(inp["w_out_even"][0])
    m["lbT"] = f(inp["hgrn_lb_logits"].reshape(2, 8, 128).transpose(2, 0, 1))
    m["hg_gT"] = f(inp["hgrn_norm_g"][0].reshape(8, 128).T)
    mu = inp["rwkv_shift_mu"][0]
    mr = np.zeros((64, 2, 4, 16), np.float32)
    for g in range(3):
        mr[:, :, g, :] = mu[:, g * 1024:(g + 1) * 1024].reshape(2, 16, 64).transpose(2, 0, 1)
    m["mu_rkv"] = mr
    m["mu_lr"] = f(mu[:, 3072:3328].reshape(2, 4, 64).transpose(2, 0, 1))
    m["w0T"] = f(inp["rwkv_w0"][0].reshape(2, 16, 64).transpose(2, 0, 1))
    m["a0T"] = f(inp["rwkv_a0"][0].reshape(2, 16, 64).transpose(2, 0, 1))
    m["w2"] = f(inp["rwkv_w2"][0])
    m["a2"] = f(inp["rwkv_a2"][0])
    m["kkT"] = f(inp["rwkv_k_k"][0].reshape(16, 64).T)
    m["kaT"] = f(inp["rwkv_k_a"][0].reshape(16, 64).T)
    m["rkT"] = f(inp["rwkv_r_k"][0].T)
    m["gngT"] = f(inp["rwkv_gn_g"][0].reshape(16, 64).T)
    m["gnbT"] = f(inp["rwkv_gn_b"][0].reshape(16, 64).T)
    s = np.arange(128)[:, None]
    t = np.arange(128)[None, :]
    same = (s // 32) == (t // 32)
    m["maskH"] = np.stack([(same & (s <= t)), (same & (s >= t))]).astype(np.float32)
    m["ident_in"] = np.eye(128, dtype=np.float32)
    s6 = np.arange(64)[:, None]
    t6 = np.arange(64)[None, :]
    mr_ = np.zeros((2, 3, 64, 128), np.float32)
    for d_, (st_, inc_) in enumerate([((s6 < t6), (s6 <= t6)), ((s6 > t6), (s6 >= t6))]):
        st_ = st_.astype(np.float32)
        inc_ = inc_.astype(np.float32)
        mr_[d_, 0, :, 0:64] = -st_
        mr_[d_, 0, :, 64:128] = -inc_
        mr_[d_, 1, :, 0:64] = st_
        mr_[d_, 1, :, 64:128] = inc_
        mr_[d_, 2, :, 0:64] = -(st_.T)
    m["maskR"] = mr_
    m["w_in_odd"] = f(inp["w_in_odd"][0])
    m["w_out_odd"] = f(inp["w_out_odd"][0])
    m["maskC"] = np.stack([(s <= t), (s >= t)]).astype(np.float32)
    sel = np.zeros((36, 4, 128), np.float32)
    gbt = np.zeros((36, 4), np.float32)
    for pb_ in (0, 32):
        for k_ in range(4):
            sel[pb_ + k_, k_, :] = 1.0
        gbt[pb_:pb_ + 4, :] = inp["mlstm_gate_b"][0].T
    m["sel_d"] = sel
    m["gbT_d"] = gbt
    m["cw_d"] = f(inp["mlstm_conv_w"][0].reshape(9, 32, 128).transpose(2, 1, 0))
    m["cb_d"] = f(inp["mlstm_conv_b"][0].reshape(32, 128).T)
    m["mng_d"] = f(inp["mlstm_norm_g"][0].reshape(16, 128).T)
    return m


def kernel(**inp):
    inp = {k: np.asarray(v) for k, v in inp.items()}
    nc, P = build()
    in_maps = [host_inputs(inp, c) for c in range(NCORES)]
    res = run_bass_kernel_spmd(nc, in_maps, core_ids=list(range(NCORES)))
    r = res.results
    y_prompt = np.zeros((16, TP, D), np.float32)
    y_sample = np.zeros((2, TS, D), np.float32)
    for c in range(NCORES):
        y_prompt[2 * c] = r[c]["yout"][TS:TS + TP]
        y_prompt[2 * c + 1] = r[c]["yout"][TS + TP:]
    for b in range(2):
        y_sample[b] = r[b]["yout"][0:TS]
    new_hgrn = np.concatenate([r[c]["o_hgrn"] for c in range(NCORES)], axis=0)[:, None]
    new_rwkv = np.concatenate([r[c]["o_rwkv"] for c in range(NCORES)], axis=0)[:, None]
    new_C = np.concatenate([r[c]["o_C"] for c in range(NCORES)], axis=0)[:, None]
    new_n = np.concatenate([r[c]["o_n"] for c in range(NCORES)], axis=0)[:, None]
    new_m = np.concatenate([r[c]["o_m"] for c in range(NCORES)], axis=0)[:, None]
    return (y_prompt, y_sample, new_hgrn.astype(np.float32), new_rwkv.astype(np.float32),
            new_C.astype(np.float32), new_n.astype(np.float32), new_m.astype(np.float32))
```

```python
import contextlib
import numpy as np
import concourse.bass as bass
import concourse.mybir as mybir
from concourse.bass_utils import run_bass_kernel_spmd

F32 = mybir.dt.float32
BF16 = mybir.dt.bfloat16
ALU = mybir.AluOpType
AF = mybir.ActivationFunctionType
AX = mybir.AxisListType

D = 1024
TS = 2048
TP = 256
TT = TS + 2 * TP
NCORES = 8


class _Rec:
    def __init__(self):
        self.calls = []

    def __getattr__(self, name):
        def f(*a, **k):
            self.calls.append((name, a, k))
            return self
        return f


class Prog:
    ENGS = ['pe', 'dve', 'act', 'pool', 'sp']
    NDMA = 16

    def __init__(self, nc):
        self.nc = nc
        self.ops = {e: [] for e in self.ENGS}
        self.cnt = {}
        self.waited = {e: {} for e in self.ENGS}
        self.last_write = {}
        self.readers = {}
        self.dma_rr = 0
        self.sem_names = list(self.ENGS) + ['d%d' % i for i in range(self.NDMA)]
        for s in self.sem_names:
            self.cnt[s] = 0
        self.n_ops = 0

    def _deps(self, eng, reads, writes):
        deps = {}

        def add(p):
            if p is None:
                return
            f, n = p
            if f == 'pe' and eng == 'pe':
                return
            if n > deps.get(f, 0):
                deps[f] = n
        for k in reads:
            add(self.last_write.get(k))
        for k in writes:
            add(self.last_write.get(k))
            for p in self.readers.get(k, ()):
                add(p)
        waits = []
        for f, n in deps.items():
            if n > self.waited[eng].get(f, 0):
                waits.append((f, n))
                self.waited[eng][f] = n
        return waits

    def _commit(self, tag, reads, writes):
        for k in reads:
            lst = self.readers.setdefault(k, [])
            lst[:] = [p for p in lst if p[0] != tag[0]]
            lst.append(tag)
        for k in writes:
            self.last_write[k] = tag
            self.readers[k] = []

    def op(self, eng, fn, reads=(), writes=()):
        rec = _Rec()
        fn(rec)
        name, a, k = rec.calls[0]
        fn = (lambda e, name=name, a=a, k=k: getattr(e, name)(*a, **k))
        waits = self._deps(eng, reads, writes)
        self.cnt[eng] += 1
        tag = (eng, self.cnt[eng])
        self.ops[eng].append((waits, fn, eng, 1))
        self._commit(tag, reads, writes)
        self.n_ops += 1

    def dma(self, out, in_, reads=(), writes=(), q='sp', **kw):
        d = 'd%d' % self.dma_rr
        self.dma_rr = (self.dma_rr + 1) % self.NDMA
        waits = self._deps(q, reads, writes)
        prev = self.cnt[d]
        if prev > self.waited[q].get(d, 0):
            waits.append((d, prev))
            self.waited[q][d] = prev
        self.cnt[d] += 16
        tag = (d, self.cnt[d])
        self.ops[q].append((waits, (lambda e: e.dma_start(out=out, in_=in_, **kw)), d, 16))
        self._commit(tag, reads, writes)
        self.n_ops += 1

    def barrier(self):
        allsems = list(self.sem_names)
        for e in self.ENGS:
            waits = []
            for f in allsems:
                if self.cnt[f] > self.waited[e].get(f, 0):
                    waits.append((f, self.cnt[f]))
                    self.waited[e][f] = self.cnt[f]
            self.ops[e].append((waits, None, None, 0))

    def finish(self, q='sp'):
        waits = []
        for i in range(self.NDMA):
            d = 'd%d' % i
            if self.cnt[d] > self.waited[q].get(d, 0):
                waits.append((d, self.cnt[d]))
                self.waited[q][d] = self.cnt[d]
        self.ops[q].append((waits, None, None, 0))

    def emit(self, sems):
        ops = self.ops

        def run(e, lst):
            for waits, fn, semname, inc in lst:
                for f, n in waits:
                    e.wait_ge(sems[f], n)
                if fn is not None:
                    fn(e).then_inc(sems[semname], inc)
        with self.nc.Block() as block:
            @block.tensor
            def _(e):
                run(e, ops['pe'])

            @block.vector
            def _(e):
                run(e, ops['dve'])

            @block.scalar
            def _(e):
                run(e, ops['act'])

            @block.gpsimd
            def _(e):
                run(e, ops['pool'])

            @block.sync
            def _(e):
                run(e, ops['sp'])


SEQS = [(0, TS, 0, True), (TS, TP, 1, False), (TS + TP, TP, 1, False)]


def build(debug=None):
    nc = bass.Bass('TRN2', target_bir_lowering=False)
    P = Prog(nc)
    es = contextlib.ExitStack()

    def din(name, shape):
        return nc.dram_tensor(name, list(shape), F32, kind="ExternalInput").ap()

    def dout(name, shape):
        return nc.dram_tensor(name, list(shape), F32, kind="ExternalOutput").ap()

    xin = din("xin", [TT, D])
    condT = din("condT", [128, 8, 2])
    s_hgrn = din("s_hgrn", [2, 8, 128, 128])
    s_rwkv = din("s_rwkv", [2, 16, 64, 64])
    s_C = din("s_C", [2, 4, 512, 512])
    s_n = din("s_n", [2, 4, 512])
    s_m = din("s_m", [2, 4])
    w_mod = din("w_mod", [2, D, 3 * D])
    b_modT = din("b_modT", [128, 2, 24])
    norm_gT = din("norm_gT", [128, 2, 8])
    fnorm_gT = din("fnorm_gT", [128, 8])
    wA = din("wA", [8, D, 640])
    wB = din("wB", [16, D, 256])
    wLR = din("wLR", [D, 256])
    w_out_even = din("w_out_even", [2 * D, D])
    lbT = din("lbT", [128, 2, 8])
    hg_gT = din("hg_gT", [128, 8])
    mu_rkv = din("mu_rkv", [64, 2, 4, 16])
    mu_lr = din("mu_lr", [64, 2, 4])
    w0T = din("w0T", [64, 2, 16])
    a0T = din("a0T", [64, 2, 16])
    w2 = din("w2", [2, 64, D])
    a2 = din("a2", [2, 64, D])
    kkT = din("kkT", [64, 16])
    kaT = din("kaT", [64, 16])
    rkT = din("rkT", [64, 16])
    gngT = din("gngT", [64, 16])
    gnbT = din("gnbT", [64, 16])
    maskR = din("maskR", [2, 3, 64, 128])
    maskH = din("maskH", [2, 128, 128])
    ident_d = din("ident_in", [128, 128])
    w_in_odd = din("w_in_odd", [D, 10256])
    w_out_odd = din("w_out_odd", [2 * D, D])
    maskC = din("maskC", [2, 128, 128])
    sel_d = din("sel_d", [36, 4, 128])
    gbT_d = din("gbT_d", [36, 4])
    cw_d = din("cw_d", [128, 32, 9])
    cb_d = din("cb_d", [128, 32])
    mng_d = din("mng_d", [128, 16])

    yout = dout("yout", [TT, D])
    o_hgrn = dout("o_hgrn", [2, 2, 8, 128, 128])
    o_rwkv = dout("o_rwkv", [2, 2, 16, 64, 64])
    o_C = dout("o_C", [2, 2, 4, 512, 512])
    o_n = dout("o_n", [2, 2, 4, 512])
    o_m = dout("o_m", [2, 2, 4])
    dbg = dout("dbg", [40, 128, TT]) if debug else None
    dslot = {}
    dumpt = {}
    x1 = dout("x1", [TT, D]) if debug else nc.dram_tensor("x1", [TT, D], F32, kind="Internal").ap()

    def sb(name, shape, dt=F32):
        return es.enter_context(nc.sbuf_tensor(name, list(shape), dt))

    pstiles = [es.enter_context(nc.psum_tensor("ps%d" % i, [128, 512], F32)) for i in range(8)]
    psrr = [0]

    def nps():
        i = psrr[0]
        psrr[0] = (i + 1) % 8
        return pstiles[i], 'ps%d' % i

    def dump(name, ap, keys, n, col0=0, parts=128):
        if not debug:
            return
        slot = dslot.setdefault(name, len(dslot))
        dt_ = dumpt['tile']
        for c0 in range(0, n, 512):
            w_ = min(512, n - c0)
            P.op('pool', (lambda e, c0=c0, w_=w_: e.tensor_copy(out=dt_[0:parts, 0:w_], in_=ap[:, c0:c0 + w_])),
                 reads=keys, writes=['dumpt'])
            P.dma(dbg[slot, 0:parts, col0 + c0:col0 + c0 + w_], dt_[0:parts, 0:w_], reads=['dumpt'])

    if debug:
        dumpt['tile'] = sb("dumpt", [128, 512])

    ident = sb("ident", [128, 128])
    ones = sb("ones", [128, 128])
    P.dma(ident[:], ident_d[:], writes=['ident'])
    P.op('dve', lambda e: e.memset(ones[:], 1.0), writes=['ones'])

    condT_sb = sb("condT_sb", [128, 8, 2])
    bmod_sb = sb("bmod_sb", [128, 2, 24])
    ng_sb = sb("ng_sb", [128, 2, 8])
    fng_sb = sb("fng_sb", [128, 8])
    lb_sb = sb("lb_sb", [128, 2, 8])
    hgg_sb = sb("hgg_sb", [128, 8])
    for t_, d_, k_ in [(condT_sb, condT, 'condT'), (bmod_sb, b_modT, 'bmod'), (ng_sb, norm_gT, 'ng'),
                       (fng_sb, fnorm_gT, 'fng'), (lb_sb, lbT, 'lb'), (hgg_sb, hg_gT, 'hgg')]:
        P.dma(t_[:], d_[:], writes=[k_])

    scT = sb("scT", [128, 8, 2])
    P.op('act', lambda e: e.activation(out=scT[:], in_=condT_sb[:], func=AF.Silu), reads=['condT'], writes=['scT'])
    mT = sb("mT", [128, 2, 24, 2])
    sc1 = sb("sc1", [128, 2, 8, 2])
    gate_bc = sb("gate_bc", [128, D])
    dg = sb("dg", [128, 128])

    def make_gate(l, c):
        if True:
            for half in range(2):
                pz, pk = nps()
                for kq in range(4):
                    kc = half * 4 + kq
                    P.op('dve', lambda e: e.tensor_scalar(
                        out=dg[:], in0=ident[:], scalar1=mT[:, l, 16 + kc, c:c + 1], scalar2=None, op0=ALU.mult),
                        reads=['ident', 'mT'], writes=['dg'])
                    P.op('pe', lambda e: e.matmul(pz[:, kq * 128:(kq + 1) * 128], ones[:], dg[:], start=True, stop=True),
                         reads=['ones', 'dg'], writes=[pk])
                P.op('act', lambda e: e.copy(out=gate_bc[:, half * 512:(half + 1) * 512], in_=pz[:]),
                     reads=[pk], writes=['gate_bc'])

    with contextlib.ExitStack() as es2:
        wm = [es2.enter_context(nc.sbuf_tensor("wm%d" % i, [128, 8, 512], F32)) for i in range(2)]
        for l in range(2):
            for cbk in range(6):
                i = (l * 6 + cbk) % 2
                P.dma(wm[i][:], w_mod[l].rearrange("(kc p) n -> p kc n", p=128)[:, :, cbk * 512:(cbk + 1) * 512],
                      writes=['wm%d' % i])
                pz, pk = nps()
                for j in range(4):
                    for kc in range(8):
                        P.op('pe', lambda e: e.matmul(pz[:, j * 2:(j + 1) * 2], wm[i][:, kc, j * 128:(j + 1) * 128],
                                                      scT[:, kc, :], start=(kc == 0), stop=(kc == 7)),
                             reads=['wm%d' % i, 'scT'], writes=[pk])
                P.op('dve', lambda e: e.tensor_tensor(out=mT[:, l, cbk * 4:(cbk + 1) * 4, :],
                                                      in0=pz[:, 0:8].rearrange("p (j c) -> p j c", c=2),
                                                      in1=bmod_sb[:, l, cbk * 4:(cbk + 1) * 4].unsqueeze(2).to_broadcast([128, 4, 2]),
                                                      op=ALU.add),
                     reads=[pk, 'bmod'], writes=['mT'])
    P.barrier()
    for l in range(2):
        P.op('dve', lambda e: e.scalar_tensor_tensor(
            out=sc1[:, l], in0=mT[:, l, 8:16, :], scalar=1.0,
            in1=ng_sb[:, l, :].unsqueeze(2).to_broadcast([128, 8, 2]), op0=ALU.add, op1=ALU.mult),
            reads=['mT', 'ng'], writes=['sc1'])
    fng_holder = {}

    def make_fng():
        fng_bc = sb("fng_bc", [128, D])
        fng_holder['t'] = fng_bc
        for half in range(2):
            pz, pk = nps()
            for kq in range(4):
                kc = half * 4 + kq
                P.op('dve', (lambda e, kc=kc: e.tensor_scalar(
                    out=dg[:], in0=ident[:], scalar1=fng_sb[:, kc:kc + 1], scalar2=None, op0=ALU.mult)),
                    reads=['ident', 'fng'], writes=['dg'])
                P.op('pe', (lambda e, pz=pz, kq=kq: e.matmul(pz[:, kq * 128:(kq + 1) * 128], ones[:], dg[:],
                                                             start=True, stop=True)),
                     reads=['ones', 'dg'], writes=[pk])
            P.op('act', (lambda e, half=half, pz=pz: e.copy(out=fng_bc[:, half * 512:(half + 1) * 512], in_=pz[:])),
                 reads=[pk], writes=['fng_bc'])

    lbv = sb("lbv", [128, 8])
    oml = sb("oml", [128, 8])
    P.op('dve', lambda e: e.tensor_tensor(out=lbv[:], in0=lb_sb[:, 0, :], in1=lb_sb[:, 1, :], op=ALU.subtract),
         reads=['lb'], writes=['lbv'])
    P.op('act', lambda e: e.activation(out=lbv[:], in_=lbv[:], func=AF.Sigmoid), reads=['lbv'], writes=['lbv'])
    P.op('act', lambda e: e.activation(out=oml[:], in_=lbv[:], func=AF.Identity, bias=1.0, scale=-1.0),
         reads=['lbv'], writes=['oml'])

    hT = sb("hT", [128, 8, TS], BF16)
    yT = sb("yT", [128, 8, TS], BF16)
    st4 = sb("st4", [128, 4])
    wst = sb("wst", [128, 8, 256])
    FT = [sb("FT%d" % i, [128, TS + 32]) for i in range(6)]
    xt = [FT[0][:, 0:D], FT[0][:, D:2 * D]]
    xn = FT[1][:, 0:D]
    junk = FT[1][:, D:2 * D]
    wo_v = [FT[2][:, 0:TS].bitcast(BF16).rearrange("p (s n) -> p s n", n=D),
            FT[3][:, 0:TS].bitcast(BF16).rearrange("p (s n) -> p s n", n=D)]

    def load_wo(src, parts, nk=8):
        v = src.rearrange("(kc p) n -> p kc n", p=parts)
        for c0 in range(0, D, 256):
            w_ = min(256, D - c0)
            P.dma(wst[0:parts, 0:nk, 0:w_], v[:, :, c0:c0 + w_], writes=['wst'])
            for hf in range(nk // 4):
                P.op('pool', lambda e: e.tensor_copy(out=wo_v[hf][0:parts, :, c0:c0 + w_], in_=wst[0:parts, hf * 4:hf * 4 + 4, 0:w_]),
                     reads=['wst'], writes=['wo_bf'])

    def make_hT(layer, xsrc, xkey, off, T, cidx):
        for tt in range(T // 128):
            i = tt % 2
            P.dma(xt[i], xsrc[off + tt * 128: off + (tt + 1) * 128, :], reads=[(xkey, off // 128 + tt)], writes=['xt%d' % i])
            P.op('act', lambda e: e.activation(out=junk, in_=xt[i], func=AF.Square, accum_out=st4[:, 0:1]),
                 reads=['xt%d' % i], writes=['junk', 'st4'])
            P.op('dve', lambda e: e.tensor_scalar(out=st4[:, 1:2], in0=st4[:, 0:1], scalar1=1.0 / D, scalar2=1e-6,
                                                  op0=ALU.mult, op1=ALU.add), reads=['st4'], writes=['st4'])
            P.op('act', lambda e: e.activation(out=st4[:, 2:3], in_=st4[:, 1:2], func=AF.Sqrt), reads=['st4'], writes=['st4'])
            P.op('dve', lambda e: e.reciprocal(out=st4[:, 3:4], in_=st4[:, 2:3]), reads=['st4'], writes=['st4'])
            P.op('dve', lambda e: e.tensor_scalar(out=xn, in0=xt[i], scalar1=st4[:, 3:4], scalar2=None, op0=ALU.mult),
                 reads=['xt%d' % i, 'st4'], writes=['xn'])
            for half in range(2):
                pz, pk = nps()
                for kq in range(4):
                    kc = half * 4 + kq
                    P.op('pe', lambda e: e.transpose(out=pz[:, kq * 128:(kq + 1) * 128], in_=xn[:, kc * 128:(kc + 1) * 128],
                                                     identity=ident[:]), reads=['xn', 'ident'], writes=[pk])
                for kq in range(4):
                    kc = half * 4 + kq
                    P.op('act', lambda e: e.activation(
                        out=hT[:, kc, tt * 128:(tt + 1) * 128], in_=pz[:, kq * 128:(kq + 1) * 128], func=AF.Identity,
                        bias=mT[:, layer, kc, cidx:cidx + 1], scale=sc1[:, layer, kc, cidx:cidx + 1]),
                        reads=[pk, 'mT', 'sc1'], writes=[('hT', tt)])

    def hT_keys(t0, t1):
        return [('hT', tt) for tt in range(t0 // 128, (t1 + 127) // 128)]

    def load_w(src, ncols, dst=None, dkey='wbf', parts=128, nk=8):
        dst = wbf if dst is None else dst
        v = src.rearrange("(kc p) n -> p kc n", p=parts)
        for c0 in range(0, ncols, 256):
            w_ = min(256, ncols - c0)
            P.dma(wst[0:parts, 0:nk, 0:w_], v[:, :, c0:c0 + w_], writes=['wst'])
            P.op('pool', lambda e: e.tensor_copy(out=dst[0:parts, 0:nk, c0:c0 + w_], in_=wst[0:parts, 0:nk, 0:w_]),
                 reads=['wst'], writes=[dkey])

    def proj(c0, M, t0, n, evac):
        pz, pk = nps()
        for kc in range(8):
            P.op('pe', lambda e: e.matmul(pz[0:M, 0:n], wbf[:, kc, c0:c0 + M], hT[:, kc, t0:t0 + n],
                                          start=(kc == 0), stop=(kc == 7)),
                 reads=['wbf'] + hT_keys(t0, t0 + n), writes=[pk])
        evac(pz, pk)

    def outproj(layer, groups, xsrc, skey, xdst, dkey, off, T, cidx, final=False):
        for tt in range(T // 128):
            i = tt % 2
            P.dma(xt[i], xsrc[off + tt * 128: off + (tt + 1) * 128, :], reads=[(skey, off // 128 + tt)], writes=['xt%d' % i])
            for half in range(2):
                pz, pk = nps()
                for gi, (K, slot) in enumerate(groups):
                    P.op('pe', lambda e: e.matmul(pz[:, 0:512], yT[0:K, slot, tt * 128:(tt + 1) * 128],
                                                  wo_v[slot // 4][0:K, slot % 4, half * 512:(half + 1) * 512],
                                                  start=(gi == 0), stop=(gi == len(groups) - 1)),
                         reads=[('yT', slot), 'wo_bf'], writes=[pk])
                P.op('dve', lambda e: e.tensor_tensor(out=xn[:, half * 512:(half + 1) * 512], in0=pz[:, 0:512],
                                                      in1=gate_bc[:, half * 512:(half + 1) * 512], op=ALU.mult),
                     reads=[pk, 'gate_bc'], writes=['xn'])
                P.op('pool', lambda e: e.tensor_tensor(out=xt[i][:, half * 512:(half + 1) * 512],
                                                       in0=xt[i][:, half * 512:(half + 1) * 512],
                                                       in1=xn[:, half * 512:(half + 1) * 512], op=ALU.add),
                     reads=['xn', 'xt%d' % i], writes=['xt%d' % i])
            if final:
                P.op('act', lambda e: e.activation(out=junk, in_=xt[i], func=AF.Square, accum_out=st4[:, 0:1]),
                     reads=['xt%d' % i], writes=['junk', 'st4'])
                P.op('dve', lambda e: e.tensor_scalar(out=st4[:, 1:2], in0=st4[:, 0:1], scalar1=1.0 / D, scalar2=1e-6,
                                                      op0=ALU.mult, op1=ALU.add), reads=['st4'], writes=['st4'])
                P.op('act', lambda e: e.activation(out=st4[:, 2:3], in_=st4[:, 1:2], func=AF.Sqrt), reads=['st4'], writes=['st4'])
                P.op('dve', lambda e: e.reciprocal(out=st4[:, 3:4], in_=st4[:, 2:3]), reads=['st4'], writes=['st4'])
                P.op('dve', lambda e: e.scalar_tensor_tensor(out=xt[i], in0=xt[i], scalar=st4[:, 3:4], in1=fng_holder['t'][:],
                                                             op0=ALU.mult, op1=ALU.mult),
                     reads=['xt%d' % i, 'st4', 'fng_bc'], writes=['xt%d' % i])
            P.dma(xdst[off + tt * 128: off + (tt + 1) * 128, :], xt[i], reads=['xt%d' % i], writes=[(dkey, off // 128 + tt)], q='pool')

    L0 = contextlib.ExitStack()

    def sb0(name, shape, dt=F32):
        return L0.enter_context(nc.sbuf_tensor(name, list(shape), dt))

    wbf = sb0("wbf", [128, 8, 384], BF16)
    TB = 256
    Fq, Fsz, Fvr, For = FT[0][:, 0:TS], FT[1][:, 0:TS], FT[2][:, 0:TS], FT[3][:, 0:TS]
    Fv = Fvr.rearrange("p (j c) -> p j c", c=128)
    Fo = For.rearrange("p (j c) -> p j c", c=128)
    BT = [sb0("BT%d" % i, [128, 256]) for i in range(18)]
    bt = {n_: BT[i] for i, n_ in enumerate(['sg', 'lf', 'kg', 'G', 'br', 'E', 'Ei', 'qt', 'kt', 'kh', 'vT'])}
    khtok = sb0("khtok", [128, TB // 128, 128])
    gam = sb0("gam", [128, TB // 32])
    gref = sb0("gref", [128, TB // 32])
    Sst = [sb0("Sst%d" % i, [128, 128]) for i in range(2)]
    attT = sb0("attT", [128, 128])
    ostat = sb0("ostat", [128, TS // 128, 4])
    mH = sb0("mH", [128, 2, 128])
    P.dma(mH[:], maskH.rearrange("d s t -> s d t"), writes=['mH'])

    def hgrn_head(h, off, T, is_sample, pidx):
        load_w(wA[h][:, 0:384], 384)
        tb = min(TB, T)
        nblk = T // tb
        for b in range(nblk):
            t0 = b * tb
            proj(0, 128, t0, tb, lambda pz, pk: P.op(
                'act', lambda e: e.copy(out=Fq[:, t0:t0 + tb], in_=pz[:, 0:tb]), reads=[pk], writes=['Fq']))
            proj(256, 128, t0, tb, lambda pz, pk: P.op(
                'act', lambda e: e.activation(out=Fsz[:, t0:t0 + tb], in_=pz[:, 0:tb], func=AF.Silu), reads=[pk], writes=['Fsz']))
            proj(128, 128, t0, tb, lambda pz, pk: P.op(
                'dve', lambda e: e.tensor_copy(out=bt['vT'][:, 0:tb], in_=pz[:, 0:tb]), reads=[pk], writes=['b_vT']))
            pz, pk = nps()
            for j in range(tb // 128):
                P.op('pe', lambda e: e.transpose(out=pz[:, j * 128:(j + 1) * 128], in_=bt['vT'][:, j * 128:(j + 1) * 128],
                                                 identity=ident[:]), reads=['b_vT', 'ident'], writes=[pk])
            P.op('dve', lambda e: e.tensor_copy(out=Fv[:, t0 // 128:(t0 + tb) // 128, :],
                                                in_=pz[:, 0:tb].rearrange("p (j c) -> p j c", c=128)),
                 reads=[pk], writes=['Fv'])
        load_w(wA[h][:, 384:640], 256)
        for d in range(2):
            rev = (d == 1)
            cur = 0
            if is_sample:
                P.dma(Sst[0][:], s_hgrn[d, h], writes=['Sst0'])
            else:
                P.op('pool', lambda e: e.memset(Sst[0][:], 0.0), writes=['Sst0'])
            blks = list(range(nblk))
            if rev:
                blks = blks[::-1]
            for b in blks:
                t0 = b * tb
                nch = tb // 32
                sg, lf, kg, G, br, E, Ei, qt, kt, kh = [bt[n_] for n_ in ['sg', 'lf', 'kg', 'G', 'br', 'E', 'Ei', 'qt', 'kt', 'kh']]
                proj(128 * d, 128, t0, tb, lambda pz, pk: P.op(
                    'act', lambda e: e.activation(out=sg[:, 0:tb], in_=pz[:, 0:tb], func=AF.Sigmoid), reads=[pk], writes=['b_sg']))
                P.op('dve', lambda e: e.tensor_scalar(out=sg[:, 0:tb], in0=sg[:, 0:tb], scalar1=oml[:, h:h + 1],
                                                      scalar2=lbv[:, h:h + 1], op0=ALU.mult, op1=ALU.add),
                     reads=['b_sg', 'oml', 'lbv'], writes=['b_sg'])
                P.op('act', lambda e: e.activation(out=lf[:, 0:tb], in_=sg[:, 0:tb], func=AF.Ln), reads=['b_sg'], writes=['b_lf'])
                P.op('pool', lambda e: e.tensor_scalar(out=kg[:, 0:tb], in0=sg[:, 0:tb], scalar1=-1.0, scalar2=1.0,
                                                       op0=ALU.mult, op1=ALU.add), reads=['b_sg'], writes=['b_kg'])
                P.op('dve', lambda e: e.memset(E[:, 0:tb], 0.0), writes=['b_E'])
                if not rev:
                    P.op('dve', lambda e: e.tensor_tensor_scan(out=G[:, 0:tb], data0=lf[:, 0:tb], data1=E[:, 0:tb],
                                                               initial=0.0, op0=ALU.add, op1=ALU.add),
                         reads=['b_lf', 'b_E'], writes=['b_G'])
                    ci_ = 0
                else:
                    P.op('dve', lambda e: e.tensor_tensor_scan(out=G[:, 0:tb][:, ::-1], data0=lf[:, 0:tb][:, ::-1],
                                                               data1=E[:, 0:tb], initial=0.0, op0=ALU.add, op1=ALU.add),
                         reads=['b_lf', 'b_E'], writes=['b_G'])
                    ci_ = 31
                G3 = G[:, 0:tb].rearrange("p (c l) -> p c l", l=32)
                lf3 = lf[:, 0:tb].rearrange("p (c l) -> p c l", l=32)
                P.op('dve', lambda e: e.tensor_tensor(out=gref[:, 0:nch], in0=G3[:, :, ci_], in1=lf3[:, :, ci_], op=ALU.subtract),
                     reads=['b_G', 'b_lf'], writes=['gref'])
                P.op('dve', lambda e: e.tensor_tensor(out=br[:, 0:tb].rearrange("p (c l) -> p c l", l=32), in0=G3,
                                                      in1=gref[:, 0:nch].unsqueeze(2).to_broadcast([128, nch, 32]), op=ALU.subtract),
                     reads=['b_G', 'gref'], writes=['b_br'])
                bend = br[:, 0:tb].rearrange("p (c l) -> p c l", l=32)[:, :, (0 if rev else 31)]
                P.op('act', lambda e: e.activation(out=gam[:, 0:nch], in_=bend, func=AF.Exp), reads=['b_br'], writes=['gam'])
                P.op('act', lambda e: e.activation(out=E[:, 0:tb], in_=br[:, 0:tb], func=AF.Exp), reads=['b_br'], writes=['b_E'])
                P.op('act', lambda e: e.activation(out=Ei[:, 0:tb], in_=br[:, 0:tb], func=AF.Exp, scale=-1.0),
                     reads=['b_br'], writes=['b_Ei'])
                P.op('dve', lambda e: e.tensor_tensor(out=qt[:, 0:tb], in0=Fq[:, t0:t0 + tb], in1=E[:, 0:tb], op=ALU.mult),
                     reads=['Fq', 'b_E'], writes=['b_qt'])
                P.op('pool', lambda e: e.tensor_tensor(out=kt[:, 0:tb], in0=kg[:, 0:tb], in1=Ei[:, 0:tb], op=ALU.mult),
                     reads=['b_kg', 'b_Ei'], writes=['b_kt'])
                P.op('dve', lambda e: e.tensor_tensor(out=kh[:, 0:tb].rearrange("p (c l) -> p c l", l=32),
                                                      in0=kt[:, 0:tb].rearrange("p (c l) -> p c l", l=32),
                                                      in1=gam[:, 0:nch].unsqueeze(2).to_broadcast([128, nch, 32]), op=ALU.mult),
                     reads=['b_kt', 'gam'], writes=['b_kh'])
                if debug and debug.get('inner') and h == head_ids[0]:
                    for n_ in ['lf', 'kg', 'br', 'E', 'qt', 'kt', 'kh']:
                        dump("%s_d%d" % (n_, d), bt[n_][:, 0:tb], ['b_' + n_], tb, col0=off + t0)
                pz, pk = nps()
                for j in range(tb // 128):
                    P.op('pe', lambda e: e.transpose(out=pz[:, j * 128:(j + 1) * 128], in_=kh[:, j * 128:(j + 1) * 128],
                                                     identity=ident[:]), reads=['b_kh', 'ident'], writes=[pk])
                P.op('act', lambda e: e.copy(out=khtok[:, 0:tb // 128, :], in_=pz[:, 0:tb].rearrange("p (j c) -> p j c", c=128)),
                     reads=[pk], writes=['khtok'])
                tiles = list(range(tb // 128))
                if rev:
                    tiles = tiles[::-1]
                for j in tiles:
                    tg = t0 // 128 + j
                    pa, pak = nps()
                    P.op('pe', lambda e: e.matmul(pa[:, 0:128], kt[:, j * 128:(j + 1) * 128], qt[:, j * 128:(j + 1) * 128],
                                                  start=True, stop=True), reads=['b_kt', 'b_qt'], writes=[pak])
                    P.op('dve', lambda e: e.tensor_tensor(out=attT[:], in0=pa[:, 0:128], in1=mH[:, d, :], op=ALU.mult),
                         reads=[pak, 'mH'], writes=['attT'])
                    po, pok = nps()
                    P.op('pe', lambda e: e.matmul(po[:, 0:128], attT[:], Fv[:, tg, :], start=True, stop=False),
                         reads=['attT', 'Fv'], writes=[pok])
                    chs = [0, 1, 2, 3]
                    if rev:
                        chs = chs[::-1]
                    for ci, c in enumerate(chs):
                        Scur = Sst[cur]
                        Snew = Sst[1 - cur]
                        P.op('pe', lambda e: e.matmul(
                            po[32 * c:32 * c + 32, 0:128], qt[:, j * 128 + 32 * c: j * 128 + 32 * c + 32], Scur[:],
                            start=False, stop=(ci == 3), tile_position=(0, 32 * c)),
                            reads=['b_qt', 'Sst%d' % cur], writes=[pok])
                        pd, pdk = nps()
                        P.op('pe', lambda e: e.matmul(
                            pd[:, 0:128], khtok[32 * c:32 * c + 32, j, :], Fv[32 * c:32 * c + 32, tg, :],
                            start=True, stop=True, tile_position=(32 * c, 0)),
                            reads=['khtok', 'Fv'], writes=[pdk])
                        gidx = j * 4 + c
                        P.op('dve', lambda e: e.scalar_tensor_tensor(
                            out=Snew[:], in0=Scur[:], scalar=gam[:, gidx:gidx + 1], in1=pd[:, 0:128],
                            op0=ALU.mult, op1=ALU.add),
                            reads=['Sst%d' % cur, 'gam', pdk], writes=['Sst%d' % (1 - cur)])
                        cur = 1 - cur
                    if d == 0:
                        P.op('act', lambda e: e.copy(out=Fo[:, tg, :], in_=po[:, 0:128]), reads=[pok], writes=[('Fo', tg)])
                    else:
                        P.op('dve', lambda e: e.tensor_tensor(out=Fo[:, tg, :], in0=Fo[:, tg, :], in1=po[:, 0:128], op=ALU.add),
                             reads=[pok, ('Fo', tg)], writes=[('Fo', tg)])
            if not is_sample:
                P.dma(o_hgrn[pidx, d, h], Sst[cur][:], reads=['Sst%d' % cur], q='pool')
        for tg in range(T // 128):
            P.op('act', lambda e: e.activation(out=attT[:], in_=Fo[:, tg, :], func=AF.Square, accum_out=ostat[:, tg, 0:1]),
                 reads=[('Fo', tg)], writes=['attT', ('ostat', tg)])
            P.op('dve', lambda e: e.tensor_scalar(out=ostat[:, tg, 1:2], in0=ostat[:, tg, 0:1], scalar1=1.0 / 128,
                                                  scalar2=1e-6, op0=ALU.mult, op1=ALU.add),
                 reads=[('ostat', tg)], writes=[('ostat', tg)])
            P.op('act', lambda e: e.activation(out=ostat[:, tg, 2:3], in_=ostat[:, tg, 1:2], func=AF.Sqrt),
                 reads=[('ostat', tg)], writes=[('ostat', tg)])
            P.op('dve', lambda e: e.reciprocal(out=ostat[:, tg, 3:4], in_=ostat[:, tg, 2:3]),
                 reads=[('ostat', tg)], writes=[('ostat', tg)])
            P.op('dve', lambda e: e.tensor_scalar(out=Fo[:, tg, :], in0=Fo[:, tg, :], scalar1=ostat[:, tg, 3:4],
                                                  scalar2=None, op0=ALU.mult),
                 reads=[('Fo', tg), ('ostat', tg)], writes=[('Fo', tg)])
        n4 = min(4, T // 128)
        for g4 in range(T // (128 * n4)):
            pz, pk = nps()
            for j in range(n4):
                tg = g4 * n4 + j
                P.op('pe', lambda e: e.transpose(out=pz[:, j * 128:(j + 1) * 128], in_=Fo[:, tg, :], identity=ident[:]),
                     reads=[('Fo', tg), 'ident'], writes=[pk])
            w_ = n4 * 128
            P.op('dve', lambda e: e.scalar_tensor_tensor(
                out=yT[:, h, g4 * w_:(g4 + 1) * w_], in0=pz[:, 0:w_], scalar=hgg_sb[:, h:h + 1],
                in1=Fsz[:, g4 * w_:(g4 + 1) * w_], op0=ALU.mult, op1=ALU.mult),
                reads=[pk, 'hgg', 'Fsz'], writes=[('yT', h)])

    TR = 256
    LR = [sb0("LR%d" % g, [64, TS], BF16) for g in range(4)]
    rb = {n_: BT[i][0:64, :] for i, n_ in enumerate(
          ['lw', 'a', 'kk', 'kq', 'kap', 'kd', 'b', 'rk', 'G', 'br', 'E', 'Ei', 'Em', 'bh', 'kh', 'Kb', 'Bb', 't1'])}
    KR = sb0("r_KR", [64, 2, TR])
    cset = [{n_: sb0("c%d_%s" % (i_, n_), [64, (128 if n_ in ('AB', 'BB') else 64)])
             for n_ in ['AB', 'BB', 'XT0', 'XT1', 'X1', 'Xw', 'Pm0', 'Pm1', 'Vt', 'Kt', 'Bt']} for i_ in range(4)]
    rsq = {n_: sb0("rq_" + n_, [64, 64]) for n_ in ['U', 'Z0', 'Z1', 'zt']}
    rgam = sb0("rgam", [64, 8])
    rgref = sb0("rgref", [64, 4])
    mR = sb0("mR", [64, 2, 3, 128])
    P.dma(mR[:], maskR.rearrange("d m s t -> s d m t"), writes=['mR'])
    prm = {}
    for n_, src_, shp in [('mu_rkv', mu_rkv, [64, 2, 4, 16]), ('mu_lr', mu_lr, [64, 2, 4]), ('w0', w0T, [64, 2, 16]),
                          ('a0', a0T, [64, 2, 16]), ('kk', kkT, [64, 16]), ('ka', kaT, [64, 16]), ('rk', rkT, [64, 16]),
                          ('gng', gngT, [64, 16]), ('gnb', gnbT, [64, 16])]:
        prm[n_] = sb0("p_" + n_, shp)
        P.dma(prm[n_][:], src_[:], writes=['p_' + n_])
    c0_rkv = sb0("c0_rkv", [64, 4, 16])
    c0_lr = sb0("c0_lr", [64, 4])
    omka = sb0("omka", [64, 16])
    P.op('dve', lambda e: e.tensor_tensor(out=c0_rkv[:], in0=prm['mu_rkv'][:, 0], in1=prm['mu_rkv'][:, 1], op=ALU.add),
         reads=['p_mu_rkv'], writes=['c0_rkv'])
    P.op('dve', lambda e: e.tensor_scalar(out=c0_rkv[:], in0=c0_rkv[:], scalar1=-1.0, scalar2=1.0, op0=ALU.mult, op1=ALU.add),
         reads=['c0_rkv'], writes=['c0_rkv'])
    P.op('dve', lambda e: e.tensor_tensor(out=c0_lr[:], in0=prm['mu_lr'][:, 0], in1=prm['mu_lr'][:, 1], op=ALU.add),
         reads=['p_mu_lr'], writes=['c0_lr'])
    P.op('dve', lambda e: e.tensor_scalar(out=c0_lr[:], in0=c0_lr[:], scalar1=-1.0, scalar2=1.0, op0=ALU.mult, op1=ALU.add),
         reads=['c0_lr'], writes=['c0_lr'])
    P.op('dve', lambda e: e.tensor_scalar(out=omka[:], in0=prm['ka'][:], scalar1=-1.0, scalar2=1.0, op0=ALU.mult, op1=ALU.add),
         reads=['p_ka'], writes=['omka'])
    w2a2 = sb0("w2a2", [64, 4, D], BF16)
    for g, src_ in enumerate([w2[0], w2[1], a2[0], a2[1]]):
        for c0 in range(0, D, 256):
            P.dma(wst[0:64, 0, 0:256], src_[:, c0:c0 + 256], writes=['wst'])
            P.op('pool', lambda e: e.tensor_copy(out=w2a2[:, g, c0:c0 + 256], in_=wst[0:64, 0, 0:256]), reads=['wst'], writes=['w2a2'])

    def shift_into(dst, dkey, raw, rkey, T, c0ap, m0ap, m1ap, t1tile, eng='dve'):
        for s0 in range(0, T, 512):
            n = min(512, T - s0)
            P.op(eng, lambda e: e.tensor_scalar(out=t1tile[:, 0:n], in0=raw[:, 16 + s0:16 + s0 + n], scalar1=c0ap, scalar2=None,
                                                op0=ALU.mult), reads=[rkey], writes=['shift_t'])
            P.op('dve', lambda e: e.scalar_tensor_tensor(out=t1tile[:, 0:n], in0=raw[:, 15 + s0:15 + s0 + n], scalar=m0ap,
                                                       in1=t1tile[:, 0:n], op0=ALU.mult, op1=ALU.add),
                 reads=[rkey, 'shift_t'], writes=['shift_t'])
            P.op('dve', lambda e: e.scalar_tensor_tensor(out=dst[:, s0:s0 + n], in0=raw[:, 17 + s0:17 + s0 + n], scalar=m1ap,
                                                       in1=t1tile[:, 0:n], op0=ALU.mult, op1=ALU.add),
                 reads=[rkey, 'shift_t'], writes=[dkey])

    shiftt = sb0("shiftt", [64, 512])

    def rwkv_seq_setup(off, T):
        load_w(wLR, 256)
        pb = min(512, T)
        for g in range(4):
            raw = FT[g]
            P.op('pool', lambda e: e.memset(raw[0:64, 15:16], 0.0), writes=['FT%d' % g])
            P.op('pool', lambda e: e.memset(raw[0:64, T + 16:T + 17], 0.0), writes=['FT%d' % g])
            for b in range(T // pb):
                t0 = b * pb
                proj(64 * g, 64, t0, pb, lambda pz, pk: P.op(
                    'act', lambda e: e.copy(out=raw[0:64, 16 + t0:16 + t0 + pb], in_=pz[0:64, 0:pb]), reads=[pk], writes=['FT%d' % g]))
            shift_into(FT[4][0:64, :], 'FT4', raw[0:64, :], 'FT%d' % g, T, c0_lr[:, g:g + 1], prm['mu_lr'][:, 0, g:g + 1],
                       prm['mu_lr'][:, 1, g:g + 1], shiftt)
            if g < 2:
                P.op('act', lambda e: e.activation(out=LR[g][:, 0:T], in_=FT[4][0:64, 0:T], func=AF.Tanh), reads=['FT4'], writes=['LR%d' % g])
            else:
                P.op('act', lambda e: e.copy(out=LR[g][:, 0:T], in_=FT[4][0:64, 0:T]), reads=['FT4'], writes=['LR%d' % g])

    def rwkv_head(h, slot, off, T, is_sample, pidx):
        P.barrier()
        load_w(wB[h], 256)
        pb = min(512, T)
        nchT = T // 64
        for g in range(3):
            raw = FT[g]
            P.op('pool', lambda e: e.memset(raw[0:64, 15:16], 0.0), writes=['FT%d' % g])
            P.op('pool', lambda e: e.memset(raw[0:64, T + 16:T + 17], 0.0), writes=['FT%d' % g])
            for b in range(T // pb):
                t0 = b * pb
                proj(64 * g, 64, t0, pb, lambda pz, pk: P.op(
                    'act', lambda e: e.copy(out=raw[0:64, 16 + t0:16 + t0 + pb], in_=pz[0:64, 0:pb]), reads=[pk], writes=['FT%d' % g]))
            shift_into(FT[3 + g][0:64, :], 'FT%d' % (3 + g), raw[0:64, :], 'FT%d' % g, T, c0_rkv[:, g, h:h + 1],
                       prm['mu_rkv'][:, 0, g, h:h + 1], prm['mu_rkv'][:, 1, g, h:h + 1], shiftt, eng=('dve' if g != 1 else 'pool'))
        rS, kS, vS = FT[3][0:64, :], FT[4][0:64, :], FT[5][0:64, :]
        szb, yaccr, bonus = FT[0][0:64, :], FT[1][0:64, 0:T], FT[2][0:64, :]
        yacc = yaccr.rearrange("p (c v) -> p c v", v=64)
        for b in range(T // pb):
            t0 = b * pb
            proj(192, 64, t0, pb, lambda pz, pk: P.op(
                'act', lambda e: e.activation(out=szb[:, t0:t0 + pb], in_=pz[0:64, 0:pb], func=AF.Silu), reads=[pk], writes=['FT0']))
        tb = min(TR, T)
        nblk = T // tb
        P.barrier()
        for d in range(2):
            rev = (d == 1)
            cur = 0
            Zt = [rsq['Z0'], rsq['Z1']]
            if is_sample:
                P.dma(rsq['zt'][:], s_rwkv[d, h], writes=['rq_zt'])
                pz, pk = nps()
                P.op('pe', lambda e: e.transpose(out=pz[0:64, 0:64], in_=rsq['zt'][:], identity=ident[0:64, 0:64]),
                     reads=['rq_zt', 'ident'], writes=[pk])
                P.op('act', lambda e: e.copy(out=Zt[0][:], in_=pz[0:64, 0:64]), reads=[pk], writes=['rq_Z0'])
            else:
                P.op('pool', lambda e: e.memset(Zt[0][:], 0.0), writes=['rq_Z0'])
            blks = list(range(nblk))
            if rev:
                blks = blks[::-1]
            for b in blks:
                t0 = b * tb
                sl = slice(t0, t0 + tb)
                nch = tb // 64
                R_ = rb
                pz, pk = nps()
                P.op('pe', lambda e: e.matmul(pz[0:64, 0:tb], w2a2[:, d, h * 64:(h + 1) * 64], LR[d][:, sl], start=True, stop=True),
                     reads=['w2a2', 'LR%d' % d], writes=[pk])
                P.op('act', lambda e: e.activation(out=R_['lw'][:, 0:tb], in_=pz[0:64, 0:tb], func=AF.Sigmoid,
                                                   bias=prm['w0'][:, d, h:h + 1], scale=1.0), reads=[pk, 'p_w0'], writes=['r_lw'])
                P.op('pool', lambda e: e.tensor_scalar(out=R_['lw'][:, 0:tb], in0=R_['lw'][:, 0:tb], scalar1=-0.6065306597126334,
                                                       scalar2=None, op0=ALU.mult), reads=['r_lw'], writes=['r_lw'])
                pz, pk = nps()
                P.op('pe', lambda e: e.matmul(pz[0:64, 0:tb], w2a2[:, 2 + d, h * 64:(h + 1) * 64], LR[2 + d][:, sl], start=True, stop=True),
                     reads=['w2a2', 'LR%d' % (2 + d)], writes=[pk])
                P.op('act', lambda e: e.activation(out=R_['a'][:, 0:tb], in_=pz[0:64, 0:tb], func=AF.Sigmoid,
                                                   bias=prm['a0'][:, d, h:h + 1], scale=1.0), reads=[pk, 'p_a0'], writes=['r_a'])
                P.op('dve', lambda e: e.tensor_scalar(out=R_['kk'][:, 0:tb], in0=kS[:, sl], scalar1=prm['kk'][:, h:h + 1],
                                                      scalar2=None, op0=ALU.mult), reads=['FT4', 'p_kk'], writes=['r_kk'])
                P.op('pool', lambda e: e.tensor_tensor(out=R_['kq'][:, 0:tb], in0=R_['kk'][:, 0:tb], in1=R_['kk'][:, 0:tb], op=ALU.mult),
                     reads=['r_kk'], writes=['r_kq'])
                pz, pk = nps()
                P.op('pe', lambda e: e.matmul(pz[0:64, 0:tb], ones[0:64, 0:64], R_['kq'][:, 0:tb], start=True, stop=True),
                     reads=['ones', 'r_kq'], writes=[pk])
                P.op('act', lambda e: e.activation(out=R_['kq'][:, 0:tb], in_=pz[0:64, 0:tb], func=AF.Sqrt), reads=[pk], writes=['r_kq'])
                P.op('dve', lambda e: e.tensor_scalar(out=R_['kq'][:, 0:tb], in0=R_['kq'][:, 0:tb], scalar1=1e-12, scalar2=None,
                                                      op0=ALU.max), reads=['r_kq'], writes=['r_kq'])
                P.op('dve', lambda e: e.reciprocal(out=R_['kq'][:, 0:tb], in_=R_['kq'][:, 0:tb]), reads=['r_kq'], writes=['r_kq'])
                P.op('dve', lambda e: e.tensor_tensor(out=R_['kap'][:, 0:tb], in0=R_['kk'][:, 0:tb], in1=R_['kq'][:, 0:tb], op=ALU.mult),
                     reads=['r_kk', 'r_kq'], writes=['r_kap'])
                P.op('pool', lambda e: e.tensor_scalar(out=R_['t1'][:, 0:tb], in0=R_['a'][:, 0:tb], scalar1=prm['ka'][:, h:h + 1],
                                                       scalar2=omka[:, h:h + 1], op0=ALU.mult, op1=ALU.add),
                     reads=['r_a', 'p_ka', 'omka'], writes=['r_t1'])
                P.op('pool', lambda e: e.tensor_tensor(out=R_['kd'][:, 0:tb], in0=kS[:, sl], in1=R_['t1'][:, 0:tb], op=ALU.mult),
                     reads=['FT4', 'r_t1'], writes=['r_kd'])
                P.op('dve', lambda e: e.tensor_tensor(out=R_['b'][:, 0:tb], in0=R_['a'][:, 0:tb], in1=R_['kap'][:, 0:tb], op=ALU.mult),
                     reads=['r_a', 'r_kap'], writes=['r_b'])
                P.op('dve', lambda e: e.scalar_tensor_tensor(out=R_['rk'][:, 0:tb], in0=rS[:, sl], scalar=prm['rk'][:, h:h + 1],
                                                             in1=R_['kd'][:, 0:tb], op0=ALU.mult, op1=ALU.mult),
                     reads=['FT3', 'p_rk', 'r_kd'], writes=['r_rk'])
                pz, pk = nps()
                P.op('pe', lambda e: e.matmul(pz[0:64, 0:tb], ones[0:64, 0:64], R_['rk'][:, 0:tb], start=True, stop=True),
                     reads=['ones', 'r_rk'], writes=[pk])
                if d == 0:
                    P.op('dve', lambda e: e.tensor_tensor(out=bonus[:, sl], in0=pz[0:64, 0:tb], in1=vS[:, sl], op=ALU.mult),
                         reads=[pk, 'FT5'], writes=[('bonus', b)])
                else:
                    P.op('dve', lambda e: e.tensor_tensor(out=R_['rk'][:, 0:tb], in0=pz[0:64, 0:tb], in1=vS[:, sl], op=ALU.mult),
                         reads=[pk, 'FT5'], writes=['r_rk'])
                    P.op('pool', lambda e: e.tensor_tensor(out=bonus[:, sl], in0=bonus[:, sl], in1=R_['rk'][:, 0:tb], op=ALU.add),
                         reads=['r_rk', ('bonus', b)], writes=[('bonus', b)])
                G, br, E, Ei, Em = R_['G'], R_['br'], R_['E'], R_['Ei'], R_['Em']
                P.op('dve', lambda e: e.memset(E[:, 0:tb], 0.0), writes=['r_E'])
                if not rev:
                    P.op('dve', lambda e: e.tensor_tensor_scan(out=G[:, 0:tb], data0=R_['lw'][:, 0:tb], data1=E[:, 0:tb],
                                                               initial=0.0, op0=ALU.add, op1=ALU.add),
                         reads=['r_lw', 'r_E'], writes=['r_G'])
                    ci_ = 0
                else:
                    P.op('dve', lambda e: e.tensor_tensor_scan(out=G[:, 0:tb][:, ::-1], data0=R_['lw'][:, 0:tb][:, ::-1],
                                                               data1=E[:, 0:tb], initial=0.0, op0=ALU.add, op1=ALU.add),
                         reads=['r_lw', 'r_E'], writes=['r_G'])
                    ci_ = 63
                G3 = G[:, 0:tb].rearrange("p (c l) -> p c l", l=64)
                lw3 = R_['lw'][:, 0:tb].rearrange("p (c l) -> p c l", l=64)
                P.op('dve', lambda e: e.tensor_tensor(out=rgref[:, 0:nch], in0=G3[:, :, ci_], in1=lw3[:, :, ci_], op=ALU.subtract),
                     reads=['r_G', 'r_lw'], writes=['rgref'])
                P.op('dve', lambda e: e.tensor_tensor(out=br[:, 0:tb].rearrange("p (c l) -> p c l", l=64), in0=G3,
                                                      in1=rgref[:, 0:nch].unsqueeze(2).to_broadcast([64, nch, 64]), op=ALU.subtract),
                     reads=['r_G', 'rgref'], writes=['r_br'])
                bend = br[:, 0:tb].rearrange("p (c l) -> p c l", l=64)[:, :, (0 if rev else 63)]
                P.op('act', lambda e: e.activation(out=rgam[:, 0:nch], in_=bend, func=AF.Exp), reads=['r_br'], writes=['rgam'])
                P.op('pool', lambda e: e.tensor_scalar(out=rgam[:, 4:4 + nch], in0=rgam[:, 0:nch], scalar1=-1.0, scalar2=None,
                                                       op0=ALU.mult), reads=['rgam'], writes=['rgam'])
                P.op('act', lambda e: e.activation(out=E[:, 0:tb], in_=br[:, 0:tb], func=AF.Exp), reads=['r_br'], writes=['r_E'])
                P.op('act', lambda e: e.activation(out=Ei[:, 0:tb], in_=br[:, 0:tb], func=AF.Exp, scale=-1.0),
                     reads=['r_br'], writes=['r_Ei'])
                P.op('pool', lambda e: e.tensor_tensor(out=R_['t1'][:, 0:tb], in0=br[:, 0:tb], in1=R_['lw'][:, 0:tb], op=ALU.subtract),
                     reads=['r_br', 'r_lw'], writes=['r_t1'])
                P.op('act', lambda e: e.activation(out=Em[:, 0:tb], in_=R_['t1'][:, 0:tb], func=AF.Exp), reads=['r_t1'], writes=['r_Em'])
                P.op('dve', lambda e: e.tensor_tensor(out=KR[:, 0, 0:tb], in0=R_['kap'][:, 0:tb], in1=Em[:, 0:tb], op=ALU.mult),
                     reads=['r_kap', 'r_Em'], writes=['r_KR'])
                P.op('pool', lambda e: e.tensor_tensor(out=KR[:, 1, 0:tb], in0=rS[:, sl], in1=E[:, 0:tb], op=ALU.mult),
                     reads=['FT3', 'r_E', 'r_KR'], writes=['r_KR'])
                P.op('dve', lambda e: e.tensor_tensor(out=R_['bh'][:, 0:tb], in0=R_['b'][:, 0:tb], in1=Ei[:, 0:tb], op=ALU.mult),
                     reads=['r_b', 'r_Ei'], writes=['r_bh'])
                P.op('pool', lambda e: e.tensor_tensor(out=R_['kh'][:, 0:tb], in0=R_['kd'][:, 0:tb], in1=Ei[:, 0:tb], op=ALU.mult),
                     reads=['r_kd', 'r_Ei'], writes=['r_kh'])
                P.op('dve', lambda e: e.tensor_tensor(out=R_['Kb'][:, 0:tb].rearrange("p (c l) -> p c l", l=64),
                                                      in0=R_['kh'][:, 0:tb].rearrange("p (c l) -> p c l", l=64),
                                                      in1=rgam[:, 0:nch].unsqueeze(2).to_broadcast([64, nch, 64]), op=ALU.mult),
                     reads=['r_kh', 'rgam'], writes=['r_Kb'])
                P.op('dve', lambda e: e.tensor_tensor(out=R_['Bb'][:, 0:tb].rearrange("p (c l) -> p c l", l=64),
                                                      in0=R_['bh'][:, 0:tb].rearrange("p (c l) -> p c l", l=64),
                                                      in1=rgam[:, 4:4 + nch].unsqueeze(2).to_broadcast([64, nch, 64]), op=ALU.mult),
                     reads=['r_bh', 'rgam'], writes=['r_Bb'])
                chs = list(range(nch))
                if rev:
                    chs = chs[::-1]
                st = {}
                for c in chs:
                    cs = slice(c * 64, (c + 1) * 64)
                    C_ = cset[c]
                    ck = (lambda n_, c=c: 'c%d_%s' % (c, n_))
                    pA, pAk = nps()
                    P.op('pe', lambda e: e.matmul(pA[0:64, 0:128], R_['bh'][:, cs], KR[:, :, cs], start=True, stop=True),
                         reads=['r_bh', 'r_KR'], writes=[pAk])
                    P.op('pe', lambda e: e.matmul(pA[0:64, 128:256], R_['kh'][:, cs], KR[:, :, cs], start=True, stop=True),
                         reads=['r_kh', 'r_KR'], writes=[pAk])
                    P.op('pe', lambda e: e.matmul(pA[0:64, 256:320], KR[:, 0, cs], R_['bh'][:, cs], start=True, stop=True),
                         reads=['r_bh', 'r_KR'], writes=[pAk])
                    AB, BB = C_['AB'], C_['BB']
                    P.op('dve', lambda e: e.tensor_tensor(out=AB[:], in0=pA[0:64, 0:128], in1=mR[:, d, 0, :], op=ALU.mult),
                         reads=[pAk, 'mR'], writes=[ck('AB')])
                    P.op('dve', lambda e: e.tensor_tensor(out=BB[:], in0=pA[0:64, 128:256], in1=mR[:, d, 1, :], op=ALU.mult),
                         reads=[pAk, 'mR'], writes=[ck('BB')])
                    P.op('dve', lambda e: e.tensor_tensor(out=C_['XT0'][:], in0=pA[0:64, 256:320], in1=mR[:, d, 2, 0:64], op=ALU.mult),
                         reads=[pAk, 'mR'], writes=[ck('XT0')])
                    P.op('dve', lambda e: e.tensor_tensor(out=C_['Pm0'][:], in0=AB[:, 0:64], in1=ident[0:64, 0:64], op=ALU.add),
                         reads=[ck('AB'), 'ident'], writes=[ck('Pm0')])
                    st[c] = dict(X=AB[:, 0:64], Xk=ck('AB'), XT=C_['XT0'], XTk=ck('XT0'), xti=0, pmi=0)
                for lev in range(5):
                    for c in chs:
                        C_ = cset[c]
                        s_ = st[c]
                        ck = (lambda n_, c=c: 'c%d_%s' % (c, n_))
                        X, Xk, XT, XTk = s_['X'], s_['Xk'], s_['XT'], s_['XTk']
                        pq, pqk = nps()
                        nXTn = 'XT1' if s_['xti'] == 0 else 'XT0'
                        nXT, nXTk = C_[nXTn], ck(nXTn)
                        P.op('pe', lambda e: e.matmul(pq[0:64, 64:128], X, XT[:], start=True, stop=True), reads=[Xk, XTk], writes=[pqk])
                        if lev < 4:
                            P.op('pe', lambda e: e.matmul(pq[0:64, 0:64], XT[:], X, start=True, stop=True), reads=[Xk, XTk], writes=[pqk])
                        P.op('act', lambda e: e.copy(out=nXT[:], in_=pq[0:64, 64:128]), reads=[pqk], writes=[nXTk])
                        pmi = s_['pmi']
                        Pc, Pn = C_['Pm%d' % pmi], C_['Pm%d' % (1 - pmi)]
                        pp, ppk = nps()
                        P.op('pe', lambda e: e.matmul(pp[0:64, 0:64], nXT[:], Pc[:], start=True, stop=True),
                             reads=[nXTk, ck('Pm%d' % pmi)], writes=[ppk])
                        P.op('dve', lambda e: e.tensor_tensor(out=Pn[:], in0=pp[0:64, 0:64], in1=Pc[:], op=ALU.add),
                             reads=[ppk, ck('Pm%d' % pmi)], writes=[ck('Pm%d' % (1 - pmi))])
                        s_['pmi'] = 1 - pmi
                        if lev < 4:
                            tn = 'X1' if lev % 2 == 0 else 'Xw'
                            P.op('dve', lambda e: e.tensor_copy(out=C_[tn][:], in_=pq[0:64, 0:64]), reads=[pqk], writes=[ck(tn)])
                            s_['X'], s_['Xk'] = C_[tn][:], ck(tn)
                        s_['XT'], s_['XTk'], s_['xti'] = nXT, nXTk, 1 - s_['xti']
                for c in chs:
                    cs = slice(c * 64, (c + 1) * 64)
                    gsl = slice(t0 + c * 64, t0 + (c + 1) * 64)
                    C_ = cset[c]
                    pt, ptk = nps()
                    P.op('pe', lambda e: e.transpose(out=pt[0:64, 0:64], in_=vS[:, gsl], identity=ident[0:64, 0:64]),
                         reads=['FT5', 'ident'], writes=[ptk])
                    P.op('pe', lambda e: e.transpose(out=pt[0:64, 64:128], in_=R_['Kb'][:, cs], identity=ident[0:64, 0:64]),
                         reads=['r_Kb', 'ident'], writes=[ptk])
                    P.op('pe', lambda e: e.transpose(out=pt[0:64, 128:192], in_=R_['Bb'][:, cs], identity=ident[0:64, 0:64]),
                         reads=['r_Bb', 'ident'], writes=[ptk])
                    P.op('act', lambda e: e.copy(out=C_['Vt'][:], in_=pt[0:64, 0:64]), reads=[ptk], writes=['c%d_Vt' % c])
                    P.op('act', lambda e: e.copy(out=C_['Kt'][:], in_=pt[0:64, 64:128]), reads=[ptk], writes=['c%d_Kt' % c])
                    P.op('act', lambda e: e.copy(out=C_['Bt'][:], in_=pt[0:64, 128:192]), reads=[ptk], writes=['c%d_Bt' % c])
                for c in chs:
                    cs = slice(c * 64, (c + 1) * 64)
                    cg = (t0 // 64) + c
                    C_ = cset[c]
                    s_ = st[c]
                    AB, BB = C_['AB'], C_['BB']
                    ABk, BBk, Vtk, Ktk, Btk = ['c%d_%s' % (c, n_) for n_ in ('AB', 'BB', 'Vt', 'Kt', 'Bt')]
                    Pm, Pmk = C_['Pm%d' % s_['pmi']], 'c%d_Pm%d' % (c, s_['pmi'])
                    Zc, Zn = Zt[cur], Zt[1 - cur]
                    zck, znk = 'rq_Z%d' % cur, 'rq_Z%d' % (1 - cur)
                    pw, pwk = nps()
                    P.op('pe', lambda e: e.matmul(pw[0:64, 0:64], KR[:, 0, cs], Zc[:], start=True, stop=False),
                         reads=['r_KR', zck], writes=[pwk])
                    P.op('pe', lambda e: e.matmul(pw[0:64, 0:64], BB[:, 0:64], C_['Vt'][:], start=False, stop=True),
                         reads=[BBk, Vtk], writes=[pwk])
                    P.op('act', lambda e: e.copy(out=rsq['zt'][:], in_=pw[0:64, 0:64]), reads=[pwk], writes=['rq_zt'])
                    pu, puk = nps()
                    P.op('pe', lambda e: e.matmul(pu[0:64, 0:64], Pm[:], rsq['zt'][:], start=True, stop=True),
                         reads=[Pmk, 'rq_zt'], writes=[puk])
                    P.op('act', lambda e: e.copy(out=rsq['U'][:], in_=pu[0:64, 0:64]), reads=[puk], writes=['rq_U'])
                    py, pyk = nps()
                    P.op('pe', lambda e: e.matmul(py[0:64, 0:64], KR[:, 1, cs], Zc[:], start=True, stop=False),
                         reads=['r_KR', zck], writes=[pyk])
                    P.op('pe', lambda e: e.matmul(py[0:64, 0:64], BB[:, 64:128], C_['Vt'][:], start=False, stop=False),
                         reads=[BBk, Vtk], writes=[pyk])
                    P.op('pe', lambda e: e.matmul(py[0:64, 0:64], AB[:, 64:128], rsq['U'][:], start=False, stop=True),
                         reads=[ABk, 'rq_U'], writes=[pyk])
                    pzz, pzk = nps()
                    P.op('pe', lambda e: e.matmul(pzz[0:64, 0:64], C_['Kt'][:], C_['Vt'][:], start=True, stop=False),
                         reads=[Ktk, Vtk], writes=[pzk])
                    P.op('pe', lambda e: e.matmul(pzz[0:64, 0:64], C_['Bt'][:], rsq['U'][:], start=False, stop=True),
                         reads=[Btk, 'rq_U'], writes=[pzk])
                    P.op('dve', lambda e: e.scalar_tensor_tensor(out=Zn[:], in0=Zc[:], scalar=rgam[:, c:c + 1], in1=pzz[0:64, 0:64],
                                                                 op0=ALU.mult, op1=ALU.add), reads=[zck, 'rgam', pzk], writes=[znk])
                    if d == 0:
                        P.op('act', lambda e: e.copy(out=yacc[:, cg, :], in_=py[0:64, 0:64]), reads=[pyk], writes=[('yacc', cg)])
                    else:
                        P.op('dve', lambda e: e.tensor_tensor(out=yacc[:, cg, :], in0=yacc[:, cg, :], in1=py[0:64, 0:64], op=ALU.add),
                             reads=[pyk, ('yacc', cg)], writes=[('yacc', cg)])
                    cur = 1 - cur
            if not is_sample:
                pz, pk = nps()
                P.op('pe', lambda e: e.transpose(out=pz[0:64, 0:64], in_=Zt[cur][:], identity=ident[0:64, 0:64]),
                     reads=['rq_Z%d' % cur, 'ident'], writes=[pk])
                P.op('act', lambda e: e.copy(out=rsq['zt'][:], in_=pz[0:64, 0:64]), reads=[pk], writes=['rq_zt'])
                P.dma(o_rwkv[pidx, d, h], rsq['zt'][:], reads=['rq_zt'], q='pool')
        ykeys = [('yacc', c) for c in range(nchT)]
        gst = ostat[0:64, :, :].rearrange("p a b -> p (a b)")
        P.op('dve', lambda e: e.tensor_reduce(out=gst[:, 0:nchT], in_=yacc, axis=AX.X, op=ALU.add), reads=ykeys, writes=['gst'])
        P.op('dve', lambda e: e.tensor_scalar(out=gst[:, 0:nchT], in0=gst[:, 0:nchT], scalar1=-1.0 / 64, scalar2=None, op0=ALU.mult),
             reads=['gst'], writes=['gst'])
        P.op('dve', lambda e: e.tensor_tensor(out=yacc, in0=yacc, in1=gst[:, 0:nchT].unsqueeze(2).to_broadcast([64, nchT, 64]), op=ALU.add),
             reads=ykeys + ['gst'], writes=ykeys)
        sq = FT[3][0:64, 0:T].rearrange("p (c v) -> p c v", v=64)
        P.op('pool', lambda e: e.tensor_tensor(out=sq, in0=yacc, in1=yacc, op=ALU.mult), reads=ykeys, writes=['FT3'])
        P.op('dve', lambda e: e.tensor_reduce(out=gst[:, 32:32 + nchT], in_=sq, axis=AX.X, op=ALU.add), reads=['FT3'], writes=['gst'])
        P.op('dve', lambda e: e.tensor_scalar(out=gst[:, 32:32 + nchT], in0=gst[:, 32:32 + nchT], scalar1=1.0 / 64, scalar2=64e-5,
                                              op0=ALU.mult, op1=ALU.add), reads=['gst'], writes=['gst'])
        P.op('act', lambda e: e.activation(out=gst[:, 32:32 + nchT], in_=gst[:, 32:32 + nchT], func=AF.Sqrt), reads=['gst'], writes=['gst'])
        P.op('dve', lambda e: e.reciprocal(out=gst[:, 32:32 + nchT], in_=gst[:, 32:32 + nchT]), reads=['gst'], writes=['gst'])
        P.op('dve', lambda e: e.tensor_tensor(out=yacc, in0=yacc, in1=gst[:, 32:32 + nchT].unsqueeze(2).to_broadcast([64, nchT, 64]),
                                              op=ALU.mult), reads=ykeys + ['gst'], writes=ykeys)
        n8 = min(8, nchT)
        for g8 in range(nchT // n8):
            pz, pk = nps()
            for j in range(n8):
                cg = g8 * n8 + j
                P.op('pe', lambda e: e.transpose(out=pz[0:64, j * 64:(j + 1) * 64], in_=yacc[:, cg, :], identity=ident[0:64, 0:64]),
                     reads=[('yacc', cg), 'ident'], writes=[pk])
            w_ = n8 * 64
            gs = slice(g8 * w_, (g8 + 1) * w_)
            P.op('dve', lambda e: e.tensor_scalar(out=shiftt[:, 0:w_], in0=pz[0:64, 0:w_], scalar1=prm['gng'][:, h:h + 1],
                                                  scalar2=prm['gnb'][:, h:h + 1], op0=ALU.mult, op1=ALU.add),
                 reads=[pk, 'p_gng', 'p_gnb'], writes=['shift_t'])
            P.op('pool', lambda e: e.tensor_tensor(out=shiftt[:, 0:w_], in0=shiftt[:, 0:w_], in1=bonus[:, gs], op=ALU.add),
                 reads=['shift_t'] + [('bonus', b) for b in range(nblk)], writes=['shift_t'])
            P.op('dve', lambda e: e.tensor_tensor(out=yT[0:64, slot, gs], in0=shiftt[:, 0:w_], in1=szb[:, gs], op=ALU.mult),
                 reads=['shift_t', 'FT0'], writes=[('yT', slot)])

    seq_ids = debug.get('seqs', [0, 1, 2]) if debug else [0, 1, 2]
    head_ids = debug.get('heads', list(range(8))) if debug else list(range(8))
    rheads = debug.get('rheads', list(range(16))) if debug else list(range(16))
    for si in seq_ids:
        off, T, cidx, is_sample = SEQS[si]
        P.barrier()
        make_gate(0, cidx)
        make_hT(0, xin, 'xin', off, T, cidx)
        P.barrier()
        if debug and debug.get('inner'):
            dump("mT", mT[:, 0].rearrange("p a b -> p (a b)"), ['mT'], 48)
            dump("sc1", sc1[:, 0].rearrange("p a b -> p (a b)"), ['sc1'], 16)
            dump("scT", scT[:].rearrange("p a b -> p (a b)"), ['scT'], 16)
            dump("xn", xn, ['xn'], 1024)
            dump("xt0", xt[0], ['xt0'], 1024)
            for kc in range(8):
                dump("hT%d" % kc, hT[:, kc, 0:T], hT_keys(0, T), T, col0=off)
        for h in head_ids:
            hgrn_head(h, off, T, is_sample, si - 1)
            if debug and debug.get('dump_y'):
                dump("yT%d" % h, yT[:, h, 0:T], [('yT', h)], T, col0=off)
        P.barrier()
        load_wo(w_out_even[0:D, :], 128)
        outproj(0, [(128, s_) for s_ in range(8)], xin, 'xin', x1, 'x1', off, T, cidx)
        P.barrier()
        rwkv_seq_setup(off, T)
        for half in range(2):
            P.barrier()
            for slot in range(8):
                h = half * 8 + slot
                if h in rheads:
                    rwkv_head(h, slot, off, T, is_sample, si - 1)
                    if debug and debug.get('dump_y'):
                        dump("yR%d" % h, yT[0:64, slot, 0:T], [('yT', slot)], T, col0=off, parts=64)
            P.barrier()
            load_wo(w_out_even[D + half * 512: D + (half + 1) * 512, :], 64)
            outproj(0, [(64, s_) for s_ in range(8)], x1, 'x1', x1, 'x1', off, T, cidx)
    P.barrier()
    L0.close()

    if not (debug and debug.get('l0only')):
        L1 = contextlib.ExitStack()

        def sb1(name, shape, dt=F32):
            return L1.enter_context(nc.sbuf_tensor(name, list(shape), dt))

        make_fng()
        LC = 128
        DH = 512
        qT = yT[:, 4:8, :]
        kT = sb1("kT", [128, 4, TS], BF16)
        vch = sb1("vch", [128, DH], BF16)
        Cst = sb1("Cst", [128, 4, DH])
        Cbf = sb1("Cbf", [128, 4, DH], BF16)
        nst = sb1("nst", [128, 8])
        nbf = sb1("nbf", [128, 4], BF16)
        ktok = sb1("ktok", [128, DH], BF16)
        vw = sb1("vw", [128, DH], BF16)
        sTs = sb1("sTs", [128, 128], BF16)
        onesb = sb1("onesb", [128, 1], BF16)
        identb = sb1("identb", [128, 128], BF16)
        mC = sb1("mC", [128, 2, 128])
        SEL = sb1("SEL", [36, 4, 128])
        XA = sb1("XA", [36, TS])
        XB = sb1("XB", [36, TS])
        zrow = sb1("zrow", [36, 512])
        sm = {n_: sb1("sm_" + n_, [36, 16]) for n_ in ['ac', 'bl', 'M', 'MP', 'mu', 'al', 'gref', 'm0']}
        Wtok = sb1("Wtok", [128, 2, 16, 8])
        Wtokb = sb1("Wtokb", [128, 2, 16, 4], BF16)
        ALb = sb1("ALb", [128, 2, 4, 16])
        dstat = sb1("dstat", [128, 8])
        wGb = sb1("wGb", [128, 8, 16], BF16)
        gbT = sb1("gbT", [36, 4])
        ngbT = sb1("ngbT", [36, 4])
        cw = sb1("cw", [128, 32, 9])
        cb = sb1("cb", [128, 32])
        mng = sb1("mng", [128, 16])
        wbf1 = sb1("wbf1", [128, 8, DH], BF16)
        P.dma(mC[:], maskC.rearrange("d s t -> s d t"), writes=['mC'])
        P.dma(SEL[:], sel_d[:], writes=['SEL'])
        P.dma(gbT[:], gbT_d[:], writes=['gbT'])
        P.dma(cw[:], cw_d[:], writes=['cw'])
        P.dma(cb[:], cb_d[:], writes=['cb'])
        P.dma(mng[:], mng_d[:], writes=['mng'])
        P.op('dve', lambda e: e.memset(onesb[:], 1.0), writes=['onesb'])
        P.op('dve', lambda e: e.memset(zrow[:], 0.0), writes=['zrow'])
        P.op('dve', lambda e: e.tensor_copy(out=identb[:], in_=ident[:]), reads=['ident'], writes=['identb'])
        P.op('dve', lambda e: e.tensor_scalar(out=ngbT[:], in0=gbT[:], scalar1=-1.0, scalar2=None, op0=ALU.mult), reads=['gbT'], writes=['ngbT'])
        P.dma(wst[:, :, 0:16], w_in_odd[:, 10240:10256].rearrange("(kc p) n -> p kc n", p=128), writes=['wst'])
        P.op('pool', lambda e: e.tensor_copy(out=wGb[:], in_=wst[:, :, 0:16]), reads=['wst'], writes=['wGb'])
        LNK = float(np.log(DH ** -0.5))

        def load_w1(c0, ncols):
            v = w_in_odd[:, c0:c0 + ncols].rearrange("(kc p) n -> p kc n", p=128)
            for q0 in range(0, ncols, 256):
                w_ = min(256, ncols - q0)
                P.dma(wst[:, :, 0:w_], v[:, :, q0:q0 + w_], writes=['wst'])
                P.op('pool', lambda e: e.tensor_copy(out=wbf1[:, :, q0:q0 + w_], in_=wst[:, :, 0:w_]), reads=['wst'], writes=['wbf1'])

        def gates_seq(T, is_sample):
            NC = T // LC
            pbk = min(512, T)
            for d in range(2):
                pb = 32 * d
                rows = slice(pb, pb + 4)
                for b in range(T // pbk):
                    t0 = b * pbk
                    pz, pk = nps()
                    for kc in range(8):
                        P.op('pe', lambda e: e.matmul(pz[pb:pb + 4, 0:pbk], wGb[:, kc, (2 + d) * 4:(3 + d) * 4], hT[:, kc, t0:t0 + pbk],
                                                      start=(kc == 0), stop=(kc == 7)), reads=['wGb'] + hT_keys(t0, t0 + pbk), writes=[pk])
                    P.op('act', lambda e: e.activation(out=XA[rows, t0:t0 + pbk], in_=pz[pb:pb + 4, 0:pbk], func=AF.Exp,
                                                       bias=ngbT[rows, 2 + d:3 + d], scale=-1.0), reads=[pk, 'ngbT'], writes=['XA'])
                P.op('act', lambda e: e.activation(out=XA[rows, 0:T], in_=XA[rows, 0:T], func=AF.Ln, bias=1.0, scale=1.0),
                     reads=['XA'], writes=['XA'])
                for b in range(T // pbk):
                    bs = slice(b * pbk, (b + 1) * pbk)
                    if d == 0:
                        P.op('dve', lambda e: e.tensor_tensor_scan(out=XB[rows, bs], data0=XA[rows, bs], data1=zrow[rows, 0:pbk],
                                                                   initial=0.0, op0=ALU.add, op1=ALU.add), reads=['XA', 'zrow'], writes=['XB'])
                    else:
                        P.op('dve', lambda e: e.tensor_tensor_scan(out=XB[rows, bs][:, ::-1], data0=XA[rows, bs][:, ::-1],
                                                                   data1=zrow[rows, 0:pbk], initial=0.0, op0=ALU.add, op1=ALU.add),
                             reads=['XA', 'zrow'], writes=['XB'])
                if d == 0:
                    ci_, ce_ = 0, LC - 1
                else:
                    ci_, ce_ = LC - 1, 0
                B3 = XB[rows, 0:T].rearrange("p (c l) -> p c l", l=LC)
                A3 = XA[rows, 0:T].rearrange("p (c l) -> p c l", l=LC)
                S = {k_: v_[rows, :] for k_, v_ in sm.items()}
                P.op('dve', lambda e: e.tensor_tensor(out=S['gref'][:, 0:NC], in0=B3[:, :, ci_], in1=A3[:, :, ci_], op=ALU.subtract),
                     reads=['XA', 'XB'], writes=['sm_gref'])
                P.op('dve', lambda e: e.tensor_tensor(out=B3, in0=B3, in1=S['gref'][:, 0:NC].unsqueeze(2).to_broadcast([4, NC, LC]),
                                                      op=ALU.subtract), reads=['XB', 'sm_gref'], writes=['XB'])
                for b in range(T // pbk):
                    t0 = b * pbk
                    pz, pk = nps()
                    for kc in range(8):
                        P.op('pe', lambda e: e.matmul(pz[pb:pb + 4, 0:pbk], wGb[:, kc, d * 4:(d + 1) * 4], hT[:, kc, t0:t0 + pbk],
                                                      start=(kc == 0), stop=(kc == 7)), reads=['wGb'] + hT_keys(t0, t0 + pbk), writes=[pk])
                    P.op('dve', lambda e: e.scalar_tensor_tensor(out=XA[rows, t0:t0 + pbk], in0=pz[pb:pb + 4, 0:pbk], scalar=gbT[rows, d:d + 1],
                                                                 in1=XB[rows, t0:t0 + pbk], op0=ALU.add, op1=ALU.add),
                         reads=[pk, 'gbT', 'XB', 'XA'], writes=['XA'])
                P.op('dve', lambda e: e.tensor_reduce(out=S['ac'][:, 0:NC], in_=A3, axis=AX.X, op=ALU.max), reads=['XA'], writes=['sm_ac'])
                P.op('dve', lambda e: e.tensor_scalar(out=S['bl'][:, 0:NC], in0=B3[:, :, ce_], scalar1=-1.0, scalar2=None, op0=ALU.mult),
                     reads=['XB'], writes=['sm_bl'])
                if is_sample:
                    P.dma(S['m0'][:, 0:1], s_m[d, :].rearrange("(h o) -> h o", o=1), writes=['sm_m0'])
                else:
                    P.op('dve', lambda e: e.memset(S['m0'][:, 0:1], 0.0), writes=['sm_m0'])
                if d == 0:
                    P.op('dve', lambda e: e.tensor_tensor_scan(out=S['M'][:, 0:NC], data0=S['ac'][:, 0:NC], data1=S['bl'][:, 0:NC],
                                                               initial=S['m0'][:, 0:1], op0=ALU.max, op1=ALU.add),
                         reads=['sm_ac', 'sm_bl', 'sm_m0'], writes=['sm_M'])
                    P.op('dve', lambda e: e.tensor_copy(out=S['MP'][:, 0:1], in_=S['m0'][:, 0:1]), reads=['sm_m0'], writes=['sm_MP'])
                    if NC > 1:
                        P.op('dve', lambda e: e.tensor_copy(out=S['MP'][:, 1:NC], in_=S['M'][:, 0:NC - 1]), reads=['sm_M', 'sm_MP'], writes=['sm_MP'])
                else:
                    P.op('dve', lambda e: e.tensor_tensor_scan(out=S['M'][:, 0:NC][:, ::-1], data0=S['ac'][:, 0:NC][:, ::-1],
                                                               data1=S['bl'][:, 0:NC][:, ::-1], initial=S['m0'][:, 0:1],
                                                               op0=ALU.max, op1=ALU.add),
                         reads=['sm_ac', 'sm_bl', 'sm_m0'], writes=['sm_M'])
                    P.op('dve', lambda e: e.tensor_copy(out=S['MP'][:, NC - 1:NC], in_=S['m0'][:, 0:1]), reads=['sm_m0'], writes=['sm_MP'])
                    if NC > 1:
                        P.op('dve', lambda e: e.tensor_copy(out=S['MP'][:, 0:NC - 1], in_=S['M'][:, 1:NC]), reads=['sm_M', 'sm_MP'], writes=['sm_MP'])
                P.op('dve', lambda e: e.tensor_tensor(out=S['mu'][:, 0:NC], in0=S['MP'][:, 0:NC], in1=S['ac'][:, 0:NC], op=ALU.max),
                     reads=['sm_MP', 'sm_ac'], writes=['sm_mu'])
                P.op('dve', lambda e: e.tensor_tensor(out=S['al'][:, 0:NC], in0=S['MP'][:, 0:NC], in1=S['mu'][:, 0:NC], op=ALU.subtract),
                     reads=['sm_MP', 'sm_mu'], writes=['sm_al'])
                P.op('act', lambda e: e.activation(out=S['al'][:, 0:NC], in_=S['al'][:, 0:NC], func=AF.Exp), reads=['sm_al'], writes=['sm_al'])
                mub = S['mu'][:, 0:NC].unsqueeze(2).to_broadcast([4, NC, LC])
                P.op('dve', lambda e: e.tensor_tensor(out=A3, in0=A3, in1=mub, op=ALU.subtract), reads=['XA', 'sm_mu'], writes=['XA'])
                P.op('dve', lambda e: e.tensor_tensor(out=B3, in0=B3, in1=mub, op=ALU.subtract), reads=['XB', 'sm_mu'], writes=['XB'])
                P.op('dve', lambda e: e.tensor_scalar(out=XA[rows, 0:T], in0=XA[rows, 0:T], scalar1=LNK, scalar2=None, op0=ALU.add),
                     reads=['XA'], writes=['XA'])
                P.op('act', lambda e: e.activation(out=XA[rows, 0:T], in_=XA[rows, 0:T], func=AF.Exp), reads=['XA'], writes=['XA'])
                P.op('act', lambda e: e.activation(out=XB[rows, 0:T], in_=XB[rows, 0:T], func=AF.Exp), reads=['XB'], writes=['XB'])
                pz, pk = nps()
                for c in range(NC):
                    P.op('pe', lambda e: e.transpose(out=pz[:, c * 8:c * 8 + 4], in_=XA[rows, c * LC:(c + 1) * LC],
                                                     identity=ident[rows, pb:pb + 4]), reads=['XA', 'ident'], writes=[pk])
                    P.op('pe', lambda e: e.transpose(out=pz[:, c * 8 + 4:c * 8 + 8], in_=XB[rows, c * LC:(c + 1) * LC],
                                                     identity=ident[rows, pb:pb + 4]), reads=['XB', 'ident'], writes=[pk])
                P.op('dve', lambda e: e.tensor_copy(out=Wtok[:, d, 0:NC, :], in_=pz[:, 0:NC * 8].rearrange("p (c k) -> p c k", k=8)),
                     reads=[pk], writes=['Wtok'])
                P.op('dve', lambda e: e.tensor_copy(out=Wtokb[:, d, 0:NC, :], in_=Wtok[:, d, 0:NC, 0:4]), reads=['Wtok'], writes=['Wtokb'])
                pz, pk = nps()
                for hd in range(4):
                    P.op('pe', lambda e: e.matmul(pz[:, hd * 16:hd * 16 + NC], SEL[rows, hd, :], S['al'][:, 0:NC], start=True, stop=True),
                         reads=['SEL', 'sm_al'], writes=[pk])
                P.op('dve', lambda e: e.tensor_copy(out=ALb[:, d, :, 0:NC], in_=pz[:, 0:64].rearrange("p (h c) -> p h c", c=16)[:, :, 0:NC]),
                     reads=[pk], writes=['ALb'])

        def conv_tile(dst, dkey, slot_j, widx, t0src, T, is_sample):
            X = FT[1][:, 0:T]
            A = FT[0][:, 0:T]
            if is_sample:
                R_, Cw = T // 64, 64
                taps = [(dr, dc) for dr in (-1, 0, 1) for dc in (-1, 0, 1)]
            else:
                R_, Cw = 1, T
                taps = [(0, dc) for dc in (-1, 0, 1)]
            X3 = X.rearrange("p (r c) -> p r c", c=Cw)
            A3 = A.rearrange("p (r c) -> p r c", c=Cw)
            P.op('dve', lambda e: e.tensor_scalar(out=A, in0=X, scalar1=cw[:, widx, 4:5], scalar2=None, op0=ALU.mult),
                 reads=['FT1', 'cw'], writes=['FT0'])
            for (dr, dc) in taps:
                if dr == 0 and dc == 0:
                    continue
                r0, r1 = max(0, -dr), R_ - max(0, dr)
                c0, c1 = max(0, -dc), Cw - max(0, dc)
                ti = (dr + 1) * 3 + (dc + 1)
                P.op('dve', lambda e: e.scalar_tensor_tensor(out=A3[:, r0:r1, c0:c1], in0=X3[:, r0 + dr:r1 + dr, c0 + dc:c1 + dc],
                                                             scalar=cw[:, widx, ti:ti + 1], in1=A3[:, r0:r1, c0:c1],
                                                             op0=ALU.mult, op1=ALU.add), reads=['FT1', 'FT0', 'cw'], writes=['FT0'])
            P.op('act', lambda e: e.activation(out=dst[:, slot_j, 0:T], in_=A, func=AF.Silu, bias=cb[:, widx:widx + 1], scale=1.0),
                 reads=['FT0', 'cb'], writes=[dkey])

        hacc = [FT[2 + i][:, 0:TS].rearrange("p (j e) -> p j e", e=DH) for i in range(4)]

        def mlstm_head(hd, off, T, is_sample, pidx):
            NC = T // LC
            NTt = T // 128
            pbk = min(512, T)
            for qk in range(2):
                load_w1(qk * 2048 + hd * DH, DH)
                for j in range(4):
                    for b in range(T // pbk):
                        t0 = b * pbk
                        pz, pk = nps()
                        for kc in range(8):
                            P.op('pe', lambda e: e.matmul(pz[:, 0:pbk], wbf1[:, kc, j * 128:(j + 1) * 128], hT[:, kc, t0:t0 + pbk],
                                                          start=(kc == 0), stop=(kc == 7)), reads=['wbf1'] + hT_keys(t0, t0 + pbk), writes=[pk])
                        P.op('act', lambda e: e.copy(out=FT[1][:, t0:t0 + pbk], in_=pz[:, 0:pbk]), reads=[pk], writes=['FT1'])
                    widx = (qk * 4 + hd) * 4 + j
                    if qk == 0:
                        conv_tile(qT, ('yT', 4 + j), j, widx, 0, T, is_sample)
                    else:
                        conv_tile(kT, 'kT', j, widx, 0, T, is_sample)
            load_w1(4096 + hd * DH, DH)
            qkeys = [('yT', 4 + j) for j in range(4)]
            for d in range(2):
                rev = (d == 1)
                if is_sample:
                    P.dma(Cst[:], s_C[d, hd].rearrange("(j p) e -> p j e", p=128), writes=['Cst'])
                    P.dma(nst[:, 0:4], s_n[d, hd].rearrange("(j p) -> p j", p=128), writes=['nst'], allow_slow_non_contiguous=True)
                else:
                    P.op('pool', lambda e: e.memset(Cst[:], 0.0), writes=['Cst'])
                    P.op('pool', lambda e: e.memset(nst[:, 0:4], 0.0), writes=['nst'])
                chunks = list(range(NC))
                if rev:
                    chunks = chunks[::-1]
                for c in chunks:
                    cs = slice(c * LC, (c + 1) * LC)
                    wcol = Wtok[:, d, c, hd:hd + 1]
                    thcol = Wtok[:, d, c, 4 + hd:5 + hd]
                    alcol = ALb[:, d, hd, c:c + 1]
                    pv, pvk = nps()
                    for kc in range(8):
                        P.op('pe', lambda e: e.matmul(pv[:, 0:DH], hT[:, kc, cs], wbf1[:, kc, :], start=(kc == 0), stop=(kc == 7)),
                             reads=['wbf1', ('hT', c)], writes=[pvk])
                    P.op('act', lambda e: e.copy(out=vch[:], in_=pv[:, 0:DH]), reads=[pvk], writes=['vch'])
                    pt, ptk = nps()
                    ptb = pt[:].bitcast(BF16)
                    for j in range(4):
                        P.op('pe', lambda e: e.transpose(out=ptb[:, j * 128:(j + 1) * 128], in_=kT[:, j, cs], identity=identb[:]),
                             reads=['kT', 'identb'], writes=[ptk])
                    P.op('act', lambda e: e.copy(out=ktok[:], in_=ptb[:, 0:DH]), reads=[ptk], writes=['ktok'])
                    ps_, psk = nps()
                    for j in range(4):
                        P.op('pe', lambda e: e.matmul(ps_[:, 0:128], kT[:, j, cs], qT[:, j, cs], start=(j == 0), stop=(j == 3)),
                             reads=['kT'] + qkeys, writes=[psk])
                    P.op('dve', lambda e: e.scalar_tensor_tensor(out=sTs[:], in0=ps_[:, 0:128], scalar=wcol, in1=mC[:, d, :],
                                                                 op0=ALU.mult, op1=ALU.mult), reads=[psk, 'Wtok', 'mC'], writes=['sTs'])
                    P.op('dve', lambda e: e.tensor_scalar(out=Cst[:], in0=Cst[:], scalar1=alcol, scalar2=None, op0=ALU.mult),
                         reads=['Cst', 'ALb'], writes=['Cst'])
                    P.op('act', lambda e: e.copy(out=Cbf[:], in_=Cst[:]), reads=['Cst'], writes=['Cbf'])
                    P.op('dve', lambda e: e.tensor_scalar(out=nst[:, 0:4], in0=nst[:, 0:4], scalar1=alcol, scalar2=None, op0=ALU.mult),
                         reads=['nst', 'ALb'], writes=['nst'])
                    P.op('dve', lambda e: e.tensor_copy(out=nbf[:], in_=nst[:, 0:4]), reads=['nst'], writes=['nbf'])
                    pn, pnk = nps()
                    for j in range(4):
                        P.op('pe', lambda e: e.matmul(pn[:, 0:DH], qT[:, j, cs], Cbf[:, j, :], start=(j == 0), stop=False),
                             reads=qkeys + ['Cbf'], writes=[pnk])
                    P.op('pe', lambda e: e.matmul(pn[:, 0:DH], sTs[:], vch[:], start=False, stop=True),
                         reads=['sTs', 'vch'], writes=[pnk])
                    pd_, pdk = nps()
                    for j in range(4):
                        P.op('pe', lambda e: e.matmul(pd_[:, 0:1], qT[:, j, cs], nbf[:, j:j + 1], start=(j == 0), stop=False),
                             reads=qkeys + ['nbf'], writes=[pdk])
                    P.op('pe', lambda e: e.matmul(pd_[:, 0:1], sTs[:], onesb[:], start=False, stop=True), reads=['sTs', 'onesb'], writes=[pdk])
                    P.op('act', lambda e: e.activation(out=dstat[:, 2:3], in_=pd_[:, 0:1], func=AF.Abs), reads=[pdk], writes=['dstat'])
                    P.op('dve', lambda e: e.tensor_tensor(out=dstat[:, 0:1], in0=dstat[:, 2:3], in1=thcol, op=ALU.max),
                         reads=['dstat', 'Wtok'], writes=['dstat'])
                    P.op('dve', lambda e: e.reciprocal(out=dstat[:, 1:2], in_=dstat[:, 0:1]), reads=['dstat'], writes=['dstat'])
                    hdst = hacc[c // 4][:, c % 4, :]
                    if d == 0:
                        P.op('act', lambda e: e.activation(out=hdst, in_=pn[:, 0:DH], func=AF.Identity, scale=dstat[:, 1:2]),
                             reads=[pnk, 'dstat'], writes=[('hacc', c)])
                    else:
                        P.op('dve', lambda e: e.scalar_tensor_tensor(out=hdst, in0=pn[:, 0:DH], scalar=dstat[:, 1:2], in1=hdst,
                                                                     op0=ALU.mult, op1=ALU.add), reads=[pnk, 'dstat', ('hacc', c)], writes=[('hacc', c)])
                    P.op('pool', lambda e: e.tensor_scalar(out=vw[:], in0=vch[:], scalar1=wcol, scalar2=None, op0=ALU.mult),
                         reads=['vch', 'Wtok'], writes=['vw'])
                    for j in range(4):
                        pc, pck = nps()
                        P.op('pe', lambda e: e.matmul(pc[:, 0:DH], ktok[:, j * 128:(j + 1) * 128], vw[:], start=True, stop=True),
                             reads=['ktok', 'vw'], writes=[pck])
                        P.op('dve', lambda e: e.tensor_tensor(out=Cst[:, j, :], in0=Cst[:, j, :], in1=pc[:, 0:DH], op=ALU.add),
                             reads=[pck, 'Cst'], writes=['Cst'])
                    pq_, pqk = nps()
                    for j in range(4):
                        P.op('pe', lambda e: e.matmul(pq_[:, j:j + 1], ktok[:, j * 128:(j + 1) * 128], Wtokb[:, d, c, hd:hd + 1],
                                                      start=True, stop=True), reads=['ktok', 'Wtokb'], writes=[pqk])
                    P.op('dve', lambda e: e.tensor_tensor(out=nst[:, 0:4], in0=nst[:, 0:4], in1=pq_[:, 0:4], op=ALU.add),
                         reads=[pqk, 'nst'], writes=['nst'])
                if not is_sample:
                    P.dma(o_C[pidx, d, hd].rearrange("(j p) e -> p j e", p=128), Cst[:], reads=['Cst'], q='pool')
                    P.dma(o_n[pidx, d, hd].rearrange("(j p) -> p j", p=128), nst[:, 0:4], reads=['nst'], q='pool', allow_slow_non_contiguous=True)
            load_w1(6144 + hd * DH, DH)
            for tt in range(NTt):
                hdst = hacc[tt // 4][:, tt % 4, :]
                pz, pk = nps()
                for kc in range(8):
                    P.op('pe', lambda e: e.matmul(pz[:, 0:DH], hT[:, kc, tt * 128:(tt + 1) * 128], wbf1[:, kc, :],
                                                  start=(kc == 0), stop=(kc == 7)), reads=['wbf1', ('hT', tt)], writes=[pk])
                P.op('act', lambda e: e.activation(out=FT[0][:, 0:DH], in_=pz[:, 0:DH], func=AF.Sigmoid), reads=[pk], writes=['FT0'])
                P.op('dve', lambda e: e.tensor_tensor(out=hdst, in0=hdst, in1=FT[0][:, 0:DH], op=ALU.mult),
                     reads=['FT0', ('hacc', tt)], writes=[('hacc', tt)])
                P.op('act', lambda e: e.activation(out=FT[0][:, 0:DH], in_=hdst, func=AF.Square, accum_out=dstat[:, 4:5]),
                     reads=[('hacc', tt), 'FT0'], writes=['FT0', 'dstat'])
                P.op('dve', lambda e: e.tensor_scalar(out=dstat[:, 5:6], in0=dstat[:, 4:5], scalar1=1.0 / DH, scalar2=1e-6,
                                                      op0=ALU.mult, op1=ALU.add), reads=['dstat'], writes=['dstat'])
                P.op('act', lambda e: e.activation(out=dstat[:, 6:7], in_=dstat[:, 5:6], func=AF.Sqrt), reads=['dstat'], writes=['dstat'])
                P.op('dve', lambda e: e.reciprocal(out=dstat[:, 7:8], in_=dstat[:, 6:7]), reads=['dstat'], writes=['dstat'])
                P.op('dve', lambda e: e.tensor_scalar(out=hdst, in0=hdst, scalar1=dstat[:, 7:8], scalar2=None, op0=ALU.mult),
                     reads=[('hacc', tt), 'dstat'], writes=[('hacc', tt)])
            load_w1(8192 + hd * DH, DH)
            for tt in range(NTt):
                hdst = hacc[tt // 4][:, tt % 4, :]
                pz, pk = nps()
                for kc in range(8):
                    P.op('pe', lambda e: e.matmul(pz[:, 0:DH], hT[:, kc, tt * 128:(tt + 1) * 128], wbf1[:, kc, :],
                                                  start=(kc == 0), stop=(kc == 7)), reads=['wbf1', ('hT', tt)], writes=[pk])
                P.op('act', lambda e: e.activation(out=FT[0][:, 0:DH], in_=pz[:, 0:DH], func=AF.Silu), reads=[pk], writes=['FT0'])
                P.op('dve', lambda e: e.tensor_tensor(out=hdst, in0=hdst, in1=FT[0][:, 0:DH], op=ALU.mult),
                     reads=['FT0', ('hacc', tt)], writes=[('hacc', tt)])
                pz, pk = nps()
                for j in range(4):
                    P.op('pe', lambda e: e.transpose(out=pz[:, j * 128:(j + 1) * 128], in_=hdst[:, j * 128:(j + 1) * 128], identity=ident[:]),
                         reads=[('hacc', tt), 'ident'], writes=[pk])
                for j in range(4):
                    P.op('act', lambda e: e.activation(out=yT[:, j, tt * 128:(tt + 1) * 128], in_=pz[:, j * 128:(j + 1) * 128],
                                                       func=AF.Identity, scale=mng[:, hd * 4 + j:hd * 4 + j + 1]),
                         reads=[pk, 'mng'], writes=[('yT', j)])

        for si in seq_ids:
            off, T, cidx, is_sample = SEQS[si]
            P.barrier()
            make_gate(1, cidx)
            make_hT(1, x1, 'x1', off, T, cidx)
            P.barrier()
            gates_seq(T, is_sample)
            if not is_sample:
                for d in range(2):
                    lastc = (T // LC - 1) if d == 0 else 0
                    P.dma(o_m[si - 1, d, :].rearrange("(h o) -> h o", o=1), sm['M'][32 * d:32 * d + 4, lastc:lastc + 1],
                          reads=['sm_M'], q='pool')
            for hd in range(4):
                P.barrier()
                mlstm_head(hd, off, T, is_sample, si - 1)
                if debug and debug.get('dump_y'):
                    for j in range(4):
                        dump("yM%d_%d" % (hd, j), yT[:, j, 0:T], [('yT', j)], T, col0=off)
                P.barrier()
                load_wo(w_out_odd[hd * DH:(hd + 1) * DH, :], 128, nk=4)
                last = (hd == 3)
                outproj(1, [(128, s_) for s_ in range(4)], x1, 'x1', (yout if last else x1), ('yout' if last else 'x1'),
                        off, T, cidx, final=last)
        P.barrier()
        L1.close()
    P.finish()
    sems = {s: es.enter_context(nc.semaphore(s)) for s in P.sem_names}
    P.emit(sems)
    es.close()
    global _last_dslot
    _last_dslot = dslot if debug else {}
    return nc, P


def host_inputs(inp, core):
    f = lambda a: np.ascontiguousarray(a, dtype=np.float32)
    b = core % 2
    m = {}
    m["xin"] = f(np.concatenate([inp["x_sample"][b], inp["x_prompt"][2 * core], inp["x_prompt"][2 * core + 1]], axis=0))
    cond = np.stack([inp["c"][b], inp["c_ctx"]], axis=0)
    m["condT"] = f(cond.reshape(2, 8, 128).transpose(2, 1, 0))
    m["s_hgrn"] = f(inp["state_hgrn"][b, 0])
    m["s_rwkv"] = f(inp["state_rwkv"][b, 0])
    m["s_C"] = f(inp["state_mlstm_C"][b, 0])
    m["s_n"] = f(inp["state_mlstm_n"][b, 0])
    m["s_m"] = f(inp["state_mlstm_m"][b, 0])
    m["w_mod"] = f(inp["w_mod"])
    m["b_modT"] = f(inp["b_mod"].reshape(2, 24, 128).transpose(2, 0, 1))
    m["norm_gT"] = f(inp["norm_g"].reshape(2, 8, 128).transpose(2, 0, 1))
    m["fnorm_gT"] = f(inp["final_norm_g"].reshape(8, 128).T)
    w = inp["w_in_even"][0]
    DA = 1024
    wA = np.stack([np.concatenate([w[:, g * DA + h * 128: g * DA + (h + 1) * 128] for g in (0, 1, 4, 2, 3)], axis=1)
                   for h in range(8)], axis=0)
    m["wA"] = f(wA)
    o = 5 * DA
    zb0 = o + 3328
    wB = np.stack([np.concatenate([w[:, o + g * 1024 + h * 64: o + g * 1024 + (h + 1) * 64] for g in (0, 1, 2)]
                                  + [w[:, zb0 + h * 64: zb0 + (h + 1) * 64]], axis=1) for h in range(16)], axis=0)
    m["wB"] = f(wB)
    m["wLR"] = f(w[:, o + 3072: o + 3328])
    m["w_out_even"] = f(inp["w_out_even"][0])
    m["lbT"] = f(inp["hgrn_lb_logits"].reshape(2, 8, 128).transpose(2, 0, 1))
    m["hg_gT"] = f(inp["hgrn_norm_g"][0].reshape(8, 128).T)
    mu = inp["rwkv_shift_mu"][0]
    mr = np.zeros((64, 2, 4, 16), np.float32)
    for g in range(3):
        mr[:, :, g, :] = mu[:, g * 1024:(g + 1) * 1024].reshape(2, 16, 64).transpose(2, 0, 1)
    m["mu_rkv"] = mr
    m["mu_lr"] = f(mu[:, 3072:3328].reshape(2, 4, 64).transpose(2, 0, 1))
    m["w0T"] = f(inp["rwkv_w0"][0].reshape(2, 16, 64).transpose(2, 0, 1))
    m["a0T"] = f(inp["rwkv_a0"][0].reshape(2, 16, 64).transpose(2, 0, 1))
    m["w2"] = f(inp["rwkv_w2"][0])
    m["a2"] = f(inp["rwkv_a2"][0])
    m["kkT"] = f(inp["rwkv_k_k"][0].reshape(16, 64).T)
    m["kaT"] = f(inp["rwkv_k_a"][0].reshape(16, 64).T)
    m["rkT"] = f(inp["rwkv_r_k"][0].T)
    m["gngT"] = f(inp["rwkv_gn_g"][0].reshape(16, 64).T)
    m["gnbT"] = f(inp["rwkv_gn_b"][0].reshape(16, 64).T)
    s = np.arange(128)[:, None]
    t = np.arange(128)[None, :]
    same = (s // 32) == (t // 32)
    m["maskH"] = np.stack([(same & (s <= t)), (same & (s >= t))]).astype(np.float32)
    m["ident_in"] = np.eye(128, dtype=np.float32)
    s6 = np.arange(64)[:, None]
    t6 = np.arange(64)[None, :]
    mr_ = np.zeros((2, 3, 64, 128), np.float32)
    for d_, (st_, inc_) in enumerate([((s6 < t6), (s6 <= t6)), ((s6 > t6), (s6 >= t6))]):
        st_ = st_.astype(np.float32)
        inc_ = inc_.astype(np.float32)
        mr_[d_, 0, :, 0:64] = -st_
        mr_[d_, 0, :, 64:128] = -inc_
        mr_[d_, 1, :, 0:64] = st_
        mr_[d_, 1, :, 64:128] = inc_
        mr_[d_, 2, :, 0:64] = -(st_.T)
    m["maskR"] = mr_
    m["w_in_odd"] = f(inp["w_in_odd"][0])
    m["w_out_odd"] = f(inp["w_out_odd"][0])
    m["maskC"] = np.stack([(s <= t), (s >= t)]).astype(np.float32)
    sel = np.zeros((36, 4, 128), np.float32)
    gbt = np.zeros((36, 4), np.float32)
    for pb_ in (0, 32):
        for k_ in range(4):
            sel[pb_ + k_, k_, :] = 1.0
        gbt[pb_:pb_ + 4, :] = inp["mlstm_gate_b"][0].T
    m["sel_d"] = sel
    m["gbT_d"] = gbt
    m["cw_d"] = f(inp["mlstm_conv_w"][0].reshape(9, 32, 128).transpose(2, 1, 0))
    m["cb_d"] = f(inp["mlstm_conv_b"][0].reshape(32, 128).T)
    m["mng_d"] = f(inp["mlstm_norm_g"][0].reshape(16, 128).T)
    return m


def kernel(**inp):
    inp = {k: np.asarray(v) for k, v in inp.items()}
    nc, P = build()
    in_maps = [host_inputs(inp, c) for c in range(NCORES)]
    res = run_bass_kernel_spmd(nc, in_maps, core_ids=list(range(NCORES)))
    r = res.results
    y_prompt = np.zeros((16, TP, D), np.float32)
    y_sample = np.zeros((2, TS, D), np.float32)
    for c in range(NCORES):
        y_prompt[2 * c] = r[c]["yout"][TS:TS + TP]
        y_prompt[2 * c + 1] = r[c]["yout"][TS + TP:]
    for b in range(2):
        y_sample[b] = r[b]["yout"][0:TS]
    new_hgrn = np.concatenate([r[c]["o_hgrn"] for c in range(NCORES)], axis=0)[:, None]
    new_rwkv = np.concatenate([r[c]["o_rwkv"] for c in range(NCORES)], axis=0)[:, None]
    new_C = np.concatenate([r[c]["o_C"] for c in range(NCORES)], axis=0)[:, None]
    new_n = np.concatenate([r[c]["o_n"] for c in range(NCORES)], axis=0)[:, None]
    new_m = np.concatenate([r[c]["o_m"] for c in range(NCORES)], axis=0)[:, None]
    return (y_prompt, y_sample, new_hgrn.astype(np.float32), new_rwkv.astype(np.float32),
            new_C.astype(np.float32), new_n.astype(np.float32), new_m.astype(np.float32))
```

```python
import contextlib
import numpy as np
import concourse.bass as bass
import concourse.mybir as mybir
from concourse.bass_utils import run_bass_kernel_spmd

F32 = mybir.dt.float32
BF16 = mybir.dt.bfloat16
ALU = mybir.AluOpType
AF = mybir.ActivationFunctionType
AX = mybir.AxisListType

D = 1024
TS = 2048
TP = 256
TT = TS + 2 * TP
NCORES = 8


class _Rec:
    def __init__(self):
        self.calls = []

    def __getattr__(self, name):
        def f(*a, **k):
            self.calls.append((name, a, k))
            return self
        return f


class Prog:
    ENGS = ['pe', 'dve', 'act', 'pool', 'sp']
    NDMA = 16

    def __init__(self, nc):
        self.nc = nc
        self.ops = {e: [] for e in self.ENGS}
        self.cnt = {}
        self.waited = {e: {} for e in self.ENGS}
        self.last_write = {}
        self.readers = {}
        self.dma_rr = 0
        self.sem_names = list(self.ENGS) + ['d%d' % i for i in range(self.NDMA)]
        for s in self.sem_names:
            self.cnt[s] = 0
        self.n_ops = 0

    def _deps(self, eng, reads, writes):
        deps = {}

        def add(p):
            if p is None:
                return
            f, n = p
            if f == 'pe' and eng == 'pe':
                return
            if n > deps.get(f, 0):
                deps[f] = n
        for k in reads:
            add(self.last_write.get(k))
        for k in writes:
            add(self.last_write.get(k))
            for p in self.readers.get(k, ()):
                add(p)
        waits = []
        for f, n in deps.items():
            if n > self.waited[eng].get(f, 0):
                waits.append((f, n))
                self.waited[eng][f] = n
        return waits

    def _commit(self, tag, reads, writes):
        for k in reads:
            lst = self.readers.setdefault(k, [])
            lst[:] = [p for p in lst if p[0] != tag[0]]
            lst.append(tag)
        for k in writes:
            self.last_write[k] = tag
            self.readers[k] = []

    def op(self, eng, fn, reads=(), writes=()):
        rec = _Rec()
        fn(rec)
        name, a, k = rec.calls[0]
        fn = (lambda e, name=name, a=a, k=k: getattr(e, name)(*a, **k))
        waits = self._deps(eng, reads, writes)
        self.cnt[eng] += 1
        tag = (eng, self.cnt[eng])
        self.ops[eng].append((waits, fn, eng, 1))
        self._commit(tag, reads, writes)
        self.n_ops += 1

    def dma(self, out, in_, reads=(), writes=(), q='sp', **kw):
        d = 'd%d' % self.dma_rr
        self.dma_rr = (self.dma_rr + 1) % self.NDMA
        waits = self._deps(q, reads, writes)
        prev = self.cnt[d]
        if prev > self.waited[q].get(d, 0):
            waits.append((d, prev))
            self.waited[q][d] = prev
        self.cnt[d] += 16
        tag = (d, self.cnt[d])
        self.ops[q].append((waits, (lambda e: e.dma_start(out=out, in_=in_, **kw)), d, 16))
        self._commit(tag, reads, writes)
        self.n_ops += 1

    def barrier(self):
        allsems = list(self.sem_names)
        for e in self.ENGS:
            waits = []
            for f in allsems:
                if self.cnt[f] > self.waited[e].get(f, 0):
                    waits.append((f, self.cnt[f]))
                    self.waited[e][f] = self.cnt[f]
            self.ops[e].append((waits, None, None, 0))

    def finish(self, q='sp'):
        waits = []
        for i in range(self.NDMA):
            d = 'd%d' % i
            if self.cnt[d] > self.waited[q].get(d, 0):
                waits.append((d, self.cnt[d]))
                self.waited[q][d] = self.cnt[d]
        self.ops[q].append((waits, None, None, 0))

    def emit(self, sems):
        ops = self.ops

        def run(e, lst):
            for waits, fn, semname, inc in lst:
                for f, n in waits:
                    e.wait_ge(sems[f], n)
                if fn is not None:
                    fn(e).then_inc(sems[semname], inc)
        with self.nc.Block() as block:
            @block.tensor
            def _(e):
                run(e, ops['pe'])

            @block.vector
            def _(e):
                run(e, ops['dve'])

            @block.scalar
            def _(e):
                run(e, ops['act'])

            @block.gpsimd
            def _(e):
                run(e, ops['pool'])

            @block.sync
            def _(e):
                run(e, ops['sp'])


SEQS = [(0, TS, 0, True), (TS, TP, 1, False), (TS + TP, TP, 1, False)]


def build(debug=None):
    nc = bass.Bass('TRN2', target_bir_lowering=False)
    P = Prog(nc)
    es = contextlib.ExitStack()

    def din(name, shape):
        return nc.dram_tensor(name, list(shape), F32, kind="ExternalInput").ap()

    def dout(name, shape):
        return nc.dram_tensor(name, list(shape), F32, kind="ExternalOutput").ap()

    xin = din("xin", [TT, D])
    condT = din("condT", [128, 8, 2])
    s_hgrn = din("s_hgrn", [2, 8, 128, 128])
    s_rwkv = din("s_rwkv", [2, 16, 64, 64])
    s_C = din("s_C", [2, 4, 512, 512])
    s_n = din("s_n", [2, 4, 512])
    s_m = din("s_m", [2, 4])
    w_mod = din("w_mod", [2, D, 3 * D])
    b_modT = din("b_modT", [128, 2, 24])
    norm_gT = din("norm_gT", [128, 2, 8])
    fnorm_gT = din("fnorm_gT", [128, 8])
    wA = din("wA", [8, D, 640])
    wB = din("wB", [16, D, 256])
    wLR = din("wLR", [D, 256])
    w_out_even = din("w_out_even", [2 * D, D])
    lbT = din("lbT", [128, 2, 8])
    hg_gT = din("hg_gT", [128, 8])
    mu_rkv = din("mu_rkv", [64, 2, 4, 16])
    mu_lr = din("mu_lr", [64, 2, 4])
    w0T = din("w0T", [64, 2, 16])
    a0T = din("a0T", [64, 2, 16])
    w2 = din("w2", [2, 64, D])
    a2 = din("a2", [2, 64, D])
    kkT = din("kkT", [64, 16])
    kaT = din("kaT", [64, 16])
    rkT = din("rkT", [64, 16])
    gngT = din("gngT", [64, 16])
    gnbT = din("gnbT", [64, 16])
    maskR = din("maskR", [2, 3, 64, 128])
    maskH = din("maskH", [2, 128, 128])
    ident_d = din("ident_in", [128, 128])
    w_in_odd = din("w_in_odd", [D, 10256])
    w_out_odd = din("w_out_odd", [2 * D, D])
    maskC = din("maskC", [2, 128, 128])
    sel_d = din("sel_d", [36, 4, 128])
    gbT_d = din("gbT_d", [36, 4])
    cw_d = din("cw_d", [128, 32, 9])
    cb_d = din("cb_d", [128, 32])
    mng_d = din("mng_d", [128, 16])

    yout = dout("yout", [TT, D])
    o_hgrn = dout("o_hgrn", [2, 2, 8, 128, 128])
    o_rwkv = dout("o_rwkv", [2, 2, 16, 64, 64])
    o_C = dout("o_C", [2, 2, 4, 512, 512])
    o_n = dout("o_n", [2, 2, 4, 512])
    o_m = dout("o_m", [2, 2, 4])
    dbg = dout("dbg", [40, 128, TT]) if debug else None
    dslot = {}
    dumpt = {}
    x1 = dout("x1", [TT, D]) if debug else nc.dram_tensor("x1", [TT, D], F32, kind="Internal").ap()

    def sb(name, shape, dt=F32):
        return es.enter_context(nc.sbuf_tensor(name, list(shape), dt))

    pstiles = [es.enter_context(nc.psum_tensor("ps%d" % i, [128, 512], F32)) for i in range(8)]
    psrr = [0]

    def nps():
        i = psrr[0]
        psrr[0] = (i + 1) % 8
        return pstiles[i], 'ps%d' % i

    def dump(name, ap, keys, n, col0=0, parts=128):
        if not debug:
            return
        slot = dslot.setdefault(name, len(dslot))
        dt_ = dumpt['tile']
        for c0 in range(0, n, 512):
            w_ = min(512, n - c0)
            P.op('pool', (lambda e, c0=c0, w_=w_: e.tensor_copy(out=dt_[0:parts, 0:w_], in_=ap[:, c0:c0 + w_])),
                 reads=keys, writes=['dumpt'])
            P.dma(dbg[slot, 0:parts, col0 + c0:col0 + c0 + w_], dt_[0:parts, 0:w_], reads=['dumpt'])

    if debug:
        dumpt['tile'] = sb("dumpt", [128, 512])

    ident = sb("ident", [128, 128])
    ones = sb("ones", [128, 128])
    P.dma(ident[:], ident_d[:], writes=['ident'])
    P.op('dve', lambda e: e.memset(ones[:], 1.0), writes=['ones'])

    condT_sb = sb("condT_sb", [128, 8, 2])
    bmod_sb = sb("bmod_sb", [128, 2, 24])
    ng_sb = sb("ng_sb", [128, 2, 8])
    fng_sb = sb("fng_sb", [128, 8])
    lb_sb = sb("lb_sb", [128, 2, 8])
    hgg_sb = sb("hgg_sb", [128, 8])
    for t_, d_, k_ in [(condT_sb, condT, 'condT'), (bmod_sb, b_modT, 'bmod'), (ng_sb, norm_gT, 'ng'),
                       (fng_sb, fnorm_gT, 'fng'), (lb_sb, lbT, 'lb'), (hgg_sb, hg_gT, 'hgg')]:
        P.dma(t_[:], d_[:], writes=[k_])

    scT = sb("scT", [128, 8, 2])
    P.op('act', lambda e: e.activation(out=scT[:], in_=condT_sb[:], func=AF.Silu), reads=['condT'], writes=['scT'])
    mT = sb("mT", [128, 2, 24, 2])
    sc1 = sb("sc1", [128, 2, 8, 2])
    gate_bc = sb("gate_bc", [128, D])
    dg = sb("dg", [128, 128])

    def make_gate(l, c):
        if True:
            for half in range(2):
                pz, pk = nps()
                for kq in range(4):
                    kc = half * 4 + kq
                    P.op('dve', lambda e: e.tensor_scalar(
                        out=dg[:], in0=ident[:], scalar1=mT[:, l, 16 + kc, c:c + 1], scalar2=None, op0=ALU.mult),
                        reads=['ident', 'mT'], writes=['dg'])
                    P.op('pe', lambda e: e.matmul(pz[:, kq * 128:(kq + 1) * 128], ones[:], dg[:], start=True, stop=True),
                         reads=['ones', 'dg'], writes=[pk])
                P.op('act', lambda e: e.copy(out=gate_bc[:, half * 512:(half + 1) * 512], in_=pz[:]),
                     reads=[pk], writes=['gate_bc'])

    with contextlib.ExitStack() as es2:
        wm = [es2.enter_context(nc.sbuf_tensor("wm%d" % i, [128, 8, 512], F32)) for i in range(2)]
        for l in range(2):
            for cbk in range(6):
                i = (l * 6 + cbk) % 2
                P.dma(wm[i][:], w_mod[l].rearrange("(kc p) n -> p kc n", p=128)[:, :, cbk * 512:(cbk + 1) * 512],
                      writes=['wm%d' % i])
                pz, pk = nps()
                for j in range(4):
                    for kc in range(8):
                        P.op('pe', lambda e: e.matmul(pz[:, j * 2:(j + 1) * 2], wm[i][:, kc, j * 128:(j + 1) * 128],
                                                      scT[:, kc, :], start=(kc == 0), stop=(kc == 7)),
                             reads=['wm%d' % i, 'scT'], writes=[pk])
                P.op('dve', lambda e: e.tensor_tensor(out=mT[:, l, cbk * 4:(cbk + 1) * 4, :],
                                                      in0=pz[:, 0:8].rearrange("p (j c) -> p j c", c=2),
                                                      in1=bmod_sb[:, l, cbk * 4:(cbk + 1) * 4].unsqueeze(2).to_broadcast([128, 4, 2]),
                                                      op=ALU.add),
                     reads=[pk, 'bmod'], writes=['mT'])
    P.barrier()
    for l in range(2):
        P.op('dve', lambda e: e.scalar_tensor_tensor(
            out=sc1[:, l], in0=mT[:, l, 8:16, :], scalar=1.0,
            in1=ng_sb[:, l, :].unsqueeze(2).to_broadcast([128, 8, 2]), op0=ALU.add, op1=ALU.mult),
            reads=['mT', 'ng'], writes=['sc1'])
    fng_holder = {}

    def make_fng():
        fng_bc = sb("fng_bc", [128, D])
        fng_holder['t'] = fng_bc
        for half in range(2):
            pz, pk = nps()
            for kq in range(4):
                kc = half * 4 + kq
                P.op('dve', (lambda e, kc=kc: e.tensor_scalar(
                    out=dg[:], in0=ident[:], scalar1=fng_sb[:, kc:kc + 1], scalar2=None, op0=ALU.mult)),
                    reads=['ident', 'fng'], writes=['dg'])
                P.op('pe', (lambda e, pz=pz, kq=kq: e.matmul(pz[:, kq * 128:(kq + 1) * 128], ones[:], dg[:],
                                                             start=True, stop=True)),
                     reads=['ones', 'dg'], writes=[pk])
            P.op('act', (lambda e, half=half, pz=pz: e.copy(out=fng_bc[:, half * 512:(half + 1) * 512], in_=pz[:])),
                 reads=[pk], writes=['fng_bc'])

    lbv = sb("lbv", [128, 8])
    oml = sb("oml", [128, 8])
    P.op('dve', lambda e: e.tensor_tensor(out=lbv[:], in0=lb_sb[:, 0, :], in1=lb_sb[:, 1, :], op=ALU.subtract),
         reads=['lb'], writes=['lbv'])
    P.op('act', lambda e: e.activation(out=lbv[:], in_=lbv[:], func=AF.Sigmoid), reads=['lbv'], writes=['lbv'])
    P.op('act', lambda e: e.activation(out=oml[:], in_=lbv[:], func=AF.Identity, bias=1.0, scale=-1.0),
         reads=['lbv'], writes=['oml'])

    hT = sb("hT", [128, 8, TS], BF16)
    yT = sb("yT", [128, 8, TS], BF16)
    st4 = sb("st4", [128, 4])
    wst = sb("wst", [128, 8, 256])
    FT = [sb("FT%d" % i, [128, TS + 32]) for i in range(6)]
    xt = [FT[0][:, 0:D], FT[0][:, D:2 * D]]
    xn = FT[1][:, 0:D]
    junk = FT[1][:, D:2 * D]
    wo_v = [FT[2][:, 0:TS].bitcast(BF16).rearrange("p (s n) -> p s n", n=D),
            FT[3][:, 0:TS].bitcast(BF16).rearrange("p (s n) -> p s n", n=D)]

    def load_wo(src, parts, nk=8):
        v = src.rearrange("(kc p) n -> p kc n", p=parts)
        for c0 in range(0, D, 256):
            w_ = min(256, D - c0)
            P.dma(wst[0:parts, 0:nk, 0:w_], v[:, :, c0:c0 + w_], writes=['wst'])
            for hf in range(nk // 4):
                P.op('pool', lambda e: e.tensor_copy(out=wo_v[hf][0:parts, :, c0:c0 + w_], in_=wst[0:parts, hf * 4:hf * 4 + 4, 0:w_]),
                     reads=['wst'], writes=['wo_bf'])

    def make_hT(layer, xsrc, xkey, off, T, cidx):
        for tt in range(T // 128):
            i = tt % 2
            P.dma(xt[i], xsrc[off + tt * 128: off + (tt + 1) * 128, :], reads=[(xkey, off // 128 + tt)], writes=['xt%d' % i])
            P.op('act', lambda e: e.activation(out=junk, in_=xt[i], func=AF.Square, accum_out=st4[:, 0:1]),
                 reads=['xt%d' % i], writes=['junk', 'st4'])
            P.op('dve', lambda e: e.tensor_scalar(out=st4[:, 1:2], in0=st4[:, 0:1], scalar1=1.0 / D, scalar2=1e-6,
                                                  op0=ALU.mult, op1=ALU.add), reads=['st4'], writes=['st4'])
            P.op('act', lambda e: e.activation(out=st4[:, 2:3], in_=st4[:, 1:2], func=AF.Sqrt), reads=['st4'], writes=['st4'])
            P.op('dve', lambda e: e.reciprocal(out=st4[:, 3:4], in_=st4[:, 2:3]), reads=['st4'], writes=['st4'])
            P.op('dve', lambda e: e.tensor_scalar(out=xn, in0=xt[i], scalar1=st4[:, 3:4], scalar2=None, op0=ALU.mult),
                 reads=['xt%d' % i, 'st4'], writes=['xn'])
            for half in range(2):
                pz, pk = nps()
                for kq in range(4):
                    kc = half * 4 + kq
                    P.op('pe', lambda e: e.transpose(out=pz[:, kq * 128:(kq + 1) * 128], in_=xn[:, kc * 128:(kc + 1) * 128],
                                                     identity=ident[:]), reads=['xn', 'ident'], writes=[pk])
                for kq in range(4):
                    kc = half * 4 + kq
                    P.op('act', lambda e: e.activation(
                        out=hT[:, kc, tt * 128:(tt + 1) * 128], in_=pz[:, kq * 128:(kq + 1) * 128], func=AF.Identity,
                        bias=mT[:, layer, kc, cidx:cidx + 1], scale=sc1[:, layer, kc, cidx:cidx + 1]),
                        reads=[pk, 'mT', 'sc1'], writes=[('hT', tt)])

    def hT_keys(t0, t1):
        return [('hT', tt) for tt in range(t0 // 128, (t1 + 127) // 128)]

    def load_w(src, ncols, dst=None, dkey='wbf', parts=128, nk=8):
        dst = wbf if dst is None else dst
        v = src.rearrange("(kc p) n -> p kc n", p=parts)
        for c0 in range(0, ncols, 256):
            w_ = min(256, ncols - c0)
            P.dma(wst[0:parts, 0:nk, 0:w_], v[:, :, c0:c0 + w_], writes=['wst'])
            P.op('pool', lambda e: e.tensor_copy(out=dst[0:parts, 0:nk, c0:c0 + w_], in_=wst[0:parts, 0:nk, 0:w_]),
                 reads=['wst'], writes=[dkey])

    def proj(c0, M, t0, n, evac):
        pz, pk = nps()
        for kc in range(8):
            P.op('pe', lambda e: e.matmul(pz[0:M, 0:n], wbf[:, kc, c0:c0 + M], hT[:, kc, t0:t0 + n],
                                          start=(kc == 0), stop=(kc == 7)),
                 reads=['wbf'] + hT_keys(t0, t0 + n), writes=[pk])
        evac(pz, pk)

    def outproj(layer, groups, xsrc, skey, xdst, dkey, off, T, cidx, final=False):
        for tt in range(T // 128):
            i = tt % 2
            P.dma(xt[i], xsrc[off + tt * 128: off + (tt + 1) * 128, :], reads=[(skey, off // 128 + tt)], writes=['xt%d' % i])
            for half in range(2):
                pz, pk = nps()
                for gi, (K, slot) in enumerate(groups):
                    P.op('pe', lambda e: e.matmul(pz[:, 0:512], yT[0:K, slot, tt * 128:(tt + 1) * 128],
                                                  wo_v[slot // 4][0:K, slot % 4, half * 512:(half + 1) * 512],
                                                  start=(gi == 0), stop=(gi == len(groups) - 1)),
                         reads=[('yT', slot), 'wo_bf'], writes=[pk])
                P.op('dve', lambda e: e.tensor_tensor(out=xn[:, half * 512:(half + 1) * 512], in0=pz[:, 0:512],
                                                      in1=gate_bc[:, half * 512:(half + 1) * 512], op=ALU.mult),
                     reads=[pk, 'gate_bc'], writes=['xn'])
                P.op('pool', lambda e: e.tensor_tensor(out=xt[i][:, half * 512:(half + 1) * 512],
                                                       in0=xt[i][:, half * 512:(half + 1) * 512],
                                                       in1=xn[:, half * 512:(half + 1) * 512], op=ALU.add),
                     reads=['xn', 'xt%d' % i], writes=['xt%d' % i])
            if final:
                P.op('act', lambda e: e.activation(out=junk, in_=xt[i], func=AF.Square, accum_out=st4[:, 0:1]),
                     reads=['xt%d' % i], writes=['junk', 'st4'])
                P.op('dve', lambda e: e.tensor_scalar(out=st4[:, 1:2], in0=st4[:, 0:1], scalar1=1.0 / D, scalar2=1e-6,
                                                      op0=ALU.mult, op1=ALU.add), reads=['st4'], writes=['st4'])
                P.op('act', lambda e: e.activation(out=st4[:, 2:3], in_=st4[:, 1:2], func=AF.Sqrt), reads=['st4'], writes=['st4'])
                P.op('dve', lambda e: e.reciprocal(out=st4[:, 3:4], in_=st4[:, 2:3]), reads=['st4'], writes=['st4'])
                P.op('dve', lambda e: e.scalar_tensor_tensor(out=xt[i], in0=xt[i], scalar=st4[:, 3:4], in1=fng_holder['t'][:],
                                                             op0=ALU.mult, op1=ALU.mult),
                     reads=['xt%d' % i, 'st4', 'fng_bc'], writes=['xt%d' % i])
            P.dma(xdst[off + tt * 128: off + (tt + 1) * 128, :], xt[i], reads=['xt%d' % i], writes=[(dkey, off // 128 + tt)], q='pool')

    L0 = contextlib.ExitStack()

    def sb0(name, shape, dt=F32):
        return L0.enter_context(nc.sbuf_tensor(name, list(shape), dt))

    wbf = sb0("wbf", [128, 8, 384], BF16)
    TB = 256
    Fq, Fsz, Fvr, For = FT[0][:, 0:TS], FT[1][:, 0:TS], FT[2][:, 0:TS], FT[3][:, 0:TS]
    Fv = Fvr.rearrange("p (j c) -> p j c", c=128)
    Fo = For.rearrange("p (j c) -> p j c", c=128)
    BT = [sb0("BT%d" % i, [128, 256]) for i in range(18)]
    bt = {n_: BT[i] for i, n_ in enumerate(['sg', 'lf', 'kg', 'G', 'br', 'E', 'Ei', 'qt', 'kt', 'kh', 'vT'])}
    khtok = sb0("khtok", [128, TB // 128, 128])
    gam = sb0("gam", [128, TB // 32])
    gref = sb0("gref", [128, TB // 32])
    Sst = [sb0("Sst%d" % i, [128, 128]) for i in range(2)]
    attT = sb0("attT", [128, 128])
    ostat = sb0("ostat", [128, TS // 128, 4])
    mH = sb0("mH", [128, 2, 128])
    P.dma(mH[:], maskH.rearrange("d s t -> s d t"), writes=['mH'])

    def hgrn_head(h, off, T, is_sample, pidx):
        load_w(wA[h][:, 0:384], 384)
        tb = min(TB, T)
        nblk = T // tb
        for b in range(nblk):
            t0 = b * tb
            proj(0, 128, t0, tb, lambda pz, pk: P.op(
                'act', lambda e: e.copy(out=Fq[:, t0:t0 + tb], in_=pz[:, 0:tb]), reads=[pk], writes=['Fq']))
            proj(256, 128, t0, tb, lambda pz, pk: P.op(
                'act', lambda e: e.activation(out=Fsz[:, t0:t0 + tb], in_=pz[:, 0:tb], func=AF.Silu), reads=[pk], writes=['Fsz']))
            proj(128, 128, t0, tb, lambda pz, pk: P.op(
                'dve', lambda e: e.tensor_copy(out=bt['vT'][:, 0:tb], in_=pz[:, 0:tb]), reads=[pk], writes=['b_vT']))
            pz, pk = nps()
            for j in range(tb // 128):
                P.op('pe', lambda e: e.transpose(out=pz[:, j * 128:(j + 1) * 128], in_=bt['vT'][:, j * 128:(j + 1) * 128],
                                                 identity=ident[:]), reads=['b_vT', 'ident'], writes=[pk])
            P.op('dve', lambda e: e.tensor_copy(out=Fv[:, t0 // 128:(t0 + tb) // 128, :],
                                                in_=pz[:, 0:tb].rearrange("p (j c) -> p j c", c=128)),
                 reads=[pk], writes=['Fv'])
        load_w(wA[h][:, 384:640], 256)
        for d in range(2):
            rev = (d == 1)
            cur = 0
            if is_sample:
                P.dma(Sst[0][:], s_hgrn[d, h], writes=['Sst0'])
            else:
                P.op('pool', lambda e: e.memset(Sst[0][:], 0.0), writes=['Sst0'])
            blks = list(range(nblk))
            if rev:
                blks = blks[::-1]
            for b in blks:
                t0 = b * tb
                nch = tb // 32
                sg, lf, kg, G, br, E, Ei, qt, kt, kh = [bt[n_] for n_ in ['sg', 'lf', 'kg', 'G', 'br', 'E', 'Ei', 'qt', 'kt', 'kh']]
                proj(128 * d, 128, t0, tb, lambda pz, pk: P.op(
                    'act', lambda e: e.activation(out=sg[:, 0:tb], in_=pz[:, 0:tb], func=AF.Sigmoid), reads=[pk], writes=['b_sg']))
                P.op('dve', lambda e: e.tensor_scalar(out=sg[:, 0:tb], in0=sg[:, 0:tb], scalar1=oml[:, h:h + 1],
                                                      scalar2=lbv[:, h:h + 1], op0=ALU.mult, op1=ALU.add),
                     reads=['b_sg', 'oml', 'lbv'], writes=['b_sg'])
                P.op('act', lambda e: e.activation(out=lf[:, 0:tb], in_=sg[:, 0:tb], func=AF.Ln), reads=['b_sg'], writes=['b_lf'])
                P.op('pool', lambda e: e.tensor_scalar(out=kg[:, 0:tb], in0=sg[:, 0:tb], scalar1=-1.0, scalar2=1.0,
                                                       op0=ALU.mult, op1=ALU.add), reads=['b_sg'], writes=['b_kg'])
                P.op('dve', lambda e: e.memset(E[:, 0:tb], 0.0), writes=['b_E'])
                if not rev:
                    P.op('dve', lambda e: e.tensor_tensor_scan(out=G[:, 0:tb], data0=lf[:, 0:tb], data1=E[:, 0:tb],
                                                               initial=0.0, op0=ALU.add, op1=ALU.add),
                         reads=['b_lf', 'b_E'], writes=['b_G'])
                    ci_ = 0
                else:
                    P.op('dve', lambda e: e.tensor_tensor_scan(out=G[:, 0:tb][:, ::-1], data0=lf[:, 0:tb][:, ::-1],
                                                               data1=E[:, 0:tb], initial=0.0, op0=ALU.add, op1=ALU.add),
                         reads=['b_lf', 'b_E'], writes=['b_G'])
                    ci_ = 31
                G3 = G[:, 0:tb].rearrange("p (c l) -> p c l", l=32)
                lf3 = lf[:, 0:tb].rearrange("p (c l) -> p c l", l=32)
                P.op('dve', lambda e: e.tensor_tensor(out=gref[:, 0:nch], in0=G3[:, :, ci_], in1=lf3[:, :, ci_], op=ALU.subtract),
                     reads=['b_G', 'b_lf'], writes=['gref'])
                P.op('dve', lambda e: e.tensor_tensor(out=br[:, 0:tb].rearrange("p (c l) -> p c l", l=32), in0=G3,
                                                      in1=gref[:, 0:nch].unsqueeze(2).to_broadcast([128, nch, 32]), op=ALU.subtract),
                     reads=['b_G', 'gref'], writes=['b_br'])
                bend = br[:, 0:tb].rearrange("p (c l) -> p c l", l=32)[:, :, (0 if rev else 31)]
                P.op('act', lambda e: e.activation(out=gam[:, 0:nch], in_=bend, func=AF.Exp), reads=['b_br'], writes=['gam'])
                P.op('act', lambda e: e.activation(out=E[:, 0:tb], in_=br[:, 0:tb], func=AF.Exp), reads=['b_br'], writes=['b_E'])
                P.op('act', lambda e: e.activation(out=Ei[:, 0:tb], in_=br[:, 0:tb], func=AF.Exp, scale=-1.0),
                     reads=['b_br'], writes=['b_Ei'])
                P.op('dve', lambda e: e.tensor_tensor(out=qt[:, 0:tb], in0=Fq[:, t0:t0 + tb], in1=E[:, 0:tb], op=ALU.mult),
                     reads=['Fq', 'b_E'], writes=['b_qt'])
                P.op('pool', lambda e: e.tensor_tensor(out=kt[:, 0:tb], in0=kg[:, 0:tb], in1=Ei[:, 0:tb], op=ALU.mult),
                     reads=['b_kg', 'b_Ei'], writes=['b_kt'])
                P.op('dve', lambda e: e.tensor_tensor(out=kh[:, 0:tb].rearrange("p (c l) -> p c l", l=32),
                                                      in0=kt[:, 0:tb].rearrange("p (c l) -> p c l", l=32),
                                                      in1=gam[:, 0:nch].unsqueeze(2).to_broadcast([128, nch, 32]), op=ALU.mult),
                     reads=['b_kt', 'gam'], writes=['b_kh'])
                if debug and debug.get('inner') and h == head_ids[0]:
                    for n_ in ['lf', 'kg', 'br', 'E', 'qt', 'kt', 'kh']:
                        dump("%s_d%d" % (n_, d), bt[n_][:, 0:tb], ['b_' + n_], tb, col0=off + t0)
                pz, pk = nps()
                for j in range(tb // 128):
                    P.op('pe', lambda e: e.transpose(out=pz[:, j * 128:(j + 1) * 128], in_=kh[:, j * 128:(j + 1) * 128],
                                                     identity=ident[:]), reads=['b_kh', 'ident'], writes=[pk])
                P.op('act', lambda e: e.copy(out=khtok[:, 0:tb // 128, :], in_=pz[:, 0:tb].rearrange("p (j c) -> p j c", c=128)),
                     reads=[pk], writes=['khtok'])
                tiles = list(range(tb // 128))
                if rev:
                    tiles = tiles[::-1]
                for j in tiles:
                    tg = t0 // 128 + j
                    pa, pak = nps()
                    P.op('pe', lambda e: e.matmul(pa[:, 0:128], kt[:, j * 128:(j + 1) * 128], qt[:, j * 128:(j + 1) * 128],
                                                  start=True, stop=True), reads=['b_kt', 'b_qt'], writes=[pak])
                    P.op('dve', lambda e: e.tensor_tensor(out=attT[:], in0=pa[:, 0:128], in1=mH[:, d, :], op=ALU.mult),
                         reads=[pak, 'mH'], writes=['attT'])
                    po, pok = nps()
                    P.op('pe', lambda e: e.matmul(po[:, 0:128], attT[:], Fv[:, tg, :], start=True, stop=False),
                         reads=['attT', 'Fv'], writes=[pok])
                    chs = [0, 1, 2, 3]
                    if rev:
                        chs = chs[::-1]
                    for ci, c in enumerate(chs):
                        Scur = Sst[cur]
                        Snew = Sst[1 - cur]
                        P.op('pe', lambda e: e.matmul(
                            po[32 * c:32 * c + 32, 0:128], qt[:, j * 128 + 32 * c: j * 128 + 32 * c + 32], Scur[:],
                            start=False, stop=(ci == 3), tile_position=(0, 32 * c)),
                            reads=['b_qt', 'Sst%d' % cur], writes=[pok])
                        pd, pdk = nps()
                        P.op('pe', lambda e: e.matmul(
                            pd[:, 0:128], khtok[32 * c:32 * c + 32, j, :], Fv[32 * c:32 * c + 32, tg, :],
                            start=True, stop=True, tile_position=(32 * c, 0)),
                            reads=['khtok', 'Fv'], writes=[pdk])
                        gidx = j * 4 + c
                        P.op('dve', lambda e: e.scalar_tensor_tensor(
                            out=Snew[:], in0=Scur[:], scalar=gam[:, gidx:gidx + 1], in1=pd[:, 0:128],
                            op0=ALU.mult, op1=ALU.add),
                            reads=['Sst%d' % cur, 'gam', pdk], writes=['Sst%d' % (1 - cur)])
                        cur = 1 - cur
                    if d == 0:
                        P.op('act', lambda e: e.copy(out=Fo[:, tg, :], in_=po[:, 0:128]), reads=[pok], writes=[('Fo', tg)])
                    else:
                        P.op('dve', lambda e: e.tensor_tensor(out=Fo[:, tg, :], in0=Fo[:, tg, :], in1=po[:, 0:128], op=ALU.add),
                             reads=[pok, ('Fo', tg)], writes=[('Fo', tg)])
            if not is_sample:
                P.dma(o_hgrn[pidx, d, h], Sst[cur][:], reads=['Sst%d' % cur], q='pool')
        for tg in range(T // 128):
            P.op('act', lambda e: e.activation(out=attT[:], in_=Fo[:, tg, :], func=AF.Square, accum_out=ostat[:, tg, 0:1]),
                 reads=[('Fo', tg)], writes=['attT', ('ostat', tg)])
            P.op('dve', lambda e: e.tensor_scalar(out=ostat[:, tg, 1:2], in0=ostat[:, tg, 0:1], scalar1=1.0 / 128,
                                                  scalar2=1e-6, op0=ALU.mult, op1=ALU.add),
                 reads=[('ostat', tg)], writes=[('ostat', tg)])
            P.op('act', lambda e: e.activation(out=ostat[:, tg, 2:3], in_=ostat[:, tg, 1:2], func=AF.Sqrt),
                 reads=[('ostat', tg)], writes=[('ostat', tg)])
            P.op('dve', lambda e: e.reciprocal(out=ostat[:, tg, 3:4], in_=ostat[:, tg, 2:3]),
                 reads=[('ostat', tg)], writes=[('ostat', tg)])
            P.op('dve', lambda e: e.tensor_scalar(out=Fo[:, tg, :], in0=Fo[:, tg, :], scalar1=ostat[:, tg, 3:4],
                                                  scalar2=None, op0=ALU.mult),
                 reads=[('Fo', tg), ('ostat', tg)], writes=[('Fo', tg)])
        n4 = min(4, T // 128)
        for g4 in range(T // (128 * n4)):
            pz, pk = nps()
            for j in range(n4):
                tg = g4 * n4 + j
                P.op('pe', lambda e: e.transpose(out=pz[:, j * 128:(j + 1) * 128], in_=Fo[:, tg, :], identity=ident[:]),
                     reads=[('Fo', tg), 'ident'], writes=[pk])
            w_ = n4 * 128
            P.op('dve', lambda e: e.scalar_tensor_tensor(
                out=yT[:, h, g4 * w_:(g4 + 1) * w_], in0=pz[:, 0:w_], scalar=hgg_sb[:, h:h + 1],
                in1=Fsz[:, g4 * w_:(g4 + 1) * w_], op0=ALU.mult, op1=ALU.mult),
                reads=[pk, 'hgg', 'Fsz'], writes=[('yT', h)])

    TR = 256
    LR = [sb0("LR%d" % g, [64, TS], BF16) for g in range(4)]
    rb = {n_: BT[i][0:64, :] for i, n_ in enumerate(
          ['lw', 'a', 'kk', 'kq', 'kap', 'kd', 'b', 'rk', 'G', 'br', 'E', 'Ei', 'Em', 'bh', 'kh', 'Kb', 'Bb', 't1'])}
    KR = sb0("r_KR", [64, 2, TR])
    cset = [{n_: sb0("c%d_%s" % (i_, n_), [64, (128 if n_ in ('AB', 'BB') else 64)])
             for n_ in ['AB', 'BB', 'XT0', 'XT1', 'X1', 'Xw', 'Pm0', 'Pm1', 'Vt', 'Kt', 'Bt']} for i_ in range(4)]
    rsq = {n_: sb0("rq_" + n_, [64, 64]) for n_ in ['U', 'Z0', 'Z1', 'zt']}
    rgam = sb0("rgam", [64, 8])
    rgref = sb0("rgref", [64, 4])
    mR = sb0("mR", [64, 2, 3, 128])
    P.dma(mR[:], maskR.rearrange("d m s t -> s d m t"), writes=['mR'])
    prm = {}
    for n_, src_, shp in [('mu_rkv', mu_rkv, [64, 2, 4, 16]), ('mu_lr', mu_lr, [64, 2, 4]), ('w0', w0T, [64, 2, 16]),
                          ('a0', a0T, [64, 2, 16]), ('kk', kkT, [64, 16]), ('ka', kaT, [64, 16]), ('rk', rkT, [64, 16]),
                          ('gng', gngT, [64, 16]), ('gnb', gnbT, [64, 16])]:
        prm[n_] = sb0("p_" + n_, shp)
        P.dma(prm[n_][:], src_[:], writes=['p_' + n_])
    c0_rkv = sb0("c0_rkv", [64, 4, 16])
    c0_lr = sb0("c0_lr", [64, 4])
    omka = sb0("omka", [64, 16])
    P.op('dve', lambda e: e.tensor_tensor(out=c0_rkv[:], in0=prm['mu_rkv'][:, 0], in1=prm['mu_rkv'][:, 1], op=ALU.add),
         reads=['p_mu_rkv'], writes=['c0_rkv'])
    P.op('dve', lambda e: e.tensor_scalar(out=c0_rkv[:], in0=c0_rkv[:], scalar1=-1.0, scalar2=1.0, op0=ALU.mult, op1=ALU.add),
         reads=['c0_rkv'], writes=['c0_rkv'])
    P.op('dve', lambda e: e.tensor_tensor(out=c0_lr[:], in0=prm['mu_lr'][:, 0], in1=prm['mu_lr'][:, 1], op=ALU.add),
         reads=['p_mu_lr'], writes=['c0_lr'])
    P.op('dve', lambda e: e.tensor_scalar(out=c0_lr[:], in0=c0_lr[:], scalar1=-1.0, scalar2=1.0, op0=ALU.mult, op1=ALU.add),
         reads=['c0_lr'], writes=['c0_lr'])
    P.op('dve', lambda e: e.tensor_scalar(out=omka[:], in0=prm['ka'][:], scalar1=-1.0, scalar2=1.0, op0=ALU.mult, op1=ALU.add),
         reads=['p_ka'], writes=['omka'])
    w2a2 = sb0("w2a2", [64, 4, D], BF16)
    for g, src_ in enumerate([w2[0], w2[1], a2[0], a2[1]]):
        for c0 in range(0, D, 256):
            P.dma(wst[0:64, 0, 0:256], src_[:, c0:c0 + 256], writes=['wst'])
            P.op('pool', lambda e: e.tensor_copy(out=w2a2[:, g, c0:c0 + 256], in_=wst[0:64, 0, 0:256]), reads=['wst'], writes=['w2a2'])

    def shift_into(dst, dkey, raw, rkey, T, c0ap, m0ap, m1ap, t1tile, eng='dve'):
        for s0 in range(0, T, 512):
            n = min(512, T - s0)
            P.op(eng, lambda e: e.tensor_scalar(out=t1tile[:, 0:n], in0=raw[:, 16 + s0:16 + s0 + n], scalar1=c0ap, scalar2=None,
                                                op0=ALU.mult), reads=[rkey], writes=['shift_t'])
            P.op('dve', lambda e: e.scalar_tensor_tensor(out=t1tile[:, 0:n], in0=raw[:, 15 + s0:15 + s0 + n], scalar=m0ap,
                                                       in1=t1tile[:, 0:n], op0=ALU.mult, op1=ALU.add),
                 reads=[rkey, 'shift_t'], writes=['shift_t'])
            P.op('dve', lambda e: e.scalar_tensor_tensor(out=dst[:, s0:s0 + n], in0=raw[:, 17 + s0:17 + s0 + n], scalar=m1ap,
                                                       in1=t1tile[:, 0:n], op0=ALU.mult, op1=ALU.add),
                 reads=[rkey, 'shift_t'], writes=[dkey])

    shiftt = sb0("shiftt", [64, 512])

    def rwkv_seq_setup(off, T):
        load_w(wLR, 256)
        pb = min(512, T)
        for g in range(4):
            raw = FT[g]
            P.op('pool', lambda e: e.memset(raw[0:64, 15:16], 0.0), writes=['FT%d' % g])
            P.op('pool', lambda e: e.memset(raw[0:64, T + 16:T + 17], 0.0), writes=['FT%d' % g])
            for b in range(T // pb):
                t0 = b * pb
                proj(64 * g, 64, t0, pb, lambda pz, pk: P.op(
                    'act', lambda e: e.copy(out=raw[0:64, 16 + t0:16 + t0 + pb], in_=pz[0:64, 0:pb]), reads=[pk], writes=['FT%d' % g]))
            shift_into(FT[4][0:64, :], 'FT4', raw[0:64, :], 'FT%d' % g, T, c0_lr[:, g:g + 1], prm['mu_lr'][:, 0, g:g + 1],
                       prm['mu_lr'][:, 1, g:g + 1], shiftt)
            if g < 2:
                P.op('act', lambda e: e.activation(out=LR[g][:, 0:T], in_=FT[4][0:64, 0:T], func=AF.Tanh), reads=['FT4'], writes=['LR%d' % g])
            else:
                P.op('act', lambda e: e.copy(out=LR[g][:, 0:T], in_=FT[4][0:64, 0:T]), reads=['FT4'], writes=['LR%d' % g])

    def rwkv_head(h, slot, off, T, is_sample, pidx):
        P.barrier()
        load_w(wB[h], 256)
        pb = min(512, T)
        nchT = T // 64
        for g in range(3):
            raw = FT[g]
            P.op('pool', lambda e: e.memset(raw[0:64, 15:16], 0.0), writes=['FT%d' % g])
            P.op('pool', lambda e: e.memset(raw[0:64, T + 16:T + 17], 0.0), writes=['FT%d' % g])
            for b in range(T // pb):
                t0 = b * pb
                proj(64 * g, 64, t0, pb, lambda pz, pk: P.op(
                    'act', lambda e: e.copy(out=raw[0:64, 16 + t0:16 + t0 + pb], in_=pz[0:64, 0:pb]), reads=[pk], writes=['FT%d' % g]))
            shift_into(FT[3 + g][0:64, :], 'FT%d' % (3 + g), raw[0:64, :], 'FT%d' % g, T, c0_rkv[:, g, h:h + 1],
                       prm['mu_rkv'][:, 0, g, h:h + 1], prm['mu_rkv'][:, 1, g, h:h + 1], shiftt, eng=('dve' if g != 1 else 'pool'))
        rS, kS, vS = FT[3][0:64, :], FT[4][0:64, :], FT[5][0:64, :]
        szb, yaccr, bonus = FT[0][0:64, :], FT[1][0:64, 0:T], FT[2][0:64, :]
        yacc = yaccr.rearrange("p (c v) -> p c v", v=64)
        for b in range(T // pb):
            t0 = b * pb
            proj(192, 64, t0, pb, lambda pz, pk: P.op(
                'act', lambda e: e.activation(out=szb[:, t0:t0 + pb], in_=pz[0:64, 0:pb], func=AF.Silu), reads=[pk], writes=['FT0']))
        tb = min(TR, T)
        nblk = T // tb
        P.barrier()
        for d in range(2):
            rev = (d == 1)
            cur = 0
            Zt = [rsq['Z0'], rsq['Z1']]
            if is_sample:
                P.dma(rsq['zt'][:], s_rwkv[d, h], writes=['rq_zt'])
                pz, pk = nps()
                P.op('pe', lambda e: e.transpose(out=pz[0:64, 0:64], in_=rsq['zt'][:], identity=ident[0:64, 0:64]),
                     reads=['rq_zt', 'ident'], writes=[pk])
                P.op('act', lambda e: e.copy(out=Zt[0][:], in_=pz[0:64, 0:64]), reads=[pk], writes=['rq_Z0'])
            else:
                P.op('pool', lambda e: e.memset(Zt[0][:], 0.0), writes=['rq_Z0'])
            blks = list(range(nblk))
            if rev:
                blks = blks[::-1]
            for b in blks:
                t0 = b * tb
                sl = slice(t0, t0 + tb)
                nch = tb // 64
                R_ = rb
                pz, pk = nps()
                P.op('pe', lambda e: e.matmul(pz[0:64, 0:tb], w2a2[:, d, h * 64:(h + 1) * 64], LR[d][:, sl], start=True, stop=True),
                     reads=['w2a2', 'LR%d' % d], writes=[pk])
                P.op('act', lambda e: e.activation(out=R_['lw'][:, 0:tb], in_=pz[0:64, 0:tb], func=AF.Sigmoid,
                                                   bias=prm['w0'][:, d, h:h + 1], scale=1.0), reads=[pk, 'p_w0'], writes=['r_lw'])
                P.op('pool', lambda e: e.tensor_scalar(out=R_['lw'][:, 0:tb], in0=R_['lw'][:, 0:tb], scalar1=-0.6065306597126334,
                                                       scalar2=None, op0=ALU.mult), reads=['r_lw'], writes=['r_lw'])
                pz, pk = nps()
                P.op('pe', lambda e: e.matmul(pz[0:64, 0:tb], w2a2[:, 2 + d, h * 64:(h + 1) * 64], LR[2 + d][:, sl], start=True, stop=True),
                     reads=['w2a2', 'LR%d' % (2 + d)], writes=[pk])
                P.op('act', lambda e: e.activation(out=R_['a'][:, 0:tb], in_=pz[0:64, 0:tb], func=AF.Sigmoid,
                                                   bias=prm['a0'][:, d, h:h + 1], scale=1.0), reads=[pk, 'p_a0'], writes=['r_a'])
                P.op('dve', lambda e: e.tensor_scalar(out=R_['kk'][:, 0:tb], in0=kS[:, sl], scalar1=prm['kk'][:, h:h + 1],
                                                      scalar2=None, op0=ALU.mult), reads=['FT4', 'p_kk'], writes=['r_kk'])
                P.op('pool', lambda e: e.tensor_tensor(out=R_['kq'][:, 0:tb], in0=R_['kk'][:, 0:tb], in1=R_['kk'][:, 0:tb], op=ALU.mult),
                     reads=['r_kk'], writes=['r_kq'])
                pz, pk = nps()
                P.op('pe', lambda e: e.matmul(pz[0:64, 0:tb], ones[0:64, 0:64], R_['kq'][:, 0:tb], start=True, stop=True),
                     reads=['ones', 'r_kq'], writes=[pk])
                P.op('act', lambda e: e.activation(out=R_['kq'][:, 0:tb], in_=pz[0:64, 0:tb], func=AF.Sqrt), reads=[pk], writes=['r_kq'])
                P.op('dve', lambda e: e.tensor_scalar(out=R_['kq'][:, 0:tb], in0=R_['kq'][:, 0:tb], scalar1=1e-12, scalar2=None,
                                                      op0=ALU.max), reads=['r_kq'], writes=['r_kq'])
                P.op('dve', lambda e: e.reciprocal(out=R_['kq'][:, 0:tb], in_=R_['kq'][:, 0:tb]), reads=['r_kq'], writes=['r_kq'])
                P.op('dve', lambda e: e.tensor_tensor(out=R_['kap'][:, 0:tb], in0=R_['kk'][:, 0:tb], in1=R_['kq'][:, 0:tb], op=ALU.mult),
                     reads=['r_kk', 'r_kq'], writes=['r_kap'])
                P.op('pool', lambda e: e.tensor_scalar(out=R_['t1'][:, 0:tb], in0=R_['a'][:, 0:tb], scalar1=prm['ka'][:, h:h + 1],
                                                       scalar2=omka[:, h:h + 1], op0=ALU.mult, op1=ALU.add),
                     reads=['r_a', 'p_ka', 'omka'], writes=['r_t1'])
                P.op('pool', lambda e: e.tensor_tensor(out=R_['kd'][:, 0:tb], in0=kS[:, sl], in1=R_['t1'][:, 0:tb], op=ALU.mult),
                     reads=['FT4', 'r_t1'], writes=['r_kd'])
                P.op('dve', lambda e: e.tensor_tensor(out=R_['b'][:, 0:tb], in0=R_['a'][:, 0:tb], in1=R_['kap'][:, 0:tb], op=ALU.mult),
                     reads=['r_a', 'r_kap'], writes=['r_b'])
                P.op('dve', lambda e: e.scalar_tensor_tensor(out=R_['rk'][:, 0:tb], in0=rS[:, sl], scalar=prm['rk'][:, h:h + 1],
                                                             in1=R_['kd'][:, 0:tb], op0=ALU.mult, op1=ALU.mult),
                     reads=['FT3', 'p_rk', 'r_kd'], writes=['r_rk'])
                pz, pk = nps()
                P.op('pe', lambda e: e.matmul(pz[0:64, 0:tb], ones[0:64, 0:64], R_['rk'][:, 0:tb], start=True, stop=True),
                     reads=['ones', 'r_rk'], writes=[pk])
                if d == 0:
                    P.op('dve', lambda e: e.tensor_tensor(out=bonus[:, sl], in0=pz[0:64, 0:tb], in1=vS[:, sl], op=ALU.mult),
                         reads=[pk, 'FT5'], writes=[('bonus', b)])
                else:
                    P.op('dve', lambda e: e.tensor_tensor(out=R_['rk'][:, 0:tb], in0=pz[0:64, 0:tb], in1=vS[:, sl], op=ALU.mult),
                         reads=[pk, 'FT5'], writes=['r_rk'])
                    P.op('pool', lambda e: e.tensor_tensor(out=bonus[:, sl], in0=bonus[:, sl], in1=R_['rk'][:, 0:tb], op=ALU.add),
                         reads=['r_rk', ('bonus', b)], writes=[('bonus', b)])
                G, br, E, Ei, Em = R_['G'], R_['br'], R_['E'], R_['Ei'], R_['Em']
                P.op('dve', lambda e: e.memset(E[:, 0:tb], 0.0), writes=['r_E'])
                if not rev:
                    P.op('dve', lambda e: e.tensor_tensor_scan(out=G[:, 0:tb], data0=R_['lw'][:, 0:tb], data1=E[:, 0:tb],
                                                               initial=0.0, op0=ALU.add, op1=ALU.add),
                         reads=['r_lw', 'r_E'], writes=['r_G'])
                    ci_ = 0
                else:
                    P.op('dve', lambda e: e.tensor_tensor_scan(out=G[:, 0:tb][:, ::-1], data0=R_['lw'][:, 0:tb][:, ::-1],
                                                               data1=E[:, 0:tb], initial=0.0, op0=ALU.add, op1=ALU.add),
                         reads=['r_lw', 'r_E'], writes=['r_G'])
                    ci_ = 63
                G3 = G[:, 0:tb].rearrange("p (c l) -> p c l", l=64)
                lw3 = R_['lw'][:, 0:tb].rearrange("p (c l) -> p c l", l=64)
                P.op('dve', lambda e: e.tensor_tensor(out=rgref[:, 0:nch], in0=G3[:, :, ci_], in1=lw3[:, :, ci_], op=ALU.subtract),
                     reads=['r_G', 'r_lw'], writes=['rgref'])
                P.op('dve', lambda e: e.tensor_tensor(out=br[:, 0:tb].rearrange("p (c l) -> p c l", l=64), in0=G3,
                                                      in1=rgref[:, 0:nch].unsqueeze(2).to_broadcast([64, nch, 64]), op=ALU.subtract),
                     reads=['r_G', 'rgref'], writes=['r_br'])
                bend = br[:, 0:tb].rearrange("p (c l) -> p c l", l=64)[:, :, (0 if rev else 63)]
                P.op('act', lambda e: e.activation(out=rgam[:, 0:nch], in_=bend, func=AF.Exp), reads=['r_br'], writes=['rgam'])
                P.op('pool', lambda e: e.tensor_scalar(out=rgam[:, 4:4 + nch], in0=rgam[:, 0:nch], scalar1=-1.0, scalar2=None,
                                                       op0=ALU.mult), reads=['rgam'], writes=['rgam'])
                P.op('act', lambda e: e.activation(out=E[:, 0:tb], in_=br[:, 0:tb], func=AF.Exp), reads=['r_br'], writes=['r_E'])
                P.op('act', lambda e: e.activation(out=Ei[:, 0:tb], in_=br[:, 0:tb], func=AF.Exp, scale=-1.0),
                     reads=['r_br'], writes=['r_Ei'])
                P.op('pool', lambda e: e.tensor_tensor(out=R_['t1'][:, 0:tb], in0=br[:, 0:tb], in1=R_['lw'][:, 0:tb], op=ALU.subtract),
                     reads=['r_br', 'r_lw'], writes=['r_t1'])
                P.op('act', lambda e: e.activation(out=Em[:, 0:tb], in_=R_['t1'][:, 0:tb], func=AF.Exp), reads=['r_t1'], writes=['r_Em'])
                P.op('dve', lambda e: e.tensor_tensor(out=KR[:, 0, 0:tb], in0=R_['kap'][:, 0:tb], in1=Em[:, 0:tb], op=ALU.mult),
                     reads=['r_kap', 'r_Em'], writes=['r_KR'])
                P.op('pool', lambda e: e.tensor_tensor(out=KR[:, 1, 0:tb], in0=rS[:, sl], in1=E[:, 0:tb], op=ALU.mult),
                     reads=['FT3', 'r_E', 'r_KR'], writes=['r_KR'])
                P.op('dve', lambda e: e.tensor_tensor(out=R_['bh'][:, 0:tb], in0=R_['b'][:, 0:tb], in1=Ei[:, 0:tb], op=ALU.mult),
                     reads=['r_b', 'r_Ei'], writes=['r_bh'])
                P.op('pool', lambda e: e.tensor_tensor(out=R_['kh'][:, 0:tb], in0=R_['kd'][:, 0:tb], in1=Ei[:, 0:tb], op=ALU.mult),
                     reads=['r_kd', 'r_Ei'], writes=['r_kh'])
                P.op('dve', lambda e: e.tensor_tensor(out=R_['Kb'][:, 0:tb].rearrange("p (c l) -> p c l", l=64),
                                                      in0=R_['kh'][:, 0:tb].rearrange("p (c l) -> p c l", l=64),
                                                      in1=rgam[:, 0:nch].unsqueeze(2).to_broadcast([64, nch, 64]), op=ALU.mult),
                     reads=['r_kh', 'rgam'], writes=['r_Kb'])
                P.op('dve', lambda e: e.tensor_tensor(out=R_['Bb'][:, 0:tb].rearrange("p (c l) -> p c l", l=64),
                                                      in0=R_['bh'][:, 0:tb].rearrange("p (c l) -> p c l", l=64),
                                                      in1=rgam[:, 4:4 + nch].unsqueeze(2).to_broadcast([64, nch, 64]), op=ALU.mult),
                     reads=['r_bh', 'rgam'], writes=['r_Bb'])
                chs = list(range(nch))
                if rev:
                    chs = chs[::-1]
                st = {}
                for c in chs:
                    cs = slice(c * 64, (c + 1) * 64)
                    C_ = cset[c]
                    ck = (lambda n_, c=c: 'c%d_%s' % (c, n_))
                    pA, pAk = nps()
                    P.op('pe', lambda e: e.matmul(pA[0:64, 0:128], R_['bh'][:, cs], KR[:, :, cs], start=True, stop=True),
                         reads=['r_bh', 'r_KR'], writes=[pAk])
                    P.op('pe', lambda e: e.matmul(pA[0:64, 128:256], R_['kh'][:, cs], KR[:, :, cs], start=True, stop=True),
                         reads=['r_kh', 'r_KR'], writes=[pAk])
                    P.op('pe', lambda e: e.matmul(pA[0:64, 256:320], KR[:, 0, cs], R_['bh'][:, cs], start=True, stop=True),
                         reads=['r_bh', 'r_KR'], writes=[pAk])
                    AB, BB = C_['AB'], C_['BB']
                    P.op('dve', lambda e: e.tensor_tensor(out=AB[:], in0=pA[0:64, 0:128], in1=mR[:, d, 0, :], op=ALU.mult),
                         reads=[pAk, 'mR'], writes=[ck('AB')])
                    P.op('dve', lambda e: e.tensor_tensor(out=BB[:], in0=pA[0:64, 128:256], in1=mR[:, d, 1, :], op=ALU.mult),
                         reads=[pAk, 'mR'], writes=[ck('BB')])
                    P.op('dve', lambda e: e.tensor_tensor(out=C_['XT0'][:], in0=pA[0:64, 256:320], in1=mR[:, d, 2, 0:64], op=ALU.mult),
                         reads=[pAk, 'mR'], writes=[ck('XT0')])
                    P.op('dve', lambda e: e.tensor_tensor(out=C_['Pm0'][:], in0=AB[:, 0:64], in1=ident[0:64, 0:64], op=ALU.add),
                         reads=[ck('AB'), 'ident'], writes=[ck('Pm0')])
                    st[c] = dict(X=AB[:, 0:64], Xk=ck('AB'), XT=C_['XT0'], XTk=ck('XT0'), xti=0, pmi=0)
                for lev in range(5):
                    for c in chs:
                        C_ = cset[c]
                        s_ = st[c]
                        ck = (lambda n_, c=c: 'c%d_%s' % (c, n_))
                        X, Xk, XT, XTk = s_['X'], s_['Xk'], s_['XT'], s_['XTk']
                        pq, pqk = nps()
                        nXTn = 'XT1' if s_['xti'] == 0 else 'XT0'
                        nXT, nXTk = C_[nXTn], ck(nXTn)
                        P.op('pe', lambda e: e.matmul(pq[0:64, 64:128], X, XT[:], start=True, stop=True), reads=[Xk, XTk], writes=[pqk])
                        if lev < 4:
                            P.op('pe', lambda e: e.matmul(pq[0:64, 0:64], XT[:], X, start=True, stop=True), reads=[Xk, XTk], writes=[pqk])
                        P.op('act', lambda e: e.copy(out=nXT[:], in_=pq[0:64, 64:128]), reads=[pqk], writes=[nXTk])
                        if lev < 4:
                            tn = 'X1' if lev % 2 == 0 else 'Xw'
                            P.op('act', lambda e: e.copy(out=C_[tn][:], in_=pq[0:64, 0:64]), reads=[pqk], writes=[ck(tn)])
                            s_['X'], s_['Xk'] = C_[tn][:], ck(tn)
                        s_['XT'], s_['XTk'], s_['xti'] = nXT, nXTk, 1 - s_['xti']
                    for c in chs:
                        C_ = cset[c]
                        s_ = st[c]
                        ck = (lambda n_, c=c: 'c%d_%s' % (c, n_))
                        nXT, nXTk = s_['XT'], s_['XTk']
                        pmi = s_['pmi']
                        Pc, Pn = C_['Pm%d' % pmi], C_['Pm%d' % (1 - pmi)]
                        pp, ppk = nps()
                        P.op('pe', lambda e: e.matmul(pp[0:64, 0:64], nXT[:], Pc[:], start=True, stop=True),
                             reads=[nXTk, ck('Pm%d' % pmi)], writes=[ppk])
                        P.op('dve', lambda e: e.tensor_tensor(out=Pn[:], in0=pp[0:64, 0:64], in1=Pc[:], op=ALU.add),
                             reads=[ppk, ck('Pm%d' % pmi)], writes=[ck('Pm%d' % (1 - pmi))])
                        s_['pmi'] = 1 - pmi
                for c in chs:
                    cs = slice(c * 64, (c + 1) * 64)
                    gsl = slice(t0 + c * 64, t0 + (c + 1) * 64)
                    C_ = cset[c]
                    pt, ptk = nps()
                    P.op('pe', lambda e: e.transpose(out=pt[0:64, 0:64], in_=vS[:, gsl], identity=ident[0:64, 0:64]),
                         reads=['FT5', 'ident'], writes=[ptk])
                    P.op('pe', lambda e: e.transpose(out=pt[0:64, 64:128], in_=R_['Kb'][:, cs], identity=ident[0:64, 0:64]),
                         reads=['r_Kb', 'ident'], writes=[ptk])
                    P.op('pe', lambda e: e.transpose(out=pt[0:64, 128:192], in_=R_['Bb'][:, cs], identity=ident[0:64, 0:64]),
                         reads=['r_Bb', 'ident'], writes=[ptk])
                    P.op('act', lambda e: e.copy(out=C_['Vt'][:], in_=pt[0:64, 0:64]), reads=[ptk], writes=['c%d_Vt' % c])
                    P.op('act', lambda e: e.copy(out=C_['Kt'][:], in_=pt[0:64, 64:128]), reads=[ptk], writes=['c%d_Kt' % c])
                    P.op('act', lambda e: e.copy(out=C_['Bt'][:], in_=pt[0:64, 128:192]), reads=[ptk], writes=['c%d_Bt' % c])
                for c in chs:
                    cs = slice(c * 64, (c + 1) * 64)
                    cg = (t0 // 64) + c
                    C_ = cset[c]
                    s_ = st[c]
                    AB, BB = C_['AB'], C_['BB']
                    ABk, BBk, Vtk, Ktk, Btk = ['c%d_%s' % (c, n_) for n_ in ('AB', 'BB', 'Vt', 'Kt', 'Bt')]
                    Pm, Pmk = C_['Pm%d' % s_['pmi']], 'c%d_Pm%d' % (c, s_['pmi'])
                    Zc, Zn = Zt[cur], Zt[1 - cur]
                    zck, znk = 'rq_Z%d' % cur, 'rq_Z%d' % (1 - cur)
                    pw, pwk = nps()
                    P.op('pe', lambda e: e.matmul(pw[0:64, 0:64], KR[:, 0, cs], Zc[:], start=True, stop=False),
                         reads=['r_KR', zck], writes=[pwk])
                    P.op('pe', lambda e: e.matmul(pw[0:64, 0:64], BB[:, 0:64], C_['Vt'][:], start=False, stop=True),
                         reads=[BBk, Vtk], writes=[pwk])
                    P.op('act', lambda e: e.copy(out=rsq['zt'][:], in_=pw[0:64, 0:64]), reads=[pwk], writes=['rq_zt'])
                    pu, puk = nps()
                    P.op('pe', lambda e: e.matmul(pu[0:64, 0:64], Pm[:], rsq['zt'][:], start=True, stop=True),
                         reads=[Pmk, 'rq_zt'], writes=[puk])
                    P.op('act', lambda e: e.copy(out=rsq['U'][:], in_=pu[0:64, 0:64]), reads=[puk], writes=['rq_U'])
                    py, pyk = nps()
                    P.op('pe', lambda e: e.matmul(py[0:64, 0:64], KR[:, 1, cs], Zc[:], start=True, stop=False),
                         reads=['r_KR', zck], writes=[pyk])
                    P.op('pe', lambda e: e.matmul(py[0:64, 0:64], BB[:, 64:128], C_['Vt'][:], start=False, stop=False),
                         reads=[BBk, Vtk], writes=[pyk])
                    P.op('pe', lambda e: e.matmul(py[0:64, 0:64], AB[:, 64:128], rsq['U'][:], start=False, stop=True),
                         reads=[ABk, 'rq_U'], writes=[pyk])
                    pzz, pzk = nps()
                    P.op('pe', lambda e: e.matmul(pzz[0:64, 0:64], C_['Kt'][:], C_['Vt'][:], start=True, stop=False),
                         reads=[Ktk, Vtk], writes=[pzk])
                    P.op('pe', lambda e: e.matmul(pzz[0:64, 0:64], C_['Bt'][:], rsq['U'][:], start=False, stop=True),
                         reads=[Btk, 'rq_U'], writes=[pzk])
                    P.op('dve', lambda e: e.scalar_tensor_tensor(out=Zn[:], in0=Zc[:], scalar=rgam[:, c:c + 1], in1=pzz[0:64, 0:64],
                                                                 op0=ALU.mult, op1=ALU.add), reads=[zck, 'rgam', pzk], writes=[znk])
                    if d == 0:
                        P.op('act', lambda e: e.copy(out=yacc[:, cg, :], in_=py[0:64, 0:64]), reads=[pyk], writes=[('yacc', cg)])
                    else:
                        P.op('dve', lambda e: e.tensor_tensor(out=yacc[:, cg, :], in0=yacc[:, cg, :], in1=py[0:64, 0:64], op=ALU.add),
                             reads=[pyk, ('yacc', cg)], writes=[('yacc', cg)])
                    cur = 1 - cur
            if not is_sample:
                pz, pk = nps()
                P.op('pe', lambda e: e.transpose(out=pz[0:64, 0:64], in_=Zt[cur][:], identity=ident[0:64, 0:64]),
                     reads=['rq_Z%d' % cur, 'ident'], writes=[pk])
                P.op('act', lambda e: e.copy(out=rsq['zt'][:], in_=pz[0:64, 0:64]), reads=[pk], writes=['rq_zt'])
                P.dma(o_rwkv[pidx, d, h], rsq['zt'][:], reads=['rq_zt'], q='pool')
        ykeys = [('yacc', c) for c in range(nchT)]
        gst = ostat[0:64, :, :].rearrange("p a b -> p (a b)")
        P.op('dve', lambda e: e.tensor_reduce(out=gst[:, 0:nchT], in_=yacc, axis=AX.X, op=ALU.add), reads=ykeys, writes=['gst'])
        P.op('dve', lambda e: e.tensor_scalar(out=gst[:, 0:nchT], in0=gst[:, 0:nchT], scalar1=-1.0 / 64, scalar2=None, op0=ALU.mult),
             reads=['gst'], writes=['gst'])
        P.op('dve', lambda e: e.tensor_tensor(out=yacc, in0=yacc, in1=gst[:, 0:nchT].unsqueeze(2).to_broadcast([64, nchT, 64]), op=ALU.add),
             reads=ykeys + ['gst'], writes=ykeys)
        sq = FT[3][0:64, 0:T].rearrange("p (c v) -> p c v", v=64)
        P.op('pool', lambda e: e.tensor_tensor(out=sq, in0=yacc, in1=yacc, op=ALU.mult), reads=ykeys, writes=['FT3'])
        P.op('dve', lambda e: e.tensor_reduce(out=gst[:, 32:32 + nchT], in_=sq, axis=AX.X, op=ALU.add), reads=['FT3'], writes=['gst'])
        P.op('dve', lambda e: e.tensor_scalar(out=gst[:, 32:32 + nchT], in0=gst[:, 32:32 + nchT], scalar1=1.0 / 64, scalar2=64e-5,
                                              op0=ALU.mult, op1=ALU.add), reads=['gst'], writes=['gst'])
        P.op('act', lambda e: e.activation(out=gst[:, 32:32 + nchT], in_=gst[:, 32:32 + nchT], func=AF.Sqrt), reads=['gst'], writes=['gst'])
        P.op('dve', lambda e: e.reciprocal(out=gst[:, 32:32 + nchT], in_=gst[:, 32:32 + nchT]), reads=['gst'], writes=['gst'])
        P.op('dve', lambda e: e.tensor_tensor(out=yacc, in0=yacc, in1=gst[:, 32:32 + nchT].unsqueeze(2).to_broadcast([64, nchT, 64]),
                                              op=ALU.mult), reads=ykeys + ['gst'], writes=ykeys)
        n8 = min(8, nchT)
        for g8 in range(nchT // n8):
            pz, pk = nps()
            for j in range(n8):
                cg = g8 * n8 + j
                P.op('pe', lambda e: e.transpose(out=pz[0:64, j * 64:(j + 1) * 64], in_=yacc[:, cg, :], identity=ident[0:64, 0:64]),
                     reads=[('yacc', cg), 'ident'], writes=[pk])
            w_ = n8 * 64
            gs = slice(g8 * w_, (g8 + 1) * w_)
            P.op('dve', lambda e: e.tensor_scalar(out=shiftt[:, 0:w_], in0=pz[0:64, 0:w_], scalar1=prm['gng'][:, h:h + 1],
                                                  scalar2=prm['gnb'][:, h:h + 1], op0=ALU.mult, op1=ALU.add),
                 reads=[pk, 'p_gng', 'p_gnb'], writes=['shift_t'])
            P.op('pool', lambda e: e.tensor_tensor(out=shiftt[:, 0:w_], in0=shiftt[:, 0:w_], in1=bonus[:, gs], op=ALU.add),
                 reads=['shift_t'] + [('bonus', b) for b in range(nblk)], writes=['shift_t'])
            P.op('dve', lambda e: e.tensor_tensor(out=yT[0:64, slot, gs], in0=shiftt[:, 0:w_], in1=szb[:, gs], op=ALU.mult),
                 reads=['shift_t', 'FT0'], writes=[('yT', slot)])

    seq_ids = debug.get('seqs', [0, 1, 2]) if debug else [0, 1, 2]
    head_ids = debug.get('heads', list(range(8))) if debug else list(range(8))
    rheads = debug.get('rheads', list(range(16))) if debug else list(range(16))
    for si in seq_ids:
        off, T, cidx, is_sample = SEQS[si]
        P.barrier()
        make_gate(0, cidx)
        make_hT(0, xin, 'xin', off, T, cidx)
        P.barrier()
        if debug and debug.get('inner'):
            dump("mT", mT[:, 0].rearrange("p a b -> p (a b)"), ['mT'], 48)
            dump("sc1", sc1[:, 0].rearrange("p a b -> p (a b)"), ['sc1'], 16)
            dump("scT", scT[:].rearrange("p a b -> p (a b)"), ['scT'], 16)
            dump("xn", xn, ['xn'], 1024)
            dump("xt0", xt[0], ['xt0'], 1024)
            for kc in range(8):
                dump("hT%d" % kc, hT[:, kc, 0:T], hT_keys(0, T), T, col0=off)
        for h in head_ids:
            hgrn_head(h, off, T, is_sample, si - 1)
            if debug and debug.get('dump_y'):
                dump("yT%d" % h, yT[:, h, 0:T], [('yT', h)], T, col0=off)
        P.barrier()
        load_wo(w_out_even[0:D, :], 128)
        outproj(0, [(128, s_) for s_ in range(8)], xin, 'xin', x1, 'x1', off, T, cidx)
        P.barrier()
        rwkv_seq_setup(off, T)
        for half in range(2):
            P.barrier()
            for slot in range(8):
                h = half * 8 + slot
                if h in rheads:
                    rwkv_head(h, slot, off, T, is_sample, si - 1)
                    if debug and debug.get('dump_y'):
                        dump("yR%d" % h, yT[0:64, slot, 0:T], [('yT', slot)], T, col0=off, parts=64)
            P.barrier()
            load_wo(w_out_even[D + half * 512: D + (half + 1) * 512, :], 64)
            outproj(0, [(64, s_) for s_ in range(8)], x1, 'x1', x1, 'x1', off, T, cidx)
    P.barrier()
    L0.close()

    if not (debug and debug.get('l0only')):
        L1 = contextlib.ExitStack()

        def sb1(name, shape, dt=F32):
            return L1.enter_context(nc.sbuf_tensor(name, list(shape), dt))

        make_fng()
        LC = 128
        DH = 512
        qT = yT[:, 4:8, :]
        kT = sb1("kT", [128, 4, TS], BF16)
        vch = sb1("vch", [128, DH], BF16)
        Cst = sb1("Cst", [128, 4, DH])
        Cbf = sb1("Cbf", [128, 4, DH], BF16)
        nst = sb1("nst", [128, 8])
        nbf = sb1("nbf", [128, 4], BF16)
        ktok = sb1("ktok", [128, DH], BF16)
        vw = sb1("vw", [128, DH], BF16)
        sTs = sb1("sTs", [128, 128], BF16)
        onesb = sb1("onesb", [128, 1], BF16)
        identb = sb1("identb", [128, 128], BF16)
        mC = sb1("mC", [128, 2, 128])
        SEL = sb1("SEL", [36, 4, 128])
        XA = sb1("XA", [36, TS])
        XB = sb1("XB", [36, TS])
        zrow = sb1("zrow", [36, 512])
        sm = {n_: sb1("sm_" + n_, [36, 16]) for n_ in ['ac', 'bl', 'M', 'MP', 'mu', 'al', 'gref', 'm0']}
        Wtok = sb1("Wtok", [128, 2, 16, 8])
        Wtokb = sb1("Wtokb", [128, 2, 16, 4], BF16)
        ALb = sb1("ALb", [128, 2, 4, 16])
        dstat = sb1("dstat", [128, 8])
        wGb = sb1("wGb", [128, 8, 16], BF16)
        gbT = sb1("gbT", [36, 4])
        ngbT = sb1("ngbT", [36, 4])
        cw = sb1("cw", [128, 32, 9])
        cb = sb1("cb", [128, 32])
        mng = sb1("mng", [128, 16])
        wbf1 = sb1("wbf1", [128, 8, DH], BF16)
        P.dma(mC[:], maskC.rearrange("d s t -> s d t"), writes=['mC'])
        P.dma(SEL[:], sel_d[:], writes=['SEL'])
        P.dma(gbT[:], gbT_d[:], writes=['gbT'])
        P.dma(cw[:], cw_d[:], writes=['cw'])
        P.dma(cb[:], cb_d[:], writes=['cb'])
        P.dma(mng[:], mng_d[:], writes=['mng'])
        P.op('dve', lambda e: e.memset(onesb[:], 1.0), writes=['onesb'])
        P.op('dve', lambda e: e.memset(zrow[:], 0.0), writes=['zrow'])
        P.op('dve', lambda e: e.tensor_copy(out=identb[:], in_=ident[:]), reads=['ident'], writes=['identb'])
        P.op('dve', lambda e: e.tensor_scalar(out=ngbT[:], in0=gbT[:], scalar1=-1.0, scalar2=None, op0=ALU.mult), reads=['gbT'], writes=['ngbT'])
        P.dma(wst[:, :, 0:16], w_in_odd[:, 10240:10256].rearrange("(kc p) n -> p kc n", p=128), writes=['wst'])
        P.op('pool', lambda e: e.tensor_copy(out=wGb[:], in_=wst[:, :, 0:16]), reads=['wst'], writes=['wGb'])
        LNK = float(np.log(DH ** -0.5))

        def load_w1(c0, ncols):
            v = w_in_odd[:, c0:c0 + ncols].rearrange("(kc p) n -> p kc n", p=128)
            for q0 in range(0, ncols, 256):
                w_ = min(256, ncols - q0)
                P.dma(wst[:, :, 0:w_], v[:, :, q0:q0 + w_], writes=['wst'])
                P.op('pool', lambda e: e.tensor_copy(out=wbf1[:, :, q0:q0 + w_], in_=wst[:, :, 0:w_]), reads=['wst'], writes=['wbf1'])

        def gates_seq(T, is_sample):
            NC = T // LC
            pbk = min(512, T)
            for d in range(2):
                pb = 32 * d
                rows = slice(pb, pb + 4)
                for b in range(T // pbk):
                    t0 = b * pbk
                    pz, pk = nps()
                    for kc in range(8):
                        P.op('pe', lambda e: e.matmul(pz[pb:pb + 4, 0:pbk], wGb[:, kc, (2 + d) * 4:(3 + d) * 4], hT[:, kc, t0:t0 + pbk],
                                                      start=(kc == 0), stop=(kc == 7)), reads=['wGb'] + hT_keys(t0, t0 + pbk), writes=[pk])
                    P.op('act', lambda e: e.activation(out=XA[rows, t0:t0 + pbk], in_=pz[pb:pb + 4, 0:pbk], func=AF.Exp,
                                                       bias=ngbT[rows, 2 + d:3 + d], scale=-1.0), reads=[pk, 'ngbT'], writes=['XA'])
                P.op('act', lambda e: e.activation(out=XA[rows, 0:T], in_=XA[rows, 0:T], func=AF.Ln, bias=1.0, scale=1.0),
                     reads=['XA'], writes=['XA'])
                for b in range(T // pbk):
                    bs = slice(b * pbk, (b + 1) * pbk)
                    if d == 0:
                        P.op('dve', lambda e: e.tensor_tensor_scan(out=XB[rows, bs], data0=XA[rows, bs], data1=zrow[rows, 0:pbk],
                                                                   initial=0.0, op0=ALU.add, op1=ALU.add), reads=['XA', 'zrow'], writes=['XB'])
                    else:
                        P.op('dve', lambda e: e.tensor_tensor_scan(out=XB[rows, bs][:, ::-1], data0=XA[rows, bs][:, ::-1],
                                                                   data1=zrow[rows, 0:pbk], initial=0.0, op0=ALU.add, op1=ALU.add),
                             reads=['XA', 'zrow'], writes=['XB'])
                if d == 0:
                    ci_, ce_ = 0, LC - 1
                else:
                    ci_, ce_ = LC - 1, 0
                B3 = XB[rows, 0:T].rearrange("p (c l) -> p c l", l=LC)
                A3 = XA[rows, 0:T].rearrange("p (c l) -> p c l", l=LC)
                S = {k_: v_[rows, :] for k_, v_ in sm.items()}
                P.op('dve', lambda e: e.tensor_tensor(out=S['gref'][:, 0:NC], in0=B3[:, :, ci_], in1=A3[:, :, ci_], op=ALU.subtract),
                     reads=['XA', 'XB'], writes=['sm_gref'])
                P.op('dve', lambda e: e.tensor_tensor(out=B3, in0=B3, in1=S['gref'][:, 0:NC].unsqueeze(2).to_broadcast([4, NC, LC]),
                                                      op=ALU.subtract), reads=['XB', 'sm_gref'], writes=['XB'])
                for b in range(T // pbk):
                    t0 = b * pbk
                    pz, pk = nps()
                    for kc in range(8):
                        P.op('pe', lambda e: e.matmul(pz[pb:pb + 4, 0:pbk], wGb[:, kc, d * 4:(d + 1) * 4], hT[:, kc, t0:t0 + pbk],
                                                      start=(kc == 0), stop=(kc == 7)), reads=['wGb'] + hT_keys(t0, t0 + pbk), writes=[pk])
                    P.op('dve', lambda e: e.scalar_tensor_tensor(out=XA[rows, t0:t0 + pbk], in0=pz[pb:pb + 4, 0:pbk], scalar=gbT[rows, d:d + 1],
                                                                 in1=XB[rows, t0:t0 + pbk], op0=ALU.add, op1=ALU.add),
                         reads=[pk, 'gbT', 'XB', 'XA'], writes=['XA'])
                P.op('dve', lambda e: e.tensor_reduce(out=S['ac'][:, 0:NC], in_=A3, axis=AX.X, op=ALU.max), reads=['XA'], writes=['sm_ac'])
                P.op('dve', lambda e: e.tensor_scalar(out=S['bl'][:, 0:NC], in0=B3[:, :, ce_], scalar1=-1.0, scalar2=None, op0=ALU.mult),
                     reads=['XB'], writes=['sm_bl'])
                if is_sample:
                    P.dma(S['m0'][:, 0:1], s_m[d, :].rearrange("(h o) -> h o", o=1), writes=['sm_m0'])
                else:
                    P.op('dve', lambda e: e.memset(S['m0'][:, 0:1], 0.0), writes=['sm_m0'])
                if d == 0:
                    P.op('dve', lambda e: e.tensor_tensor_scan(out=S['M'][:, 0:NC], data0=S['ac'][:, 0:NC], data1=S['bl'][:, 0:NC],
                                                               initial=S['m0'][:, 0:1], op0=ALU.max, op1=ALU.add),
                         reads=['sm_ac', 'sm_bl', 'sm_m0'], writes=['sm_M'])
                    P.op('dve', lambda e: e.tensor_copy(out=S['MP'][:, 0:1], in_=S['m0'][:, 0:1]), reads=['sm_m0'], writes=['sm_MP'])
                    if NC > 1:
                        P.op('dve', lambda e: e.tensor_copy(out=S['MP'][:, 1:NC], in_=S['M'][:, 0:NC - 1]), reads=['sm_M', 'sm_MP'], writes=['sm_MP'])
                else:
                    P.op('dve', lambda e: e.tensor_tensor_scan(out=S['M'][:, 0:NC][:, ::-1], data0=S['ac'][:, 0:NC][:, ::-1],
                                                               data1=S['bl'][:, 0:NC][:, ::-1], initial=S['m0'][:, 0:1],
                                                               op0=ALU.max, op1=ALU.add),
                         reads=['sm_ac', 'sm_bl', 'sm_m0'], writes=['sm_M'])
                    P.op('dve', lambda e: e.tensor_copy(out=S['MP'][:, NC - 1:NC], in_=S['m0'][:, 0:1]), reads=['sm_m0'], writes=['sm_MP'])
                    if NC > 1:
                        P.op('dve', lambda e: e.tensor_copy(out=S['MP'][:, 0:NC - 1], in_=S['M'][:, 1:NC]), reads=['sm_M', 'sm_MP'], writes=['sm_MP'])
                P.op('dve', lambda e: e.tensor_tensor(out=S['mu'][:, 0:NC], in0=S['MP'][:, 0:NC], in1=S['ac'][:, 0:NC], op=ALU.max),
                     reads=['sm_MP', 'sm_ac'], writes=['sm_mu'])
                P.op('dve', lambda e: e.tensor_tensor(out=S['al'][:, 0:NC], in0=S['MP'][:, 0:NC], in1=S['mu'][:, 0:NC], op=ALU.subtract),
                     reads=['sm_MP', 'sm_mu'], writes=['sm_al'])
                P.op('act', lambda e: e.activation(out=S['al'][:, 0:NC], in_=S['al'][:, 0:NC], func=AF.Exp), reads=['sm_al'], writes=['sm_al'])
                mub = S['mu'][:, 0:NC].unsqueeze(2).to_broadcast([4, NC, LC])
                P.op('dve', lambda e: e.tensor_tensor(out=A3, in0=A3, in1=mub, op=ALU.subtract), reads=['XA', 'sm_mu'], writes=['XA'])
                P.op('dve', lambda e: e.tensor_tensor(out=B3, in0=B3, in1=mub, op=ALU.subtract), reads=['XB', 'sm_mu'], writes=['XB'])
                P.op('dve', lambda e: e.tensor_scalar(out=XA[rows, 0:T], in0=XA[rows, 0:T], scalar1=LNK, scalar2=None, op0=ALU.add),
                     reads=['XA'], writes=['XA'])
                P.op('act', lambda e: e.activation(out=XA[rows, 0:T], in_=XA[rows, 0:T], func=AF.Exp), reads=['XA'], writes=['XA'])
                P.op('act', lambda e: e.activation(out=XB[rows, 0:T], in_=XB[rows, 0:T], func=AF.Exp), reads=['XB'], writes=['XB'])
                pz, pk = nps()
                for c in range(NC):
                    P.op('pe', lambda e: e.transpose(out=pz[:, c * 8:c * 8 + 4], in_=XA[rows, c * LC:(c + 1) * LC],
                                                     identity=ident[rows, pb:pb + 4]), reads=['XA', 'ident'], writes=[pk])
                    P.op('pe', lambda e: e.transpose(out=pz[:, c * 8 + 4:c * 8 + 8], in_=XB[rows, c * LC:(c + 1) * LC],
                                                     identity=ident[rows, pb:pb + 4]), reads=['XB', 'ident'], writes=[pk])
                P.op('dve', lambda e: e.tensor_copy(out=Wtok[:, d, 0:NC, :], in_=pz[:, 0:NC * 8].rearrange("p (c k) -> p c k", k=8)),
                     reads=[pk], writes=['Wtok'])
                P.op('dve', lambda e: e.tensor_copy(out=Wtokb[:, d, 0:NC, :], in_=Wtok[:, d, 0:NC, 0:4]), reads=['Wtok'], writes=['Wtokb'])
                pz, pk = nps()
                for hd in range(4):
                    P.op('pe', lambda e: e.matmul(pz[:, hd * 16:hd * 16 + NC], SEL[rows, hd, :], S['al'][:, 0:NC], start=True, stop=True),
                         reads=['SEL', 'sm_al'], writes=[pk])
                P.op('dve', lambda e: e.tensor_copy(out=ALb[:, d, :, 0:NC], in_=pz[:, 0:64].rearrange("p (h c) -> p h c", c=16)[:, :, 0:NC]),
                     reads=[pk], writes=['ALb'])

        def conv_tile(dst, dkey, slot_j, widx, t0src, T, is_sample):
            X = FT[1][:, 0:T]
            A = FT[0][:, 0:T]
            if is_sample:
                R_, Cw = T // 64, 64
                taps = [(dr, dc) for dr in (-1, 0, 1) for dc in (-1, 0, 1)]
            else:
                R_, Cw = 1, T
                taps = [(0, dc) for dc in (-1, 0, 1)]
            X3 = X.rearrange("p (r c) -> p r c", c=Cw)
            A3 = A.rearrange("p (r c) -> p r c", c=Cw)
            P.op('dve', lambda e: e.tensor_scalar(out=A, in0=X, scalar1=cw[:, widx, 4:5], scalar2=None, op0=ALU.mult),
                 reads=['FT1', 'cw'], writes=['FT0'])
            for (dr, dc) in taps:
                if dr == 0 and dc == 0:
                    continue
                r0, r1 = max(0, -dr), R_ - max(0, dr)
                c0, c1 = max(0, -dc), Cw - max(0, dc)
                ti = (dr + 1) * 3 + (dc + 1)
                P.op('dve', lambda e: e.scalar_tensor_tensor(out=A3[:, r0:r1, c0:c1], in0=X3[:, r0 + dr:r1 + dr, c0 + dc:c1 + dc],
                                                             scalar=cw[:, widx, ti:ti + 1], in1=A3[:, r0:r1, c0:c1],
                                                             op0=ALU.mult, op1=ALU.add), reads=['FT1', 'FT0', 'cw'], writes=['FT0'])
            P.op('act', lambda e: e.activation(out=dst[:, slot_j, 0:T], in_=A, func=AF.Silu, bias=cb[:, widx:widx + 1], scale=1.0),
                 reads=['FT0', 'cb'], writes=[dkey])

        hacc = [FT[2 + i][:, 0:TS].rearrange("p (j e) -> p j e", e=DH) for i in range(4)]

        def mlstm_head(hd, off, T, is_sample, pidx):
            NC = T // LC
            NTt = T // 128
            pbk = min(512, T)
            for qk in range(2):
                load_w1(qk * 2048 + hd * DH, DH)
                for j in range(4):
                    for b in range(T // pbk):
                        t0 = b * pbk
                        pz, pk = nps()
                        for kc in range(8):
                            P.op('pe', lambda e: e.matmul(pz[:, 0:pbk], wbf1[:, kc, j * 128:(j + 1) * 128], hT[:, kc, t0:t0 + pbk],
                                                          start=(kc == 0), stop=(kc == 7)), reads=['wbf1'] + hT_keys(t0, t0 + pbk), writes=[pk])
                        P.op('act', lambda e: e.copy(out=FT[1][:, t0:t0 + pbk], in_=pz[:, 0:pbk]), reads=[pk], writes=['FT1'])
                    widx = (qk * 4 + hd) * 4 + j
                    if qk == 0:
                        conv_tile(qT, ('yT', 4 + j), j, widx, 0, T, is_sample)
                    else:
                        conv_tile(kT, 'kT', j, widx, 0, T, is_sample)
            load_w1(4096 + hd * DH, DH)
            qkeys = [('yT', 4 + j) for j in range(4)]
            for d in range(2):
                rev = (d == 1)
                if is_sample:
                    P.dma(Cst[:], s_C[d, hd].rearrange("(j p) e -> p j e", p=128), writes=['Cst'])
                    P.dma(nst[:, 0:4], s_n[d, hd].rearrange("(j p) -> p j", p=128), writes=['nst'], allow_slow_non_contiguous=True)
                else:
                    P.op('pool', lambda e: e.memset(Cst[:], 0.0), writes=['Cst'])
                    P.op('pool', lambda e: e.memset(nst[:, 0:4], 0.0), writes=['nst'])
                chunks = list(range(NC))
                if rev:
                    chunks = chunks[::-1]
                for c in chunks:
                    cs = slice(c * LC, (c + 1) * LC)
                    wcol = Wtok[:, d, c, hd:hd + 1]
                    thcol = Wtok[:, d, c, 4 + hd:5 + hd]
                    alcol = ALb[:, d, hd, c:c + 1]
                    pv, pvk = nps()
                    for kc in range(8):
                        P.op('pe', lambda e: e.matmul(pv[:, 0:DH], hT[:, kc, cs], wbf1[:, kc, :], start=(kc == 0), stop=(kc == 7)),
                             reads=['wbf1', ('hT', c)], writes=[pvk])
                    P.op('act', lambda e: e.copy(out=vch[:], in_=pv[:, 0:DH]), reads=[pvk], writes=['vch'])
                    pt, ptk = nps()
                    ptb = pt[:].bitcast(BF16)
                    for j in range(4):
                        P.op('pe', lambda e: e.transpose(out=ptb[:, j * 128:(j + 1) * 128], in_=kT[:, j, cs], identity=identb[:]),
                             reads=['kT', 'identb'], writes=[ptk])
                    P.op('act', lambda e: e.copy(out=ktok[:], in_=ptb[:, 0:DH]), reads=[ptk], writes=['ktok'])
                    ps_, psk = nps()
                    for j in range(4):
                        P.op('pe', lambda e: e.matmul(ps_[:, 0:128], kT[:, j, cs], qT[:, j, cs], start=(j == 0), stop=(j == 3)),
                             reads=['kT'] + qkeys, writes=[psk])
                    P.op('dve', lambda e: e.scalar_tensor_tensor(out=sTs[:], in0=ps_[:, 0:128], scalar=wcol, in1=mC[:, d, :],
                                                                 op0=ALU.mult, op1=ALU.mult), reads=[psk, 'Wtok', 'mC'], writes=['sTs'])
                    P.op('dve', lambda e: e.tensor_scalar(out=Cst[:], in0=Cst[:], scalar1=alcol, scalar2=None, op0=ALU.mult),
                         reads=['Cst', 'ALb'], writes=['Cst'])
                    P.op('act', lambda e: e.copy(out=Cbf[:], in_=Cst[:]), reads=['Cst'], writes=['Cbf'])
                    P.op('dve', lambda e: e.tensor_scalar(out=nst[:, 0:4], in0=nst[:, 0:4], scalar1=alcol, scalar2=None, op0=ALU.mult),
                         reads=['nst', 'ALb'], writes=['nst'])
                    P.op('dve', lambda e: e.tensor_copy(out=nbf[:], in_=nst[:, 0:4]), reads=['nst'], writes=['nbf'])
                    pn, pnk = nps()
                    for j in range(4):
                        P.op('pe', lambda e: e.matmul(pn[:, 0:DH], qT[:, j, cs], Cbf[:, j, :], start=(j == 0), stop=False),
                             reads=qkeys + ['Cbf'], writes=[pnk])
                    P.op('pe', lambda e: e.matmul(pn[:, 0:DH], sTs[:], vch[:], start=False, stop=True),
                         reads=['sTs', 'vch'], writes=[pnk])
                    pd_, pdk = nps()
                    for j in range(4):
                        P.op('pe', lambda e: e.matmul(pd_[:, 0:1], qT[:, j, cs], nbf[:, j:j + 1], start=(j == 0), stop=False),
                             reads=qkeys + ['nbf'], writes=[pdk])
                    P.op('pe', lambda e: e.matmul(pd_[:, 0:1], sTs[:], onesb[:], start=False, stop=True), reads=['sTs', 'onesb'], writes=[pdk])
                    P.op('act', lambda e: e.activation(out=dstat[:, 2:3], in_=pd_[:, 0:1], func=AF.Abs), reads=[pdk], writes=['dstat'])
                    P.op('dve', lambda e: e.tensor_tensor(out=dstat[:, 0:1], in0=dstat[:, 2:3], in1=thcol, op=ALU.max),
                         reads=['dstat', 'Wtok'], writes=['dstat'])
                    P.op('dve', lambda e: e.reciprocal(out=dstat[:, 1:2], in_=dstat[:, 0:1]), reads=['dstat'], writes=['dstat'])
                    hdst = hacc[c // 4][:, c % 4, :]
                    if d == 0:
                        P.op('act', lambda e: e.activation(out=hdst, in_=pn[:, 0:DH], func=AF.Identity, scale=dstat[:, 1:2]),
                             reads=[pnk, 'dstat'], writes=[('hacc', c)])
                    else:
                        P.op('dve', lambda e: e.scalar_tensor_tensor(out=hdst, in0=pn[:, 0:DH], scalar=dstat[:, 1:2], in1=hdst,
                                                                     op0=ALU.mult, op1=ALU.add), reads=[pnk, 'dstat', ('hacc', c)], writes=[('hacc', c)])
                    P.op('pool', lambda e: e.tensor_scalar(out=vw[:], in0=vch[:], scalar1=wcol, scalar2=None, op0=ALU.mult),
                         reads=['vch', 'Wtok'], writes=['vw'])
                    for j in range(4):
                        pc, pck = nps()
                        P.op('pe', lambda e: e.matmul(pc[:, 0:DH], ktok[:, j * 128:(j + 1) * 128], vw[:], start=True, stop=True),
                             reads=['ktok', 'vw'], writes=[pck])
                        P.op('dve', lambda e: e.tensor_tensor(out=Cst[:, j, :], in0=Cst[:, j, :], in1=pc[:, 0:DH], op=ALU.add),
                             reads=[pck, 'Cst'], writes=['Cst'])
                    pq_, pqk = nps()
                    for j in range(4):
                        P.op('pe', lambda e: e.matmul(pq_[:, j:j + 1], ktok[:, j * 128:(j + 1) * 128], Wtokb[:, d, c, hd:hd + 1],
                                                      start=True, stop=True), reads=['ktok', 'Wtokb'], writes=[pqk])
                    P.op('dve', lambda e: e.tensor_tensor(out=nst[:, 0:4], in0=nst[:, 0:4], in1=pq_[:, 0:4], op=ALU.add),
                         reads=[pqk, 'nst'], writes=['nst'])
                if not is_sample:
                    P.dma(o_C[pidx, d, hd].rearrange("(j p) e -> p j e", p=128), Cst[:], reads=['Cst'], q='pool')
                    P.dma(o_n[pidx, d, hd].rearrange("(j p) -> p j", p=128), nst[:, 0:4], reads=['nst'], q='pool', allow_slow_non_contiguous=True)
            load_w1(6144 + hd * DH, DH)
            for tt in range(NTt):
                hdst = hacc[tt // 4][:, tt % 4, :]
                pz, pk = nps()
                for kc in range(8):
                    P.op('pe', lambda e: e.matmul(pz[:, 0:DH], hT[:, kc, tt * 128:(tt + 1) * 128], wbf1[:, kc, :],
                                                  start=(kc == 0), stop=(kc == 7)), reads=['wbf1', ('hT', tt)], writes=[pk])
                P.op('act', lambda e: e.activation(out=FT[0][:, 0:DH], in_=pz[:, 0:DH], func=AF.Sigmoid), reads=[pk], writes=['FT0'])
                P.op('dve', lambda e: e.tensor_tensor(out=hdst, in0=hdst, in1=FT[0][:, 0:DH], op=ALU.mult),
                     reads=['FT0', ('hacc', tt)], writes=[('hacc', tt)])
                P.op('act', lambda e: e.activation(out=FT[0][:, 0:DH], in_=hdst, func=AF.Square, accum_out=dstat[:, 4:5]),
                     reads=[('hacc', tt), 'FT0'], writes=['FT0', 'dstat'])
                P.op('dve', lambda e: e.tensor_scalar(out=dstat[:, 5:6], in0=dstat[:, 4:5], scalar1=1.0 / DH, scalar2=1e-6,
                                                      op0=ALU.mult, op1=ALU.add), reads=['dstat'], writes=['dstat'])
                P.op('act', lambda e: e.activation(out=dstat[:, 6:7], in_=dstat[:, 5:6], func=AF.Sqrt), reads=['dstat'], writes=['dstat'])
                P.op('dve', lambda e: e.reciprocal(out=dstat[:, 7:8], in_=dstat[:, 6:7]), reads=['dstat'], writes=['dstat'])
                P.op('dve', lambda e: e.tensor_scalar(out=hdst, in0=hdst, scalar1=dstat[:, 7:8], scalar2=None, op0=ALU.mult),
                     reads=[('hacc', tt), 'dstat'], writes=[('hacc', tt)])
            load_w1(8192 + hd * DH, DH)
            for tt in range(NTt):
                hdst = hacc[tt // 4][:, tt % 4, :]
                pz, pk = nps()
                for kc in range(8):
                    P.op('pe', lambda e: e.matmul(pz[:, 0:DH], hT[:, kc, tt * 128:(tt + 1) * 128], wbf1[:, kc, :],
                                                  start=(kc == 0), stop=(kc == 7)), reads=['wbf1', ('hT', tt)], writes=[pk])
                P.op('act', lambda e: e.activation(out=FT[0][:, 0:DH], in_=pz[:, 0:DH], func=AF.Silu), reads=[pk], writes=['FT0'])
                P.op('dve', lambda e: e.tensor_tensor(out=hdst, in0=hdst, in1=FT[0][:, 0:DH], op=ALU.mult),
                     reads=['FT0', ('hacc', tt)], writes=[('hacc', tt)])
                pz, pk = nps()
                for j in range(4):
                    P.op('pe', lambda e: e.transpose(out=pz[:, j * 128:(j + 1) * 128], in_=hdst[:, j * 128:(j + 1) * 128], identity=ident[:]),
                         reads=[('hacc', tt), 'ident'], writes=[pk])
                for j in range(4):
                    P.op('act', lambda e: e.activation(out=yT[:, j, tt * 128:(tt + 1) * 128], in_=pz[:, j * 128:(j + 1) * 128],
                                                       func=AF.Identity, scale=mng[:, hd * 4 + j:hd * 4 + j + 1]),
                         reads=[pk, 'mng'], writes=[('yT', j)])

        for si in seq_ids:
            off, T, cidx, is_sample = SEQS[si]
            P.barrier()
            make_gate(1, cidx)
            make_hT(1, x1, 'x1', off, T, cidx)
            P.barrier()
            gates_seq(T, is_sample)
            if not is_sample:
                for d in range(2):
                    lastc = (T // LC - 1) if d == 0 else 0
                    P.dma(o_m[si - 1, d, :].rearrange("(h o) -> h o", o=1), sm['M'][32 * d:32 * d + 4, lastc:lastc + 1],
                          reads=['sm_M'], q='pool')
            for hd in range(4):
                P.barrier()
                mlstm_head(hd, off, T, is_sample, si - 1)
                if debug and debug.get('dump_y'):
                    for j in range(4):
                        dump("yM%d_%d" % (hd, j), yT[:, j, 0:T], [('yT', j)], T, col0=off)
                P.barrier()
                load_wo(w_out_odd[hd * DH:(hd + 1) * DH, :], 128, nk=4)
                last = (hd == 3)
                outproj(1, [(128, s_) for s_ in range(4)], x1, 'x1', (yout if last else x1), ('yout' if last else 'x1'),
                        off, T, cidx, final=last)
        P.barrier()
        L1.close()
    P.finish()
    sems = {s: es.enter_context(nc.semaphore(s)) for s in P.sem_names}
    P.emit(sems)
    es.close()
    global _last_dslot
    _last_dslot = dslot if debug else {}
    return nc, P


def host_inputs(inp, core):
    f = lambda a: np.ascontiguousarray(a, dtype=np.float32)
    b = core % 2
    m = {}
    m["xin"] = f(np.concatenate([inp["x_sample"][b], inp["x_prompt"][2 * core], inp["x_prompt"][2 * core + 1]], axis=0))
    cond = np.stack([inp["c"][b], inp["c_ctx"]], axis=0)
    m["condT"] = f(cond.reshape(2, 8, 128).transpose(2, 1, 0))
    m["s_hgrn"] = f(inp["state_hgrn"][b, 0])
    m["s_rwkv"] = f(inp["state_rwkv"][b, 0])
    m["s_C"] = f(inp["state_mlstm_C"][b, 0])
    m["s_n"] = f(inp["state_mlstm_n"][b, 0])
    m["s_m"] = f(inp["state_mlstm_m"][b, 0])
    m["w_mod"] = f(inp["w_mod"])
    m["b_modT"] = f(inp["b_mod"].reshape(2, 24, 128).transpose(2, 0, 1))
    m["norm_gT"] = f(inp["norm_g"].reshape(2, 8, 128).transpose(2, 0, 1))
    m["fnorm_gT"] = f(inp["final_norm_g"].reshape(8, 128).T)
    w = inp["w_in_even"][0]
    DA = 1024
    wA = np.stack([np.concatenate([w[:, g * DA + h * 128: g * DA + (h + 1) * 128] for g in (0, 1, 4, 2, 3)], axis=1)
                   for h in range(8)], axis=0)
    m["wA"] = f(wA)
    o = 5 * DA
    zb0 = o + 3328
    wB = np.stack([np.concatenate([w[:, o + g * 1024 + h * 64: o + g * 1024 + (h + 1) * 64] for g in (0, 1, 2)]
                                  + [w[:, zb0 + h * 64: zb0 + (h + 1) * 64]], axis=1) for h in range(16)], axis=0)
    m["wB"] = f(wB)
    m["wLR"] = f(w[:, o + 3072: o + 3328])
    m["w_out_even"] = f(inp["w_out_even"][0])
    m["lbT"] = f(inp["hgrn_lb_logits"].reshape(2, 8, 128).transpose(2, 0, 1))
    m["hg_gT"] = f(inp["hgrn_norm_g"][0].reshape(8, 128).T)
    mu = inp["rwkv_shift_mu"][0]
    mr = np.zeros((64, 2, 4, 16), np.float32)
    for g in range(3):
        mr[:, :, g, :] = mu[:, g * 1024:(g + 1) * 1024].reshape(2, 16, 64).transpose(2, 0, 1)
    m["mu_rkv"] = mr
    m["mu_lr"] = f(mu[:, 3072:3328].reshape(2, 4, 64).transpose(2, 0, 1))
    m["w0T"] = f(inp["rwkv_w0"][0].reshape(2, 16, 64).transpose(2, 0, 1))
    m["a0T"] = f(inp["rwkv_a0"][0].reshape(2, 16, 64).transpose(2, 0, 1))
    m["w2"] = f(inp["rwkv_w2"][0])
    m["a2"] = f(inp["rwkv_a2"][0])
    m["kkT"] = f(inp["rwkv_k_k"][0].reshape(16, 64).T)
    m["kaT"] = f(inp["rwkv_k_a"][0].reshape(16, 64).T)
    m["rkT"] = f(inp["rwkv_r_k"][0].T)
    m["gngT"] = f(inp["rwkv_gn_g"][0].reshape(16, 64).T)
    m["gnbT"] = f(inp["rwkv_gn_b"][0].reshape(16, 64).T)
    s = np.arange(128)[:, None]
    t = np.arange(128)[None, :]
    same = (s // 32) == (t // 32)
    m["maskH"] = np.stack([(same & (s <= t)), (same & (s >= t))]).astype(np.float32)
    m["ident_in"] = np.eye(128, dtype=np.float32)
    s6 = np.arange(64)[:, None]
    t6 = np.arange(64)[None, :]
    mr_ = np.zeros((2, 3, 64, 128), np.float32)
    for d_, (st_, inc_) in enumerate([((s6 < t6), (s6 <= t6)), ((s6 > t6), (s6 >= t6))]):
        st_ = st_.astype(np.float32)
        inc_ = inc_.astype(np.float32)
        mr_[d_, 0, :, 0:64] = -st_
        mr_[d_, 0, :, 64:128] = -inc_
        mr_[d_, 1, :, 0:64] = st_
        mr_[d_, 1, :, 64:128] = inc_
        mr_[d_, 2, :, 0:64] = -(st_.T)
    m["maskR"] = mr_
    m["w_in_odd"] = f(inp["w_in_odd"][0])
    m["w_out_odd"] = f(inp["w_out_odd"][0])
    m["maskC"] = np.stack([(s <= t), (s >= t)]).astype(np.float32)
    sel = np.zeros((36, 4, 128), np.float32)
    gbt = np.zeros((36, 4), np.float32)
    for pb_ in (0, 32):
        for k_ in range(4):
            sel[pb_ + k_, k_, :] = 1.0
        gbt[pb_:pb_ + 4, :] = inp["mlstm_gate_b"][0].T
    m["sel_d"] = sel
    m["gbT_d"] = gbt
    m["cw_d"] = f(inp["mlstm_conv_w"][0].reshape(9, 32, 128).transpose(2, 1, 0))
    m["cb_d"] = f(inp["mlstm_conv_b"][0].reshape(32, 128).T)
    m["mng_d"] = f(inp["mlstm_norm_g"][0].reshape(16, 128).T)
    return m


def kernel(**inp):
    inp = {k: np.asarray(v) for k, v in inp.items()}
    nc, P = build()
    in_maps = [host_inputs(inp, c) for c in range(NCORES)]
    res = run_bass_kernel_spmd(nc, in_maps, core_ids=list(range(NCORES)))
    r = res.results
    y_prompt = np.zeros((16, TP, D), np.float32)
    y_sample = np.zeros((2, TS, D), np.float32)
    for c in range(NCORES):
        y_prompt[2 * c] = r[c]["yout"][TS:TS + TP]
        y_prompt[2 * c + 1] = r[c]["yout"][TS + TP:]
    for b in range(2):
        y_sample[b] = r[b]["yout"][0:TS]
    new_hgrn = np.concatenate([r[c]["o_hgrn"] for c in range(NCORES)], axis=0)[:, None]
    new_rwkv = np.concatenate([r[c]["o_rwkv"] for c in range(NCORES)], axis=0)[:, None]
    new_C = np.concatenate([r[c]["o_C"] for c in range(NCORES)], axis=0)[:, None]
    new_n = np.concatenate([r[c]["o_n"] for c in range(NCORES)], axis=0)[:, None]
    new_m = np.concatenate([r[c]["o_m"] for c in range(NCORES)], axis=0)[:, None]
    return (y_prompt, y_sample, new_hgrn.astype(np.float32), new_rwkv.astype(np.float32),
            new_C.astype(np.float32), new_n.astype(np.float32), new_m.astype(np.float32))
```

```python
import contextlib
import numpy as np
import concourse.bass as bass
import concourse.mybir as mybir
from concourse.bass_utils import run_bass_kernel_spmd

F32 = mybir.dt.float32
BF16 = mybir.dt.bfloat16
ALU = mybir.AluOpType
AF = mybir.ActivationFunctionType
AX = mybir.AxisListType

D = 1024
TS = 2048
TP = 256
TT = TS + 2 * TP
NCORES = 8


class _Rec:
    def __init__(self):
        self.calls = []

    def __getattr__(self, name):
        def f(*a, **k):
            self.calls.append((name, a, k))
            return self
        return f


class Prog:
    ENGS = ['pe', 'dve', 'act', 'pool', 'sp']
    NDMA = 16

    def __init__(self, nc):
        self.nc = nc
        self.ops = {e: [] for e in self.ENGS}
        self.cnt = {}
        self.waited = {e: {} for e in self.ENGS}
        self.last_write = {}
        self.readers = {}
        self.dma_rr = 0
        self.sem_names = list(self.ENGS) + ['d%d' % i for i in range(self.NDMA)]
        for s in self.sem_names:
            self.cnt[s] = 0
        self.n_ops = 0

    def _deps(self, eng, reads, writes):
        deps = {}

        def add(p):
            if p is None:
                return
            f, n = p
            if f == 'pe' and eng == 'pe':
                return
            if n > deps.get(f, 0):
                deps[f] = n
        for k in reads:
            add(self.last_write.get(k))
        for k in writes:
            add(self.last_write.get(k))
            for p in self.readers.get(k, ()):
                add(p)
        waits = []
        for f, n in deps.items():
            if n > self.waited[eng].get(f, 0):
                waits.append((f, n))
                self.waited[eng][f] = n
        return waits

    def _commit(self, tag, reads, writes):
        for k in reads:
            lst = self.readers.setdefault(k, [])
            lst[:] = [p for p in lst if p[0] != tag[0]]
            lst.append(tag)
        for k in writes:
            self.last_write[k] = tag
            self.readers[k] = []

    def op(self, eng, fn, reads=(), writes=()):
        rec = _Rec()
        fn(rec)
        name, a, k = rec.calls[0]
        fn = (lambda e, name=name, a=a, k=k: getattr(e, name)(*a, **k))
        waits = self._deps(eng, reads, writes)
        self.cnt[eng] += 1
        tag = (eng, self.cnt[eng])
        self.ops[eng].append((waits, fn, eng, 1))
        self._commit(tag, reads, writes)
        self.n_ops += 1

    def dma(self, out, in_, reads=(), writes=(), q='sp', **kw):
        d = 'd%d' % self.dma_rr
        self.dma_rr = (self.dma_rr + 1) % self.NDMA
        waits = self._deps(q, reads, writes)
        prev = self.cnt[d]
        if prev > self.waited[q].get(d, 0):
            waits.append((d, prev))
            self.waited[q][d] = prev
        self.cnt[d] += 16
        tag = (d, self.cnt[d])
        self.ops[q].append((waits, (lambda e: e.dma_start(out=out, in_=in_, **kw)), d, 16))
        self._commit(tag, reads, writes)
        self.n_ops += 1

    def barrier(self):
        allsems = list(self.sem_names)
        for e in self.ENGS:
            waits = []
            for f in allsems:
                if self.cnt[f] > self.waited[e].get(f, 0):
                    waits.append((f, self.cnt[f]))
                    self.waited[e][f] = self.cnt[f]
            self.ops[e].append((waits, None, None, 0))

    def finish(self, q='sp'):
        waits = []
        for i in range(self.NDMA):
            d = 'd%d' % i
            if self.cnt[d] > self.waited[q].get(d, 0):
                waits.append((d, self.cnt[d]))
                self.waited[q][d] = self.cnt[d]
        self.ops[q].append((waits, None, None, 0))

    def emit(self, sems):
        ops = self.ops

        def run(e, lst):
            for waits, fn, semname, inc in lst:
                for f, n in waits:
                    e.wait_ge(sems[f], n)
                if fn is not None:
                    fn(e).then_inc(sems[semname], inc)
        with self.nc.Block() as block:
            @block.tensor
            def _(e):
                run(e, ops['pe'])

            @block.vector
            def _(e):
                run(e, ops['dve'])

            @block.scalar
            def _(e):
                run(e, ops['act'])

            @block.gpsimd
            def _(e):
                run(e, ops['pool'])

            @block.sync
            def _(e):
                run(e, ops['sp'])


SEQS = [(0, TS, 0, True), (TS, TP, 1, False), (TS + TP, TP, 1, False)]


def build(debug=None):
    nc = bass.Bass('TRN2', target_bir_lowering=False)
    P = Prog(nc)
    es = contextlib.ExitStack()

    def din(name, shape):
        return nc.dram_tensor(name, list(shape), F32, kind="ExternalInput").ap()

    def dout(name, shape):
        return nc.dram_tensor(name, list(shape), F32, kind="ExternalOutput").ap()

    xin = din("xin", [TT, D])
    condT = din("condT", [128, 8, 2])
    s_hgrn = din("s_hgrn", [2, 8, 128, 128])
    s_rwkv = din("s_rwkv", [2, 16, 64, 64])
    s_C = din("s_C", [2, 4, 512, 512])
    s_n = din("s_n", [2, 4, 512])
    s_m = din("s_m", [2, 4])
    w_mod = din("w_mod", [2, D, 3 * D])
    b_modT = din("b_modT", [128, 2, 24])
    norm_gT = din("norm_gT", [128, 2, 8])
    fnorm_gT = din("fnorm_gT", [128, 8])
    wA = din("wA", [8, D, 640])
    wB = din("wB", [16, D, 256])
    wLR = din("wLR", [D, 256])
    w_out_even = din("w_out_even", [2 * D, D])
    lbT = din("lbT", [128, 2, 8])
    hg_gT = din("hg_gT", [128, 8])
    mu_rkv = din("mu_rkv", [64, 2, 4, 16])
    mu_lr = din("mu_lr", [64, 2, 4])
    w0T = din("w0T", [64, 2, 16])
    a0T = din("a0T", [64, 2, 16])
    w2 = din("w2", [2, 64, D])
    a2 = din("a2", [2, 64, D])
    kkT = din("kkT", [64, 16])
    kaT = din("kaT", [64, 16])
    rkT = din("rkT", [64, 16])
    gngT = din("gngT", [64, 16])
    gnbT = din("gnbT", [64, 16])
    maskR = din("maskR", [2, 3, 64, 128])
    maskH = din("maskH", [2, 128, 128])
    ident_d = din("ident_in", [128, 128])
    w_in_odd = din("w_in_odd", [D, 10256])
    w_out_odd = din("w_out_odd", [2 * D, D])
    maskC = din("maskC", [2, 128, 128])
    sel_d = din("sel_d", [36, 4, 128])
    gbT_d = din("gbT_d", [36, 4])
    cw_d = din("cw_d", [128, 32, 9])
    cb_d = din("cb_d", [128, 32])
    mng_d = din("mng_d", [128, 16])

    yout = dout("yout", [TT, D])
    o_hgrn = dout("o_hgrn", [2, 2, 8, 128, 128])
    o_rwkv = dout("o_rwkv", [2, 2, 16, 64, 64])
    o_C = dout("o_C", [2, 2, 4, 512, 512])
    o_n = dout("o_n", [2, 2, 4, 512])
    o_m = dout("o_m", [2, 2, 4])
    dbg = dout("dbg", [40, 128, TT]) if debug else None
    dslot = {}
    dumpt = {}
    x1 = dout("x1", [TT, D]) if debug else nc.dram_tensor("x1", [TT, D], F32, kind="Internal").ap()

    def sb(name, shape, dt=F32):
        return es.enter_context(nc.sbuf_tensor(name, list(shape), dt))

    pstiles = [es.enter_context(nc.psum_tensor("ps%d" % i, [128, 512], F32)) for i in range(8)]
    psrr = [0]

    def nps():
        i = psrr[0]
        psrr[0] = (i + 1) % 8
        return pstiles[i], 'ps%d' % i

    def dump(name, ap, keys, n, col0=0, parts=128):
        if not debug:
            return
        slot = dslot.setdefault(name, len(dslot))
        dt_ = dumpt['tile']
        for c0 in range(0, n, 512):
            w_ = min(512, n - c0)
            P.op('pool', (lambda e, c0=c0, w_=w_: e.tensor_copy(out=dt_[0:parts, 0:w_], in_=ap[:, c0:c0 + w_])),
                 reads=keys, writes=['dumpt'])
            P.dma(dbg[slot, 0:parts, col0 + c0:col0 + c0 + w_], dt_[0:parts, 0:w_], reads=['dumpt'])

    if debug:
        dumpt['tile'] = sb("dumpt", [128, 512])

    ident = sb("ident", [128, 128])
    ones = sb("ones", [128, 128])
    P.dma(ident[:], ident_d[:], writes=['ident'])
    P.op('dve', lambda e: e.memset(ones[:], 1.0), writes=['ones'])

    condT_sb = sb("condT_sb", [128, 8, 2])
    bmod_sb = sb("bmod_sb", [128, 2, 24])
    ng_sb = sb("ng_sb", [128, 2, 8])
    fng_sb = sb("fng_sb", [128, 8])
    lb_sb = sb("lb_sb", [128, 2, 8])
    hgg_sb = sb("hgg_sb", [128, 8])
    for t_, d_, k_ in [(condT_sb, condT, 'condT'), (bmod_sb, b_modT, 'bmod'), (ng_sb, norm_gT, 'ng'),
                       (fng_sb, fnorm_gT, 'fng'), (lb_sb, lbT, 'lb'), (hgg_sb, hg_gT, 'hgg')]:
        P.dma(t_[:], d_[:], writes=[k_])

    scT = sb("scT", [128, 8, 2])
    P.op('act', lambda e: e.activation(out=scT[:], in_=condT_sb[:], func=AF.Silu), reads=['condT'], writes=['scT'])
    mT = sb("mT", [128, 2, 24, 2])
    sc1 = sb("sc1", [128, 2, 8, 2])
    gate_bc = sb("gate_bc", [128, D])
    dg = sb("dg", [128, 128])

    def make_gate(l, c):
        if True:
            for half in range(2):
                pz, pk = nps()
                for kq in range(4):
                    kc = half * 4 + kq
                    P.op('dve', lambda e: e.tensor_scalar(
                        out=dg[:], in0=ident[:], scalar1=mT[:, l, 16 + kc, c:c + 1], scalar2=None, op0=ALU.mult),
                        reads=['ident', 'mT'], writes=['dg'])
                    P.op('pe', lambda e: e.matmul(pz[:, kq * 128:(kq + 1) * 128], ones[:], dg[:], start=True, stop=True),
                         reads=['ones', 'dg'], writes=[pk])
                P.op('act', lambda e: e.copy(out=gate_bc[:, half * 512:(half + 1) * 512], in_=pz[:]),
                     reads=[pk], writes=['gate_bc'])

    with contextlib.ExitStack() as es2:
        wm = [es2.enter_context(nc.sbuf_tensor("wm%d" % i, [128, 8, 512], F32)) for i in range(2)]
        for l in range(2):
            for cbk in range(6):
                i = (l * 6 + cbk) % 2
                P.dma(wm[i][:], w_mod[l].rearrange("(kc p) n -> p kc n", p=128)[:, :, cbk * 512:(cbk + 1) * 512],
                      writes=['wm%d' % i])
                pz, pk = nps()
                for j in range(4):
                    for kc in range(8):
                        P.op('pe', lambda e: e.matmul(pz[:, j * 2:(j + 1) * 2], wm[i][:, kc, j * 128:(j + 1) * 128],
                                                      scT[:, kc, :], start=(kc == 0), stop=(kc == 7)),
                             reads=['wm%d' % i, 'scT'], writes=[pk])
                P.op('dve', lambda e: e.tensor_tensor(out=mT[:, l, cbk * 4:(cbk + 1) * 4, :],
                                                      in0=pz[:, 0:8].rearrange("p (j c) -> p j c", c=2),
                                                      in1=bmod_sb[:, l, cbk * 4:(cbk + 1) * 4].unsqueeze(2).to_broadcast([128, 4, 2]),
                                                      op=ALU.add),
                     reads=[pk, 'bmod'], writes=['mT'])
    P.barrier()
    for l in range(2):
        P.op('dve', lambda e: e.scalar_tensor_tensor(
            out=sc1[:, l], in0=mT[:, l, 8:16, :], scalar=1.0,
            in1=ng_sb[:, l, :].unsqueeze(2).to_broadcast([128, 8, 2]), op0=ALU.add, op1=ALU.mult),
            reads=['mT', 'ng'], writes=['sc1'])
    fng_holder = {}

    def make_fng():
        fng_bc = sb("fng_bc", [128, D])
        fng_holder['t'] = fng_bc
        for half in range(2):
            pz, pk = nps()
            for kq in range(4):
                kc = half * 4 + kq
                P.op('dve', (lambda e, kc=kc: e.tensor_scalar(
                    out=dg[:], in0=ident[:], scalar1=fng_sb[:, kc:kc + 1], scalar2=None, op0=ALU.mult)),
                    reads=['ident', 'fng'], writes=['dg'])
                P.op('pe', (lambda e, pz=pz, kq=kq: e.matmul(pz[:, kq * 128:(kq + 1) * 128], ones[:], dg[:],
                                                             start=True, stop=True)),
                     reads=['ones', 'dg'], writes=[pk])
            P.op('act', (lambda e, half=half, pz=pz: e.copy(out=fng_bc[:, half * 512:(half + 1) * 512], in_=pz[:])),
                 reads=[pk], writes=['fng_bc'])

    lbv = sb("lbv", [128, 8])
    oml = sb("oml", [128, 8])
    P.op('dve', lambda e: e.tensor_tensor(out=lbv[:], in0=lb_sb[:, 0, :], in1=lb_sb[:, 1, :], op=ALU.subtract),
         reads=['lb'], writes=['lbv'])
    P.op('act', lambda e: e.activation(out=lbv[:], in_=lbv[:], func=AF.Sigmoid), reads=['lbv'], writes=['lbv'])
    P.op('act', lambda e: e.activation(out=oml[:], in_=lbv[:], func=AF.Identity, bias=1.0, scale=-1.0),
         reads=['lbv'], writes=['oml'])

    hT = sb("hT", [128, 8, TS], BF16)
    yT = sb("yT", [128, 8, TS], BF16)
    st4 = sb("st4", [128, 4])
    wst = sb("wst", [128, 8, 256])
    FT = [sb("FT%d" % i, [128, TS + 32]) for i in range(6)]
    xt = [FT[0][:, 0:D], FT[0][:, D:2 * D]]
    xn = FT[1][:, 0:D]
    junk = FT[1][:, D:2 * D]
    wo_v = [FT[2][:, 0:TS].bitcast(BF16).rearrange("p (s n) -> p s n", n=D),
            FT[3][:, 0:TS].bitcast(BF16).rearrange("p (s n) -> p s n", n=D)]

    def load_wo(src, parts, nk=8):
        v = src.rearrange("(kc p) n -> p kc n", p=parts)
        for c0 in range(0, D, 256):
            w_ = min(256, D - c0)
            P.dma(wst[0:parts, 0:nk, 0:w_], v[:, :, c0:c0 + w_], writes=['wst'])
            for hf in range(nk // 4):
                P.op('pool', lambda e: e.tensor_copy(out=wo_v[hf][0:parts, :, c0:c0 + w_], in_=wst[0:parts, hf * 4:hf * 4 + 4, 0:w_]),
                     reads=['wst'], writes=['wo_bf'])

    def make_hT(layer, xsrc, xkey, off, T, cidx):
        for tt in range(T // 128):
            i = tt % 2
            P.dma(xt[i], xsrc[off + tt * 128: off + (tt + 1) * 128, :], reads=[(xkey, off // 128 + tt)], writes=['xt%d' % i])
            P.op('act', lambda e: e.activation(out=junk, in_=xt[i], func=AF.Square, accum_out=st4[:, 0:1]),
                 reads=['xt%d' % i], writes=['junk', 'st4'])
            P.op('dve', lambda e: e.tensor_scalar(out=st4[:, 1:2], in0=st4[:, 0:1], scalar1=1.0 / D, scalar2=1e-6,
                                                  op0=ALU.mult, op1=ALU.add), reads=['st4'], writes=['st4'])
            P.op('act', lambda e: e.activation(out=st4[:, 2:3], in_=st4[:, 1:2], func=AF.Sqrt), reads=['st4'], writes=['st4'])
            P.op('dve', lambda e: e.reciprocal(out=st4[:, 3:4], in_=st4[:, 2:3]), reads=['st4'], writes=['st4'])
            P.op('dve', lambda e: e.tensor_scalar(out=xn, in0=xt[i], scalar1=st4[:, 3:4], scalar2=None, op0=ALU.mult),
                 reads=['xt%d' % i, 'st4'], writes=['xn'])
            for half in range(2):
                pz, pk = nps()
                for kq in range(4):
                    kc = half * 4 + kq
                    P.op('pe', lambda e: e.transpose(out=pz[:, kq * 128:(kq + 1) * 128], in_=xn[:, kc * 128:(kc + 1) * 128],
                                                     identity=ident[:]), reads=['xn', 'ident'], writes=[pk])
                for kq in range(4):
                    kc = half * 4 + kq
                    P.op('act', lambda e: e.activation(
                        out=hT[:, kc, tt * 128:(tt + 1) * 128], in_=pz[:, kq * 128:(kq + 1) * 128], func=AF.Identity,
                        bias=mT[:, layer, kc, cidx:cidx + 1], scale=sc1[:, layer, kc, cidx:cidx + 1]),
                        reads=[pk, 'mT', 'sc1'], writes=[('hT', tt)])

    def hT_keys(t0, t1):
        return [('hT', tt) for tt in range(t0 // 128, (t1 + 127) // 128)]

    def load_w(src, ncols, dst=None, dkey='wbf', parts=128, nk=8):
        dst = wbf if dst is None else dst
        v = src.rearrange("(kc p) n -> p kc n", p=parts)
        for c0 in range(0, ncols, 256):
            w_ = min(256, ncols - c0)
            P.dma(wst[0:parts, 0:nk, 0:w_], v[:, :, c0:c0 + w_], writes=['wst'])
            P.op('pool', lambda e: e.tensor_copy(out=dst[0:parts, 0:nk, c0:c0 + w_], in_=wst[0:parts, 0:nk, 0:w_]),
                 reads=['wst'], writes=[dkey])

    def proj(c0, M, t0, n, evac):
        pz, pk = nps()
        for kc in range(8):
            P.op('pe', lambda e: e.matmul(pz[0:M, 0:n], wbf[:, kc, c0:c0 + M], hT[:, kc, t0:t0 + n],
                                          start=(kc == 0), stop=(kc == 7)),
                 reads=['wbf'] + hT_keys(t0, t0 + n), writes=[pk])
        evac(pz, pk)

    def outproj(layer, groups, xsrc, skey, xdst, dkey, off, T, cidx, final=False):
        for tt in range(T // 128):
            i = tt % 2
            P.dma(xt[i], xsrc[off + tt * 128: off + (tt + 1) * 128, :], reads=[(skey, off // 128 + tt)], writes=['xt%d' % i])
            for half in range(2):
                pz, pk = nps()
                for gi, (K, slot) in enumerate(groups):
                    P.op('pe', lambda e: e.matmul(pz[:, 0:512], yT[0:K, slot, tt * 128:(tt + 1) * 128],
                                                  wo_v[slot // 4][0:K, slot % 4, half * 512:(half + 1) * 512],
                                                  start=(gi == 0), stop=(gi == len(groups) - 1)),
                         reads=[('yT', slot), 'wo_bf'], writes=[pk])
                P.op('dve', lambda e: e.tensor_tensor(out=xn[:, half * 512:(half + 1) * 512], in0=pz[:, 0:512],
                                                      in1=gate_bc[:, half * 512:(half + 1) * 512], op=ALU.mult),
                     reads=[pk, 'gate_bc'], writes=['xn'])
                P.op('pool', lambda e: e.tensor_tensor(out=xt[i][:, half * 512:(half + 1) * 512],
                                                       in0=xt[i][:, half * 512:(half + 1) * 512],
                                                       in1=xn[:, half * 512:(half + 1) * 512], op=ALU.add),
                     reads=['xn', 'xt%d' % i], writes=['xt%d' % i])
            if final:
                P.op('act', lambda e: e.activation(out=junk, in_=xt[i], func=AF.Square, accum_out=st4[:, 0:1]),
                     reads=['xt%d' % i], writes=['junk', 'st4'])
                P.op('dve', lambda e: e.tensor_scalar(out=st4[:, 1:2], in0=st4[:, 0:1], scalar1=1.0 / D, scalar2=1e-6,
                                                      op0=ALU.mult, op1=ALU.add), reads=['st4'], writes=['st4'])
                P.op('act', lambda e: e.activation(out=st4[:, 2:3], in_=st4[:, 1:2], func=AF.Sqrt), reads=['st4'], writes=['st4'])
                P.op('dve', lambda e: e.reciprocal(out=st4[:, 3:4], in_=st4[:, 2:3]), reads=['st4'], writes=['st4'])
                P.op('dve', lambda e: e.scalar_tensor_tensor(out=xt[i], in0=xt[i], scalar=st4[:, 3:4], in1=fng_holder['t'][:],
                                                             op0=ALU.mult, op1=ALU.mult),
                     reads=['xt%d' % i, 'st4', 'fng_bc'], writes=['xt%d' % i])
            P.dma(xdst[off + tt * 128: off + (tt + 1) * 128, :], xt[i], reads=['xt%d' % i], writes=[(dkey, off // 128 + tt)], q='pool')

    L0 = contextlib.ExitStack()

    def sb0(name, shape, dt=F32):
        return L0.enter_context(nc.sbuf_tensor(name, list(shape), dt))

    wbf = sb0("wbf", [128, 8, 384], BF16)
    TB = 256
    Fq, Fsz, Fvr, For = FT[0][:, 0:TS], FT[1][:, 0:TS], FT[2][:, 0:TS], FT[3][:, 0:TS]
    Fv = Fvr.rearrange("p (j c) -> p j c", c=128)
    Fo = For.rearrange("p (j c) -> p j c", c=128)
    BT = [sb0("BT%d" % i, [128, 256]) for i in range(18)]
    bt = {n_: BT[i] for i, n_ in enumerate(['sg', 'lf', 'kg', 'G', 'br', 'E', 'Ei', 'qt', 'kt', 'kh', 'vT'])}
    khtok = sb0("khtok", [128, TB // 128, 128])
    gam = sb0("gam", [128, TB // 32])
    gref = sb0("gref", [128, TB // 32])
    Sst = [sb0("Sst%d" % i, [128, 128]) for i in range(2)]
    attT = sb0("attT", [128, 128])
    ostat = sb0("ostat", [128, TS // 128, 4])
    mH = sb0("mH", [128, 2, 128])
    P.dma(mH[:], maskH.rearrange("d s t -> s d t"), writes=['mH'])

    def hgrn_head(h, off, T, is_sample, pidx):
        load_w(wA[h][:, 0:384], 384)
        tb = min(TB, T)
        nblk = T // tb
        for b in range(nblk):
            t0 = b * tb
            proj(0, 128, t0, tb, lambda pz, pk: P.op(
                'act', lambda e: e.copy(out=Fq[:, t0:t0 + tb], in_=pz[:, 0:tb]), reads=[pk], writes=['Fq']))
            proj(256, 128, t0, tb, lambda pz, pk: P.op(
                'act', lambda e: e.activation(out=Fsz[:, t0:t0 + tb], in_=pz[:, 0:tb], func=AF.Silu), reads=[pk], writes=['Fsz']))
            proj(128, 128, t0, tb, lambda pz, pk: P.op(
                'dve', lambda e: e.tensor_copy(out=bt['vT'][:, 0:tb], in_=pz[:, 0:tb]), reads=[pk], writes=['b_vT']))
            pz, pk = nps()
            for j in range(tb // 128):
                P.op('pe', lambda e: e.transpose(out=pz[:, j * 128:(j + 1) * 128], in_=bt['vT'][:, j * 128:(j + 1) * 128],
                                                 identity=ident[:]), reads=['b_vT', 'ident'], writes=[pk])
            P.op('dve', lambda e: e.tensor_copy(out=Fv[:, t0 // 128:(t0 + tb) // 128, :],
                                                in_=pz[:, 0:tb].rearrange("p (j c) -> p j c", c=128)),
                 reads=[pk], writes=['Fv'])
        load_w(wA[h][:, 384:640], 256)
        for d in range(2):
            rev = (d == 1)
            cur = 0
            if is_sample:
                P.dma(Sst[0][:], s_hgrn[d, h], writes=['Sst0'])
            else:
                P.op('pool', lambda e: e.memset(Sst[0][:], 0.0), writes=['Sst0'])
            blks = list(range(nblk))
            if rev:
                blks = blks[::-1]
            for b in blks:
                t0 = b * tb
                nch = tb // 32
                sg, lf, kg, G, br, E, Ei, qt, kt, kh = [bt[n_] for n_ in ['sg', 'lf', 'kg', 'G', 'br', 'E', 'Ei', 'qt', 'kt', 'kh']]
                proj(128 * d, 128, t0, tb, lambda pz, pk: P.op(
                    'act', lambda e: e.activation(out=sg[:, 0:tb], in_=pz[:, 0:tb], func=AF.Sigmoid), reads=[pk], writes=['b_sg']))
                P.op('dve', lambda e: e.tensor_scalar(out=sg[:, 0:tb], in0=sg[:, 0:tb], scalar1=oml[:, h:h + 1],
                                                      scalar2=lbv[:, h:h + 1], op0=ALU.mult, op1=ALU.add),
                     reads=['b_sg', 'oml', 'lbv'], writes=['b_sg'])
                P.op('act', lambda e: e.activation(out=lf[:, 0:tb], in_=sg[:, 0:tb], func=AF.Ln), reads=['b_sg'], writes=['b_lf'])
                P.op('pool', lambda e: e.tensor_scalar(out=kg[:, 0:tb], in0=sg[:, 0:tb], scalar1=-1.0, scalar2=1.0,
                                                       op0=ALU.mult, op1=ALU.add), reads=['b_sg'], writes=['b_kg'])
                P.op('dve', lambda e: e.memset(E[:, 0:tb], 0.0), writes=['b_E'])
                if not rev:
                    P.op('dve', lambda e: e.tensor_tensor_scan(out=G[:, 0:tb], data0=lf[:, 0:tb], data1=E[:, 0:tb],
                                                               initial=0.0, op0=ALU.add, op1=ALU.add),
                         reads=['b_lf', 'b_E'], writes=['b_G'])
                    ci_ = 0
                else:
                    P.op('dve', lambda e: e.tensor_tensor_scan(out=G[:, 0:tb][:, ::-1], data0=lf[:, 0:tb][:, ::-1],
                                                               data1=E[:, 0:tb], initial=0.0, op0=ALU.add, op1=ALU.add),
                         reads=['b_lf', 'b_E'], writes=['b_G'])
                    ci_ = 31
                G3 = G[:, 0:tb].rearrange("p (c l) -> p c l", l=32)
                lf3 = lf[:, 0:tb].rearrange("p (c l) -> p c l", l=32)
                P.op('dve', lambda e: e.tensor_tensor(out=gref[:, 0:nch], in0=G3[:, :, ci_], in1=lf3[:, :, ci_], op=ALU.subtract),
                     reads=['b_G', 'b_lf'], writes=['gref'])
                P.op('dve', lambda e: e.tensor_tensor(out=br[:, 0:tb].rearrange("p (c l) -> p c l", l=32), in0=G3,
                                                      in1=gref[:, 0:nch].unsqueeze(2).to_broadcast([128, nch, 32]), op=ALU.subtract),
                     reads=['b_G', 'gref'], writes=['b_br'])
                bend = br[:, 0:tb].rearrange("p (c l) -> p c l", l=32)[:, :, (0 if rev else 31)]
                P.op('act', lambda e: e.activation(out=gam[:, 0:nch], in_=bend, func=AF.Exp), reads=['b_br'], writes=['gam'])
                P.op('act', lambda e: e.activation(out=E[:, 0:tb], in_=br[:, 0:tb], func=AF.Exp), reads=['b_br'], writes=['b_E'])
                P.op('act', lambda e: e.activation(out=Ei[:, 0:tb], in_=br[:, 0:tb], func=AF.Exp, scale=-1.0),
                     reads=['b_br'], writes=['b_Ei'])
                P.op('dve', lambda e: e.tensor_tensor(out=qt[:, 0:tb], in0=Fq[:, t0:t0 + tb], in1=E[:, 0:tb], op=ALU.mult),
                     reads=['Fq', 'b_E'], writes=['b_qt'])
                P.op('pool', lambda e: e.tensor_tensor(out=kt[:, 0:tb], in0=kg[:, 0:tb], in1=Ei[:, 0:tb], op=ALU.mult),
                     reads=['b_kg', 'b_Ei'], writes=['b_kt'])
                P.op('dve', lambda e: e.tensor_tensor(out=kh[:, 0:tb].rearrange("p (c l) -> p c l", l=32),
                                                      in0=kt[:, 0:tb].rearrange("p (c l) -> p c l", l=32),
                                                      in1=gam[:, 0:nch].unsqueeze(2).to_broadcast([128, nch, 32]), op=ALU.mult),
                     reads=['b_kt', 'gam'], writes=['b_kh'])
                if debug and debug.get('inner') and h == head_ids[0]:
                    for n_ in ['lf', 'kg', 'br', 'E', 'qt', 'kt', 'kh']:
                        dump("%s_d%d" % (n_, d), bt[n_][:, 0:tb], ['b_' + n_], tb, col0=off + t0)
                pz, pk = nps()
                for j in range(tb // 128):
                    P.op('pe', lambda e: e.transpose(out=pz[:, j * 128:(j + 1) * 128], in_=kh[:, j * 128:(j + 1) * 128],
                                                     identity=ident[:]), reads=['b_kh', 'ident'], writes=[pk])
                P.op('act', lambda e: e.copy(out=khtok[:, 0:tb // 128, :], in_=pz[:, 0:tb].rearrange("p (j c) -> p j c", c=128)),
                     reads=[pk], writes=['khtok'])
                tiles = list(range(tb // 128))
                if rev:
                    tiles = tiles[::-1]
                for j in tiles:
                    tg = t0 // 128 + j
                    pa, pak = nps()
                    P.op('pe', lambda e: e.matmul(pa[:, 0:128], kt[:, j * 128:(j + 1) * 128], qt[:, j * 128:(j + 1) * 128],
                                                  start=True, stop=True), reads=['b_kt', 'b_qt'], writes=[pak])
                    P.op('dve', lambda e: e.tensor_tensor(out=attT[:], in0=pa[:, 0:128], in1=mH[:, d, :], op=ALU.mult),
                         reads=[pak, 'mH'], writes=['attT'])
                    po, pok = nps()
                    P.op('pe', lambda e: e.matmul(po[:, 0:128], attT[:], Fv[:, tg, :], start=True, stop=False),
                         reads=['attT', 'Fv'], writes=[pok])
                    chs = [0, 1, 2, 3]
                    if rev:
                        chs = chs[::-1]
                    for ci, c in enumerate(chs):
                        Scur = Sst[cur]
                        Snew = Sst[1 - cur]
                        P.op('pe', lambda e: e.matmul(
                            po[32 * c:32 * c + 32, 0:128], qt[:, j * 128 + 32 * c: j * 128 + 32 * c + 32], Scur[:],
                            start=False, stop=(ci == 3), tile_position=(0, 32 * c)),
                            reads=['b_qt', 'Sst%d' % cur], writes=[pok])
                        pd, pdk = nps()
                        P.op('pe', lambda e: e.matmul(
                            pd[:, 0:128], khtok[32 * c:32 * c + 32, j, :], Fv[32 * c:32 * c + 32, tg, :],
                            start=True, stop=True, tile_position=(32 * c, 0)),
                            reads=['khtok', 'Fv'], writes=[pdk])
                        gidx = j * 4 + c
                        P.op('dve', lambda e: e.scalar_tensor_tensor(
                            out=Snew[:], in0=Scur[:], scalar=gam[:, gidx:gidx + 1], in1=pd[:, 0:128],
                            op0=ALU.mult, op1=ALU.add),
                            reads=['Sst%d' % cur, 'gam', pdk], writes=['Sst%d' % (1 - cur)])
                        cur = 1 - cur
                    if d == 0:
                        P.op('act', lambda e: e.copy(out=Fo[:, tg, :], in_=po[:, 0:128]), reads=[pok], writes=[('Fo', tg)])
                    else:
                        P.op('dve', lambda e: e.tensor_tensor(out=Fo[:, tg, :], in0=Fo[:, tg, :], in1=po[:, 0:128], op=ALU.add),
                             reads=[pok, ('Fo', tg)], writes=[('Fo', tg)])
            if not is_sample:
                P.dma(o_hgrn[pidx, d, h], Sst[cur][:], reads=['Sst%d' % cur], q='pool')
        for tg in range(T // 128):
            P.op('act', lambda e: e.activation(out=attT[:], in_=Fo[:, tg, :], func=AF.Square, accum_out=ostat[:, tg, 0:1]),
                 reads=[('Fo', tg)], writes=['attT', ('ostat', tg)])
            P.op('dve', lambda e: e.tensor_scalar(out=ostat[:, tg, 1:2], in0=ostat[:, tg, 0:1], scalar1=1.0 / 128,
                                                  scalar2=1e-6, op0=ALU.mult, op1=ALU.add),
                 reads=[('ostat', tg)], writes=[('ostat', tg)])
            P.op('act', lambda e: e.activation(out=ostat[:, tg, 2:3], in_=ostat[:, tg, 1:2], func=AF.Sqrt),
                 reads=[('ostat', tg)], writes=[('ostat', tg)])
            P.op('dve', lambda e: e.reciprocal(out=ostat[:, tg, 3:4], in_=ostat[:, tg, 2:3]),
                 reads=[('ostat', tg)], writes=[('ostat', tg)])
            P.op('dve', lambda e: e.tensor_scalar(out=Fo[:, tg, :], in0=Fo[:, tg, :], scalar1=ostat[:, tg, 3:4],
                                                  scalar2=None, op0=ALU.mult),
                 reads=[('Fo', tg), ('ostat', tg)], writes=[('Fo', tg)])
        n4 = min(4, T // 128)
        for g4 in range(T // (128 * n4)):
            pz, pk = nps()
            for j in range(n4):
                tg = g4 * n4 + j
                P.op('pe', lambda e: e.transpose(out=pz[:, j * 128:(j + 1) * 128], in_=Fo[:, tg, :], identity=ident[:]),
                     reads=[('Fo', tg), 'ident'], writes=[pk])
            w_ = n4 * 128
            P.op('dve', lambda e: e.scalar_tensor_tensor(
                out=yT[:, h, g4 * w_:(g4 + 1) * w_], in0=pz[:, 0:w_], scalar=hgg_sb[:, h:h + 1],
                in1=Fsz[:, g4 * w_:(g4 + 1) * w_], op0=ALU.mult, op1=ALU.mult),
                reads=[pk, 'hgg', 'Fsz'], writes=[('yT', h)])

    TR = 256
    LWC = -0.6065306597126334
    LR = [sb0("LR%d" % g, [64, TS], BF16) for g in range(4)]
    rb = {n_: BT[i][0:64, :] for i, n_ in enumerate(
          ['lw', 'a', 'kk', 'kq', 'kap', 'kd', 'b', 'rk', 'G', 'br', 'E', 'Ei', 'Em', 'bh', 'kh', 'Kb', 'Bb', 't1'])}
    KR = sb0("r_KR", [64, 2, TR])
    cset = [{n_: sb0("c%d_%s" % (i_, n_), [64, (128 if n_ in ('AB', 'BB') else 64)])
             for n_ in ['AB', 'BB', 'XT0', 'XT1', 'X1', 'Xw', 'Pm0', 'Pm1', 'Vt', 'Kt', 'Bt']} for i_ in range(4)]
    rsq = {n_: sb0("rq_" + n_, [64, 64]) for n_ in ['U', 'Z0', 'Z1', 'zt']}
    rgam = sb0("rgam", [64, 8])
    rgref = sb0("rgref", [64, 4])
    mR = sb0("mR", [64, 2, 3, 128])
    P.dma(mR[:], maskR.rearrange("d m s t -> s d m t"), writes=['mR'])
    prm = {}
    for n_, src_, shp in [('mu_rkv', mu_rkv, [64, 2, 4, 16]), ('mu_lr', mu_lr, [64, 2, 4]), ('w0', w0T, [64, 2, 16]),
                          ('a0', a0T, [64, 2, 16]), ('kk', kkT, [64, 16]), ('ka', kaT, [64, 16]), ('rk', rkT, [64, 16]),
                          ('gng', gngT, [64, 16]), ('gnb', gnbT, [64, 16])]:
        prm[n_] = sb0("p_" + n_, shp)
        P.dma(prm[n_][:], src_[:], writes=['p_' + n_])
    c0_rkv = sb0("c0_rkv", [64, 4, 16])
    c0_lr = sb0("c0_lr", [64, 4])
    omka = sb0("omka", [64, 16])
    P.op('dve', lambda e: e.tensor_tensor(out=c0_rkv[:], in0=prm['mu_rkv'][:, 0], in1=prm['mu_rkv'][:, 1], op=ALU.add),
         reads=['p_mu_rkv'], writes=['c0_rkv'])
    P.op('dve', lambda e: e.tensor_scalar(out=c0_rkv[:], in0=c0_rkv[:], scalar1=-1.0, scalar2=1.0, op0=ALU.mult, op1=ALU.add),
         reads=['c0_rkv'], writes=['c0_rkv'])
    P.op('dve', lambda e: e.tensor_tensor(out=c0_lr[:], in0=prm['mu_lr'][:, 0], in1=prm['mu_lr'][:, 1], op=ALU.add),
         reads=['p_mu_lr'], writes=['c0_lr'])
    P.op('dve', lambda e: e.tensor_scalar(out=c0_lr[:], in0=c0_lr[:], scalar1=-1.0, scalar2=1.0, op0=ALU.mult, op1=ALU.add),
         reads=['c0_lr'], writes=['c0_lr'])
    P.op('dve', lambda e: e.tensor_scalar(out=omka[:], in0=prm['ka'][:], scalar1=-1.0, scalar2=1.0, op0=ALU.mult, op1=ALU.add),
         reads=['p_ka'], writes=['omka'])
    w2a2 = sb0("w2a2", [64, 4, D], BF16)
    for g, src_ in enumerate([w2[0], w2[1], a2[0], a2[1]]):
        for c0 in range(0, D, 256):
            P.dma(wst[0:64, 0, 0:256], src_[:, c0:c0 + 256], writes=['wst'])
            P.op('pool', lambda e: e.tensor_copy(out=w2a2[:, g, c0:c0 + 256], in_=wst[0:64, 0, 0:256]), reads=['wst'], writes=['w2a2'])

    def shift_into(dst, dkey, raw, rkey, T, c0ap, m0ap, m1ap, t1tile, eng='dve'):
        for s0 in range(0, T, 512):
            n = min(512, T - s0)
            P.op(eng, lambda e: e.tensor_scalar(out=t1tile[:, 0:n], in0=raw[:, 16 + s0:16 + s0 + n], scalar1=c0ap, scalar2=None,
                                                op0=ALU.mult), reads=[rkey], writes=['shift_t'])
            P.op('dve', lambda e: e.scalar_tensor_tensor(out=t1tile[:, 0:n], in0=raw[:, 15 + s0:15 + s0 + n], scalar=m0ap,
                                                       in1=t1tile[:, 0:n], op0=ALU.mult, op1=ALU.add),
                 reads=[rkey, 'shift_t'], writes=['shift_t'])
            P.op('dve', lambda e: e.scalar_tensor_tensor(out=dst[:, s0:s0 + n], in0=raw[:, 17 + s0:17 + s0 + n], scalar=m1ap,
                                                       in1=t1tile[:, 0:n], op0=ALU.mult, op1=ALU.add),
                 reads=[rkey, 'shift_t'], writes=[dkey])

    shiftt = sb0("shiftt", [64, 512])

    def rwkv_seq_setup(off, T):
        load_w(wLR, 256)
        pb = min(512, T)
        for g in range(4):
            raw = FT[g]
            P.op('pool', lambda e: e.memset(raw[0:64, 15:16], 0.0), writes=['FT%d' % g])
            P.op('pool', lambda e: e.memset(raw[0:64, T + 16:T + 17], 0.0), writes=['FT%d' % g])
            for b in range(T // pb):
                t0 = b * pb
                proj(64 * g, 64, t0, pb, lambda pz, pk: P.op(
                    'act', lambda e: e.copy(out=raw[0:64, 16 + t0:16 + t0 + pb], in_=pz[0:64, 0:pb]), reads=[pk], writes=['FT%d' % g]))
            shift_into(FT[4][0:64, :], 'FT4', raw[0:64, :], 'FT%d' % g, T, c0_lr[:, g:g + 1], prm['mu_lr'][:, 0, g:g + 1],
                       prm['mu_lr'][:, 1, g:g + 1], shiftt)
            if g < 2:
                P.op('act', lambda e: e.activation(out=LR[g][:, 0:T], in_=FT[4][0:64, 0:T], func=AF.Tanh), reads=['FT4'], writes=['LR%d' % g])
            else:
                P.op('act', lambda e: e.copy(out=LR[g][:, 0:T], in_=FT[4][0:64, 0:T]), reads=['FT4'], writes=['LR%d' % g])

    def rwkv_head(h, slot, off, T, is_sample, pidx):
        P.barrier()
        load_w(wB[h], 256)
        pb = min(512, T)
        nchT = T // 64
        for g in range(3):
            raw = FT[g]
            P.op('pool', lambda e: e.memset(raw[0:64, 15:16], 0.0), writes=['FT%d' % g])
            P.op('pool', lambda e: e.memset(raw[0:64, T + 16:T + 17], 0.0), writes=['FT%d' % g])
            for b in range(T // pb):
                t0 = b * pb
                proj(64 * g, 64, t0, pb, lambda pz, pk: P.op(
                    'act', lambda e: e.copy(out=raw[0:64, 16 + t0:16 + t0 + pb], in_=pz[0:64, 0:pb]), reads=[pk], writes=['FT%d' % g]))
            shift_into(FT[3 + g][0:64, :], 'FT%d' % (3 + g), raw[0:64, :], 'FT%d' % g, T, c0_rkv[:, g, h:h + 1],
                       prm['mu_rkv'][:, 0, g, h:h + 1], prm['mu_rkv'][:, 1, g, h:h + 1], shiftt, eng=('dve' if g != 1 else 'pool'))
        rS, kS, vS = FT[3][0:64, :], FT[4][0:64, :], FT[5][0:64, :]
        szb, yaccr, bonus = FT[0][0:64, :], FT[1][0:64, 0:T], FT[2][0:64, :]
        yacc = yaccr.rearrange("p (c v) -> p c v", v=64)
        for b in range(T // pb):
            t0 = b * pb
            proj(192, 64, t0, pb, lambda pz, pk: P.op(
                'act', lambda e: e.activation(out=szb[:, t0:t0 + pb], in_=pz[0:64, 0:pb], func=AF.Silu), reads=[pk], writes=['FT0']))
        tb = min(TR, T)
        nblk = T // tb
        P.barrier()
        for d in range(2):
            rev = (d == 1)
            cur = 0
            Zt = [rsq['Z0'], rsq['Z1']]
            if is_sample:
                P.dma(rsq['zt'][:], s_rwkv[d, h], writes=['rq_zt'])
                pz, pk = nps()
                P.op('pe', lambda e: e.transpose(out=pz[0:64, 0:64], in_=rsq['zt'][:], identity=ident[0:64, 0:64]),
                     reads=['rq_zt', 'ident'], writes=[pk])
                P.op('act', lambda e: e.copy(out=Zt[0][:], in_=pz[0:64, 0:64]), reads=[pk], writes=['rq_Z0'])
            else:
                P.op('pool', lambda e: e.memset(Zt[0][:], 0.0), writes=['rq_Z0'])
            blks = list(range(nblk))
            if rev:
                blks = blks[::-1]
            for b in blks:
                t0 = b * tb
                sl = slice(t0, t0 + tb)
                nch = tb // 64
                R_ = rb
                pz, pk = nps()
                P.op('pe', lambda e: e.matmul(pz[0:64, 0:tb], w2a2[:, d, h * 64:(h + 1) * 64], LR[d][:, sl], start=True, stop=True),
                     reads=['w2a2', 'LR%d' % d], writes=[pk])
                P.op('act', lambda e: e.activation(out=R_['lw'][:, 0:tb], in_=pz[0:64, 0:tb], func=AF.Sigmoid,
                                                   bias=prm['w0'][:, d, h:h + 1], scale=1.0), reads=[pk, 'p_w0'], writes=['r_lw'])
                pz, pk = nps()
                P.op('pe', lambda e: e.matmul(pz[0:64, 0:tb], w2a2[:, 2 + d, h * 64:(h + 1) * 64], LR[2 + d][:, sl], start=True, stop=True),
                     reads=['w2a2', 'LR%d' % (2 + d)], writes=[pk])
                P.op('act', lambda e: e.activation(out=R_['a'][:, 0:tb], in_=pz[0:64, 0:tb], func=AF.Sigmoid,
                                                   bias=prm['a0'][:, d, h:h + 1], scale=1.0), reads=[pk, 'p_a0'], writes=['r_a'])
                P.op('dve', lambda e: e.tensor_scalar(out=R_['kk'][:, 0:tb], in0=kS[:, sl], scalar1=prm['kk'][:, h:h + 1],
                                                      scalar2=None, op0=ALU.mult), reads=['FT4', 'p_kk'], writes=['r_kk'])
                P.op('dve', lambda e: e.tensor_tensor(out=R_['kq'][:, 0:tb], in0=R_['kk'][:, 0:tb], in1=R_['kk'][:, 0:tb], op=ALU.mult),
                     reads=['r_kk'], writes=['r_kq'])
                pz, pk = nps()
                P.op('pe', lambda e: e.matmul(pz[0:64, 0:tb], ones[0:64, 0:64], R_['kq'][:, 0:tb], start=True, stop=True),
                     reads=['ones', 'r_kq'], writes=[pk])
                P.op('act', lambda e: e.activation(out=R_['kq'][:, 0:tb], in_=pz[0:64, 0:tb], func=AF.Sqrt), reads=[pk], writes=['r_kq'])
                P.op('dve', lambda e: e.tensor_scalar(out=R_['kq'][:, 0:tb], in0=R_['kq'][:, 0:tb], scalar1=1e-12, scalar2=None,
                                                      op0=ALU.max), reads=['r_kq'], writes=['r_kq'])
                P.op('dve', lambda e: e.reciprocal(out=R_['kq'][:, 0:tb], in_=R_['kq'][:, 0:tb]), reads=['r_kq'], writes=['r_kq'])
                P.op('dve', lambda e: e.tensor_tensor(out=R_['kap'][:, 0:tb], in0=R_['kk'][:, 0:tb], in1=R_['kq'][:, 0:tb], op=ALU.mult),
                     reads=['r_kk', 'r_kq'], writes=['r_kap'])
                P.op('pool', lambda e: e.tensor_scalar(out=R_['t1'][:, 0:tb], in0=R_['a'][:, 0:tb], scalar1=prm['ka'][:, h:h + 1],
                                                       scalar2=omka[:, h:h + 1], op0=ALU.mult, op1=ALU.add),
                     reads=['r_a', 'p_ka', 'omka'], writes=['r_t1'])
                P.op('pool', lambda e: e.tensor_tensor(out=R_['kd'][:, 0:tb], in0=kS[:, sl], in1=R_['t1'][:, 0:tb], op=ALU.mult),
                     reads=['FT4', 'r_t1'], writes=['r_kd'])
                P.op('dve', lambda e: e.tensor_tensor(out=R_['b'][:, 0:tb], in0=R_['a'][:, 0:tb], in1=R_['kap'][:, 0:tb], op=ALU.mult),
                     reads=['r_a', 'r_kap'], writes=['r_b'])
                P.op('dve', lambda e: e.scalar_tensor_tensor(out=R_['rk'][:, 0:tb], in0=rS[:, sl], scalar=prm['rk'][:, h:h + 1],
                                                             in1=R_['kd'][:, 0:tb], op0=ALU.mult, op1=ALU.mult),
                     reads=['FT3', 'p_rk', 'r_kd'], writes=['r_rk'])
                pz, pk = nps()
                P.op('pe', lambda e: e.matmul(pz[0:64, 0:tb], ones[0:64, 0:64], R_['rk'][:, 0:tb], start=True, stop=True),
                     reads=['ones', 'r_rk'], writes=[pk])
                if d == 0:
                    P.op('dve', lambda e: e.tensor_tensor(out=bonus[:, sl], in0=pz[0:64, 0:tb], in1=vS[:, sl], op=ALU.mult),
                         reads=[pk, 'FT5'], writes=[('bonus', b)])
                else:
                    P.op('dve', lambda e: e.tensor_tensor(out=R_['rk'][:, 0:tb], in0=pz[0:64, 0:tb], in1=vS[:, sl], op=ALU.mult),
                         reads=[pk, 'FT5'], writes=['r_rk'])
                    P.op('pool', lambda e: e.tensor_tensor(out=bonus[:, sl], in0=bonus[:, sl], in1=R_['rk'][:, 0:tb], op=ALU.add),
                         reads=['r_rk', ('bonus', b)], writes=[('bonus', b)])
                G, br, E, Ei, Em = R_['G'], R_['br'], R_['E'], R_['Ei'], R_['Em']
                P.op('dve', lambda e: e.memset(E[:, 0:tb], 0.0), writes=['r_E'])
                if not rev:
                    P.op('dve', lambda e: e.tensor_tensor_scan(out=G[:, 0:tb], data0=R_['lw'][:, 0:tb], data1=E[:, 0:tb],
                                                               initial=0.0, op0=ALU.add, op1=ALU.add),
                         reads=['r_lw', 'r_E'], writes=['r_G'])
                    ci_ = 0
                else:
                    P.op('dve', lambda e: e.tensor_tensor_scan(out=G[:, 0:tb][:, ::-1], data0=R_['lw'][:, 0:tb][:, ::-1],
                                                               data1=E[:, 0:tb], initial=0.0, op0=ALU.add, op1=ALU.add),
                         reads=['r_lw', 'r_E'], writes=['r_G'])
                    ci_ = 63
                G3 = G[:, 0:tb].rearrange("p (c l) -> p c l", l=64)
                lw3 = R_['lw'][:, 0:tb].rearrange("p (c l) -> p c l", l=64)
                P.op('dve', lambda e: e.tensor_tensor(out=rgref[:, 0:nch], in0=G3[:, :, ci_], in1=lw3[:, :, ci_], op=ALU.subtract),
                     reads=['r_G', 'r_lw'], writes=['rgref'])
                P.op('dve', lambda e: e.tensor_tensor(out=br[:, 0:tb].rearrange("p (c l) -> p c l", l=64), in0=G3,
                                                      in1=rgref[:, 0:nch].unsqueeze(2).to_broadcast([64, nch, 64]), op=ALU.subtract),
                     reads=['r_G', 'rgref'], writes=['r_br'])
                bend = br[:, 0:tb].rearrange("p (c l) -> p c l", l=64)[:, :, (0 if rev else 63)]
                P.op('act', lambda e: e.activation(out=rgam[:, 0:nch], in_=bend, func=AF.Exp, scale=LWC), reads=['r_br'], writes=['rgam'])
                P.op('pool', lambda e: e.tensor_scalar(out=rgam[:, 4:4 + nch], in0=rgam[:, 0:nch], scalar1=-1.0, scalar2=None,
                                                       op0=ALU.mult), reads=['rgam'], writes=['rgam'])
                P.op('act', lambda e: e.activation(out=E[:, 0:tb], in_=br[:, 0:tb], func=AF.Exp, scale=LWC), reads=['r_br'], writes=['r_E'])
                P.op('act', lambda e: e.activation(out=Ei[:, 0:tb], in_=br[:, 0:tb], func=AF.Exp, scale=-LWC),
                     reads=['r_br'], writes=['r_Ei'])
                P.op('pool', lambda e: e.tensor_tensor(out=R_['t1'][:, 0:tb], in0=br[:, 0:tb], in1=R_['lw'][:, 0:tb], op=ALU.subtract),
                     reads=['r_br', 'r_lw'], writes=['r_t1'])
                P.op('act', lambda e: e.activation(out=Em[:, 0:tb], in_=R_['t1'][:, 0:tb], func=AF.Exp, scale=LWC), reads=['r_t1'], writes=['r_Em'])
                P.op('dve', lambda e: e.tensor_tensor(out=KR[:, 0, 0:tb], in0=R_['kap'][:, 0:tb], in1=Em[:, 0:tb], op=ALU.mult),
                     reads=['r_kap', 'r_Em'], writes=['r_KR'])
                P.op('pool', lambda e: e.tensor_tensor(out=KR[:, 1, 0:tb], in0=rS[:, sl], in1=E[:, 0:tb], op=ALU.mult),
                     reads=['FT3', 'r_E', 'r_KR'], writes=['r_KR'])
                P.op('dve', lambda e: e.tensor_tensor(out=R_['bh'][:, 0:tb], in0=R_['b'][:, 0:tb], in1=Ei[:, 0:tb], op=ALU.mult),
                     reads=['r_b', 'r_Ei'], writes=['r_bh'])
                P.op('pool', lambda e: e.tensor_tensor(out=R_['kh'][:, 0:tb], in0=R_['kd'][:, 0:tb], in1=Ei[:, 0:tb], op=ALU.mult),
                     reads=['r_kd', 'r_Ei'], writes=['r_kh'])
                P.op('dve', lambda e: e.tensor_tensor(out=R_['Kb'][:, 0:tb].rearrange("p (c l) -> p c l", l=64),
                                                      in0=R_['kh'][:, 0:tb].rearrange("p (c l) -> p c l", l=64),
                                                      in1=rgam[:, 0:nch].unsqueeze(2).to_broadcast([64, nch, 64]), op=ALU.mult),
                     reads=['r_kh', 'rgam'], writes=['r_Kb'])
                P.op('dve', lambda e: e.tensor_tensor(out=R_['Bb'][:, 0:tb].rearrange("p (c l) -> p c l", l=64),
                                                      in0=R_['bh'][:, 0:tb].rearrange("p (c l) -> p c l", l=64),
                                                      in1=rgam[:, 4:4 + nch].unsqueeze(2).to_broadcast([64, nch, 64]), op=ALU.mult),
                     reads=['r_bh', 'rgam'], writes=['r_Bb'])
                chs = list(range(nch))
                if rev:
                    chs = chs[::-1]
                st = {}
                for c in chs:
                    cs = slice(c * 64, (c + 1) * 64)
                    C_ = cset[c]
                    ck = (lambda n_, c=c: 'c%d_%s' % (c, n_))
                    pA, pAk = nps()
                    P.op('pe', lambda e: e.matmul(pA[0:64, 0:128], R_['bh'][:, cs], KR[:, :, cs], start=True, stop=True),
                         reads=['r_bh', 'r_KR'], writes=[pAk])
                    P.op('pe', lambda e: e.matmul(pA[0:64, 128:256], R_['kh'][:, cs], KR[:, :, cs], start=True, stop=True),
                         reads=['r_kh', 'r_KR'], writes=[pAk])
                    P.op('pe', lambda e: e.matmul(pA[0:64, 256:320], KR[:, 0, cs], R_['bh'][:, cs], start=True, stop=True),
                         reads=['r_bh', 'r_KR'], writes=[pAk])
                    AB, BB = C_['AB'], C_['BB']
                    P.op('dve', lambda e: e.tensor_tensor(out=AB[:], in0=pA[0:64, 0:128], in1=mR[:, d, 0, :], op=ALU.mult),
                         reads=[pAk, 'mR'], writes=[ck('AB')])
                    P.op('dve', lambda e: e.tensor_tensor(out=BB[:], in0=pA[0:64, 128:256], in1=mR[:, d, 1, :], op=ALU.mult),
                         reads=[pAk, 'mR'], writes=[ck('BB')])
                    P.op('dve', lambda e: e.tensor_tensor(out=C_['XT0'][:], in0=pA[0:64, 256:320], in1=mR[:, d, 2, 0:64], op=ALU.mult),
                         reads=[pAk, 'mR'], writes=[ck('XT0')])
                    P.op('dve', lambda e: e.tensor_tensor(out=C_['Pm0'][:], in0=AB[:, 0:64], in1=ident[0:64, 0:64], op=ALU.add),
                         reads=[ck('AB'), 'ident'], writes=[ck('Pm0')])
                    st[c] = dict(X=AB[:, 0:64], Xk=ck('AB'), XT=C_['XT0'], XTk=ck('XT0'), xti=0, pmi=0)
                for lev in range(5):
                    for c in chs:
                        C_ = cset[c]
                        s_ = st[c]
                        ck = (lambda n_, c=c: 'c%d_%s' % (c, n_))
                        X, Xk, XT, XTk = s_['X'], s_['Xk'], s_['XT'], s_['XTk']
                        pq, pqk = nps()
                        nXTn = 'XT1' if s_['xti'] == 0 else 'XT0'
                        nXT, nXTk = C_[nXTn], ck(nXTn)
                        P.op('pe', lambda e: e.matmul(pq[0:64, 64:128], X, XT[:], start=True, stop=True), reads=[Xk, XTk], writes=[pqk])
                        if lev < 4:
                            P.op('pe', lambda e: e.matmul(pq[0:64, 0:64], XT[:], X, start=True, stop=True), reads=[Xk, XTk], writes=[pqk])
                        P.op('act', lambda e: e.copy(out=nXT[:], in_=pq[0:64, 64:128]), reads=[pqk], writes=[nXTk])
                        if lev < 4:
                            tn = 'X1' if lev % 2 == 0 else 'Xw'
                            P.op('act', lambda e: e.copy(out=C_[tn][:], in_=pq[0:64, 0:64]), reads=[pqk], writes=[ck(tn)])
                            s_['X'], s_['Xk'] = C_[tn][:], ck(tn)
                        s_['XT'], s_['XTk'], s_['xti'] = nXT, nXTk, 1 - s_['xti']
                    for c in chs:
                        C_ = cset[c]
                        s_ = st[c]
                        ck = (lambda n_, c=c: 'c%d_%s' % (c, n_))
                        nXT, nXTk = s_['XT'], s_['XTk']
                        pmi = s_['pmi']
                        Pc, Pn = C_['Pm%d' % pmi], C_['Pm%d' % (1 - pmi)]
                        pp, ppk = nps()
                        P.op('pe', lambda e: e.matmul(pp[0:64, 0:64], nXT[:], Pc[:], start=True, stop=True),
                             reads=[nXTk, ck('Pm%d' % pmi)], writes=[ppk])
                        P.op('dve', lambda e: e.tensor_tensor(out=Pn[:], in0=pp[0:64, 0:64], in1=Pc[:], op=ALU.add),
                             reads=[ppk, ck('Pm%d' % pmi)], writes=[ck('Pm%d' % (1 - pmi))])
                        s_['pmi'] = 1 - pmi
                for c in chs:
                    cs = slice(c * 64, (c + 1) * 64)
                    gsl = slice(t0 + c * 64, t0 + (c + 1) * 64)
                    C_ = cset[c]
                    pt, ptk = nps()
                    P.op('pe', lambda e: e.transpose(out=pt[0:64, 0:64], in_=vS[:, gsl], identity=ident[0:64, 0:64]),
                         reads=['FT5', 'ident'], writes=[ptk])
                    P.op('pe', lambda e: e.transpose(out=pt[0:64, 64:128], in_=R_['Kb'][:, cs], identity=ident[0:64, 0:64]),
                         reads=['r_Kb', 'ident'], writes=[ptk])
                    P.op('pe', lambda e: e.transpose(out=pt[0:64, 128:192], in_=R_['Bb'][:, cs], identity=ident[0:64, 0:64]),
                         reads=['r_Bb', 'ident'], writes=[ptk])
                    P.op('act', lambda e: e.copy(out=C_['Vt'][:], in_=pt[0:64, 0:64]), reads=[ptk], writes=['c%d_Vt' % c])
                    P.op('act', lambda e: e.copy(out=C_['Kt'][:], in_=pt[0:64, 64:128]), reads=[ptk], writes=['c%d_Kt' % c])
                    P.op('act', lambda e: e.copy(out=C_['Bt'][:], in_=pt[0:64, 128:192]), reads=[ptk], writes=['c%d_Bt' % c])
                for c in chs:
                    cs = slice(c * 64, (c + 1) * 64)
                    cg = (t0 // 64) + c
                    C_ = cset[c]
                    s_ = st[c]
                    AB, BB = C_['AB'], C_['BB']
                    ABk, BBk, Vtk, Ktk, Btk = ['c%d_%s' % (c, n_) for n_ in ('AB', 'BB', 'Vt', 'Kt', 'Bt')]
                    Pm, Pmk = C_['Pm%d' % s_['pmi']], 'c%d_Pm%d' % (c, s_['pmi'])
                    Zc, Zn = Zt[cur], Zt[1 - cur]
                    zck, znk = 'rq_Z%d' % cur, 'rq_Z%d' % (1 - cur)
                    pw, pwk = nps()
                    P.op('pe', lambda e: e.matmul(pw[0:64, 0:64], KR[:, 0, cs], Zc[:], start=True, stop=False),
                         reads=['r_KR', zck], writes=[pwk])
                    P.op('pe', lambda e: e.matmul(pw[0:64, 0:64], BB[:, 0:64], C_['Vt'][:], start=False, stop=True),
                         reads=[BBk, Vtk], writes=[pwk])
                    P.op('act', lambda e: e.copy(out=rsq['zt'][:], in_=pw[0:64, 0:64]), reads=[pwk], writes=['rq_zt'])
                    pu, puk = nps()
                    P.op('pe', lambda e: e.matmul(pu[0:64, 0:64], Pm[:], rsq['zt'][:], start=True, stop=True),
                         reads=[Pmk, 'rq_zt'], writes=[puk])
                    P.op('act', lambda e: e.copy(out=rsq['U'][:], in_=pu[0:64, 0:64]), reads=[puk], writes=['rq_U'])
                    pzz, pzk = nps()
                    P.op('pe', lambda e: e.matmul(pzz[0:64, 0:64], C_['Kt'][:], C_['Vt'][:], start=True, stop=False),
                         reads=[Ktk, Vtk], writes=[pzk])
                    P.op('pe', lambda e: e.matmul(pzz[0:64, 0:64], C_['Bt'][:], rsq['U'][:], start=False, stop=True),
                         reads=[Btk, 'rq_U'], writes=[pzk])
                    P.op('dve', lambda e: e.scalar_tensor_tensor(out=Zn[:], in0=Zc[:], scalar=rgam[:, c:c + 1], in1=pzz[0:64, 0:64],
                                                                 op0=ALU.mult, op1=ALU.add), reads=[zck, 'rgam', pzk], writes=[znk])
                    py, pyk = nps()
                    P.op('pe', lambda e: e.matmul(py[0:64, 0:64], KR[:, 1, cs], Zc[:], start=True, stop=False),
                         reads=['r_KR', zck], writes=[pyk])
                    P.op('pe', lambda e: e.matmul(py[0:64, 0:64], BB[:, 64:128], C_['Vt'][:], start=False, stop=False),
                         reads=[BBk, Vtk], writes=[pyk])
                    P.op('pe', lambda e: e.matmul(py[0:64, 0:64], AB[:, 64:128], rsq['U'][:], start=False, stop=True),
                         reads=[ABk, 'rq_U'], writes=[pyk])
                    if d == 0:
                        P.op('act', lambda e: e.copy(out=yacc[:, cg, :], in_=py[0:64, 0:64]), reads=[pyk], writes=[('yacc', cg)])
                    else:
                        P.op('dve', lambda e: e.tensor_tensor(out=yacc[:, cg, :], in0=yacc[:, cg, :], in1=py[0:64, 0:64], op=ALU.add),
                             reads=[pyk, ('yacc', cg)], writes=[('yacc', cg)])
                    cur = 1 - cur
            if not is_sample:
                pz, pk = nps()
                P.op('pe', lambda e: e.transpose(out=pz[0:64, 0:64], in_=Zt[cur][:], identity=ident[0:64, 0:64]),
                     reads=['rq_Z%d' % cur, 'ident'], writes=[pk])
                P.op('act', lambda e: e.copy(out=rsq['zt'][:], in_=pz[0:64, 0:64]), reads=[pk], writes=['rq_zt'])
                P.dma(o_rwkv[pidx, d, h], rsq['zt'][:], reads=['rq_zt'], q='pool')
        ykeys = [('yacc', c) for c in range(nchT)]
        gst = ostat[0:64, :, :].rearrange("p a b -> p (a b)")
        P.op('dve', lambda e: e.tensor_reduce(out=gst[:, 0:nchT], in_=yacc, axis=AX.X, op=ALU.add), reads=ykeys, writes=['gst'])
        P.op('dve', lambda e: e.tensor_scalar(out=gst[:, 0:nchT], in0=gst[:, 0:nchT], scalar1=-1.0 / 64, scalar2=None, op0=ALU.mult),
             reads=['gst'], writes=['gst'])
        P.op('dve', lambda e: e.tensor_tensor(out=yacc, in0=yacc, in1=gst[:, 0:nchT].unsqueeze(2).to_broadcast([64, nchT, 64]), op=ALU.add),
             reads=ykeys + ['gst'], writes=ykeys)
        sq = FT[3][0:64, 0:T].rearrange("p (c v) -> p c v", v=64)
        P.op('pool', lambda e: e.tensor_tensor(out=sq, in0=yacc, in1=yacc, op=ALU.mult), reads=ykeys, writes=['FT3'])
        P.op('dve', lambda e: e.tensor_reduce(out=gst[:, 32:32 + nchT], in_=sq, axis=AX.X, op=ALU.add), reads=['FT3'], writes=['gst'])
        P.op('dve', lambda e: e.tensor_scalar(out=gst[:, 32:32 + nchT], in0=gst[:, 32:32 + nchT], scalar1=1.0 / 64, scalar2=64e-5,
                                              op0=ALU.mult, op1=ALU.add), reads=['gst'], writes=['gst'])
        P.op('act', lambda e: e.activation(out=gst[:, 32:32 + nchT], in_=gst[:, 32:32 + nchT], func=AF.Sqrt), reads=['gst'], writes=['gst'])
        P.op('dve', lambda e: e.reciprocal(out=gst[:, 32:32 + nchT], in_=gst[:, 32:32 + nchT]), reads=['gst'], writes=['gst'])
        P.op('dve', lambda e: e.tensor_tensor(out=yacc, in0=yacc, in1=gst[:, 32:32 + nchT].unsqueeze(2).to_broadcast([64, nchT, 64]),
                                              op=ALU.mult), reads=ykeys + ['gst'], writes=ykeys)
        n8 = min(8, nchT)
        for g8 in range(nchT // n8):
            pz, pk = nps()
            for j in range(n8):
                cg = g8 * n8 + j
                P.op('pe', lambda e: e.transpose(out=pz[0:64, j * 64:(j + 1) * 64], in_=yacc[:, cg, :], identity=ident[0:64, 0:64]),
                     reads=[('yacc', cg), 'ident'], writes=[pk])
            w_ = n8 * 64
            gs = slice(g8 * w_, (g8 + 1) * w_)
            P.op('dve', lambda e: e.tensor_scalar(out=shiftt[:, 0:w_], in0=pz[0:64, 0:w_], scalar1=prm['gng'][:, h:h + 1],
                                                  scalar2=prm['gnb'][:, h:h + 1], op0=ALU.mult, op1=ALU.add),
                 reads=[pk, 'p_gng', 'p_gnb'], writes=['shift_t'])
            P.op('pool', lambda e: e.tensor_tensor(out=shiftt[:, 0:w_], in0=shiftt[:, 0:w_], in1=bonus[:, gs], op=ALU.add),
                 reads=['shift_t'] + [('bonus', b) for b in range(nblk)], writes=['shift_t'])
            P.op('dve', lambda e: e.tensor_tensor(out=yT[0:64, slot, gs], in0=shiftt[:, 0:w_], in1=szb[:, gs], op=ALU.mult),
                 reads=['shift_t', 'FT0'], writes=[('yT', slot)])

    seq_ids = debug.get('seqs', [0, 1, 2]) if debug else [0, 1, 2]
    head_ids = debug.get('heads', list(range(8))) if debug else list(range(8))
    rheads = debug.get('rheads', list(range(16))) if debug else list(range(16))
    for si in seq_ids:
        off, T, cidx, is_sample = SEQS[si]
        P.barrier()
        make_gate(0, cidx)
        make_hT(0, xin, 'xin', off, T, cidx)
        P.barrier()
        if debug and debug.get('inner'):
            dump("mT", mT[:, 0].rearrange("p a b -> p (a b)"), ['mT'], 48)
            dump("sc1", sc1[:, 0].rearrange("p a b -> p (a b)"), ['sc1'], 16)
            dump("scT", scT[:].rearrange("p a b -> p (a b)"), ['scT'], 16)
            dump("xn", xn, ['xn'], 1024)
            dump("xt0", xt[0], ['xt0'], 1024)
            for kc in range(8):
                dump("hT%d" % kc, hT[:, kc, 0:T], hT_keys(0, T), T, col0=off)
        for h in head_ids:
            hgrn_head(h, off, T, is_sample, si - 1)
            if debug and debug.get('dump_y'):
                dump("yT%d" % h, yT[:, h, 0:T], [('yT', h)], T, col0=off)
        P.barrier()
        load_wo(w_out_even[0:D, :], 128)
        outproj(0, [(128, s_) for s_ in range(8)], xin, 'xin', x1, 'x1', off, T, cidx)
        P.barrier()
        rwkv_seq_setup(off, T)
        for half in range(2):
            P.barrier()
            for slot in range(8):
                h = half * 8 + slot
                if h in rheads:
                    rwkv_head(h, slot, off, T, is_sample, si - 1)
                    if debug and debug.get('dump_y'):
                        dump("yR%d" % h, yT[0:64, slot, 0:T], [('yT', slot)], T, col0=off, parts=64)
            P.barrier()
            load_wo(w_out_even[D + half * 512: D + (half + 1) * 512, :], 64)
            outproj(0, [(64, s_) for s_ in range(8)], x1, 'x1', x1, 'x1', off, T, cidx)
    P.barrier()
    L0.close()

    if not (debug and debug.get('l0only')):
        L1 = contextlib.ExitStack()

        def sb1(name, shape, dt=F32):
            return L1.enter_context(nc.sbuf_tensor(name, list(shape), dt))

        make_fng()
        LC = 128
        DH = 512
        qT = yT[:, 4:8, :]
        kT = sb1("kT", [128, 4, TS], BF16)
        vch = sb1("vch", [128, DH], BF16)
        Cst = sb1("Cst", [128, 4, DH])
        Cbf = sb1("Cbf", [128, 4, DH], BF16)
        nst = sb1("nst", [128, 8])
        nbf = sb1("nbf", [128, 4], BF16)
        ktok = sb1("ktok", [128, DH], BF16)
        vw = sb1("vw", [128, DH], BF16)
        sTs = sb1("sTs", [128, 128], BF16)
        onesb = sb1("onesb", [128, 1], BF16)
        identb = sb1("identb", [128, 128], BF16)
        mC = sb1("mC", [128, 2, 128])
        SEL = sb1("SEL", [36, 4, 128])
        XA = sb1("XA", [36, TS])
        XB = sb1("XB", [36, TS])
        zrow = sb1("zrow", [36, 512])
        sm = {n_: sb1("sm_" + n_, [36, 16]) for n_ in ['ac', 'bl', 'M', 'MP', 'mu', 'al', 'gref', 'm0']}
        Wtok = sb1("Wtok", [128, 2, 16, 8])
        Wtokb = sb1("Wtokb", [128, 2, 16, 4], BF16)
        ALb = sb1("ALb", [128, 2, 4, 16])
        dstat = sb1("dstat", [128, 8])
        wGb = sb1("wGb", [128, 8, 16], BF16)
        gbT = sb1("gbT", [36, 4])
        ngbT = sb1("ngbT", [36, 4])
        cw = sb1("cw", [128, 32, 9])
        cb = sb1("cb", [128, 32])
        mng = sb1("mng", [128, 16])
        wbf1 = sb1("wbf1", [128, 8, DH], BF16)
        P.dma(mC[:], maskC.rearrange("d s t -> s d t"), writes=['mC'])
        P.dma(SEL[:], sel_d[:], writes=['SEL'])
        P.dma(gbT[:], gbT_d[:], writes=['gbT'])
        P.dma(cw[:], cw_d[:], writes=['cw'])
        P.dma(cb[:], cb_d[:], writes=['cb'])
        P.dma(mng[:], mng_d[:], writes=['mng'])
        P.op('dve', lambda e: e.memset(onesb[:], 1.0), writes=['onesb'])
        P.op('dve', lambda e: e.memset(zrow[:], 0.0), writes=['zrow'])
        P.op('dve', lambda e: e.tensor_copy(out=identb[:], in_=ident[:]), reads=['ident'], writes=['identb'])
        P.op('dve', lambda e: e.tensor_scalar(out=ngbT[:], in0=gbT[:], scalar1=-1.0, scalar2=None, op0=ALU.mult), reads=['gbT'], writes=['ngbT'])
        P.dma(wst[:, :, 0:16], w_in_odd[:, 10240:10256].rearrange("(kc p) n -> p kc n", p=128), writes=['wst'])
        P.op('pool', lambda e: e.tensor_copy(out=wGb[:], in_=wst[:, :, 0:16]), reads=['wst'], writes=['wGb'])
        LNK = float(np.log(DH ** -0.5))

        def load_w1(c0, ncols):
            v = w_in_odd[:, c0:c0 + ncols].rearrange("(kc p) n -> p kc n", p=128)
            for q0 in range(0, ncols, 256):
                w_ = min(256, ncols - q0)
                P.dma(wst[:, :, 0:w_], v[:, :, q0:q0 + w_], writes=['wst'])
                P.op('pool', lambda e: e.tensor_copy(out=wbf1[:, :, q0:q0 + w_], in_=wst[:, :, 0:w_]), reads=['wst'], writes=['wbf1'])

        def gates_seq(T, is_sample):
            NC = T // LC
            pbk = min(512, T)
            for d in range(2):
                pb = 32 * d
                rows = slice(pb, pb + 4)
                for b in range(T // pbk):
                    t0 = b * pbk
                    pz, pk = nps()
                    for kc in range(8):
                        P.op('pe', lambda e: e.matmul(pz[pb:pb + 4, 0:pbk], wGb[:, kc, (2 + d) * 4:(3 + d) * 4], hT[:, kc, t0:t0 + pbk],
                                                      start=(kc == 0), stop=(kc == 7)), reads=['wGb'] + hT_keys(t0, t0 + pbk), writes=[pk])
                    P.op('act', lambda e: e.activation(out=XA[rows, t0:t0 + pbk], in_=pz[pb:pb + 4, 0:pbk], func=AF.Exp,
                                                       bias=ngbT[rows, 2 + d:3 + d], scale=-1.0), reads=[pk, 'ngbT'], writes=['XA'])
                P.op('act', lambda e: e.activation(out=XA[rows, 0:T], in_=XA[rows, 0:T], func=AF.Ln, bias=1.0, scale=1.0),
                     reads=['XA'], writes=['XA'])
                for b in range(T // pbk):
                    bs = slice(b * pbk, (b + 1) * pbk)
                    if d == 0:
                        P.op('dve', lambda e: e.tensor_tensor_scan(out=XB[rows, bs], data0=XA[rows, bs], data1=zrow[rows, 0:pbk],
                                                                   initial=0.0, op0=ALU.add, op1=ALU.add), reads=['XA', 'zrow'], writes=['XB'])
                    else:
                        P.op('dve', lambda e: e.tensor_tensor_scan(out=XB[rows, bs][:, ::-1], data0=XA[rows, bs][:, ::-1],
                                                                   data1=zrow[rows, 0:pbk], initial=0.0, op0=ALU.add, op1=ALU.add),
                             reads=['XA', 'zrow'], writes=['XB'])
                if d == 0:
                    ci_, ce_ = 0, LC - 1
                else:
                    ci_, ce_ = LC - 1, 0
                B3 = XB[rows, 0:T].rearrange("p (c l) -> p c l", l=LC)
                A3 = XA[rows, 0:T].rearrange("p (c l) -> p c l", l=LC)
                S = {k_: v_[rows, :] for k_, v_ in sm.items()}
                P.op('dve', lambda e: e.tensor_tensor(out=S['gref'][:, 0:NC], in0=B3[:, :, ci_], in1=A3[:, :, ci_], op=ALU.subtract),
                     reads=['XA', 'XB'], writes=['sm_gref'])
                P.op('dve', lambda e: e.tensor_tensor(out=B3, in0=B3, in1=S['gref'][:, 0:NC].unsqueeze(2).to_broadcast([4, NC, LC]),
                                                      op=ALU.subtract), reads=['XB', 'sm_gref'], writes=['XB'])
                for b in range(T // pbk):
                    t0 = b * pbk
                    pz, pk = nps()
                    for kc in range(8):
                        P.op('pe', lambda e: e.matmul(pz[pb:pb + 4, 0:pbk], wGb[:, kc, d * 4:(d + 1) * 4], hT[:, kc, t0:t0 + pbk],
                                                      start=(kc == 0), stop=(kc == 7)), reads=['wGb'] + hT_keys(t0, t0 + pbk), writes=[pk])
                    P.op('dve', lambda e: e.scalar_tensor_tensor(out=XA[rows, t0:t0 + pbk], in0=pz[pb:pb + 4, 0:pbk], scalar=gbT[rows, d:d + 1],
                                                                 in1=XB[rows, t0:t0 + pbk], op0=ALU.add, op1=ALU.add),
                         reads=[pk, 'gbT', 'XB', 'XA'], writes=['XA'])
                P.op('dve', lambda e: e.tensor_reduce(out=S['ac'][:, 0:NC], in_=A3, axis=AX.X, op=ALU.max), reads=['XA'], writes=['sm_ac'])
                P.op('dve', lambda e: e.tensor_scalar(out=S['bl'][:, 0:NC], in0=B3[:, :, ce_], scalar1=-1.0, scalar2=None, op0=ALU.mult),
                     reads=['XB'], writes=['sm_bl'])
                if is_sample:
                    P.dma(S['m0'][:, 0:1], s_m[d, :].rearrange("(h o) -> h o", o=1), writes=['sm_m0'])
                else:
                    P.op('dve', lambda e: e.memset(S['m0'][:, 0:1], 0.0), writes=['sm_m0'])
                if d == 0:
                    P.op('dve', lambda e: e.tensor_tensor_scan(out=S['M'][:, 0:NC], data0=S['ac'][:, 0:NC], data1=S['bl'][:, 0:NC],
                                                               initial=S['m0'][:, 0:1], op0=ALU.max, op1=ALU.add),
                         reads=['sm_ac', 'sm_bl', 'sm_m0'], writes=['sm_M'])
                    P.op('dve', lambda e: e.tensor_copy(out=S['MP'][:, 0:1], in_=S['m0'][:, 0:1]), reads=['sm_m0'], writes=['sm_MP'])
                    if NC > 1:
                        P.op('dve', lambda e: e.tensor_copy(out=S['MP'][:, 1:NC], in_=S['M'][:, 0:NC - 1]), reads=['sm_M', 'sm_MP'], writes=['sm_MP'])
                else:
                    P.op('dve', lambda e: e.tensor_tensor_scan(out=S['M'][:, 0:NC][:, ::-1], data0=S['ac'][:, 0:NC][:, ::-1],
                                                               data1=S['bl'][:, 0:NC][:, ::-1], initial=S['m0'][:, 0:1],
                                                               op0=ALU.max, op1=ALU.add),
                         reads=['sm_ac', 'sm_bl', 'sm_m0'], writes=['sm_M'])
                    P.op('dve', lambda e: e.tensor_copy(out=S['MP'][:, NC - 1:NC], in_=S['m0'][:, 0:1]), reads=['sm_m0'], writes=['sm_MP'])
                    if NC > 1:
                        P.op('dve', lambda e: e.tensor_copy(out=S['MP'][:, 0:NC - 1], in_=S['M'][:, 1:NC]), reads=['sm_M', 'sm_MP'], writes=['sm_MP'])
                P.op('dve', lambda e: e.tensor_tensor(out=S['mu'][:, 0:NC], in0=S['MP'][:, 0:NC], in1=S['ac'][:, 0:NC], op=ALU.max),
                     reads=['sm_MP', 'sm_ac'], writes=['sm_mu'])
                P.op('dve', lambda e: e.tensor_tensor(out=S['al'][:, 0:NC], in0=S['MP'][:, 0:NC], in1=S['mu'][:, 0:NC], op=ALU.subtract),
                     reads=['sm_MP', 'sm_mu'], writes=['sm_al'])
                P.op('act', lambda e: e.activation(out=S['al'][:, 0:NC], in_=S['al'][:, 0:NC], func=AF.Exp), reads=['sm_al'], writes=['sm_al'])
                mub = S['mu'][:, 0:NC].unsqueeze(2).to_broadcast([4, NC, LC])
                P.op('dve', lambda e: e.tensor_tensor(out=A3, in0=A3, in1=mub, op=ALU.subtract), reads=['XA', 'sm_mu'], writes=['XA'])
                P.op('dve', lambda e: e.tensor_tensor(out=B3, in0=B3, in1=mub, op=ALU.subtract), reads=['XB', 'sm_mu'], writes=['XB'])
                P.op('dve', lambda e: e.tensor_scalar(out=XA[rows, 0:T], in0=XA[rows, 0:T], scalar1=LNK, scalar2=None, op0=ALU.add),
                     reads=['XA'], writes=['XA'])
                P.op('act', lambda e: e.activation(out=XA[rows, 0:T], in_=XA[rows, 0:T], func=AF.Exp), reads=['XA'], writes=['XA'])
                P.op('act', lambda e: e.activation(out=XB[rows, 0:T], in_=XB[rows, 0:T], func=AF.Exp), reads=['XB'], writes=['XB'])
                pz, pk = nps()
                for c in range(NC):
                    P.op('pe', lambda e: e.transpose(out=pz[:, c * 8:c * 8 + 4], in_=XA[rows, c * LC:(c + 1) * LC],
                                                     identity=ident[rows, pb:pb + 4]), reads=['XA', 'ident'], writes=[pk])
                    P.op('pe', lambda e: e.transpose(out=pz[:, c * 8 + 4:c * 8 + 8], in_=XB[rows, c * LC:(c + 1) * LC],
                                                     identity=ident[rows, pb:pb + 4]), reads=['XB', 'ident'], writes=[pk])
                P.op('dve', lambda e: e.tensor_copy(out=Wtok[:, d, 0:NC, :], in_=pz[:, 0:NC * 8].rearrange("p (c k) -> p c k", k=8)),
                     reads=[pk], writes=['Wtok'])
                P.op('dve', lambda e: e.tensor_copy(out=Wtokb[:, d, 0:NC, :], in_=Wtok[:, d, 0:NC, 0:4]), reads=['Wtok'], writes=['Wtokb'])
                pz, pk = nps()
                for hd in range(4):
                    P.op('pe', lambda e: e.matmul(pz[:, hd * 16:hd * 16 + NC], SEL[rows, hd, :], S['al'][:, 0:NC], start=True, stop=True),
                         reads=['SEL', 'sm_al'], writes=[pk])
                P.op('dve', lambda e: e.tensor_copy(out=ALb[:, d, :, 0:NC], in_=pz[:, 0:64].rearrange("p (h c) -> p h c", c=16)[:, :, 0:NC]),
                     reads=[pk], writes=['ALb'])

        def conv_tile(dst, dkey, slot_j, widx, t0src, T, is_sample):
            X = FT[1][:, 0:T]
            A = FT[0][:, 0:T]
            if is_sample:
                R_, Cw = T // 64, 64
                taps = [(dr, dc) for dr in (-1, 0, 1) for dc in (-1, 0, 1)]
            else:
                R_, Cw = 1, T
                taps = [(0, dc) for dc in (-1, 0, 1)]
            X3 = X.rearrange("p (r c) -> p r c", c=Cw)
            A3 = A.rearrange("p (r c) -> p r c", c=Cw)
            P.op('dve', lambda e: e.tensor_scalar(out=A, in0=X, scalar1=cw[:, widx, 4:5], scalar2=None, op0=ALU.mult),
                 reads=['FT1', 'cw'], writes=['FT0'])
            for (dr, dc) in taps:
                if dr == 0 and dc == 0:
                    continue
                r0, r1 = max(0, -dr), R_ - max(0, dr)
                c0, c1 = max(0, -dc), Cw - max(0, dc)
                ti = (dr + 1) * 3 + (dc + 1)
                P.op('dve', lambda e: e.scalar_tensor_tensor(out=A3[:, r0:r1, c0:c1], in0=X3[:, r0 + dr:r1 + dr, c0 + dc:c1 + dc],
                                                             scalar=cw[:, widx, ti:ti + 1], in1=A3[:, r0:r1, c0:c1],
                                                             op0=ALU.mult, op1=ALU.add), reads=['FT1', 'FT0', 'cw'], writes=['FT0'])
            P.op('act', lambda e: e.activation(out=dst[:, slot_j, 0:T], in_=A, func=AF.Silu, bias=cb[:, widx:widx + 1], scale=1.0),
                 reads=['FT0', 'cb'], writes=[dkey])

        hacc = [FT[2 + i][:, 0:TS].rearrange("p (j e) -> p j e", e=DH) for i in range(4)]

        def mlstm_head(hd, off, T, is_sample, pidx):
            NC = T // LC
            NTt = T // 128
            pbk = min(512, T)
            for qk in range(2):
                load_w1(qk * 2048 + hd * DH, DH)
                for j in range(4):
                    for b in range(T // pbk):
                        t0 = b * pbk
                        pz, pk = nps()
                        for kc in range(8):
                            P.op('pe', lambda e: e.matmul(pz[:, 0:pbk], wbf1[:, kc, j * 128:(j + 1) * 128], hT[:, kc, t0:t0 + pbk],
                                                          start=(kc == 0), stop=(kc == 7)), reads=['wbf1'] + hT_keys(t0, t0 + pbk), writes=[pk])
                        P.op('act', lambda e: e.copy(out=FT[1][:, t0:t0 + pbk], in_=pz[:, 0:pbk]), reads=[pk], writes=['FT1'])
                    widx = (qk * 4 + hd) * 4 + j
                    if qk == 0:
                        conv_tile(qT, ('yT', 4 + j), j, widx, 0, T, is_sample)
                    else:
                        conv_tile(kT, 'kT', j, widx, 0, T, is_sample)
            load_w1(4096 + hd * DH, DH)
            qkeys = [('yT', 4 + j) for j in range(4)]
            for d in range(2):
                rev = (d == 1)
                if is_sample:
                    P.dma(Cst[:], s_C[d, hd].rearrange("(j p) e -> p j e", p=128), writes=['Cst'])
                    P.dma(nst[:, 0:4], s_n[d, hd].rearrange("(j p) -> p j", p=128), writes=['nst'], allow_slow_non_contiguous=True)
                else:
                    P.op('pool', lambda e: e.memset(Cst[:], 0.0), writes=['Cst'])
                    P.op('pool', lambda e: e.memset(nst[:, 0:4], 0.0), writes=['nst'])
                chunks = list(range(NC))
                if rev:
                    chunks = chunks[::-1]
                for c in chunks:
                    cs = slice(c * LC, (c + 1) * LC)
                    wcol = Wtok[:, d, c, hd:hd + 1]
                    thcol = Wtok[:, d, c, 4 + hd:5 + hd]
                    alcol = ALb[:, d, hd, c:c + 1]
                    pv, pvk = nps()
                    for kc in range(8):
                        P.op('pe', lambda e: e.matmul(pv[:, 0:DH], hT[:, kc, cs], wbf1[:, kc, :], start=(kc == 0), stop=(kc == 7)),
                             reads=['wbf1', ('hT', c)], writes=[pvk])
                    P.op('act', lambda e: e.copy(out=vch[:], in_=pv[:, 0:DH]), reads=[pvk], writes=['vch'])
                    pt, ptk = nps()
                    ptb = pt[:].bitcast(BF16)
                    for j in range(4):
                        P.op('pe', lambda e: e.transpose(out=ptb[:, j * 128:(j + 1) * 128], in_=kT[:, j, cs], identity=identb[:]),
                             reads=['kT', 'identb'], writes=[ptk])
                    P.op('act', lambda e: e.copy(out=ktok[:], in_=ptb[:, 0:DH]), reads=[ptk], writes=['ktok'])
                    ps_, psk = nps()
                    for j in range(4):
                        P.op('pe', lambda e: e.matmul(ps_[:, 0:128], kT[:, j, cs], qT[:, j, cs], start=(j == 0), stop=(j == 3)),
                             reads=['kT'] + qkeys, writes=[psk])
                    P.op('dve', lambda e: e.scalar_tensor_tensor(out=sTs[:], in0=ps_[:, 0:128], scalar=wcol, in1=mC[:, d, :],
                                                                 op0=ALU.mult, op1=ALU.mult), reads=[psk, 'Wtok', 'mC'], writes=['sTs'])
                    P.op('dve', lambda e: e.tensor_scalar(out=Cst[:], in0=Cst[:], scalar1=alcol, scalar2=None, op0=ALU.mult),
                         reads=['Cst', 'ALb'], writes=['Cst'])
                    P.op('act', lambda e: e.copy(out=Cbf[:], in_=Cst[:]), reads=['Cst'], writes=['Cbf'])
                    P.op('dve', lambda e: e.tensor_scalar(out=nst[:, 0:4], in0=nst[:, 0:4], scalar1=alcol, scalar2=None, op0=ALU.mult),
                         reads=['nst', 'ALb'], writes=['nst'])
                    P.op('dve', lambda e: e.tensor_copy(out=nbf[:], in_=nst[:, 0:4]), reads=['nst'], writes=['nbf'])
                    pn, pnk = nps()
                    for j in range(4):
                        P.op('pe', lambda e: e.matmul(pn[:, 0:DH], qT[:, j, cs], Cbf[:, j, :], start=(j == 0), stop=False),
                             reads=qkeys + ['Cbf'], writes=[pnk])
                    P.op('pe', lambda e: e.matmul(pn[:, 0:DH], sTs[:], vch[:], start=False, stop=True),
                         reads=['sTs', 'vch'], writes=[pnk])
                    pd_, pdk = nps()
                    for j in range(4):
                        P.op('pe', lambda e: e.matmul(pd_[:, 0:1], qT[:, j, cs], nbf[:, j:j + 1], start=(j == 0), stop=False),
                             reads=qkeys + ['nbf'], writes=[pdk])
                    P.op('pe', lambda e: e.matmul(pd_[:, 0:1], sTs[:], onesb[:], start=False, stop=True), reads=['sTs', 'onesb'], writes=[pdk])
                    P.op('act', lambda e: e.activation(out=dstat[:, 2:3], in_=pd_[:, 0:1], func=AF.Abs), reads=[pdk], writes=['dstat'])
                    P.op('dve', lambda e: e.tensor_tensor(out=dstat[:, 0:1], in0=dstat[:, 2:3], in1=thcol, op=ALU.max),
                         reads=['dstat', 'Wtok'], writes=['dstat'])
                    P.op('dve', lambda e: e.reciprocal(out=dstat[:, 1:2], in_=dstat[:, 0:1]), reads=['dstat'], writes=['dstat'])
                    hdst = hacc[c // 4][:, c % 4, :]
                    if d == 0:
                        P.op('act', lambda e: e.activation(out=hdst, in_=pn[:, 0:DH], func=AF.Identity, scale=dstat[:, 1:2]),
                             reads=[pnk, 'dstat'], writes=[('hacc', c)])
                    else:
                        P.op('dve', lambda e: e.scalar_tensor_tensor(out=hdst, in0=pn[:, 0:DH], scalar=dstat[:, 1:2], in1=hdst,
                                                                     op0=ALU.mult, op1=ALU.add), reads=[pnk, 'dstat', ('hacc', c)], writes=[('hacc', c)])
                    P.op('pool', lambda e: e.tensor_scalar(out=vw[:], in0=vch[:], scalar1=wcol, scalar2=None, op0=ALU.mult),
                         reads=['vch', 'Wtok'], writes=['vw'])
                    for j in range(4):
                        pc, pck = nps()
                        P.op('pe', lambda e: e.matmul(pc[:, 0:DH], ktok[:, j * 128:(j + 1) * 128], vw[:], start=True, stop=True),
                             reads=['ktok', 'vw'], writes=[pck])
                        P.op('dve', lambda e: e.tensor_tensor(out=Cst[:, j, :], in0=Cst[:, j, :], in1=pc[:, 0:DH], op=ALU.add),
                             reads=[pck, 'Cst'], writes=['Cst'])
                    pq_, pqk = nps()
                    for j in range(4):
                        P.op('pe', lambda e: e.matmul(pq_[:, j:j + 1], ktok[:, j * 128:(j + 1) * 128], Wtokb[:, d, c, hd:hd + 1],
                                                      start=True, stop=True), reads=['ktok', 'Wtokb'], writes=[pqk])
                    P.op('dve', lambda e: e.tensor_tensor(out=nst[:, 0:4], in0=nst[:, 0:4], in1=pq_[:, 0:4], op=ALU.add),
                         reads=[pqk, 'nst'], writes=['nst'])
                if not is_sample:
                    P.dma(o_C[pidx, d, hd].rearrange("(j p) e -> p j e", p=128), Cst[:], reads=['Cst'], q='pool')
                    P.dma(o_n[pidx, d, hd].rearrange("(j p) -> p j", p=128), nst[:, 0:4], reads=['nst'], q='pool', allow_slow_non_contiguous=True)
            load_w1(6144 + hd * DH, DH)
            for tt in range(NTt):
                hdst = hacc[tt // 4][:, tt % 4, :]
                pz, pk = nps()
                for kc in range(8):
                    P.op('pe', lambda e: e.matmul(pz[:, 0:DH], hT[:, kc, tt * 128:(tt + 1) * 128], wbf1[:, kc, :],
                                                  start=(kc == 0), stop=(kc == 7)), reads=['wbf1', ('hT', tt)], writes=[pk])
                P.op('act', lambda e: e.activation(out=FT[0][:, 0:DH], in_=pz[:, 0:DH], func=AF.Sigmoid), reads=[pk], writes=['FT0'])
                P.op('dve', lambda e: e.tensor_tensor(out=hdst, in0=hdst, in1=FT[0][:, 0:DH], op=ALU.mult),
                     reads=['FT0', ('hacc', tt)], writes=[('hacc', tt)])
                P.op('act', lambda e: e.activation(out=FT[0][:, 0:DH], in_=hdst, func=AF.Square, accum_out=dstat[:, 4:5]),
                     reads=[('hacc', tt), 'FT0'], writes=['FT0', 'dstat'])
                P.op('dve', lambda e: e.tensor_scalar(out=dstat[:, 5:6], in0=dstat[:, 4:5], scalar1=1.0 / DH, scalar2=1e-6,
                                                      op0=ALU.mult, op1=ALU.add), reads=['dstat'], writes=['dstat'])
                P.op('act', lambda e: e.activation(out=dstat[:, 6:7], in_=dstat[:, 5:6], func=AF.Sqrt), reads=['dstat'], writes=['dstat'])
                P.op('dve', lambda e: e.reciprocal(out=dstat[:, 7:8], in_=dstat[:, 6:7]), reads=['dstat'], writes=['dstat'])
                P.op('dve', lambda e: e.tensor_scalar(out=hdst, in0=hdst, scalar1=dstat[:, 7:8], scalar2=None, op0=ALU.mult),
                     reads=[('hacc', tt), 'dstat'], writes=[('hacc', tt)])
            load_w1(8192 + hd * DH, DH)
            for tt in range(NTt):
                hdst = hacc[tt // 4][:, tt % 4, :]
                pz, pk = nps()
                for kc in range(8):
                    P.op('pe', lambda e: e.matmul(pz[:, 0:DH], hT[:, kc, tt * 128:(tt + 1) * 128], wbf1[:, kc, :],
                                                  start=(kc == 0), stop=(kc == 7)), reads=['wbf1', ('hT', tt)], writes=[pk])
                P.op('act', lambda e: e.activation(out=FT[0][:, 0:DH], in_=pz[:, 0:DH], func=AF.Silu), reads=[pk], writes=['FT0'])
                P.op('dve', lambda e: e.tensor_tensor(out=hdst, in0=hdst, in1=FT[0][:, 0:DH], op=ALU.mult),
                     reads=['FT0', ('hacc', tt)], writes=[('hacc', tt)])
                pz, pk = nps()
                for j in range(4):
                    P.op('pe', lambda e: e.transpose(out=pz[:, j * 128:(j + 1) * 128], in_=hdst[:, j * 128:(j + 1) * 128], identity=ident[:]),
                         reads=[('hacc', tt), 'ident'], writes=[pk])
                for j in range(4):
                    P.op('act', lambda e: e.activation(out=yT[:, j, tt * 128:(tt + 1) * 128], in_=pz[:, j * 128:(j + 1) * 128],
                                                       func=AF.Identity, scale=mng[:, hd * 4 + j:hd * 4 + j + 1]),
                         reads=[pk, 'mng'], writes=[('yT', j)])

        for si in seq_ids:
            off, T, cidx, is_sample = SEQS[si]
            P.barrier()
            make_gate(1, cidx)
            make_hT(1, x1, 'x1', off, T, cidx)
            P.barrier()
            gates_seq(T, is_sample)
            if not is_sample:
                for d in range(2):
                    lastc = (T // LC - 1) if d == 0 else 0
                    P.dma(o_m[si - 1, d, :].rearrange("(h o) -> h o", o=1), sm['M'][32 * d:32 * d + 4, lastc:lastc + 1],
                          reads=['sm_M'], q='pool')
            for hd in range(4):
                P.barrier()
                mlstm_head(hd, off, T, is_sample, si - 1)
                if debug and debug.get('dump_y'):
                    for j in range(4):
                        dump("yM%d_%d" % (hd, j), yT[:, j, 0:T], [('yT', j)], T, col0=off)
                P.barrier()
                load_wo(w_out_odd[hd * DH:(hd + 1) * DH, :], 128, nk=4)
                last = (hd == 3)
                outproj(1, [(128, s_) for s_ in range(4)], x1, 'x1', (yout if last else x1), ('yout' if last else 'x1'),
                        off, T, cidx, final=last)
        P.barrier()
        L1.close()
    P.finish()
    sems = {s: es.enter_context(nc.semaphore(s)) for s in P.sem_names}
    P.emit(sems)
    es.close()
    global _last_dslot
    _last_dslot = dslot if debug else {}
    return nc, P


def host_inputs(inp, core):
    f = lambda a: np.ascontiguousarray(a, dtype=np.float32)
    b = core % 2
    m = {}
    m["xin"] = f(np.concatenate([inp["x_sample"][b], inp["x_prompt"][2 * core], inp["x_prompt"][2 * core + 1]], axis=0))
    cond = np.stack([inp["c"][b], inp["c_ctx"]], axis=0)
    m["condT"] = f(cond.reshape(2, 8, 128).transpose(2, 1, 0))
    m["s_hgrn"] = f(inp["state_hgrn"][b, 0])
    m["s_rwkv"] = f(inp["state_rwkv"][b, 0])
    m["s_C"] = f(inp["state_mlstm_C"][b, 0])
    m["s_n"] = f(inp["state_mlstm_n"][b, 0])
    m["s_m"] = f(inp["state_mlstm_m"][b, 0])
    m["w_mod"] = f(inp["w_mod"])
    m["b_modT"] = f(inp["b_mod"].reshape(2, 24, 128).transpose(2, 0, 1))
    m["norm_gT"] = f(inp["norm_g"].reshape(2, 8, 128).transpose(2, 0, 1))
    m["fnorm_gT"] = f(inp["final_norm_g"].reshape(8, 128).T)
    w = inp["w_in_even"][0]
    DA = 1024
    wA = np.stack([np.concatenate([w[:, g * DA + h * 128: g * DA + (h + 1) * 128] for g in (0, 1, 4, 2, 3)], axis=1)
                   for h in range(8)], axis=0)
    m["wA"] = f(wA)
    o = 5 * DA
    zb0 = o + 3328
    wB = np.stack([np.concatenate([w[:, o + g * 1024 + h * 64: o + g * 1024 + (h + 1) * 64] for g in (0, 1, 2)]
                                  + [w[:, zb0 + h * 64: zb0 + (h + 1) * 64]], axis=1) for h in range(16)], axis=0)
    m["wB"] = f(wB)
    m["wLR"] = f(w[:, o + 3072: o + 3328])
    m["w_out_even"] = f(inp["w_out_even"][0])
    m["lbT"] = f(inp["hgrn_lb_logits"].reshape(2, 8, 128).transpose(2, 0, 1))
    m["hg_gT"] = f(inp["hgrn_norm_g"][0].reshape(8, 128).T)
    mu = inp["rwkv_shift_mu"][0]
    mr = np.zeros((64, 2, 4, 16), np.float32)
    for g in range(3):
        mr[:, :, g, :] = mu[:, g * 1024:(g + 1) * 1024].reshape(2, 16, 64).transpose(2, 0, 1)
    m["mu_rkv"] = mr
    m["mu_lr"] = f(mu[:, 3072:3328].reshape(2, 4, 64).transpose(2, 0, 1))
    m["w0T"] = f(inp["rwkv_w0"][0].reshape(2, 16, 64).transpose(2, 0, 1))
    m["a0T"] = f(inp["rwkv_a0"][0].reshape(2, 16, 64).transpose(2, 0, 1))
    m["w2"] = f(inp["rwkv_w2"][0])
    m["a2"] = f(inp["rwkv_a2"][0])
    m["kkT"] = f(inp["rwkv_k_k"][0].reshape(16, 64).T)
    m["kaT"] = f(inp["rwkv_k_a"][0].reshape(16, 64).T)
    m["rkT"] = f(inp["rwkv_r_k"][0].T)
    m["gngT"] = f(inp["rwkv_gn_g"][0].reshape(16, 64).T)
    m["gnbT"] = f(inp["rwkv_gn_b"][0].reshape(16, 64).T)
    s = np.arange(128)[:, None]
    t = np.arange(128)[None, :]
    same = (s // 32) == (t // 32)
    m["maskH"] = np.stack([(same & (s <= t)), (same & (s >= t))]).astype(np.float32)
    m["ident_in"] = np.eye(128, dtype=np.float32)
    s6 = np.arange(64)[:, None]
    t6 = np.arange(64)[None, :]
    mr_ = np.zeros((2, 3, 64, 128), np.float32)
    for d_, (st_, inc_) in enumerate([((s6 < t6), (s6 <= t6)), ((s6 > t6), (s6 >= t6))]):
        st_ = st_.astype(np.float32)
        inc_ = inc_.astype(np.float32)
        mr_[d_, 0, :, 0:64] = -st_
        mr_[d_, 0, :, 64:128] = -inc_
        mr_[d_, 1, :, 0:64] = st_
        mr_[d_, 1, :, 64:128] = inc_
        mr_[d_, 2, :, 0:64] = -(st_.T)
    m["maskR"] = mr_
    m["w_in_odd"] = f(inp["w_in_odd"][0])
    m["w_out_odd"] = f(inp["w_out_odd"][0])
    m["maskC"] = np.stack([(s <= t), (s >= t)]).astype(np.float32)
    sel = np.zeros((36, 4, 128), np.float32)
    gbt = np.zeros((36, 4), np.float32)
    for pb_ in (0, 32):
        for k_ in range(4):
            sel[pb_ + k_, k_, :] = 1.0
        gbt[pb_:pb_ + 4, :] = inp["mlstm_gate_b"][0].T
    m["sel_d"] = sel
    m["gbT_d"] = gbt
    m["cw_d"] = f(inp["mlstm_conv_w"][0].reshape(9, 32, 128).transpose(2, 1, 0))
    m["cb_d"] = f(inp["mlstm_conv_b"][0].reshape(32, 128).T)
    m["mng_d"] = f(inp["mlstm_norm_g"][0].reshape(16, 128).T)
    return m


def kernel(**inp):
    inp = {k: np.asarray(v) for k, v in inp.items()}
    nc, P = build()
    in_maps = [host_inputs(inp, c) for c in range(NCORES)]
    res = run_bass_kernel_spmd(nc, in_maps, core_ids=list(range(NCORES)))
    r = res.results
    y_prompt = np.zeros((16, TP, D), np.float32)
    y_sample = np.zeros((2, TS, D), np.float32)
    for c in range(NCORES):
        y_prompt[2 * c] = r[c]["yout"][TS:TS + TP]
        y_prompt[2 * c + 1] = r[c]["yout"][TS + TP:]
    for b in range(2):
        y_sample[b] = r[b]["yout"][0:TS]
    new_hgrn = np.concatenate([r[c]["o_hgrn"] for c in range(NCORES)], axis=0)[:, None]
    new_rwkv = np.concatenate([r[c]["o_rwkv"] for c in range(NCORES)], axis=0)[:, None]
    new_C = np.concatenate([r[c]["o_C"] for c in range(NCORES)], axis=0)[:, None]
    new_n = np.concatenate([r[c]["o_n"] for c in range(NCORES)], axis=0)[:, None]
    new_m = np.concatenate([r[c]["o_m"] for c in range(NCORES)], axis=0)[:, None]
    return (y_prompt, y_sample, new_hgrn.astype(np.float32), new_rwkv.astype(np.float32),
            new_C.astype(np.float32), new_n.astype(np.float32), new_m.astype(np.float32))
```

```python
import contextlib
import numpy as np
import concourse.bass as bass
import concourse.mybir as mybir
from concourse.bass_utils import run_bass_kernel_spmd

F32 = mybir.dt.float32
BF16 = mybir.dt.bfloat16
ALU = mybir.AluOpType
AF = mybir.ActivationFunctionType
AX = mybir.AxisListType

D = 1024
TS = 2048
TP = 256
TT = TS + 2 * TP
NCORES = 8


class _Rec:
    def __init__(self):
        self.calls = []

    def __getattr__(self, name):
        def f(*a, **k):
            self.calls.append((name, a, k))
            return self
        return f


class Prog:
    ENGS = ['pe', 'dve', 'act', 'pool', 'sp']
    NDMA = 16

    def __init__(self, nc):
        self.nc = nc
        self.ops = {e: [] for e in self.ENGS}
        self.cnt = {}
        self.waited = {e: {} for e in self.ENGS}
        self.last_write = {}
        self.readers = {}
        self.dma_rr = 0
        self.sem_names = list(self.ENGS) + ['d%d' % i for i in range(self.NDMA)]
        for s in self.sem_names:
            self.cnt[s] = 0
        self.n_ops = 0

    def _deps(self, eng, reads, writes):
        deps = {}

        def add(p):
            if p is None:
                return
            f, n = p
            if f == 'pe' and eng == 'pe':
                return
            if n > deps.get(f, 0):
                deps[f] = n
        for k in reads:
            add(self.last_write.get(k))
        for k in writes:
            add(self.last_write.get(k))
            for p in self.readers.get(k, ()):
                add(p)
        waits = []
        for f, n in deps.items():
            if n > self.waited[eng].get(f, 0):
                waits.append((f, n))
                self.waited[eng][f] = n
        return waits

    def _commit(self, tag, reads, writes):
        for k in reads:
            lst = self.readers.setdefault(k, [])
            lst[:] = [p for p in lst if p[0] != tag[0]]
            lst.append(tag)
        for k in writes:
            self.last_write[k] = tag
            self.readers[k] = []

    def op(self, eng, fn, reads=(), writes=()):
        rec = _Rec()
        fn(rec)
        name, a, k = rec.calls[0]
        fn = (lambda e, name=name, a=a, k=k: getattr(e, name)(*a, **k))
        waits = self._deps(eng, reads, writes)
        self.cnt[eng] += 1
        tag = (eng, self.cnt[eng])
        self.ops[eng].append((waits, fn, eng, 1))
        self._commit(tag, reads, writes)
        self.n_ops += 1

    def dma(self, out, in_, reads=(), writes=(), q='sp', **kw):
        d = 'd%d' % self.dma_rr
        self.dma_rr = (self.dma_rr + 1) % self.NDMA
        waits = self._deps(q, reads, writes)
        prev = self.cnt[d]
        if prev > self.waited[q].get(d, 0):
            waits.append((d, prev))
            self.waited[q][d] = prev
        self.cnt[d] += 16
        tag = (d, self.cnt[d])
        self.ops[q].append((waits, (lambda e: e.dma_start(out=out, in_=in_, **kw)), d, 16))
        self._commit(tag, reads, writes)
        self.n_ops += 1

    def barrier(self):
        allsems = list(self.sem_names)
        for e in self.ENGS:
            waits = []
            for f in allsems:
                if self.cnt[f] > self.waited[e].get(f, 0):
                    waits.append((f, self.cnt[f]))
                    self.waited[e][f] = self.cnt[f]
            self.ops[e].append((waits, None, None, 0))

    def finish(self, q='sp'):
        waits = []
        for i in range(self.NDMA):
            d = 'd%d' % i
            if self.cnt[d] > self.waited[q].get(d, 0):
                waits.append((d, self.cnt[d]))
                self.waited[q][d] = self.cnt[d]
        self.ops[q].append((waits, None, None, 0))

    def emit(self, sems):
        ops = self.ops

        def run(e, lst):
            for waits, fn, semname, inc in lst:
                for f, n in waits:
                    e.wait_ge(sems[f], n)
                if fn is not None:
                    fn(e).then_inc(sems[semname], inc)
        with self.nc.Block() as block:
            @block.tensor
            def _(e):
                run(e, ops['pe'])

            @block.vector
            def _(e):
                run(e, ops['dve'])

            @block.scalar
            def _(e):
                run(e, ops['act'])

            @block.gpsimd
            def _(e):
                run(e, ops['pool'])

            @block.sync
            def _(e):
                run(e, ops['sp'])


SEQS = [(0, TS, 0, True), (TS, TP, 1, False), (TS + TP, TP, 1, False)]


def build(debug=None):
    nc = bass.Bass('TRN2', target_bir_lowering=False)
    P = Prog(nc)
    es = contextlib.ExitStack()

    def din(name, shape):
        return nc.dram_tensor(name, list(shape), F32, kind="ExternalInput").ap()

    def dout(name, shape):
        return nc.dram_tensor(name, list(shape), F32, kind="ExternalOutput").ap()

    xin = din("xin", [TT, D])
    condT = din("condT", [128, 8, 2])
    s_hgrn = din("s_hgrn", [2, 8, 128, 128])
    s_rwkv = din("s_rwkv", [2, 16, 64, 64])
    s_C = din("s_C", [2, 4, 512, 512])
    s_n = din("s_n", [2, 4, 512])
    s_m = din("s_m", [2, 4])
    w_mod = din("w_mod", [2, D, 3 * D])
    b_modT = din("b_modT", [128, 2, 24])
    norm_gT = din("norm_gT", [128, 2, 8])
    fnorm_gT = din("fnorm_gT", [128, 8])
    wA = din("wA", [8, D, 640])
    wB = din("wB", [16, D, 256])
    wLR = din("wLR", [D, 256])
    w_out_even = din("w_out_even", [2 * D, D])
    lbT = din("lbT", [128, 2, 8])
    hg_gT = din("hg_gT", [128, 8])
    mu_rkv = din("mu_rkv", [64, 2, 4, 16])
    mu_lr = din("mu_lr", [64, 2, 4])
    w0T = din("w0T", [64, 2, 16])
    a0T = din("a0T", [64, 2, 16])
    w2 = din("w2", [2, 64, D])
    a2 = din("a2", [2, 64, D])
    kkT = din("kkT", [64, 16])
    kaT = din("kaT", [64, 16])
    rkT = din("rkT", [64, 16])
    gngT = din("gngT", [64, 16])
    gnbT = din("gnbT", [64, 16])
    maskR = din("maskR", [2, 3, 64, 128])
    maskH = din("maskH", [2, 128, 128])
    ident_d = din("ident_in", [128, 128])
    w_in_odd = din("w_in_odd", [D, 10256])
    w_out_odd = din("w_out_odd", [2 * D, D])
    maskC = din("maskC", [2, 128, 128])
    sel_d = din("sel_d", [36, 4, 128])
    gbT_d = din("gbT_d", [36, 4])
    cw_d = din("cw_d", [128, 32, 9])
    cb_d = din("cb_d", [128, 32])
    mng_d = din("mng_d", [128, 16])

    yout = dout("yout", [TT, D])
    o_hgrn = dout("o_hgrn", [2, 2, 8, 128, 128])
    o_rwkv = dout("o_rwkv", [2, 2, 16, 64, 64])
    o_C = dout("o_C", [2, 2, 4, 512, 512])
    o_n = dout("o_n", [2, 2, 4, 512])
    o_m = dout("o_m", [2, 2, 4])
    dbg = dout("dbg", [40, 128, TT]) if debug else None
    dslot = {}
    dumpt = {}
    x1 = dout("x1", [TT, D]) if debug else nc.dram_tensor("x1", [TT, D], F32, kind="Internal").ap()

    def sb(name, shape, dt=F32):
        return es.enter_context(nc.sbuf_tensor(name, list(shape), dt))

    pstiles = [es.enter_context(nc.psum_tensor("ps%d" % i, [128, 512], F32)) for i in range(8)]
    psrr = [0]

    def nps():
        i = psrr[0]
        psrr[0] = (i + 1) % 8
        return pstiles[i], 'ps%d' % i

    def dump(name, ap, keys, n, col0=0, parts=128):
        if not debug:
            return
        slot = dslot.setdefault(name, len(dslot))
        dt_ = dumpt['tile']
        for c0 in range(0, n, 512):
            w_ = min(512, n - c0)
            P.op('pool', (lambda e, c0=c0, w_=w_: e.tensor_copy(out=dt_[0:parts, 0:w_], in_=ap[:, c0:c0 + w_])),
                 reads=keys, writes=['dumpt'])
            P.dma(dbg[slot, 0:parts, col0 + c0:col0 + c0 + w_], dt_[0:parts, 0:w_], reads=['dumpt'])

    if debug:
        dumpt['tile'] = sb("dumpt", [128, 512])

    ident = sb("ident", [128, 128])
    ones = sb("ones", [128, 128])
    P.dma(ident[:], ident_d[:], writes=['ident'])
    P.op('dve', lambda e: e.memset(ones[:], 1.0), writes=['ones'])

    condT_sb = sb("condT_sb", [128, 8, 2])
    bmod_sb = sb("bmod_sb", [128, 2, 24])
    ng_sb = sb("ng_sb", [128, 2, 8])
    fng_sb = sb("fng_sb", [128, 8])
    lb_sb = sb("lb_sb", [128, 2, 8])
    hgg_sb = sb("hgg_sb", [128, 8])
    for t_, d_, k_ in [(condT_sb, condT, 'condT'), (bmod_sb, b_modT, 'bmod'), (ng_sb, norm_gT, 'ng'),
                       (fng_sb, fnorm_gT, 'fng'), (lb_sb, lbT, 'lb'), (hgg_sb, hg_gT, 'hgg')]:
        P.dma(t_[:], d_[:], writes=[k_])

    scT = sb("scT", [128, 8, 2])
    P.op('act', lambda e: e.activation(out=scT[:], in_=condT_sb[:], func=AF.Silu), reads=['condT'], writes=['scT'])
    mT = sb("mT", [128, 2, 24, 2])
    sc1 = sb("sc1", [128, 2, 8, 2])
    gate_bc = sb("gate_bc", [128, D])
    dg = sb("dg", [128, 128])

    def make_gate(l, c):
        if True:
            for half in range(2):
                pz, pk = nps()
                for kq in range(4):
                    kc = half * 4 + kq
                    P.op('dve', lambda e: e.tensor_scalar(
                        out=dg[:], in0=ident[:], scalar1=mT[:, l, 16 + kc, c:c + 1], scalar2=None, op0=ALU.mult),
                        reads=['ident', 'mT'], writes=['dg'])
                    P.op('pe', lambda e: e.matmul(pz[:, kq * 128:(kq + 1) * 128], ones[:], dg[:], start=True, stop=True),
                         reads=['ones', 'dg'], writes=[pk])
                P.op('act', lambda e: e.copy(out=gate_bc[:, half * 512:(half + 1) * 512], in_=pz[:]),
                     reads=[pk], writes=['gate_bc'])

    with contextlib.ExitStack() as es2:
        wm = [es2.enter_context(nc.sbuf_tensor("wm%d" % i, [128, 8, 512], F32)) for i in range(2)]
        for l in range(2):
            for cbk in range(6):
                i = (l * 6 + cbk) % 2
                P.dma(wm[i][:], w_mod[l].rearrange("(kc p) n -> p kc n", p=128)[:, :, cbk * 512:(cbk + 1) * 512],
                      writes=['wm%d' % i])
                pz, pk = nps()
                for j in range(4):
                    for kc in range(8):
                        P.op('pe', lambda e: e.matmul(pz[:, j * 2:(j + 1) * 2], wm[i][:, kc, j * 128:(j + 1) * 128],
                                                      scT[:, kc, :], start=(kc == 0), stop=(kc == 7)),
                             reads=['wm%d' % i, 'scT'], writes=[pk])
                P.op('dve', lambda e: e.tensor_tensor(out=mT[:, l, cbk * 4:(cbk + 1) * 4, :],
                                                      in0=pz[:, 0:8].rearrange("p (j c) -> p j c", c=2),
                                                      in1=bmod_sb[:, l, cbk * 4:(cbk + 1) * 4].unsqueeze(2).to_broadcast([128, 4, 2]),
                                                      op=ALU.add),
                     reads=[pk, 'bmod'], writes=['mT'])
    P.barrier()
    for l in range(2):
        P.op('dve', lambda e: e.scalar_tensor_tensor(
            out=sc1[:, l], in0=mT[:, l, 8:16, :], scalar=1.0,
            in1=ng_sb[:, l, :].unsqueeze(2).to_broadcast([128, 8, 2]), op0=ALU.add, op1=ALU.mult),
            reads=['mT', 'ng'], writes=['sc1'])
    fng_holder = {}

    def make_fng():
        fng_bc = sb("fng_bc", [128, D])
        fng_holder['t'] = fng_bc
        for half in range(2):
            pz, pk = nps()
            for kq in range(4):
                kc = half * 4 + kq
                P.op('dve', (lambda e, kc=kc: e.tensor_scalar(
                    out=dg[:], in0=ident[:], scalar1=fng_sb[:, kc:kc + 1], scalar2=None, op0=ALU.mult)),
                    reads=['ident', 'fng'], writes=['dg'])
                P.op('pe', (lambda e, pz=pz, kq=kq: e.matmul(pz[:, kq * 128:(kq + 1) * 128], ones[:], dg[:],
                                                             start=True, stop=True)),
                     reads=['ones', 'dg'], writes=[pk])
            P.op('act', (lambda e, half=half, pz=pz: e.copy(out=fng_bc[:, half * 512:(half + 1) * 512], in_=pz[:])),
                 reads=[pk], writes=['fng_bc'])

    lbv = sb("lbv", [128, 8])
    oml = sb("oml", [128, 8])
    P.op('dve', lambda e: e.tensor_tensor(out=lbv[:], in0=lb_sb[:, 0, :], in1=lb_sb[:, 1, :], op=ALU.subtract),
         reads=['lb'], writes=['lbv'])
    P.op('act', lambda e: e.activation(out=lbv[:], in_=lbv[:], func=AF.Sigmoid), reads=['lbv'], writes=['lbv'])
    P.op('act', lambda e: e.activation(out=oml[:], in_=lbv[:], func=AF.Identity, bias=1.0, scale=-1.0),
         reads=['lbv'], writes=['oml'])

    hT = sb("hT", [128, 8, TS], BF16)
    yT = sb("yT", [128, 8, TS], BF16)
    st4 = sb("st4", [128, 4])
    wst = sb("wst", [128, 8, 256])
    FT = [sb("FT%d" % i, [128, TS + 32]) for i in range(6)]
    xt = [FT[0][:, 0:D], FT[0][:, D:2 * D]]
    xn = FT[1][:, 0:D]
    junk = FT[1][:, D:2 * D]
    wo_v = [FT[2][:, 0:TS].bitcast(BF16).rearrange("p (s n) -> p s n", n=D),
            FT[3][:, 0:TS].bitcast(BF16).rearrange("p (s n) -> p s n", n=D)]

    def load_wo(src, parts, nk=8):
        v = src.rearrange("(kc p) n -> p kc n", p=parts)
        for c0 in range(0, D, 256):
            w_ = min(256, D - c0)
            P.dma(wst[0:parts, 0:nk, 0:w_], v[:, :, c0:c0 + w_], writes=['wst'])
            for hf in range(nk // 4):
                P.op('pool', lambda e: e.tensor_copy(out=wo_v[hf][0:parts, :, c0:c0 + w_], in_=wst[0:parts, hf * 4:hf * 4 + 4, 0:w_]),
                     reads=['wst'], writes=['wo_bf'])

    def make_hT(layer, xsrc, xkey, off, T, cidx):
        for tt in range(T // 128):
            i = tt % 2
            P.dma(xt[i], xsrc[off + tt * 128: off + (tt + 1) * 128, :], reads=[(xkey, off // 128 + tt)], writes=['xt%d' % i])
            P.op('act', lambda e: e.activation(out=junk, in_=xt[i], func=AF.Square, accum_out=st4[:, 0:1]),
                 reads=['xt%d' % i], writes=['junk', 'st4'])
            P.op('dve', lambda e: e.tensor_scalar(out=st4[:, 1:2], in0=st4[:, 0:1], scalar1=1.0 / D, scalar2=1e-6,
                                                  op0=ALU.mult, op1=ALU.add), reads=['st4'], writes=['st4'])
            P.op('act', lambda e: e.activation(out=st4[:, 2:3], in_=st4[:, 1:2], func=AF.Sqrt), reads=['st4'], writes=['st4'])
            P.op('dve', lambda e: e.reciprocal(out=st4[:, 3:4], in_=st4[:, 2:3]), reads=['st4'], writes=['st4'])
            P.op('dve', lambda e: e.tensor_scalar(out=xn, in0=xt[i], scalar1=st4[:, 3:4], scalar2=None, op0=ALU.mult),
                 reads=['xt%d' % i, 'st4'], writes=['xn'])
            for half in range(2):
                pz, pk = nps()
                for kq in range(4):
                    kc = half * 4 + kq
                    P.op('pe', lambda e: e.transpose(out=pz[:, kq * 128:(kq + 1) * 128], in_=xn[:, kc * 128:(kc + 1) * 128],
                                                     identity=ident[:]), reads=['xn', 'ident'], writes=[pk])
                for kq in range(4):
                    kc = half * 4 + kq
                    P.op('act', lambda e: e.activation(
                        out=hT[:, kc, tt * 128:(tt + 1) * 128], in_=pz[:, kq * 128:(kq + 1) * 128], func=AF.Identity,
                        bias=mT[:, layer, kc, cidx:cidx + 1], scale=sc1[:, layer, kc, cidx:cidx + 1]),
                        reads=[pk, 'mT', 'sc1'], writes=[('hT', tt)])

    def hT_keys(t0, t1):
        return [('hT', tt) for tt in range(t0 // 128, (t1 + 127) // 128)]

    def load_w(src, ncols, dst=None, dkey='wbf', parts=128, nk=8):
        dst = wbf if dst is None else dst
        v = src.rearrange("(kc p) n -> p kc n", p=parts)
        for c0 in range(0, ncols, 256):
            w_ = min(256, ncols - c0)
            P.dma(wst[0:parts, 0:nk, 0:w_], v[:, :, c0:c0 + w_], writes=['wst'])
            P.op('pool', lambda e: e.tensor_copy(out=dst[0:parts, 0:nk, c0:c0 + w_], in_=wst[0:parts, 0:nk, 0:w_]),
                 reads=['wst'], writes=[dkey])

    def proj(c0, M, t0, n, evac):
        pz, pk = nps()
        for kc in range(8):
            P.op('pe', lambda e: e.matmul(pz[0:M, 0:n], wbf[:, kc, c0:c0 + M], hT[:, kc, t0:t0 + n],
                                          start=(kc == 0), stop=(kc == 7)),
                 reads=['wbf'] + hT_keys(t0, t0 + n), writes=[pk])
        evac(pz, pk)

    def outproj(layer, groups, xsrc, skey, xdst, dkey, off, T, cidx, final=False):
        for tt in range(T // 128):
            i = tt % 2
            P.dma(xt[i], xsrc[off + tt * 128: off + (tt + 1) * 128, :], reads=[(skey, off // 128 + tt)], writes=['xt%d' % i])
            for half in range(2):
                pz, pk = nps()
                for gi, (K, slot) in enumerate(groups):
                    P.op('pe', lambda e: e.matmul(pz[:, 0:512], yT[0:K, slot, tt * 128:(tt + 1) * 128],
                                                  wo_v[slot // 4][0:K, slot % 4, half * 512:(half + 1) * 512],
                                                  start=(gi == 0), stop=(gi == len(groups) - 1)),
                         reads=[('yT', slot), 'wo_bf'], writes=[pk])
                P.op('dve', lambda e: e.tensor_tensor(out=xn[:, half * 512:(half + 1) * 512], in0=pz[:, 0:512],
                                                      in1=gate_bc[:, half * 512:(half + 1) * 512], op=ALU.mult),
                     reads=[pk, 'gate_bc'], writes=['xn'])
                P.op('pool', lambda e: e.tensor_tensor(out=xt[i][:, half * 512:(half + 1) * 512],
                                                       in0=xt[i][:, half * 512:(half + 1) * 512],
                                                       in1=xn[:, half * 512:(half + 1) * 512], op=ALU.add),
                     reads=['xn', 'xt%d' % i], writes=['xt%d' % i])
            if final:
                P.op('act', lambda e: e.activation(out=junk, in_=xt[i], func=AF.Square, accum_out=st4[:, 0:1]),
                     reads=['xt%d' % i], writes=['junk', 'st4'])
                P.op('dve', lambda e: e.tensor_scalar(out=st4[:, 1:2], in0=st4[:, 0:1], scalar1=1.0 / D, scalar2=1e-6,
                                                      op0=ALU.mult, op1=ALU.add), reads=['st4'], writes=['st4'])
                P.op('act', lambda e: e.activation(out=st4[:, 2:3], in_=st4[:, 1:2], func=AF.Sqrt), reads=['st4'], writes=['st4'])
                P.op('dve', lambda e: e.reciprocal(out=st4[:, 3:4], in_=st4[:, 2:3]), reads=['st4'], writes=['st4'])
                P.op('dve', lambda e: e.scalar_tensor_tensor(out=xt[i], in0=xt[i], scalar=st4[:, 3:4], in1=fng_holder['t'][:],
                                                             op0=ALU.mult, op1=ALU.mult),
                     reads=['xt%d' % i, 'st4', 'fng_bc'], writes=['xt%d' % i])
            P.dma(xdst[off + tt * 128: off + (tt + 1) * 128, :], xt[i], reads=['xt%d' % i], writes=[(dkey, off // 128 + tt)], q='pool')

    L0 = contextlib.ExitStack()

    def sb0(name, shape, dt=F32):
        return L0.enter_context(nc.sbuf_tensor(name, list(shape), dt))

    wbf = sb0("wbf", [128, 8, 384], BF16)
    TB = 256
    Fq, Fsz, Fvr, For = FT[0][:, 0:TS], FT[1][:, 0:TS], FT[2][:, 0:TS], FT[3][:, 0:TS]
    Fv = Fvr.rearrange("p (j c) -> p j c", c=128)
    Fo = For.rearrange("p (j c) -> p j c", c=128)
    BT = [sb0("BT%d" % i, [128, 256]) for i in range(18)]
    bt = {n_: BT[i] for i, n_ in enumerate(['sg', 'lf', 'kg', 'G', 'br', 'E', 'Ei', 'qt', 'kt', 'kh', 'vT'])}
    khtok = sb0("khtok", [128, TB // 128, 128])
    gam = sb0("gam", [128, TB // 32])
    gref = sb0("gref", [128, TB // 32])
    Sst = [sb0("Sst%d" % i, [128, 128]) for i in range(2)]
    attT = sb0("attT", [128, 128])
    ostat = sb0("ostat", [128, TS // 128, 4])
    mH = sb0("mH", [128, 2, 128])
    P.dma(mH[:], maskH.rearrange("d s t -> s d t"), writes=['mH'])

    def hgrn_head(h, off, T, is_sample, pidx):
        load_w(wA[h][:, 0:384], 384)
        tb = min(TB, T)
        nblk = T // tb
        for b in range(nblk):
            t0 = b * tb
            proj(0, 128, t0, tb, lambda pz, pk: P.op(
                'act', lambda e: e.copy(out=Fq[:, t0:t0 + tb], in_=pz[:, 0:tb]), reads=[pk], writes=['Fq']))
            proj(256, 128, t0, tb, lambda pz, pk: P.op(
                'act', lambda e: e.activation(out=Fsz[:, t0:t0 + tb], in_=pz[:, 0:tb], func=AF.Silu), reads=[pk], writes=['Fsz']))
            proj(128, 128, t0, tb, lambda pz, pk: P.op(
                'dve', lambda e: e.tensor_copy(out=bt['vT'][:, 0:tb], in_=pz[:, 0:tb]), reads=[pk], writes=['b_vT']))
            pz, pk = nps()
            for j in range(tb // 128):
                P.op('pe', lambda e: e.transpose(out=pz[:, j * 128:(j + 1) * 128], in_=bt['vT'][:, j * 128:(j + 1) * 128],
                                                 identity=ident[:]), reads=['b_vT', 'ident'], writes=[pk])
            P.op('dve', lambda e: e.tensor_copy(out=Fv[:, t0 // 128:(t0 + tb) // 128, :],
                                                in_=pz[:, 0:tb].rearrange("p (j c) -> p j c", c=128)),
                 reads=[pk], writes=['Fv'])
        load_w(wA[h][:, 384:640], 256)
        for d in range(2):
            rev = (d == 1)
            cur = 0
            if is_sample:
                P.dma(Sst[0][:], s_hgrn[d, h], writes=['Sst0'])
            else:
                P.op('pool', lambda e: e.memset(Sst[0][:], 0.0), writes=['Sst0'])
            blks = list(range(nblk))
            if rev:
                blks = blks[::-1]
            for b in blks:
                t0 = b * tb
                nch = tb // 32
                sg, lf, kg, G, br, E, Ei, qt, kt, kh = [bt[n_] for n_ in ['sg', 'lf', 'kg', 'G', 'br', 'E', 'Ei', 'qt', 'kt', 'kh']]
                proj(128 * d, 128, t0, tb, lambda pz, pk: P.op(
                    'act', lambda e: e.activation(out=sg[:, 0:tb], in_=pz[:, 0:tb], func=AF.Sigmoid), reads=[pk], writes=['b_sg']))
                P.op('dve', lambda e: e.tensor_scalar(out=sg[:, 0:tb], in0=sg[:, 0:tb], scalar1=oml[:, h:h + 1],
                                                      scalar2=lbv[:, h:h + 1], op0=ALU.mult, op1=ALU.add),
                     reads=['b_sg', 'oml', 'lbv'], writes=['b_sg'])
                P.op('act', lambda e: e.activation(out=lf[:, 0:tb], in_=sg[:, 0:tb], func=AF.Ln), reads=['b_sg'], writes=['b_lf'])
                P.op('pool', lambda e: e.tensor_scalar(out=kg[:, 0:tb], in0=sg[:, 0:tb], scalar1=-1.0, scalar2=1.0,
                                                       op0=ALU.mult, op1=ALU.add), reads=['b_sg'], writes=['b_kg'])
                P.op('dve', lambda e: e.memset(E[:, 0:tb], 0.0), writes=['b_E'])
                if not rev:
                    P.op('dve', lambda e: e.tensor_tensor_scan(out=G[:, 0:tb], data0=lf[:, 0:tb], data1=E[:, 0:tb],
                                                               initial=0.0, op0=ALU.add, op1=ALU.add),
                         reads=['b_lf', 'b_E'], writes=['b_G'])
                    ci_ = 0
                else:
                    P.op('dve', lambda e: e.tensor_tensor_scan(out=G[:, 0:tb][:, ::-1], data0=lf[:, 0:tb][:, ::-1],
                                                               data1=E[:, 0:tb], initial=0.0, op0=ALU.add, op1=ALU.add),
                         reads=['b_lf', 'b_E'], writes=['b_G'])
                    ci_ = 31
                G3 = G[:, 0:tb].rearrange("p (c l) -> p c l", l=32)
                lf3 = lf[:, 0:tb].rearrange("p (c l) -> p c l", l=32)
                P.op('dve', lambda e: e.tensor_tensor(out=gref[:, 0:nch], in0=G3[:, :, ci_], in1=lf3[:, :, ci_], op=ALU.subtract),
                     reads=['b_G', 'b_lf'], writes=['gref'])
                P.op('dve', lambda e: e.tensor_tensor(out=br[:, 0:tb].rearrange("p (c l) -> p c l", l=32), in0=G3,
                                                      in1=gref[:, 0:nch].unsqueeze(2).to_broadcast([128, nch, 32]), op=ALU.subtract),
                     reads=['b_G', 'gref'], writes=['b_br'])
                bend = br[:, 0:tb].rearrange("p (c l) -> p c l", l=32)[:, :, (0 if rev else 31)]
                P.op('act', lambda e: e.activation(out=gam[:, 0:nch], in_=bend, func=AF.Exp), reads=['b_br'], writes=['gam'])
                P.op('act', lambda e: e.activation(out=E[:, 0:tb], in_=br[:, 0:tb], func=AF.Exp), reads=['b_br'], writes=['b_E'])
                P.op('act', lambda e: e.activation(out=Ei[:, 0:tb], in_=br[:, 0:tb], func=AF.Exp, scale=-1.0),
                     reads=['b_br'], writes=['b_Ei'])
                P.op('dve', lambda e: e.tensor_tensor(out=qt[:, 0:tb], in0=Fq[:, t0:t0 + tb], in1=E[:, 0:tb], op=ALU.mult),
                     reads=['Fq', 'b_E'], writes=['b_qt'])
                P.op('pool', lambda e: e.tensor_tensor(out=kt[:, 0:tb], in0=kg[:, 0:tb], in1=Ei[:, 0:tb], op=ALU.mult),
                     reads=['b_kg', 'b_Ei'], writes=['b_kt'])
                P.op('dve', lambda e: e.tensor_tensor(out=kh[:, 0:tb].rearrange("p (c l) -> p c l", l=32),
                                                      in0=kt[:, 0:tb].rearrange("p (c l) -> p c l", l=32),
                                                      in1=gam[:, 0:nch].unsqueeze(2).to_broadcast([128, nch, 32]), op=ALU.mult),
                     reads=['b_kt', 'gam'], writes=['b_kh'])
                if debug and debug.get('inner') and h == head_ids[0]:
                    for n_ in ['lf', 'kg', 'br', 'E', 'qt', 'kt', 'kh']:
                        dump("%s_d%d" % (n_, d), bt[n_][:, 0:tb], ['b_' + n_], tb, col0=off + t0)
                pz, pk = nps()
                for j in range(tb // 128):
                    P.op('pe', lambda e: e.transpose(out=pz[:, j * 128:(j + 1) * 128], in_=kh[:, j * 128:(j + 1) * 128],
                                                     identity=ident[:]), reads=['b_kh', 'ident'], writes=[pk])
                P.op('act', lambda e: e.copy(out=khtok[:, 0:tb // 128, :], in_=pz[:, 0:tb].rearrange("p (j c) -> p j c", c=128)),
                     reads=[pk], writes=['khtok'])
                tiles = list(range(tb // 128))
                if rev:
                    tiles = tiles[::-1]
                for j in tiles:
                    tg = t0 // 128 + j
                    pa, pak = nps()
                    P.op('pe', lambda e: e.matmul(pa[:, 0:128], kt[:, j * 128:(j + 1) * 128], qt[:, j * 128:(j + 1) * 128],
                                                  start=True, stop=True), reads=['b_kt', 'b_qt'], writes=[pak])
                    P.op('dve', lambda e: e.tensor_tensor(out=attT[:], in0=pa[:, 0:128], in1=mH[:, d, :], op=ALU.mult),
                         reads=[pak, 'mH'], writes=['attT'])
                    po, pok = nps()
                    P.op('pe', lambda e: e.matmul(po[:, 0:128], attT[:], Fv[:, tg, :], start=True, stop=False),
                         reads=['attT', 'Fv'], writes=[pok])
                    chs = [0, 1, 2, 3]
                    if rev:
                        chs = chs[::-1]
                    for ci, c in enumerate(chs):
                        Scur = Sst[cur]
                        Snew = Sst[1 - cur]
                        P.op('pe', lambda e: e.matmul(
                            po[32 * c:32 * c + 32, 0:128], qt[:, j * 128 + 32 * c: j * 128 + 32 * c + 32], Scur[:],
                            start=False, stop=(ci == 3), tile_position=(0, 32 * c)),
                            reads=['b_qt', 'Sst%d' % cur], writes=[pok])
                        pd, pdk = nps()
                        P.op('pe', lambda e: e.matmul(
                            pd[:, 0:128], khtok[32 * c:32 * c + 32, j, :], Fv[32 * c:32 * c + 32, tg, :],
                            start=True, stop=True, tile_position=(32 * c, 0)),
                            reads=['khtok', 'Fv'], writes=[pdk])
                        gidx = j * 4 + c
                        P.op('dve', lambda e: e.scalar_tensor_tensor(
                            out=Snew[:], in0=Scur[:], scalar=gam[:, gidx:gidx + 1], in1=pd[:, 0:128],
                            op0=ALU.mult, op1=ALU.add),
                            reads=['Sst%d' % cur, 'gam', pdk], writes=['Sst%d' % (1 - cur)])
                        cur = 1 - cur
                    if d == 0:
                        P.op('act', lambda e: e.copy(out=Fo[:, tg, :], in_=po[:, 0:128]), reads=[pok], writes=[('Fo', tg)])
                    else:
                        P.op('dve', lambda e: e.tensor_tensor(out=Fo[:, tg, :], in0=Fo[:, tg, :], in1=po[:, 0:128], op=ALU.add),
                             reads=[pok, ('Fo', tg)], writes=[('Fo', tg)])
            if not is_sample:
                P.dma(o_hgrn[pidx, d, h], Sst[cur][:], reads=['Sst%d' % cur], q='pool')
        for tg in range(T // 128):
            P.op('act', lambda e: e.activation(out=attT[:], in_=Fo[:, tg, :], func=AF.Square, accum_out=ostat[:, tg, 0:1]),
                 reads=[('Fo', tg)], writes=['attT', ('ostat', tg)])
            P.op('dve', lambda e: e.tensor_scalar(out=ostat[:, tg, 1:2], in0=ostat[:, tg, 0:1], scalar1=1.0 / 128,
                                                  scalar2=1e-6, op0=ALU.mult, op1=ALU.add),
                 reads=[('ostat', tg)], writes=[('ostat', tg)])
            P.op('act', lambda e: e.activation(out=ostat[:, tg, 2:3], in_=ostat[:, tg, 1:2], func=AF.Sqrt),
                 reads=[('ostat', tg)], writes=[('ostat', tg)])
            P.op('dve', lambda e: e.reciprocal(out=ostat[:, tg, 3:4], in_=ostat[:, tg, 2:3]),
                 reads=[('ostat', tg)], writes=[('ostat', tg)])
            P.op('dve', lambda e: e.tensor_scalar(out=Fo[:, tg, :], in0=Fo[:, tg, :], scalar1=ostat[:, tg, 3:4],
                                                  scalar2=None, op0=ALU.mult),
                 reads=[('Fo', tg), ('ostat', tg)], writes=[('Fo', tg)])
        n4 = min(4, T // 128)
        for g4 in range(T // (128 * n4)):
            pz, pk = nps()
            for j in range(n4):
                tg = g4 * n4 + j
                P.op('pe', lambda e: e.transpose(out=pz[:, j * 128:(j + 1) * 128], in_=Fo[:, tg, :], identity=ident[:]),
                     reads=[('Fo', tg), 'ident'], writes=[pk])
            w_ = n4 * 128
            P.op('dve', lambda e: e.scalar_tensor_tensor(
                out=yT[:, h, g4 * w_:(g4 + 1) * w_], in0=pz[:, 0:w_], scalar=hgg_sb[:, h:h + 1],
                in1=Fsz[:, g4 * w_:(g4 + 1) * w_], op0=ALU.mult, op1=ALU.mult),
                reads=[pk, 'hgg', 'Fsz'], writes=[('yT', h)])

    TR = 256
    LWC = -0.6065306597126334
    LR = [sb0("LR%d" % g, [64, TS], BF16) for g in range(4)]
    rb = {n_: BT[i][0:64, :] for i, n_ in enumerate(
          ['lw', 'a', 'kk', 'kq', 'kap', 'kd', 'b', 'rk', 'G', 'br', 'E', 'Ei', 'Em', 'bh', 'kh', 'Kb', 'Bb', 't1'])}
    KR = sb0("r_KR", [64, 2, TR])
    cset = [{n_: sb0("c%d_%s" % (i_, n_), [64, (128 if n_ in ('AB', 'BB') else 64)])
             for n_ in ['AB', 'BB', 'XT0', 'XT1', 'X1', 'Xw', 'Pm0', 'Pm1', 'Vt', 'Kt', 'Bt']} for i_ in range(4)]
    rsq = {n_: sb0("rq_" + n_, [64, 64]) for n_ in ['U', 'Z0', 'Z1', 'zt']}
    rgam = sb0("rgam", [64, 8])
    rgref = sb0("rgref", [64, 4])
    mR = sb0("mR", [64, 2, 3, 128])
    P.dma(mR[:], maskR.rearrange("d m s t -> s d m t"), writes=['mR'])
    prm = {}
    for n_, src_, shp in [('mu_rkv', mu_rkv, [64, 2, 4, 16]), ('mu_lr', mu_lr, [64, 2, 4]), ('w0', w0T, [64, 2, 16]),
                          ('a0', a0T, [64, 2, 16]), ('kk', kkT, [64, 16]), ('ka', kaT, [64, 16]), ('rk', rkT, [64, 16]),
                          ('gng', gngT, [64, 16]), ('gnb', gnbT, [64, 16])]:
        prm[n_] = sb0("p_" + n_, shp)
        P.dma(prm[n_][:], src_[:], writes=['p_' + n_])
    c0_rkv = sb0("c0_rkv", [64, 4, 16])
    c0_lr = sb0("c0_lr", [64, 4])
    omka = sb0("omka", [64, 16])
    P.op('dve', lambda e: e.tensor_tensor(out=c0_rkv[:], in0=prm['mu_rkv'][:, 0], in1=prm['mu_rkv'][:, 1], op=ALU.add),
         reads=['p_mu_rkv'], writes=['c0_rkv'])
    P.op('dve', lambda e: e.tensor_scalar(out=c0_rkv[:], in0=c0_rkv[:], scalar1=-1.0, scalar2=1.0, op0=ALU.mult, op1=ALU.add),
         reads=['c0_rkv'], writes=['c0_rkv'])
    P.op('dve', lambda e: e.tensor_tensor(out=c0_lr[:], in0=prm['mu_lr'][:, 0], in1=prm['mu_lr'][:, 1], op=ALU.add),
         reads=['p_mu_lr'], writes=['c0_lr'])
    P.op('dve', lambda e: e.tensor_scalar(out=c0_lr[:], in0=c0_lr[:], scalar1=-1.0, scalar2=1.0, op0=ALU.mult, op1=ALU.add),
         reads=['c0_lr'], writes=['c0_lr'])
    P.op('dve', lambda e: e.tensor_scalar(out=omka[:], in0=prm['ka'][:], scalar1=-1.0, scalar2=1.0, op0=ALU.mult, op1=ALU.add),
         reads=['p_ka'], writes=['omka'])
    w2a2 = sb0("w2a2", [64, 4, D], BF16)
    for g, src_ in enumerate([w2[0], w2[1], a2[0], a2[1]]):
        for c0 in range(0, D, 256):
            P.dma(wst[0:64, 0, 0:256], src_[:, c0:c0 + 256], writes=['wst'])
            P.op('pool', lambda e: e.tensor_copy(out=w2a2[:, g, c0:c0 + 256], in_=wst[0:64, 0, 0:256]), reads=['wst'], writes=['w2a2'])

    def shift_into(dst, dkey, raw, rkey, T, c0ap, m0ap, m1ap, t1tile, eng='dve'):
        for s0 in range(0, T, 512):
            n = min(512, T - s0)
            P.op(eng, lambda e: e.tensor_scalar(out=t1tile[:, 0:n], in0=raw[:, 16 + s0:16 + s0 + n], scalar1=c0ap, scalar2=None,
                                                op0=ALU.mult), reads=[rkey], writes=['shift_t'])
            P.op('dve', lambda e: e.scalar_tensor_tensor(out=t1tile[:, 0:n], in0=raw[:, 15 + s0:15 + s0 + n], scalar=m0ap,
                                                       in1=t1tile[:, 0:n], op0=ALU.mult, op1=ALU.add),
                 reads=[rkey, 'shift_t'], writes=['shift_t'])
            P.op('dve', lambda e: e.scalar_tensor_tensor(out=dst[:, s0:s0 + n], in0=raw[:, 17 + s0:17 + s0 + n], scalar=m1ap,
                                                       in1=t1tile[:, 0:n], op0=ALU.mult, op1=ALU.add),
                 reads=[rkey, 'shift_t'], writes=[dkey])

    shiftt = sb0("shiftt", [64, 512])

    def rwkv_seq_setup(off, T):
        load_w(wLR, 256)
        pb = min(512, T)
        for g in range(4):
            raw = FT[g]
            P.op('pool', lambda e: e.memset(raw[0:64, 15:16], 0.0), writes=['FT%d' % g])
            P.op('pool', lambda e: e.memset(raw[0:64, T + 16:T + 17], 0.0), writes=['FT%d' % g])
            for b in range(T // pb):
                t0 = b * pb
                proj(64 * g, 64, t0, pb, lambda pz, pk: P.op(
                    'act', lambda e: e.copy(out=raw[0:64, 16 + t0:16 + t0 + pb], in_=pz[0:64, 0:pb]), reads=[pk], writes=['FT%d' % g]))
            shift_into(FT[4][0:64, :], 'FT4', raw[0:64, :], 'FT%d' % g, T, c0_lr[:, g:g + 1], prm['mu_lr'][:, 0, g:g + 1],
                       prm['mu_lr'][:, 1, g:g + 1], shiftt)
            if g < 2:
                P.op('act', lambda e: e.activation(out=LR[g][:, 0:T], in_=FT[4][0:64, 0:T], func=AF.Tanh), reads=['FT4'], writes=['LR%d' % g])
            else:
                P.op('act', lambda e: e.copy(out=LR[g][:, 0:T], in_=FT[4][0:64, 0:T]), reads=['FT4'], writes=['LR%d' % g])

    def rwkv_head(h, slot, off, T, is_sample, pidx):
        P.barrier()
        load_w(wB[h], 256)
        pb = min(512, T)
        nchT = T // 64
        for g in range(3):
            raw = FT[g]
            P.op('pool', lambda e: e.memset(raw[0:64, 15:16], 0.0), writes=['FT%d' % g])
            P.op('pool', lambda e: e.memset(raw[0:64, T + 16:T + 17], 0.0), writes=['FT%d' % g])
            for b in range(T // pb):
                t0 = b * pb
                proj(64 * g, 64, t0, pb, lambda pz, pk: P.op(
                    'act', lambda e: e.copy(out=raw[0:64, 16 + t0:16 + t0 + pb], in_=pz[0:64, 0:pb]), reads=[pk], writes=['FT%d' % g]))
            shift_into(FT[3 + g][0:64, :], 'FT%d' % (3 + g), raw[0:64, :], 'FT%d' % g, T, c0_rkv[:, g, h:h + 1],
                       prm['mu_rkv'][:, 0, g, h:h + 1], prm['mu_rkv'][:, 1, g, h:h + 1], shiftt, eng=('dve' if g != 1 else 'pool'))
        rS, kS, vS = FT[3][0:64, :], FT[4][0:64, :], FT[5][0:64, :]
        szb, yaccr, bonus = FT[0][0:64, :], FT[1][0:64, 0:T], FT[2][0:64, :]
        yacc = yaccr.rearrange("p (c v) -> p c v", v=64)
        for b in range(T // pb):
            t0 = b * pb
            proj(192, 64, t0, pb, lambda pz, pk: P.op(
                'act', lambda e: e.activation(out=szb[:, t0:t0 + pb], in_=pz[0:64, 0:pb], func=AF.Silu), reads=[pk], writes=['FT0']))
        tb = min(TR, T)
        nblk = T // tb
        P.barrier()
        for d in range(2):
            rev = (d == 1)
            cur = 0
            Zt = [rsq['Z0'], rsq['Z1']]
            if is_sample:
                P.dma(rsq['zt'][:], s_rwkv[d, h], writes=['rq_zt'])
                pz, pk = nps()
                P.op('pe', lambda e: e.transpose(out=pz[0:64, 0:64], in_=rsq['zt'][:], identity=ident[0:64, 0:64]),
                     reads=['rq_zt', 'ident'], writes=[pk])
                P.op('act', lambda e: e.copy(out=Zt[0][:], in_=pz[0:64, 0:64]), reads=[pk], writes=['rq_Z0'])
            else:
                P.op('pool', lambda e: e.memset(Zt[0][:], 0.0), writes=['rq_Z0'])
            blks = list(range(nblk))
            if rev:
                blks = blks[::-1]
            for b in blks:
                t0 = b * tb
                sl = slice(t0, t0 + tb)
                nch = tb // 64
                R_ = rb
                pz, pk = nps()
                P.op('pe', lambda e: e.matmul(pz[0:64, 0:tb], w2a2[:, d, h * 64:(h + 1) * 64], LR[d][:, sl], start=True, stop=True),
                     reads=['w2a2', 'LR%d' % d], writes=[pk])
                P.op('act', lambda e: e.activation(out=R_['lw'][:, 0:tb], in_=pz[0:64, 0:tb], func=AF.Sigmoid,
                                                   bias=prm['w0'][:, d, h:h + 1], scale=1.0), reads=[pk, 'p_w0'], writes=['r_lw'])
                pz, pk = nps()
                P.op('pe', lambda e: e.matmul(pz[0:64, 0:tb], w2a2[:, 2 + d, h * 64:(h + 1) * 64], LR[2 + d][:, sl], start=True, stop=True),
                     reads=['w2a2', 'LR%d' % (2 + d)], writes=[pk])
                P.op('act', lambda e: e.activation(out=R_['a'][:, 0:tb], in_=pz[0:64, 0:tb], func=AF.Sigmoid,
                                                   bias=prm['a0'][:, d, h:h + 1], scale=1.0), reads=[pk, 'p_a0'], writes=['r_a'])
                P.op('dve', lambda e: e.tensor_scalar(out=R_['kk'][:, 0:tb], in0=kS[:, sl], scalar1=prm['kk'][:, h:h + 1],
                                                      scalar2=None, op0=ALU.mult), reads=['FT4', 'p_kk'], writes=['r_kk'])
                P.op('dve', lambda e: e.tensor_tensor(out=R_['kq'][:, 0:tb], in0=R_['kk'][:, 0:tb], in1=R_['kk'][:, 0:tb], op=ALU.mult),
                     reads=['r_kk'], writes=['r_kq'])
                pz, pk = nps()
                P.op('pe', lambda e: e.matmul(pz[0:64, 0:tb], ones[0:64, 0:64], R_['kq'][:, 0:tb], start=True, stop=True),
                     reads=['ones', 'r_kq'], writes=[pk])
                P.op('dve', lambda e: e.tensor_scalar(out=R_['kq'][:, 0:tb], in0=pz[0:64, 0:tb], scalar1=1e-24, scalar2=None,
                                                      op0=ALU.max), reads=[pk], writes=['r_kq'])
                P.op('act', lambda e: e.activation(out=R_['kq'][:, 0:tb], in_=R_['kq'][:, 0:tb], func=AF.Ln), reads=['r_kq'], writes=['r_kq'])
                P.op('act', lambda e: e.activation(out=R_['kq'][:, 0:tb], in_=R_['kq'][:, 0:tb], func=AF.Exp, scale=-0.5),
                     reads=['r_kq'], writes=['r_kq'])
                P.op('dve', lambda e: e.tensor_tensor(out=R_['kap'][:, 0:tb], in0=R_['kk'][:, 0:tb], in1=R_['kq'][:, 0:tb], op=ALU.mult),
                     reads=['r_kk', 'r_kq'], writes=['r_kap'])
                P.op('pool', lambda e: e.tensor_scalar(out=R_['t1'][:, 0:tb], in0=R_['a'][:, 0:tb], scalar1=prm['ka'][:, h:h + 1],
                                                       scalar2=omka[:, h:h + 1], op0=ALU.mult, op1=ALU.add),
                     reads=['r_a', 'p_ka', 'omka'], writes=['r_t1'])
                P.op('pool', lambda e: e.tensor_tensor(out=R_['kd'][:, 0:tb], in0=kS[:, sl], in1=R_['t1'][:, 0:tb], op=ALU.mult),
                     reads=['FT4', 'r_t1'], writes=['r_kd'])
                P.op('dve', lambda e: e.tensor_tensor(out=R_['b'][:, 0:tb], in0=R_['a'][:, 0:tb], in1=R_['kap'][:, 0:tb], op=ALU.mult),
                     reads=['r_a', 'r_kap'], writes=['r_b'])
                P.op('dve', lambda e: e.scalar_tensor_tensor(out=R_['rk'][:, 0:tb], in0=rS[:, sl], scalar=prm['rk'][:, h:h + 1],
                                                             in1=R_['kd'][:, 0:tb], op0=ALU.mult, op1=ALU.mult),
                     reads=['FT3', 'p_rk', 'r_kd'], writes=['r_rk'])
                pz, pk = nps()
                P.op('pe', lambda e: e.matmul(pz[0:64, 0:tb], ones[0:64, 0:64], R_['rk'][:, 0:tb], start=True, stop=True),
                     reads=['ones', 'r_rk'], writes=[pk])
                if d == 0:
                    P.op('dve', lambda e: e.tensor_tensor(out=bonus[:, sl], in0=pz[0:64, 0:tb], in1=vS[:, sl], op=ALU.mult),
                         reads=[pk, 'FT5'], writes=[('bonus', b)])
                else:
                    P.op('dve', lambda e: e.tensor_tensor(out=R_['rk'][:, 0:tb], in0=pz[0:64, 0:tb], in1=vS[:, sl], op=ALU.mult),
                         reads=[pk, 'FT5'], writes=['r_rk'])
                    P.op('pool', lambda e: e.tensor_tensor(out=bonus[:, sl], in0=bonus[:, sl], in1=R_['rk'][:, 0:tb], op=ALU.add),
                         reads=['r_rk', ('bonus', b)], writes=[('bonus', b)])
                G, br, E, Ei, Em = R_['G'], R_['br'], R_['E'], R_['Ei'], R_['Em']
                P.op('dve', lambda e: e.memset(E[:, 0:tb], 0.0), writes=['r_E'])
                if not rev:
                    P.op('dve', lambda e: e.tensor_tensor_scan(out=G[:, 0:tb], data0=R_['lw'][:, 0:tb], data1=E[:, 0:tb],
                                                               initial=0.0, op0=ALU.add, op1=ALU.add),
                         reads=['r_lw', 'r_E'], writes=['r_G'])
                    ci_ = 0
                else:
                    P.op('dve', lambda e: e.tensor_tensor_scan(out=G[:, 0:tb][:, ::-1], data0=R_['lw'][:, 0:tb][:, ::-1],
                                                               data1=E[:, 0:tb], initial=0.0, op0=ALU.add, op1=ALU.add),
                         reads=['r_lw', 'r_E'], writes=['r_G'])
                    ci_ = 63
                G3 = G[:, 0:tb].rearrange("p (c l) -> p c l", l=64)
                lw3 = R_['lw'][:, 0:tb].rearrange("p (c l) -> p c l", l=64)
                P.op('dve', lambda e: e.tensor_tensor(out=rgref[:, 0:nch], in0=G3[:, :, ci_], in1=lw3[:, :, ci_], op=ALU.subtract),
                     reads=['r_G', 'r_lw'], writes=['rgref'])
                P.op('dve', lambda e: e.tensor_tensor(out=br[:, 0:tb].rearrange("p (c l) -> p c l", l=64), in0=G3,
                                                      in1=rgref[:, 0:nch].unsqueeze(2).to_broadcast([64, nch, 64]), op=ALU.subtract),
                     reads=['r_G', 'rgref'], writes=['r_br'])
                bend = br[:, 0:tb].rearrange("p (c l) -> p c l", l=64)[:, :, (0 if rev else 63)]
                P.op('act', lambda e: e.activation(out=rgam[:, 0:nch], in_=bend, func=AF.Exp, scale=LWC), reads=['r_br'], writes=['rgam'])
                P.op('pool', lambda e: e.tensor_scalar(out=rgam[:, 4:4 + nch], in0=rgam[:, 0:nch], scalar1=-1.0, scalar2=None,
                                                       op0=ALU.mult), reads=['rgam'], writes=['rgam'])
                P.op('act', lambda e: e.activation(out=E[:, 0:tb], in_=br[:, 0:tb], func=AF.Exp, scale=LWC), reads=['r_br'], writes=['r_E'])
                P.op('act', lambda e: e.activation(out=Ei[:, 0:tb], in_=br[:, 0:tb], func=AF.Exp, scale=-LWC),
                     reads=['r_br'], writes=['r_Ei'])
                P.op('pool', lambda e: e.tensor_tensor(out=R_['t1'][:, 0:tb], in0=br[:, 0:tb], in1=R_['lw'][:, 0:tb], op=ALU.subtract),
                     reads=['r_br', 'r_lw'], writes=['r_t1'])
                P.op('act', lambda e: e.activation(out=Em[:, 0:tb], in_=R_['t1'][:, 0:tb], func=AF.Exp, scale=LWC), reads=['r_t1'], writes=['r_Em'])
                P.op('dve', lambda e: e.tensor_tensor(out=KR[:, 0, 0:tb], in0=R_['kap'][:, 0:tb], in1=Em[:, 0:tb], op=ALU.mult),
                     reads=['r_kap', 'r_Em'], writes=['r_KR'])
                P.op('pool', lambda e: e.tensor_tensor(out=KR[:, 1, 0:tb], in0=rS[:, sl], in1=E[:, 0:tb], op=ALU.mult),
                     reads=['FT3', 'r_E', 'r_KR'], writes=['r_KR'])
                P.op('dve', lambda e: e.tensor_tensor(out=R_['bh'][:, 0:tb], in0=R_['b'][:, 0:tb], in1=Ei[:, 0:tb], op=ALU.mult),
                     reads=['r_b', 'r_Ei'], writes=['r_bh'])
                P.op('pool', lambda e: e.tensor_tensor(out=R_['kh'][:, 0:tb], in0=R_['kd'][:, 0:tb], in1=Ei[:, 0:tb], op=ALU.mult),
                     reads=['r_kd', 'r_Ei'], writes=['r_kh'])
                P.op('dve', lambda e: e.tensor_tensor(out=R_['Kb'][:, 0:tb].rearrange("p (c l) -> p c l", l=64),
                                                      in0=R_['kh'][:, 0:tb].rearrange("p (c l) -> p c l", l=64),
                                                      in1=rgam[:, 0:nch].unsqueeze(2).to_broadcast([64, nch, 64]), op=ALU.mult),
                     reads=['r_kh', 'rgam'], writes=['r_Kb'])
                P.op('dve', lambda e: e.tensor_tensor(out=R_['Bb'][:, 0:tb].rearrange("p (c l) -> p c l", l=64),
                                                      in0=R_['bh'][:, 0:tb].rearrange("p (c l) -> p c l", l=64),
                                                      in1=rgam[:, 4:4 + nch].unsqueeze(2).to_broadcast([64, nch, 64]), op=ALU.mult),
                     reads=['r_bh', 'rgam'], writes=['r_Bb'])
                chs = list(range(nch))
                if rev:
                    chs = chs[::-1]
                st = {}
                for c in chs:
                    cs = slice(c * 64, (c + 1) * 64)
                    C_ = cset[c]
                    ck = (lambda n_, c=c: 'c%d_%s' % (c, n_))
                    pA, pAk = nps()
                    P.op('pe', lambda e: e.matmul(pA[0:64, 0:128], R_['bh'][:, cs], KR[:, :, cs], start=True, stop=True),
                         reads=['r_bh', 'r_KR'], writes=[pAk])
                    P.op('pe', lambda e: e.matmul(pA[0:64, 128:256], R_['kh'][:, cs], KR[:, :, cs], start=True, stop=True),
                         reads=['r_kh', 'r_KR'], writes=[pAk])
                    P.op('pe', lambda e: e.matmul(pA[0:64, 256:320], KR[:, 0, cs], R_['bh'][:, cs], start=True, stop=True),
                         reads=['r_bh', 'r_KR'], writes=[pAk])
                    AB, BB = C_['AB'], C_['BB']
                    P.op('dve', lambda e: e.tensor_tensor(out=AB[:], in0=pA[0:64, 0:128], in1=mR[:, d, 0, :], op=ALU.mult),
                         reads=[pAk, 'mR'], writes=[ck('AB')])
                    P.op('dve', lambda e: e.tensor_tensor(out=BB[:], in0=pA[0:64, 128:256], in1=mR[:, d, 1, :], op=ALU.mult),
                         reads=[pAk, 'mR'], writes=[ck('BB')])
                    P.op('dve', lambda e: e.tensor_tensor(out=C_['XT0'][:], in0=pA[0:64, 256:320], in1=mR[:, d, 2, 0:64], op=ALU.mult),
                         reads=[pAk, 'mR'], writes=[ck('XT0')])
                    P.op('dve', lambda e: e.tensor_tensor(out=C_['Pm0'][:], in0=AB[:, 0:64], in1=ident[0:64, 0:64], op=ALU.add),
                         reads=[ck('AB'), 'ident'], writes=[ck('Pm0')])
                    st[c] = dict(X=AB[:, 0:64], Xk=ck('AB'), XT=C_['XT0'], XTk=ck('XT0'), xti=0, pmi=0)
                for lev in range(5):
                    for c in chs:
                        C_ = cset[c]
                        s_ = st[c]
                        ck = (lambda n_, c=c: 'c%d_%s' % (c, n_))
                        X, Xk, XT, XTk = s_['X'], s_['Xk'], s_['XT'], s_['XTk']
                        pq, pqk = nps()
                        nXTn = 'XT1' if s_['xti'] == 0 else 'XT0'
                        nXT, nXTk = C_[nXTn], ck(nXTn)
                        P.op('pe', lambda e: e.matmul(pq[0:64, 64:128], X, XT[:], start=True, stop=True), reads=[Xk, XTk], writes=[pqk])
                        if lev < 4:
                            P.op('pe', lambda e: e.matmul(pq[0:64, 0:64], XT[:], X, start=True, stop=True), reads=[Xk, XTk], writes=[pqk])
                        P.op('act', lambda e: e.copy(out=nXT[:], in_=pq[0:64, 64:128]), reads=[pqk], writes=[nXTk])
                        if lev < 4:
                            tn = 'X1' if lev % 2 == 0 else 'Xw'
                            P.op('act', lambda e: e.copy(out=C_[tn][:], in_=pq[0:64, 0:64]), reads=[pqk], writes=[ck(tn)])
                            s_['X'], s_['Xk'] = C_[tn][:], ck(tn)
                        s_['XT'], s_['XTk'], s_['xti'] = nXT, nXTk, 1 - s_['xti']
                    for c in chs:
                        C_ = cset[c]
                        s_ = st[c]
                        ck = (lambda n_, c=c: 'c%d_%s' % (c, n_))
                        nXT, nXTk = s_['XT'], s_['XTk']
                        pmi = s_['pmi']
                        Pc, Pn = C_['Pm%d' % pmi], C_['Pm%d' % (1 - pmi)]
                        pp, ppk = nps()
                        P.op('pe', lambda e: e.matmul(pp[0:64, 0:64], nXT[:], Pc[:], start=True, stop=True),
                             reads=[nXTk, ck('Pm%d' % pmi)], writes=[ppk])
                        P.op('dve', lambda e: e.tensor_tensor(out=Pn[:], in0=pp[0:64, 0:64], in1=Pc[:], op=ALU.add),
                             reads=[ppk, ck('Pm%d' % pmi)], writes=[ck('Pm%d' % (1 - pmi))])
                        s_['pmi'] = 1 - pmi
                for c in chs:
                    cs = slice(c * 64, (c + 1) * 64)
                    gsl = slice(t0 + c * 64, t0 + (c + 1) * 64)
                    C_ = cset[c]
                    pt, ptk = nps()
                    P.op('pe', lambda e: e.transpose(out=pt[0:64, 0:64], in_=vS[:, gsl], identity=ident[0:64, 0:64]),
                         reads=['FT5', 'ident'], writes=[ptk])
                    P.op('pe', lambda e: e.transpose(out=pt[0:64, 64:128], in_=R_['Kb'][:, cs], identity=ident[0:64, 0:64]),
                         reads=['r_Kb', 'ident'], writes=[ptk])
                    P.op('pe', lambda e: e.transpose(out=pt[0:64, 128:192], in_=R_['Bb'][:, cs], identity=ident[0:64, 0:64]),
                         reads=['r_Bb', 'ident'], writes=[ptk])
                    P.op('act', lambda e: e.copy(out=C_['Vt'][:], in_=pt[0:64, 0:64]), reads=[ptk], writes=['c%d_Vt' % c])
                    P.op('act', lambda e: e.copy(out=C_['Kt'][:], in_=pt[0:64, 64:128]), reads=[ptk], writes=['c%d_Kt' % c])
                    P.op('act', lambda e: e.copy(out=C_['Bt'][:], in_=pt[0:64, 128:192]), reads=[ptk], writes=['c%d_Bt' % c])
                for c in chs:
                    cs = slice(c * 64, (c + 1) * 64)
                    cg = (t0 // 64) + c
                    C_ = cset[c]
                    s_ = st[c]
                    AB, BB = C_['AB'], C_['BB']
                    ABk, BBk, Vtk, Ktk, Btk = ['c%d_%s' % (c, n_) for n_ in ('AB', 'BB', 'Vt', 'Kt', 'Bt')]
                    Pm, Pmk = C_['Pm%d' % s_['pmi']], 'c%d_Pm%d' % (c, s_['pmi'])
                    Zc, Zn = Zt[cur], Zt[1 - cur]
                    zck, znk = 'rq_Z%d' % cur, 'rq_Z%d' % (1 - cur)
                    pw, pwk = nps()
                    P.op('pe', lambda e: e.matmul(pw[0:64, 0:64], KR[:, 0, cs], Zc[:], start=True, stop=False),
                         reads=['r_KR', zck], writes=[pwk])
                    P.op('pe', lambda e: e.matmul(pw[0:64, 0:64], BB[:, 0:64], C_['Vt'][:], start=False, stop=True),
                         reads=[BBk, Vtk], writes=[pwk])
                    P.op('act', lambda e: e.copy(out=rsq['zt'][:], in_=pw[0:64, 0:64]), reads=[pwk], writes=['rq_zt'])
                    pu, puk = nps()
                    P.op('pe', lambda e: e.matmul(pu[0:64, 0:64], Pm[:], rsq['zt'][:], start=True, stop=True),
                         reads=[Pmk, 'rq_zt'], writes=[puk])
                    P.op('act', lambda e: e.copy(out=rsq['U'][:], in_=pu[0:64, 0:64]), reads=[puk], writes=['rq_U'])
                    pzz, pzk = nps()
                    P.op('pe', lambda e: e.matmul(pzz[0:64, 0:64], C_['Kt'][:], C_['Vt'][:], start=True, stop=False),
                         reads=[Ktk, Vtk], writes=[pzk])
                    P.op('pe', lambda e: e.matmul(pzz[0:64, 0:64], C_['Bt'][:], rsq['U'][:], start=False, stop=True),
                         reads=[Btk, 'rq_U'], writes=[pzk])
                    P.op('dve', lambda e: e.scalar_tensor_tensor(out=Zn[:], in0=Zc[:], scalar=rgam[:, c:c + 1], in1=pzz[0:64, 0:64],
                                                                 op0=ALU.mult, op1=ALU.add), reads=[zck, 'rgam', pzk], writes=[znk])
                    py, pyk = nps()
                    P.op('pe', lambda e: e.matmul(py[0:64, 0:64], KR[:, 1, cs], Zc[:], start=True, stop=False),
                         reads=['r_KR', zck], writes=[pyk])
                    P.op('pe', lambda e: e.matmul(py[0:64, 0:64], BB[:, 64:128], C_['Vt'][:], start=False, stop=False),
                         reads=[BBk, Vtk], writes=[pyk])
                    P.op('pe', lambda e: e.matmul(py[0:64, 0:64], AB[:, 64:128], rsq['U'][:], start=False, stop=True),
                         reads=[ABk, 'rq_U'], writes=[pyk])
                    if d == 0:
                        P.op('act', lambda e: e.copy(out=yacc[:, cg, :], in_=py[0:64, 0:64]), reads=[pyk], writes=[('yacc', cg)])
                    else:
                        P.op('dve', lambda e: e.tensor_tensor(out=yacc[:, cg, :], in0=yacc[:, cg, :], in1=py[0:64, 0:64], op=ALU.add),
                             reads=[pyk, ('yacc', cg)], writes=[('yacc', cg)])
                    cur = 1 - cur
            if not is_sample:
                pz, pk = nps()
                P.op('pe', lambda e: e.transpose(out=pz[0:64, 0:64], in_=Zt[cur][:], identity=ident[0:64, 0:64]),
                     reads=['rq_Z%d' % cur, 'ident'], writes=[pk])
                P.op('act', lambda e: e.copy(out=rsq['zt'][:], in_=pz[0:64, 0:64]), reads=[pk], writes=['rq_zt'])
                P.dma(o_rwkv[pidx, d, h], rsq['zt'][:], reads=['rq_zt'], q='pool')
        ykeys = [('yacc', c) for c in range(nchT)]
        gst = ostat[0:64, :, :].rearrange("p a b -> p (a b)")
        P.op('dve', lambda e: e.tensor_reduce(out=gst[:, 0:nchT], in_=yacc, axis=AX.X, op=ALU.add), reads=ykeys, writes=['gst'])
        P.op('dve', lambda e: e.tensor_scalar(out=gst[:, 0:nchT], in0=gst[:, 0:nchT], scalar1=-1.0 / 64, scalar2=None, op0=ALU.mult),
             reads=['gst'], writes=['gst'])
        P.op('dve', lambda e: e.tensor_tensor(out=yacc, in0=yacc, in1=gst[:, 0:nchT].unsqueeze(2).to_broadcast([64, nchT, 64]), op=ALU.add),
             reads=ykeys + ['gst'], writes=ykeys)
        sq = FT[3][0:64, 0:T].rearrange("p (c v) -> p c v", v=64)
        P.op('pool', lambda e: e.tensor_tensor(out=sq, in0=yacc, in1=yacc, op=ALU.mult), reads=ykeys, writes=['FT3'])
        P.op('dve', lambda e: e.tensor_reduce(out=gst[:, 32:32 + nchT], in_=sq, axis=AX.X, op=ALU.add), reads=['FT3'], writes=['gst'])
        P.op('dve', lambda e: e.tensor_scalar(out=gst[:, 32:32 + nchT], in0=gst[:, 32:32 + nchT], scalar1=1.0 / 64, scalar2=64e-5,
                                              op0=ALU.mult, op1=ALU.add), reads=['gst'], writes=['gst'])
        P.op('act', lambda e: e.activation(out=gst[:, 32:32 + nchT], in_=gst[:, 32:32 + nchT], func=AF.Sqrt), reads=['gst'], writes=['gst'])
        P.op('dve', lambda e: e.reciprocal(out=gst[:, 32:32 + nchT], in_=gst[:, 32:32 + nchT]), reads=['gst'], writes=['gst'])
        P.op('dve', lambda e: e.tensor_tensor(out=yacc, in0=yacc, in1=gst[:, 32:32 + nchT].unsqueeze(2).to_broadcast([64, nchT, 64]),
                                              op=ALU.mult), reads=ykeys + ['gst'], writes=ykeys)
        n8 = min(8, nchT)
        for g8 in range(nchT // n8):
            pz, pk = nps()
            for j in range(n8):
                cg = g8 * n8 + j
                P.op('pe', lambda e: e.transpose(out=pz[0:64, j * 64:(j + 1) * 64], in_=yacc[:, cg, :], identity=ident[0:64, 0:64]),
                     reads=[('yacc', cg), 'ident'], writes=[pk])
            w_ = n8 * 64
            gs = slice(g8 * w_, (g8 + 1) * w_)
            P.op('dve', lambda e: e.tensor_scalar(out=shiftt[:, 0:w_], in0=pz[0:64, 0:w_], scalar1=prm['gng'][:, h:h + 1],
                                                  scalar2=prm['gnb'][:, h:h + 1], op0=ALU.mult, op1=ALU.add),
                 reads=[pk, 'p_gng', 'p_gnb'], writes=['shift_t'])
            P.op('pool', lambda e: e.tensor_tensor(out=shiftt[:, 0:w_], in0=shiftt[:, 0:w_], in1=bonus[:, gs], op=ALU.add),
                 reads=['shift_t'] + [('bonus', b) for b in range(nblk)], writes=['shift_t'])
            P.op('dve', lambda e: e.tensor_tensor(out=yT[0:64, slot, gs], in0=shiftt[:, 0:w_], in1=szb[:, gs], op=ALU.mult),
                 reads=['shift_t', 'FT0'], writes=[('yT', slot)])

    seq_ids = debug.get('seqs', [0, 1, 2]) if debug else [0, 1, 2]
    head_ids = debug.get('heads', list(range(8))) if debug else list(range(8))
    rheads = debug.get('rheads', list(range(16))) if debug else list(range(16))
    for si in seq_ids:
        off, T, cidx, is_sample = SEQS[si]
        P.barrier()
        make_gate(0, cidx)
        make_hT(0, xin, 'xin', off, T, cidx)
        P.barrier()
        if debug and debug.get('inner'):
            dump("mT", mT[:, 0].rearrange("p a b -> p (a b)"), ['mT'], 48)
            dump("sc1", sc1[:, 0].rearrange("p a b -> p (a b)"), ['sc1'], 16)
            dump("scT", scT[:].rearrange("p a b -> p (a b)"), ['scT'], 16)
            dump("xn", xn, ['xn'], 1024)
            dump("xt0", xt[0], ['xt0'], 1024)
            for kc in range(8):
                dump("hT%d" % kc, hT[:, kc, 0:T], hT_keys(0, T), T, col0=off)
        for h in head_ids:
            hgrn_head(h, off, T, is_sample, si - 1)
            if debug and debug.get('dump_y'):
                dump("yT%d" % h, yT[:, h, 0:T], [('yT', h)], T, col0=off)
        P.barrier()
        load_wo(w_out_even[0:D, :], 128)
        outproj(0, [(128, s_) for s_ in range(8)], xin, 'xin', x1, 'x1', off, T, cidx)
        P.barrier()
        rwkv_seq_setup(off, T)
        for half in range(2):
            P.barrier()
            for slot in range(8):
                h = half * 8 + slot
                if h in rheads:
                    rwkv_head(h, slot, off, T, is_sample, si - 1)
                    if debug and debug.get('dump_y'):
                        dump("yR%d" % h, yT[0:64, slot, 0:T], [('yT', slot)], T, col0=off, parts=64)
            P.barrier()
            load_wo(w_out_even[D + half * 512: D + (half + 1) * 512, :], 64)
            outproj(0, [(64, s_) for s_ in range(8)], x1, 'x1', x1, 'x1', off, T, cidx)
    P.barrier()
    L0.close()

    if not (debug and debug.get('l0only')):
        L1 = contextlib.ExitStack()

        def sb1(name, shape, dt=F32):
            return L1.enter_context(nc.sbuf_tensor(name, list(shape), dt))

        make_fng()
        LC = 128
        DH = 512
        qT = yT[:, 4:8, :]
        kT = sb1("kT", [128, 4, TS], BF16)
        vch = sb1("vch", [128, DH], BF16)
        Cst = sb1("Cst", [128, 4, DH])
        Cbf = sb1("Cbf", [128, 4, DH], BF16)
        nst = sb1("nst", [128, 8])
        nbf = sb1("nbf", [128, 4], BF16)
        ktok = sb1("ktok", [128, DH], BF16)
        vw = sb1("vw", [128, DH], BF16)
        sTs = sb1("sTs", [128, 128], BF16)
        onesb = sb1("onesb", [128, 1], BF16)
        identb = sb1("identb", [128, 128], BF16)
        mC = sb1("mC", [128, 2, 128])
        SEL = sb1("SEL", [36, 4, 128])
        XA = sb1("XA", [36, TS])
        XB = sb1("XB", [36, TS])
        zrow = sb1("zrow", [36, 512])
        sm = {n_: sb1("sm_" + n_, [36, 16]) for n_ in ['ac', 'bl', 'M', 'MP', 'mu', 'al', 'gref', 'm0']}
        Wtok = sb1("Wtok", [128, 2, 16, 8])
        Wtokb = sb1("Wtokb", [128, 2, 16, 4], BF16)
        ALb = sb1("ALb", [128, 2, 4, 16])
        dstat = sb1("dstat", [128, 8])
        wGb = sb1("wGb", [128, 8, 16], BF16)
        gbT = sb1("gbT", [36, 4])
        ngbT = sb1("ngbT", [36, 4])
        cw = sb1("cw", [128, 32, 9])
        cb = sb1("cb", [128, 32])
        mng = sb1("mng", [128, 16])
        wbf1 = sb1("wbf1", [128, 8, DH], BF16)
        P.dma(mC[:], maskC.rearrange("d s t -> s d t"), writes=['mC'])
        P.dma(SEL[:], sel_d[:], writes=['SEL'])
        P.dma(gbT[:], gbT_d[:], writes=['gbT'])
        P.dma(cw[:], cw_d[:], writes=['cw'])
        P.dma(cb[:], cb_d[:], writes=['cb'])
        P.dma(mng[:], mng_d[:], writes=['mng'])
        P.op('dve', lambda e: e.memset(onesb[:], 1.0), writes=['onesb'])
        P.op('dve', lambda e: e.memset(zrow[:], 0.0), writes=['zrow'])
        P.op('dve', lambda e: e.tensor_copy(out=identb[:], in_=ident[:]), reads=['ident'], writes=['identb'])
        P.op('dve', lambda e: e.tensor_scalar(out=ngbT[:], in0=gbT[:], scalar1=-1.0, scalar2=None, op0=ALU.mult), reads=['gbT'], writes=['ngbT'])
        P.dma(wst[:, :, 0:16], w_in_odd[:, 10240:10256].rearrange("(kc p) n -> p kc n", p=128), writes=['wst'])
        P.op('pool', lambda e: e.tensor_copy(out=wGb[:], in_=wst[:, :, 0:16]), reads=['wst'], writes=['wGb'])
        LNK = float(np.log(DH ** -0.5))

        def load_w1(c0, ncols):
            v = w_in_odd[:, c0:c0 + ncols].rearrange("(kc p) n -> p kc n", p=128)
            for q0 in range(0, ncols, 256):
                w_ = min(256, ncols - q0)
                P.dma(wst[:, :, 0:w_], v[:, :, q0:q0 + w_], writes=['wst'])
                P.op('pool', lambda e: e.tensor_copy(out=wbf1[:, :, q0:q0 + w_], in_=wst[:, :, 0:w_]), reads=['wst'], writes=['wbf1'])

        def gates_seq(T, is_sample):
            NC = T // LC
            pbk = min(512, T)
            for d in range(2):
                pb = 32 * d
                rows = slice(pb, pb + 4)
                for b in range(T // pbk):
                    t0 = b * pbk
                    pz, pk = nps()
                    for kc in range(8):
                        P.op('pe', lambda e: e.matmul(pz[pb:pb + 4, 0:pbk], wGb[:, kc, (2 + d) * 4:(3 + d) * 4], hT[:, kc, t0:t0 + pbk],
                                                      start=(kc == 0), stop=(kc == 7)), reads=['wGb'] + hT_keys(t0, t0 + pbk), writes=[pk])
                    P.op('act', lambda e: e.activation(out=XA[rows, t0:t0 + pbk], in_=pz[pb:pb + 4, 0:pbk], func=AF.Exp,
                                                       bias=ngbT[rows, 2 + d:3 + d], scale=-1.0), reads=[pk, 'ngbT'], writes=['XA'])
                P.op('act', lambda e: e.activation(out=XA[rows, 0:T], in_=XA[rows, 0:T], func=AF.Ln, bias=1.0, scale=1.0),
                     reads=['XA'], writes=['XA'])
                for b in range(T // pbk):
                    bs = slice(b * pbk, (b + 1) * pbk)
                    if d == 0:
                        P.op('dve', lambda e: e.tensor_tensor_scan(out=XB[rows, bs], data0=XA[rows, bs], data1=zrow[rows, 0:pbk],
                                                                   initial=0.0, op0=ALU.add, op1=ALU.add), reads=['XA', 'zrow'], writes=['XB'])
                    else:
                        P.op('dve', lambda e: e.tensor_tensor_scan(out=XB[rows, bs][:, ::-1], data0=XA[rows, bs][:, ::-1],
                                                                   data1=zrow[rows, 0:pbk], initial=0.0, op0=ALU.add, op1=ALU.add),
                             reads=['XA', 'zrow'], writes=['XB'])
                if d == 0:
                    ci_, ce_ = 0, LC - 1
                else:
                    ci_, ce_ = LC - 1, 0
                B3 = XB[rows, 0:T].rearrange("p (c l) -> p c l", l=LC)
                A3 = XA[rows, 0:T].rearrange("p (c l) -> p c l", l=LC)
                S = {k_: v_[rows, :] for k_, v_ in sm.items()}
                P.op('dve', lambda e: e.tensor_tensor(out=S['gref'][:, 0:NC], in0=B3[:, :, ci_], in1=A3[:, :, ci_], op=ALU.subtract),
                     reads=['XA', 'XB'], writes=['sm_gref'])
                P.op('dve', lambda e: e.tensor_tensor(out=B3, in0=B3, in1=S['gref'][:, 0:NC].unsqueeze(2).to_broadcast([4, NC, LC]),
                                                      op=ALU.subtract), reads=['XB', 'sm_gref'], writes=['XB'])
                for b in range(T // pbk):
                    t0 = b * pbk
                    pz, pk = nps()
                    for kc in range(8):
                        P.op('pe', lambda e: e.matmul(pz[pb:pb + 4, 0:pbk], wGb[:, kc, d * 4:(d + 1) * 4], hT[:, kc, t0:t0 + pbk],
                                                      start=(kc == 0), stop=(kc == 7)), reads=['wGb'] + hT_keys(t0, t0 + pbk), writes=[pk])
                    P.op('dve', lambda e: e.scalar_tensor_tensor(out=XA[rows, t0:t0 + pbk], in0=pz[pb:pb + 4, 0:pbk], scalar=gbT[rows, d:d + 1],
                                                                 in1=XB[rows, t0:t0 + pbk], op0=ALU.add, op1=ALU.add),
                         reads=[pk, 'gbT', 'XB', 'XA'], writes=['XA'])
                P.op('dve', lambda e: e.tensor_reduce(out=S['ac'][:, 0:NC], in_=A3, axis=AX.X, op=ALU.max), reads=['XA'], writes=['sm_ac'])
                P.op('dve', lambda e: e.tensor_scalar(out=S['bl'][:, 0:NC], in0=B3[:, :, ce_], scalar1=-1.0, scalar2=None, op0=ALU.mult),
                     reads=['XB'], writes=['sm_bl'])
                if is_sample:
                    P.dma(S['m0'][:, 0:1], s_m[d, :].rearrange("(h o) -> h o", o=1), writes=['sm_m0'])
                else:
                    P.op('dve', lambda e: e.memset(S['m0'][:, 0:1], 0.0), writes=['sm_m0'])
                if d == 0:
                    P.op('dve', lambda e: e.tensor_tensor_scan(out=S['M'][:, 0:NC], data0=S['ac'][:, 0:NC], data1=S['bl'][:, 0:NC],
                                                               initial=S['m0'][:, 0:1], op0=ALU.max, op1=ALU.add),
                         reads=['sm_ac', 'sm_bl', 'sm_m0'], writes=['sm_M'])
                    P.op('dve', lambda e: e.tensor_copy(out=S['MP'][:, 0:1], in_=S['m0'][:, 0:1]), reads=['sm_m0'], writes=['sm_MP'])
                    if NC > 1:
                        P.op('dve', lambda e: e.tensor_copy(out=S['MP'][:, 1:NC], in_=S['M'][:, 0:NC - 1]), reads=['sm_M', 'sm_MP'], writes=['sm_MP'])
                else:
                    P.op('dve', lambda e: e.tensor_tensor_scan(out=S['M'][:, 0:NC][:, ::-1], data0=S['ac'][:, 0:NC][:, ::-1],
                                                               data1=S['bl'][:, 0:NC][:, ::-1], initial=S['m0'][:, 0:1],
                                                               op0=ALU.max, op1=ALU.add),
                         reads=['sm_ac', 'sm_bl', 'sm_m0'], writes=['sm_M'])
                    P.op('dve', lambda e: e.tensor_copy(out=S['MP'][:, NC - 1:NC], in_=S['m0'][:, 0:1]), reads=['sm_m0'], writes=['sm_MP'])
                    if NC > 1:
                        P.op('dve', lambda e: e.tensor_copy(out=S['MP'][:, 0:NC - 1], in_=S['M'][:, 1:NC]), reads=['sm_M', 'sm_MP'], writes=['sm_MP'])
                P.op('dve', lambda e: e.tensor_tensor(out=S['mu'][:, 0:NC], in0=S['MP'][:, 0:NC], in1=S['ac'][:, 0:NC], op=ALU.max),
                     reads=['sm_MP', 'sm_ac'], writes=['sm_mu'])
                P.op('dve', lambda e: e.tensor_tensor(out=S['al'][:, 0:NC], in0=S['MP'][:, 0:NC], in1=S['mu'][:, 0:NC], op=ALU.subtract),
                     reads=['sm_MP', 'sm_mu'], writes=['sm_al'])
                P.op('act', lambda e: e.activation(out=S['al'][:, 0:NC], in_=S['al'][:, 0:NC], func=AF.Exp), reads=['sm_al'], writes=['sm_al'])
                mub = S['mu'][:, 0:NC].unsqueeze(2).to_broadcast([4, NC, LC])
                P.op('dve', lambda e: e.tensor_tensor(out=A3, in0=A3, in1=mub, op=ALU.subtract), reads=['XA', 'sm_mu'], writes=['XA'])
                P.op('dve', lambda e: e.tensor_tensor(out=B3, in0=B3, in1=mub, op=ALU.subtract), reads=['XB', 'sm_mu'], writes=['XB'])
                P.op('dve', lambda e: e.tensor_scalar(out=XA[rows, 0:T], in0=XA[rows, 0:T], scalar1=LNK, scalar2=None, op0=ALU.add),
                     reads=['XA'], writes=['XA'])
                P.op('act', lambda e: e.activation(out=XA[rows, 0:T], in_=XA[rows, 0:T], func=AF.Exp), reads=['XA'], writes=['XA'])
                P.op('act', lambda e: e.activation(out=XB[rows, 0:T], in_=XB[rows, 0:T], func=AF.Exp), reads=['XB'], writes=['XB'])
                pz, pk = nps()
                for c in range(NC):
                    P.op('pe', lambda e: e.transpose(out=pz[:, c * 8:c * 8 + 4], in_=XA[rows, c * LC:(c + 1) * LC],
                                                     identity=ident[rows, pb:pb + 4]), reads=['XA', 'ident'], writes=[pk])
                    P.op('pe', lambda e: e.transpose(out=pz[:, c * 8 + 4:c * 8 + 8], in_=XB[rows, c * LC:(c + 1) * LC],
                                                     identity=ident[rows, pb:pb + 4]), reads=['XB', 'ident'], writes=[pk])
                P.op('dve', lambda e: e.tensor_copy(out=Wtok[:, d, 0:NC, :], in_=pz[:, 0:NC * 8].rearrange("p (c k) -> p c k", k=8)),
                     reads=[pk], writes=['Wtok'])
                P.op('dve', lambda e: e.tensor_copy(out=Wtokb[:, d, 0:NC, :], in_=Wtok[:, d, 0:NC, 0:4]), reads=['Wtok'], writes=['Wtokb'])
                pz, pk = nps()
                for hd in range(4):
                    P.op('pe', lambda e: e.matmul(pz[:, hd * 16:hd * 16 + NC], SEL[rows, hd, :], S['al'][:, 0:NC], start=True, stop=True),
                         reads=['SEL', 'sm_al'], writes=[pk])
                P.op('dve', lambda e: e.tensor_copy(out=ALb[:, d, :, 0:NC], in_=pz[:, 0:64].rearrange("p (h c) -> p h c", c=16)[:, :, 0:NC]),
                     reads=[pk], writes=['ALb'])

        def conv_tile(dst, dkey, slot_j, widx, t0src, T, is_sample):
            X = FT[1][:, 0:T]
            A = FT[0][:, 0:T]
            if is_sample:
                R_, Cw = T // 64, 64
                taps = [(dr, dc) for dr in (-1, 0, 1) for dc in (-1, 0, 1)]
            else:
                R_, Cw = 1, T
                taps = [(0, dc) for dc in (-1, 0, 1)]
            X3 = X.rearrange("p (r c) -> p r c", c=Cw)
            A3 = A.rearrange("p (r c) -> p r c", c=Cw)
            P.op('dve', lambda e: e.tensor_scalar(out=A, in0=X, scalar1=cw[:, widx, 4:5], scalar2=None, op0=ALU.mult),
                 reads=['FT1', 'cw'], writes=['FT0'])
            for (dr, dc) in taps:
                if dr == 0 and dc == 0:
                    continue
                r0, r1 = max(0, -dr), R_ - max(0, dr)
                c0, c1 = max(0, -dc), Cw - max(0, dc)
                ti = (dr + 1) * 3 + (dc + 1)
                P.op('dve', lambda e: e.scalar_tensor_tensor(out=A3[:, r0:r1, c0:c1], in0=X3[:, r0 + dr:r1 + dr, c0 + dc:c1 + dc],
                                                             scalar=cw[:, widx, ti:ti + 1], in1=A3[:, r0:r1, c0:c1],
                                                             op0=ALU.mult, op1=ALU.add), reads=['FT1', 'FT0', 'cw'], writes=['FT0'])
            P.op('act', lambda e: e.activation(out=dst[:, slot_j, 0:T], in_=A, func=AF.Silu, bias=cb[:, widx:widx + 1], scale=1.0),
                 reads=['FT0', 'cb'], writes=[dkey])

        hacc = [FT[2 + i][:, 0:TS].rearrange("p (j e) -> p j e", e=DH) for i in range(4)]

        def mlstm_head(hd, off, T, is_sample, pidx):
            NC = T // LC
            NTt = T // 128
            pbk = min(512, T)
            for qk in range(2):
                load_w1(qk * 2048 + hd * DH, DH)
                for j in range(4):
                    for b in range(T // pbk):
                        t0 = b * pbk
                        pz, pk = nps()
                        for kc in range(8):
                            P.op('pe', lambda e: e.matmul(pz[:, 0:pbk], wbf1[:, kc, j * 128:(j + 1) * 128], hT[:, kc, t0:t0 + pbk],
                                                          start=(kc == 0), stop=(kc == 7)), reads=['wbf1'] + hT_keys(t0, t0 + pbk), writes=[pk])
                        P.op('act', lambda e: e.copy(out=FT[1][:, t0:t0 + pbk], in_=pz[:, 0:pbk]), reads=[pk], writes=['FT1'])
                    widx = (qk * 4 + hd) * 4 + j
                    if qk == 0:
                        conv_tile(qT, ('yT', 4 + j), j, widx, 0, T, is_sample)
                    else:
                        conv_tile(kT, 'kT', j, widx, 0, T, is_sample)
            load_w1(4096 + hd * DH, DH)
            qkeys = [('yT', 4 + j) for j in range(4)]
            for d in range(2):
                rev = (d == 1)
                if is_sample:
                    P.dma(Cst[:], s_C[d, hd].rearrange("(j p) e -> p j e", p=128), writes=['Cst'])
                    P.dma(nst[:, 0:4], s_n[d, hd].rearrange("(j p) -> p j", p=128), writes=['nst'], allow_slow_non_contiguous=True)
                else:
                    P.op('pool', lambda e: e.memset(Cst[:], 0.0), writes=['Cst'])
                    P.op('pool', lambda e: e.memset(nst[:, 0:4], 0.0), writes=['nst'])
                chunks = list(range(NC))
                if rev:
                    chunks = chunks[::-1]
                for c in chunks:
                    cs = slice(c * LC, (c + 1) * LC)
                    wcol = Wtok[:, d, c, hd:hd + 1]
                    thcol = Wtok[:, d, c, 4 + hd:5 + hd]
                    alcol = ALb[:, d, hd, c:c + 1]
                    pv, pvk = nps()
                    for kc in range(8):
                        P.op('pe', lambda e: e.matmul(pv[:, 0:DH], hT[:, kc, cs], wbf1[:, kc, :], start=(kc == 0), stop=(kc == 7)),
                             reads=['wbf1', ('hT', c)], writes=[pvk])
                    P.op('act', lambda e: e.copy(out=vch[:], in_=pv[:, 0:DH]), reads=[pvk], writes=['vch'])
                    pt, ptk = nps()
                    ptb = pt[:].bitcast(BF16)
                    for j in range(4):
                        P.op('pe', lambda e: e.transpose(out=ptb[:, j * 128:(j + 1) * 128], in_=kT[:, j, cs], identity=identb[:]),
                             reads=['kT', 'identb'], writes=[ptk])
                    P.op('act', lambda e: e.copy(out=ktok[:], in_=ptb[:, 0:DH]), reads=[ptk], writes=['ktok'])
                    ps_, psk = nps()
                    for j in range(4):
                        P.op('pe', lambda e: e.matmul(ps_[:, 0:128], kT[:, j, cs], qT[:, j, cs], start=(j == 0), stop=(j == 3)),
                             reads=['kT'] + qkeys, writes=[psk])
                    P.op('dve', lambda e: e.scalar_tensor_tensor(out=sTs[:], in0=ps_[:, 0:128], scalar=wcol, in1=mC[:, d, :],
                                                                 op0=ALU.mult, op1=ALU.mult), reads=[psk, 'Wtok', 'mC'], writes=['sTs'])
                    P.op('dve', lambda e: e.tensor_scalar(out=Cst[:], in0=Cst[:], scalar1=alcol, scalar2=None, op0=ALU.mult),
                         reads=['Cst', 'ALb'], writes=['Cst'])
                    P.op('act', lambda e: e.copy(out=Cbf[:], in_=Cst[:]), reads=['Cst'], writes=['Cbf'])
                    P.op('dve', lambda e: e.tensor_scalar(out=nst[:, 0:4], in0=nst[:, 0:4], scalar1=alcol, scalar2=None, op0=ALU.mult),
                         reads=['nst', 'ALb'], writes=['nst'])
                    P.op('dve', lambda e: e.tensor_copy(out=nbf[:], in_=nst[:, 0:4]), reads=['nst'], writes=['nbf'])
                    pn, pnk = nps()
                    for j in range(4):
                        P.op('pe', lambda e: e.matmul(pn[:, 0:DH], qT[:, j, cs], Cbf[:, j, :], start=(j == 0), stop=False),
                             reads=qkeys + ['Cbf'], writes=[pnk])
                    P.op('pe', lambda e: e.matmul(pn[:, 0:DH], sTs[:], vch[:], start=False, stop=True),
                         reads=['sTs', 'vch'], writes=[pnk])
                    pd_, pdk = nps()
                    for j in range(4):
                        P.op('pe', lambda e: e.matmul(pd_[:, 0:1], qT[:, j, cs], nbf[:, j:j + 1], start=(j == 0), stop=False),
                             reads=qkeys + ['nbf'], writes=[pdk])
                    P.op('pe', lambda e: e.matmul(pd_[:, 0:1], sTs[:], onesb[:], start=False, stop=True), reads=['sTs', 'onesb'], writes=[pdk])
                    P.op('act', lambda e: e.activation(out=dstat[:, 2:3], in_=pd_[:, 0:1], func=AF.Abs), reads=[pdk], writes=['dstat'])
                    P.op('dve', lambda e: e.tensor_tensor(out=dstat[:, 0:1], in0=dstat[:, 2:3], in1=thcol, op=ALU.max),
                         reads=['dstat', 'Wtok'], writes=['dstat'])
                    P.op('dve', lambda e: e.reciprocal(out=dstat[:, 1:2], in_=dstat[:, 0:1]), reads=['dstat'], writes=['dstat'])
                    hdst = hacc[c // 4][:, c % 4, :]
                    if d == 0:
                        P.op('act', lambda e: e.activation(out=hdst, in_=pn[:, 0:DH], func=AF.Identity, scale=dstat[:, 1:2]),
                             reads=[pnk, 'dstat'], writes=[('hacc', c)])
                    else:
                        P.op('dve', lambda e: e.scalar_tensor_tensor(out=hdst, in0=pn[:, 0:DH], scalar=dstat[:, 1:2], in1=hdst,
                                                                     op0=ALU.mult, op1=ALU.add), reads=[pnk, 'dstat', ('hacc', c)], writes=[('hacc', c)])
                    P.op('pool', lambda e: e.tensor_scalar(out=vw[:], in0=vch[:], scalar1=wcol, scalar2=None, op0=ALU.mult),
                         reads=['vch', 'Wtok'], writes=['vw'])
                    for j in range(4):
                        pc, pck = nps()
                        P.op('pe', lambda e: e.matmul(pc[:, 0:DH], ktok[:, j * 128:(j + 1) * 128], vw[:], start=True, stop=True),
                             reads=['ktok', 'vw'], writes=[pck])
                        P.op('dve', lambda e: e.tensor_tensor(out=Cst[:, j, :], in0=Cst[:, j, :], in1=pc[:, 0:DH], op=ALU.add),
                             reads=[pck, 'Cst'], writes=['Cst'])
                    pq_, pqk = nps()
                    for j in range(4):
                        P.op('pe', lambda e: e.matmul(pq_[:, j:j + 1], ktok[:, j * 128:(j + 1) * 128], Wtokb[:, d, c, hd:hd + 1],
                                                      start=True, stop=True), reads=['ktok', 'Wtokb'], writes=[pqk])
                    P.op('dve', lambda e: e.tensor_tensor(out=nst[:, 0:4], in0=nst[:, 0:4], in1=pq_[:, 0:4], op=ALU.add),
                         reads=[pqk, 'nst'], writes=['nst'])
                if not is_sample:
                    P.dma(o_C[pidx, d, hd].rearrange("(j p) e -> p j e", p=128), Cst[:], reads=['Cst'], q='pool')
                    P.dma(o_n[pidx, d, hd].rearrange("(j p) -> p j", p=128), nst[:, 0:4], reads=['nst'], q='pool', allow_slow_non_contiguous=True)
            load_w1(6144 + hd * DH, DH)
            for tt in range(NTt):
                hdst = hacc[tt // 4][:, tt % 4, :]
                pz, pk = nps()
                for kc in range(8):
                    P.op('pe', lambda e: e.matmul(pz[:, 0:DH], hT[:, kc, tt * 128:(tt + 1) * 128], wbf1[:, kc, :],
                                                  start=(kc == 0), stop=(kc == 7)), reads=['wbf1', ('hT', tt)], writes=[pk])
                P.op('act', lambda e: e.activation(out=FT[0][:, 0:DH], in_=pz[:, 0:DH], func=AF.Sigmoid), reads=[pk], writes=['FT0'])
                P.op('dve', lambda e: e.tensor_tensor(out=hdst, in0=hdst, in1=FT[0][:, 0:DH], op=ALU.mult),
                     reads=['FT0', ('hacc', tt)], writes=[('hacc', tt)])
                P.op('act', lambda e: e.activation(out=FT[0][:, 0:DH], in_=hdst, func=AF.Square, accum_out=dstat[:, 4:5]),
                     reads=[('hacc', tt), 'FT0'], writes=['FT0', 'dstat'])
                P.op('dve', lambda e: e.tensor_scalar(out=dstat[:, 5:6], in0=dstat[:, 4:5], scalar1=1.0 / DH, scalar2=1e-6,
                                                      op0=ALU.mult, op1=ALU.add), reads=['dstat'], writes=['dstat'])
                P.op('act', lambda e: e.activation(out=dstat[:, 6:7], in_=dstat[:, 5:6], func=AF.Sqrt), reads=['dstat'], writes=['dstat'])
                P.op('dve', lambda e: e.reciprocal(out=dstat[:, 7:8], in_=dstat[:, 6:7]), reads=['dstat'], writes=['dstat'])
                P.op('dve', lambda e: e.tensor_scalar(out=hdst, in0=hdst, scalar1=dstat[:, 7:8], scalar2=None, op0=ALU.mult),
                     reads=[('hacc', tt), 'dstat'], writes=[('hacc', tt)])
            load_w1(8192 + hd * DH, DH)
            for tt in range(NTt):
                hdst = hacc[tt // 4][:, tt % 4, :]
                pz, pk = nps()
                for kc in range(8):
                    P.op('pe', lambda e: e.matmul(pz[:, 0:DH], hT[:, kc, tt * 128:(tt + 1) * 128], wbf1[:, kc, :],
                                                  start=(kc == 0), stop=(kc == 7)), reads=['wbf1', ('hT', tt)], writes=[pk])
                P.op('act', lambda e: e.activation(out=FT[0][:, 0:DH], in_=pz[:, 0:DH], func=AF.Silu), reads=[pk], writes=['FT0'])
                P.op('dve', lambda e: e.tensor_tensor(out=hdst, in0=hdst, in1=FT[0][:, 0:DH], op=ALU.mult),
                     reads=['FT0', ('hacc', tt)], writes=[('hacc', tt)])
                pz, pk = nps()
                for j in range(4):
                    P.op('pe', lambda e: e.transpose(out=pz[:, j * 128:(j + 1) * 128], in_=hdst[:, j * 128:(j + 1) * 128], identity=ident[:]),
                         reads=[('hacc', tt), 'ident'], writes=[pk])
                for j in range(4):
                    P.op('act', lambda e: e.activation(out=yT[:, j, tt * 128:(tt + 1) * 128], in_=pz[:, j * 128:(j + 1) * 128],
                                                       func=AF.Identity, scale=mng[:, hd * 4 + j:hd * 4 + j + 1]),
                         reads=[pk, 'mng'], writes=[('yT', j)])

        for si in seq_ids:
            off, T, cidx, is_sample = SEQS[si]
            P.barrier()
            make_gate(1, cidx)
            make_hT(1, x1, 'x1', off, T, cidx)
            P.barrier()
            gates_seq(T, is_sample)
            if not is_sample:
                for d in range(2):
                    lastc = (T // LC - 1) if d == 0 else 0
                    P.dma(o_m[si - 1, d, :].rearrange("(h o) -> h o", o=1), sm['M'][32 * d:32 * d + 4, lastc:lastc + 1],
                          reads=['sm_M'], q='pool')
            for hd in range(4):
                P.barrier()
                mlstm_head(hd, off, T, is_sample, si - 1)
                if debug and debug.get('dump_y'):
                    for j in range(4):
                        dump("yM%d_%d" % (hd, j), yT[:, j, 0:T], [('yT', j)], T, col0=off)
                P.barrier()
                load_wo(w_out_odd[hd * DH:(hd + 1) * DH, :], 128, nk=4)
                last = (hd == 3)
                outproj(1, [(128, s_) for s_ in range(4)], x1, 'x1', (yout if last else x1), ('yout' if last else 'x1'),
                        off, T, cidx, final=last)
        P.barrier()
        L1.close()
    P.finish()
    sems = {s: es.enter_context(nc.semaphore(s)) for s in P.sem_names}
    P.emit(sems)
    es.close()
    global _last_dslot
    _last_dslot = dslot if debug else {}
    return nc, P


def host_inputs(inp, core):
    f = lambda a: np.ascontiguousarray(a, dtype=np.float32)
    b = core % 2
    m = {}
    m["xin"] = f(np.concatenate([inp["x_sample"][b], inp["x_prompt"][2 * core], inp["x_prompt"][2 * core + 1]], axis=0))
    cond = np.stack([inp["c"][b], inp["c_ctx"]], axis=0)
    m["condT"] = f(cond.reshape(2, 8, 128).transpose(2, 1, 0))
    m["s_hgrn"] = f(inp["state_hgrn"][b, 0])
    m["s_rwkv"] = f(inp["state_rwkv"][b, 0])
    m["s_C"] = f(inp["state_mlstm_C"][b, 0])
    m["s_n"] = f(inp["state_mlstm_n"][b, 0])
    m["s_m"] = f(inp["state_mlstm_m"][b, 0])
    m["w_mod"] = f(inp["w_mod"])
    m["b_modT"] = f(inp["b_mod"].reshape(2, 24, 128).transpose(2, 0, 1))
    m["norm_gT"] = f(inp["norm_g"].reshape(2, 8, 128).transpose(2, 0, 1))
    m["fnorm_gT"] = f(inp["final_norm_g"].reshape(8, 128).T)
    w = inp["w_in_even"][0]
    DA = 1024
    wA = np.stack([np.concatenate([w[:, g * DA + h * 128: g * DA + (h + 1) * 128] for g in (0, 1, 4, 2, 3)], axis=1)
                   for h in range(8)], axis=0)
    m["wA"] = f(wA)
    o = 5 * DA
    zb0 = o + 3328
    wB = np.stack([np.concatenate([w[:, o + g * 1024 + h * 64: o + g * 1024 + (h + 1) * 64] for g in (0, 1, 2)]
                                  + [w[:, zb0 + h * 64: zb0 + (h + 1) * 64]], axis=1) for h in range(16)], axis=0)
    m["wB"] = f(wB)
    m["wLR"] = f(w[:, o + 3072: o + 3328])
    m["w_out_even"] = f(inp["w_out_even"][0])
    m["lbT"] = f(inp["hgrn_lb_logits"].reshape(2, 8, 128).transpose(2, 0, 1))
    m["hg_gT"] = f(inp["hgrn_norm_g"][0].reshape(8, 128).T)
    mu = inp["rwkv_shift_mu"][0]
    mr = np.zeros((64, 2, 4, 16), np.float32)
    for g in range(3):
        mr[:, :, g, :] = mu[:, g * 1024:(g + 1) * 1024].reshape(2, 16, 64).transpose(2, 0, 1)
    m["mu_rkv"] = mr
    m["mu_lr"] = f(mu[:, 3072:3328].reshape(2, 4, 64).transpose(2, 0, 1))
    m["w0T"] = f(inp["rwkv_w0"][0].reshape(2, 16, 64).transpose(2, 0, 1))
    m["a0T"] = f(inp["rwkv_a0"][0].reshape(2, 16, 64).transpose(2, 0, 1))
    m["w2"] = f(inp["rwkv_w2"][0])
    m["a2"] = f(inp["rwkv_a2"][0])
    m["kkT"] = f(inp["rwkv_k_k"][0].reshape(16, 64).T)
    m["kaT"] = f(inp["rwkv_k_a"][0].reshape(16, 64).T)
    m["rkT"] = f(inp["rwkv_r_k"][0].T)
    m["gngT"] = f(inp["rwkv_gn_g"][0].reshape(16, 64).T)
    m["gnbT"] = f(inp["rwkv_gn_b"][0].reshape(16, 64).T)
    s = np.arange(128)[:, None]
    t = np.arange(128)[None, :]
    same = (s // 32) == (t // 32)
    m["maskH"] = np.stack([(same & (s <= t)), (same & (s >= t))]).astype(np.float32)
    m["ident_in"] = np.eye(128, dtype=np.float32)
    s6 = np.arange(64)[:, None]
    t6 = np.arange(64)[None, :]
    mr_ = np.zeros((2, 3, 64, 128), np.float32)
    for d_, (st_, inc_) in enumerate([((s6 < t6), (s6 <= t6)), ((s6 > t6), (s6 >= t6))]):
        st_ = st_.astype(np.float32)
        inc_ = inc_.astype(np.float32)
        mr_[d_, 0, :, 0:64] = -st_
        mr_[d_, 0, :, 64:128] = -inc_
        mr_[d_, 1, :, 0:64] = st_
        mr_[d_, 1, :, 64:128] = inc_
        mr_[d_, 2, :, 0:64] = -(st_.T)
    m["maskR"] = mr_
    m["w_in_odd"] = f(inp["w_in_odd"][0])
    m["w_out_odd"] = f(inp["w_out_odd"][0])
    m["maskC"] = np.stack([(s <= t), (s >= t)]).astype(np.float32)
    sel = np.zeros((36, 4, 128), np.float32)
    gbt = np.zeros((36, 4), np.float32)
    for pb_ in (0, 32):
        for k_ in range(4):
            sel[pb_ + k_, k_, :] = 1.0
        gbt[pb_:pb_ + 4, :] = inp["mlstm_gate_b"][0].T
    m["sel_d"] = sel
    m["gbT_d"] = gbt
    m["cw_d"] = f(inp["mlstm_conv_w"][0].reshape(9, 32, 128).transpose(2, 1, 0))
    m["cb_d"] = f(inp["mlstm_conv_b"][0].reshape(32, 128).T)
    m["mng_d"] = f(inp["mlstm_norm_g"][0].reshape(16, 128).T)
    return m


def kernel(**inp):
    inp = {k: np.asarray(v) for k, v in inp.items()}
    nc, P = build()
    in_maps = [host_inputs(inp, c) for c in range(NCORES)]
    res = run_bass_kernel_spmd(nc, in_maps, core_ids=list(range(NCORES)))
    r = res.results
    y_prompt = np.zeros((16, TP, D), np.float32)
    y_sample = np.zeros((2, TS, D), np.float32)
    for c in range(NCORES):
        y_prompt[2 * c] = r[c]["yout"][TS:TS + TP]
        y_prompt[2 * c + 1] = r[c]["yout"][TS + TP:]
    for b in range(2):
        y_sample[b] = r[b]["yout"][0:TS]
    new_hgrn = np.concatenate([r[c]["o_hgrn"] for c in range(NCORES)], axis=0)[:, None]
    new_rwkv = np.concatenate([r[c]["o_rwkv"] for c in range(NCORES)], axis=0)[:, None]
    new_C = np.concatenate([r[c]["o_C"] for c in range(NCORES)], axis=0)[:, None]
    new_n = np.concatenate([r[c]["o_n"] for c in range(NCORES)], axis=0)[:, None]
    new_m = np.concatenate([r[c]["o_m"] for c in range(NCORES)], axis=0)[:, None]
    return (y_prompt, y_sample, new_hgrn.astype(np.float32), new_rwkv.astype(np.float32),
            new_C.astype(np.float32), new_n.astype(np.float32), new_m.astype(np.float32))
```

```python
import contextlib
import numpy as np
import concourse.bass as bass
import concourse.mybir as mybir
from concourse.bass_utils import run_bass_kernel_spmd

F32 = mybir.dt.float32
BF16 = mybir.dt.bfloat16
ALU = mybir.AluOpType
AF = mybir.ActivationFunctionType
AX = mybir.AxisListType

D = 1024
TS = 2048
TP = 256
TT = TS + 2 * TP
NCORES = 8


class _Rec:
    def __init__(self):
        self.calls = []

    def __getattr__(self, name):
        def f(*a, **k):
            self.calls.append((name, a, k))
            return self
        return f


class Prog:
    ENGS = ['pe', 'dve', 'act', 'pool', 'sp']
    NDMA = 16

    def __init__(self, nc):
        self.nc = nc
        self.ops = {e: [] for e in self.ENGS}
        self.cnt = {}
        self.waited = {e: {} for e in self.ENGS}
        self.last_write = {}
        self.readers = {}
        self.dma_rr = 0
        self.sem_names = list(self.ENGS) + ['d%d' % i for i in range(self.NDMA)]
        for s in self.sem_names:
            self.cnt[s] = 0
        self.n_ops = 0

    def _deps(self, eng, reads, writes):
        deps = {}

        def add(p):
            if p is None:
                return
            f, n = p
            if f == 'pe' and eng == 'pe':
                return
            if n > deps.get(f, 0):
                deps[f] = n
        for k in reads:
            add(self.last_write.get(k))
        for k in writes:
            add(self.last_write.get(k))
            for p in self.readers.get(k, ()):
                add(p)
        waits = []
        for f, n in deps.items():
            if n > self.waited[eng].get(f, 0):
                waits.append((f, n))
                self.waited[eng][f] = n
        return waits

    def _commit(self, tag, reads, writes):
        for k in reads:
            lst = self.readers.setdefault(k, [])
            lst[:] = [p for p in lst if p[0] != tag[0]]
            lst.append(tag)
        for k in writes:
            self.last_write[k] = tag
            self.readers[k] = []

    def op(self, eng, fn, reads=(), writes=()):
        rec = _Rec()
        fn(rec)
        name, a, k = rec.calls[0]
        fn = (lambda e, name=name, a=a, k=k: getattr(e, name)(*a, **k))
        waits = self._deps(eng, reads, writes)
        self.cnt[eng] += 1
        tag = (eng, self.cnt[eng])
        self.ops[eng].append((waits, fn, eng, 1))
        self._commit(tag, reads, writes)
        self.n_ops += 1

    def dma(self, out, in_, reads=(), writes=(), q='sp', **kw):
        d = 'd%d' % self.dma_rr
        self.dma_rr = (self.dma_rr + 1) % self.NDMA
        waits = self._deps(q, reads, writes)
        prev = self.cnt[d]
        if prev > self.waited[q].get(d, 0):
            waits.append((d, prev))
            self.waited[q][d] = prev
        self.cnt[d] += 16
        tag = (d, self.cnt[d])
        self.ops[q].append((waits, (lambda e: e.dma_start(out=out, in_=in_, **kw)), d, 16))
        self._commit(tag, reads, writes)
        self.n_ops += 1

    def barrier(self):
        allsems = list(self.sem_names)
        for e in self.ENGS:
            waits = []
            for f in allsems:
                if self.cnt[f] > self.waited[e].get(f, 0):
                    waits.append((f, self.cnt[f]))
                    self.waited[e][f] = self.cnt[f]
            self.ops[e].append((waits, None, None, 0))

    def finish(self, q='sp'):
        waits = []
        for i in range(self.NDMA):
            d = 'd%d' % i
            if self.cnt[d] > self.waited[q].get(d, 0):
                waits.append((d, self.cnt[d]))
                self.waited[q][d] = self.cnt[d]
        self.ops[q].append((waits, None, None, 0))

    def emit(self, sems):
        ops = self.ops

        def run(e, lst):
            for waits, fn, semname, inc in lst:
                for f, n in waits:
                    e.wait_ge(sems[f], n)
                if fn is not None:
                    fn(e).then_inc(sems[semname], inc)
        with self.nc.Block() as block:
            @block.tensor
            def _(e):
                run(e, ops['pe'])

            @block.vector
            def _(e):
                run(e, ops['dve'])

            @block.scalar
            def _(e):
                run(e, ops['act'])

            @block.gpsimd
            def _(e):
                run(e, ops['pool'])

            @block.sync
            def _(e):
                run(e, ops['sp'])


SEQS = [(0, TS, 0, True), (TS, TP, 1, False), (TS + TP, TP, 1, False)]


def build(debug=None):
    nc = bass.Bass('TRN2', target_bir_lowering=False)
    P = Prog(nc)
    es = contextlib.ExitStack()

    def din(name, shape):
        return nc.dram_tensor(name, list(shape), F32, kind="ExternalInput").ap()

    def dout(name, shape):
        return nc.dram_tensor(name, list(shape), F32, kind="ExternalOutput").ap()

    xin = din("xin", [TT, D])
    condT = din("condT", [128, 8, 2])
    s_hgrn = din("s_hgrn", [2, 8, 128, 128])
    s_rwkv = din("s_rwkv", [2, 16, 64, 64])
    s_C = din("s_C", [2, 4, 512, 512])
    s_n = din("s_n", [2, 4, 512])
    s_m = din("s_m", [2, 4])
    w_mod = din("w_mod", [2, D, 3 * D])
    b_modT = din("b_modT", [128, 2, 24])
    norm_gT = din("norm_gT", [128, 2, 8])
    fnorm_gT = din("fnorm_gT", [128, 8])
    wA = din("wA", [8, D, 640])
    wB = din("wB", [16, D, 256])
    wLR = din("wLR", [D, 256])
    w_out_even = din("w_out_even", [2 * D, D])
    lbT = din("lbT", [128, 2, 8])
    hg_gT = din("hg_gT", [128, 8])
    mu_rkv = din("mu_rkv", [64, 2, 4, 16])
    mu_lr = din("mu_lr", [64, 2, 4])
    w0T = din("w0T", [64, 2, 16])
    a0T = din("a0T", [64, 2, 16])
    w2 = din("w2", [2, 64, D])
    a2 = din("a2", [2, 64, D])
    kkT = din("kkT", [64, 16])
    kaT = din("kaT", [64, 16])
    rkT = din("rkT", [64, 16])
    gngT = din("gngT", [64, 16])
    gnbT = din("gnbT", [64, 16])
    maskR = din("maskR", [2, 3, 64, 128])
    maskH = din("maskH", [2, 128, 128])
    ident_d = din("ident_in", [128, 128])
    w_in_odd = din("w_in_odd", [D, 10256])
    w_out_odd = din("w_out_odd", [2 * D, D])
    maskC = din("maskC", [2, 128, 128])
    sel_d = din("sel_d", [36, 4, 128])
    gbT_d = din("gbT_d", [36, 4])
    cw_d = din("cw_d", [128, 32, 9])
    cb_d = din("cb_d", [128, 32])
    mng_d = din("mng_d", [128, 16])

    yout = dout("yout", [TT, D])
    o_hgrn = dout("o_hgrn", [2, 2, 8, 128, 128])
    o_rwkv = dout("o_rwkv", [2, 2, 16, 64, 64])
    o_C = dout("o_C", [2, 2, 4, 512, 512])
    o_n = dout("o_n", [2, 2, 4, 512])
    o_m = dout("o_m", [2, 2, 4])
    dbg = dout("dbg", [40, 128, TT]) if debug else None
    dslot = {}
    dumpt = {}
    x1 = dout("x1", [TT, D]) if debug else nc.dram_tensor("x1", [TT, D], F32, kind="Internal").ap()

    def sb(name, shape, dt=F32):
        return es.enter_context(nc.sbuf_tensor(name, list(shape), dt))

    pstiles = [es.enter_context(nc.psum_tensor("ps%d" % i, [128, 512], F32)) for i in range(8)]
    psrr = [0]

    def nps():
        i = psrr[0]
        psrr[0] = (i + 1) % 8
        return pstiles[i], 'ps%d' % i

    def dump(name, ap, keys, n, col0=0, parts=128):
        if not debug:
            return
        slot = dslot.setdefault(name, len(dslot))
        dt_ = dumpt['tile']
        for c0 in range(0, n, 512):
            w_ = min(512, n - c0)
            P.op('pool', (lambda e, c0=c0, w_=w_: e.tensor_copy(out=dt_[0:parts, 0:w_], in_=ap[:, c0:c0 + w_])),
                 reads=keys, writes=['dumpt'])
            P.dma(dbg[slot, 0:parts, col0 + c0:col0 + c0 + w_], dt_[0:parts, 0:w_], reads=['dumpt'])

    if debug:
        dumpt['tile'] = sb("dumpt", [128, 512])

    ident = sb("ident", [128, 128])
    ones = sb("ones", [128, 128])
    P.dma(ident[:], ident_d[:], writes=['ident'])
    P.op('dve', lambda e: e.memset(ones[:], 1.0), writes=['ones'])

    condT_sb = sb("condT_sb", [128, 8, 2])
    bmod_sb = sb("bmod_sb", [128, 2, 24])
    ng_sb = sb("ng_sb", [128, 2, 8])
    fng_sb = sb("fng_sb", [128, 8])
    lb_sb = sb("lb_sb", [128, 2, 8])
    hgg_sb = sb("hgg_sb", [128, 8])
    for t_, d_, k_ in [(condT_sb, condT, 'condT'), (bmod_sb, b_modT, 'bmod'), (ng_sb, norm_gT, 'ng'),
                       (fng_sb, fnorm_gT, 'fng'), (lb_sb, lbT, 'lb'), (hgg_sb, hg_gT, 'hgg')]:
        P.dma(t_[:], d_[:], writes=[k_])

    scT = sb("scT", [128, 8, 2])
    P.op('act', lambda e: e.activation(out=scT[:], in_=condT_sb[:], func=AF.Silu), reads=['condT'], writes=['scT'])
    mT = sb("mT", [128, 2, 24, 2])
    sc1 = sb("sc1", [128, 2, 8, 2])
    gate_bc = sb("gate_bc", [128, D])
    dg = sb("dg", [128, 128])

    def make_gate(l, c):
        if True:
            for half in range(2):
                pz, pk = nps()
                for kq in range(4):
                    kc = half * 4 + kq
                    P.op('dve', lambda e: e.tensor_scalar(
                        out=dg[:], in0=ident[:], scalar1=mT[:, l, 16 + kc, c:c + 1], scalar2=None, op0=ALU.mult),
                        reads=['ident', 'mT'], writes=['dg'])
                    P.op('pe', lambda e: e.matmul(pz[:, kq * 128:(kq + 1) * 128], ones[:], dg[:], start=True, stop=True),
                         reads=['ones', 'dg'], writes=[pk])
                P.op('act', lambda e: e.copy(out=gate_bc[:, half * 512:(half + 1) * 512], in_=pz[:]),
                     reads=[pk], writes=['gate_bc'])

    with contextlib.ExitStack() as es2:
        wm = [es2.enter_context(nc.sbuf_tensor("wm%d" % i, [128, 8, 512], F32)) for i in range(2)]
        for l in range(2):
            for cbk in range(6):
                i = (l * 6 + cbk) % 2
                P.dma(wm[i][:], w_mod[l].rearrange("(kc p) n -> p kc n", p=128)[:, :, cbk * 512:(cbk + 1) * 512],
                      writes=['wm%d' % i])
                pz, pk = nps()
                for j in range(4):
                    for kc in range(8):
                        P.op('pe', lambda e: e.matmul(pz[:, j * 2:(j + 1) * 2], wm[i][:, kc, j * 128:(j + 1) * 128],
                                                      scT[:, kc, :], start=(kc == 0), stop=(kc == 7)),
                             reads=['wm%d' % i, 'scT'], writes=[pk])
                P.op('dve', lambda e: e.tensor_tensor(out=mT[:, l, cbk * 4:(cbk + 1) * 4, :],
                                                      in0=pz[:, 0:8].rearrange("p (j c) -> p j c", c=2),
                                                      in1=bmod_sb[:, l, cbk * 4:(cbk + 1) * 4].unsqueeze(2).to_broadcast([128, 4, 2]),
                                                      op=ALU.add),
                     reads=[pk, 'bmod'], writes=['mT'])
    P.barrier()
    for l in range(2):
        P.op('dve', lambda e: e.scalar_tensor_tensor(
            out=sc1[:, l], in0=mT[:, l, 8:16, :], scalar=1.0,
            in1=ng_sb[:, l, :].unsqueeze(2).to_broadcast([128, 8, 2]), op0=ALU.add, op1=ALU.mult),
            reads=['mT', 'ng'], writes=['sc1'])
    fng_holder = {}

    def make_fng():
        fng_bc = sb("fng_bc", [128, D])
        fng_holder['t'] = fng_bc
        for half in range(2):
            pz, pk = nps()
            for kq in range(4):
                kc = half * 4 + kq
                P.op('dve', (lambda e, kc=kc: e.tensor_scalar(
                    out=dg[:], in0=ident[:], scalar1=fng_sb[:, kc:kc + 1], scalar2=None, op0=ALU.mult)),
                    reads=['ident', 'fng'], writes=['dg'])
                P.op('pe', (lambda e, pz=pz, kq=kq: e.matmul(pz[:, kq * 128:(kq + 1) * 128], ones[:], dg[:],
                                                             start=True, stop=True)),
                     reads=['ones', 'dg'], writes=[pk])
            P.op('act', (lambda e, half=half, pz=pz: e.copy(out=fng_bc[:, half * 512:(half + 1) * 512], in_=pz[:])),
                 reads=[pk], writes=['fng_bc'])

    lbv = sb("lbv", [128, 8])
    oml = sb("oml", [128, 8])
    P.op('dve', lambda e: e.tensor_tensor(out=lbv[:], in0=lb_sb[:, 0, :], in1=lb_sb[:, 1, :], op=ALU.subtract),
         reads=['lb'], writes=['lbv'])
    P.op('act', lambda e: e.activation(out=lbv[:], in_=lbv[:], func=AF.Sigmoid), reads=['lbv'], writes=['lbv'])
    P.op('act', lambda e: e.activation(out=oml[:], in_=lbv[:], func=AF.Identity, bias=1.0, scale=-1.0),
         reads=['lbv'], writes=['oml'])

    hT = sb("hT", [128, 8, TS], BF16)
    yT = sb("yT", [128, 8, TS], BF16)
    st4 = sb("st4", [128, 4])
    wst = sb("wst", [128, 8, 256])
    FT = [sb("FT%d" % i, [128, TS + 32]) for i in range(6)]
    xt = [FT[0][:, 0:D], FT[0][:, D:2 * D]]
    xn = FT[1][:, 0:D]
    junk = FT[1][:, D:2 * D]
    wo_v = [FT[2][:, 0:TS].bitcast(BF16).rearrange("p (s n) -> p s n", n=D),
            FT[3][:, 0:TS].bitcast(BF16).rearrange("p (s n) -> p s n", n=D)]

    def load_wo(src, parts, nk=8):
        v = src.rearrange("(kc p) n -> p kc n", p=parts)
        for c0 in range(0, D, 256):
            w_ = min(256, D - c0)
            P.dma(wst[0:parts, 0:nk, 0:w_], v[:, :, c0:c0 + w_], writes=['wst'])
            for hf in range(nk // 4):
                P.op('act', lambda e: e.copy(out=wo_v[hf][0:parts, :, c0:c0 + w_], in_=wst[0:parts, hf * 4:hf * 4 + 4, 0:w_]),
                     reads=['wst'], writes=['wo_bf'])

    def make_hT(layer, xsrc, xkey, off, T, cidx):
        for tt in range(T // 128):
            i = tt % 2
            P.dma(xt[i], xsrc[off + tt * 128: off + (tt + 1) * 128, :], reads=[(xkey, off // 128 + tt)], writes=['xt%d' % i])
            P.op('act', lambda e: e.activation(out=junk, in_=xt[i], func=AF.Square, accum_out=st4[:, 0:1]),
                 reads=['xt%d' % i], writes=['junk', 'st4'])
            P.op('dve', lambda e: e.tensor_scalar(out=st4[:, 1:2], in0=st4[:, 0:1], scalar1=1.0 / D, scalar2=1e-6,
                                                  op0=ALU.mult, op1=ALU.add), reads=['st4'], writes=['st4'])
            P.op('act', lambda e: e.activation(out=st4[:, 2:3], in_=st4[:, 1:2], func=AF.Sqrt), reads=['st4'], writes=['st4'])
            P.op('dve', lambda e: e.reciprocal(out=st4[:, 3:4], in_=st4[:, 2:3]), reads=['st4'], writes=['st4'])
            P.op('dve', lambda e: e.tensor_scalar(out=xn, in0=xt[i], scalar1=st4[:, 3:4], scalar2=None, op0=ALU.mult),
                 reads=['xt%d' % i, 'st4'], writes=['xn'])
            for half in range(2):
                pz, pk = nps()
                for kq in range(4):
                    kc = half * 4 + kq
                    P.op('pe', lambda e: e.transpose(out=pz[:, kq * 128:(kq + 1) * 128], in_=xn[:, kc * 128:(kc + 1) * 128],
                                                     identity=ident[:]), reads=['xn', 'ident'], writes=[pk])
                for kq in range(4):
                    kc = half * 4 + kq
                    P.op('act', lambda e: e.activation(
                        out=hT[:, kc, tt * 128:(tt + 1) * 128], in_=pz[:, kq * 128:(kq + 1) * 128], func=AF.Identity,
                        bias=mT[:, layer, kc, cidx:cidx + 1], scale=sc1[:, layer, kc, cidx:cidx + 1]),
                        reads=[pk, 'mT', 'sc1'], writes=[('hT', tt)])

    def hT_keys(t0, t1):
        return [('hT', tt) for tt in range(t0 // 128, (t1 + 127) // 128)]

    def load_w(src, ncols, dst=None, dkey='wbf', parts=128, nk=8):
        dst = wbf if dst is None else dst
        v = src.rearrange("(kc p) n -> p kc n", p=parts)
        for c0 in range(0, ncols, 256):
            w_ = min(256, ncols - c0)
            P.dma(wst[0:parts, 0:nk, 0:w_], v[:, :, c0:c0 + w_], writes=['wst'])
            P.op('act', lambda e: e.copy(out=dst[0:parts, 0:nk, c0:c0 + w_], in_=wst[0:parts, 0:nk, 0:w_]),
                 reads=['wst'], writes=[dkey])

    def proj(c0, M, t0, n, evac):
        pz, pk = nps()
        for kc in range(8):
            P.op('pe', lambda e: e.matmul(pz[0:M, 0:n], wbf[:, kc, c0:c0 + M], hT[:, kc, t0:t0 + n],
                                          start=(kc == 0), stop=(kc == 7)),
                 reads=['wbf'] + hT_keys(t0, t0 + n), writes=[pk])
        evac(pz, pk)

    def outproj(layer, groups, xsrc, skey, xdst, dkey, off, T, cidx, final=False):
        for tt in range(T // 128):
            i = tt % 2
            P.dma(xt[i], xsrc[off + tt * 128: off + (tt + 1) * 128, :], reads=[(skey, off // 128 + tt)], writes=['xt%d' % i])
            for half in range(2):
                pz, pk = nps()
                for gi, (K, slot) in enumerate(groups):
                    P.op('pe', lambda e: e.matmul(pz[:, 0:512], yT[0:K, slot, tt * 128:(tt + 1) * 128],
                                                  wo_v[slot // 4][0:K, slot % 4, half * 512:(half + 1) * 512],
                                                  start=(gi == 0), stop=(gi == len(groups) - 1)),
                         reads=[('yT', slot), 'wo_bf'], writes=[pk])
                P.op('dve', lambda e: e.tensor_tensor(out=xn[:, half * 512:(half + 1) * 512], in0=pz[:, 0:512],
                                                      in1=gate_bc[:, half * 512:(half + 1) * 512], op=ALU.mult),
                     reads=[pk, 'gate_bc'], writes=['xn'])
                P.op('dve', lambda e: e.tensor_tensor(out=xt[i][:, half * 512:(half + 1) * 512],
                                                      in0=xt[i][:, half * 512:(half + 1) * 512],
                                                      in1=xn[:, half * 512:(half + 1) * 512], op=ALU.add),
                     reads=['xn', 'xt%d' % i], writes=['xt%d' % i])
            if final:
                P.op('act', lambda e: e.activation(out=junk, in_=xt[i], func=AF.Square, accum_out=st4[:, 0:1]),
                     reads=['xt%d' % i], writes=['junk', 'st4'])
                P.op('dve', lambda e: e.tensor_scalar(out=st4[:, 1:2], in0=st4[:, 0:1], scalar1=1.0 / D, scalar2=1e-6,
                                                      op0=ALU.mult, op1=ALU.add), reads=['st4'], writes=['st4'])
                P.op('act', lambda e: e.activation(out=st4[:, 2:3], in_=st4[:, 1:2], func=AF.Sqrt), reads=['st4'], writes=['st4'])
                P.op('dve', lambda e: e.reciprocal(out=st4[:, 3:4], in_=st4[:, 2:3]), reads=['st4'], writes=['st4'])
                P.op('dve', lambda e: e.scalar_tensor_tensor(out=xt[i], in0=xt[i], scalar=st4[:, 3:4], in1=fng_holder['t'][:],
                                                             op0=ALU.mult, op1=ALU.mult),
                     reads=['xt%d' % i, 'st4', 'fng_bc'], writes=['xt%d' % i])
            P.dma(xdst[off + tt * 128: off + (tt + 1) * 128, :], xt[i], reads=['xt%d' % i], writes=[(dkey, off // 128 + tt)], q='pool')

    L0 = contextlib.ExitStack()

    def sb0(name, shape, dt=F32):
        return L0.enter_context(nc.sbuf_tensor(name, list(shape), dt))

    wbf = sb0("wbf", [128, 8, 384], BF16)
    TB = 256
    Fq, Fsz, Fvr, For = FT[0][:, 0:TS], FT[1][:, 0:TS], FT[2][:, 0:TS], FT[3][:, 0:TS]
    Fv = Fvr.rearrange("p (j c) -> p j c", c=128)
    Fo = For.rearrange("p (j c) -> p j c", c=128)
    BT = [sb0("BT%d" % i, [128, 256]) for i in range(18)]
    bt = {n_: BT[i] for i, n_ in enumerate(['sg', 'lf', 'kg', 'G', 'br', 'E', 'Ei', 'qt', 'kt', 'kh', 'vT'])}
    khtok = sb0("khtok", [128, TB // 128, 128])
    gam = sb0("gam", [128, TB // 32])
    gref = sb0("gref", [128, TB // 32])
    Sst = [sb0("Sst%d" % i, [128, 128]) for i in range(2)]
    attT = sb0("attT", [128, 128])
    ostat = sb0("ostat", [128, TS // 128, 4])
    mH = sb0("mH", [128, 2, 128])
    P.dma(mH[:], maskH.rearrange("d s t -> s d t"), writes=['mH'])

    def hgrn_head(h, off, T, is_sample, pidx):
        load_w(wA[h][:, 0:384], 384)
        tb = min(TB, T)
        nblk = T // tb
        for b in range(nblk):
            t0 = b * tb
            proj(0, 128, t0, tb, lambda pz, pk: P.op(
                'act', lambda e: e.copy(out=Fq[:, t0:t0 + tb], in_=pz[:, 0:tb]), reads=[pk], writes=['Fq']))
            proj(256, 128, t0, tb, lambda pz, pk: P.op(
                'act', lambda e: e.activation(out=Fsz[:, t0:t0 + tb], in_=pz[:, 0:tb], func=AF.Silu), reads=[pk], writes=['Fsz']))
            proj(128, 128, t0, tb, lambda pz, pk: P.op(
                'dve', lambda e: e.tensor_copy(out=bt['vT'][:, 0:tb], in_=pz[:, 0:tb]), reads=[pk], writes=['b_vT']))
            pz, pk = nps()
            for j in range(tb // 128):
                P.op('pe', lambda e: e.transpose(out=pz[:, j * 128:(j + 1) * 128], in_=bt['vT'][:, j * 128:(j + 1) * 128],
                                                 identity=ident[:]), reads=['b_vT', 'ident'], writes=[pk])
            P.op('dve', lambda e: e.tensor_copy(out=Fv[:, t0 // 128:(t0 + tb) // 128, :],
                                                in_=pz[:, 0:tb].rearrange("p (j c) -> p j c", c=128)),
                 reads=[pk], writes=['Fv'])
        load_w(wA[h][:, 384:640], 256)
        for d in range(2):
            rev = (d == 1)
            cur = 0
            if is_sample:
                P.dma(Sst[0][:], s_hgrn[d, h], writes=['Sst0'])
            else:
                P.op('pool', lambda e: e.memset(Sst[0][:], 0.0), writes=['Sst0'])
            blks = list(range(nblk))
            if rev:
                blks = blks[::-1]
            for b in blks:
                t0 = b * tb
                nch = tb // 32
                sg, lf, kg, G, br, E, Ei, qt, kt, kh = [bt[n_] for n_ in ['sg', 'lf', 'kg', 'G', 'br', 'E', 'Ei', 'qt', 'kt', 'kh']]
                proj(128 * d, 128, t0, tb, lambda pz, pk: P.op(
                    'act', lambda e: e.activation(out=sg[:, 0:tb], in_=pz[:, 0:tb], func=AF.Sigmoid), reads=[pk], writes=['b_sg']))
                P.op('dve', lambda e: e.tensor_scalar(out=sg[:, 0:tb], in0=sg[:, 0:tb], scalar1=oml[:, h:h + 1],
                                                      scalar2=lbv[:, h:h + 1], op0=ALU.mult, op1=ALU.add),
                     reads=['b_sg', 'oml', 'lbv'], writes=['b_sg'])
                P.op('act', lambda e: e.activation(out=lf[:, 0:tb], in_=sg[:, 0:tb], func=AF.Ln), reads=['b_sg'], writes=['b_lf'])
                P.op('dve', lambda e: e.tensor_scalar(out=kg[:, 0:tb], in0=sg[:, 0:tb], scalar1=-1.0, scalar2=1.0,
                                                       op0=ALU.mult, op1=ALU.add), reads=['b_sg'], writes=['b_kg'])
                P.op('dve', lambda e: e.memset(E[:, 0:tb], 0.0), writes=['b_E'])
                if not rev:
                    P.op('dve', lambda e: e.tensor_tensor_scan(out=G[:, 0:tb], data0=lf[:, 0:tb], data1=E[:, 0:tb],
                                                               initial=0.0, op0=ALU.add, op1=ALU.add),
                         reads=['b_lf', 'b_E'], writes=['b_G'])
                    ci_ = 0
                else:
                    P.op('dve', lambda e: e.tensor_tensor_scan(out=G[:, 0:tb][:, ::-1], data0=lf[:, 0:tb][:, ::-1],
                                                               data1=E[:, 0:tb], initial=0.0, op0=ALU.add, op1=ALU.add),
                         reads=['b_lf', 'b_E'], writes=['b_G'])
                    ci_ = 31
                G3 = G[:, 0:tb].rearrange("p (c l) -> p c l", l=32)
                lf3 = lf[:, 0:tb].rearrange("p (c l) -> p c l", l=32)
                P.op('dve', lambda e: e.tensor_tensor(out=gref[:, 0:nch], in0=G3[:, :, ci_], in1=lf3[:, :, ci_], op=ALU.subtract),
                     reads=['b_G', 'b_lf'], writes=['gref'])
                P.op('dve', lambda e: e.tensor_tensor(out=br[:, 0:tb].rearrange("p (c l) -> p c l", l=32), in0=G3,
                                                      in1=gref[:, 0:nch].unsqueeze(2).to_broadcast([128, nch, 32]), op=ALU.subtract),
                     reads=['b_G', 'gref'], writes=['b_br'])
                bend = br[:, 0:tb].rearrange("p (c l) -> p c l", l=32)[:, :, (0 if rev else 31)]
                P.op('act', lambda e: e.activation(out=gam[:, 0:nch], in_=bend, func=AF.Exp), reads=['b_br'], writes=['gam'])
                P.op('act', lambda e: e.activation(out=E[:, 0:tb], in_=br[:, 0:tb], func=AF.Exp), reads=['b_br'], writes=['b_E'])
                P.op('act', lambda e: e.activation(out=Ei[:, 0:tb], in_=br[:, 0:tb], func=AF.Exp, scale=-1.0),
                     reads=['b_br'], writes=['b_Ei'])
                P.op('dve', lambda e: e.tensor_tensor(out=qt[:, 0:tb], in0=Fq[:, t0:t0 + tb], in1=E[:, 0:tb], op=ALU.mult),
                     reads=['Fq', 'b_E'], writes=['b_qt'])
                P.op('dve', lambda e: e.tensor_tensor(out=kt[:, 0:tb], in0=kg[:, 0:tb], in1=Ei[:, 0:tb], op=ALU.mult),
                     reads=['b_kg', 'b_Ei'], writes=['b_kt'])
                P.op('dve', lambda e: e.tensor_tensor(out=kh[:, 0:tb].rearrange("p (c l) -> p c l", l=32),
                                                      in0=kt[:, 0:tb].rearrange("p (c l) -> p c l", l=32),
                                                      in1=gam[:, 0:nch].unsqueeze(2).to_broadcast([128, nch, 32]), op=ALU.mult),
                     reads=['b_kt', 'gam'], writes=['b_kh'])
                if debug and debug.get('inner') and h == head_ids[0]:
                    for n_ in ['lf', 'kg', 'br', 'E', 'qt', 'kt', 'kh']:
                        dump("%s_d%d" % (n_, d), bt[n_][:, 0:tb], ['b_' + n_], tb, col0=off + t0)
                pz, pk = nps()
                for j in range(tb // 128):
                    P.op('pe', lambda e: e.transpose(out=pz[:, j * 128:(j + 1) * 128], in_=kh[:, j * 128:(j + 1) * 128],
                                                     identity=ident[:]), reads=['b_kh', 'ident'], writes=[pk])
                P.op('act', lambda e: e.copy(out=khtok[:, 0:tb // 128, :], in_=pz[:, 0:tb].rearrange("p (j c) -> p j c", c=128)),
                     reads=[pk], writes=['khtok'])
                tiles = list(range(tb // 128))
                if rev:
                    tiles = tiles[::-1]
                for j in tiles:
                    tg = t0 // 128 + j
                    pa, pak = nps()
                    P.op('pe', lambda e: e.matmul(pa[:, 0:128], kt[:, j * 128:(j + 1) * 128], qt[:, j * 128:(j + 1) * 128],
                                                  start=True, stop=True), reads=['b_kt', 'b_qt'], writes=[pak])
                    P.op('dve', lambda e: e.tensor_tensor(out=attT[:], in0=pa[:, 0:128], in1=mH[:, d, :], op=ALU.mult),
                         reads=[pak, 'mH'], writes=['attT'])
                    po, pok = nps()
                    P.op('pe', lambda e: e.matmul(po[:, 0:128], attT[:], Fv[:, tg, :], start=True, stop=False),
                         reads=['attT', 'Fv'], writes=[pok])
                    chs = [0, 1, 2, 3]
                    if rev:
                        chs = chs[::-1]
                    for ci, c in enumerate(chs):
                        Scur = Sst[cur]
                        Snew = Sst[1 - cur]
                        P.op('pe', lambda e: e.matmul(
                            po[32 * c:32 * c + 32, 0:128], qt[:, j * 128 + 32 * c: j * 128 + 32 * c + 32], Scur[:],
                            start=False, stop=(ci == 3), tile_position=(0, 32 * c)),
                            reads=['b_qt', 'Sst%d' % cur], writes=[pok])
                        pd, pdk = nps()
                        P.op('pe', lambda e: e.matmul(
                            pd[:, 0:128], khtok[32 * c:32 * c + 32, j, :], Fv[32 * c:32 * c + 32, tg, :],
                            start=True, stop=True, tile_position=(32 * c, 0)),
                            reads=['khtok', 'Fv'], writes=[pdk])
                        gidx = j * 4 + c
                        P.op('dve', lambda e: e.scalar_tensor_tensor(
                            out=Snew[:], in0=Scur[:], scalar=gam[:, gidx:gidx + 1], in1=pd[:, 0:128],
                            op0=ALU.mult, op1=ALU.add),
                            reads=['Sst%d' % cur, 'gam', pdk], writes=['Sst%d' % (1 - cur)])
                        cur = 1 - cur
                    if d == 0:
                        P.op('act', lambda e: e.copy(out=Fo[:, tg, :], in_=po[:, 0:128]), reads=[pok], writes=[('Fo', tg)])
                    else:
                        P.op('dve', lambda e: e.tensor_tensor(out=Fo[:, tg, :], in0=Fo[:, tg, :], in1=po[:, 0:128], op=ALU.add),
                             reads=[pok, ('Fo', tg)], writes=[('Fo', tg)])
            if not is_sample:
                P.dma(o_hgrn[pidx, d, h], Sst[cur][:], reads=['Sst%d' % cur], q='pool')
        for tg in range(T // 128):
            P.op('act', lambda e: e.activation(out=attT[:], in_=Fo[:, tg, :], func=AF.Square, accum_out=ostat[:, tg, 0:1]),
                 reads=[('Fo', tg)], writes=['attT', ('ostat', tg)])
            P.op('dve', lambda e: e.tensor_scalar(out=ostat[:, tg, 1:2], in0=ostat[:, tg, 0:1], scalar1=1.0 / 128,
                                                  scalar2=1e-6, op0=ALU.mult, op1=ALU.add),
                 reads=[('ostat', tg)], writes=[('ostat', tg)])
            P.op('act', lambda e: e.activation(out=ostat[:, tg, 2:3], in_=ostat[:, tg, 1:2], func=AF.Sqrt),
                 reads=[('ostat', tg)], writes=[('ostat', tg)])
            P.op('dve', lambda e: e.reciprocal(out=ostat[:, tg, 3:4], in_=ostat[:, tg, 2:3]),
                 reads=[('ostat', tg)], writes=[('ostat', tg)])
            P.op('dve', lambda e: e.tensor_scalar(out=Fo[:, tg, :], in0=Fo[:, tg, :], scalar1=ostat[:, tg, 3:4],
                                                  scalar2=None, op0=ALU.mult),
                 reads=[('Fo', tg), ('ostat', tg)], writes=[('Fo', tg)])
        n4 = min(4, T // 128)
        for g4 in range(T // (128 * n4)):
            pz, pk = nps()
            for j in range(n4):
                tg = g4 * n4 + j
                P.op('pe', lambda e: e.transpose(out=pz[:, j * 128:(j + 1) * 128], in_=Fo[:, tg, :], identity=ident[:]),
                     reads=[('Fo', tg), 'ident'], writes=[pk])
            w_ = n4 * 128
            P.op('dve', lambda e: e.scalar_tensor_tensor(
                out=yT[:, h, g4 * w_:(g4 + 1) * w_], in0=pz[:, 0:w_], scalar=hgg_sb[:, h:h + 1],
                in1=Fsz[:, g4 * w_:(g4 + 1) * w_], op0=ALU.mult, op1=ALU.mult),
                reads=[pk, 'hgg', 'Fsz'], writes=[('yT', h)])

    TR = 256
    LWC = -0.6065306597126334
    LR = [sb0("LR%d" % g, [64, TS], BF16) for g in range(4)]
    rb = {n_: BT[i][0:64, :] for i, n_ in enumerate(
          ['lw', 'a', 'kk', 'kq', 'kap', 'kd', 'b', 'rk', 'G', 'br', 'E', 'Ei', 'Em', 'bh', 'kh', 'Kb', 'Bb', 't1'])}
    KR = sb0("r_KR", [64, 2, TR])
    cset = [{n_: sb0("c%d_%s" % (i_, n_), [64, (128 if n_ in ('AB', 'BB') else 64)])
             for n_ in ['AB', 'BB', 'XT0', 'XT1', 'X1', 'Xw', 'Pm0', 'Pm1', 'Vt', 'Kt', 'Bt']} for i_ in range(4)]
    rsq = {n_: sb0("rq_" + n_, [64, 64]) for n_ in ['U', 'Z0', 'Z1', 'zt']}
    rgam = sb0("rgam", [64, 8])
    rgref = sb0("rgref", [64, 4])
    mR = sb0("mR", [64, 2, 3, 128])
    P.dma(mR[:], maskR.rearrange("d m s t -> s d m t"), writes=['mR'])
    prm = {}
    for n_, src_, shp in [('mu_rkv', mu_rkv, [64, 2, 4, 16]), ('mu_lr', mu_lr, [64, 2, 4]), ('w0', w0T, [64, 2, 16]),
                          ('a0', a0T, [64, 2, 16]), ('kk', kkT, [64, 16]), ('ka', kaT, [64, 16]), ('rk', rkT, [64, 16]),
                          ('gng', gngT, [64, 16]), ('gnb', gnbT, [64, 16])]:
        prm[n_] = sb0("p_" + n_, shp)
        P.dma(prm[n_][:], src_[:], writes=['p_' + n_])
    c0_rkv = sb0("c0_rkv", [64, 4, 16])
    c0_lr = sb0("c0_lr", [64, 4])
    omka = sb0("omka", [64, 16])
    P.op('dve', lambda e: e.tensor_tensor(out=c0_rkv[:], in0=prm['mu_rkv'][:, 0], in1=prm['mu_rkv'][:, 1], op=ALU.add),
         reads=['p_mu_rkv'], writes=['c0_rkv'])
    P.op('dve', lambda e: e.tensor_scalar(out=c0_rkv[:], in0=c0_rkv[:], scalar1=-1.0, scalar2=1.0, op0=ALU.mult, op1=ALU.add),
         reads=['c0_rkv'], writes=['c0_rkv'])
    P.op('dve', lambda e: e.tensor_tensor(out=c0_lr[:], in0=prm['mu_lr'][:, 0], in1=prm['mu_lr'][:, 1], op=ALU.add),
         reads=['p_mu_lr'], writes=['c0_lr'])
    P.op('dve', lambda e: e.tensor_scalar(out=c0_lr[:], in0=c0_lr[:], scalar1=-1.0, scalar2=1.0, op0=ALU.mult, op1=ALU.add),
         reads=['c0_lr'], writes=['c0_lr'])
    P.op('dve', lambda e: e.tensor_scalar(out=omka[:], in0=prm['ka'][:], scalar1=-1.0, scalar2=1.0, op0=ALU.mult, op1=ALU.add),
         reads=['p_ka'], writes=['omka'])
    w2a2 = sb0("w2a2", [64, 4, D], BF16)
    for g, src_ in enumerate([w2[0], w2[1], a2[0], a2[1]]):
        for c0 in range(0, D, 256):
            P.dma(wst[0:64, 0, 0:256], src_[:, c0:c0 + 256], writes=['wst'])
            P.op('pool', lambda e: e.tensor_copy(out=w2a2[:, g, c0:c0 + 256], in_=wst[0:64, 0, 0:256]), reads=['wst'], writes=['w2a2'])

    def shift_into(dst, dkey, raw, rkey, T, c0ap, m0ap, m1ap, t1tile, eng='dve'):
        for s0 in range(0, T, 512):
            n = min(512, T - s0)
            P.op('dve', lambda e: e.tensor_scalar(out=t1tile[:, 0:n], in0=raw[:, 16 + s0:16 + s0 + n], scalar1=c0ap, scalar2=None,
                                                op0=ALU.mult), reads=[rkey], writes=['shift_t'])
            P.op('dve', lambda e: e.scalar_tensor_tensor(out=t1tile[:, 0:n], in0=raw[:, 15 + s0:15 + s0 + n], scalar=m0ap,
                                                       in1=t1tile[:, 0:n], op0=ALU.mult, op1=ALU.add),
                 reads=[rkey, 'shift_t'], writes=['shift_t'])
            P.op('dve', lambda e: e.scalar_tensor_tensor(out=dst[:, s0:s0 + n], in0=raw[:, 17 + s0:17 + s0 + n], scalar=m1ap,
                                                       in1=t1tile[:, 0:n], op0=ALU.mult, op1=ALU.add),
                 reads=[rkey, 'shift_t'], writes=[dkey])

    shiftt = sb0("shiftt", [64, 512])

    def rwkv_seq_setup(off, T):
        load_w(wLR, 256)
        pb = min(512, T)
        for g in range(4):
            raw = FT[g]
            P.op('pool', lambda e: e.memset(raw[0:64, 15:16], 0.0), writes=['FT%d' % g])
            P.op('pool', lambda e: e.memset(raw[0:64, T + 16:T + 17], 0.0), writes=['FT%d' % g])
            for b in range(T // pb):
                t0 = b * pb
                proj(64 * g, 64, t0, pb, lambda pz, pk: P.op(
                    'act', lambda e: e.copy(out=raw[0:64, 16 + t0:16 + t0 + pb], in_=pz[0:64, 0:pb]), reads=[pk], writes=['FT%d' % g]))
            shift_into(FT[4][0:64, :], 'FT4', raw[0:64, :], 'FT%d' % g, T, c0_lr[:, g:g + 1], prm['mu_lr'][:, 0, g:g + 1],
                       prm['mu_lr'][:, 1, g:g + 1], shiftt)
            if g < 2:
                P.op('act', lambda e: e.activation(out=LR[g][:, 0:T], in_=FT[4][0:64, 0:T], func=AF.Tanh), reads=['FT4'], writes=['LR%d' % g])
            else:
                P.op('act', lambda e: e.copy(out=LR[g][:, 0:T], in_=FT[4][0:64, 0:T]), reads=['FT4'], writes=['LR%d' % g])

    def rwkv_head(h, slot, off, T, is_sample, pidx):
        P.barrier()
        load_w(wB[h], 256)
        pb = min(512, T)
        nchT = T // 64
        for g in range(3):
            raw = FT[g]
            P.op('pool', lambda e: e.memset(raw[0:64, 15:16], 0.0), writes=['FT%d' % g])
            P.op('pool', lambda e: e.memset(raw[0:64, T + 16:T + 17], 0.0), writes=['FT%d' % g])
            for b in range(T // pb):
                t0 = b * pb
                proj(64 * g, 64, t0, pb, lambda pz, pk: P.op(
                    'act', lambda e: e.copy(out=raw[0:64, 16 + t0:16 + t0 + pb], in_=pz[0:64, 0:pb]), reads=[pk], writes=['FT%d' % g]))
            shift_into(FT[3 + g][0:64, :], 'FT%d' % (3 + g), raw[0:64, :], 'FT%d' % g, T, c0_rkv[:, g, h:h + 1],
                       prm['mu_rkv'][:, 0, g, h:h + 1], prm['mu_rkv'][:, 1, g, h:h + 1], shiftt, eng=('dve' if g != 1 else 'pool'))
        rS, kS, vS = FT[3][0:64, :], FT[4][0:64, :], FT[5][0:64, :]
        szb, yaccr, bonus = FT[0][0:64, :], FT[1][0:64, 0:T], FT[2][0:64, :]
        yacc = yaccr.rearrange("p (c v) -> p c v", v=64)
        for b in range(T // pb):
            t0 = b * pb
            proj(192, 64, t0, pb, lambda pz, pk: P.op(
                'act', lambda e: e.activation(out=szb[:, t0:t0 + pb], in_=pz[0:64, 0:pb], func=AF.Silu), reads=[pk], writes=['FT0']))
        tb = min(TR, T)
        nblk = T // tb
        P.barrier()
        for d in range(2):
            rev = (d == 1)
            cur = 0
            Zt = [rsq['Z0'], rsq['Z1']]
            if is_sample:
                P.dma(rsq['zt'][:], s_rwkv[d, h], writes=['rq_zt'])
                pz, pk = nps()
                P.op('pe', lambda e: e.transpose(out=pz[0:64, 0:64], in_=rsq['zt'][:], identity=ident[0:64, 0:64]),
                     reads=['rq_zt', 'ident'], writes=[pk])
                P.op('act', lambda e: e.copy(out=Zt[0][:], in_=pz[0:64, 0:64]), reads=[pk], writes=['rq_Z0'])
            else:
                P.op('pool', lambda e: e.memset(Zt[0][:], 0.0), writes=['rq_Z0'])
            blks = list(range(nblk))
            if rev:
                blks = blks[::-1]
            for b in blks:
                t0 = b * tb
                sl = slice(t0, t0 + tb)
                nch = tb // 64
                R_ = rb
                pz, pk = nps()
                P.op('pe', lambda e: e.matmul(pz[0:64, 0:tb], w2a2[:, d, h * 64:(h + 1) * 64], LR[d][:, sl], start=True, stop=True),
                     reads=['w2a2', 'LR%d' % d], writes=[pk])
                P.op('act', lambda e: e.activation(out=R_['lw'][:, 0:tb], in_=pz[0:64, 0:tb], func=AF.Sigmoid,
                                                   bias=prm['w0'][:, d, h:h + 1], scale=1.0), reads=[pk, 'p_w0'], writes=['r_lw'])
                pz, pk = nps()
                P.op('pe', lambda e: e.matmul(pz[0:64, 0:tb], w2a2[:, 2 + d, h * 64:(h + 1) * 64], LR[2 + d][:, sl], start=True, stop=True),
                     reads=['w2a2', 'LR%d' % (2 + d)], writes=[pk])
                P.op('act', lambda e: e.activation(out=R_['a'][:, 0:tb], in_=pz[0:64, 0:tb], func=AF.Sigmoid,
                                                   bias=prm['a0'][:, d, h:h + 1], scale=1.0), reads=[pk, 'p_a0'], writes=['r_a'])
                P.op('dve', lambda e: e.tensor_scalar(out=R_['kk'][:, 0:tb], in0=kS[:, sl], scalar1=prm['kk'][:, h:h + 1],
                                                      scalar2=None, op0=ALU.mult), reads=['FT4', 'p_kk'], writes=['r_kk'])
                P.op('dve', lambda e: e.tensor_tensor(out=R_['kq'][:, 0:tb], in0=R_['kk'][:, 0:tb], in1=R_['kk'][:, 0:tb], op=ALU.mult),
                     reads=['r_kk'], writes=['r_kq'])
                pz, pk = nps()
                P.op('pe', lambda e: e.matmul(pz[0:64, 0:tb], ones[0:64, 0:64], R_['kq'][:, 0:tb], start=True, stop=True),
                     reads=['ones', 'r_kq'], writes=[pk])
                P.op('dve', lambda e: e.tensor_scalar(out=R_['kq'][:, 0:tb], in0=pz[0:64, 0:tb], scalar1=1e-24, scalar2=None,
                                                      op0=ALU.max), reads=[pk], writes=['r_kq'])
                P.op('act', lambda e: e.activation(out=R_['kq'][:, 0:tb], in_=R_['kq'][:, 0:tb], func=AF.Ln), reads=['r_kq'], writes=['r_kq'])
                P.op('act', lambda e: e.activation(out=R_['kq'][:, 0:tb], in_=R_['kq'][:, 0:tb], func=AF.Exp, scale=-0.5),
                     reads=['r_kq'], writes=['r_kq'])
                P.op('dve', lambda e: e.tensor_tensor(out=R_['kap'][:, 0:tb], in0=R_['kk'][:, 0:tb], in1=R_['kq'][:, 0:tb], op=ALU.mult),
                     reads=['r_kk', 'r_kq'], writes=['r_kap'])
                P.op('dve', lambda e: e.tensor_scalar(out=R_['t1'][:, 0:tb], in0=R_['a'][:, 0:tb], scalar1=prm['ka'][:, h:h + 1],
                                                       scalar2=omka[:, h:h + 1], op0=ALU.mult, op1=ALU.add),
                     reads=['r_a', 'p_ka', 'omka'], writes=['r_t1'])
                P.op('dve', lambda e: e.tensor_tensor(out=R_['kd'][:, 0:tb], in0=kS[:, sl], in1=R_['t1'][:, 0:tb], op=ALU.mult),
                     reads=['FT4', 'r_t1'], writes=['r_kd'])
                P.op('dve', lambda e: e.tensor_tensor(out=R_['b'][:, 0:tb], in0=R_['a'][:, 0:tb], in1=R_['kap'][:, 0:tb], op=ALU.mult),
                     reads=['r_a', 'r_kap'], writes=['r_b'])
                P.op('dve', lambda e: e.scalar_tensor_tensor(out=R_['rk'][:, 0:tb], in0=rS[:, sl], scalar=prm['rk'][:, h:h + 1],
                                                             in1=R_['kd'][:, 0:tb], op0=ALU.mult, op1=ALU.mult),
                     reads=['FT3', 'p_rk', 'r_kd'], writes=['r_rk'])
                pz, pk = nps()
                P.op('pe', lambda e: e.matmul(pz[0:64, 0:tb], ones[0:64, 0:64], R_['rk'][:, 0:tb], start=True, stop=True),
                     reads=['ones', 'r_rk'], writes=[pk])
                if d == 0:
                    P.op('dve', lambda e: e.tensor_tensor(out=bonus[:, sl], in0=pz[0:64, 0:tb], in1=vS[:, sl], op=ALU.mult),
                         reads=[pk, 'FT5'], writes=[('bonus', b)])
                else:
                    P.op('dve', lambda e: e.tensor_tensor(out=R_['rk'][:, 0:tb], in0=pz[0:64, 0:tb], in1=vS[:, sl], op=ALU.mult),
                         reads=[pk, 'FT5'], writes=['r_rk'])
                    P.op('dve', lambda e: e.tensor_tensor(out=bonus[:, sl], in0=bonus[:, sl], in1=R_['rk'][:, 0:tb], op=ALU.add),
                         reads=['r_rk', ('bonus', b)], writes=[('bonus', b)])
                G, br, E, Ei, Em = R_['G'], R_['br'], R_['E'], R_['Ei'], R_['Em']
                P.op('dve', lambda e: e.memset(E[:, 0:tb], 0.0), writes=['r_E'])
                if not rev:
                    P.op('dve', lambda e: e.tensor_tensor_scan(out=G[:, 0:tb], data0=R_['lw'][:, 0:tb], data1=E[:, 0:tb],
                                                               initial=0.0, op0=ALU.add, op1=ALU.add),
                         reads=['r_lw', 'r_E'], writes=['r_G'])
                    ci_ = 0
                else:
                    P.op('dve', lambda e: e.tensor_tensor_scan(out=G[:, 0:tb][:, ::-1], data0=R_['lw'][:, 0:tb][:, ::-1],
                                                               data1=E[:, 0:tb], initial=0.0, op0=ALU.add, op1=ALU.add),
                         reads=['r_lw', 'r_E'], writes=['r_G'])
                    ci_ = 63
                G3 = G[:, 0:tb].rearrange("p (c l) -> p c l", l=64)
                lw3 = R_['lw'][:, 0:tb].rearrange("p (c l) -> p c l", l=64)
                P.op('dve', lambda e: e.tensor_tensor(out=rgref[:, 0:nch], in0=G3[:, :, ci_], in1=lw3[:, :, ci_], op=ALU.subtract),
                     reads=['r_G', 'r_lw'], writes=['rgref'])
                P.op('dve', lambda e: e.tensor_tensor(out=br[:, 0:tb].rearrange("p (c l) -> p c l", l=64), in0=G3,
                                                      in1=rgref[:, 0:nch].unsqueeze(2).to_broadcast([64, nch, 64]), op=ALU.subtract),
                     reads=['r_G', 'rgref'], writes=['r_br'])
                bend = br[:, 0:tb].rearrange("p (c l) -> p c l", l=64)[:, :, (0 if rev else 63)]
                P.op('act', lambda e: e.activation(out=rgam[:, 0:nch], in_=bend, func=AF.Exp, scale=LWC), reads=['r_br'], writes=['rgam'])
                P.op('dve', lambda e: e.tensor_scalar(out=rgam[:, 4:4 + nch], in0=rgam[:, 0:nch], scalar1=-1.0, scalar2=None,
                                                       op0=ALU.mult), reads=['rgam'], writes=['rgam'])
                P.op('act', lambda e: e.activation(out=E[:, 0:tb], in_=br[:, 0:tb], func=AF.Exp, scale=LWC), reads=['r_br'], writes=['r_E'])
                P.op('act', lambda e: e.activation(out=Ei[:, 0:tb], in_=br[:, 0:tb], func=AF.Exp, scale=-LWC),
                     reads=['r_br'], writes=['r_Ei'])
                P.op('dve', lambda e: e.tensor_tensor(out=R_['t1'][:, 0:tb], in0=br[:, 0:tb], in1=R_['lw'][:, 0:tb], op=ALU.subtract),
                     reads=['r_br', 'r_lw'], writes=['r_t1'])
                P.op('act', lambda e: e.activation(out=Em[:, 0:tb], in_=R_['t1'][:, 0:tb], func=AF.Exp, scale=LWC), reads=['r_t1'], writes=['r_Em'])
                P.op('dve', lambda e: e.tensor_tensor(out=KR[:, 0, 0:tb], in0=R_['kap'][:, 0:tb], in1=Em[:, 0:tb], op=ALU.mult),
                     reads=['r_kap', 'r_Em'], writes=['r_KR'])
                P.op('dve', lambda e: e.tensor_tensor(out=KR[:, 1, 0:tb], in0=rS[:, sl], in1=E[:, 0:tb], op=ALU.mult),
                     reads=['FT3', 'r_E', 'r_KR'], writes=['r_KR'])
                P.op('dve', lambda e: e.tensor_tensor(out=R_['bh'][:, 0:tb], in0=R_['b'][:, 0:tb], in1=Ei[:, 0:tb], op=ALU.mult),
                     reads=['r_b', 'r_Ei'], writes=['r_bh'])
                P.op('dve', lambda e: e.tensor_tensor(out=R_['kh'][:, 0:tb], in0=R_['kd'][:, 0:tb], in1=Ei[:, 0:tb], op=ALU.mult),
                     reads=['r_kd', 'r_Ei'], writes=['r_kh'])
                P.op('dve', lambda e: e.tensor_tensor(out=R_['Kb'][:, 0:tb].rearrange("p (c l) -> p c l", l=64),
                                                      in0=R_['kh'][:, 0:tb].rearrange("p (c l) -> p c l", l=64),
                                                      in1=rgam[:, 0:nch].unsqueeze(2).to_broadcast([64, nch, 64]), op=ALU.mult),
                     reads=['r_kh', 'rgam'], writes=['r_Kb'])
                P.op('dve', lambda e: e.tensor_tensor(out=R_['Bb'][:, 0:tb].rearrange("p (c l) -> p c l", l=64),
                                                      in0=R_['bh'][:, 0:tb].rearrange("p (c l) -> p c l", l=64),
                                                      in1=rgam[:, 4:4 + nch].unsqueeze(2).to_broadcast([64, nch, 64]), op=ALU.mult),
                     reads=['r_bh', 'rgam'], writes=['r_Bb'])
                chs = list(range(nch))
                if rev:
                    chs = chs[::-1]
                st = {}
                for c in chs:
                    cs = slice(c * 64, (c + 1) * 64)
                    C_ = cset[c]
                    ck = (lambda n_, c=c: 'c%d_%s' % (c, n_))
                    pA, pAk = nps()
                    P.op('pe', lambda e: e.matmul(pA[0:64, 0:128], R_['bh'][:, cs], KR[:, :, cs], start=True, stop=True),
                         reads=['r_bh', 'r_KR'], writes=[pAk])
                    P.op('pe', lambda e: e.matmul(pA[0:64, 128:256], R_['kh'][:, cs], KR[:, :, cs], start=True, stop=True),
                         reads=['r_kh', 'r_KR'], writes=[pAk])
                    P.op('pe', lambda e: e.matmul(pA[0:64, 256:320], KR[:, 0, cs], R_['bh'][:, cs], start=True, stop=True),
                         reads=['r_bh', 'r_KR'], writes=[pAk])
                    AB, BB = C_['AB'], C_['BB']
                    P.op('dve', lambda e: e.tensor_tensor(out=AB[:], in0=pA[0:64, 0:128], in1=mR[:, d, 0, :], op=ALU.mult),
                         reads=[pAk, 'mR'], writes=[ck('AB')])
                    P.op('dve', lambda e: e.tensor_tensor(out=BB[:], in0=pA[0:64, 128:256], in1=mR[:, d, 1, :], op=ALU.mult),
                         reads=[pAk, 'mR'], writes=[ck('BB')])
                    P.op('dve', lambda e: e.tensor_tensor(out=C_['XT0'][:], in0=pA[0:64, 256:320], in1=mR[:, d, 2, 0:64], op=ALU.mult),
                         reads=[pAk, 'mR'], writes=[ck('XT0')])
                    P.op('dve', lambda e: e.tensor_tensor(out=C_['Pm0'][:], in0=AB[:, 0:64], in1=ident[0:64, 0:64], op=ALU.add),
                         reads=[ck('AB'), 'ident'], writes=[ck('Pm0')])
                    st[c] = dict(X=AB[:, 0:64], Xk=ck('AB'), XT=C_['XT0'], XTk=ck('XT0'), xti=0, pmi=0)
                for lev in range(5):
                    for c in chs:
                        C_ = cset[c]
                        s_ = st[c]
                        ck = (lambda n_, c=c: 'c%d_%s' % (c, n_))
                        X, Xk, XT, XTk = s_['X'], s_['Xk'], s_['XT'], s_['XTk']
                        pq, pqk = nps()
                        nXTn = 'XT1' if s_['xti'] == 0 else 'XT0'
                        nXT, nXTk = C_[nXTn], ck(nXTn)
                        P.op('pe', lambda e: e.matmul(pq[0:64, 64:128], X, XT[:], start=True, stop=True), reads=[Xk, XTk], writes=[pqk])
                        if lev < 4:
                            P.op('pe', lambda e: e.matmul(pq[0:64, 0:64], XT[:], X, start=True, stop=True), reads=[Xk, XTk], writes=[pqk])
                        P.op('act', lambda e: e.copy(out=nXT[:], in_=pq[0:64, 64:128]), reads=[pqk], writes=[nXTk])
                        if lev < 4:
                            tn = 'X1' if lev % 2 == 0 else 'Xw'
                            P.op('act', lambda e: e.copy(out=C_[tn][:], in_=pq[0:64, 0:64]), reads=[pqk], writes=[ck(tn)])
                            s_['X'], s_['Xk'] = C_[tn][:], ck(tn)
                        s_['XT'], s_['XTk'], s_['xti'] = nXT, nXTk, 1 - s_['xti']
                    for c in chs:
                        C_ = cset[c]
                        s_ = st[c]
                        ck = (lambda n_, c=c: 'c%d_%s' % (c, n_))
                        nXT, nXTk = s_['XT'], s_['XTk']
                        pmi = s_['pmi']
                        Pc, Pn = C_['Pm%d' % pmi], C_['Pm%d' % (1 - pmi)]
                        pp, ppk = nps()
                        P.op('pe', lambda e: e.matmul(pp[0:64, 0:64], nXT[:], Pc[:], start=True, stop=True),
                             reads=[nXTk, ck('Pm%d' % pmi)], writes=[ppk])
                        P.op('dve', lambda e: e.tensor_tensor(out=Pn[:], in0=pp[0:64, 0:64], in1=Pc[:], op=ALU.add),
                             reads=[ppk, ck('Pm%d' % pmi)], writes=[ck('Pm%d' % (1 - pmi))])
                        s_['pmi'] = 1 - pmi
                for c in chs:
                    cs = slice(c * 64, (c + 1) * 64)
                    gsl = slice(t0 + c * 64, t0 + (c + 1) * 64)
                    C_ = cset[c]
                    pt, ptk = nps()
                    P.op('pe', lambda e: e.transpose(out=pt[0:64, 0:64], in_=vS[:, gsl], identity=ident[0:64, 0:64]),
                         reads=['FT5', 'ident'], writes=[ptk])
                    P.op('pe', lambda e: e.transpose(out=pt[0:64, 64:128], in_=R_['Kb'][:, cs], identity=ident[0:64, 0:64]),
                         reads=['r_Kb', 'ident'], writes=[ptk])
                    P.op('pe', lambda e: e.transpose(out=pt[0:64, 128:192], in_=R_['Bb'][:, cs], identity=ident[0:64, 0:64]),
                         reads=['r_Bb', 'ident'], writes=[ptk])
                    P.op('act', lambda e: e.copy(out=C_['Vt'][:], in_=pt[0:64, 0:64]), reads=[ptk], writes=['c%d_Vt' % c])
                    P.op('act', lambda e: e.copy(out=C_['Kt'][:], in_=pt[0:64, 64:128]), reads=[ptk], writes=['c%d_Kt' % c])
                    P.op('act', lambda e: e.copy(out=C_['Bt'][:], in_=pt[0:64, 128:192]), reads=[ptk], writes=['c%d_Bt' % c])
                for c in chs:
                    cs = slice(c * 64, (c + 1) * 64)
                    cg = (t0 // 64) + c
                    C_ = cset[c]
                    s_ = st[c]
                    AB, BB = C_['AB'], C_['BB']
                    ABk, BBk, Vtk, Ktk, Btk = ['c%d_%s' % (c, n_) for n_ in ('AB', 'BB', 'Vt', 'Kt', 'Bt')]
                    Pm, Pmk = C_['Pm%d' % s_['pmi']], 'c%d_Pm%d' % (c, s_['pmi'])
                    Zc, Zn = Zt[cur], Zt[1 - cur]
                    zck, znk = 'rq_Z%d' % cur, 'rq_Z%d' % (1 - cur)
                    pw, pwk = nps()
                    P.op('pe', lambda e: e.matmul(pw[0:64, 0:64], KR[:, 0, cs], Zc[:], start=True, stop=False),
                         reads=['r_KR', zck], writes=[pwk])
                    P.op('pe', lambda e: e.matmul(pw[0:64, 0:64], BB[:, 0:64], C_['Vt'][:], start=False, stop=True),
                         reads=[BBk, Vtk], writes=[pwk])
                    P.op('act', lambda e: e.copy(out=rsq['zt'][:], in_=pw[0:64, 0:64]), reads=[pwk], writes=['rq_zt'])
                    pu, puk = nps()
                    P.op('pe', lambda e: e.matmul(pu[0:64, 0:64], Pm[:], rsq['zt'][:], start=True, stop=True),
                         reads=[Pmk, 'rq_zt'], writes=[puk])
                    P.op('act', lambda e: e.copy(out=rsq['U'][:], in_=pu[0:64, 0:64]), reads=[puk], writes=['rq_U'])
                    pzz, pzk = nps()
                    P.op('pe', lambda e: e.matmul(pzz[0:64, 0:64], C_['Kt'][:], C_['Vt'][:], start=True, stop=False),
                         reads=[Ktk, Vtk], writes=[pzk])
                    P.op('pe', lambda e: e.matmul(pzz[0:64, 0:64], C_['Bt'][:], rsq['U'][:], start=False, stop=True),
                         reads=[Btk, 'rq_U'], writes=[pzk])
                    P.op('dve', lambda e: e.scalar_tensor_tensor(out=Zn[:], in0=Zc[:], scalar=rgam[:, c:c + 1], in1=pzz[0:64, 0:64],
                                                                 op0=ALU.mult, op1=ALU.add), reads=[zck, 'rgam', pzk], writes=[znk])
                    py, pyk = nps()
                    P.op('pe', lambda e: e.matmul(py[0:64, 0:64], KR[:, 1, cs], Zc[:], start=True, stop=False),
                         reads=['r_KR', zck], writes=[pyk])
                    P.op('pe', lambda e: e.matmul(py[0:64, 0:64], BB[:, 64:128], C_['Vt'][:], start=False, stop=False),
                         reads=[BBk, Vtk], writes=[pyk])
                    P.op('pe', lambda e: e.matmul(py[0:64, 0:64], AB[:, 64:128], rsq['U'][:], start=False, stop=True),
                         reads=[ABk, 'rq_U'], writes=[pyk])
                    if d == 0:
                        P.op('act', lambda e: e.copy(out=yacc[:, cg, :], in_=py[0:64, 0:64]), reads=[pyk], writes=[('yacc', cg)])
                    else:
                        P.op('dve', lambda e: e.tensor_tensor(out=yacc[:, cg, :], in0=yacc[:, cg, :], in1=py[0:64, 0:64], op=ALU.add),
                             reads=[pyk, ('yacc', cg)], writes=[('yacc', cg)])
                    cur = 1 - cur
            if not is_sample:
                pz, pk = nps()
                P.op('pe', lambda e: e.transpose(out=pz[0:64, 0:64], in_=Zt[cur][:], identity=ident[0:64, 0:64]),
                     reads=['rq_Z%d' % cur, 'ident'], writes=[pk])
                P.op('act', lambda e: e.copy(out=rsq['zt'][:], in_=pz[0:64, 0:64]), reads=[pk], writes=['rq_zt'])
                P.dma(o_rwkv[pidx, d, h], rsq['zt'][:], reads=['rq_zt'], q='pool')
        ykeys = [('yacc', c) for c in range(nchT)]
        gst = ostat[0:64, :, :].rearrange("p a b -> p (a b)")
        P.op('dve', lambda e: e.tensor_reduce(out=gst[:, 0:nchT], in_=yacc, axis=AX.X, op=ALU.add), reads=ykeys, writes=['gst'])
        P.op('dve', lambda e: e.tensor_scalar(out=gst[:, 0:nchT], in0=gst[:, 0:nchT], scalar1=-1.0 / 64, scalar2=None, op0=ALU.mult),
             reads=['gst'], writes=['gst'])
        P.op('dve', lambda e: e.tensor_tensor(out=yacc, in0=yacc, in1=gst[:, 0:nchT].unsqueeze(2).to_broadcast([64, nchT, 64]), op=ALU.add),
             reads=ykeys + ['gst'], writes=ykeys)
        sq = FT[3][0:64, 0:T].rearrange("p (c v) -> p c v", v=64)
        P.op('dve', lambda e: e.tensor_tensor(out=sq, in0=yacc, in1=yacc, op=ALU.mult), reads=ykeys, writes=['FT3'])
        P.op('dve', lambda e: e.tensor_reduce(out=gst[:, 32:32 + nchT], in_=sq, axis=AX.X, op=ALU.add), reads=['FT3'], writes=['gst'])
        P.op('dve', lambda e: e.tensor_scalar(out=gst[:, 32:32 + nchT], in0=gst[:, 32:32 + nchT], scalar1=1.0 / 64, scalar2=64e-5,
                                              op0=ALU.mult, op1=ALU.add), reads=['gst'], writes=['gst'])
        P.op('act', lambda e: e.activation(out=gst[:, 32:32 + nchT], in_=gst[:, 32:32 + nchT], func=AF.Sqrt), reads=['gst'], writes=['gst'])
        P.op('dve', lambda e: e.reciprocal(out=gst[:, 32:32 + nchT], in_=gst[:, 32:32 + nchT]), reads=['gst'], writes=['gst'])
        P.op('dve', lambda e: e.tensor_tensor(out=yacc, in0=yacc, in1=gst[:, 32:32 + nchT].unsqueeze(2).to_broadcast([64, nchT, 64]),
                                              op=ALU.mult), reads=ykeys + ['gst'], writes=ykeys)
        n8 = min(8, nchT)
        for g8 in range(nchT // n8):
            pz, pk = nps()
            for j in range(n8):
                cg = g8 * n8 + j
                P.op('pe', lambda e: e.transpose(out=pz[0:64, j * 64:(j + 1) * 64], in_=yacc[:, cg, :], identity=ident[0:64, 0:64]),
                     reads=[('yacc', cg), 'ident'], writes=[pk])
            w_ = n8 * 64
            gs = slice(g8 * w_, (g8 + 1) * w_)
            P.op('dve', lambda e: e.tensor_scalar(out=shiftt[:, 0:w_], in0=pz[0:64, 0:w_], scalar1=prm['gng'][:, h:h + 1],
                                                  scalar2=prm['gnb'][:, h:h + 1], op0=ALU.mult, op1=ALU.add),
                 reads=[pk, 'p_gng', 'p_gnb'], writes=['shift_t'])
            P.op('dve', lambda e: e.tensor_tensor(out=shiftt[:, 0:w_], in0=shiftt[:, 0:w_], in1=bonus[:, gs], op=ALU.add),
                 reads=['shift_t'] + [('bonus', b) for b in range(nblk)], writes=['shift_t'])
            P.op('dve', lambda e: e.tensor_tensor(out=yT[0:64, slot, gs], in0=shiftt[:, 0:w_], in1=szb[:, gs], op=ALU.mult),
                 reads=['shift_t', 'FT0'], writes=[('yT', slot)])

    seq_ids = debug.get('seqs', [0, 1, 2]) if debug else [0, 1, 2]
    head_ids = debug.get('heads', list(range(8))) if debug else list(range(8))
    rheads = debug.get('rheads', list(range(16))) if debug else list(range(16))
    for si in seq_ids:
        off, T, cidx, is_sample = SEQS[si]
        P.barrier()
        make_gate(0, cidx)
        make_hT(0, xin, 'xin', off, T, cidx)
        P.barrier()
        if debug and debug.get('inner'):
            dump("mT", mT[:, 0].rearrange("p a b -> p (a b)"), ['mT'], 48)
            dump("sc1", sc1[:, 0].rearrange("p a b -> p (a b)"), ['sc1'], 16)
            dump("scT", scT[:].rearrange("p a b -> p (a b)"), ['scT'], 16)
            dump("xn", xn, ['xn'], 1024)
            dump("xt0", xt[0], ['xt0'], 1024)
            for kc in range(8):
                dump("hT%d" % kc, hT[:, kc, 0:T], hT_keys(0, T), T, col0=off)
        for h in head_ids:
            hgrn_head(h, off, T, is_sample, si - 1)
            if debug and debug.get('dump_y'):
                dump("yT%d" % h, yT[:, h, 0:T], [('yT', h)], T, col0=off)
        P.barrier()
        load_wo(w_out_even[0:D, :], 128)
        outproj(0, [(128, s_) for s_ in range(8)], xin, 'xin', x1, 'x1', off, T, cidx)
        P.barrier()
        rwkv_seq_setup(off, T)
        for half in range(2):
            P.barrier()
            for slot in range(8):
                h = half * 8 + slot
                if h in rheads:
                    rwkv_head(h, slot, off, T, is_sample, si - 1)
                    if debug and debug.get('dump_y'):
                        dump("yR%d" % h, yT[0:64, slot, 0:T], [('yT', slot)], T, col0=off, parts=64)
            P.barrier()
            load_wo(w_out_even[D + half * 512: D + (half + 1) * 512, :], 64)
            outproj(0, [(64, s_) for s_ in range(8)], x1, 'x1', x1, 'x1', off, T, cidx)
    P.barrier()
    L0.close()

    if not (debug and debug.get('l0only')):
        L1 = contextlib.ExitStack()

        def sb1(name, shape, dt=F32):
            return L1.enter_context(nc.sbuf_tensor(name, list(shape), dt))

        make_fng()
        LC = 128
        DH = 512
        qT = yT[:, 4:8, :]
        kT = sb1("kT", [128, 4, TS], BF16)
        vch = sb1("vch", [128, DH], BF16)
        Cst = sb1("Cst", [128, 4, DH])
        Cbf = sb1("Cbf", [128, 4, DH], BF16)
        nst = sb1("nst", [128, 8])
        nbf = sb1("nbf", [128, 4], BF16)
        ktok = sb1("ktok", [128, DH], BF16)
        vw = sb1("vw", [128, DH], BF16)
        sTs = sb1("sTs", [128, 128], BF16)
        onesb = sb1("onesb", [128, 1], BF16)
        identb = sb1("identb", [128, 128], BF16)
        mC = sb1("mC", [128, 2, 128])
        SEL = sb1("SEL", [36, 4, 128])
        XA = sb1("XA", [36, TS])
        XB = sb1("XB", [36, TS])
        zrow = sb1("zrow", [36, 512])
        sm = {n_: sb1("sm_" + n_, [36, 16]) for n_ in ['ac', 'bl', 'M', 'MP', 'mu', 'al', 'gref', 'm0']}
        Wtok = sb1("Wtok", [128, 2, 16, 8])
        Wtokb = sb1("Wtokb", [128, 2, 16, 4], BF16)
        ALb = sb1("ALb", [128, 2, 4, 16])
        dstat = sb1("dstat", [128, 8])
        wGb = sb1("wGb", [128, 8, 16], BF16)
        gbT = sb1("gbT", [36, 4])
        ngbT = sb1("ngbT", [36, 4])
        cw = sb1("cw", [128, 32, 9])
        cb = sb1("cb", [128, 32])
        mng = sb1("mng", [128, 16])
        wbf1 = sb1("wbf1", [128, 8, DH], BF16)
        P.dma(mC[:], maskC.rearrange("d s t -> s d t"), writes=['mC'])
        P.dma(SEL[:], sel_d[:], writes=['SEL'])
        P.dma(gbT[:], gbT_d[:], writes=['gbT'])
        P.dma(cw[:], cw_d[:], writes=['cw'])
        P.dma(cb[:], cb_d[:], writes=['cb'])
        P.dma(mng[:], mng_d[:], writes=['mng'])
        P.op('dve', lambda e: e.memset(onesb[:], 1.0), writes=['onesb'])
        P.op('dve', lambda e: e.memset(zrow[:], 0.0), writes=['zrow'])
        P.op('dve', lambda e: e.tensor_copy(out=identb[:], in_=ident[:]), reads=['ident'], writes=['identb'])
        P.op('dve', lambda e: e.tensor_scalar(out=ngbT[:], in0=gbT[:], scalar1=-1.0, scalar2=None, op0=ALU.mult), reads=['gbT'], writes=['ngbT'])
        P.dma(wst[:, :, 0:16], w_in_odd[:, 10240:10256].rearrange("(kc p) n -> p kc n", p=128), writes=['wst'])
        P.op('pool', lambda e: e.tensor_copy(out=wGb[:], in_=wst[:, :, 0:16]), reads=['wst'], writes=['wGb'])
        LNK = float(np.log(DH ** -0.5))

        def load_w1(c0, ncols):
            v = w_in_odd[:, c0:c0 + ncols].rearrange("(kc p) n -> p kc n", p=128)
            for q0 in range(0, ncols, 256):
                w_ = min(256, ncols - q0)
                P.dma(wst[:, :, 0:w_], v[:, :, q0:q0 + w_], writes=['wst'])
                P.op('act', lambda e: e.copy(out=wbf1[:, :, q0:q0 + w_], in_=wst[:, :, 0:w_]), reads=['wst'], writes=['wbf1'])

        def gates_seq(T, is_sample):
            NC = T // LC
            pbk = min(512, T)
            for d in range(2):
                pb = 32 * d
                rows = slice(pb, pb + 4)
                for b in range(T // pbk):
                    t0 = b * pbk
                    pz, pk = nps()
                    for kc in range(8):
                        P.op('pe', lambda e: e.matmul(pz[pb:pb + 4, 0:pbk], wGb[:, kc, (2 + d) * 4:(3 + d) * 4], hT[:, kc, t0:t0 + pbk],
                                                      start=(kc == 0), stop=(kc == 7)), reads=['wGb'] + hT_keys(t0, t0 + pbk), writes=[pk])
                    P.op('act', lambda e: e.activation(out=XA[rows, t0:t0 + pbk], in_=pz[pb:pb + 4, 0:pbk], func=AF.Exp,
                                                       bias=ngbT[rows, 2 + d:3 + d], scale=-1.0), reads=[pk, 'ngbT'], writes=['XA'])
                P.op('act', lambda e: e.activation(out=XA[rows, 0:T], in_=XA[rows, 0:T], func=AF.Ln, bias=1.0, scale=1.0),
                     reads=['XA'], writes=['XA'])
                for b in range(T // pbk):
                    bs = slice(b * pbk, (b + 1) * pbk)
                    if d == 0:
                        P.op('dve', lambda e: e.tensor_tensor_scan(out=XB[rows, bs], data0=XA[rows, bs], data1=zrow[rows, 0:pbk],
                                                                   initial=0.0, op0=ALU.add, op1=ALU.add), reads=['XA', 'zrow'], writes=['XB'])
                    else:
                        P.op('dve', lambda e: e.tensor_tensor_scan(out=XB[rows, bs][:, ::-1], data0=XA[rows, bs][:, ::-1],
                                                                   data1=zrow[rows, 0:pbk], initial=0.0, op0=ALU.add, op1=ALU.add),
                             reads=['XA', 'zrow'], writes=['XB'])
                if d == 0:
                    ci_, ce_ = 0, LC - 1
                else:
                    ci_, ce_ = LC - 1, 0
                B3 = XB[rows, 0:T].rearrange("p (c l) -> p c l", l=LC)
                A3 = XA[rows, 0:T].rearrange("p (c l) -> p c l", l=LC)
                S = {k_: v_[rows, :] for k_, v_ in sm.items()}
                P.op('dve', lambda e: e.tensor_tensor(out=S['gref'][:, 0:NC], in0=B3[:, :, ci_], in1=A3[:, :, ci_], op=ALU.subtract),
                     reads=['XA', 'XB'], writes=['sm_gref'])
                P.op('dve', lambda e: e.tensor_tensor(out=B3, in0=B3, in1=S['gref'][:, 0:NC].unsqueeze(2).to_broadcast([4, NC, LC]),
                                                      op=ALU.subtract), reads=['XB', 'sm_gref'], writes=['XB'])
                for b in range(T // pbk):
                    t0 = b * pbk
                    pz, pk = nps()
                    for kc in range(8):
                        P.op('pe', lambda e: e.matmul(pz[pb:pb + 4, 0:pbk], wGb[:, kc, d * 4:(d + 1) * 4], hT[:, kc, t0:t0 + pbk],
                                                      start=(kc == 0), stop=(kc == 7)), reads=['wGb'] + hT_keys(t0, t0 + pbk), writes=[pk])
                    P.op('dve', lambda e: e.scalar_tensor_tensor(out=XA[rows, t0:t0 + pbk], in0=pz[pb:pb + 4, 0:pbk], scalar=gbT[rows, d:d + 1],
                                                                 in1=XB[rows, t0:t0 + pbk], op0=ALU.add, op1=ALU.add),
                         reads=[pk, 'gbT', 'XB', 'XA'], writes=['XA'])
                P.op('dve', lambda e: e.tensor_reduce(out=S['ac'][:, 0:NC], in_=A3, axis=AX.X, op=ALU.max), reads=['XA'], writes=['sm_ac'])
                P.op('dve', lambda e: e.tensor_scalar(out=S['bl'][:, 0:NC], in0=B3[:, :, ce_], scalar1=-1.0, scalar2=None, op0=ALU.mult),
                     reads=['XB'], writes=['sm_bl'])
                if is_sample:
                    P.dma(S['m0'][:, 0:1], s_m[d, :].rearrange("(h o) -> h o", o=1), writes=['sm_m0'])
                else:
                    P.op('dve', lambda e: e.memset(S['m0'][:, 0:1], 0.0), writes=['sm_m0'])
                if d == 0:
                    P.op('dve', lambda e: e.tensor_tensor_scan(out=S['M'][:, 0:NC], data0=S['ac'][:, 0:NC], data1=S['bl'][:, 0:NC],
                                                               initial=S['m0'][:, 0:1], op0=ALU.max, op1=ALU.add),
                         reads=['sm_ac', 'sm_bl', 'sm_m0'], writes=['sm_M'])
                    P.op('dve', lambda e: e.tensor_copy(out=S['MP'][:, 0:1], in_=S['m0'][:, 0:1]), reads=['sm_m0'], writes=['sm_MP'])
                    if NC > 1:
                        P.op('dve', lambda e: e.tensor_copy(out=S['MP'][:, 1:NC], in_=S['M'][:, 0:NC - 1]), reads=['sm_M', 'sm_MP'], writes=['sm_MP'])
                else:
                    P.op('dve', lambda e: e.tensor_tensor_scan(out=S['M'][:, 0:NC][:, ::-1], data0=S['ac'][:, 0:NC][:, ::-1],
                                                               data1=S['bl'][:, 0:NC][:, ::-1], initial=S['m0'][:, 0:1],
                                                               op0=ALU.max, op1=ALU.add),
                         reads=['sm_ac', 'sm_bl', 'sm_m0'], writes=['sm_M'])
                    P.op('dve', lambda e: e.tensor_copy(out=S['MP'][:, NC - 1:NC], in_=S['m0'][:, 0:1]), reads=['sm_m0'], writes=['sm_MP'])
                    if NC > 1:
                        P.op('dve', lambda e: e.tensor_copy(out=S['MP'][:, 0:NC - 1], in_=S['M'][:, 1:NC]), reads=['sm_M', 'sm_MP'], writes=['sm_MP'])
                P.op('dve', lambda e: e.tensor_tensor(out=S['mu'][:, 0:NC], in0=S['MP'][:, 0:NC], in1=S['ac'][:, 0:NC], op=ALU.max),
                     reads=['sm_MP', 'sm_ac'], writes=['sm_mu'])
                P.op('dve', lambda e: e.tensor_tensor(out=S['al'][:, 0:NC], in0=S['MP'][:, 0:NC], in1=S['mu'][:, 0:NC], op=ALU.subtract),
                     reads=['sm_MP', 'sm_mu'], writes=['sm_al'])
                P.op('act', lambda e: e.activation(out=S['al'][:, 0:NC], in_=S['al'][:, 0:NC], func=AF.Exp), reads=['sm_al'], writes=['sm_al'])
                mub = S['mu'][:, 0:NC].unsqueeze(2).to_broadcast([4, NC, LC])
                P.op('dve', lambda e: e.tensor_tensor(out=A3, in0=A3, in1=mub, op=ALU.subtract), reads=['XA', 'sm_mu'], writes=['XA'])
                P.op('dve', lambda e: e.tensor_tensor(out=B3, in0=B3, in1=mub, op=ALU.subtract), reads=['XB', 'sm_mu'], writes=['XB'])
                P.op('dve', lambda e: e.tensor_scalar(out=XA[rows, 0:T], in0=XA[rows, 0:T], scalar1=LNK, scalar2=None, op0=ALU.add),
                     reads=['XA'], writes=['XA'])
                P.op('act', lambda e: e.activation(out=XA[rows, 0:T], in_=XA[rows, 0:T], func=AF.Exp), reads=['XA'], writes=['XA'])
                P.op('act', lambda e: e.activation(out=XB[rows, 0:T], in_=XB[rows, 0:T], func=AF.Exp), reads=['XB'], writes=['XB'])
                pz, pk = nps()
                for c in range(NC):
                    P.op('pe', lambda e: e.transpose(out=pz[:, c * 8:c * 8 + 4], in_=XA[rows, c * LC:(c + 1) * LC],
                                                     identity=ident[rows, pb:pb + 4]), reads=['XA', 'ident'], writes=[pk])
                    P.op('pe', lambda e: e.transpose(out=pz[:, c * 8 + 4:c * 8 + 8], in_=XB[rows, c * LC:(c + 1) * LC],
                                                     identity=ident[rows, pb:pb + 4]), reads=['XB', 'ident'], writes=[pk])
                P.op('dve', lambda e: e.tensor_copy(out=Wtok[:, d, 0:NC, :], in_=pz[:, 0:NC * 8].rearrange("p (c k) -> p c k", k=8)),
                     reads=[pk], writes=['Wtok'])
                P.op('dve', lambda e: e.tensor_copy(out=Wtokb[:, d, 0:NC, :], in_=Wtok[:, d, 0:NC, 0:4]), reads=['Wtok'], writes=['Wtokb'])
                pz, pk = nps()
                for hd in range(4):
                    P.op('pe', lambda e: e.matmul(pz[:, hd * 16:hd * 16 + NC], SEL[rows, hd, :], S['al'][:, 0:NC], start=True, stop=True),
                         reads=['SEL', 'sm_al'], writes=[pk])
                P.op('dve', lambda e: e.tensor_copy(out=ALb[:, d, :, 0:NC], in_=pz[:, 0:64].rearrange("p (h c) -> p h c", c=16)[:, :, 0:NC]),
                     reads=[pk], writes=['ALb'])

        def conv_tile(dst, dkey, slot_j, widx, t0src, T, is_sample):
            X = FT[1][:, 0:T]
            A = FT[0][:, 0:T]
            if is_sample:
                R_, Cw = T // 64, 64
                taps = [(dr, dc) for dr in (-1, 0, 1) for dc in (-1, 0, 1)]
            else:
                R_, Cw = 1, T
                taps = [(0, dc) for dc in (-1, 0, 1)]
            X3 = X.rearrange("p (r c) -> p r c", c=Cw)
            A3 = A.rearrange("p (r c) -> p r c", c=Cw)
            P.op('dve', lambda e: e.tensor_scalar(out=A, in0=X, scalar1=cw[:, widx, 4:5], scalar2=None, op0=ALU.mult),
                 reads=['FT1', 'cw'], writes=['FT0'])
            for (dr, dc) in taps:
                if dr == 0 and dc == 0:
                    continue
                r0, r1 = max(0, -dr), R_ - max(0, dr)
                c0, c1 = max(0, -dc), Cw - max(0, dc)
                ti = (dr + 1) * 3 + (dc + 1)
                P.op('dve', lambda e: e.scalar_tensor_tensor(out=A3[:, r0:r1, c0:c1], in0=X3[:, r0 + dr:r1 + dr, c0 + dc:c1 + dc],
                                                             scalar=cw[:, widx, ti:ti + 1], in1=A3[:, r0:r1, c0:c1],
                                                             op0=ALU.mult, op1=ALU.add), reads=['FT1', 'FT0', 'cw'], writes=['FT0'])
            P.op('act', lambda e: e.activation(out=dst[:, slot_j, 0:T], in_=A, func=AF.Silu, bias=cb[:, widx:widx + 1], scale=1.0),
                 reads=['FT0', 'cb'], writes=[dkey])

        hacc = [FT[2 + i][:, 0:TS].rearrange("p (j e) -> p j e", e=DH) for i in range(4)]

        def mlstm_head(hd, off, T, is_sample, pidx):
            NC = T // LC
            NTt = T // 128
            pbk = min(512, T)
            for qk in range(2):
                load_w1(qk * 2048 + hd * DH, DH)
                for j in range(4):
                    for b in range(T // pbk):
                        t0 = b * pbk
                        pz, pk = nps()
                        for kc in range(8):
                            P.op('pe', lambda e: e.matmul(pz[:, 0:pbk], wbf1[:, kc, j * 128:(j + 1) * 128], hT[:, kc, t0:t0 + pbk],
                                                          start=(kc == 0), stop=(kc == 7)), reads=['wbf1'] + hT_keys(t0, t0 + pbk), writes=[pk])
                        P.op('act', lambda e: e.copy(out=FT[1][:, t0:t0 + pbk], in_=pz[:, 0:pbk]), reads=[pk], writes=['FT1'])
                    widx = (qk * 4 + hd) * 4 + j
                    if qk == 0:
                        conv_tile(qT, ('yT', 4 + j), j, widx, 0, T, is_sample)
                    else:
                        conv_tile(kT, 'kT', j, widx, 0, T, is_sample)
            load_w1(4096 + hd * DH, DH)
            qkeys = [('yT', 4 + j) for j in range(4)]
            for d in range(2):
                rev = (d == 1)
                if is_sample:
                    P.dma(Cst[:], s_C[d, hd].rearrange("(j p) e -> p j e", p=128), writes=['Cst'])
                    P.dma(nst[:, 0:4], s_n[d, hd].rearrange("(j p) -> p j", p=128), writes=['nst'], allow_slow_non_contiguous=True)
                else:
                    P.op('pool', lambda e: e.memset(Cst[:], 0.0), writes=['Cst'])
                    P.op('pool', lambda e: e.memset(nst[:, 0:4], 0.0), writes=['nst'])
                chunks = list(range(NC))
                if rev:
                    chunks = chunks[::-1]
                for c in chunks:
                    cs = slice(c * LC, (c + 1) * LC)
                    wcol = Wtok[:, d, c, hd:hd + 1]
                    thcol = Wtok[:, d, c, 4 + hd:5 + hd]
                    alcol = ALb[:, d, hd, c:c + 1]
                    pv, pvk = nps()
                    for kc in range(8):
                        P.op('pe', lambda e: e.matmul(pv[:, 0:DH], hT[:, kc, cs], wbf1[:, kc, :], start=(kc == 0), stop=(kc == 7)),
                             reads=['wbf1', ('hT', c)], writes=[pvk])
                    P.op('act', lambda e: e.copy(out=vch[:], in_=pv[:, 0:DH]), reads=[pvk], writes=['vch'])
                    pt, ptk = nps()
                    ptb = pt[:].bitcast(BF16)
                    for j in range(4):
                        P.op('pe', lambda e: e.transpose(out=ptb[:, j * 128:(j + 1) * 128], in_=kT[:, j, cs], identity=identb[:]),
                             reads=['kT', 'identb'], writes=[ptk])
                    P.op('act', lambda e: e.copy(out=ktok[:], in_=ptb[:, 0:DH]), reads=[ptk], writes=['ktok'])
                    ps_, psk = nps()
                    for j in range(4):
                        P.op('pe', lambda e: e.matmul(ps_[:, 0:128], kT[:, j, cs], qT[:, j, cs], start=(j == 0), stop=(j == 3)),
                             reads=['kT'] + qkeys, writes=[psk])
                    P.op('dve', lambda e: e.scalar_tensor_tensor(out=sTs[:], in0=ps_[:, 0:128], scalar=wcol, in1=mC[:, d, :],
                                                                 op0=ALU.mult, op1=ALU.mult), reads=[psk, 'Wtok', 'mC'], writes=['sTs'])
                    P.op('dve', lambda e: e.tensor_scalar(out=Cst[:], in0=Cst[:], scalar1=alcol, scalar2=None, op0=ALU.mult),
                         reads=['Cst', 'ALb'], writes=['Cst'])
                    P.op('act', lambda e: e.copy(out=Cbf[:], in_=Cst[:]), reads=['Cst'], writes=['Cbf'])
                    P.op('dve', lambda e: e.tensor_scalar(out=nst[:, 0:4], in0=nst[:, 0:4], scalar1=alcol, scalar2=None, op0=ALU.mult),
                         reads=['nst', 'ALb'], writes=['nst'])
                    P.op('dve', lambda e: e.tensor_copy(out=nbf[:], in_=nst[:, 0:4]), reads=['nst'], writes=['nbf'])
                    pn, pnk = nps()
                    for j in range(4):
                        P.op('pe', lambda e: e.matmul(pn[:, 0:DH], qT[:, j, cs], Cbf[:, j, :], start=(j == 0), stop=False),
                             reads=qkeys + ['Cbf'], writes=[pnk])
                    P.op('pe', lambda e: e.matmul(pn[:, 0:DH], sTs[:], vch[:], start=False, stop=True),
                         reads=['sTs', 'vch'], writes=[pnk])
                    pd_, pdk = nps()
                    for j in range(4):
                        P.op('pe', lambda e: e.matmul(pd_[:, 0:1], qT[:, j, cs], nbf[:, j:j + 1], start=(j == 0), stop=False),
                             reads=qkeys + ['nbf'], writes=[pdk])
                    P.op('pe', lambda e: e.matmul(pd_[:, 0:1], sTs[:], onesb[:], start=False, stop=True), reads=['sTs', 'onesb'], writes=[pdk])
                    P.op('act', lambda e: e.activation(out=dstat[:, 2:3], in_=pd_[:, 0:1], func=AF.Abs), reads=[pdk], writes=['dstat'])
                    P.op('dve', lambda e: e.tensor_tensor(out=dstat[:, 0:1], in0=dstat[:, 2:3], in1=thcol, op=ALU.max),
                         reads=['dstat', 'Wtok'], writes=['dstat'])
                    P.op('dve', lambda e: e.reciprocal(out=dstat[:, 1:2], in_=dstat[:, 0:1]), reads=['dstat'], writes=['dstat'])
                    hdst = hacc[c // 4][:, c % 4, :]
                    if d == 0:
                        P.op('act', lambda e: e.activation(out=hdst, in_=pn[:, 0:DH], func=AF.Identity, scale=dstat[:, 1:2]),
                             reads=[pnk, 'dstat'], writes=[('hacc', c)])
                    else:
                        P.op('dve', lambda e: e.scalar_tensor_tensor(out=hdst, in0=pn[:, 0:DH], scalar=dstat[:, 1:2], in1=hdst,
                                                                     op0=ALU.mult, op1=ALU.add), reads=[pnk, 'dstat', ('hacc', c)], writes=[('hacc', c)])
                    P.op('act', lambda e: e.activation(out=vw[:], in_=vch[:], func=AF.Identity, scale=wcol),
                         reads=['vch', 'Wtok'], writes=['vw'])
                    for j in range(4):
                        pc, pck = nps()
                        P.op('pe', lambda e: e.matmul(pc[:, 0:DH], ktok[:, j * 128:(j + 1) * 128], vw[:], start=True, stop=True),
                             reads=['ktok', 'vw'], writes=[pck])
                        P.op('dve', lambda e: e.tensor_tensor(out=Cst[:, j, :], in0=Cst[:, j, :], in1=pc[:, 0:DH], op=ALU.add),
                             reads=[pck, 'Cst'], writes=['Cst'])
                    pq_, pqk = nps()
                    for j in range(4):
                        P.op('pe', lambda e: e.matmul(pq_[:, j:j + 1], ktok[:, j * 128:(j + 1) * 128], Wtokb[:, d, c, hd:hd + 1],
                                                      start=True, stop=True), reads=['ktok', 'Wtokb'], writes=[pqk])
                    P.op('dve', lambda e: e.tensor_tensor(out=nst[:, 0:4], in0=nst[:, 0:4], in1=pq_[:, 0:4], op=ALU.add),
                         reads=[pqk, 'nst'], writes=['nst'])
                if not is_sample:
                    P.dma(o_C[pidx, d, hd].rearrange("(j p) e -> p j e", p=128), Cst[:], reads=['Cst'], q='pool')
                    P.dma(o_n[pidx, d, hd].rearrange("(j p) -> p j", p=128), nst[:, 0:4], reads=['nst'], q='pool', allow_slow_non_contiguous=True)
            load_w1(6144 + hd * DH, DH)
            for tt in range(NTt):
                hdst = hacc[tt // 4][:, tt % 4, :]
                pz, pk = nps()
                for kc in range(8):
                    P.op('pe', lambda e: e.matmul(pz[:, 0:DH], hT[:, kc, tt * 128:(tt + 1) * 128], wbf1[:, kc, :],
                                                  start=(kc == 0), stop=(kc == 7)), reads=['wbf1', ('hT', tt)], writes=[pk])
                P.op('act', lambda e: e.activation(out=FT[0][:, 0:DH], in_=pz[:, 0:DH], func=AF.Sigmoid), reads=[pk], writes=['FT0'])
                P.op('dve', lambda e: e.tensor_tensor(out=hdst, in0=hdst, in1=FT[0][:, 0:DH], op=ALU.mult),
                     reads=['FT0', ('hacc', tt)], writes=[('hacc', tt)])
                P.op('act', lambda e: e.activation(out=FT[0][:, 0:DH], in_=hdst, func=AF.Square, accum_out=dstat[:, 4:5]),
                     reads=[('hacc', tt), 'FT0'], writes=['FT0', 'dstat'])
                P.op('dve', lambda e: e.tensor_scalar(out=dstat[:, 5:6], in0=dstat[:, 4:5], scalar1=1.0 / DH, scalar2=1e-6,
                                                      op0=ALU.mult, op1=ALU.add), reads=['dstat'], writes=['dstat'])
                P.op('act', lambda e: e.activation(out=dstat[:, 6:7], in_=dstat[:, 5:6], func=AF.Sqrt), reads=['dstat'], writes=['dstat'])
                P.op('dve', lambda e: e.reciprocal(out=dstat[:, 7:8], in_=dstat[:, 6:7]), reads=['dstat'], writes=['dstat'])
                P.op('dve', lambda e: e.tensor_scalar(out=hdst, in0=hdst, scalar1=dstat[:, 7:8], scalar2=None, op0=ALU.mult),
                     reads=[('hacc', tt), 'dstat'], writes=[('hacc', tt)])
            load_w1(8192 + hd * DH, DH)
            for tt in range(NTt):
                hdst = hacc[tt // 4][:, tt % 4, :]
                pz, pk = nps()
                for kc in range(8):
                    P.op('pe', lambda e: e.matmul(pz[:, 0:DH], hT[:, kc, tt * 128:(tt + 1) * 128], wbf1[:, kc, :],
                                                  start=(kc == 0), stop=(kc == 7)), reads=['wbf1', ('hT', tt)], writes=[pk])
                P.op('act', lambda e: e.activation(out=FT[0][:, 0:DH], in_=pz[:, 0:DH], func=AF.Silu), reads=[pk], writes=['FT0'])
                P.op('dve', lambda e: e.tensor_tensor(out=hdst, in0=hdst, in1=FT[0][:, 0:DH], op=ALU.mult),
                     reads=['FT0', ('hacc', tt)], writes=[('hacc', tt)])
                pz, pk = nps()
                for j in range(4):
                    P.op('pe', lambda e: e.transpose(out=pz[:, j * 128:(j + 1) * 128], in_=hdst[:, j * 128:(j + 1) * 128], identity=ident[:]),
                         reads=[('hacc', tt), 'ident'], writes=[pk])
                for j in range(4):
                    P.op('act', lambda e: e.activation(out=yT[:, j, tt * 128:(tt + 1) * 128], in_=pz[:, j * 128:(j + 1) * 128],
                                                       func=AF.Identity, scale=mng[:, hd * 4 + j:hd * 4 + j + 1]),
                         reads=[pk, 'mng'], writes=[('yT', j)])

        for si in seq_ids:
            off, T, cidx, is_sample = SEQS[si]
            P.barrier()
            make_gate(1, cidx)
            make_hT(1, x1, 'x1', off, T, cidx)
            P.barrier()
            gates_seq(T, is_sample)
            if not is_sample:
                for d in range(2):
                    lastc = (T // LC - 1) if d == 0 else 0
                    P.dma(o_m[si - 1, d, :].rearrange("(h o) -> h o", o=1), sm['M'][32 * d:32 * d + 4, lastc:lastc + 1],
                          reads=['sm_M'], q='pool')
            for hd in range(4):
                P.barrier()
                mlstm_head(hd, off, T, is_sample, si - 1)
                if debug and debug.get('dump_y'):
                    for j in range(4):
                        dump("yM%d_%d" % (hd, j), yT[:, j, 0:T], [('yT', j)], T, col0=off)
                P.barrier()
                load_wo(w_out_odd[hd * DH:(hd + 1) * DH, :], 128, nk=4)
                last = (hd == 3)
                outproj(1, [(128, s_) for s_ in range(4)], x1, 'x1', (yout if last else x1), ('yout' if last else 'x1'),
                        off, T, cidx, final=last)
        P.barrier()
        L1.close()
    P.finish()
    sems = {s: es.enter_context(nc.semaphore(s)) for s in P.sem_names}
    P.emit(sems)
    es.close()
    global _last_dslot
    _last_dslot = dslot if debug else {}
    return nc, P


def host_inputs(inp, core):
    f = lambda a: np.ascontiguousarray(a, dtype=np.float32)
    b = core % 2
    m = {}
    m["xin"] = f(np.concatenate([inp["x_sample"][b], inp["x_prompt"][2 * core], inp["x_prompt"][2 * core + 1]], axis=0))
    cond = np.stack([inp["c"][b], inp["c_ctx"]], axis=0)
    m["condT"] = f(cond.reshape(2, 8, 128).transpose(2, 1, 0))
    m["s_hgrn"] = f(inp["state_hgrn"][b, 0])
    m["s_rwkv"] = f(inp["state_rwkv"][b, 0])
    m["s_C"] = f(inp["state_mlstm_C"][b, 0])
    m["s_n"] = f(inp["state_mlstm_n"][b, 0])
    m["s_m"] = f(inp["state_mlstm_m"][b, 0])
    m["w_mod"] = f(inp["w_mod"])
    m["b_modT"] = f(inp["b_mod"].reshape(2, 24, 128).transpose(2, 0, 1))
    m["norm_gT"] = f(inp["norm_g"].reshape(2, 8, 128).transpose(2, 0, 1))
    m["fnorm_gT"] = f(inp["final_norm_g"].reshape(8, 128).T)
    w = inp["w_in_even"][0]
    DA = 1024
    wA = np.stack([np.concatenate([w[:, g * DA + h * 128: g * DA + (h + 1) * 128] for g in (0, 1, 4, 2, 3)], axis=1)
                   for h in range(8)], axis=0)
    m["wA"] = f(wA)
    o = 5 * DA
    zb0 = o + 3328
    wB = np.stack([np.concatenate([w[:, o + g * 1024 + h * 64: o + g * 1024 + (h + 1) * 64] for g in (0, 1, 2)]
                                  + [w[:, zb0 + h * 64: zb0 + (h + 1) * 64]], axis=1) for h in range(16)], axis=0)
    m["wB"] = f(wB)
    m["wLR"] = f(w[:, o + 3072: o + 3328])
    m["w_out_even"] = f(inp["w_out_even"][0])
    m["lbT"] = f(inp["hgrn_lb_logits"].reshape(2, 8, 128).transpose(2, 0, 1))
    m["hg_gT"] = f(inp["hgrn_norm_g"][0].reshape(8, 128).T)
    mu = inp["rwkv_shift_mu"][0]
    mr = np.zeros((64, 2, 4, 16), np.float32)
    for g in range(3):
        mr[:, :, g, :] = mu[:, g * 1024:(g + 1) * 1024].reshape(2, 16, 64).transpose(2, 0, 1)
    m["mu_rkv"] = mr
    m["mu_lr"] = f(mu[:, 3072:3328].reshape(2, 4, 64).transpose(2, 0, 1))
    m["w0T"] = f(inp["rwkv_w0"][0].reshape(2, 16, 64).transpose(2, 0, 1))
    m["a0T"] = f(inp["rwkv_a0"][0].reshape(2, 16, 64).transpose(2, 0, 1))
    m["w2"] = f(inp["rwkv_w2"][0])
    m["a2"] = f(inp["rwkv_a2"][0])
    m["kkT"] = f(inp["rwkv_k_k"][0].reshape(16, 64).T)
    m["kaT"] = f(inp["rwkv_k_a"][0].reshape(16, 64).T)
    m["rkT"] = f(inp["rwkv_r_k"][0].T)
    m["gngT"] = f(inp["rwkv_gn_g"][0].reshape(16, 64).T)
    m["gnbT"] = f(inp["rwkv_gn_b"][0].reshape(16, 64).T)
    s = np.arange(128)[:, None]
    t = np.arange(128)[None, :]
    same = (s // 32) == (t // 32)
    m["maskH"] = np.stack([(same & (s <= t)), (same & (s >= t))]).astype(np.float32)
    m["ident_in"] = np.eye(128, dtype=np.float32)
    s6 = np.arange(64)[:, None]
    t6 = np.arange(64)[None, :]
    mr_ = np.zeros((2, 3, 64, 128), np.float32)
    for d_, (st_, inc_) in enumerate([((s6 < t6), (s6 <= t6)), ((s6 > t6), (s6 >= t6))]):
        st_ = st_.astype(np.float32)
        inc_ = inc_.astype(np.float32)
        mr_[d_, 0, :, 0:64] = -st_
        mr_[d_, 0, :, 64:128] = -inc_
        mr_[d_, 1, :, 0:64] = st_
        mr_[d_, 1, :, 64:128] = inc_
        mr_[d_, 2, :, 0:64] = -(st_.T)
    m["maskR"] = mr_
    m["w_in_odd"] = f(inp["w_in_odd"][0])
    m["w_out_odd"] = f(inp["w_out_odd"][0])
    m["maskC"] = np.stack([(s <= t), (s >= t)]).astype(np.float32)
    sel = np.zeros((36, 4, 128), np.float32)
    gbt = np.zeros((36, 4), np.float32)
    for pb_ in (0, 32):
        for k_ in range(4):
            sel[pb_ + k_, k_, :] = 1.0
        gbt[pb_:pb_ + 4, :] = inp["mlstm_gate_b"][0].T
    m["sel_d"] = sel
    m["gbT_d"] = gbt
    m["cw_d"] = f(inp["mlstm_conv_w"][0].reshape(9, 32, 128).transpose(2, 1, 0))
    m["cb_d"] = f(inp["mlstm_conv_b"][0].reshape(32, 128).T)
    m["mng_d"] = f(inp["mlstm_norm_g"][0].reshape(16, 128).T)
    return m


def kernel(**inp):
    inp = {k: np.asarray(v) for k, v in inp.items()}
    nc, P = build()
    in_maps = [host_inputs(inp, c) for c in range(NCORES)]
    res = run_bass_kernel_spmd(nc, in_maps, core_ids=list(range(NCORES)))
    r = res.results
    y_prompt = np.zeros((16, TP, D), np.float32)
    y_sample = np.zeros((2, TS, D), np.float32)
    for c in range(NCORES):
        y_prompt[2 * c] = r[c]["yout"][TS:TS + TP]
        y_prompt[2 * c + 1] = r[c]["yout"][TS + TP:]
    for b in range(2):
        y_sample[b] = r[b]["yout"][0:TS]
    new_hgrn = np.concatenate([r[c]["o_hgrn"] for c in range(NCORES)], axis=0)[:, None]
    new_rwkv = np.concatenate([r[c]["o_rwkv"] for c in range(NCORES)], axis=0)[:, None]
    new_C = np.concatenate([r[c]["o_C"] for c in range(NCORES)], axis=0)[:, None]
    new_n = np.concatenate([r[c]["o_n"] for c in range(NCORES)], axis=0)[:, None]
    new_m = np.concatenate([r[c]["o_m"] for c in range(NCORES)], axis=0)[:, None]
    return (y_prompt, y_sample, new_hgrn.astype(np.float32), new_rwkv.astype(np.float32),
            new_C.astype(np.float32), new_n.astype(np.float32), new_m.astype(np.float32))
```

```python
import contextlib
import numpy as np
import concourse.bass as bass
import concourse.mybir as mybir
from concourse.bass_utils import run_bass_kernel_spmd

F32 = mybir.dt.float32
BF16 = mybir.dt.bfloat16
ALU = mybir.AluOpType
AF = mybir.ActivationFunctionType
AX = mybir.AxisListType

D = 1024
TS = 2048
TP = 256
TT = TS + 2 * TP
NCORES = 8


class _Rec:
    def __init__(self):
        self.calls = []

    def __getattr__(self, name):
        def f(*a, **k):
            self.calls.append((name, a, k))
            return self
        return f


class Prog:
    ENGS = ['pe', 'dve', 'act', 'pool', 'sp']
    NDMA = 16

    def __init__(self, nc):
        self.nc = nc
        self.ops = {e: [] for e in self.ENGS}
        self.cnt = {}
        self.waited = {e: {} for e in self.ENGS}
        self.last_write = {}
        self.readers = {}
        self.dma_rr = 0
        self.sem_names = list(self.ENGS) + ['d%d' % i for i in range(self.NDMA)]
        for s in self.sem_names:
            self.cnt[s] = 0
        self.n_ops = 0

    def _deps(self, eng, reads, writes):
        deps = {}

        def add(p):
            if p is None:
                return
            f, n = p
            if f == 'pe' and eng == 'pe':
                return
            if n > deps.get(f, 0):
                deps[f] = n
        for k in reads:
            add(self.last_write.get(k))
        for k in writes:
            add(self.last_write.get(k))
            for p in self.readers.get(k, ()):
                add(p)
        waits = []
        for f, n in deps.items():
            if n > self.waited[eng].get(f, 0):
                waits.append((f, n))
                self.waited[eng][f] = n
        return waits

    def _commit(self, tag, reads, writes):
        for k in reads:
            lst = self.readers.setdefault(k, [])
            lst[:] = [p for p in lst if p[0] != tag[0]]
            lst.append(tag)
        for k in writes:
            self.last_write[k] = tag
            self.readers[k] = []

    def op(self, eng, fn, reads=(), writes=()):
        rec = _Rec()
        fn(rec)
        name, a, k = rec.calls[0]
        fn = (lambda e, name=name, a=a, k=k: getattr(e, name)(*a, **k))
        waits = self._deps(eng, reads, writes)
        self.cnt[eng] += 1
        tag = (eng, self.cnt[eng])
        self.ops[eng].append((waits, fn, eng, 1))
        self._commit(tag, reads, writes)
        self.n_ops += 1

    def dma(self, out, in_, reads=(), writes=(), q='sp', **kw):
        d = 'd%d' % self.dma_rr
        self.dma_rr = (self.dma_rr + 1) % self.NDMA
        waits = self._deps(q, reads, writes)
        prev = self.cnt[d]
        if prev > self.waited[q].get(d, 0):
            waits.append((d, prev))
            self.waited[q][d] = prev
        self.cnt[d] += 16
        tag = (d, self.cnt[d])
        self.ops[q].append((waits, (lambda e: e.dma_start(out=out, in_=in_, **kw)), d, 16))
        self._commit(tag, reads, writes)
        self.n_ops += 1

    def barrier(self):
        allsems = list(self.sem_names)
        for e in self.ENGS:
            waits = []
            for f in allsems:
                if self.cnt[f] > self.waited[e].get(f, 0):
                    waits.append((f, self.cnt[f]))
                    self.waited[e][f] = self.cnt[f]
            self.ops[e].append((waits, None, None, 0))

    def finish(self, q='sp'):
        waits = []
        for i in range(self.NDMA):
            d = 'd%d' % i
            if self.cnt[d] > self.waited[q].get(d, 0):
                waits.append((d, self.cnt[d]))
                self.waited[q][d] = self.cnt[d]
        self.ops[q].append((waits, None, None, 0))

    def emit(self, sems):
        ops = self.ops

        def run(e, lst):
            for waits, fn, semname, inc in lst:
                for f, n in waits:
                    e.wait_ge(sems[f], n)
                if fn is not None:
                    fn(e).then_inc(sems[semname], inc)
        with self.nc.Block() as block:
            @block.tensor
            def _(e):
                run(e, ops['pe'])

            @block.vector
            def _(e):
                run(e, ops['dve'])

            @block.scalar
            def _(e):
                run(e, ops['act'])

            @block.gpsimd
            def _(e):
                run(e, ops['pool'])

            @block.sync
            def _(e):
                run(e, ops['sp'])


SEQS = [(0, TS, 0, True), (TS, TP, 1, False), (TS + TP, TP, 1, False)]


def build(debug=None):
    nc = bass.Bass('TRN2', target_bir_lowering=False)
    P = Prog(nc)
    es = contextlib.ExitStack()

    def din(name, shape):
        return nc.dram_tensor(name, list(shape), F32, kind="ExternalInput").ap()

    def dout(name, shape):
        return nc.dram_tensor(name, list(shape), F32, kind="ExternalOutput").ap()

    xin = din("xin", [TT, D])
    condT = din("condT", [128, 8, 2])
    s_hgrn = din("s_hgrn", [2, 8, 128, 128])
    s_rwkv = din("s_rwkv", [2, 16, 64, 64])
    s_C = din("s_C", [2, 4, 512, 512])
    s_n = din("s_n", [2, 4, 512])
    s_m = din("s_m", [2, 4])
    w_mod = din("w_mod", [2, D, 3 * D])
    b_modT = din("b_modT", [128, 2, 24])
    norm_gT = din("norm_gT", [128, 2, 8])
    fnorm_gT = din("fnorm_gT", [128, 8])
    wA = din("wA", [8, D, 640])
    wB = din("wB", [16, D, 256])
    wLR = din("wLR", [D, 256])
    w_out_even = din("w_out_even", [2 * D, D])
    lbT = din("lbT", [128, 2, 8])
    hg_gT = din("hg_gT", [128, 8])
    mu_rkv = din("mu_rkv", [64, 2, 4, 16])
    mu_lr = din("mu_lr", [64, 2, 4])
    w0T = din("w0T", [64, 2, 16])
    a0T = din("a0T", [64, 2, 16])
    w2 = din("w2", [2, 64, D])
    a2 = din("a2", [2, 64, D])
    kkT = din("kkT", [64, 16])
    kaT = din("kaT", [64, 16])
    rkT = din("rkT", [64, 16])
    gngT = din("gngT", [64, 16])
    gnbT = din("gnbT", [64, 16])
    maskR = din("maskR", [2, 3, 64, 128])
    maskH = din("maskH", [2, 128, 128])
    ident_d = din("ident_in", [128, 128])
    w_in_odd = din("w_in_odd", [D, 10256])
    w_out_odd = din("w_out_odd", [2 * D, D])
    maskC = din("maskC", [2, 128, 128])
    sel_d = din("sel_d", [36, 4, 128])
    gbT_d = din("gbT_d", [36, 4])
    cw_d = din("cw_d", [128, 32, 9])
    cb_d = din("cb_d", [128, 32])
    mng_d = din("mng_d", [128, 16])

    yout = dout("yout", [TT, D])
    o_hgrn = dout("o_hgrn", [2, 2, 8, 128, 128])
    o_rwkv = dout("o_rwkv", [2, 2, 16, 64, 64])
    o_C = dout("o_C", [2, 2, 4, 512, 512])
    o_n = dout("o_n", [2, 2, 4, 512])
    o_m = dout("o_m", [2, 2, 4])
    dbg = dout("dbg", [40, 128, TT]) if debug else None
    dslot = {}
    dumpt = {}
    x1 = dout("x1", [TT, D]) if debug else nc.dram_tensor("x1", [TT, D], F32, kind="Internal").ap()

    def sb(name, shape, dt=F32):
        return es.enter_context(nc.sbuf_tensor(name, list(shape), dt))

    pstiles = [es.enter_context(nc.psum_tensor("ps%d" % i, [128, 512], F32)) for i in range(8)]
    psrr = [0]

    def nps():
        i = psrr[0]
        psrr[0] = (i + 1) % 8
        return pstiles[i], 'ps%d' % i

    def dump(name, ap, keys, n, col0=0, parts=128):
        if not debug:
            return
        slot = dslot.setdefault(name, len(dslot))
        dt_ = dumpt['tile']
        for c0 in range(0, n, 512):
            w_ = min(512, n - c0)
            P.op('pool', (lambda e, c0=c0, w_=w_: e.tensor_copy(out=dt_[0:parts, 0:w_], in_=ap[:, c0:c0 + w_])),
                 reads=keys, writes=['dumpt'])
            P.dma(dbg[slot, 0:parts, col0 + c0:col0 + c0 + w_], dt_[0:parts, 0:w_], reads=['dumpt'])

    if debug:
        dumpt['tile'] = sb("dumpt", [128, 512])

    ident = sb("ident", [128, 128])
    ones = sb("ones", [128, 128])
    P.dma(ident[:], ident_d[:], writes=['ident'])
    P.op('dve', lambda e: e.memset(ones[:], 1.0), writes=['ones'])

    condT_sb = sb("condT_sb", [128, 8, 2])
    bmod_sb = sb("bmod_sb", [128, 2, 24])
    ng_sb = sb("ng_sb", [128, 2, 8])
    fng_sb = sb("fng_sb", [128, 8])
    lb_sb = sb("lb_sb", [128, 2, 8])
    hgg_sb = sb("hgg_sb", [128, 8])
    for t_, d_, k_ in [(condT_sb, condT, 'condT'), (bmod_sb, b_modT, 'bmod'), (ng_sb, norm_gT, 'ng'),
                       (fng_sb, fnorm_gT, 'fng'), (lb_sb, lbT, 'lb'), (hgg_sb, hg_gT, 'hgg')]:
        P.dma(t_[:], d_[:], writes=[k_])

    scT = sb("scT", [128, 8, 2])
    P.op('act', lambda e: e.activation(out=scT[:], in_=condT_sb[:], func=AF.Silu), reads=['condT'], writes=['scT'])
    mT = sb("mT", [128, 2, 24, 2])
    sc1 = sb("sc1", [128, 2, 8, 2])
    gate_bc = sb("gate_bc", [128, D])
    dg = sb("dg", [128, 128])

    def make_gate(l, c):
        if True:
            for half in range(2):
                pz, pk = nps()
                for kq in range(4):
                    kc = half * 4 + kq
                    P.op('dve', lambda e: e.tensor_scalar(
                        out=dg[:], in0=ident[:], scalar1=mT[:, l, 16 + kc, c:c + 1], scalar2=None, op0=ALU.mult),
                        reads=['ident', 'mT'], writes=['dg'])
                    P.op('pe', lambda e: e.matmul(pz[:, kq * 128:(kq + 1) * 128], ones[:], dg[:], start=True, stop=True),
                         reads=['ones', 'dg'], writes=[pk])
                P.op('act', lambda e: e.copy(out=gate_bc[:, half * 512:(half + 1) * 512], in_=pz[:]),
                     reads=[pk], writes=['gate_bc'])

    with contextlib.ExitStack() as es2:
        wm = [es2.enter_context(nc.sbuf_tensor("wm%d" % i, [128, 8, 512], F32)) for i in range(2)]
        for l in range(2):
            for cbk in range(6):
                i = (l * 6 + cbk) % 2
                P.dma(wm[i][:], w_mod[l].rearrange("(kc p) n -> p kc n", p=128)[:, :, cbk * 512:(cbk + 1) * 512],
                      writes=['wm%d' % i])
                pz, pk = nps()
                for j in range(4):
                    for kc in range(8):
                        P.op('pe', lambda e: e.matmul(pz[:, j * 2:(j + 1) * 2], wm[i][:, kc, j * 128:(j + 1) * 128],
                                                      scT[:, kc, :], start=(kc == 0), stop=(kc == 7)),
                             reads=['wm%d' % i, 'scT'], writes=[pk])
                P.op('dve', lambda e: e.tensor_tensor(out=mT[:, l, cbk * 4:(cbk + 1) * 4, :],
                                                      in0=pz[:, 0:8].rearrange("p (j c) -> p j c", c=2),
                                                      in1=bmod_sb[:, l, cbk * 4:(cbk + 1) * 4].unsqueeze(2).to_broadcast([128, 4, 2]),
                                                      op=ALU.add),
                     reads=[pk, 'bmod'], writes=['mT'])
    P.barrier()
    for l in range(2):
        P.op('dve', lambda e: e.scalar_tensor_tensor(
            out=sc1[:, l], in0=mT[:, l, 8:16, :], scalar=1.0,
            in1=ng_sb[:, l, :].unsqueeze(2).to_broadcast([128, 8, 2]), op0=ALU.add, op1=ALU.mult),
            reads=['mT', 'ng'], writes=['sc1'])
    fng_holder = {}

    def make_fng():
        fng_bc = sb("fng_bc", [128, D])
        fng_holder['t'] = fng_bc
        for half in range(2):
            pz, pk = nps()
            for kq in range(4):
                kc = half * 4 + kq
                P.op('dve', (lambda e, kc=kc: e.tensor_scalar(
                    out=dg[:], in0=ident[:], scalar1=fng_sb[:, kc:kc + 1], scalar2=None, op0=ALU.mult)),
                    reads=['ident', 'fng'], writes=['dg'])
                P.op('pe', (lambda e, pz=pz, kq=kq: e.matmul(pz[:, kq * 128:(kq + 1) * 128], ones[:], dg[:],
                                                             start=True, stop=True)),
                     reads=['ones', 'dg'], writes=[pk])
            P.op('act', (lambda e, half=half, pz=pz: e.copy(out=fng_bc[:, half * 512:(half + 1) * 512], in_=pz[:])),
                 reads=[pk], writes=['fng_bc'])

    lbv = sb("lbv", [128, 8])
    oml = sb("oml", [128, 8])
    P.op('dve', lambda e: e.tensor_tensor(out=lbv[:], in0=lb_sb[:, 0, :], in1=lb_sb[:, 1, :], op=ALU.subtract),
         reads=['lb'], writes=['lbv'])
    P.op('act', lambda e: e.activation(out=lbv[:], in_=lbv[:], func=AF.Sigmoid), reads=['lbv'], writes=['lbv'])
    P.op('act', lambda e: e.activation(out=oml[:], in_=lbv[:], func=AF.Identity, bias=1.0, scale=-1.0),
         reads=['lbv'], writes=['oml'])

    hT = sb("hT", [128, 8, TS], BF16)
    yT = sb("yT", [128, 8, TS], BF16)
    st4 = sb("st4", [128, 4])
    wst = sb("wst", [128, 8, 256])
    FT = [sb("FT%d" % i, [128, TS + 32]) for i in range(6)]
    xt = [FT[0][:, 0:D], FT[0][:, D:2 * D]]
    xn = FT[1][:, 0:D]
    junk = FT[1][:, D:2 * D]
    wo_v = [FT[2][:, 0:TS].bitcast(BF16).rearrange("p (s n) -> p s n", n=D),
            FT[3][:, 0:TS].bitcast(BF16).rearrange("p (s n) -> p s n", n=D)]

    def load_wo(src, parts, nk=8):
        v = src.rearrange("(kc p) n -> p kc n", p=parts)
        for c0 in range(0, D, 256):
            w_ = min(256, D - c0)
            P.dma(wst[0:parts, 0:nk, 0:w_], v[:, :, c0:c0 + w_], writes=['wst'])
            for hf in range(nk // 4):
                P.op('act', lambda e: e.copy(out=wo_v[hf][0:parts, :, c0:c0 + w_], in_=wst[0:parts, hf * 4:hf * 4 + 4, 0:w_]),
                     reads=['wst'], writes=['wo_bf'])

    def make_hT(layer, xsrc, xkey, off, T, cidx):
        for tt in range(T // 128):
            i = tt % 2
            P.dma(xt[i], xsrc[off + tt * 128: off + (tt + 1) * 128, :], reads=[(xkey, off // 128 + tt)], writes=['xt%d' % i])
            P.op('act', lambda e: e.activation(out=junk, in_=xt[i], func=AF.Square, accum_out=st4[:, 0:1]),
                 reads=['xt%d' % i], writes=['junk', 'st4'])
            P.op('dve', lambda e: e.tensor_scalar(out=st4[:, 1:2], in0=st4[:, 0:1], scalar1=1.0 / D, scalar2=1e-6,
                                                  op0=ALU.mult, op1=ALU.add), reads=['st4'], writes=['st4'])
            P.op('act', lambda e: e.activation(out=st4[:, 2:3], in_=st4[:, 1:2], func=AF.Sqrt), reads=['st4'], writes=['st4'])
            P.op('dve', lambda e: e.reciprocal(out=st4[:, 3:4], in_=st4[:, 2:3]), reads=['st4'], writes=['st4'])
            P.op('dve', lambda e: e.tensor_scalar(out=xn, in0=xt[i], scalar1=st4[:, 3:4], scalar2=None, op0=ALU.mult),
                 reads=['xt%d' % i, 'st4'], writes=['xn'])
            for half in range(2):
                pz, pk = nps()
                for kq in range(4):
                    kc = half * 4 + kq
                    P.op('pe', lambda e: e.transpose(out=pz[:, kq * 128:(kq + 1) * 128], in_=xn[:, kc * 128:(kc + 1) * 128],
                                                     identity=ident[:]), reads=['xn', 'ident'], writes=[pk])
                for kq in range(4):
                    kc = half * 4 + kq
                    P.op('act', lambda e: e.activation(
                        out=hT[:, kc, tt * 128:(tt + 1) * 128], in_=pz[:, kq * 128:(kq + 1) * 128], func=AF.Identity,
                        bias=mT[:, layer, kc, cidx:cidx + 1], scale=sc1[:, layer, kc, cidx:cidx + 1]),
                        reads=[pk, 'mT', 'sc1'], writes=[('hT', tt)])

    def hT_keys(t0, t1):
        return [('hT', tt) for tt in range(t0 // 128, (t1 + 127) // 128)]

    def load_w(src, ncols, dst=None, dkey='wbf', parts=128, nk=8):
        dst = wbf if dst is None else dst
        v = src.rearrange("(kc p) n -> p kc n", p=parts)
        for c0 in range(0, ncols, 256):
            w_ = min(256, ncols - c0)
            P.dma(wst[0:parts, 0:nk, 0:w_], v[:, :, c0:c0 + w_], writes=['wst'])
            P.op('act', lambda e: e.copy(out=dst[0:parts, 0:nk, c0:c0 + w_], in_=wst[0:parts, 0:nk, 0:w_]),
                 reads=['wst'], writes=[dkey])

    def proj(c0, M, t0, n, evac):
        pz, pk = nps()
        for kc in range(8):
            P.op('pe', lambda e: e.matmul(pz[0:M, 0:n], wbf[:, kc, c0:c0 + M], hT[:, kc, t0:t0 + n],
                                          start=(kc == 0), stop=(kc == 7)),
                 reads=['wbf'] + hT_keys(t0, t0 + n), writes=[pk])
        evac(pz, pk)

    def outproj(layer, groups, xsrc, skey, xdst, dkey, off, T, cidx, final=False):
        for tt in range(T // 128):
            i = tt % 2
            P.dma(xt[i], xsrc[off + tt * 128: off + (tt + 1) * 128, :], reads=[(skey, off // 128 + tt)], writes=['xt%d' % i])
            for half in range(2):
                pz, pk = nps()
                for gi, (K, slot) in enumerate(groups):
                    P.op('pe', lambda e: e.matmul(pz[:, 0:512], yT[0:K, slot, tt * 128:(tt + 1) * 128],
                                                  wo_v[slot // 4][0:K, slot % 4, half * 512:(half + 1) * 512],
                                                  start=(gi == 0), stop=(gi == len(groups) - 1)),
                         reads=[('yT', slot), 'wo_bf'], writes=[pk])
                P.op('dve', lambda e: e.tensor_tensor(out=xn[:, half * 512:(half + 1) * 512], in0=pz[:, 0:512],
                                                      in1=gate_bc[:, half * 512:(half + 1) * 512], op=ALU.mult),
                     reads=[pk, 'gate_bc'], writes=['xn'])
                P.op('dve', lambda e: e.tensor_tensor(out=xt[i][:, half * 512:(half + 1) * 512],
                                                      in0=xt[i][:, half * 512:(half + 1) * 512],
                                                      in1=xn[:, half * 512:(half + 1) * 512], op=ALU.add),
                     reads=['xn', 'xt%d' % i], writes=['xt%d' % i])
            if final:
                P.op('act', lambda e: e.activation(out=junk, in_=xt[i], func=AF.Square, accum_out=st4[:, 0:1]),
                     reads=['xt%d' % i], writes=['junk', 'st4'])
                P.op('dve', lambda e: e.tensor_scalar(out=st4[:, 1:2], in0=st4[:, 0:1], scalar1=1.0 / D, scalar2=1e-6,
                                                      op0=ALU.mult, op1=ALU.add), reads=['st4'], writes=['st4'])
                P.op('act', lambda e: e.activation(out=st4[:, 2:3], in_=st4[:, 1:2], func=AF.Sqrt), reads=['st4'], writes=['st4'])
                P.op('dve', lambda e: e.reciprocal(out=st4[:, 3:4], in_=st4[:, 2:3]), reads=['st4'], writes=['st4'])
                P.op('dve', lambda e: e.scalar_tensor_tensor(out=xt[i], in0=xt[i], scalar=st4[:, 3:4], in1=fng_holder['t'][:],
                                                             op0=ALU.mult, op1=ALU.mult),
                     reads=['xt%d' % i, 'st4', 'fng_bc'], writes=['xt%d' % i])
            P.dma(xdst[off + tt * 128: off + (tt + 1) * 128, :], xt[i], reads=['xt%d' % i], writes=[(dkey, off // 128 + tt)], q='pool')

    L0 = contextlib.ExitStack()

    def sb0(name, shape, dt=F32):
        return L0.enter_context(nc.sbuf_tensor(name, list(shape), dt))

    wbf = sb0("wbf", [128, 8, 384], BF16)
    TB = 256
    Fq, Fsz, Fvr, For = FT[0][:, 0:TS], FT[1][:, 0:TS], FT[2][:, 0:TS], FT[3][:, 0:TS]
    Fv = Fvr.rearrange("p (j c) -> p j c", c=128)
    Fo = For.rearrange("p (j c) -> p j c", c=128)
    BT = [sb0("BT%d" % i, [128, 256]) for i in range(18)]
    bt = {n_: BT[i] for i, n_ in enumerate(['sg', 'lf', 'kg', 'G', 'br', 'E', 'Ei', 'qt', 'kt', 'kh', 'vT'])}
    khtok = sb0("khtok", [128, TB // 128, 128])
    gam = sb0("gam", [128, TB // 32])
    gref = sb0("gref", [128, TB // 32])
    Sst = [sb0("Sst%d" % i, [128, 128]) for i in range(2)]
    attT = sb0("attT", [128, 128])
    ostat = sb0("ostat", [128, TS // 128, 4])
    mH = sb0("mH", [128, 2, 128])
    P.dma(mH[:], maskH.rearrange("d s t -> s d t"), writes=['mH'])

    def hgrn_head(h, off, T, is_sample, pidx):
        load_w(wA[h][:, 0:384], 384)
        tb = min(TB, T)
        nblk = T // tb
        for b in range(nblk):
            t0 = b * tb
            proj(0, 128, t0, tb, lambda pz, pk: P.op(
                'act', lambda e: e.copy(out=Fq[:, t0:t0 + tb], in_=pz[:, 0:tb]), reads=[pk], writes=['Fq']))
            proj(256, 128, t0, tb, lambda pz, pk: P.op(
                'act', lambda e: e.activation(out=Fsz[:, t0:t0 + tb], in_=pz[:, 0:tb], func=AF.Silu), reads=[pk], writes=['Fsz']))
            proj(128, 128, t0, tb, lambda pz, pk: P.op(
                'dve', lambda e: e.tensor_copy(out=bt['vT'][:, 0:tb], in_=pz[:, 0:tb]), reads=[pk], writes=['b_vT']))
            pz, pk = nps()
            for j in range(tb // 128):
                P.op('pe', lambda e: e.transpose(out=pz[:, j * 128:(j + 1) * 128], in_=bt['vT'][:, j * 128:(j + 1) * 128],
                                                 identity=ident[:]), reads=['b_vT', 'ident'], writes=[pk])
            P.op('dve', lambda e: e.tensor_copy(out=Fv[:, t0 // 128:(t0 + tb) // 128, :],
                                                in_=pz[:, 0:tb].rearrange("p (j c) -> p j c", c=128)),
                 reads=[pk], writes=['Fv'])
        load_w(wA[h][:, 384:640], 256)
        for d in range(2):
            rev = (d == 1)
            cur = 0
            if is_sample:
                P.dma(Sst[0][:], s_hgrn[d, h], writes=['Sst0'])
            else:
                P.op('pool', lambda e: e.memset(Sst[0][:], 0.0), writes=['Sst0'])
            blks = list(range(nblk))
            if rev:
                blks = blks[::-1]
            for b in blks:
                t0 = b * tb
                nch = tb // 32
                sg, lf, kg, G, br, E, Ei, qt, kt, kh = [bt[n_] for n_ in ['sg', 'lf', 'kg', 'G', 'br', 'E', 'Ei', 'qt', 'kt', 'kh']]
                proj(128 * d, 128, t0, tb, lambda pz, pk: P.op(
                    'act', lambda e: e.activation(out=sg[:, 0:tb], in_=pz[:, 0:tb], func=AF.Sigmoid), reads=[pk], writes=['b_sg']))
                P.op('dve', lambda e: e.tensor_scalar(out=sg[:, 0:tb], in0=sg[:, 0:tb], scalar1=oml[:, h:h + 1],
                                                      scalar2=lbv[:, h:h + 1], op0=ALU.mult, op1=ALU.add),
                     reads=['b_sg', 'oml', 'lbv'], writes=['b_sg'])
                P.op('act', lambda e: e.activation(out=lf[:, 0:tb], in_=sg[:, 0:tb], func=AF.Ln), reads=['b_sg'], writes=['b_lf'])
                P.op('dve', lambda e: e.tensor_scalar(out=kg[:, 0:tb], in0=sg[:, 0:tb], scalar1=-1.0, scalar2=1.0,
                                                       op0=ALU.mult, op1=ALU.add), reads=['b_sg'], writes=['b_kg'])
                P.op('dve', lambda e: e.memset(E[:, 0:tb], 0.0), writes=['b_E'])
                if not rev:
                    P.op('dve', lambda e: e.tensor_tensor_scan(out=G[:, 0:tb], data0=lf[:, 0:tb], data1=E[:, 0:tb],
                                                               initial=0.0, op0=ALU.add, op1=ALU.add),
                         reads=['b_lf', 'b_E'], writes=['b_G'])
                    ci_ = 0
                else:
                    P.op('dve', lambda e: e.tensor_tensor_scan(out=G[:, 0:tb][:, ::-1], data0=lf[:, 0:tb][:, ::-1],
                                                               data1=E[:, 0:tb], initial=0.0, op0=ALU.add, op1=ALU.add),
                         reads=['b_lf', 'b_E'], writes=['b_G'])
                    ci_ = 31
                G3 = G[:, 0:tb].rearrange("p (c l) -> p c l", l=32)
                lf3 = lf[:, 0:tb].rearrange("p (c l) -> p c l", l=32)
                P.op('dve', lambda e: e.tensor_tensor(out=gref[:, 0:nch], in0=G3[:, :, ci_], in1=lf3[:, :, ci_], op=ALU.subtract),
                     reads=['b_G', 'b_lf'], writes=['gref'])
                P.op('dve', lambda e: e.tensor_tensor(out=br[:, 0:tb].rearrange("p (c l) -> p c l", l=32), in0=G3,
                                                      in1=gref[:, 0:nch].unsqueeze(2).to_broadcast([128, nch, 32]), op=ALU.subtract),
                     reads=['b_G', 'gref'], writes=['b_br'])
                bend = br[:, 0:tb].rearrange("p (c l) -> p c l", l=32)[:, :, (0 if rev else 31)]
                P.op('act', lambda e: e.activation(out=gam[:, 0:nch], in_=bend, func=AF.Exp), reads=['b_br'], writes=['gam'])
                P.op('act', lambda e: e.activation(out=E[:, 0:tb], in_=br[:, 0:tb], func=AF.Exp), reads=['b_br'], writes=['b_E'])
                P.op('act', lambda e: e.activation(out=Ei[:, 0:tb], in_=br[:, 0:tb], func=AF.Exp, scale=-1.0),
                     reads=['b_br'], writes=['b_Ei'])
                P.op('dve', lambda e: e.tensor_tensor(out=qt[:, 0:tb], in0=Fq[:, t0:t0 + tb], in1=E[:, 0:tb], op=ALU.mult),
                     reads=['Fq', 'b_E'], writes=['b_qt'])
                P.op('dve', lambda e: e.tensor_tensor(out=kt[:, 0:tb], in0=kg[:, 0:tb], in1=Ei[:, 0:tb], op=ALU.mult),
                     reads=['b_kg', 'b_Ei'], writes=['b_kt'])
                P.op('dve', lambda e: e.tensor_tensor(out=kh[:, 0:tb].rearrange("p (c l) -> p c l", l=32),
                                                      in0=kt[:, 0:tb].rearrange("p (c l) -> p c l", l=32),
                                                      in1=gam[:, 0:nch].unsqueeze(2).to_broadcast([128, nch, 32]), op=ALU.mult),
                     reads=['b_kt', 'gam'], writes=['b_kh'])
                if debug and debug.get('inner') and h == head_ids[0]:
                    for n_ in ['lf', 'kg', 'br', 'E', 'qt', 'kt', 'kh']:
                        dump("%s_d%d" % (n_, d), bt[n_][:, 0:tb], ['b_' + n_], tb, col0=off + t0)
                pz, pk = nps()
                for j in range(tb // 128):
                    P.op('pe', lambda e: e.transpose(out=pz[:, j * 128:(j + 1) * 128], in_=kh[:, j * 128:(j + 1) * 128],
                                                     identity=ident[:]), reads=['b_kh', 'ident'], writes=[pk])
                P.op('act', lambda e: e.copy(out=khtok[:, 0:tb // 128, :], in_=pz[:, 0:tb].rearrange("p (j c) -> p j c", c=128)),
                     reads=[pk], writes=['khtok'])
                tiles = list(range(tb // 128))
                if rev:
                    tiles = tiles[::-1]
                for j in tiles:
                    tg = t0 // 128 + j
                    pa, pak = nps()
                    P.op('pe', lambda e: e.matmul(pa[:, 0:128], kt[:, j * 128:(j + 1) * 128], qt[:, j * 128:(j + 1) * 128],
                                                  start=True, stop=True), reads=['b_kt', 'b_qt'], writes=[pak])
                    P.op('dve', lambda e: e.tensor_tensor(out=attT[:], in0=pa[:, 0:128], in1=mH[:, d, :], op=ALU.mult),
                         reads=[pak, 'mH'], writes=['attT'])
                    po, pok = nps()
                    P.op('pe', lambda e: e.matmul(po[:, 0:128], attT[:], Fv[:, tg, :], start=True, stop=False),
                         reads=['attT', 'Fv'], writes=[pok])
                    chs = [0, 1, 2, 3]
                    if rev:
                        chs = chs[::-1]
                    for ci, c in enumerate(chs):
                        Scur = Sst[cur]
                        Snew = Sst[1 - cur]
                        P.op('pe', lambda e: e.matmul(
                            po[32 * c:32 * c + 32, 0:128], qt[:, j * 128 + 32 * c: j * 128 + 32 * c + 32], Scur[:],
                            start=False, stop=(ci == 3), tile_position=(0, 32 * c)),
                            reads=['b_qt', 'Sst%d' % cur], writes=[pok])
                        pd, pdk = nps()
                        P.op('pe', lambda e: e.matmul(
                            pd[:, 0:128], khtok[32 * c:32 * c + 32, j, :], Fv[32 * c:32 * c + 32, tg, :],
                            start=True, stop=True, tile_position=(32 * c, 0)),
                            reads=['khtok', 'Fv'], writes=[pdk])
                        gidx = j * 4 + c
                        P.op('dve', lambda e: e.scalar_tensor_tensor(
                            out=Snew[:], in0=Scur[:], scalar=gam[:, gidx:gidx + 1], in1=pd[:, 0:128],
                            op0=ALU.mult, op1=ALU.add),
                            reads=['Sst%d' % cur, 'gam', pdk], writes=['Sst%d' % (1 - cur)])
                        cur = 1 - cur
                    if d == 0:
                        P.op('act', lambda e: e.copy(out=Fo[:, tg, :], in_=po[:, 0:128]), reads=[pok], writes=[('Fo', tg)])
                    else:
                        P.op('dve', lambda e: e.tensor_tensor(out=Fo[:, tg, :], in0=Fo[:, tg, :], in1=po[:, 0:128], op=ALU.add),
                             reads=[pok, ('Fo', tg)], writes=[('Fo', tg)])
            if not is_sample:
                P.dma(o_hgrn[pidx, d, h], Sst[cur][:], reads=['Sst%d' % cur], q='pool')
        for tg in range(T // 128):
            P.op('act', lambda e: e.activation(out=attT[:], in_=Fo[:, tg, :], func=AF.Square, accum_out=ostat[:, tg, 0:1]),
                 reads=[('Fo', tg)], writes=['attT', ('ostat', tg)])
            P.op('dve', lambda e: e.tensor_scalar(out=ostat[:, tg, 1:2], in0=ostat[:, tg, 0:1], scalar1=1.0 / 128,
                                                  scalar2=1e-6, op0=ALU.mult, op1=ALU.add),
                 reads=[('ostat', tg)], writes=[('ostat', tg)])
            P.op('act', lambda e: e.activation(out=ostat[:, tg, 2:3], in_=ostat[:, tg, 1:2], func=AF.Sqrt),
                 reads=[('ostat', tg)], writes=[('ostat', tg)])
            P.op('dve', lambda e: e.reciprocal(out=ostat[:, tg, 3:4], in_=ostat[:, tg, 2:3]),
                 reads=[('ostat', tg)], writes=[('ostat', tg)])
            P.op('dve', lambda e: e.tensor_scalar(out=Fo[:, tg, :], in0=Fo[:, tg, :], scalar1=ostat[:, tg, 3:4],
                                                  scalar2=None, op0=ALU.mult),
                 reads=[('Fo', tg), ('ostat', tg)], writes=[('Fo', tg)])
        n4 = min(4, T // 128)
        for g4 in range(T // (128 * n4)):
            pz, pk = nps()
            for j in range(n4):
                tg = g4 * n4 + j
                P.op('pe', lambda e: e.transpose(out=pz[:, j * 128:(j + 1) * 128], in_=Fo[:, tg, :], identity=ident[:]),
                     reads=[('Fo', tg), 'ident'], writes=[pk])
            w_ = n4 * 128
            P.op('dve', lambda e: e.scalar_tensor_tensor(
                out=yT[:, h, g4 * w_:(g4 + 1) * w_], in0=pz[:, 0:w_], scalar=hgg_sb[:, h:h + 1],
                in1=Fsz[:, g4 * w_:(g4 + 1) * w_], op0=ALU.mult, op1=ALU.mult),
                reads=[pk, 'hgg', 'Fsz'], writes=[('yT', h)])

    TR = 256
    LWC = -0.6065306597126334
    LR = [sb0("LR%d" % g, [64, TS], BF16) for g in range(4)]
    rb = {n_: BT[i][0:64, :] for i, n_ in enumerate(
          ['lw', 'a', 'kk', 'kq', 'kap', 'kd', 'b', 'rk', 'G', 'br', 'E', 'Ei', 'Em', 'bh', 'kh', 'Kb', 'Bb', 't1'])}
    KR = sb0("r_KR", [64, 2, TR])
    cset = [{n_: sb0("c%d_%s" % (i_, n_), [64, (128 if n_ in ('AB', 'BB') else 64)],
                     (BF16 if n_ in ('XTa', 'XTb', 'Xa', 'Xb', 'Pm0', 'Pm1') else F32))
             for n_ in ['AB', 'BB', 'XT0', 'XTa', 'XTb', 'Xa', 'Xb', 'Pm0', 'Pm1', 'Vt', 'Kt', 'Bt']} for i_ in range(4)]
    rsq = {n_: sb0("rq_" + n_, [64, 64]) for n_ in ['U', 'Z0', 'Z1', 'zt']}
    rsq['Wb'] = sb0("rq_Wb", [64, 64], BF16)
    rgam = sb0("rgam", [64, 8])
    rgref = sb0("rgref", [64, 4])
    mR = sb0("mR", [64, 2, 3, 128])
    P.dma(mR[:], maskR.rearrange("d m s t -> s d m t"), writes=['mR'])
    prm = {}
    for n_, src_, shp in [('mu_rkv', mu_rkv, [64, 2, 4, 16]), ('mu_lr', mu_lr, [64, 2, 4]), ('w0', w0T, [64, 2, 16]),
                          ('a0', a0T, [64, 2, 16]), ('kk', kkT, [64, 16]), ('ka', kaT, [64, 16]), ('rk', rkT, [64, 16]),
                          ('gng', gngT, [64, 16]), ('gnb', gnbT, [64, 16])]:
        prm[n_] = sb0("p_" + n_, shp)
        P.dma(prm[n_][:], src_[:], writes=['p_' + n_])
    c0_rkv = sb0("c0_rkv", [64, 4, 16])
    c0_lr = sb0("c0_lr", [64, 4])
    omka = sb0("omka", [64, 16])
    P.op('dve', lambda e: e.tensor_tensor(out=c0_rkv[:], in0=prm['mu_rkv'][:, 0], in1=prm['mu_rkv'][:, 1], op=ALU.add),
         reads=['p_mu_rkv'], writes=['c0_rkv'])
    P.op('dve', lambda e: e.tensor_scalar(out=c0_rkv[:], in0=c0_rkv[:], scalar1=-1.0, scalar2=1.0, op0=ALU.mult, op1=ALU.add),
         reads=['c0_rkv'], writes=['c0_rkv'])
    P.op('dve', lambda e: e.tensor_tensor(out=c0_lr[:], in0=prm['mu_lr'][:, 0], in1=prm['mu_lr'][:, 1], op=ALU.add),
         reads=['p_mu_lr'], writes=['c0_lr'])
    P.op('dve', lambda e: e.tensor_scalar(out=c0_lr[:], in0=c0_lr[:], scalar1=-1.0, scalar2=1.0, op0=ALU.mult, op1=ALU.add),
         reads=['c0_lr'], writes=['c0_lr'])
    P.op('dve', lambda e: e.tensor_scalar(out=omka[:], in0=prm['ka'][:], scalar1=-1.0, scalar2=1.0, op0=ALU.mult, op1=ALU.add),
         reads=['p_ka'], writes=['omka'])
    w2a2 = sb0("w2a2", [64, 4, D], BF16)
    for g, src_ in enumerate([w2[0], w2[1], a2[0], a2[1]]):
        for c0 in range(0, D, 256):
            P.dma(wst[0:64, 0, 0:256], src_[:, c0:c0 + 256], writes=['wst'])
            P.op('pool', lambda e: e.tensor_copy(out=w2a2[:, g, c0:c0 + 256], in_=wst[0:64, 0, 0:256]), reads=['wst'], writes=['w2a2'])

    def shift_into(dst, dkey, raw, rkey, T, c0ap, m0ap, m1ap, t1tile, eng='dve'):
        for s0 in range(0, T, 512):
            n = min(512, T - s0)
            P.op('dve', lambda e: e.tensor_scalar(out=t1tile[:, 0:n], in0=raw[:, 16 + s0:16 + s0 + n], scalar1=c0ap, scalar2=None,
                                                op0=ALU.mult), reads=[rkey], writes=['shift_t'])
            P.op('dve', lambda e: e.scalar_tensor_tensor(out=t1tile[:, 0:n], in0=raw[:, 15 + s0:15 + s0 + n], scalar=m0ap,
                                                       in1=t1tile[:, 0:n], op0=ALU.mult, op1=ALU.add),
                 reads=[rkey, 'shift_t'], writes=['shift_t'])
            P.op('dve', lambda e: e.scalar_tensor_tensor(out=dst[:, s0:s0 + n], in0=raw[:, 17 + s0:17 + s0 + n], scalar=m1ap,
                                                       in1=t1tile[:, 0:n], op0=ALU.mult, op1=ALU.add),
                 reads=[rkey, 'shift_t'], writes=[dkey])

    shiftt = sb0("shiftt", [64, 512])

    def rwkv_seq_setup(off, T):
        load_w(wLR, 256)
        pb = min(512, T)
        for g in range(4):
            raw = FT[g]
            P.op('pool', lambda e: e.memset(raw[0:64, 15:16], 0.0), writes=['FT%d' % g])
            P.op('pool', lambda e: e.memset(raw[0:64, T + 16:T + 17], 0.0), writes=['FT%d' % g])
            for b in range(T // pb):
                t0 = b * pb
                proj(64 * g, 64, t0, pb, lambda pz, pk: P.op(
                    'act', lambda e: e.copy(out=raw[0:64, 16 + t0:16 + t0 + pb], in_=pz[0:64, 0:pb]), reads=[pk], writes=['FT%d' % g]))
            shift_into(FT[4][0:64, :], 'FT4', raw[0:64, :], 'FT%d' % g, T, c0_lr[:, g:g + 1], prm['mu_lr'][:, 0, g:g + 1],
                       prm['mu_lr'][:, 1, g:g + 1], shiftt)
            if g < 2:
                P.op('act', lambda e: e.activation(out=LR[g][:, 0:T], in_=FT[4][0:64, 0:T], func=AF.Tanh), reads=['FT4'], writes=['LR%d' % g])
            else:
                P.op('act', lambda e: e.copy(out=LR[g][:, 0:T], in_=FT[4][0:64, 0:T]), reads=['FT4'], writes=['LR%d' % g])

    def rwkv_head(h, slot, off, T, is_sample, pidx):
        P.barrier()
        load_w(wB[h], 256)
        pb = min(512, T)
        nchT = T // 64
        for g in range(3):
            raw = FT[g]
            P.op('pool', lambda e: e.memset(raw[0:64, 15:16], 0.0), writes=['FT%d' % g])
            P.op('pool', lambda e: e.memset(raw[0:64, T + 16:T + 17], 0.0), writes=['FT%d' % g])
            for b in range(T // pb):
                t0 = b * pb
                proj(64 * g, 64, t0, pb, lambda pz, pk: P.op(
                    'act', lambda e: e.copy(out=raw[0:64, 16 + t0:16 + t0 + pb], in_=pz[0:64, 0:pb]), reads=[pk], writes=['FT%d' % g]))
            shift_into(FT[3 + g][0:64, :], 'FT%d' % (3 + g), raw[0:64, :], 'FT%d' % g, T, c0_rkv[:, g, h:h + 1],
                       prm['mu_rkv'][:, 0, g, h:h + 1], prm['mu_rkv'][:, 1, g, h:h + 1], shiftt, eng=('dve' if g != 1 else 'pool'))
        rS, kS, vS = FT[3][0:64, :], FT[4][0:64, :], FT[5][0:64, :]
        szb, yaccr, bonus = FT[0][0:64, :], FT[1][0:64, 0:T], FT[2][0:64, :]
        yacc = yaccr.rearrange("p (c v) -> p c v", v=64)
        for b in range(T // pb):
            t0 = b * pb
            proj(192, 64, t0, pb, lambda pz, pk: P.op(
                'act', lambda e: e.activation(out=szb[:, t0:t0 + pb], in_=pz[0:64, 0:pb], func=AF.Silu), reads=[pk], writes=['FT0']))
        tb = min(TR, T)
        nblk = T // tb
        P.barrier()
        for d in range(2):
            rev = (d == 1)
            cur = 0
            Zt = [rsq['Z0'], rsq['Z1']]
            if is_sample:
                P.dma(rsq['zt'][:], s_rwkv[d, h], writes=['rq_zt'])
                pz, pk = nps()
                P.op('pe', lambda e: e.transpose(out=pz[0:64, 0:64], in_=rsq['zt'][:], identity=ident[0:64, 0:64]),
                     reads=['rq_zt', 'ident'], writes=[pk])
                P.op('act', lambda e: e.copy(out=Zt[0][:], in_=pz[0:64, 0:64]), reads=[pk], writes=['rq_Z0'])
            else:
                P.op('pool', lambda e: e.memset(Zt[0][:], 0.0), writes=['rq_Z0'])
            blks = list(range(nblk))
            if rev:
                blks = blks[::-1]
            for b in blks:
                t0 = b * tb
                sl = slice(t0, t0 + tb)
                nch = tb // 64
                R_ = rb
                pz, pk = nps()
                P.op('pe', lambda e: e.matmul(pz[0:64, 0:tb], w2a2[:, d, h * 64:(h + 1) * 64], LR[d][:, sl], start=True, stop=True),
                     reads=['w2a2', 'LR%d' % d], writes=[pk])
                P.op('act', lambda e: e.activation(out=R_['lw'][:, 0:tb], in_=pz[0:64, 0:tb], func=AF.Sigmoid,
                                                   bias=prm['w0'][:, d, h:h + 1], scale=1.0), reads=[pk, 'p_w0'], writes=['r_lw'])
                pz, pk = nps()
                P.op('pe', lambda e: e.matmul(pz[0:64, 0:tb], w2a2[:, 2 + d, h * 64:(h + 1) * 64], LR[2 + d][:, sl], start=True, stop=True),
                     reads=['w2a2', 'LR%d' % (2 + d)], writes=[pk])
                P.op('act', lambda e: e.activation(out=R_['a'][:, 0:tb], in_=pz[0:64, 0:tb], func=AF.Sigmoid,
                                                   bias=prm['a0'][:, d, h:h + 1], scale=1.0), reads=[pk, 'p_a0'], writes=['r_a'])
                P.op('dve', lambda e: e.tensor_scalar(out=R_['kk'][:, 0:tb], in0=kS[:, sl], scalar1=prm['kk'][:, h:h + 1],
                                                      scalar2=None, op0=ALU.mult), reads=['FT4', 'p_kk'], writes=['r_kk'])
                P.op('dve', lambda e: e.tensor_tensor(out=R_['kq'][:, 0:tb], in0=R_['kk'][:, 0:tb], in1=R_['kk'][:, 0:tb], op=ALU.mult),
                     reads=['r_kk'], writes=['r_kq'])
                pz, pk = nps()
                P.op('pe', lambda e: e.matmul(pz[0:64, 0:tb], ones[0:64, 0:64], R_['kq'][:, 0:tb], start=True, stop=True),
                     reads=['ones', 'r_kq'], writes=[pk])
                P.op('dve', lambda e: e.tensor_scalar(out=R_['kq'][:, 0:tb], in0=pz[0:64, 0:tb], scalar1=1e-24, scalar2=None,
                                                      op0=ALU.max), reads=[pk], writes=['r_kq'])
                P.op('act', lambda e: e.activation(out=R_['kq'][:, 0:tb], in_=R_['kq'][:, 0:tb], func=AF.Ln), reads=['r_kq'], writes=['r_kq'])
                P.op('act', lambda e: e.activation(out=R_['kq'][:, 0:tb], in_=R_['kq'][:, 0:tb], func=AF.Exp, scale=-0.5),
                     reads=['r_kq'], writes=['r_kq'])
                P.op('dve', lambda e: e.tensor_tensor(out=R_['kap'][:, 0:tb], in0=R_['kk'][:, 0:tb], in1=R_['kq'][:, 0:tb], op=ALU.mult),
                     reads=['r_kk', 'r_kq'], writes=['r_kap'])
                P.op('dve', lambda e: e.tensor_scalar(out=R_['t1'][:, 0:tb], in0=R_['a'][:, 0:tb], scalar1=prm['ka'][:, h:h + 1],
                                                       scalar2=omka[:, h:h + 1], op0=ALU.mult, op1=ALU.add),
                     reads=['r_a', 'p_ka', 'omka'], writes=['r_t1'])
                P.op('dve', lambda e: e.tensor_tensor(out=R_['kd'][:, 0:tb], in0=kS[:, sl], in1=R_['t1'][:, 0:tb], op=ALU.mult),
                     reads=['FT4', 'r_t1'], writes=['r_kd'])
                P.op('dve', lambda e: e.tensor_tensor(out=R_['b'][:, 0:tb], in0=R_['a'][:, 0:tb], in1=R_['kap'][:, 0:tb], op=ALU.mult),
                     reads=['r_a', 'r_kap'], writes=['r_b'])
                P.op('dve', lambda e: e.scalar_tensor_tensor(out=R_['rk'][:, 0:tb], in0=rS[:, sl], scalar=prm['rk'][:, h:h + 1],
                                                             in1=R_['kd'][:, 0:tb], op0=ALU.mult, op1=ALU.mult),
                     reads=['FT3', 'p_rk', 'r_kd'], writes=['r_rk'])
                pz, pk = nps()
                P.op('pe', lambda e: e.matmul(pz[0:64, 0:tb], ones[0:64, 0:64], R_['rk'][:, 0:tb], start=True, stop=True),
                     reads=['ones', 'r_rk'], writes=[pk])
                if d == 0:
                    P.op('dve', lambda e: e.tensor_tensor(out=bonus[:, sl], in0=pz[0:64, 0:tb], in1=vS[:, sl], op=ALU.mult),
                         reads=[pk, 'FT5'], writes=[('bonus', b)])
                else:
                    P.op('dve', lambda e: e.tensor_tensor(out=R_['rk'][:, 0:tb], in0=pz[0:64, 0:tb], in1=vS[:, sl], op=ALU.mult),
                         reads=[pk, 'FT5'], writes=['r_rk'])
                    P.op('dve', lambda e: e.tensor_tensor(out=bonus[:, sl], in0=bonus[:, sl], in1=R_['rk'][:, 0:tb], op=ALU.add),
                         reads=['r_rk', ('bonus', b)], writes=[('bonus', b)])
                G, br, E, Ei, Em = R_['G'], R_['br'], R_['E'], R_['Ei'], R_['Em']
                P.op('dve', lambda e: e.memset(E[:, 0:tb], 0.0), writes=['r_E'])
                if not rev:
                    P.op('dve', lambda e: e.tensor_tensor_scan(out=G[:, 0:tb], data0=R_['lw'][:, 0:tb], data1=E[:, 0:tb],
                                                               initial=0.0, op0=ALU.add, op1=ALU.add),
                         reads=['r_lw', 'r_E'], writes=['r_G'])
                    ci_ = 0
                else:
                    P.op('dve', lambda e: e.tensor_tensor_scan(out=G[:, 0:tb][:, ::-1], data0=R_['lw'][:, 0:tb][:, ::-1],
                                                               data1=E[:, 0:tb], initial=0.0, op0=ALU.add, op1=ALU.add),
                         reads=['r_lw', 'r_E'], writes=['r_G'])
                    ci_ = 63
                G3 = G[:, 0:tb].rearrange("p (c l) -> p c l", l=64)
                lw3 = R_['lw'][:, 0:tb].rearrange("p (c l) -> p c l", l=64)
                P.op('dve', lambda e: e.tensor_tensor(out=rgref[:, 0:nch], in0=G3[:, :, ci_], in1=lw3[:, :, ci_], op=ALU.subtract),
                     reads=['r_G', 'r_lw'], writes=['rgref'])
                P.op('dve', lambda e: e.tensor_tensor(out=br[:, 0:tb].rearrange("p (c l) -> p c l", l=64), in0=G3,
                                                      in1=rgref[:, 0:nch].unsqueeze(2).to_broadcast([64, nch, 64]), op=ALU.subtract),
                     reads=['r_G', 'rgref'], writes=['r_br'])
                bend = br[:, 0:tb].rearrange("p (c l) -> p c l", l=64)[:, :, (0 if rev else 63)]
                P.op('act', lambda e: e.activation(out=rgam[:, 0:nch], in_=bend, func=AF.Exp, scale=LWC), reads=['r_br'], writes=['rgam'])
                P.op('dve', lambda e: e.tensor_scalar(out=rgam[:, 4:4 + nch], in0=rgam[:, 0:nch], scalar1=-1.0, scalar2=None,
                                                       op0=ALU.mult), reads=['rgam'], writes=['rgam'])
                P.op('act', lambda e: e.activation(out=E[:, 0:tb], in_=br[:, 0:tb], func=AF.Exp, scale=LWC), reads=['r_br'], writes=['r_E'])
                P.op('act', lambda e: e.activation(out=Ei[:, 0:tb], in_=br[:, 0:tb], func=AF.Exp, scale=-LWC),
                     reads=['r_br'], writes=['r_Ei'])
                P.op('dve', lambda e: e.tensor_tensor(out=R_['t1'][:, 0:tb], in0=br[:, 0:tb], in1=R_['lw'][:, 0:tb], op=ALU.subtract),
                     reads=['r_br', 'r_lw'], writes=['r_t1'])
                P.op('act', lambda e: e.activation(out=Em[:, 0:tb], in_=R_['t1'][:, 0:tb], func=AF.Exp, scale=LWC), reads=['r_t1'], writes=['r_Em'])
                P.op('dve', lambda e: e.tensor_tensor(out=KR[:, 0, 0:tb], in0=R_['kap'][:, 0:tb], in1=Em[:, 0:tb], op=ALU.mult),
                     reads=['r_kap', 'r_Em'], writes=['r_KR'])
                P.op('dve', lambda e: e.tensor_tensor(out=KR[:, 1, 0:tb], in0=rS[:, sl], in1=E[:, 0:tb], op=ALU.mult),
                     reads=['FT3', 'r_E', 'r_KR'], writes=['r_KR'])
                P.op('dve', lambda e: e.tensor_tensor(out=R_['bh'][:, 0:tb], in0=R_['b'][:, 0:tb], in1=Ei[:, 0:tb], op=ALU.mult),
                     reads=['r_b', 'r_Ei'], writes=['r_bh'])
                P.op('dve', lambda e: e.tensor_tensor(out=R_['kh'][:, 0:tb], in0=R_['kd'][:, 0:tb], in1=Ei[:, 0:tb], op=ALU.mult),
                     reads=['r_kd', 'r_Ei'], writes=['r_kh'])
                P.op('dve', lambda e: e.tensor_tensor(out=R_['Kb'][:, 0:tb].rearrange("p (c l) -> p c l", l=64),
                                                      in0=R_['kh'][:, 0:tb].rearrange("p (c l) -> p c l", l=64),
                                                      in1=rgam[:, 0:nch].unsqueeze(2).to_broadcast([64, nch, 64]), op=ALU.mult),
                     reads=['r_kh', 'rgam'], writes=['r_Kb'])
                P.op('dve', lambda e: e.tensor_tensor(out=R_['Bb'][:, 0:tb].rearrange("p (c l) -> p c l", l=64),
                                                      in0=R_['bh'][:, 0:tb].rearrange("p (c l) -> p c l", l=64),
                                                      in1=rgam[:, 4:4 + nch].unsqueeze(2).to_broadcast([64, nch, 64]), op=ALU.mult),
                     reads=['r_bh', 'rgam'], writes=['r_Bb'])
                chs = list(range(nch))
                if rev:
                    chs = chs[::-1]
                st = {}
                for c in chs:
                    cs = slice(c * 64, (c + 1) * 64)
                    C_ = cset[c]
                    ck = (lambda n_, c=c: 'c%d_%s' % (c, n_))
                    pA, pAk = nps()
                    P.op('pe', lambda e: e.matmul(pA[0:64, 0:128], R_['bh'][:, cs], KR[:, :, cs], start=True, stop=True),
                         reads=['r_bh', 'r_KR'], writes=[pAk])
                    P.op('pe', lambda e: e.matmul(pA[0:64, 128:256], R_['kh'][:, cs], KR[:, :, cs], start=True, stop=True),
                         reads=['r_kh', 'r_KR'], writes=[pAk])
                    P.op('pe', lambda e: e.matmul(pA[0:64, 256:320], KR[:, 0, cs], R_['bh'][:, cs], start=True, stop=True),
                         reads=['r_bh', 'r_KR'], writes=[pAk])
                    AB, BB = C_['AB'], C_['BB']
                    P.op('dve', lambda e: e.tensor_tensor(out=AB[:], in0=pA[0:64, 0:128], in1=mR[:, d, 0, :], op=ALU.mult),
                         reads=[pAk, 'mR'], writes=[ck('AB')])
                    P.op('dve', lambda e: e.tensor_tensor(out=BB[:], in0=pA[0:64, 128:256], in1=mR[:, d, 1, :], op=ALU.mult),
                         reads=[pAk, 'mR'], writes=[ck('BB')])
                    P.op('dve', lambda e: e.tensor_tensor(out=C_['XT0'][:], in0=pA[0:64, 256:320], in1=mR[:, d, 2, 0:64], op=ALU.mult),
                         reads=[pAk, 'mR'], writes=[ck('XT0')])
                    P.op('dve', lambda e: e.tensor_tensor(out=C_['Pm0'][:], in0=AB[:, 0:64], in1=ident[0:64, 0:64], op=ALU.add),
                         reads=[ck('AB'), 'ident'], writes=[ck('Pm0')])
                    st[c] = dict(X=AB[:, 0:64], Xk=ck('AB'), XT=C_['XT0'], XTk=ck('XT0'), xti=0, pmi=0)
                for lev in range(5):
                    for c in chs:
                        C_ = cset[c]
                        s_ = st[c]
                        ck = (lambda n_, c=c: 'c%d_%s' % (c, n_))
                        X, Xk, XT, XTk = s_['X'], s_['Xk'], s_['XT'], s_['XTk']
                        pq, pqk = nps()
                        nXTn = 'XTa' if lev % 2 == 0 else 'XTb'
                        nXT, nXTk = C_[nXTn], ck(nXTn)
                        P.op('pe', lambda e: e.matmul(pq[0:64, 64:128], X, XT[:], start=True, stop=True), reads=[Xk, XTk], writes=[pqk])
                        if lev < 4:
                            P.op('pe', lambda e: e.matmul(pq[0:64, 0:64], XT[:], X, start=True, stop=True), reads=[Xk, XTk], writes=[pqk])
                        P.op('act', lambda e: e.copy(out=nXT[:], in_=pq[0:64, 64:128]), reads=[pqk], writes=[nXTk])
                        if lev < 4:
                            tn = 'Xa' if lev % 2 == 0 else 'Xb'
                            P.op('act', lambda e: e.copy(out=C_[tn][:], in_=pq[0:64, 0:64]), reads=[pqk], writes=[ck(tn)])
                            s_['X'], s_['Xk'] = C_[tn][:], ck(tn)
                        s_['XT'], s_['XTk'], s_['xti'] = nXT, nXTk, 1 - s_['xti']
                    for c in chs:
                        C_ = cset[c]
                        s_ = st[c]
                        ck = (lambda n_, c=c: 'c%d_%s' % (c, n_))
                        nXT, nXTk = s_['XT'], s_['XTk']
                        pmi = s_['pmi']
                        Pc, Pn = C_['Pm%d' % pmi], C_['Pm%d' % (1 - pmi)]
                        pp, ppk = nps()
                        P.op('pe', lambda e: e.matmul(pp[0:64, 0:64], nXT[:], Pc[:], start=True, stop=True),
                             reads=[nXTk, ck('Pm%d' % pmi)], writes=[ppk])
                        P.op('dve', lambda e: e.tensor_tensor(out=Pn[:], in0=pp[0:64, 0:64], in1=Pc[:], op=ALU.add),
                             reads=[ppk, ck('Pm%d' % pmi)], writes=[ck('Pm%d' % (1 - pmi))])
                        s_['pmi'] = 1 - pmi
                for c in chs:
                    cs = slice(c * 64, (c + 1) * 64)
                    gsl = slice(t0 + c * 64, t0 + (c + 1) * 64)
                    C_ = cset[c]
                    pt, ptk = nps()
                    P.op('pe', lambda e: e.transpose(out=pt[0:64, 0:64], in_=vS[:, gsl], identity=ident[0:64, 0:64]),
                         reads=['FT5', 'ident'], writes=[ptk])
                    P.op('pe', lambda e: e.transpose(out=pt[0:64, 64:128], in_=R_['Kb'][:, cs], identity=ident[0:64, 0:64]),
                         reads=['r_Kb', 'ident'], writes=[ptk])
                    P.op('pe', lambda e: e.transpose(out=pt[0:64, 128:192], in_=R_['Bb'][:, cs], identity=ident[0:64, 0:64]),
                         reads=['r_Bb', 'ident'], writes=[ptk])
                    P.op('act', lambda e: e.copy(out=C_['Vt'][:], in_=pt[0:64, 0:64]), reads=[ptk], writes=['c%d_Vt' % c])
                    P.op('act', lambda e: e.copy(out=C_['Kt'][:], in_=pt[0:64, 64:128]), reads=[ptk], writes=['c%d_Kt' % c])
                    P.op('act', lambda e: e.copy(out=C_['Bt'][:], in_=pt[0:64, 128:192]), reads=[ptk], writes=['c%d_Bt' % c])
                for c in chs:
                    cs = slice(c * 64, (c + 1) * 64)
                    cg = (t0 // 64) + c
                    C_ = cset[c]
                    s_ = st[c]
                    AB, BB = C_['AB'], C_['BB']
                    ABk, BBk, Vtk, Ktk, Btk = ['c%d_%s' % (c, n_) for n_ in ('AB', 'BB', 'Vt', 'Kt', 'Bt')]
                    Pm, Pmk = C_['Pm%d' % s_['pmi']], 'c%d_Pm%d' % (c, s_['pmi'])
                    Zc, Zn = Zt[cur], Zt[1 - cur]
                    zck, znk = 'rq_Z%d' % cur, 'rq_Z%d' % (1 - cur)
                    pw, pwk = nps()
                    P.op('pe', lambda e: e.matmul(pw[0:64, 0:64], KR[:, 0, cs], Zc[:], start=True, stop=False),
                         reads=['r_KR', zck], writes=[pwk])
                    P.op('pe', lambda e: e.matmul(pw[0:64, 0:64], BB[:, 0:64], C_['Vt'][:], start=False, stop=True),
                         reads=[BBk, Vtk], writes=[pwk])
                    P.op('act', lambda e: e.copy(out=rsq['Wb'][:], in_=pw[0:64, 0:64]), reads=[pwk], writes=['rq_Wb'])
                    pu, puk = nps()
                    P.op('pe', lambda e: e.matmul(pu[0:64, 0:64], Pm[:], rsq['Wb'][:], start=True, stop=True),
                         reads=[Pmk, 'rq_Wb'], writes=[puk])
                    P.op('act', lambda e: e.copy(out=rsq['U'][:], in_=pu[0:64, 0:64]), reads=[puk], writes=['rq_U'])
                    pzz, pzk = nps()
                    P.op('pe', lambda e: e.matmul(pzz[0:64, 0:64], C_['Kt'][:], C_['Vt'][:], start=True, stop=False),
                         reads=[Ktk, Vtk], writes=[pzk])
                    P.op('pe', lambda e: e.matmul(pzz[0:64, 0:64], C_['Bt'][:], rsq['U'][:], start=False, stop=True),
                         reads=[Btk, 'rq_U'], writes=[pzk])
                    P.op('dve', lambda e: e.scalar_tensor_tensor(out=Zn[:], in0=Zc[:], scalar=rgam[:, c:c + 1], in1=pzz[0:64, 0:64],
                                                                 op0=ALU.mult, op1=ALU.add), reads=[zck, 'rgam', pzk], writes=[znk])
                    py, pyk = nps()
                    P.op('pe', lambda e: e.matmul(py[0:64, 0:64], KR[:, 1, cs], Zc[:], start=True, stop=False),
                         reads=['r_KR', zck], writes=[pyk])
                    P.op('pe', lambda e: e.matmul(py[0:64, 0:64], BB[:, 64:128], C_['Vt'][:], start=False, stop=False),
                         reads=[BBk, Vtk], writes=[pyk])
                    P.op('pe', lambda e: e.matmul(py[0:64, 0:64], AB[:, 64:128], rsq['U'][:], start=False, stop=True),
                         reads=[ABk, 'rq_U'], writes=[pyk])
                    if d == 0:
                        P.op('act', lambda e: e.copy(out=yacc[:, cg, :], in_=py[0:64, 0:64]), reads=[pyk], writes=[('yacc', cg)])
                    else:
                        P.op('dve', lambda e: e.tensor_tensor(out=yacc[:, cg, :], in0=yacc[:, cg, :], in1=py[0:64, 0:64], op=ALU.add),
                             reads=[pyk, ('yacc', cg)], writes=[('yacc', cg)])
                    cur = 1 - cur
            if not is_sample:
                pz, pk = nps()
                P.op('pe', lambda e: e.transpose(out=pz[0:64, 0:64], in_=Zt[cur][:], identity=ident[0:64, 0:64]),
                     reads=['rq_Z%d' % cur, 'ident'], writes=[pk])
                P.op('act', lambda e: e.copy(out=rsq['zt'][:], in_=pz[0:64, 0:64]), reads=[pk], writes=['rq_zt'])
                P.dma(o_rwkv[pidx, d, h], rsq['zt'][:], reads=['rq_zt'], q='pool')
        ykeys = [('yacc', c) for c in range(nchT)]
        gst = ostat[0:64, :, :].rearrange("p a b -> p (a b)")
        P.op('dve', lambda e: e.tensor_reduce(out=gst[:, 0:nchT], in_=yacc, axis=AX.X, op=ALU.add), reads=ykeys, writes=['gst'])
        P.op('dve', lambda e: e.tensor_scalar(out=gst[:, 0:nchT], in0=gst[:, 0:nchT], scalar1=-1.0 / 64, scalar2=None, op0=ALU.mult),
             reads=['gst'], writes=['gst'])
        P.op('dve', lambda e: e.tensor_tensor(out=yacc, in0=yacc, in1=gst[:, 0:nchT].unsqueeze(2).to_broadcast([64, nchT, 64]), op=ALU.add),
             reads=ykeys + ['gst'], writes=ykeys)
        sq = FT[3][0:64, 0:T].rearrange("p (c v) -> p c v", v=64)
        P.op('dve', lambda e: e.tensor_tensor(out=sq, in0=yacc, in1=yacc, op=ALU.mult), reads=ykeys, writes=['FT3'])
        P.op('dve', lambda e: e.tensor_reduce(out=gst[:, 32:32 + nchT], in_=sq, axis=AX.X, op=ALU.add), reads=['FT3'], writes=['gst'])
        P.op('dve', lambda e: e.tensor_scalar(out=gst[:, 32:32 + nchT], in0=gst[:, 32:32 + nchT], scalar1=1.0 / 64, scalar2=64e-5,
                                              op0=ALU.mult, op1=ALU.add), reads=['gst'], writes=['gst'])
        P.op('act', lambda e: e.activation(out=gst[:, 32:32 + nchT], in_=gst[:, 32:32 + nchT], func=AF.Sqrt), reads=['gst'], writes=['gst'])
        P.op('dve', lambda e: e.reciprocal(out=gst[:, 32:32 + nchT], in_=gst[:, 32:32 + nchT]), reads=['gst'], writes=['gst'])
        P.op('dve', lambda e: e.tensor_tensor(out=yacc, in0=yacc, in1=gst[:, 32:32 + nchT].unsqueeze(2).to_broadcast([64, nchT, 64]),
                                              op=ALU.mult), reads=ykeys + ['gst'], writes=ykeys)
        n8 = min(8, nchT)
        for g8 in range(nchT // n8):
            pz, pk = nps()
            for j in range(n8):
                cg = g8 * n8 + j
                P.op('pe', lambda e: e.transpose(out=pz[0:64, j * 64:(j + 1) * 64], in_=yacc[:, cg, :], identity=ident[0:64, 0:64]),
                     reads=[('yacc', cg), 'ident'], writes=[pk])
            w_ = n8 * 64
            gs = slice(g8 * w_, (g8 + 1) * w_)
            P.op('dve', lambda e: e.tensor_scalar(out=shiftt[:, 0:w_], in0=pz[0:64, 0:w_], scalar1=prm['gng'][:, h:h + 1],
                                                  scalar2=prm['gnb'][:, h:h + 1], op0=ALU.mult, op1=ALU.add),
                 reads=[pk, 'p_gng', 'p_gnb'], writes=['shift_t'])
            P.op('dve', lambda e: e.tensor_tensor(out=shiftt[:, 0:w_], in0=shiftt[:, 0:w_], in1=bonus[:, gs], op=ALU.add),
                 reads=['shift_t'] + [('bonus', b) for b in range(nblk)], writes=['shift_t'])
            P.op('dve', lambda e: e.tensor_tensor(out=yT[0:64, slot, gs], in0=shiftt[:, 0:w_], in1=szb[:, gs], op=ALU.mult),
                 reads=['shift_t', 'FT0'], writes=[('yT', slot)])

    seq_ids = debug.get('seqs', [0, 1, 2]) if debug else [0, 1, 2]
    head_ids = debug.get('heads', list(range(8))) if debug else list(range(8))
    rheads = debug.get('rheads', list(range(16))) if debug else list(range(16))
    for si in seq_ids:
        off, T, cidx, is_sample = SEQS[si]
        P.barrier()
        make_gate(0, cidx)
        make_hT(0, xin, 'xin', off, T, cidx)
        P.barrier()
        if debug and debug.get('inner'):
            dump("mT", mT[:, 0].rearrange("p a b -> p (a b)"), ['mT'], 48)
            dump("sc1", sc1[:, 0].rearrange("p a b -> p (a b)"), ['sc1'], 16)
            dump("scT", scT[:].rearrange("p a b -> p (a b)"), ['scT'], 16)
            dump("xn", xn, ['xn'], 1024)
            dump("xt0", xt[0], ['xt0'], 1024)
            for kc in range(8):
                dump("hT%d" % kc, hT[:, kc, 0:T], hT_keys(0, T), T, col0=off)
        for h in head_ids:
            hgrn_head(h, off, T, is_sample, si - 1)
            if debug and debug.get('dump_y'):
                dump("yT%d" % h, yT[:, h, 0:T], [('yT', h)], T, col0=off)
        P.barrier()
        load_wo(w_out_even[0:D, :], 128)
        outproj(0, [(128, s_) for s_ in range(8)], xin, 'xin', x1, 'x1', off, T, cidx)
        P.barrier()
        rwkv_seq_setup(off, T)
        for half in range(2):
            P.barrier()
            for slot in range(8):
                h = half * 8 + slot
                if h in rheads:
                    rwkv_head(h, slot, off, T, is_sample, si - 1)
                    if debug and debug.get('dump_y'):
                        dump("yR%d" % h, yT[0:64, slot, 0:T], [('yT', slot)], T, col0=off, parts=64)
            P.barrier()
            load_wo(w_out_even[D + half * 512: D + (half + 1) * 512, :], 64)
            outproj(0, [(64, s_) for s_ in range(8)], x1, 'x1', x1, 'x1', off, T, cidx)
    P.barrier()
    L0.close()

    if not (debug and debug.get('l0only')):
        L1 = contextlib.ExitStack()

        def sb1(name, shape, dt=F32):
            return L1.enter_context(nc.sbuf_tensor(name, list(shape), dt))

        make_fng()
        LC = 128
        DH = 512
        qT = yT[:, 4:8, :]
        kT = sb1("kT", [128, 4, TS], BF16)
        vch = sb1("vch", [128, DH], BF16)
        Cst = sb1("Cst", [128, 4, DH])
        Cbf = sb1("Cbf", [128, 4, DH], BF16)
        nst = sb1("nst", [128, 8])
        nbf = sb1("nbf", [128, 4], BF16)
        ktok = sb1("ktok", [128, DH], BF16)
        vw = sb1("vw", [128, DH], BF16)
        sTs = sb1("sTs", [128, 128], BF16)
        onesb = sb1("onesb", [128, 1], BF16)
        identb = sb1("identb", [128, 128], BF16)
        mC = sb1("mC", [128, 2, 128])
        SEL = sb1("SEL", [36, 4, 128])
        XA = sb1("XA", [36, TS])
        XB = sb1("XB", [36, TS])
        zrow = sb1("zrow", [36, 512])
        sm = {n_: sb1("sm_" + n_, [36, 16]) for n_ in ['ac', 'bl', 'M', 'MP', 'mu', 'al', 'gref', 'm0']}
        Wtok = sb1("Wtok", [128, 2, 16, 8])
        Wtokb = sb1("Wtokb", [128, 2, 16, 4], BF16)
        ALb = sb1("ALb", [128, 2, 4, 16])
        dstat = sb1("dstat", [128, 8])
        wGb = sb1("wGb", [128, 8, 16], BF16)
        gbT = sb1("gbT", [36, 4])
        ngbT = sb1("ngbT", [36, 4])
        cw = sb1("cw", [128, 32, 9])
        cb = sb1("cb", [128, 32])
        mng = sb1("mng", [128, 16])
        wbf1 = sb1("wbf1", [128, 8, DH], BF16)
        P.dma(mC[:], maskC.rearrange("d s t -> s d t"), writes=['mC'])
        P.dma(SEL[:], sel_d[:], writes=['SEL'])
        P.dma(gbT[:], gbT_d[:], writes=['gbT'])
        P.dma(cw[:], cw_d[:], writes=['cw'])
        P.dma(cb[:], cb_d[:], writes=['cb'])
        P.dma(mng[:], mng_d[:], writes=['mng'])
        P.op('dve', lambda e: e.memset(onesb[:], 1.0), writes=['onesb'])
        P.op('dve', lambda e: e.memset(zrow[:], 0.0), writes=['zrow'])
        P.op('dve', lambda e: e.tensor_copy(out=identb[:], in_=ident[:]), reads=['ident'], writes=['identb'])
        P.op('dve', lambda e: e.tensor_scalar(out=ngbT[:], in0=gbT[:], scalar1=-1.0, scalar2=None, op0=ALU.mult), reads=['gbT'], writes=['ngbT'])
        P.dma(wst[:, :, 0:16], w_in_odd[:, 10240:10256].rearrange("(kc p) n -> p kc n", p=128), writes=['wst'])
        P.op('pool', lambda e: e.tensor_copy(out=wGb[:], in_=wst[:, :, 0:16]), reads=['wst'], writes=['wGb'])
        LNK = float(np.log(DH ** -0.5))

        def load_w1(c0, ncols):
            v = w_in_odd[:, c0:c0 + ncols].rearrange("(kc p) n -> p kc n", p=128)
            for q0 in range(0, ncols, 256):
                w_ = min(256, ncols - q0)
                P.dma(wst[:, :, 0:w_], v[:, :, q0:q0 + w_], writes=['wst'])
                P.op('act', lambda e: e.copy(out=wbf1[:, :, q0:q0 + w_], in_=wst[:, :, 0:w_]), reads=['wst'], writes=['wbf1'])

        def gates_seq(T, is_sample):
            NC = T // LC
            pbk = min(512, T)
            for d in range(2):
                pb = 32 * d
                rows = slice(pb, pb + 4)
                for b in range(T // pbk):
                    t0 = b * pbk
                    pz, pk = nps()
                    for kc in range(8):
                        P.op('pe', lambda e: e.matmul(pz[pb:pb + 4, 0:pbk], wGb[:, kc, (2 + d) * 4:(3 + d) * 4], hT[:, kc, t0:t0 + pbk],
                                                      start=(kc == 0), stop=(kc == 7)), reads=['wGb'] + hT_keys(t0, t0 + pbk), writes=[pk])
                    P.op('act', lambda e: e.activation(out=XA[rows, t0:t0 + pbk], in_=pz[pb:pb + 4, 0:pbk], func=AF.Exp,
                                                       bias=ngbT[rows, 2 + d:3 + d], scale=-1.0), reads=[pk, 'ngbT'], writes=['XA'])
                P.op('act', lambda e: e.activation(out=XA[rows, 0:T], in_=XA[rows, 0:T], func=AF.Ln, bias=1.0, scale=1.0),
                     reads=['XA'], writes=['XA'])
                for b in range(T // pbk):
                    bs = slice(b * pbk, (b + 1) * pbk)
                    if d == 0:
                        P.op('dve', lambda e: e.tensor_tensor_scan(out=XB[rows, bs], data0=XA[rows, bs], data1=zrow[rows, 0:pbk],
                                                                   initial=0.0, op0=ALU.add, op1=ALU.add), reads=['XA', 'zrow'], writes=['XB'])
                    else:
                        P.op('dve', lambda e: e.tensor_tensor_scan(out=XB[rows, bs][:, ::-1], data0=XA[rows, bs][:, ::-1],
                                                                   data1=zrow[rows, 0:pbk], initial=0.0, op0=ALU.add, op1=ALU.add),
                             reads=['XA', 'zrow'], writes=['XB'])
                if d == 0:
                    ci_, ce_ = 0, LC - 1
                else:
                    ci_, ce_ = LC - 1, 0
                B3 = XB[rows, 0:T].rearrange("p (c l) -> p c l", l=LC)
                A3 = XA[rows, 0:T].rearrange("p (c l) -> p c l", l=LC)
                S = {k_: v_[rows, :] for k_, v_ in sm.items()}
                P.op('dve', lambda e: e.tensor_tensor(out=S['gref'][:, 0:NC], in0=B3[:, :, ci_], in1=A3[:, :, ci_], op=ALU.subtract),
                     reads=['XA', 'XB'], writes=['sm_gref'])
                P.op('dve', lambda e: e.tensor_tensor(out=B3, in0=B3, in1=S['gref'][:, 0:NC].unsqueeze(2).to_broadcast([4, NC, LC]),
                                                      op=ALU.subtract), reads=['XB', 'sm_gref'], writes=['XB'])
                for b in range(T // pbk):
                    t0 = b * pbk
                    pz, pk = nps()
                    for kc in range(8):
                        P.op('pe', lambda e: e.matmul(pz[pb:pb + 4, 0:pbk], wGb[:, kc, d * 4:(d + 1) * 4], hT[:, kc, t0:t0 + pbk],
                                                      start=(kc == 0), stop=(kc == 7)), reads=['wGb'] + hT_keys(t0, t0 + pbk), writes=[pk])
                    P.op('dve', lambda e: e.scalar_tensor_tensor(out=XA[rows, t0:t0 + pbk], in0=pz[pb:pb + 4, 0:pbk], scalar=gbT[rows, d:d + 1],
                                                                 in1=XB[rows, t0:t0 + pbk], op0=ALU.add, op1=ALU.add),
                         reads=[pk, 'gbT', 'XB', 'XA'], writes=['XA'])
                P.op('dve', lambda e: e.tensor_reduce(out=S['ac'][:, 0:NC], in_=A3, axis=AX.X, op=ALU.max), reads=['XA'], writes=['sm_ac'])
                P.op('dve', lambda e: e.tensor_scalar(out=S['bl'][:, 0:NC], in0=B3[:, :, ce_], scalar1=-1.0, scalar2=None, op0=ALU.mult),
                     reads=['XB'], writes=['sm_bl'])
                if is_sample:
                    P.dma(S['m0'][:, 0:1], s_m[d, :].rearrange("(h o) -> h o", o=1), writes=['sm_m0'])
                else:
                    P.op('dve', lambda e: e.memset(S['m0'][:, 0:1], 0.0), writes=['sm_m0'])
                if d == 0:
                    P.op('dve', lambda e: e.tensor_tensor_scan(out=S['M'][:, 0:NC], data0=S['ac'][:, 0:NC], data1=S['bl'][:, 0:NC],
                                                               initial=S['m0'][:, 0:1], op0=ALU.max, op1=ALU.add),
                         reads=['sm_ac', 'sm_bl', 'sm_m0'], writes=['sm_M'])
                    P.op('dve', lambda e: e.tensor_copy(out=S['MP'][:, 0:1], in_=S['m0'][:, 0:1]), reads=['sm_m0'], writes=['sm_MP'])
                    if NC > 1:
                        P.op('dve', lambda e: e.tensor_copy(out=S['MP'][:, 1:NC], in_=S['M'][:, 0:NC - 1]), reads=['sm_M', 'sm_MP'], writes=['sm_MP'])
                else:
                    P.op('dve', lambda e: e.tensor_tensor_scan(out=S['M'][:, 0:NC][:, ::-1], data0=S['ac'][:, 0:NC][:, ::-1],
                                                               data1=S['bl'][:, 0:NC][:, ::-1], initial=S['m0'][:, 0:1],
                                                               op0=ALU.max, op1=ALU.add),
                         reads=['sm_ac', 'sm_bl', 'sm_m0'], writes=['sm_M'])
                    P.op('dve', lambda e: e.tensor_copy(out=S['MP'][:, NC - 1:NC], in_=S['m0'][:, 0:1]), reads=['sm_m0'], writes=['sm_MP'])
                    if NC > 1:
                        P.op('dve', lambda e: e.tensor_copy(out=S['MP'][:, 0:NC - 1], in_=S['M'][:, 1:NC]), reads=['sm_M', 'sm_MP'], writes=['sm_MP'])
                P.op('dve', lambda e: e.tensor_tensor(out=S['mu'][:, 0:NC], in0=S['MP'][:, 0:NC], in1=S['ac'][:, 0:NC], op=ALU.max),
                     reads=['sm_MP', 'sm_ac'], writes=['sm_mu'])
                P.op('dve', lambda e: e.tensor_tensor(out=S['al'][:, 0:NC], in0=S['MP'][:, 0:NC], in1=S['mu'][:, 0:NC], op=ALU.subtract),
                     reads=['sm_MP', 'sm_mu'], writes=['sm_al'])
                P.op('act', lambda e: e.activation(out=S['al'][:, 0:NC], in_=S['al'][:, 0:NC], func=AF.Exp), reads=['sm_al'], writes=['sm_al'])
                mub = S['mu'][:, 0:NC].unsqueeze(2).to_broadcast([4, NC, LC])
                P.op('dve', lambda e: e.tensor_tensor(out=A3, in0=A3, in1=mub, op=ALU.subtract), reads=['XA', 'sm_mu'], writes=['XA'])
                P.op('dve', lambda e: e.tensor_tensor(out=B3, in0=B3, in1=mub, op=ALU.subtract), reads=['XB', 'sm_mu'], writes=['XB'])
                P.op('dve', lambda e: e.tensor_scalar(out=XA[rows, 0:T], in0=XA[rows, 0:T], scalar1=LNK, scalar2=None, op0=ALU.add),
                     reads=['XA'], writes=['XA'])
                P.op('act', lambda e: e.activation(out=XA[rows, 0:T], in_=XA[rows, 0:T], func=AF.Exp), reads=['XA'], writes=['XA'])
                P.op('act', lambda e: e.activation(out=XB[rows, 0:T], in_=XB[rows, 0:T], func=AF.Exp), reads=['XB'], writes=['XB'])
                pz, pk = nps()
                for c in range(NC):
                    P.op('pe', lambda e: e.transpose(out=pz[:, c * 8:c * 8 + 4], in_=XA[rows, c * LC:(c + 1) * LC],
                                                     identity=ident[rows, pb:pb + 4]), reads=['XA', 'ident'], writes=[pk])
                    P.op('pe', lambda e: e.transpose(out=pz[:, c * 8 + 4:c * 8 + 8], in_=XB[rows, c * LC:(c + 1) * LC],
                                                     identity=ident[rows, pb:pb + 4]), reads=['XB', 'ident'], writes=[pk])
                P.op('dve', lambda e: e.tensor_copy(out=Wtok[:, d, 0:NC, :], in_=pz[:, 0:NC * 8].rearrange("p (c k) -> p c k", k=8)),
                     reads=[pk], writes=['Wtok'])
                P.op('dve', lambda e: e.tensor_copy(out=Wtokb[:, d, 0:NC, :], in_=Wtok[:, d, 0:NC, 0:4]), reads=['Wtok'], writes=['Wtokb'])
                pz, pk = nps()
                for hd in range(4):
                    P.op('pe', lambda e: e.matmul(pz[:, hd * 16:hd * 16 + NC], SEL[rows, hd, :], S['al'][:, 0:NC], start=True, stop=True),
                         reads=['SEL', 'sm_al'], writes=[pk])
                P.op('dve', lambda e: e.tensor_copy(out=ALb[:, d, :, 0:NC], in_=pz[:, 0:64].rearrange("p (h c) -> p h c", c=16)[:, :, 0:NC]),
                     reads=[pk], writes=['ALb'])

        def conv_tile(dst, dkey, slot_j, widx, t0src, T, is_sample):
            X = FT[1][:, 0:T]
            A = FT[0][:, 0:T]
            if is_sample:
                R_, Cw = T // 64, 64
                taps = [(dr, dc) for dr in (-1, 0, 1) for dc in (-1, 0, 1)]
            else:
                R_, Cw = 1, T
                taps = [(0, dc) for dc in (-1, 0, 1)]
            X3 = X.rearrange("p (r c) -> p r c", c=Cw)
            A3 = A.rearrange("p (r c) -> p r c", c=Cw)
            P.op('dve', lambda e: e.tensor_scalar(out=A, in0=X, scalar1=cw[:, widx, 4:5], scalar2=None, op0=ALU.mult),
                 reads=['FT1', 'cw'], writes=['FT0'])
            for (dr, dc) in taps:
                if dr == 0 and dc == 0:
                    continue
                r0, r1 = max(0, -dr), R_ - max(0, dr)
                c0, c1 = max(0, -dc), Cw - max(0, dc)
                ti = (dr + 1) * 3 + (dc + 1)
                P.op('dve', lambda e: e.scalar_tensor_tensor(out=A3[:, r0:r1, c0:c1], in0=X3[:, r0 + dr:r1 + dr, c0 + dc:c1 + dc],
                                                             scalar=cw[:, widx, ti:ti + 1], in1=A3[:, r0:r1, c0:c1],
                                                             op0=ALU.mult, op1=ALU.add), reads=['FT1', 'FT0', 'cw'], writes=['FT0'])
            P.op('act', lambda e: e.activation(out=dst[:, slot_j, 0:T], in_=A, func=AF.Silu, bias=cb[:, widx:widx + 1], scale=1.0),
                 reads=['FT0', 'cb'], writes=[dkey])

        hacc = [FT[2 + i][:, 0:TS].rearrange("p (j e) -> p j e", e=DH) for i in range(4)]

        def mlstm_head(hd, off, T, is_sample, pidx):
            NC = T // LC
            NTt = T // 128
            pbk = min(512, T)
            for qk in range(2):
                load_w1(qk * 2048 + hd * DH, DH)
                for j in range(4):
                    for b in range(T // pbk):
                        t0 = b * pbk
                        pz, pk = nps()
                        for kc in range(8):
                            P.op('pe', lambda e: e.matmul(pz[:, 0:pbk], wbf1[:, kc, j * 128:(j + 1) * 128], hT[:, kc, t0:t0 + pbk],
                                                          start=(kc == 0), stop=(kc == 7)), reads=['wbf1'] + hT_keys(t0, t0 + pbk), writes=[pk])
                        P.op('act', lambda e: e.copy(out=FT[1][:, t0:t0 + pbk], in_=pz[:, 0:pbk]), reads=[pk], writes=['FT1'])
                    widx = (qk * 4 + hd) * 4 + j
                    if qk == 0:
                        conv_tile(qT, ('yT', 4 + j), j, widx, 0, T, is_sample)
                    else:
                        conv_tile(kT, 'kT', j, widx, 0, T, is_sample)
            load_w1(4096 + hd * DH, DH)
            qkeys = [('yT', 4 + j) for j in range(4)]
            for d in range(2):
                rev = (d == 1)
                if is_sample:
                    P.dma(Cst[:], s_C[d, hd].rearrange("(j p) e -> p j e", p=128), writes=['Cst'])
                    P.dma(nst[:, 0:4], s_n[d, hd].rearrange("(j p) -> p j", p=128), writes=['nst'], allow_slow_non_contiguous=True)
                else:
                    P.op('pool', lambda e: e.memset(Cst[:], 0.0), writes=['Cst'])
                    P.op('pool', lambda e: e.memset(nst[:, 0:4], 0.0), writes=['nst'])
                chunks = list(range(NC))
                if rev:
                    chunks = chunks[::-1]
                for c in chunks:
                    cs = slice(c * LC, (c + 1) * LC)
                    wcol = Wtok[:, d, c, hd:hd + 1]
                    thcol = Wtok[:, d, c, 4 + hd:5 + hd]
                    alcol = ALb[:, d, hd, c:c + 1]
                    pv, pvk = nps()
                    for kc in range(8):
                        P.op('pe', lambda e: e.matmul(pv[:, 0:DH], hT[:, kc, cs], wbf1[:, kc, :], start=(kc == 0), stop=(kc == 7)),
                             reads=['wbf1', ('hT', c)], writes=[pvk])
                    P.op('act', lambda e: e.copy(out=vch[:], in_=pv[:, 0:DH]), reads=[pvk], writes=['vch'])
                    pt, ptk = nps()
                    ptb = pt[:].bitcast(BF16)
                    for j in range(4):
                        P.op('pe', lambda e: e.transpose(out=ptb[:, j * 128:(j + 1) * 128], in_=kT[:, j, cs], identity=identb[:]),
                             reads=['kT', 'identb'], writes=[ptk])
                    P.op('act', lambda e: e.copy(out=ktok[:], in_=ptb[:, 0:DH]), reads=[ptk], writes=['ktok'])
                    ps_, psk = nps()
                    for j in range(4):
                        P.op('pe', lambda e: e.matmul(ps_[:, 0:128], kT[:, j, cs], qT[:, j, cs], start=(j == 0), stop=(j == 3)),
                             reads=['kT'] + qkeys, writes=[psk])
                    P.op('dve', lambda e: e.scalar_tensor_tensor(out=sTs[:], in0=ps_[:, 0:128], scalar=wcol, in1=mC[:, d, :],
                                                                 op0=ALU.mult, op1=ALU.mult), reads=[psk, 'Wtok', 'mC'], writes=['sTs'])
                    P.op('dve', lambda e: e.tensor_scalar(out=Cst[:], in0=Cst[:], scalar1=alcol, scalar2=None, op0=ALU.mult),
                         reads=['Cst', 'ALb'], writes=['Cst'])
                    P.op('act', lambda e: e.copy(out=Cbf[:], in_=Cst[:]), reads=['Cst'], writes=['Cbf'])
                    P.op('dve', lambda e: e.tensor_scalar(out=nst[:, 0:4], in0=nst[:, 0:4], scalar1=alcol, scalar2=None, op0=ALU.mult),
                         reads=['nst', 'ALb'], writes=['nst'])
                    P.op('dve', lambda e: e.tensor_copy(out=nbf[:], in_=nst[:, 0:4]), reads=['nst'], writes=['nbf'])
                    pn, pnk = nps()
                    for j in range(4):
                        P.op('pe', lambda e: e.matmul(pn[:, 0:DH], qT[:, j, cs], Cbf[:, j, :], start=(j == 0), stop=False),
                             reads=qkeys + ['Cbf'], writes=[pnk])
                    P.op('pe', lambda e: e.matmul(pn[:, 0:DH], sTs[:], vch[:], start=False, stop=True),
                         reads=['sTs', 'vch'], writes=[pnk])
                    pd_, pdk = nps()
                    for j in range(4):
                        P.op('pe', lambda e: e.matmul(pd_[:, 0:1], qT[:, j, cs], nbf[:, j:j + 1], start=(j == 0), stop=False),
                             reads=qkeys + ['nbf'], writes=[pdk])
                    P.op('pe', lambda e: e.matmul(pd_[:, 0:1], sTs[:], onesb[:], start=False, stop=True), reads=['sTs', 'onesb'], writes=[pdk])
                    P.op('act', lambda e: e.activation(out=dstat[:, 2:3], in_=pd_[:, 0:1], func=AF.Abs), reads=[pdk], writes=['dstat'])
                    P.op('dve', lambda e: e.tensor_tensor(out=dstat[:, 0:1], in0=dstat[:, 2:3], in1=thcol, op=ALU.max),
                         reads=['dstat', 'Wtok'], writes=['dstat'])
                    P.op('dve', lambda e: e.reciprocal(out=dstat[:, 1:2], in_=dstat[:, 0:1]), reads=['dstat'], writes=['dstat'])
                    hdst = hacc[c // 4][:, c % 4, :]
                    if d == 0:
                        P.op('act', lambda e: e.activation(out=hdst, in_=pn[:, 0:DH], func=AF.Identity, scale=dstat[:, 1:2]),
                             reads=[pnk, 'dstat'], writes=[('hacc', c)])
                    else:
                        P.op('dve', lambda e: e.scalar_tensor_tensor(out=hdst, in0=pn[:, 0:DH], scalar=dstat[:, 1:2], in1=hdst,
                                                                     op0=ALU.mult, op1=ALU.add), reads=[pnk, 'dstat', ('hacc', c)], writes=[('hacc', c)])
                    P.op('act', lambda e: e.activation(out=vw[:], in_=vch[:], func=AF.Identity, scale=wcol),
                         reads=['vch', 'Wtok'], writes=['vw'])
                    for j in range(4):
                        pc, pck = nps()
                        P.op('pe', lambda e: e.matmul(pc[:, 0:DH], ktok[:, j * 128:(j + 1) * 128], vw[:], start=True, stop=True),
                             reads=['ktok', 'vw'], writes=[pck])
                        P.op('dve', lambda e: e.tensor_tensor(out=Cst[:, j, :], in0=Cst[:, j, :], in1=pc[:, 0:DH], op=ALU.add),
                             reads=[pck, 'Cst'], writes=['Cst'])
                    pq_, pqk = nps()
                    for j in range(4):
                        P.op('pe', lambda e: e.matmul(pq_[:, j:j + 1], ktok[:, j * 128:(j + 1) * 128], Wtokb[:, d, c, hd:hd + 1],
                                                      start=True, stop=True), reads=['ktok', 'Wtokb'], writes=[pqk])
                    P.op('dve', lambda e: e.tensor_tensor(out=nst[:, 0:4], in0=nst[:, 0:4], in1=pq_[:, 0:4], op=ALU.add),
                         reads=[pqk, 'nst'], writes=['nst'])
                if not is_sample:
                    P.dma(o_C[pidx, d, hd].rearrange("(j p) e -> p j e", p=128), Cst[:], reads=['Cst'], q='pool')
                    P.dma(o_n[pidx, d, hd].rearrange("(j p) -> p j", p=128), nst[:, 0:4], reads=['nst'], q='pool', allow_slow_non_contiguous=True)
            load_w1(6144 + hd * DH, DH)
            for tt in range(NTt):
                hdst = hacc[tt // 4][:, tt % 4, :]
                pz, pk = nps()
                for kc in range(8):
                    P.op('pe', lambda e: e.matmul(pz[:, 0:DH], hT[:, kc, tt * 128:(tt + 1) * 128], wbf1[:, kc, :],
                                                  start=(kc == 0), stop=(kc == 7)), reads=['wbf1', ('hT', tt)], writes=[pk])
                P.op('act', lambda e: e.activation(out=FT[0][:, 0:DH], in_=pz[:, 0:DH], func=AF.Sigmoid), reads=[pk], writes=['FT0'])
                P.op('dve', lambda e: e.tensor_tensor(out=hdst, in0=hdst, in1=FT[0][:, 0:DH], op=ALU.mult),
                     reads=['FT0', ('hacc', tt)], writes=[('hacc', tt)])
                P.op('act', lambda e: e.activation(out=FT[0][:, 0:DH], in_=hdst, func=AF.Square, accum_out=dstat[:, 4:5]),
                     reads=[('hacc', tt), 'FT0'], writes=['FT0', 'dstat'])
                P.op('dve', lambda e: e.tensor_scalar(out=dstat[:, 5:6], in0=dstat[:, 4:5], scalar1=1.0 / DH, scalar2=1e-6,
                                                      op0=ALU.mult, op1=ALU.add), reads=['dstat'], writes=['dstat'])
                P.op('act', lambda e: e.activation(out=dstat[:, 6:7], in_=dstat[:, 5:6], func=AF.Sqrt), reads=['dstat'], writes=['dstat'])
                P.op('dve', lambda e: e.reciprocal(out=dstat[:, 7:8], in_=dstat[:, 6:7]), reads=['dstat'], writes=['dstat'])
                P.op('dve', lambda e: e.tensor_scalar(out=hdst, in0=hdst, scalar1=dstat[:, 7:8], scalar2=None, op0=ALU.mult),
                     reads=[('hacc', tt), 'dstat'], writes=[('hacc', tt)])
            load_w1(8192 + hd * DH, DH)
            for tt in range(NTt):
                hdst = hacc[tt // 4][:, tt % 4, :]
                pz, pk = nps()
                for kc in range(8):
                    P.op('pe', lambda e: e.matmul(pz[:, 0:DH], hT[:, kc, tt * 128:(tt + 1) * 128], wbf1[:, kc, :],
                                                  start=(kc == 0), stop=(kc == 7)), reads=['wbf1', ('hT', tt)], writes=[pk])
                P.op('act', lambda e: e.activation(out=FT[0][:, 0:DH], in_=pz[:, 0:DH], func=AF.Silu), reads=[pk], writes=['FT0'])
                P.op('dve', lambda e: e.tensor_tensor(out=hdst, in0=hdst, in1=FT[0][:, 0:DH], op=ALU.mult),
                     reads=['FT0', ('hacc', tt)], writes=[('hacc', tt)])
                pz, pk = nps()
                for j in range(4):
                    P.op('pe', lambda e: e.transpose(out=pz[:, j * 128:(j + 1) * 128], in_=hdst[:, j * 128:(j + 1) * 128], identity=ident[:]),
                         reads=[('hacc', tt), 'ident'], writes=[pk])
                for j in range(4):
                    P.op('act', lambda e: e.activation(out=yT[:, j, tt * 128:(tt + 1) * 128], in_=pz[:, j * 128:(j + 1) * 128],
                                                       func=AF.Identity, scale=mng[:, hd * 4 + j:hd * 4 + j + 1]),
                         reads=[pk, 'mng'], writes=[('yT', j)])

        for si in seq_ids:
            off, T, cidx, is_sample = SEQS[si]
            P.barrier()
            make_gate(1, cidx)
            make_hT(1, x1, 'x1', off, T, cidx)
            P.barrier()
            gates_seq(T, is_sample)
            if not is_sample:
                for d in range(2):
                    lastc = (T // LC - 1) if d == 0 else 0
                    P.dma(o_m[si - 1, d, :].rearrange("(h o) -> h o", o=1), sm['M'][32 * d:32 * d + 4, lastc:lastc + 1],
                          reads=['sm_M'], q='pool')
            for hd in range(4):
                P.barrier()
                mlstm_head(hd, off, T, is_sample, si - 1)
                if debug and debug.get('dump_y'):
                    for j in range(4):
                        dump("yM%d_%d" % (hd, j), yT[:, j, 0:T], [('yT', j)], T, col0=off)
                P.barrier()
                load_wo(w_out_odd[hd * DH:(hd + 1) * DH, :], 128, nk=4)
                last = (hd == 3)
                outproj(1, [(128, s_) for s_ in range(4)], x1, 'x1', (yout if last else x1), ('yout' if last else 'x1'),
                        off, T, cidx, final=last)
        P.barrier()
        L1.close()
    P.finish()
    sems = {s: es.enter_context(nc.semaphore(s)) for s in P.sem_names}
    P.emit(sems)
    es.close()
    global _last_dslot
    _last_dslot = dslot if debug else {}
    return nc, P


def host_inputs(inp, core):
    f = lambda a: np.ascontiguousarray(a, dtype=np.float32)
    b = core % 2
    m = {}
    m["xin"] = f(np.concatenate([inp["x_sample"][b], inp["x_prompt"][2 * core], inp["x_prompt"][2 * core + 1]], axis=0))
    cond = np.stack([inp["c"][b], inp["c_ctx"]], axis=0)
    m["condT"] = f(cond.reshape(2, 8, 128).transpose(2, 1, 0))
    m["s_hgrn"] = f(inp["state_hgrn"][b, 0])
    m["s_rwkv"] = f(inp["state_rwkv"][b, 0])
    m["s_C"] = f(inp["state_mlstm_C"][b, 0])
    m["s_n"] = f(inp["state_mlstm_n"][b, 0])
    m["s_m"] = f(inp["state_mlstm_m"][b, 0])
    m["w_mod"] = f(inp["w_mod"])
    m["b_modT"] = f(inp["b_mod"].reshape(2, 24, 128).transpose(2, 0, 1))
    m["norm_gT"] = f(inp["norm_g"].reshape(2, 8, 128).transpose(2, 0, 1))
    m["fnorm_gT"] = f(inp["final_norm_g"].reshape(8, 128).T)
    w = inp["w_in_even"][0]
    DA = 1024
    wA = np.stack([np.concatenate([w[:, g * DA + h * 128: g * DA + (h + 1) * 128] for g in (0, 1, 4, 2, 3)], axis=1)
                   for h in range(8)], axis=0)
    m["wA"] = f(wA)
    o = 5 * DA
    zb0 = o + 3328
    wB = np.stack([np.concatenate([w[:, o + g * 1024 + h * 64: o + g * 1024 + (h + 1) * 64] for g in (0, 1, 2)]
                                  + [w[:, zb0 + h * 64: zb0 + (h + 1) * 64]], axis=1) for h in range(16)], axis=0)
    m["wB"] = f(wB)
    m["wLR"] = f(w[:, o + 3072: o + 3328])
    m["w_out_even"] = f(inp["w_out_even"][0])
    m["lbT"] = f(inp["hgrn_lb_logits"].reshape(2, 8, 128).transpose(2, 0, 1))
    m["hg_gT"] = f(inp["hgrn_norm_g"][0].reshape(8, 128).T)
    mu = inp["rwkv_shift_mu"][0]
    mr = np.zeros((64, 2, 4, 16), np.float32)
    for g in range(3):
        mr[:, :, g, :] = mu[:, g * 1024:(g + 1) * 1024].reshape(2, 16, 64).transpose(2, 0, 1)
    m["mu_rkv"] = mr
    m["mu_lr"] = f(mu[:, 3072:3328].reshape(2, 4, 64).transpose(2, 0, 1))
    m["w0T"] = f(inp["rwkv_w0"][0].reshape(2, 16, 64).transpose(2, 0, 1))
    m["a0T"] = f(inp["rwkv_a0"][0].reshape(2, 16, 64).transpose(2, 0, 1))
    m["w2"] = f(inp["rwkv_w2"][0])
    m["a2"] = f(inp["rwkv_a2"][0])
    m["kkT"] = f(inp["rwkv_k_k"][0].reshape(16, 64).T)
    m["kaT"] = f(inp["rwkv_k_a"][0].reshape(16, 64).T)
    m["rkT"] = f(inp["rwkv_r_k"][0].T)
    m["gngT"] = f(inp["rwkv_gn_g"][0].reshape(16, 64).T)
    m["gnbT"] = f(inp["rwkv_gn_b"][0].reshape(16, 64).T)
    s = np.arange(128)[:, None]
    t = np.arange(128)[None, :]
    same = (s // 32) == (t // 32)
    m["maskH"] = np.stack([(same & (s <= t)), (same & (s >= t))]).astype(np.float32)
    m["ident_in"] = np.eye(128, dtype=np.float32)
    s6 = np.arange(64)[:, None]
    t6 = np.arange(64)[None, :]
    mr_ = np.zeros((2, 3, 64, 128), np.float32)
    for d_, (st_, inc_) in enumerate([((s6 < t6), (s6 <= t6)), ((s6 > t6), (s6 >= t6))]):
        st_ = st_.astype(np.float32)
        inc_ = inc_.astype(np.float32)
        mr_[d_, 0, :, 0:64] = -st_
        mr_[d_, 0, :, 64:128] = -inc_
        mr_[d_, 1, :, 0:64] = st_
        mr_[d_, 1, :, 64:128] = inc_
        mr_[d_, 2, :, 0:64] = -(st_.T)
    m["maskR"] = mr_
    m["w_in_odd"] = f(inp["w_in_odd"][0])
    m["w_out_odd"] = f(inp["w_out_odd"][0])
    m["maskC"] = np.stack([(s <= t), (s >= t)]).astype(np.float32)
    sel = np.zeros((36, 4, 128), np.float32)
    gbt = np.zeros((36, 4), np.float32)
    for pb_ in (0, 32):
        for k_ in range(4):
            sel[pb_ + k_, k_, :] = 1.0
        gbt[pb_:pb_ + 4, :] = inp["mlstm_gate_b"][0].T
    m["sel_d"] = sel
    m["gbT_d"] = gbt
    m["cw_d"] = f(inp["mlstm_conv_w"][0].reshape(9, 32, 128).transpose(2, 1, 0))
    m["cb_d"] = f(inp["mlstm_conv_b"][0].reshape(32, 128).T)
    m["mng_d"] = f(inp["mlstm_norm_g"][0].reshape(16, 128).T)
    return m


def kernel(**inp):
    inp = {k: np.asarray(v) for k, v in inp.items()}
    nc, P = build()
    in_maps = [host_inputs(inp, c) for c in range(NCORES)]
    res = run_bass_kernel_spmd(nc, in_maps, core_ids=list(range(NCORES)))
    r = res.results
    y_prompt = np.zeros((16, TP, D), np.float32)
    y_sample = np.zeros((2, TS, D), np.float32)
    for c in range(NCORES):
        y_prompt[2 * c] = r[c]["yout"][TS:TS + TP]
        y_prompt[2 * c + 1] = r[c]["yout"][TS + TP:]
    for b in range(2):
        y_sample[b] = r[b]["yout"][0:TS]
    new_hgrn = np.concatenate([r[c]["o_hgrn"] for c in range(NCORES)], axis=0)[:, None]
    new_rwkv = np.concatenate([r[c]["o_rwkv"] for c in range(NCORES)], axis=0)[:, None]
    new_C = np.concatenate([r[c]["o_C"] for c in range(NCORES)], axis=0)[:, None]
    new_n = np.concatenate([r[c]["o_n"] for c in range(NCORES)], axis=0)[:, None]
    new_m = np.concatenate([r[c]["o_m"] for c in range(NCORES)], axis=0)[:, None]
    return (y_prompt, y_sample, new_hgrn.astype(np.float32), new_rwkv.astype(np.float32),
            new_C.astype(np.float32), new_n.astype(np.float32), new_m.astype(np.float32))
```

```python
import contextlib
import numpy as np
import concourse.bass as bass
import concourse.mybir as mybir
from concourse.bass_utils import run_bass_kernel_spmd

F32 = mybir.dt.float32
BF16 = mybir.dt.bfloat16
ALU = mybir.AluOpType
AF = mybir.ActivationFunctionType
AX = mybir.AxisListType

D = 1024
TS = 2048
TP = 256
TT = TS + 2 * TP
NCORES = 8


class _Rec:
    def __init__(self):
        self.calls = []

    def __getattr__(self, name):
        def f(*a, **k):
            self.calls.append((name, a, k))
            return self
        return f


class Prog:
    ENGS = ['pe', 'dve', 'act', 'pool', 'sp']
    NDMA = 16

    def __init__(self, nc):
        self.nc = nc
        self.ops = {e: [] for e in self.ENGS}
        self.cnt = {}
        self.waited = {e: {} for e in self.ENGS}
        self.last_write = {}
        self.readers = {}
        self.dma_rr = 0
        self.sem_names = list(self.ENGS) + ['d%d' % i for i in range(self.NDMA)]
        for s in self.sem_names:
            self.cnt[s] = 0
        self.n_ops = 0

    def _deps(self, eng, reads, writes):
        deps = {}

        def add(p):
            if p is None:
                return
            f, n = p
            if f == 'pe' and eng == 'pe':
                return
            if n > deps.get(f, 0):
                deps[f] = n
        for k in reads:
            add(self.last_write.get(k))
        for k in writes:
            add(self.last_write.get(k))
            for p in self.readers.get(k, ()):
                add(p)
        waits = []
        for f, n in deps.items():
            if n > self.waited[eng].get(f, 0):
                waits.append((f, n))
                self.waited[eng][f] = n
        return waits

    def _commit(self, tag, reads, writes):
        for k in reads:
            lst = self.readers.setdefault(k, [])
            lst[:] = [p for p in lst if p[0] != tag[0]]
            lst.append(tag)
        for k in writes:
            self.last_write[k] = tag
            self.readers[k] = []

    def op(self, eng, fn, reads=(), writes=()):
        rec = _Rec()
        fn(rec)
        name, a, k = rec.calls[0]
        fn = (lambda e, name=name, a=a, k=k: getattr(e, name)(*a, **k))
        waits = self._deps(eng, reads, writes)
        self.cnt[eng] += 1
        tag = (eng, self.cnt[eng])
        self.ops[eng].append((waits, fn, eng, 1))
        self._commit(tag, reads, writes)
        self.n_ops += 1

    def dma(self, out, in_, reads=(), writes=(), q='sp', **kw):
        d = 'd%d' % self.dma_rr
        self.dma_rr = (self.dma_rr + 1) % self.NDMA
        waits = self._deps(q, reads, writes)
        prev = self.cnt[d]
        if prev > self.waited[q].get(d, 0):
            waits.append((d, prev))
            self.waited[q][d] = prev
        self.cnt[d] += 16
        tag = (d, self.cnt[d])
        self.ops[q].append((waits, (lambda e: e.dma_start(out=out, in_=in_, **kw)), d, 16))
        self._commit(tag, reads, writes)
        self.n_ops += 1

    def barrier(self):
        allsems = list(self.sem_names)
        for e in self.ENGS:
            waits = []
            for f in allsems:
                if self.cnt[f] > self.waited[e].get(f, 0):
                    waits.append((f, self.cnt[f]))
                    self.waited[e][f] = self.cnt[f]
            self.ops[e].append((waits, None, None, 0))

    def finish(self, q='sp'):
        waits = []
        for i in range(self.NDMA):
            d = 'd%d' % i
            if self.cnt[d] > self.waited[q].get(d, 0):
                waits.append((d, self.cnt[d]))
                self.waited[q][d] = self.cnt[d]
        self.ops[q].append((waits, None, None, 0))

    def emit(self, sems):
        ops = self.ops

        def run(e, lst):
            for waits, fn, semname, inc in lst:
                for f, n in waits:
                    e.wait_ge(sems[f], n)
                if fn is not None:
                    fn(e).then_inc(sems[semname], inc)
        with self.nc.Block() as block:
            @block.tensor
            def _(e):
                run(e, ops['pe'])

            @block.vector
            def _(e):
                run(e, ops['dve'])

            @block.scalar
            def _(e):
                run(e, ops['act'])

            @block.gpsimd
            def _(e):
                run(e, ops['pool'])

            @block.sync
            def _(e):
                run(e, ops['sp'])


SEQS = [(0, TS, 0, True), (TS, TP, 1, False), (TS + TP, TP, 1, False)]


def build(debug=None):
    nc = bass.Bass('TRN2', target_bir_lowering=False)
    P = Prog(nc)
    es = contextlib.ExitStack()

    def din(name, shape):
        return nc.dram_tensor(name, list(shape), F32, kind="ExternalInput").ap()

    def dout(name, shape):
        return nc.dram_tensor(name, list(shape), F32, kind="ExternalOutput").ap()

    xin = din("xin", [TT, D])
    condT = din("condT", [128, 8, 2])
    s_hgrn = din("s_hgrn", [2, 8, 128, 128])
    s_rwkv = din("s_rwkv", [2, 16, 64, 64])
    s_C = din("s_C", [2, 4, 512, 512])
    s_n = din("s_n", [2, 4, 512])
    s_m = din("s_m", [2, 4])
    w_mod = din("w_mod", [2, D, 3 * D])
    b_modT = din("b_modT", [128, 2, 24])
    norm_gT = din("norm_gT", [128, 2, 8])
    fnorm_gT = din("fnorm_gT", [128, 8])
    wA = din("wA", [8, D, 640])
    wB = din("wB", [16, D, 256])
    wLR = din("wLR", [D, 256])
    w_out_even = din("w_out_even", [2 * D, D])
    lbT = din("lbT", [128, 2, 8])
    hg_gT = din("hg_gT", [128, 8])
    mu_rkv = din("mu_rkv", [64, 2, 4, 16])
    mu_lr = din("mu_lr", [64, 2, 4])
    w0T = din("w0T", [64, 2, 16])
    a0T = din("a0T", [64, 2, 16])
    w2 = din("w2", [2, 64, D])
    a2 = din("a2", [2, 64, D])
    kkT = din("kkT", [64, 16])
    kaT = din("kaT", [64, 16])
    rkT = din("rkT", [64, 16])
    gngT = din("gngT", [64, 16])
    gnbT = din("gnbT", [64, 16])
    maskR = din("maskR", [2, 3, 64, 128])
    maskH = din("maskH", [2, 128, 128])
    ident_d = din("ident_in", [128, 128])
    w_in_odd = din("w_in_odd", [D, 10256])
    w_out_odd = din("w_out_odd", [2 * D, D])
    maskC = din("maskC", [2, 128, 128])
    sel_d = din("sel_d", [36, 4, 128])
    gbT_d = din("gbT_d", [36, 4])
    cw_d = din("cw_d", [128, 32, 9])
    cb_d = din("cb_d", [128, 32])
    mng_d = din("mng_d", [128, 16])

    yout = dout("yout", [TT, D])
    o_hgrn = dout("o_hgrn", [2, 2, 8, 128, 128])
    o_rwkv = dout("o_rwkv", [2, 2, 16, 64, 64])
    o_C = dout("o_C", [2, 2, 4, 512, 512])
    o_n = dout("o_n", [2, 2, 4, 512])
    o_m = dout("o_m", [2, 2, 4])
    dbg = dout("dbg", [40, 128, TT]) if debug else None
    dslot = {}
    dumpt = {}
    x1 = dout("x1", [TT, D]) if debug else nc.dram_tensor("x1", [TT, D], F32, kind="Internal").ap()

    def sb(name, shape, dt=F32):
        return es.enter_context(nc.sbuf_tensor(name, list(shape), dt))

    pstiles = [es.enter_context(nc.psum_tensor("ps%d" % i, [128, 512], F32)) for i in range(8)]
    psrr = [0]

    def nps():
        i = psrr[0]
        psrr[0] = (i + 1) % 8
        return pstiles[i], 'ps%d' % i

    def dump(name, ap, keys, n, col0=0, parts=128):
        if not debug:
            return
        slot = dslot.setdefault(name, len(dslot))
        dt_ = dumpt['tile']
        for c0 in range(0, n, 512):
            w_ = min(512, n - c0)
            P.op('pool', (lambda e, c0=c0, w_=w_: e.tensor_copy(out=dt_[0:parts, 0:w_], in_=ap[:, c0:c0 + w_])),
                 reads=keys, writes=['dumpt'])
            P.dma(dbg[slot, 0:parts, col0 + c0:col0 + c0 + w_], dt_[0:parts, 0:w_], reads=['dumpt'])

    if debug:
        dumpt['tile'] = sb("dumpt", [128, 512])

    ident = sb("ident", [128, 128])
    ones = sb("ones", [128, 128])
    P.dma(ident[:], ident_d[:], writes=['ident'])
    P.op('dve', lambda e: e.memset(ones[:], 1.0), writes=['ones'])

    condT_sb = sb("condT_sb", [128, 8, 2])
    bmod_sb = sb("bmod_sb", [128, 2, 24])
    ng_sb = sb("ng_sb", [128, 2, 8])
    fng_sb = sb("fng_sb", [128, 8])
    lb_sb = sb("lb_sb", [128, 2, 8])
    hgg_sb = sb("hgg_sb", [128, 8])
    for t_, d_, k_ in [(condT_sb, condT, 'condT'), (bmod_sb, b_modT, 'bmod'), (ng_sb, norm_gT, 'ng'),
                       (fng_sb, fnorm_gT, 'fng'), (lb_sb, lbT, 'lb'), (hgg_sb, hg_gT, 'hgg')]:
        P.dma(t_[:], d_[:], writes=[k_])

    scT = sb("scT", [128, 8, 2])
    P.op('act', lambda e: e.activation(out=scT[:], in_=condT_sb[:], func=AF.Silu), reads=['condT'], writes=['scT'])
    mT = sb("mT", [128, 2, 24, 2])
    sc1 = sb("sc1", [128, 2, 8, 2])
    gate_bc = sb("gate_bc", [128, D])
    dg = sb("dg", [128, 128])

    def make_gate(l, c):
        if True:
            for half in range(2):
                pz, pk = nps()
                for kq in range(4):
                    kc = half * 4 + kq
                    P.op('dve', lambda e: e.tensor_scalar(
                        out=dg[:], in0=ident[:], scalar1=mT[:, l, 16 + kc, c:c + 1], scalar2=None, op0=ALU.mult),
                        reads=['ident', 'mT'], writes=['dg'])
                    P.op('pe', lambda e: e.matmul(pz[:, kq * 128:(kq + 1) * 128], ones[:], dg[:], start=True, stop=True),
                         reads=['ones', 'dg'], writes=[pk])
                P.op('act', lambda e: e.copy(out=gate_bc[:, half * 512:(half + 1) * 512], in_=pz[:]),
                     reads=[pk], writes=['gate_bc'])

    with contextlib.ExitStack() as es2:
        wm = [es2.enter_context(nc.sbuf_tensor("wm%d" % i, [128, 8, 512], F32)) for i in range(2)]
        for l in range(2):
            for cbk in range(6):
                i = (l * 6 + cbk) % 2
                P.dma(wm[i][:], w_mod[l].rearrange("(kc p) n -> p kc n", p=128)[:, :, cbk * 512:(cbk + 1) * 512],
                      writes=['wm%d' % i])
                pz, pk = nps()
                for j in range(4):
                    for kc in range(8):
                        P.op('pe', lambda e: e.matmul(pz[:, j * 2:(j + 1) * 2], wm[i][:, kc, j * 128:(j + 1) * 128],
                                                      scT[:, kc, :], start=(kc == 0), stop=(kc == 7)),
                             reads=['wm%d' % i, 'scT'], writes=[pk])
                P.op('dve', lambda e: e.tensor_tensor(out=mT[:, l, cbk * 4:(cbk + 1) * 4, :],
                                                      in0=pz[:, 0:8].rearrange("p (j c) -> p j c", c=2),
                                                      in1=bmod_sb[:, l, cbk * 4:(cbk + 1) * 4].unsqueeze(2).to_broadcast([128, 4, 2]),
                                                      op=ALU.add),
                     reads=[pk, 'bmod'], writes=['mT'])
    P.barrier()
    for l in range(2):
        P.op('dve', lambda e: e.scalar_tensor_tensor(
            out=sc1[:, l], in0=mT[:, l, 8:16, :], scalar=1.0,
            in1=ng_sb[:, l, :].unsqueeze(2).to_broadcast([128, 8, 2]), op0=ALU.add, op1=ALU.mult),
            reads=['mT', 'ng'], writes=['sc1'])
    fng_holder = {}

    def make_fng():
        fng_bc = sb("fng_bc", [128, D])
        fng_holder['t'] = fng_bc
        for half in range(2):
            pz, pk = nps()
            for kq in range(4):
                kc = half * 4 + kq
                P.op('dve', (lambda e, kc=kc: e.tensor_scalar(
                    out=dg[:], in0=ident[:], scalar1=fng_sb[:, kc:kc + 1], scalar2=None, op0=ALU.mult)),
                    reads=['ident', 'fng'], writes=['dg'])
                P.op('pe', (lambda e, pz=pz, kq=kq: e.matmul(pz[:, kq * 128:(kq + 1) * 128], ones[:], dg[:],
                                                             start=True, stop=True)),
                     reads=['ones', 'dg'], writes=[pk])
            P.op('act', (lambda e, half=half, pz=pz: e.copy(out=fng_bc[:, half * 512:(half + 1) * 512], in_=pz[:])),
                 reads=[pk], writes=['fng_bc'])

    lbv = sb("lbv", [128, 8])
    oml = sb("oml", [128, 8])
    P.op('dve', lambda e: e.tensor_tensor(out=lbv[:], in0=lb_sb[:, 0, :], in1=lb_sb[:, 1, :], op=ALU.subtract),
         reads=['lb'], writes=['lbv'])
    P.op('act', lambda e: e.activation(out=lbv[:], in_=lbv[:], func=AF.Sigmoid), reads=['lbv'], writes=['lbv'])
    P.op('act', lambda e: e.activation(out=oml[:], in_=lbv[:], func=AF.Identity, bias=1.0, scale=-1.0),
         reads=['lbv'], writes=['oml'])

    hT = sb("hT", [128, 8, TS], BF16)
    yT = sb("yT", [128, 8, TS], BF16)
    st4 = sb("st4", [128, 4])
    wst = sb("wst", [128, 8, 256])
    FT = [sb("FT%d" % i, [128, TS + 32]) for i in range(6)]
    xt = [FT[0][:, 0:D], FT[0][:, D:2 * D]]
    xn = FT[1][:, 0:D]
    junk = FT[1][:, D:2 * D]
    wo_v = [FT[2][:, 0:TS].bitcast(BF16).rearrange("p (s n) -> p s n", n=D),
            FT[3][:, 0:TS].bitcast(BF16).rearrange("p (s n) -> p s n", n=D)]

    def load_wo(src, parts, nk=8):
        v = src.rearrange("(kc p) n -> p kc n", p=parts)
        for c0 in range(0, D, 256):
            w_ = min(256, D - c0)
            P.dma(wst[0:parts, 0:nk, 0:w_], v[:, :, c0:c0 + w_], writes=['wst'])
            for hf in range(nk // 4):
                P.op('act', lambda e: e.copy(out=wo_v[hf][0:parts, :, c0:c0 + w_], in_=wst[0:parts, hf * 4:hf * 4 + 4, 0:w_]),
                     reads=['wst'], writes=['wo_bf'])

    def make_hT(layer, xsrc, xkey, off, T, cidx):
        for tt in range(T // 128):
            i = tt % 2
            P.dma(xt[i], xsrc[off + tt * 128: off + (tt + 1) * 128, :], reads=[(xkey, off // 128 + tt)], writes=['xt%d' % i])
            P.op('act', lambda e: e.activation(out=junk, in_=xt[i], func=AF.Square, accum_out=st4[:, 0:1]),
                 reads=['xt%d' % i], writes=['junk', 'st4'])
            P.op('dve', lambda e: e.tensor_scalar(out=st4[:, 1:2], in0=st4[:, 0:1], scalar1=1.0 / D, scalar2=1e-6,
                                                  op0=ALU.mult, op1=ALU.add), reads=['st4'], writes=['st4'])
            P.op('act', lambda e: e.activation(out=st4[:, 2:3], in_=st4[:, 1:2], func=AF.Sqrt), reads=['st4'], writes=['st4'])
            P.op('dve', lambda e: e.reciprocal(out=st4[:, 3:4], in_=st4[:, 2:3]), reads=['st4'], writes=['st4'])
            P.op('dve', lambda e: e.tensor_scalar(out=xn, in0=xt[i], scalar1=st4[:, 3:4], scalar2=None, op0=ALU.mult),
                 reads=['xt%d' % i, 'st4'], writes=['xn'])
            for half in range(2):
                pz, pk = nps()
                for kq in range(4):
                    kc = half * 4 + kq
                    P.op('pe', lambda e: e.transpose(out=pz[:, kq * 128:(kq + 1) * 128], in_=xn[:, kc * 128:(kc + 1) * 128],
                                                     identity=ident[:]), reads=['xn', 'ident'], writes=[pk])
                for kq in range(4):
                    kc = half * 4 + kq
                    P.op('act', lambda e: e.activation(
                        out=hT[:, kc, tt * 128:(tt + 1) * 128], in_=pz[:, kq * 128:(kq + 1) * 128], func=AF.Identity,
                        bias=mT[:, layer, kc, cidx:cidx + 1], scale=sc1[:, layer, kc, cidx:cidx + 1]),
                        reads=[pk, 'mT', 'sc1'], writes=[('hT', tt)])

    def hT_keys(t0, t1):
        return [('hT', tt) for tt in range(t0 // 128, (t1 + 127) // 128)]

    class WB:
        def __init__(self, tiles, keys):
            self.t, self.k, self.cur, self.pending = tiles, keys, 0, None

        def prefetch(self, tag, loader):
            if self.pending is not None:
                return
            i = 1 - self.cur
            loader(self.t[i], self.k[i])
            self.pending = tag

        def use(self, tag, loader):
            if self.pending == tag:
                self.cur = 1 - self.cur
                self.pending = None
            else:
                assert self.pending is None, (self.pending, tag)
                i = 1 - self.cur
                loader(self.t[i], self.k[i])
                self.cur = i

        @property
        def tile(self):
            return self.t[self.cur]

        @property
        def key(self):
            return self.k[self.cur]

    def load_w(src, ncols, dst=None, dkey='wbf', parts=128, nk=8):
        v = src.rearrange("(kc p) n -> p kc n", p=parts)
        for c0 in range(0, ncols, 256):
            w_ = min(256, ncols - c0)
            P.dma(wst[0:parts, 0:nk, 0:w_], v[:, :, c0:c0 + w_], writes=['wst'])
            P.op('act', lambda e: e.copy(out=dst[0:parts, 0:nk, c0:c0 + w_], in_=wst[0:parts, 0:nk, 0:w_]),
                 reads=['wst'], writes=[dkey])

    def proj(c0, M, t0, n, evac):
        pz, pk = nps()
        for kc in range(8):
            P.op('pe', lambda e: e.matmul(pz[0:M, 0:n], wb0.tile[:, kc, c0:c0 + M], hT[:, kc, t0:t0 + n],
                                          start=(kc == 0), stop=(kc == 7)),
                 reads=[wb0.key] + hT_keys(t0, t0 + n), writes=[pk])
        evac(pz, pk)

    def outproj(layer, groups, xsrc, skey, xdst, dkey, off, T, cidx, final=False):
        for tt in range(T // 128):
            i = tt % 2
            P.dma(xt[i], xsrc[off + tt * 128: off + (tt + 1) * 128, :], reads=[(skey, off // 128 + tt)], writes=['xt%d' % i])
            for half in range(2):
                pz, pk = nps()
                for gi, (K, slot) in enumerate(groups):
                    P.op('pe', lambda e: e.matmul(pz[:, 0:512], yT[0:K, slot, tt * 128:(tt + 1) * 128],
                                                  wo_v[slot // 4][0:K, slot % 4, half * 512:(half + 1) * 512],
                                                  start=(gi == 0), stop=(gi == len(groups) - 1)),
                         reads=[('yT', slot), 'wo_bf'], writes=[pk])
                P.op('dve', lambda e: e.tensor_tensor(out=xn[:, half * 512:(half + 1) * 512], in0=pz[:, 0:512],
                                                      in1=gate_bc[:, half * 512:(half + 1) * 512], op=ALU.mult),
                     reads=[pk, 'gate_bc'], writes=['xn'])
                P.op('dve', lambda e: e.tensor_tensor(out=xt[i][:, half * 512:(half + 1) * 512],
                                                      in0=xt[i][:, half * 512:(half + 1) * 512],
                                                      in1=xn[:, half * 512:(half + 1) * 512], op=ALU.add),
                     reads=['xn', 'xt%d' % i], writes=['xt%d' % i])
            if final:
                P.op('act', lambda e: e.activation(out=junk, in_=xt[i], func=AF.Square, accum_out=st4[:, 0:1]),
                     reads=['xt%d' % i], writes=['junk', 'st4'])
                P.op('dve', lambda e: e.tensor_scalar(out=st4[:, 1:2], in0=st4[:, 0:1], scalar1=1.0 / D, scalar2=1e-6,
                                                      op0=ALU.mult, op1=ALU.add), reads=['st4'], writes=['st4'])
                P.op('act', lambda e: e.activation(out=st4[:, 2:3], in_=st4[:, 1:2], func=AF.Sqrt), reads=['st4'], writes=['st4'])
                P.op('dve', lambda e: e.reciprocal(out=st4[:, 3:4], in_=st4[:, 2:3]), reads=['st4'], writes=['st4'])
                P.op('dve', lambda e: e.scalar_tensor_tensor(out=xt[i], in0=xt[i], scalar=st4[:, 3:4], in1=fng_holder['t'][:],
                                                             op0=ALU.mult, op1=ALU.mult),
                     reads=['xt%d' % i, 'st4', 'fng_bc'], writes=['xt%d' % i])
            P.dma(xdst[off + tt * 128: off + (tt + 1) * 128, :], xt[i], reads=['xt%d' % i], writes=[(dkey, off // 128 + tt)], q='pool')

    L0 = contextlib.ExitStack()

    def sb0(name, shape, dt=F32):
        return L0.enter_context(nc.sbuf_tensor(name, list(shape), dt))

    wb0 = WB([sb0("wbfA", [128, 8, 384], BF16), sb0("wbfB", [128, 8, 384], BF16)], ['wbfA', 'wbfB'])
    TB = 256
    Fq, Fsz, Fvr, For = FT[0][:, 0:TS], FT[1][:, 0:TS], FT[2][:, 0:TS], FT[3][:, 0:TS]
    Fv = Fvr.rearrange("p (j c) -> p j c", c=128)
    Fo = For.rearrange("p (j c) -> p j c", c=128)
    BT = [sb0("BT%d" % i, [128, 256]) for i in range(18)]
    bt = {n_: BT[i] for i, n_ in enumerate(['sg', 'lf', 'kg', 'G', 'br', 'E', 'Ei', 'qt', 'kt', 'kh', 'vT'])}
    khtok = sb0("khtok", [128, TB // 128, 128])
    gam = sb0("gam", [128, TB // 32])
    gref = sb0("gref", [128, TB // 32])
    Sst = [sb0("Sst%d" % i, [128, 128]) for i in range(2)]
    attT = sb0("attT", [128, 128])
    ostat = sb0("ostat", [128, TS // 128, 4])
    mH = sb0("mH", [128, 2, 128])
    P.dma(mH[:], maskH.rearrange("d s t -> s d t"), writes=['mH'])

    def hgrn_head(h, off, T, is_sample, pidx, nxt=None):
        ld_qvz = lambda hh: (lambda dst, dk: load_w(wA[hh][:, 0:384], 384, dst=dst, dkey=dk))
        ld_ffb = lambda hh: (lambda dst, dk: load_w(wA[hh][:, 384:640], 256, dst=dst, dkey=dk))
        wb0.use(('qvz', off, h), ld_qvz(h))
        wb0.prefetch(('ffb', off, h), ld_ffb(h))
        tb = min(TB, T)
        nblk = T // tb
        for b in range(nblk):
            t0 = b * tb
            proj(0, 128, t0, tb, lambda pz, pk: P.op(
                'act', lambda e: e.copy(out=Fq[:, t0:t0 + tb], in_=pz[:, 0:tb]), reads=[pk], writes=['Fq']))
            proj(256, 128, t0, tb, lambda pz, pk: P.op(
                'act', lambda e: e.activation(out=Fsz[:, t0:t0 + tb], in_=pz[:, 0:tb], func=AF.Silu), reads=[pk], writes=['Fsz']))
            proj(128, 128, t0, tb, lambda pz, pk: P.op(
                'dve', lambda e: e.tensor_copy(out=bt['vT'][:, 0:tb], in_=pz[:, 0:tb]), reads=[pk], writes=['b_vT']))
            pz, pk = nps()
            for j in range(tb // 128):
                P.op('pe', lambda e: e.transpose(out=pz[:, j * 128:(j + 1) * 128], in_=bt['vT'][:, j * 128:(j + 1) * 128],
                                                 identity=ident[:]), reads=['b_vT', 'ident'], writes=[pk])
            P.op('dve', lambda e: e.tensor_copy(out=Fv[:, t0 // 128:(t0 + tb) // 128, :],
                                                in_=pz[:, 0:tb].rearrange("p (j c) -> p j c", c=128)),
                 reads=[pk], writes=['Fv'])
        wb0.use(('ffb', off, h), ld_ffb(h))
        if nxt is not None:
            wb0.prefetch(('qvz', off, nxt), ld_qvz(nxt))
        for d in range(2):
            rev = (d == 1)
            cur = 0
            if is_sample:
                P.dma(Sst[0][:], s_hgrn[d, h], writes=['Sst0'])
            else:
                P.op('pool', lambda e: e.memset(Sst[0][:], 0.0), writes=['Sst0'])
            blks = list(range(nblk))
            if rev:
                blks = blks[::-1]
            for b in blks:
                t0 = b * tb
                nch = tb // 32
                sg, lf, kg, G, br, E, Ei, qt, kt, kh = [bt[n_] for n_ in ['sg', 'lf', 'kg', 'G', 'br', 'E', 'Ei', 'qt', 'kt', 'kh']]
                proj(128 * d, 128, t0, tb, lambda pz, pk: P.op(
                    'act', lambda e: e.activation(out=sg[:, 0:tb], in_=pz[:, 0:tb], func=AF.Sigmoid), reads=[pk], writes=['b_sg']))
                P.op('dve', lambda e: e.tensor_scalar(out=sg[:, 0:tb], in0=sg[:, 0:tb], scalar1=oml[:, h:h + 1],
                                                      scalar2=lbv[:, h:h + 1], op0=ALU.mult, op1=ALU.add),
                     reads=['b_sg', 'oml', 'lbv'], writes=['b_sg'])
                P.op('act', lambda e: e.activation(out=lf[:, 0:tb], in_=sg[:, 0:tb], func=AF.Ln), reads=['b_sg'], writes=['b_lf'])
                P.op('dve', lambda e: e.tensor_scalar(out=kg[:, 0:tb], in0=sg[:, 0:tb], scalar1=-1.0, scalar2=1.0,
                                                       op0=ALU.mult, op1=ALU.add), reads=['b_sg'], writes=['b_kg'])
                P.op('dve', lambda e: e.memset(E[:, 0:tb], 0.0), writes=['b_E'])
                if not rev:
                    P.op('dve', lambda e: e.tensor_tensor_scan(out=G[:, 0:tb], data0=lf[:, 0:tb], data1=E[:, 0:tb],
                                                               initial=0.0, op0=ALU.add, op1=ALU.add),
                         reads=['b_lf', 'b_E'], writes=['b_G'])
                    ci_ = 0
                else:
                    P.op('dve', lambda e: e.tensor_tensor_scan(out=G[:, 0:tb][:, ::-1], data0=lf[:, 0:tb][:, ::-1],
                                                               data1=E[:, 0:tb], initial=0.0, op0=ALU.add, op1=ALU.add),
                         reads=['b_lf', 'b_E'], writes=['b_G'])
                    ci_ = 31
                G3 = G[:, 0:tb].rearrange("p (c l) -> p c l", l=32)
                lf3 = lf[:, 0:tb].rearrange("p (c l) -> p c l", l=32)
                P.op('dve', lambda e: e.tensor_tensor(out=gref[:, 0:nch], in0=G3[:, :, ci_], in1=lf3[:, :, ci_], op=ALU.subtract),
                     reads=['b_G', 'b_lf'], writes=['gref'])
                P.op('dve', lambda e: e.tensor_tensor(out=br[:, 0:tb].rearrange("p (c l) -> p c l", l=32), in0=G3,
                                                      in1=gref[:, 0:nch].unsqueeze(2).to_broadcast([128, nch, 32]), op=ALU.subtract),
                     reads=['b_G', 'gref'], writes=['b_br'])
                bend = br[:, 0:tb].rearrange("p (c l) -> p c l", l=32)[:, :, (0 if rev else 31)]
                P.op('act', lambda e: e.activation(out=gam[:, 0:nch], in_=bend, func=AF.Exp), reads=['b_br'], writes=['gam'])
                P.op('act', lambda e: e.activation(out=E[:, 0:tb], in_=br[:, 0:tb], func=AF.Exp), reads=['b_br'], writes=['b_E'])
                P.op('act', lambda e: e.activation(out=Ei[:, 0:tb], in_=br[:, 0:tb], func=AF.Exp, scale=-1.0),
                     reads=['b_br'], writes=['b_Ei'])
                P.op('dve', lambda e: e.tensor_tensor(out=qt[:, 0:tb], in0=Fq[:, t0:t0 + tb], in1=E[:, 0:tb], op=ALU.mult),
                     reads=['Fq', 'b_E'], writes=['b_qt'])
                P.op('dve', lambda e: e.tensor_tensor(out=kt[:, 0:tb], in0=kg[:, 0:tb], in1=Ei[:, 0:tb], op=ALU.mult),
                     reads=['b_kg', 'b_Ei'], writes=['b_kt'])
                P.op('dve', lambda e: e.tensor_tensor(out=kh[:, 0:tb].rearrange("p (c l) -> p c l", l=32),
                                                      in0=kt[:, 0:tb].rearrange("p (c l) -> p c l", l=32),
                                                      in1=gam[:, 0:nch].unsqueeze(2).to_broadcast([128, nch, 32]), op=ALU.mult),
                     reads=['b_kt', 'gam'], writes=['b_kh'])
                if debug and debug.get('inner') and h == head_ids[0]:
                    for n_ in ['lf', 'kg', 'br', 'E', 'qt', 'kt', 'kh']:
                        dump("%s_d%d" % (n_, d), bt[n_][:, 0:tb], ['b_' + n_], tb, col0=off + t0)
                pz, pk = nps()
                for j in range(tb // 128):
                    P.op('pe', lambda e: e.transpose(out=pz[:, j * 128:(j + 1) * 128], in_=kh[:, j * 128:(j + 1) * 128],
                                                     identity=ident[:]), reads=['b_kh', 'ident'], writes=[pk])
                P.op('act', lambda e: e.copy(out=khtok[:, 0:tb // 128, :], in_=pz[:, 0:tb].rearrange("p (j c) -> p j c", c=128)),
                     reads=[pk], writes=['khtok'])
                tiles = list(range(tb // 128))
                if rev:
                    tiles = tiles[::-1]
                for j in tiles:
                    tg = t0 // 128 + j
                    pa, pak = nps()
                    P.op('pe', lambda e: e.matmul(pa[:, 0:128], kt[:, j * 128:(j + 1) * 128], qt[:, j * 128:(j + 1) * 128],
                                                  start=True, stop=True), reads=['b_kt', 'b_qt'], writes=[pak])
                    P.op('dve', lambda e: e.tensor_tensor(out=attT[:], in0=pa[:, 0:128], in1=mH[:, d, :], op=ALU.mult),
                         reads=[pak, 'mH'], writes=['attT'])
                    po, pok = nps()
                    P.op('pe', lambda e: e.matmul(po[:, 0:128], attT[:], Fv[:, tg, :], start=True, stop=False),
                         reads=['attT', 'Fv'], writes=[pok])
                    chs = [0, 1, 2, 3]
                    if rev:
                        chs = chs[::-1]
                    for ci, c in enumerate(chs):
                        Scur = Sst[cur]
                        Snew = Sst[1 - cur]
                        P.op('pe', lambda e: e.matmul(
                            po[32 * c:32 * c + 32, 0:128], qt[:, j * 128 + 32 * c: j * 128 + 32 * c + 32], Scur[:],
                            start=False, stop=(ci == 3), tile_position=(0, 32 * c)),
                            reads=['b_qt', 'Sst%d' % cur], writes=[pok])
                        pd, pdk = nps()
                        P.op('pe', lambda e: e.matmul(
                            pd[:, 0:128], khtok[32 * c:32 * c + 32, j, :], Fv[32 * c:32 * c + 32, tg, :],
                            start=True, stop=True, tile_position=(32 * c, 0)),
                            reads=['khtok', 'Fv'], writes=[pdk])
                        gidx = j * 4 + c
                        P.op('dve', lambda e: e.scalar_tensor_tensor(
                            out=Snew[:], in0=Scur[:], scalar=gam[:, gidx:gidx + 1], in1=pd[:, 0:128],
                            op0=ALU.mult, op1=ALU.add),
                            reads=['Sst%d' % cur, 'gam', pdk], writes=['Sst%d' % (1 - cur)])
                        cur = 1 - cur
                    if d == 0:
                        P.op('act', lambda e: e.copy(out=Fo[:, tg, :], in_=po[:, 0:128]), reads=[pok], writes=[('Fo', tg)])
                    else:
                        P.op('dve', lambda e: e.tensor_tensor(out=Fo[:, tg, :], in0=Fo[:, tg, :], in1=po[:, 0:128], op=ALU.add),
                             reads=[pok, ('Fo', tg)], writes=[('Fo', tg)])
            if not is_sample:
                P.dma(o_hgrn[pidx, d, h], Sst[cur][:], reads=['Sst%d' % cur], q='pool')
        for tg in range(T // 128):
            P.op('act', lambda e: e.activation(out=attT[:], in_=Fo[:, tg, :], func=AF.Square, accum_out=ostat[:, tg, 0:1]),
                 reads=[('Fo', tg)], writes=['attT', ('ostat', tg)])
            P.op('dve', lambda e: e.tensor_scalar(out=ostat[:, tg, 1:2], in0=ostat[:, tg, 0:1], scalar1=1.0 / 128,
                                                  scalar2=1e-6, op0=ALU.mult, op1=ALU.add),
                 reads=[('ostat', tg)], writes=[('ostat', tg)])
            P.op('act', lambda e: e.activation(out=ostat[:, tg, 2:3], in_=ostat[:, tg, 1:2], func=AF.Sqrt),
                 reads=[('ostat', tg)], writes=[('ostat', tg)])
            P.op('dve', lambda e: e.reciprocal(out=ostat[:, tg, 3:4], in_=ostat[:, tg, 2:3]),
                 reads=[('ostat', tg)], writes=[('ostat', tg)])
            P.op('dve', lambda e: e.tensor_scalar(out=Fo[:, tg, :], in0=Fo[:, tg, :], scalar1=ostat[:, tg, 3:4],
                                                  scalar2=None, op0=ALU.mult),
                 reads=[('Fo', tg), ('ostat', tg)], writes=[('Fo', tg)])
        n4 = min(4, T // 128)
        for g4 in range(T // (128 * n4)):
            pz, pk = nps()
            for j in range(n4):
                tg = g4 * n4 + j
                P.op('pe', lambda e: e.transpose(out=pz[:, j * 128:(j + 1) * 128], in_=Fo[:, tg, :], identity=ident[:]),
                     reads=[('Fo', tg), 'ident'], writes=[pk])
            w_ = n4 * 128
            P.op('dve', lambda e: e.scalar_tensor_tensor(
                out=yT[:, h, g4 * w_:(g4 + 1) * w_], in0=pz[:, 0:w_], scalar=hgg_sb[:, h:h + 1],
                in1=Fsz[:, g4 * w_:(g4 + 1) * w_], op0=ALU.mult, op1=ALU.mult),
                reads=[pk, 'hgg', 'Fsz'], writes=[('yT', h)])

    TR = 256
    LWC = -0.6065306597126334
    LR = [sb0("LR%d" % g, [64, TS], BF16) for g in range(4)]
    rb = {n_: BT[i][0:64, :] for i, n_ in enumerate(
          ['lw', 'a', 'kk', 'kq', 'kap', 'kd', 'b', 'rk', 'G', 'br', 'E', 'Ei', 'Em', 'bh', 'kh', 'Kb', 'Bb', 't1'])}
    KR = sb0("r_KR", [64, 2, TR])
    cset = [{n_: sb0("c%d_%s" % (i_, n_), [64, (128 if n_ in ('AB', 'BB') else 64)],
                     (BF16 if n_ in ('XTa', 'XTb', 'Xa', 'Xb', 'Pm0', 'Pm1') else F32))
             for n_ in ['AB', 'BB', 'XT0', 'XTa', 'XTb', 'Xa', 'Xb', 'Pm0', 'Pm1', 'Vt', 'Kt', 'Bt']} for i_ in range(4)]
    rsq = {n_: sb0("rq_" + n_, [64, 64]) for n_ in ['U', 'Z0', 'Z1', 'zt']}
    rsq['Wb'] = sb0("rq_Wb", [64, 64], BF16)
    rgam = sb0("rgam", [64, 8])
    rgref = sb0("rgref", [64, 4])
    mR = sb0("mR", [64, 2, 3, 128])
    P.dma(mR[:], maskR.rearrange("d m s t -> s d m t"), writes=['mR'])
    prm = {}
    for n_, src_, shp in [('mu_rkv', mu_rkv, [64, 2, 4, 16]), ('mu_lr', mu_lr, [64, 2, 4]), ('w0', w0T, [64, 2, 16]),
                          ('a0', a0T, [64, 2, 16]), ('kk', kkT, [64, 16]), ('ka', kaT, [64, 16]), ('rk', rkT, [64, 16]),
                          ('gng', gngT, [64, 16]), ('gnb', gnbT, [64, 16])]:
        prm[n_] = sb0("p_" + n_, shp)
        P.dma(prm[n_][:], src_[:], writes=['p_' + n_])
    c0_rkv = sb0("c0_rkv", [64, 4, 16])
    c0_lr = sb0("c0_lr", [64, 4])
    omka = sb0("omka", [64, 16])
    P.op('dve', lambda e: e.tensor_tensor(out=c0_rkv[:], in0=prm['mu_rkv'][:, 0], in1=prm['mu_rkv'][:, 1], op=ALU.add),
         reads=['p_mu_rkv'], writes=['c0_rkv'])
    P.op('dve', lambda e: e.tensor_scalar(out=c0_rkv[:], in0=c0_rkv[:], scalar1=-1.0, scalar2=1.0, op0=ALU.mult, op1=ALU.add),
         reads=['c0_rkv'], writes=['c0_rkv'])
    P.op('dve', lambda e: e.tensor_tensor(out=c0_lr[:], in0=prm['mu_lr'][:, 0], in1=prm['mu_lr'][:, 1], op=ALU.add),
         reads=['p_mu_lr'], writes=['c0_lr'])
    P.op('dve', lambda e: e.tensor_scalar(out=c0_lr[:], in0=c0_lr[:], scalar1=-1.0, scalar2=1.0, op0=ALU.mult, op1=ALU.add),
         reads=['c0_lr'], writes=['c0_lr'])
    P.op('dve', lambda e: e.tensor_scalar(out=omka[:], in0=prm['ka'][:], scalar1=-1.0, scalar2=1.0, op0=ALU.mult, op1=ALU.add),
         reads=['p_ka'], writes=['omka'])
    w2a2 = sb0("w2a2", [64, 4, D], BF16)
    for g, src_ in enumerate([w2[0], w2[1], a2[0], a2[1]]):
        for c0 in range(0, D, 256):
            P.dma(wst[0:64, 0, 0:256], src_[:, c0:c0 + 256], writes=['wst'])
            P.op('pool', lambda e: e.tensor_copy(out=w2a2[:, g, c0:c0 + 256], in_=wst[0:64, 0, 0:256]), reads=['wst'], writes=['w2a2'])

    def shift_into(dst, dkey, raw, rkey, T, c0ap, m0ap, m1ap, t1tile, eng='dve'):
        for s0 in range(0, T, 512):
            n = min(512, T - s0)
            P.op('dve', lambda e: e.tensor_scalar(out=t1tile[:, 0:n], in0=raw[:, 16 + s0:16 + s0 + n], scalar1=c0ap, scalar2=None,
                                                op0=ALU.mult), reads=[rkey], writes=['shift_t'])
            P.op('dve', lambda e: e.scalar_tensor_tensor(out=t1tile[:, 0:n], in0=raw[:, 15 + s0:15 + s0 + n], scalar=m0ap,
                                                       in1=t1tile[:, 0:n], op0=ALU.mult, op1=ALU.add),
                 reads=[rkey, 'shift_t'], writes=['shift_t'])
            P.op('dve', lambda e: e.scalar_tensor_tensor(out=dst[:, s0:s0 + n], in0=raw[:, 17 + s0:17 + s0 + n], scalar=m1ap,
                                                       in1=t1tile[:, 0:n], op0=ALU.mult, op1=ALU.add),
                 reads=[rkey, 'shift_t'], writes=[dkey])

    shiftt = sb0("shiftt", [64, 512])

    def rwkv_seq_setup(off, T):
        wb0.use(('lr', off), lambda dst, dk: load_w(wLR, 256, dst=dst, dkey=dk))
        pb = min(512, T)
        for g in range(4):
            raw = FT[g]
            P.op('pool', lambda e: e.memset(raw[0:64, 15:16], 0.0), writes=['FT%d' % g])
            P.op('pool', lambda e: e.memset(raw[0:64, T + 16:T + 17], 0.0), writes=['FT%d' % g])
            for b in range(T // pb):
                t0 = b * pb
                proj(64 * g, 64, t0, pb, lambda pz, pk: P.op(
                    'act', lambda e: e.copy(out=raw[0:64, 16 + t0:16 + t0 + pb], in_=pz[0:64, 0:pb]), reads=[pk], writes=['FT%d' % g]))
            shift_into(FT[4][0:64, :], 'FT4', raw[0:64, :], 'FT%d' % g, T, c0_lr[:, g:g + 1], prm['mu_lr'][:, 0, g:g + 1],
                       prm['mu_lr'][:, 1, g:g + 1], shiftt)
            if g < 2:
                P.op('act', lambda e: e.activation(out=LR[g][:, 0:T], in_=FT[4][0:64, 0:T], func=AF.Tanh), reads=['FT4'], writes=['LR%d' % g])
            else:
                P.op('act', lambda e: e.copy(out=LR[g][:, 0:T], in_=FT[4][0:64, 0:T]), reads=['FT4'], writes=['LR%d' % g])

    def rwkv_head(h, slot, off, T, is_sample, pidx, nxt=None):
        P.barrier()
        ld_b = lambda hh: (lambda dst, dk: load_w(wB[hh], 256, dst=dst, dkey=dk))
        wb0.use(('wB', off, h), ld_b(h))
        pb = min(512, T)
        nchT = T // 64
        for g in range(3):
            raw = FT[g]
            P.op('pool', lambda e: e.memset(raw[0:64, 15:16], 0.0), writes=['FT%d' % g])
            P.op('pool', lambda e: e.memset(raw[0:64, T + 16:T + 17], 0.0), writes=['FT%d' % g])
            for b in range(T // pb):
                t0 = b * pb
                proj(64 * g, 64, t0, pb, lambda pz, pk: P.op(
                    'act', lambda e: e.copy(out=raw[0:64, 16 + t0:16 + t0 + pb], in_=pz[0:64, 0:pb]), reads=[pk], writes=['FT%d' % g]))
            shift_into(FT[3 + g][0:64, :], 'FT%d' % (3 + g), raw[0:64, :], 'FT%d' % g, T, c0_rkv[:, g, h:h + 1],
                       prm['mu_rkv'][:, 0, g, h:h + 1], prm['mu_rkv'][:, 1, g, h:h + 1], shiftt, eng=('dve' if g != 1 else 'pool'))
        rS, kS, vS = FT[3][0:64, :], FT[4][0:64, :], FT[5][0:64, :]
        szb, yaccr, bonus = FT[0][0:64, :], FT[1][0:64, 0:T], FT[2][0:64, :]
        yacc = yaccr.rearrange("p (c v) -> p c v", v=64)
        for b in range(T // pb):
            t0 = b * pb
            proj(192, 64, t0, pb, lambda pz, pk: P.op(
                'act', lambda e: e.activation(out=szb[:, t0:t0 + pb], in_=pz[0:64, 0:pb], func=AF.Silu), reads=[pk], writes=['FT0']))
        tb = min(TR, T)
        nblk = T // tb
        if nxt is not None:
            wb0.prefetch(('wB', off, nxt), ld_b(nxt))
        P.barrier()
        for d in range(2):
            rev = (d == 1)
            cur = 0
            Zt = [rsq['Z0'], rsq['Z1']]
            if is_sample:
                P.dma(rsq['zt'][:], s_rwkv[d, h], writes=['rq_zt'])
                pz, pk = nps()
                P.op('pe', lambda e: e.transpose(out=pz[0:64, 0:64], in_=rsq['zt'][:], identity=ident[0:64, 0:64]),
                     reads=['rq_zt', 'ident'], writes=[pk])
                P.op('act', lambda e: e.copy(out=Zt[0][:], in_=pz[0:64, 0:64]), reads=[pk], writes=['rq_Z0'])
            else:
                P.op('pool', lambda e: e.memset(Zt[0][:], 0.0), writes=['rq_Z0'])
            blks = list(range(nblk))
            if rev:
                blks = blks[::-1]
            for b in blks:
                t0 = b * tb
                sl = slice(t0, t0 + tb)
                nch = tb // 64
                R_ = rb
                pz, pk = nps()
                P.op('pe', lambda e: e.matmul(pz[0:64, 0:tb], w2a2[:, d, h * 64:(h + 1) * 64], LR[d][:, sl], start=True, stop=True),
                     reads=['w2a2', 'LR%d' % d], writes=[pk])
                P.op('act', lambda e: e.activation(out=R_['lw'][:, 0:tb], in_=pz[0:64, 0:tb], func=AF.Sigmoid,
                                                   bias=prm['w0'][:, d, h:h + 1], scale=1.0), reads=[pk, 'p_w0'], writes=['r_lw'])
                pz, pk = nps()
                P.op('pe', lambda e: e.matmul(pz[0:64, 0:tb], w2a2[:, 2 + d, h * 64:(h + 1) * 64], LR[2 + d][:, sl], start=True, stop=True),
                     reads=['w2a2', 'LR%d' % (2 + d)], writes=[pk])
                P.op('act', lambda e: e.activation(out=R_['a'][:, 0:tb], in_=pz[0:64, 0:tb], func=AF.Sigmoid,
                                                   bias=prm['a0'][:, d, h:h + 1], scale=1.0), reads=[pk, 'p_a0'], writes=['r_a'])
                P.op('dve', lambda e: e.tensor_scalar(out=R_['kk'][:, 0:tb], in0=kS[:, sl], scalar1=prm['kk'][:, h:h + 1],
                                                      scalar2=None, op0=ALU.mult), reads=['FT4', 'p_kk'], writes=['r_kk'])
                P.op('dve', lambda e: e.tensor_tensor(out=R_['kq'][:, 0:tb], in0=R_['kk'][:, 0:tb], in1=R_['kk'][:, 0:tb], op=ALU.mult),
                     reads=['r_kk'], writes=['r_kq'])
                pz, pk = nps()
                P.op('pe', lambda e: e.matmul(pz[0:64, 0:tb], ones[0:64, 0:64], R_['kq'][:, 0:tb], start=True, stop=True),
                     reads=['ones', 'r_kq'], writes=[pk])
                P.op('dve', lambda e: e.tensor_scalar(out=R_['kq'][:, 0:tb], in0=pz[0:64, 0:tb], scalar1=1e-24, scalar2=None,
                                                      op0=ALU.max), reads=[pk], writes=['r_kq'])
                P.op('act', lambda e: e.activation(out=R_['kq'][:, 0:tb], in_=R_['kq'][:, 0:tb], func=AF.Ln), reads=['r_kq'], writes=['r_kq'])
                P.op('act', lambda e: e.activation(out=R_['kq'][:, 0:tb], in_=R_['kq'][:, 0:tb], func=AF.Exp, scale=-0.5),
                     reads=['r_kq'], writes=['r_kq'])
                P.op('dve', lambda e: e.tensor_tensor(out=R_['kap'][:, 0:tb], in0=R_['kk'][:, 0:tb], in1=R_['kq'][:, 0:tb], op=ALU.mult),
                     reads=['r_kk', 'r_kq'], writes=['r_kap'])
                P.op('dve', lambda e: e.tensor_scalar(out=R_['t1'][:, 0:tb], in0=R_['a'][:, 0:tb], scalar1=prm['ka'][:, h:h + 1],
                                                       scalar2=omka[:, h:h + 1], op0=ALU.mult, op1=ALU.add),
                     reads=['r_a', 'p_ka', 'omka'], writes=['r_t1'])
                P.op('dve', lambda e: e.tensor_tensor(out=R_['kd'][:, 0:tb], in0=kS[:, sl], in1=R_['t1'][:, 0:tb], op=ALU.mult),
                     reads=['FT4', 'r_t1'], writes=['r_kd'])
                P.op('dve', lambda e: e.tensor_tensor(out=R_['b'][:, 0:tb], in0=R_['a'][:, 0:tb], in1=R_['kap'][:, 0:tb], op=ALU.mult),
                     reads=['r_a', 'r_kap'], writes=['r_b'])
                P.op('dve', lambda e: e.scalar_tensor_tensor(out=R_['rk'][:, 0:tb], in0=rS[:, sl], scalar=prm['rk'][:, h:h + 1],
                                                             in1=R_['kd'][:, 0:tb], op0=ALU.mult, op1=ALU.mult),
                     reads=['FT3', 'p_rk', 'r_kd'], writes=['r_rk'])
                pz, pk = nps()
                P.op('pe', lambda e: e.matmul(pz[0:64, 0:tb], ones[0:64, 0:64], R_['rk'][:, 0:tb], start=True, stop=True),
                     reads=['ones', 'r_rk'], writes=[pk])
                if d == 0:
                    P.op('dve', lambda e: e.tensor_tensor(out=bonus[:, sl], in0=pz[0:64, 0:tb], in1=vS[:, sl], op=ALU.mult),
                         reads=[pk, 'FT5'], writes=[('bonus', b)])
                else:
                    P.op('dve', lambda e: e.tensor_tensor(out=R_['rk'][:, 0:tb], in0=pz[0:64, 0:tb], in1=vS[:, sl], op=ALU.mult),
                         reads=[pk, 'FT5'], writes=['r_rk'])
                    P.op('dve', lambda e: e.tensor_tensor(out=bonus[:, sl], in0=bonus[:, sl], in1=R_['rk'][:, 0:tb], op=ALU.add),
                         reads=['r_rk', ('bonus', b)], writes=[('bonus', b)])
                G, br, E, Ei, Em = R_['G'], R_['br'], R_['E'], R_['Ei'], R_['Em']
                P.op('dve', lambda e: e.memset(E[:, 0:tb], 0.0), writes=['r_E'])
                if not rev:
                    P.op('dve', lambda e: e.tensor_tensor_scan(out=G[:, 0:tb], data0=R_['lw'][:, 0:tb], data1=E[:, 0:tb],
                                                               initial=0.0, op0=ALU.add, op1=ALU.add),
                         reads=['r_lw', 'r_E'], writes=['r_G'])
                    ci_ = 0
                else:
                    P.op('dve', lambda e: e.tensor_tensor_scan(out=G[:, 0:tb][:, ::-1], data0=R_['lw'][:, 0:tb][:, ::-1],
                                                               data1=E[:, 0:tb], initial=0.0, op0=ALU.add, op1=ALU.add),
                         reads=['r_lw', 'r_E'], writes=['r_G'])
                    ci_ = 63
                G3 = G[:, 0:tb].rearrange("p (c l) -> p c l", l=64)
                lw3 = R_['lw'][:, 0:tb].rearrange("p (c l) -> p c l", l=64)
                P.op('dve', lambda e: e.tensor_tensor(out=rgref[:, 0:nch], in0=G3[:, :, ci_], in1=lw3[:, :, ci_], op=ALU.subtract),
                     reads=['r_G', 'r_lw'], writes=['rgref'])
                P.op('dve', lambda e: e.tensor_tensor(out=br[:, 0:tb].rearrange("p (c l) -> p c l", l=64), in0=G3,
                                                      in1=rgref[:, 0:nch].unsqueeze(2).to_broadcast([64, nch, 64]), op=ALU.subtract),
                     reads=['r_G', 'rgref'], writes=['r_br'])
                bend = br[:, 0:tb].rearrange("p (c l) -> p c l", l=64)[:, :, (0 if rev else 63)]
                P.op('act', lambda e: e.activation(out=rgam[:, 0:nch], in_=bend, func=AF.Exp, scale=LWC), reads=['r_br'], writes=['rgam'])
                P.op('dve', lambda e: e.tensor_scalar(out=rgam[:, 4:4 + nch], in0=rgam[:, 0:nch], scalar1=-1.0, scalar2=None,
                                                       op0=ALU.mult), reads=['rgam'], writes=['rgam'])
                P.op('act', lambda e: e.activation(out=E[:, 0:tb], in_=br[:, 0:tb], func=AF.Exp, scale=LWC), reads=['r_br'], writes=['r_E'])
                P.op('act', lambda e: e.activation(out=Ei[:, 0:tb], in_=br[:, 0:tb], func=AF.Exp, scale=-LWC),
                     reads=['r_br'], writes=['r_Ei'])
                P.op('dve', lambda e: e.tensor_tensor(out=R_['t1'][:, 0:tb], in0=br[:, 0:tb], in1=R_['lw'][:, 0:tb], op=ALU.subtract),
                     reads=['r_br', 'r_lw'], writes=['r_t1'])
                P.op('act', lambda e: e.activation(out=Em[:, 0:tb], in_=R_['t1'][:, 0:tb], func=AF.Exp, scale=LWC), reads=['r_t1'], writes=['r_Em'])
                P.op('dve', lambda e: e.tensor_tensor(out=KR[:, 0, 0:tb], in0=R_['kap'][:, 0:tb], in1=Em[:, 0:tb], op=ALU.mult),
                     reads=['r_kap', 'r_Em'], writes=['r_KR'])
                P.op('dve', lambda e: e.tensor_tensor(out=KR[:, 1, 0:tb], in0=rS[:, sl], in1=E[:, 0:tb], op=ALU.mult),
                     reads=['FT3', 'r_E', 'r_KR'], writes=['r_KR'])
                P.op('dve', lambda e: e.tensor_tensor(out=R_['bh'][:, 0:tb], in0=R_['b'][:, 0:tb], in1=Ei[:, 0:tb], op=ALU.mult),
                     reads=['r_b', 'r_Ei'], writes=['r_bh'])
                P.op('dve', lambda e: e.tensor_tensor(out=R_['kh'][:, 0:tb], in0=R_['kd'][:, 0:tb], in1=Ei[:, 0:tb], op=ALU.mult),
                     reads=['r_kd', 'r_Ei'], writes=['r_kh'])
                P.op('dve', lambda e: e.tensor_tensor(out=R_['Kb'][:, 0:tb].rearrange("p (c l) -> p c l", l=64),
                                                      in0=R_['kh'][:, 0:tb].rearrange("p (c l) -> p c l", l=64),
                                                      in1=rgam[:, 0:nch].unsqueeze(2).to_broadcast([64, nch, 64]), op=ALU.mult),
                     reads=['r_kh', 'rgam'], writes=['r_Kb'])
                P.op('dve', lambda e: e.tensor_tensor(out=R_['Bb'][:, 0:tb].rearrange("p (c l) -> p c l", l=64),
                                                      in0=R_['bh'][:, 0:tb].rearrange("p (c l) -> p c l", l=64),
                                                      in1=rgam[:, 4:4 + nch].unsqueeze(2).to_broadcast([64, nch, 64]), op=ALU.mult),
                     reads=['r_bh', 'rgam'], writes=['r_Bb'])
                chs = list(range(nch))
                if rev:
                    chs = chs[::-1]
                st = {}
                for c in chs:
                    cs = slice(c * 64, (c + 1) * 64)
                    C_ = cset[c]
                    ck = (lambda n_, c=c: 'c%d_%s' % (c, n_))
                    pA, pAk = nps()
                    P.op('pe', lambda e: e.matmul(pA[0:64, 0:128], R_['bh'][:, cs], KR[:, :, cs], start=True, stop=True),
                         reads=['r_bh', 'r_KR'], writes=[pAk])
                    P.op('pe', lambda e: e.matmul(pA[0:64, 128:256], R_['kh'][:, cs], KR[:, :, cs], start=True, stop=True),
                         reads=['r_kh', 'r_KR'], writes=[pAk])
                    P.op('pe', lambda e: e.matmul(pA[0:64, 256:320], KR[:, 0, cs], R_['bh'][:, cs], start=True, stop=True),
                         reads=['r_bh', 'r_KR'], writes=[pAk])
                    AB, BB = C_['AB'], C_['BB']
                    P.op('dve', lambda e: e.tensor_tensor(out=AB[:], in0=pA[0:64, 0:128], in1=mR[:, d, 0, :], op=ALU.mult),
                         reads=[pAk, 'mR'], writes=[ck('AB')])
                    P.op('dve', lambda e: e.tensor_tensor(out=BB[:], in0=pA[0:64, 128:256], in1=mR[:, d, 1, :], op=ALU.mult),
                         reads=[pAk, 'mR'], writes=[ck('BB')])
                    P.op('dve', lambda e: e.tensor_tensor(out=C_['XT0'][:], in0=pA[0:64, 256:320], in1=mR[:, d, 2, 0:64], op=ALU.mult),
                         reads=[pAk, 'mR'], writes=[ck('XT0')])
                    P.op('dve', lambda e: e.tensor_tensor(out=C_['Pm0'][:], in0=AB[:, 0:64], in1=ident[0:64, 0:64], op=ALU.add),
                         reads=[ck('AB'), 'ident'], writes=[ck('Pm0')])
                    st[c] = dict(X=AB[:, 0:64], Xk=ck('AB'), XT=C_['XT0'], XTk=ck('XT0'), xti=0, pmi=0)
                for lev in range(5):
                    for c in chs:
                        C_ = cset[c]
                        s_ = st[c]
                        ck = (lambda n_, c=c: 'c%d_%s' % (c, n_))
                        X, Xk, XT, XTk = s_['X'], s_['Xk'], s_['XT'], s_['XTk']
                        pq, pqk = nps()
                        nXTn = 'XTa' if lev % 2 == 0 else 'XTb'
                        nXT, nXTk = C_[nXTn], ck(nXTn)
                        P.op('pe', lambda e: e.matmul(pq[0:64, 64:128], X, XT[:], start=True, stop=True), reads=[Xk, XTk], writes=[pqk])
                        if lev < 4:
                            P.op('pe', lambda e: e.matmul(pq[0:64, 0:64], XT[:], X, start=True, stop=True), reads=[Xk, XTk], writes=[pqk])
                        P.op('act', lambda e: e.copy(out=nXT[:], in_=pq[0:64, 64:128]), reads=[pqk], writes=[nXTk])
                        if lev < 4:
                            tn = 'Xa' if lev % 2 == 0 else 'Xb'
                            P.op('act', lambda e: e.copy(out=C_[tn][:], in_=pq[0:64, 0:64]), reads=[pqk], writes=[ck(tn)])
                            s_['X'], s_['Xk'] = C_[tn][:], ck(tn)
                        s_['XT'], s_['XTk'], s_['xti'] = nXT, nXTk, 1 - s_['xti']
                    for c in chs:
                        C_ = cset[c]
                        s_ = st[c]
                        ck = (lambda n_, c=c: 'c%d_%s' % (c, n_))
                        nXT, nXTk = s_['XT'], s_['XTk']
                        pmi = s_['pmi']
                        Pc, Pn = C_['Pm%d' % pmi], C_['Pm%d' % (1 - pmi)]
                        pp, ppk = nps()
                        P.op('pe', lambda e: e.matmul(pp[0:64, 0:64], nXT[:], Pc[:], start=True, stop=True),
                             reads=[nXTk, ck('Pm%d' % pmi)], writes=[ppk])
                        P.op('dve', lambda e: e.tensor_tensor(out=Pn[:], in0=pp[0:64, 0:64], in1=Pc[:], op=ALU.add),
                             reads=[ppk, ck('Pm%d' % pmi)], writes=[ck('Pm%d' % (1 - pmi))])
                        s_['pmi'] = 1 - pmi
                for c in chs:
                    cs = slice(c * 64, (c + 1) * 64)
                    gsl = slice(t0 + c * 64, t0 + (c + 1) * 64)
                    C_ = cset[c]
                    pt, ptk = nps()
                    P.op('pe', lambda e: e.transpose(out=pt[0:64, 0:64], in_=vS[:, gsl], identity=ident[0:64, 0:64]),
                         reads=['FT5', 'ident'], writes=[ptk])
                    P.op('pe', lambda e: e.transpose(out=pt[0:64, 64:128], in_=R_['Kb'][:, cs], identity=ident[0:64, 0:64]),
                         reads=['r_Kb', 'ident'], writes=[ptk])
                    P.op('pe', lambda e: e.transpose(out=pt[0:64, 128:192], in_=R_['Bb'][:, cs], identity=ident[0:64, 0:64]),
                         reads=['r_Bb', 'ident'], writes=[ptk])
                    P.op('act', lambda e: e.copy(out=C_['Vt'][:], in_=pt[0:64, 0:64]), reads=[ptk], writes=['c%d_Vt' % c])
                    P.op('act', lambda e: e.copy(out=C_['Kt'][:], in_=pt[0:64, 64:128]), reads=[ptk], writes=['c%d_Kt' % c])
                    P.op('act', lambda e: e.copy(out=C_['Bt'][:], in_=pt[0:64, 128:192]), reads=[ptk], writes=['c%d_Bt' % c])
                for c in chs:
                    cs = slice(c * 64, (c + 1) * 64)
                    cg = (t0 // 64) + c
                    C_ = cset[c]
                    s_ = st[c]
                    AB, BB = C_['AB'], C_['BB']
                    ABk, BBk, Vtk, Ktk, Btk = ['c%d_%s' % (c, n_) for n_ in ('AB', 'BB', 'Vt', 'Kt', 'Bt')]
                    Pm, Pmk = C_['Pm%d' % s_['pmi']], 'c%d_Pm%d' % (c, s_['pmi'])
                    Zc, Zn = Zt[cur], Zt[1 - cur]
                    zck, znk = 'rq_Z%d' % cur, 'rq_Z%d' % (1 - cur)
                    pw, pwk = nps()
                    P.op('pe', lambda e: e.matmul(pw[0:64, 0:64], KR[:, 0, cs], Zc[:], start=True, stop=False),
                         reads=['r_KR', zck], writes=[pwk])
                    P.op('pe', lambda e: e.matmul(pw[0:64, 0:64], BB[:, 0:64], C_['Vt'][:], start=False, stop=True),
                         reads=[BBk, Vtk], writes=[pwk])
                    P.op('act', lambda e: e.copy(out=rsq['Wb'][:], in_=pw[0:64, 0:64]), reads=[pwk], writes=['rq_Wb'])
                    pu, puk = nps()
                    P.op('pe', lambda e: e.matmul(pu[0:64, 0:64], Pm[:], rsq['Wb'][:], start=True, stop=True),
                         reads=[Pmk, 'rq_Wb'], writes=[puk])
                    P.op('act', lambda e: e.copy(out=rsq['U'][:], in_=pu[0:64, 0:64]), reads=[puk], writes=['rq_U'])
                    pzz, pzk = nps()
                    P.op('pe', lambda e: e.matmul(pzz[0:64, 0:64], C_['Kt'][:], C_['Vt'][:], start=True, stop=False),
                         reads=[Ktk, Vtk], writes=[pzk])
                    P.op('pe', lambda e: e.matmul(pzz[0:64, 0:64], C_['Bt'][:], rsq['U'][:], start=False, stop=True),
                         reads=[Btk, 'rq_U'], writes=[pzk])
                    P.op('dve', lambda e: e.scalar_tensor_tensor(out=Zn[:], in0=Zc[:], scalar=rgam[:, c:c + 1], in1=pzz[0:64, 0:64],
                                                                 op0=ALU.mult, op1=ALU.add), reads=[zck, 'rgam', pzk], writes=[znk])
                    py, pyk = nps()
                    P.op('pe', lambda e: e.matmul(py[0:64, 0:64], KR[:, 1, cs], Zc[:], start=True, stop=False),
                         reads=['r_KR', zck], writes=[pyk])
                    P.op('pe', lambda e: e.matmul(py[0:64, 0:64], BB[:, 64:128], C_['Vt'][:], start=False, stop=False),
                         reads=[BBk, Vtk], writes=[pyk])
                    P.op('pe', lambda e: e.matmul(py[0:64, 0:64], AB[:, 64:128], rsq['U'][:], start=False, stop=True),
                         reads=[ABk, 'rq_U'], writes=[pyk])
                    if d == 0:
                        P.op('act', lambda e: e.copy(out=yacc[:, cg, :], in_=py[0:64, 0:64]), reads=[pyk], writes=[('yacc', cg)])
                    else:
                        P.op('dve', lambda e: e.tensor_tensor(out=yacc[:, cg, :], in0=yacc[:, cg, :], in1=py[0:64, 0:64], op=ALU.add),
                             reads=[pyk, ('yacc', cg)], writes=[('yacc', cg)])
                    cur = 1 - cur
            if not is_sample:
                pz, pk = nps()
                P.op('pe', lambda e: e.transpose(out=pz[0:64, 0:64], in_=Zt[cur][:], identity=ident[0:64, 0:64]),
                     reads=['rq_Z%d' % cur, 'ident'], writes=[pk])
                P.op('act', lambda e: e.copy(out=rsq['zt'][:], in_=pz[0:64, 0:64]), reads=[pk], writes=['rq_zt'])
                P.dma(o_rwkv[pidx, d, h], rsq['zt'][:], reads=['rq_zt'], q='pool')
        ykeys = [('yacc', c) for c in range(nchT)]
        gst = ostat[0:64, :, :].rearrange("p a b -> p (a b)")
        P.op('dve', lambda e: e.tensor_reduce(out=gst[:, 0:nchT], in_=yacc, axis=AX.X, op=ALU.add), reads=ykeys, writes=['gst'])
        P.op('dve', lambda e: e.tensor_scalar(out=gst[:, 0:nchT], in0=gst[:, 0:nchT], scalar1=-1.0 / 64, scalar2=None, op0=ALU.mult),
             reads=['gst'], writes=['gst'])
        P.op('dve', lambda e: e.tensor_tensor(out=yacc, in0=yacc, in1=gst[:, 0:nchT].unsqueeze(2).to_broadcast([64, nchT, 64]), op=ALU.add),
             reads=ykeys + ['gst'], writes=ykeys)
        sq = FT[3][0:64, 0:T].rearrange("p (c v) -> p c v", v=64)
        P.op('dve', lambda e: e.tensor_tensor(out=sq, in0=yacc, in1=yacc, op=ALU.mult), reads=ykeys, writes=['FT3'])
        P.op('dve', lambda e: e.tensor_reduce(out=gst[:, 32:32 + nchT], in_=sq, axis=AX.X, op=ALU.add), reads=['FT3'], writes=['gst'])
        P.op('dve', lambda e: e.tensor_scalar(out=gst[:, 32:32 + nchT], in0=gst[:, 32:32 + nchT], scalar1=1.0 / 64, scalar2=64e-5,
                                              op0=ALU.mult, op1=ALU.add), reads=['gst'], writes=['gst'])
        P.op('act', lambda e: e.activation(out=gst[:, 32:32 + nchT], in_=gst[:, 32:32 + nchT], func=AF.Sqrt), reads=['gst'], writes=['gst'])
        P.op('dve', lambda e: e.reciprocal(out=gst[:, 32:32 + nchT], in_=gst[:, 32:32 + nchT]), reads=['gst'], writes=['gst'])
        P.op('dve', lambda e: e.tensor_tensor(out=yacc, in0=yacc, in1=gst[:, 32:32 + nchT].unsqueeze(2).to_broadcast([64, nchT, 64]),
                                              op=ALU.mult), reads=ykeys + ['gst'], writes=ykeys)
        n8 = min(8, nchT)
        for g8 in range(nchT // n8):
            pz, pk = nps()
            for j in range(n8):
                cg = g8 * n8 + j
                P.op('pe', lambda e: e.transpose(out=pz[0:64, j * 64:(j + 1) * 64], in_=yacc[:, cg, :], identity=ident[0:64, 0:64]),
                     reads=[('yacc', cg), 'ident'], writes=[pk])
            w_ = n8 * 64
            gs = slice(g8 * w_, (g8 + 1) * w_)
            P.op('dve', lambda e: e.tensor_scalar(out=shiftt[:, 0:w_], in0=pz[0:64, 0:w_], scalar1=prm['gng'][:, h:h + 1],
                                                  scalar2=prm['gnb'][:, h:h + 1], op0=ALU.mult, op1=ALU.add),
                 reads=[pk, 'p_gng', 'p_gnb'], writes=['shift_t'])
            P.op('dve', lambda e: e.tensor_tensor(out=shiftt[:, 0:w_], in0=shiftt[:, 0:w_], in1=bonus[:, gs], op=ALU.add),
                 reads=['shift_t'] + [('bonus', b) for b in range(nblk)], writes=['shift_t'])
            P.op('dve', lambda e: e.tensor_tensor(out=yT[0:64, slot, gs], in0=shiftt[:, 0:w_], in1=szb[:, gs], op=ALU.mult),
                 reads=['shift_t', 'FT0'], writes=[('yT', slot)])

    seq_ids = debug.get('seqs', [0, 1, 2]) if debug else [0, 1, 2]
    head_ids = debug.get('heads', list(range(8))) if debug else list(range(8))
    rheads = debug.get('rheads', list(range(16))) if debug else list(range(16))
    for si in seq_ids:
        off, T, cidx, is_sample = SEQS[si]
        P.barrier()
        make_gate(0, cidx)
        make_hT(0, xin, 'xin', off, T, cidx)
        P.barrier()
        if debug and debug.get('inner'):
            dump("mT", mT[:, 0].rearrange("p a b -> p (a b)"), ['mT'], 48)
            dump("sc1", sc1[:, 0].rearrange("p a b -> p (a b)"), ['sc1'], 16)
            dump("scT", scT[:].rearrange("p a b -> p (a b)"), ['scT'], 16)
            dump("xn", xn, ['xn'], 1024)
            dump("xt0", xt[0], ['xt0'], 1024)
            for kc in range(8):
                dump("hT%d" % kc, hT[:, kc, 0:T], hT_keys(0, T), T, col0=off)
        for hi_, h in enumerate(head_ids):
            hgrn_head(h, off, T, is_sample, si - 1, nxt=(head_ids[hi_ + 1] if hi_ + 1 < len(head_ids) else None))
            if debug and debug.get('dump_y'):
                dump("yT%d" % h, yT[:, h, 0:T], [('yT', h)], T, col0=off)
        P.barrier()
        load_wo(w_out_even[0:D, :], 128)
        outproj(0, [(128, s_) for s_ in range(8)], xin, 'xin', x1, 'x1', off, T, cidx)
        P.barrier()
        rwkv_seq_setup(off, T)
        for half in range(2):
            P.barrier()
            for slot in range(8):
                h = half * 8 + slot
                if h in rheads:
                    nh_ = h + 1 if (slot < 7 and (h + 1) in rheads) else None
                    rwkv_head(h, slot, off, T, is_sample, si - 1, nxt=nh_)
                    if debug and debug.get('dump_y'):
                        dump("yR%d" % h, yT[0:64, slot, 0:T], [('yT', slot)], T, col0=off, parts=64)
            P.barrier()
            load_wo(w_out_even[D + half * 512: D + (half + 1) * 512, :], 64)
            outproj(0, [(64, s_) for s_ in range(8)], x1, 'x1', x1, 'x1', off, T, cidx)
    P.barrier()
    L0.close()

    if not (debug and debug.get('l0only')):
        L1 = contextlib.ExitStack()

        def sb1(name, shape, dt=F32):
            return L1.enter_context(nc.sbuf_tensor(name, list(shape), dt))

        make_fng()
        LC = 128
        DH = 512
        qT = yT[:, 4:8, :]
        kT = sb1("kT", [128, 4, TS], BF16)
        vch = sb1("vch", [128, DH], BF16)
        Cst = sb1("Cst", [128, 4, DH])
        Cbf = sb1("Cbf", [128, 4, DH], BF16)
        nst = sb1("nst", [128, 8])
        nbf = sb1("nbf", [128, 4], BF16)
        ktok = sb1("ktok", [128, DH], BF16)
        vw = sb1("vw", [128, DH], BF16)
        sTs = sb1("sTs", [128, 128], BF16)
        onesb = sb1("onesb", [128, 1], BF16)
        identb = sb1("identb", [128, 128], BF16)
        mC = sb1("mC", [128, 2, 128])
        SEL = sb1("SEL", [36, 4, 128])
        XA = sb1("XA", [36, TS])
        XB = sb1("XB", [36, TS])
        zrow = sb1("zrow", [36, 512])
        sm = {n_: sb1("sm_" + n_, [36, 16]) for n_ in ['ac', 'bl', 'M', 'MP', 'mu', 'al', 'gref', 'm0']}
        Wtok = sb1("Wtok", [128, 2, 16, 8])
        Wtokb = sb1("Wtokb", [128, 2, 16, 4], BF16)
        ALb = sb1("ALb", [128, 2, 4, 16])
        dstat = sb1("dstat", [128, 8])
        wGb = sb1("wGb", [128, 8, 16], BF16)
        gbT = sb1("gbT", [36, 4])
        ngbT = sb1("ngbT", [36, 4])
        cw = sb1("cw", [128, 32, 9])
        cb = sb1("cb", [128, 32])
        mng = sb1("mng", [128, 16])
        wb1 = WB([sb1("wbf1A", [128, 8, DH], BF16), sb1("wbf1B", [128, 8, DH], BF16)], ['wbf1A', 'wbf1B'])
        P.dma(mC[:], maskC.rearrange("d s t -> s d t"), writes=['mC'])
        P.dma(SEL[:], sel_d[:], writes=['SEL'])
        P.dma(gbT[:], gbT_d[:], writes=['gbT'])
        P.dma(cw[:], cw_d[:], writes=['cw'])
        P.dma(cb[:], cb_d[:], writes=['cb'])
        P.dma(mng[:], mng_d[:], writes=['mng'])
        P.op('dve', lambda e: e.memset(onesb[:], 1.0), writes=['onesb'])
        P.op('dve', lambda e: e.memset(zrow[:], 0.0), writes=['zrow'])
        P.op('dve', lambda e: e.tensor_copy(out=identb[:], in_=ident[:]), reads=['ident'], writes=['identb'])
        P.op('dve', lambda e: e.tensor_scalar(out=ngbT[:], in0=gbT[:], scalar1=-1.0, scalar2=None, op0=ALU.mult), reads=['gbT'], writes=['ngbT'])
        P.dma(wst[:, :, 0:16], w_in_odd[:, 10240:10256].rearrange("(kc p) n -> p kc n", p=128), writes=['wst'])
        P.op('pool', lambda e: e.tensor_copy(out=wGb[:], in_=wst[:, :, 0:16]), reads=['wst'], writes=['wGb'])
        LNK = float(np.log(DH ** -0.5))

        def load_w1(c0, ncols, dst, dk):
            v = w_in_odd[:, c0:c0 + ncols].rearrange("(kc p) n -> p kc n", p=128)
            for q0 in range(0, ncols, 256):
                w_ = min(256, ncols - q0)
                P.dma(wst[:, :, 0:w_], v[:, :, q0:q0 + w_], writes=['wst'])
                P.op('act', lambda e: e.copy(out=dst[:, :, q0:q0 + w_], in_=wst[:, :, 0:w_]), reads=['wst'], writes=[dk])

        def ld1(c0):
            return lambda dst, dk: load_w1(c0, DH, dst, dk)

        def gates_seq(T, is_sample):
            NC = T // LC
            pbk = min(512, T)
            for d in range(2):
                pb = 32 * d
                rows = slice(pb, pb + 4)
                for b in range(T // pbk):
                    t0 = b * pbk
                    pz, pk = nps()
                    for kc in range(8):
                        P.op('pe', lambda e: e.matmul(pz[pb:pb + 4, 0:pbk], wGb[:, kc, (2 + d) * 4:(3 + d) * 4], hT[:, kc, t0:t0 + pbk],
                                                      start=(kc == 0), stop=(kc == 7)), reads=['wGb'] + hT_keys(t0, t0 + pbk), writes=[pk])
                    P.op('act', lambda e: e.activation(out=XA[rows, t0:t0 + pbk], in_=pz[pb:pb + 4, 0:pbk], func=AF.Exp,
                                                       bias=ngbT[rows, 2 + d:3 + d], scale=-1.0), reads=[pk, 'ngbT'], writes=['XA'])
                P.op('act', lambda e: e.activation(out=XA[rows, 0:T], in_=XA[rows, 0:T], func=AF.Ln, bias=1.0, scale=1.0),
                     reads=['XA'], writes=['XA'])
                for b in range(T // pbk):
                    bs = slice(b * pbk, (b + 1) * pbk)
                    if d == 0:
                        P.op('dve', lambda e: e.tensor_tensor_scan(out=XB[rows, bs], data0=XA[rows, bs], data1=zrow[rows, 0:pbk],
                                                                   initial=0.0, op0=ALU.add, op1=ALU.add), reads=['XA', 'zrow'], writes=['XB'])
                    else:
                        P.op('dve', lambda e: e.tensor_tensor_scan(out=XB[rows, bs][:, ::-1], data0=XA[rows, bs][:, ::-1],
                                                                   data1=zrow[rows, 0:pbk], initial=0.0, op0=ALU.add, op1=ALU.add),
                             reads=['XA', 'zrow'], writes=['XB'])
                if d == 0:
                    ci_, ce_ = 0, LC - 1
                else:
                    ci_, ce_ = LC - 1, 0
                B3 = XB[rows, 0:T].rearrange("p (c l) -> p c l", l=LC)
                A3 = XA[rows, 0:T].rearrange("p (c l) -> p c l", l=LC)
                S = {k_: v_[rows, :] for k_, v_ in sm.items()}
                P.op('dve', lambda e: e.tensor_tensor(out=S['gref'][:, 0:NC], in0=B3[:, :, ci_], in1=A3[:, :, ci_], op=ALU.subtract),
                     reads=['XA', 'XB'], writes=['sm_gref'])
                P.op('dve', lambda e: e.tensor_tensor(out=B3, in0=B3, in1=S['gref'][:, 0:NC].unsqueeze(2).to_broadcast([4, NC, LC]),
                                                      op=ALU.subtract), reads=['XB', 'sm_gref'], writes=['XB'])
                for b in range(T // pbk):
                    t0 = b * pbk
                    pz, pk = nps()
                    for kc in range(8):
                        P.op('pe', lambda e: e.matmul(pz[pb:pb + 4, 0:pbk], wGb[:, kc, d * 4:(d + 1) * 4], hT[:, kc, t0:t0 + pbk],
                                                      start=(kc == 0), stop=(kc == 7)), reads=['wGb'] + hT_keys(t0, t0 + pbk), writes=[pk])
                    P.op('dve', lambda e: e.scalar_tensor_tensor(out=XA[rows, t0:t0 + pbk], in0=pz[pb:pb + 4, 0:pbk], scalar=gbT[rows, d:d + 1],
                                                                 in1=XB[rows, t0:t0 + pbk], op0=ALU.add, op1=ALU.add),
                         reads=[pk, 'gbT', 'XB', 'XA'], writes=['XA'])
                P.op('dve', lambda e: e.tensor_reduce(out=S['ac'][:, 0:NC], in_=A3, axis=AX.X, op=ALU.max), reads=['XA'], writes=['sm_ac'])
                P.op('dve', lambda e: e.tensor_scalar(out=S['bl'][:, 0:NC], in0=B3[:, :, ce_], scalar1=-1.0, scalar2=None, op0=ALU.mult),
                     reads=['XB'], writes=['sm_bl'])
                if is_sample:
                    P.dma(S['m0'][:, 0:1], s_m[d, :].rearrange("(h o) -> h o", o=1), writes=['sm_m0'])
                else:
                    P.op('dve', lambda e: e.memset(S['m0'][:, 0:1], 0.0), writes=['sm_m0'])
                if d == 0:
                    P.op('dve', lambda e: e.tensor_tensor_scan(out=S['M'][:, 0:NC], data0=S['ac'][:, 0:NC], data1=S['bl'][:, 0:NC],
                                                               initial=S['m0'][:, 0:1], op0=ALU.max, op1=ALU.add),
                         reads=['sm_ac', 'sm_bl', 'sm_m0'], writes=['sm_M'])
                    P.op('dve', lambda e: e.tensor_copy(out=S['MP'][:, 0:1], in_=S['m0'][:, 0:1]), reads=['sm_m0'], writes=['sm_MP'])
                    if NC > 1:
                        P.op('dve', lambda e: e.tensor_copy(out=S['MP'][:, 1:NC], in_=S['M'][:, 0:NC - 1]), reads=['sm_M', 'sm_MP'], writes=['sm_MP'])
                else:
                    P.op('dve', lambda e: e.tensor_tensor_scan(out=S['M'][:, 0:NC][:, ::-1], data0=S['ac'][:, 0:NC][:, ::-1],
                                                               data1=S['bl'][:, 0:NC][:, ::-1], initial=S['m0'][:, 0:1],
                                                               op0=ALU.max, op1=ALU.add),
                         reads=['sm_ac', 'sm_bl', 'sm_m0'], writes=['sm_M'])
                    P.op('dve', lambda e: e.tensor_copy(out=S['MP'][:, NC - 1:NC], in_=S['m0'][:, 0:1]), reads=['sm_m0'], writes=['sm_MP'])
                    if NC > 1:
                        P.op('dve', lambda e: e.tensor_copy(out=S['MP'][:, 0:NC - 1], in_=S['M'][:, 1:NC]), reads=['sm_M', 'sm_MP'], writes=['sm_MP'])
                P.op('dve', lambda e: e.tensor_tensor(out=S['mu'][:, 0:NC], in0=S['MP'][:, 0:NC], in1=S['ac'][:, 0:NC], op=ALU.max),
                     reads=['sm_MP', 'sm_ac'], writes=['sm_mu'])
                P.op('dve', lambda e: e.tensor_tensor(out=S['al'][:, 0:NC], in0=S['MP'][:, 0:NC], in1=S['mu'][:, 0:NC], op=ALU.subtract),
                     reads=['sm_MP', 'sm_mu'], writes=['sm_al'])
                P.op('act', lambda e: e.activation(out=S['al'][:, 0:NC], in_=S['al'][:, 0:NC], func=AF.Exp), reads=['sm_al'], writes=['sm_al'])
                mub = S['mu'][:, 0:NC].unsqueeze(2).to_broadcast([4, NC, LC])
                P.op('dve', lambda e: e.tensor_tensor(out=A3, in0=A3, in1=mub, op=ALU.subtract), reads=['XA', 'sm_mu'], writes=['XA'])
                P.op('dve', lambda e: e.tensor_tensor(out=B3, in0=B3, in1=mub, op=ALU.subtract), reads=['XB', 'sm_mu'], writes=['XB'])
                P.op('dve', lambda e: e.tensor_scalar(out=XA[rows, 0:T], in0=XA[rows, 0:T], scalar1=LNK, scalar2=None, op0=ALU.add),
                     reads=['XA'], writes=['XA'])
                P.op('act', lambda e: e.activation(out=XA[rows, 0:T], in_=XA[rows, 0:T], func=AF.Exp), reads=['XA'], writes=['XA'])
                P.op('act', lambda e: e.activation(out=XB[rows, 0:T], in_=XB[rows, 0:T], func=AF.Exp), reads=['XB'], writes=['XB'])
                pz, pk = nps()
                for c in range(NC):
                    P.op('pe', lambda e: e.transpose(out=pz[:, c * 8:c * 8 + 4], in_=XA[rows, c * LC:(c + 1) * LC],
                                                     identity=ident[rows, pb:pb + 4]), reads=['XA', 'ident'], writes=[pk])
                    P.op('pe', lambda e: e.transpose(out=pz[:, c * 8 + 4:c * 8 + 8], in_=XB[rows, c * LC:(c + 1) * LC],
                                                     identity=ident[rows, pb:pb + 4]), reads=['XB', 'ident'], writes=[pk])
                P.op('dve', lambda e: e.tensor_copy(out=Wtok[:, d, 0:NC, :], in_=pz[:, 0:NC * 8].rearrange("p (c k) -> p c k", k=8)),
                     reads=[pk], writes=['Wtok'])
                P.op('dve', lambda e: e.tensor_copy(out=Wtokb[:, d, 0:NC, :], in_=Wtok[:, d, 0:NC, 0:4]), reads=['Wtok'], writes=['Wtokb'])
                pz, pk = nps()
                for hd in range(4):
                    P.op('pe', lambda e: e.matmul(pz[:, hd * 16:hd * 16 + NC], SEL[rows, hd, :], S['al'][:, 0:NC], start=True, stop=True),
                         reads=['SEL', 'sm_al'], writes=[pk])
                P.op('dve', lambda e: e.tensor_copy(out=ALb[:, d, :, 0:NC], in_=pz[:, 0:64].rearrange("p (h c) -> p h c", c=16)[:, :, 0:NC]),
                     reads=[pk], writes=['ALb'])

        def conv_tile(dst, dkey, slot_j, widx, t0src, T, is_sample):
            X = FT[1][:, 0:T]
            A = FT[0][:, 0:T]
            if is_sample:
                R_, Cw = T // 64, 64
                taps = [(dr, dc) for dr in (-1, 0, 1) for dc in (-1, 0, 1)]
            else:
                R_, Cw = 1, T
                taps = [(0, dc) for dc in (-1, 0, 1)]
            X3 = X.rearrange("p (r c) -> p r c", c=Cw)
            A3 = A.rearrange("p (r c) -> p r c", c=Cw)
            P.op('dve', lambda e: e.tensor_scalar(out=A, in0=X, scalar1=cw[:, widx, 4:5], scalar2=None, op0=ALU.mult),
                 reads=['FT1', 'cw'], writes=['FT0'])
            for (dr, dc) in taps:
                if dr == 0 and dc == 0:
                    continue
                r0, r1 = max(0, -dr), R_ - max(0, dr)
                c0, c1 = max(0, -dc), Cw - max(0, dc)
                ti = (dr + 1) * 3 + (dc + 1)
                P.op('dve', lambda e: e.scalar_tensor_tensor(out=A3[:, r0:r1, c0:c1], in0=X3[:, r0 + dr:r1 + dr, c0 + dc:c1 + dc],
                                                             scalar=cw[:, widx, ti:ti + 1], in1=A3[:, r0:r1, c0:c1],
                                                             op0=ALU.mult, op1=ALU.add), reads=['FT1', 'FT0', 'cw'], writes=['FT0'])
            P.op('act', lambda e: e.activation(out=dst[:, slot_j, 0:T], in_=A, func=AF.Silu, bias=cb[:, widx:widx + 1], scale=1.0),
                 reads=['FT0', 'cb'], writes=[dkey])

        hacc = [FT[2 + i][:, 0:TS].rearrange("p (j e) -> p j e", e=DH) for i in range(4)]

        def mlstm_head(hd, off, T, is_sample, pidx):
            NC = T // LC
            NTt = T // 128
            pbk = min(512, T)
            for qk in range(2):
                wb1.use(('qk', off, hd, qk), ld1(qk * 2048 + hd * DH))
                if qk == 0:
                    wb1.prefetch(('qk', off, hd, 1), ld1(2048 + hd * DH))
                else:
                    wb1.prefetch(('v', off, hd), ld1(4096 + hd * DH))
                for j in range(4):
                    for b in range(T // pbk):
                        t0 = b * pbk
                        pz, pk = nps()
                        for kc in range(8):
                            P.op('pe', lambda e: e.matmul(pz[:, 0:pbk], wb1.tile[:, kc, j * 128:(j + 1) * 128], hT[:, kc, t0:t0 + pbk],
                                                          start=(kc == 0), stop=(kc == 7)), reads=[wb1.key] + hT_keys(t0, t0 + pbk), writes=[pk])
                        P.op('act', lambda e: e.copy(out=FT[1][:, t0:t0 + pbk], in_=pz[:, 0:pbk]), reads=[pk], writes=['FT1'])
                    widx = (qk * 4 + hd) * 4 + j
                    if qk == 0:
                        conv_tile(qT, ('yT', 4 + j), j, widx, 0, T, is_sample)
                    else:
                        conv_tile(kT, 'kT', j, widx, 0, T, is_sample)
            wb1.use(('v', off, hd), ld1(4096 + hd * DH))
            wb1.prefetch(('o', off, hd), ld1(6144 + hd * DH))
            qkeys = [('yT', 4 + j) for j in range(4)]
            for d in range(2):
                rev = (d == 1)
                if is_sample:
                    P.dma(Cst[:], s_C[d, hd].rearrange("(j p) e -> p j e", p=128), writes=['Cst'])
                    P.dma(nst[:, 0:4], s_n[d, hd].rearrange("(j p) -> p j", p=128), writes=['nst'], allow_slow_non_contiguous=True)
                else:
                    P.op('pool', lambda e: e.memset(Cst[:], 0.0), writes=['Cst'])
                    P.op('pool', lambda e: e.memset(nst[:, 0:4], 0.0), writes=['nst'])
                chunks = list(range(NC))
                if rev:
                    chunks = chunks[::-1]
                for c in chunks:
                    cs = slice(c * LC, (c + 1) * LC)
                    wcol = Wtok[:, d, c, hd:hd + 1]
                    thcol = Wtok[:, d, c, 4 + hd:5 + hd]
                    alcol = ALb[:, d, hd, c:c + 1]
                    pv, pvk = nps()
                    for kc in range(8):
                        P.op('pe', lambda e: e.matmul(pv[:, 0:DH], hT[:, kc, cs], wb1.tile[:, kc, :], start=(kc == 0), stop=(kc == 7)),
                             reads=[wb1.key, ('hT', c)], writes=[pvk])
                    P.op('act', lambda e: e.copy(out=vch[:], in_=pv[:, 0:DH]), reads=[pvk], writes=['vch'])
                    pt, ptk = nps()
                    ptb = pt[:].bitcast(BF16)
                    for j in range(4):
                        P.op('pe', lambda e: e.transpose(out=ptb[:, j * 128:(j + 1) * 128], in_=kT[:, j, cs], identity=identb[:]),
                             reads=['kT', 'identb'], writes=[ptk])
                    P.op('act', lambda e: e.copy(out=ktok[:], in_=ptb[:, 0:DH]), reads=[ptk], writes=['ktok'])
                    ps_, psk = nps()
                    for j in range(4):
                        P.op('pe', lambda e: e.matmul(ps_[:, 0:128], kT[:, j, cs], qT[:, j, cs], start=(j == 0), stop=(j == 3)),
                             reads=['kT'] + qkeys, writes=[psk])
                    P.op('dve', lambda e: e.scalar_tensor_tensor(out=sTs[:], in0=ps_[:, 0:128], scalar=wcol, in1=mC[:, d, :],
                                                                 op0=ALU.mult, op1=ALU.mult), reads=[psk, 'Wtok', 'mC'], writes=['sTs'])
                    P.op('dve', lambda e: e.tensor_scalar(out=Cst[:], in0=Cst[:], scalar1=alcol, scalar2=None, op0=ALU.mult),
                         reads=['Cst', 'ALb'], writes=['Cst'])
                    P.op('act', lambda e: e.copy(out=Cbf[:], in_=Cst[:]), reads=['Cst'], writes=['Cbf'])
                    P.op('dve', lambda e: e.tensor_scalar(out=nst[:, 0:4], in0=nst[:, 0:4], scalar1=alcol, scalar2=None, op0=ALU.mult),
                         reads=['nst', 'ALb'], writes=['nst'])
                    P.op('dve', lambda e: e.tensor_copy(out=nbf[:], in_=nst[:, 0:4]), reads=['nst'], writes=['nbf'])
                    pn, pnk = nps()
                    for j in range(4):
                        P.op('pe', lambda e: e.matmul(pn[:, 0:DH], qT[:, j, cs], Cbf[:, j, :], start=(j == 0), stop=False),
                             reads=qkeys + ['Cbf'], writes=[pnk])
                    P.op('pe', lambda e: e.matmul(pn[:, 0:DH], sTs[:], vch[:], start=False, stop=True),
                         reads=['sTs', 'vch'], writes=[pnk])
                    pd_, pdk = nps()
                    for j in range(4):
                        P.op('pe', lambda e: e.matmul(pd_[:, 0:1], qT[:, j, cs], nbf[:, j:j + 1], start=(j == 0), stop=False),
                             reads=qkeys + ['nbf'], writes=[pdk])
                    P.op('pe', lambda e: e.matmul(pd_[:, 0:1], sTs[:], onesb[:], start=False, stop=True), reads=['sTs', 'onesb'], writes=[pdk])
                    P.op('act', lambda e: e.activation(out=dstat[:, 2:3], in_=pd_[:, 0:1], func=AF.Abs), reads=[pdk], writes=['dstat'])
                    P.op('dve', lambda e: e.tensor_tensor(out=dstat[:, 0:1], in0=dstat[:, 2:3], in1=thcol, op=ALU.max),
                         reads=['dstat', 'Wtok'], writes=['dstat'])
                    P.op('dve', lambda e: e.reciprocal(out=dstat[:, 1:2], in_=dstat[:, 0:1]), reads=['dstat'], writes=['dstat'])
                    hdst = hacc[c // 4][:, c % 4, :]
                    if d == 0:
                        P.op('act', lambda e: e.activation(out=hdst, in_=pn[:, 0:DH], func=AF.Identity, scale=dstat[:, 1:2]),
                             reads=[pnk, 'dstat'], writes=[('hacc', c)])
                    else:
                        P.op('dve', lambda e: e.scalar_tensor_tensor(out=hdst, in0=pn[:, 0:DH], scalar=dstat[:, 1:2], in1=hdst,
                                                                     op0=ALU.mult, op1=ALU.add), reads=[pnk, 'dstat', ('hacc', c)], writes=[('hacc', c)])
                    P.op('act', lambda e: e.activation(out=vw[:], in_=vch[:], func=AF.Identity, scale=wcol),
                         reads=['vch', 'Wtok'], writes=['vw'])
                    for j in range(4):
                        pc, pck = nps()
                        P.op('pe', lambda e: e.matmul(pc[:, 0:DH], ktok[:, j * 128:(j + 1) * 128], vw[:], start=True, stop=True),
                             reads=['ktok', 'vw'], writes=[pck])
                        P.op('dve', lambda e: e.tensor_tensor(out=Cst[:, j, :], in0=Cst[:, j, :], in1=pc[:, 0:DH], op=ALU.add),
                             reads=[pck, 'Cst'], writes=['Cst'])
                    pq_, pqk = nps()
                    for j in range(4):
                        P.op('pe', lambda e: e.matmul(pq_[:, j:j + 1], ktok[:, j * 128:(j + 1) * 128], Wtokb[:, d, c, hd:hd + 1],
                                                      start=True, stop=True), reads=['ktok', 'Wtokb'], writes=[pqk])
                    P.op('dve', lambda e: e.tensor_tensor(out=nst[:, 0:4], in0=nst[:, 0:4], in1=pq_[:, 0:4], op=ALU.add),
                         reads=[pqk, 'nst'], writes=['nst'])
                if not is_sample:
                    P.dma(o_C[pidx, d, hd].rearrange("(j p) e -> p j e", p=128), Cst[:], reads=['Cst'], q='pool')
                    P.dma(o_n[pidx, d, hd].rearrange("(j p) -> p j", p=128), nst[:, 0:4], reads=['nst'], q='pool', allow_slow_non_contiguous=True)
            wb1.use(('o', off, hd), ld1(6144 + hd * DH))
            wb1.prefetch(('z', off, hd), ld1(8192 + hd * DH))
            for tt in range(NTt):
                hdst = hacc[tt // 4][:, tt % 4, :]
                pz, pk = nps()
                for kc in range(8):
                    P.op('pe', lambda e: e.matmul(pz[:, 0:DH], hT[:, kc, tt * 128:(tt + 1) * 128], wb1.tile[:, kc, :],
                                                  start=(kc == 0), stop=(kc == 7)), reads=[wb1.key, ('hT', tt)], writes=[pk])
                P.op('act', lambda e: e.activation(out=FT[0][:, 0:DH], in_=pz[:, 0:DH], func=AF.Sigmoid), reads=[pk], writes=['FT0'])
                P.op('dve', lambda e: e.tensor_tensor(out=hdst, in0=hdst, in1=FT[0][:, 0:DH], op=ALU.mult),
                     reads=['FT0', ('hacc', tt)], writes=[('hacc', tt)])
                P.op('act', lambda e: e.activation(out=FT[0][:, 0:DH], in_=hdst, func=AF.Square, accum_out=dstat[:, 4:5]),
                     reads=[('hacc', tt), 'FT0'], writes=['FT0', 'dstat'])
                P.op('dve', lambda e: e.tensor_scalar(out=dstat[:, 5:6], in0=dstat[:, 4:5], scalar1=1.0 / DH, scalar2=1e-6,
                                                      op0=ALU.mult, op1=ALU.add), reads=['dstat'], writes=['dstat'])
                P.op('act', lambda e: e.activation(out=dstat[:, 6:7], in_=dstat[:, 5:6], func=AF.Sqrt), reads=['dstat'], writes=['dstat'])
                P.op('dve', lambda e: e.reciprocal(out=dstat[:, 7:8], in_=dstat[:, 6:7]), reads=['dstat'], writes=['dstat'])
                P.op('dve', lambda e: e.tensor_scalar(out=hdst, in0=hdst, scalar1=dstat[:, 7:8], scalar2=None, op0=ALU.mult),
                     reads=[('hacc', tt), 'dstat'], writes=[('hacc', tt)])
            wb1.use(('z', off, hd), ld1(8192 + hd * DH))
            if hd < 3:
                wb1.prefetch(('qk', off, hd + 1, 0), ld1((hd + 1) * DH))
            for tt in range(NTt):
                hdst = hacc[tt // 4][:, tt % 4, :]
                pz, pk = nps()
                for kc in range(8):
                    P.op('pe', lambda e: e.matmul(pz[:, 0:DH], hT[:, kc, tt * 128:(tt + 1) * 128], wb1.tile[:, kc, :],
                                                  start=(kc == 0), stop=(kc == 7)), reads=[wb1.key, ('hT', tt)], writes=[pk])
                P.op('act', lambda e: e.activation(out=FT[0][:, 0:DH], in_=pz[:, 0:DH], func=AF.Silu), reads=[pk], writes=['FT0'])
                P.op('dve', lambda e: e.tensor_tensor(out=hdst, in0=hdst, in1=FT[0][:, 0:DH], op=ALU.mult),
                     reads=['FT0', ('hacc', tt)], writes=[('hacc', tt)])
                pz, pk = nps()
                for j in range(4):
                    P.op('pe', lambda e: e.transpose(out=pz[:, j * 128:(j + 1) * 128], in_=hdst[:, j * 128:(j + 1) * 128], identity=ident[:]),
                         reads=[('hacc', tt), 'ident'], writes=[pk])
                for j in range(4):
                    P.op('act', lambda e: e.activation(out=yT[:, j, tt * 128:(tt + 1) * 128], in_=pz[:, j * 128:(j + 1) * 128],
                                                       func=AF.Identity, scale=mng[:, hd * 4 + j:hd * 4 + j + 1]),
                         reads=[pk, 'mng'], writes=[('yT', j)])

        for si in seq_ids:
            off, T, cidx, is_sample = SEQS[si]
            P.barrier()
            make_gate(1, cidx)
            make_hT(1, x1, 'x1', off, T, cidx)
            P.barrier()
            gates_seq(T, is_sample)
            if not is_sample:
                for d in range(2):
                    lastc = (T // LC - 1) if d == 0 else 0
                    P.dma(o_m[si - 1, d, :].rearrange("(h o) -> h o", o=1), sm['M'][32 * d:32 * d + 4, lastc:lastc + 1],
                          reads=['sm_M'], q='pool')
            for hd in range(4):
                P.barrier()
                mlstm_head(hd, off, T, is_sample, si - 1)
                if debug and debug.get('dump_y'):
                    for j in range(4):
                        dump("yM%d_%d" % (hd, j), yT[:, j, 0:T], [('yT', j)], T, col0=off)
                P.barrier()
                load_wo(w_out_odd[hd * DH:(hd + 1) * DH, :], 128, nk=4)
                last = (hd == 3)
                outproj(1, [(128, s_) for s_ in range(4)], x1, 'x1', (yout if last else x1), ('yout' if last else 'x1'),
                        off, T, cidx, final=last)
        P.barrier()
        L1.close()
    P.finish()
    sems = {s: es.enter_context(nc.semaphore(s)) for s in P.sem_names}
    P.emit(sems)
    es.close()
    global _last_dslot
    _last_dslot = dslot if debug else {}
    return nc, P


def host_inputs(inp, core):
    f = lambda a: np.ascontiguousarray(a, dtype=np.float32)
    b = core % 2
    m = {}
    m["xin"] = f(np.concatenate([inp["x_sample"][b], inp["x_prompt"][2 * core], inp["x_prompt"][2 * core + 1]], axis=0))
    cond = np.stack([inp["c"][b], inp["c_ctx"]], axis=0)
    m["condT"] = f(cond.reshape(2, 8, 128).transpose(2, 1, 0))
    m["s_hgrn"] = f(inp["state_hgrn"][b, 0])
    m["s_rwkv"] = f(inp["state_rwkv"][b, 0])
    m["s_C"] = f(inp["state_mlstm_C"][b, 0])
    m["s_n"] = f(inp["state_mlstm_n"][b, 0])
    m["s_m"] = f(inp["state_mlstm_m"][b, 0])
    m["w_mod"] = f(inp["w_mod"])
    m["b_modT"] = f(inp["b_mod"].reshape(2, 24, 128).transpose(2, 0, 1))
    m["norm_gT"] = f(inp["norm_g"].reshape(2, 8, 128).transpose(2, 0, 1))
    m["fnorm_gT"] = f(inp["final_norm_g"].reshape(8, 128).T)
    w = inp["w_in_even"][0]
    DA = 1024
    wA = np.stack([np.concatenate([w[:, g * DA + h * 128: g * DA + (h + 1) * 128] for g in (0, 1, 4, 2, 3)], axis=1)
                   for h in range(8)], axis=0)
    m["wA"] = f(wA)
    o = 5 * DA
    zb0 = o + 3328
    wB = np.stack([np.concatenate([w[:, o + g * 1024 + h * 64: o + g * 1024 + (h + 1) * 64] for g in (0, 1, 2)]
                                  + [w[:, zb0 + h * 64: zb0 + (h + 1) * 64]], axis=1) for h in range(16)], axis=0)
    m["wB"] = f(wB)
    m["wLR"] = f(w[:, o + 3072: o + 3328])
    m["w_out_even"] = f(inp["w_out_even"][0])
    m["lbT"] = f(inp["hgrn_lb_logits"].reshape(2, 8, 128).transpose(2, 0, 1))
    m["hg_gT"] = f(inp["hgrn_norm_g"][0].reshape(8, 128).T)
    mu = inp["rwkv_shift_mu"][0]
    mr = np.zeros((64, 2, 4, 16), np.float32)
    for g in range(3):
        mr[:, :, g, :] = mu[:, g * 1024:(g + 1) * 1024].reshape(2, 16, 64).transpose(2, 0, 1)
    m["mu_rkv"] = mr
    m["mu_lr"] = f(mu[:, 3072:3328].reshape(2, 4, 64).transpose(2, 0, 1))
    m["w0T"] = f(inp["rwkv_w0"][0].reshape(2, 16, 64).transpose(2, 0, 1))
    m["a0T"] = f(inp["rwkv_a0"][0].reshape(2, 16, 64).transpose(2, 0, 1))
    m["w2"] = f(inp["rwkv_w2"][0])
    m["a2"] = f(inp["rwkv_a2"][0])
    m["kkT"] = f(inp["rwkv_k_k"][0].reshape(16, 64).T)
    m["kaT"] = f(inp["rwkv_k_a"][0].reshape(16, 64).T)
    m["rkT"] = f(inp["rwkv_r_k"][0].T)
    m["gngT"] = f(inp["rwkv_gn_g"][0].reshape(16, 64).T)
    m["gnbT"] = f(inp["rwkv_gn_b"][0].reshape(16, 64).T)
    s = np.arange(128)[:, None]
    t = np.arange(128)[None, :]
    same = (s // 32) == (t // 32)
    m["maskH"] = np.stack([(same & (s <= t)), (same & (s >= t))]).astype(np.float32)
    m["ident_in"] = np.eye(128, dtype=np.float32)
    s6 = np.arange(64)[:, None]
    t6 = np.arange(64)[None, :]
    mr_ = np.zeros((2, 3, 64, 128), np.float32)
    for d_, (st_, inc_) in enumerate([((s6 < t6), (s6 <= t6)), ((s6 > t6), (s6 >= t6))]):
        st_ = st_.astype(np.float32)
        inc_ = inc_.astype(np.float32)
        mr_[d_, 0, :, 0:64] = -st_
        mr_[d_, 0, :, 64:128] = -inc_
        mr_[d_, 1, :, 0:64] = st_
        mr_[d_, 1, :, 64:128] = inc_
        mr_[d_, 2, :, 0:64] = -(st_.T)
    m["maskR"] = mr_
    m["w_in_odd"] = f(inp["w_in_odd"][0])
    m["w_out_odd"] = f(inp["w_out_odd"][0])
    m["maskC"] = np.stack([(s <= t), (s >= t)]).astype(np.float32)
    sel = np.zeros((36, 4, 128), np.float32)
    gbt = np.zeros((36, 4), np.float32)
    for pb_ in (0, 32):
        for k_ in range(4):
            sel[pb_ + k_, k_, :] = 1.0
        gbt[pb_:pb_ + 4, :] = inp["mlstm_gate_b"][0].T
    m["sel_d"] = sel
    m["gbT_d"] = gbt
    m["cw_d"] = f(inp["mlstm_conv_w"][0].reshape(9, 32, 128).transpose(2, 1, 0))
    m["cb_d"] = f(inp["mlstm_conv_b"][0].reshape(32, 128).T)
    m["mng_d"] = f(inp["mlstm_norm_g"][0].reshape(16, 128).T)
    return m


def kernel(**inp):
    inp = {k: np.asarray(v) for k, v in inp.items()}
    nc, P = build()
    in_maps = [host_inputs(inp, c) for c in range(NCORES)]
    res = run_bass_kernel_spmd(nc, in_maps, core_ids=list(range(NCORES)))
    r = res.results
    y_prompt = np.zeros((16, TP, D), np.float32)
    y_sample = np.zeros((2, TS, D), np.float32)
    for c in range(NCORES):
        y_prompt[2 * c] = r[c]["yout"][TS:TS + TP]
        y_prompt[2 * c + 1] = r[c]["yout"][TS + TP:]
    for b in range(2):
        y_sample[b] = r[b]["yout"][0:TS]
    new_hgrn = np.concatenate([r[c]["o_hgrn"] for c in range(NCORES)], axis=0)[:, None]
    new_rwkv = np.concatenate([r[c]["o_rwkv"] for c in range(NCORES)], axis=0)[:, None]
    new_C = np.concatenate([r[c]["o_C"] for c in range(NCORES)], axis=0)[:, None]
    new_n = np.concatenate([r[c]["o_n"] for c in range(NCORES)], axis=0)[:, None]
    new_m = np.concatenate([r[c]["o_m"] for c in range(NCORES)], axis=0)[:, None]
    return (y_prompt, y_sample, new_hgrn.astype(np.float32), new_rwkv.astype(np.float32),
            new_C.astype(np.float32), new_n.astype(np.float32), new_m.astype(np.float32))
```

```python
import contextlib
import numpy as np
import concourse.bass as bass
import concourse.mybir as mybir
from concourse.bass_utils import run_bass_kernel_spmd

F32 = mybir.dt.float32
BF16 = mybir.dt.bfloat16
ALU = mybir.AluOpType
AF = mybir.ActivationFunctionType
AX = mybir.AxisListType

D = 1024
TS = 2048
TP = 256
TT = TS + 2 * TP
NCORES = 8


class _Rec:
    def __init__(self):
        self.calls = []

    def __getattr__(self, name):
        def f(*a, **k):
            self.calls.append((name, a, k))
            return self
        return f


class Prog:
    ENGS = ['pe', 'dve', 'act', 'pool', 'sp']
    NDMA = 16

    def __init__(self, nc):
        self.nc = nc
        self.ops = {e: [] for e in self.ENGS}
        self.cnt = {}
        self.waited = {e: {} for e in self.ENGS}
        self.last_write = {}
        self.readers = {}
        self.dma_rr = 0
        self.sem_names = list(self.ENGS) + ['d%d' % i for i in range(self.NDMA)]
        for s in self.sem_names:
            self.cnt[s] = 0
        self.n_ops = 0

    def _deps(self, eng, reads, writes):
        deps = {}

        def add(p):
            if p is None:
                return
            f, n = p
            if f == 'pe' and eng == 'pe':
                return
            if n > deps.get(f, 0):
                deps[f] = n
        for k in reads:
            add(self.last_write.get(k))
        for k in writes:
            add(self.last_write.get(k))
            for p in self.readers.get(k, ()):
                add(p)
        waits = []
        for f, n in deps.items():
            if n > self.waited[eng].get(f, 0):
                waits.append((f, n))
                self.waited[eng][f] = n
        return waits

    def _commit(self, tag, reads, writes):
        for k in reads:
            lst = self.readers.setdefault(k, [])
            lst[:] = [p for p in lst if p[0] != tag[0]]
            lst.append(tag)
        for k in writes:
            self.last_write[k] = tag
            self.readers[k] = []

    def op(self, eng, fn, reads=(), writes=()):
        rec = _Rec()
        fn(rec)
        name, a, k = rec.calls[0]
        fn = (lambda e, name=name, a=a, k=k: getattr(e, name)(*a, **k))
        waits = self._deps(eng, reads, writes)
        self.cnt[eng] += 1
        tag = (eng, self.cnt[eng])
        self.ops[eng].append((waits, fn, eng, 1))
        self._commit(tag, reads, writes)
        self.n_ops += 1

    def dma(self, out, in_, reads=(), writes=(), q='sp', **kw):
        d = 'd%d' % self.dma_rr
        self.dma_rr = (self.dma_rr + 1) % self.NDMA
        waits = self._deps(q, reads, writes)
        prev = self.cnt[d]
        if prev > self.waited[q].get(d, 0):
            waits.append((d, prev))
            self.waited[q][d] = prev
        self.cnt[d] += 16
        tag = (d, self.cnt[d])
        self.ops[q].append((waits, (lambda e: e.dma_start(out=out, in_=in_, **kw)), d, 16))
        self._commit(tag, reads, writes)
        self.n_ops += 1

    def barrier(self):
        allsems = list(self.sem_names)
        for e in self.ENGS:
            waits = []
            for f in allsems:
                if self.cnt[f] > self.waited[e].get(f, 0):
                    waits.append((f, self.cnt[f]))
                    self.waited[e][f] = self.cnt[f]
            self.ops[e].append((waits, None, None, 0))

    def finish(self, q='sp'):
        waits = []
        for i in range(self.NDMA):
            d = 'd%d' % i
            if self.cnt[d] > self.waited[q].get(d, 0):
                waits.append((d, self.cnt[d]))
                self.waited[q][d] = self.cnt[d]
        self.ops[q].append((waits, None, None, 0))

    def emit(self, sems):
        ops = self.ops

        def run(e, lst):
            for waits, fn, semname, inc in lst:
                for f, n in waits:
                    e.wait_ge(sems[f], n)
                if fn is not None:
                    fn(e).then_inc(sems[semname], inc)
        with self.nc.Block() as block:
            @block.tensor
            def _(e):
                run(e, ops['pe'])

            @block.vector
            def _(e):
                run(e, ops['dve'])

            @block.scalar
            def _(e):
                run(e, ops['act'])

            @block.gpsimd
            def _(e):
                run(e, ops['pool'])

            @block.sync
            def _(e):
                run(e, ops['sp'])


SEQS = [(0, TS, 0, True), (TS, TP, 1, False), (TS + TP, TP, 1, False)]


def build(debug=None):
    nc = bass.Bass('TRN2', target_bir_lowering=False)
    P = Prog(nc)
    es = contextlib.ExitStack()

    def din(name, shape):
        return nc.dram_tensor(name, list(shape), F32, kind="ExternalInput").ap()

    def dout(name, shape):
        return nc.dram_tensor(name, list(shape), F32, kind="ExternalOutput").ap()

    xin = din("xin", [TT, D])
    condT = din("condT", [128, 8, 2])
    s_hgrn = din("s_hgrn", [2, 8, 128, 128])
    s_rwkv = din("s_rwkv", [2, 16, 64, 64])
    s_C = din("s_C", [2, 4, 512, 512])
    s_n = din("s_n", [2, 4, 512])
    s_m = din("s_m", [2, 4])
    w_mod = din("w_mod", [2, D, 3 * D])
    b_modT = din("b_modT", [128, 2, 24])
    norm_gT = din("norm_gT", [128, 2, 8])
    fnorm_gT = din("fnorm_gT", [128, 8])
    wA = din("wA", [8, D, 640])
    wB = din("wB", [16, D, 256])
    wLR = din("wLR", [D, 256])
    w_out_even = din("w_out_even", [2 * D, D])
    lbT = din("lbT", [128, 2, 8])
    hg_gT = din("hg_gT", [128, 8])
    mu_rkv = din("mu_rkv", [64, 2, 4, 16])
    mu_lr = din("mu_lr", [64, 2, 4])
    w0T = din("w0T", [64, 2, 16])
    a0T = din("a0T", [64, 2, 16])
    w2 = din("w2", [2, 64, D])
    a2 = din("a2", [2, 64, D])
    kkT = din("kkT", [64, 16])
    kaT = din("kaT", [64, 16])
    rkT = din("rkT", [64, 16])
    gngT = din("gngT", [64, 16])
    gnbT = din("gnbT", [64, 16])
    maskR = din("maskR", [2, 3, 64, 128])
    maskH = din("maskH", [2, 128, 128])
    ident_d = din("ident_in", [128, 128])
    w_in_odd = din("w_in_odd", [D, 10256])
    w_out_odd = din("w_out_odd", [2 * D, D])
    maskC = din("maskC", [2, 128, 128])
    sel_d = din("sel_d", [36, 4, 128])
    gbT_d = din("gbT_d", [36, 4])
    cw_d = din("cw_d", [128, 32, 9])
    cb_d = din("cb_d", [128, 32])
    mng_d = din("mng_d", [128, 16])

    yout = dout("yout", [TT, D])
    o_hgrn = dout("o_hgrn", [2, 2, 8, 128, 128])
    o_rwkv = dout("o_rwkv", [2, 2, 16, 64, 64])
    o_C = dout("o_C", [2, 2, 4, 512, 512])
    o_n = dout("o_n", [2, 2, 4, 512])
    o_m = dout("o_m", [2, 2, 4])
    dbg = dout("dbg", [40, 128, TT]) if debug else None
    dslot = {}
    dumpt = {}
    x1 = dout("x1", [TT, D]) if debug else nc.dram_tensor("x1", [TT, D], F32, kind="Internal").ap()

    def sb(name, shape, dt=F32):
        return es.enter_context(nc.sbuf_tensor(name, list(shape), dt))

    pstiles = [es.enter_context(nc.psum_tensor("ps%d" % i, [128, 512], F32)) for i in range(8)]
    psrr = [0]

    def nps():
        i = psrr[0]
        psrr[0] = (i + 1) % 8
        return pstiles[i], 'ps%d' % i

    def dump(name, ap, keys, n, col0=0, parts=128):
        if not debug:
            return
        slot = dslot.setdefault(name, len(dslot))
        dt_ = dumpt['tile']
        for c0 in range(0, n, 256):
            w_ = min(256, n - c0)
            P.op('pool', (lambda e, c0=c0, w_=w_: e.tensor_copy(out=dt_[0:parts, 0:w_], in_=ap[:, c0:c0 + w_])),
                 reads=keys, writes=['dumpt'])
            P.dma(dbg[slot, 0:parts, col0 + c0:col0 + c0 + w_], dt_[0:parts, 0:w_], reads=['dumpt'])

    if debug:
        dumpt['tile'] = sb("dumpt", [128, 256])

    ident = sb("ident", [128, 128])
    ones = sb("ones", [128, 128])
    P.dma(ident[:], ident_d[:], writes=['ident'])
    P.op('dve', lambda e: e.memset(ones[:], 1.0), writes=['ones'])

    condT_sb = sb("condT_sb", [128, 8, 2])
    bmod_sb = sb("bmod_sb", [128, 2, 24])
    ng_sb = sb("ng_sb", [128, 2, 8])
    fng_sb = sb("fng_sb", [128, 8])
    lb_sb = sb("lb_sb", [128, 2, 8])
    hgg_sb = sb("hgg_sb", [128, 8])
    for t_, d_, k_ in [(condT_sb, condT, 'condT'), (bmod_sb, b_modT, 'bmod'), (ng_sb, norm_gT, 'ng'),
                       (fng_sb, fnorm_gT, 'fng'), (lb_sb, lbT, 'lb'), (hgg_sb, hg_gT, 'hgg')]:
        P.dma(t_[:], d_[:], writes=[k_])

    scT = sb("scT", [128, 8, 2])
    P.op('act', lambda e: e.activation(out=scT[:], in_=condT_sb[:], func=AF.Silu), reads=['condT'], writes=['scT'])
    mT = sb("mT", [128, 2, 24, 2])
    sc1 = sb("sc1", [128, 2, 8, 2])
    gate_bc = sb("gate_bc", [128, D])
    dg = sb("dg", [128, 128])

    def make_gate(l, c):
        if True:
            for half in range(2):
                pz, pk = nps()
                for kq in range(4):
                    kc = half * 4 + kq
                    P.op('dve', lambda e: e.tensor_scalar(
                        out=dg[:], in0=ident[:], scalar1=mT[:, l, 16 + kc, c:c + 1], scalar2=None, op0=ALU.mult),
                        reads=['ident', 'mT'], writes=['dg'])
                    P.op('pe', lambda e: e.matmul(pz[:, kq * 128:(kq + 1) * 128], ones[:], dg[:], start=True, stop=True),
                         reads=['ones', 'dg'], writes=[pk])
                P.op('act', lambda e: e.copy(out=gate_bc[:, half * 512:(half + 1) * 512], in_=pz[:]),
                     reads=[pk], writes=['gate_bc'])

    with contextlib.ExitStack() as es2:
        wm = [es2.enter_context(nc.sbuf_tensor("wm%d" % i, [128, 8, 512], F32)) for i in range(2)]
        for l in range(2):
            for cbk in range(6):
                i = (l * 6 + cbk) % 2
                P.dma(wm[i][:], w_mod[l].rearrange("(kc p) n -> p kc n", p=128)[:, :, cbk * 512:(cbk + 1) * 512],
                      writes=['wm%d' % i])
                pz, pk = nps()
                for j in range(4):
                    for kc in range(8):
                        P.op('pe', lambda e: e.matmul(pz[:, j * 2:(j + 1) * 2], wm[i][:, kc, j * 128:(j + 1) * 128],
                                                      scT[:, kc, :], start=(kc == 0), stop=(kc == 7)),
                             reads=['wm%d' % i, 'scT'], writes=[pk])
                P.op('dve', lambda e: e.tensor_tensor(out=mT[:, l, cbk * 4:(cbk + 1) * 4, :],
                                                      in0=pz[:, 0:8].rearrange("p (j c) -> p j c", c=2),
                                                      in1=bmod_sb[:, l, cbk * 4:(cbk + 1) * 4].unsqueeze(2).to_broadcast([128, 4, 2]),
                                                      op=ALU.add),
                     reads=[pk, 'bmod'], writes=['mT'])
    P.barrier()
    for l in range(2):
        P.op('dve', lambda e: e.scalar_tensor_tensor(
            out=sc1[:, l], in0=mT[:, l, 8:16, :], scalar=1.0,
            in1=ng_sb[:, l, :].unsqueeze(2).to_broadcast([128, 8, 2]), op0=ALU.add, op1=ALU.mult),
            reads=['mT', 'ng'], writes=['sc1'])
    fng_holder = {}

    def make_fng():
        fng_bc = sb("fng_bc", [128, D])
        fng_holder['t'] = fng_bc
        for half in range(2):
            pz, pk = nps()
            for kq in range(4):
                kc = half * 4 + kq
                P.op('dve', (lambda e, kc=kc: e.tensor_scalar(
                    out=dg[:], in0=ident[:], scalar1=fng_sb[:, kc:kc + 1], scalar2=None, op0=ALU.mult)),
                    reads=['ident', 'fng'], writes=['dg'])
                P.op('pe', (lambda e, pz=pz, kq=kq: e.matmul(pz[:, kq * 128:(kq + 1) * 128], ones[:], dg[:],
                                                             start=True, stop=True)),
                     reads=['ones', 'dg'], writes=[pk])
            P.op('act', (lambda e, half=half, pz=pz: e.copy(out=fng_bc[:, half * 512:(half + 1) * 512], in_=pz[:])),
                 reads=[pk], writes=['fng_bc'])

    lbv = sb("lbv", [128, 8])
    oml = sb("oml", [128, 8])
    P.op('dve', lambda e: e.tensor_tensor(out=lbv[:], in0=lb_sb[:, 0, :], in1=lb_sb[:, 1, :], op=ALU.subtract),
         reads=['lb'], writes=['lbv'])
    P.op('act', lambda e: e.activation(out=lbv[:], in_=lbv[:], func=AF.Sigmoid), reads=['lbv'], writes=['lbv'])
    P.op('act', lambda e: e.activation(out=oml[:], in_=lbv[:], func=AF.Identity, bias=1.0, scale=-1.0),
         reads=['lbv'], writes=['oml'])

    hT = sb("hT", [128, 8, TS], BF16)
    yT = sb("yT", [128, 8, TS], BF16)
    st4 = sb("st4", [128, 4])
    wst = sb("wst", [128, 8, 256])
    FT = [sb("FT%d" % i, [128, TS + 32]) for i in range(6)]
    xt = [FT[0][:, 0:D], FT[0][:, D:2 * D]]
    xn = FT[1][:, 0:D]
    junk = FT[1][:, D:2 * D]
    wo_v = [FT[2][:, 0:TS].bitcast(BF16).rearrange("p (s n) -> p s n", n=D),
            FT[3][:, 0:TS].bitcast(BF16).rearrange("p (s n) -> p s n", n=D)]

    def load_wo(src, parts, nk=8):
        v = src.rearrange("(kc p) n -> p kc n", p=parts)
        for c0 in range(0, D, 256):
            w_ = min(256, D - c0)
            P.dma(wst[0:parts, 0:nk, 0:w_], v[:, :, c0:c0 + w_], writes=['wst'])
            for hf in range(nk // 4):
                P.op('act', lambda e: e.copy(out=wo_v[hf][0:parts, :, c0:c0 + w_], in_=wst[0:parts, hf * 4:hf * 4 + 4, 0:w_]),
                     reads=['wst'], writes=['wo_bf'])

    def make_hT(layer, xsrc, xkey, off, T, cidx):
        for tt in range(T // 128):
            i = tt % 2
            P.dma(xt[i], xsrc[off + tt * 128: off + (tt + 1) * 128, :], reads=[(xkey, off // 128 + tt)], writes=['xt%d' % i])
            P.op('act', lambda e: e.activation(out=junk, in_=xt[i], func=AF.Square, accum_out=st4[:, 0:1]),
                 reads=['xt%d' % i], writes=['junk', 'st4'])
            P.op('dve', lambda e: e.tensor_scalar(out=st4[:, 1:2], in0=st4[:, 0:1], scalar1=1.0 / D, scalar2=1e-6,
                                                  op0=ALU.mult, op1=ALU.add), reads=['st4'], writes=['st4'])
            P.op('act', lambda e: e.activation(out=st4[:, 2:3], in_=st4[:, 1:2], func=AF.Sqrt), reads=['st4'], writes=['st4'])
            P.op('dve', lambda e: e.reciprocal(out=st4[:, 3:4], in_=st4[:, 2:3]), reads=['st4'], writes=['st4'])
            P.op('dve', lambda e: e.tensor_scalar(out=xn, in0=xt[i], scalar1=st4[:, 3:4], scalar2=None, op0=ALU.mult),
                 reads=['xt%d' % i, 'st4'], writes=['xn'])
            for half in range(2):
                pz, pk = nps()
                for kq in range(4):
                    kc = half * 4 + kq
                    P.op('pe', lambda e: e.transpose(out=pz[:, kq * 128:(kq + 1) * 128], in_=xn[:, kc * 128:(kc + 1) * 128],
                                                     identity=ident[:]), reads=['xn', 'ident'], writes=[pk])
                for kq in range(4):
                    kc = half * 4 + kq
                    P.op('act', lambda e: e.activation(
                        out=hT[:, kc, tt * 128:(tt + 1) * 128], in_=pz[:, kq * 128:(kq + 1) * 128], func=AF.Identity,
                        bias=mT[:, layer, kc, cidx:cidx + 1], scale=sc1[:, layer, kc, cidx:cidx + 1]),
                        reads=[pk, 'mT', 'sc1'], writes=[('hT', tt)])

    def hT_keys(t0, t1):
        return [('hT', tt) for tt in range(t0 // 128, (t1 + 127) // 128)]

    class WB:
        def __init__(self, tiles, keys):
            self.t, self.k, self.cur, self.pending = tiles, keys, 0, None

        def prefetch(self, tag, loader):
            if self.pending is not None:
                return
            i = 1 - self.cur
            loader(self.t[i], self.k[i])
            self.pending = tag

        def use(self, tag, loader):
            if self.pending == tag:
                self.cur = 1 - self.cur
                self.pending = None
            else:
                assert self.pending is None, (self.pending, tag)
                i = 1 - self.cur
                loader(self.t[i], self.k[i])
                self.cur = i

        @property
        def tile(self):
            return self.t[self.cur]

        @property
        def key(self):
            return self.k[self.cur]

    def load_w(src, ncols, dst=None, dkey='wbf', parts=128, nk=8):
        v = src.rearrange("(kc p) n -> p kc n", p=parts)
        for c0 in range(0, ncols, 256):
            w_ = min(256, ncols - c0)
            P.dma(wst[0:parts, 0:nk, 0:w_], v[:, :, c0:c0 + w_], writes=['wst'])
            P.op('act', lambda e: e.copy(out=dst[0:parts, 0:nk, c0:c0 + w_], in_=wst[0:parts, 0:nk, 0:w_]),
                 reads=['wst'], writes=[dkey])

    def proj(c0, M, t0, n, evac):
        pz, pk = nps()
        for kc in range(8):
            P.op('pe', lambda e: e.matmul(pz[0:M, 0:n], wb0.tile[:, kc, c0:c0 + M], hT[:, kc, t0:t0 + n],
                                          start=(kc == 0), stop=(kc == 7)),
                 reads=[wb0.key] + hT_keys(t0, t0 + n), writes=[pk])
        evac(pz, pk)

    def outproj(layer, groups, xsrc, skey, xdst, dkey, off, T, cidx, final=False):
        for tt in range(T // 128):
            i = tt % 2
            P.dma(xt[i], xsrc[off + tt * 128: off + (tt + 1) * 128, :], reads=[(skey, off // 128 + tt)], writes=['xt%d' % i])
            for half in range(2):
                pz, pk = nps()
                for gi, (K, slot) in enumerate(groups):
                    P.op('pe', lambda e: e.matmul(pz[:, 0:512], yT[0:K, slot, tt * 128:(tt + 1) * 128],
                                                  wo_v[slot // 4][0:K, slot % 4, half * 512:(half + 1) * 512],
                                                  start=(gi == 0), stop=(gi == len(groups) - 1)),
                         reads=[('yT', slot), 'wo_bf'], writes=[pk])
                P.op('dve', lambda e: e.tensor_tensor(out=xn[:, half * 512:(half + 1) * 512], in0=pz[:, 0:512],
                                                      in1=gate_bc[:, half * 512:(half + 1) * 512], op=ALU.mult),
                     reads=[pk, 'gate_bc'], writes=['xn'])
                P.op('dve', lambda e: e.tensor_tensor(out=xt[i][:, half * 512:(half + 1) * 512],
                                                      in0=xt[i][:, half * 512:(half + 1) * 512],
                                                      in1=xn[:, half * 512:(half + 1) * 512], op=ALU.add),
                     reads=['xn', 'xt%d' % i], writes=['xt%d' % i])
            if final:
                P.op('act', lambda e: e.activation(out=junk, in_=xt[i], func=AF.Square, accum_out=st4[:, 0:1]),
                     reads=['xt%d' % i], writes=['junk', 'st4'])
                P.op('dve', lambda e: e.tensor_scalar(out=st4[:, 1:2], in0=st4[:, 0:1], scalar1=1.0 / D, scalar2=1e-6,
                                                      op0=ALU.mult, op1=ALU.add), reads=['st4'], writes=['st4'])
                P.op('act', lambda e: e.activation(out=st4[:, 2:3], in_=st4[:, 1:2], func=AF.Sqrt), reads=['st4'], writes=['st4'])
                P.op('dve', lambda e: e.reciprocal(out=st4[:, 3:4], in_=st4[:, 2:3]), reads=['st4'], writes=['st4'])
                P.op('dve', lambda e: e.scalar_tensor_tensor(out=xt[i], in0=xt[i], scalar=st4[:, 3:4], in1=fng_holder['t'][:],
                                                             op0=ALU.mult, op1=ALU.mult),
                     reads=['xt%d' % i, 'st4', 'fng_bc'], writes=['xt%d' % i])
            P.dma(xdst[off + tt * 128: off + (tt + 1) * 128, :], xt[i], reads=['xt%d' % i], writes=[(dkey, off // 128 + tt)], q='pool')

    L0 = contextlib.ExitStack()

    def sb0(name, shape, dt=F32):
        return L0.enter_context(nc.sbuf_tensor(name, list(shape), dt))

    wb0 = WB([sb0("wbfA", [128, 8, 384], BF16), sb0("wbfB", [128, 8, 384], BF16)], ['wbfA', 'wbfB'])
    TB = 256
    Fq, Fsz, Fvr, For = FT[0][:, 0:TS], FT[1][:, 0:TS], FT[2][:, 0:TS], FT[3][:, 0:TS]
    Fv = Fvr.rearrange("p (j c) -> p j c", c=128)
    Fo = For.rearrange("p (j c) -> p j c", c=128)
    BT = [sb0("BT%d" % i, [128, 256]) for i in range(18)]
    bt = {n_: BT[i] for i, n_ in enumerate(['sg', 'lf', 'kg', 'G', 'br', 'E', 'Ei', 'qt', 'kt', 'kh', 'vT'])}
    khtok = sb0("khtok", [128, TB // 128, 128])
    gam = sb0("gam", [128, TB // 32])
    gref = sb0("gref", [128, TB // 32])
    NS_ = 3 if debug else 5
    Sst = [sb0("Sst%d" % i, [128, 128]) for i in range(NS_)]
    attT = sb0("attT", [128, 128])
    ostat = sb0("ostat", [128, TS // 128, 4])
    mH = sb0("mH", [128, 2, 128])
    P.dma(mH[:], maskH.rearrange("d s t -> s d t"), writes=['mH'])

    def hgrn_head(h, off, T, is_sample, pidx, nxt=None):
        ld_qvz = lambda hh: (lambda dst, dk: load_w(wA[hh][:, 0:384], 384, dst=dst, dkey=dk))
        ld_ffb = lambda hh: (lambda dst, dk: load_w(wA[hh][:, 384:640], 256, dst=dst, dkey=dk))
        wb0.use(('qvz', off, h), ld_qvz(h))
        wb0.prefetch(('ffb', off, h), ld_ffb(h))
        tb = min(TB, T)
        nblk = T // tb
        for b in range(nblk):
            t0 = b * tb
            proj(0, 128, t0, tb, lambda pz, pk: P.op(
                'act', lambda e: e.copy(out=Fq[:, t0:t0 + tb], in_=pz[:, 0:tb]), reads=[pk], writes=['Fq']))
            proj(256, 128, t0, tb, lambda pz, pk: P.op(
                'act', lambda e: e.activation(out=Fsz[:, t0:t0 + tb], in_=pz[:, 0:tb], func=AF.Silu), reads=[pk], writes=['Fsz']))
            proj(128, 128, t0, tb, lambda pz, pk: P.op(
                'dve', lambda e: e.tensor_copy(out=bt['vT'][:, 0:tb], in_=pz[:, 0:tb]), reads=[pk], writes=['b_vT']))
            pz, pk = nps()
            for j in range(tb // 128):
                P.op('pe', lambda e: e.transpose(out=pz[:, j * 128:(j + 1) * 128], in_=bt['vT'][:, j * 128:(j + 1) * 128],
                                                 identity=ident[:]), reads=['b_vT', 'ident'], writes=[pk])
            P.op('dve', lambda e: e.tensor_copy(out=Fv[:, t0 // 128:(t0 + tb) // 128, :],
                                                in_=pz[:, 0:tb].rearrange("p (j c) -> p j c", c=128)),
                 reads=[pk], writes=['Fv'])
        wb0.use(('ffb', off, h), ld_ffb(h))
        if nxt is not None:
            wb0.prefetch(('qvz', off, nxt), ld_qvz(nxt))
        for d in range(2):
            rev = (d == 1)
            cur = 0
            if is_sample:
                P.dma(Sst[0][:], s_hgrn[d, h], writes=['Sst0'])
            else:
                P.op('pool', lambda e: e.memset(Sst[0][:], 0.0), writes=['Sst0'])
            blks = list(range(nblk))
            if rev:
                blks = blks[::-1]
            for b in blks:
                t0 = b * tb
                nch = tb // 32
                sg, lf, kg, G, br, E, Ei, qt, kt, kh = [bt[n_] for n_ in ['sg', 'lf', 'kg', 'G', 'br', 'E', 'Ei', 'qt', 'kt', 'kh']]
                proj(128 * d, 128, t0, tb, lambda pz, pk: P.op(
                    'act', lambda e: e.activation(out=sg[:, 0:tb], in_=pz[:, 0:tb], func=AF.Sigmoid), reads=[pk], writes=['b_sg']))
                P.op('dve', lambda e: e.tensor_scalar(out=sg[:, 0:tb], in0=sg[:, 0:tb], scalar1=oml[:, h:h + 1],
                                                      scalar2=lbv[:, h:h + 1], op0=ALU.mult, op1=ALU.add),
                     reads=['b_sg', 'oml', 'lbv'], writes=['b_sg'])
                P.op('act', lambda e: e.activation(out=lf[:, 0:tb], in_=sg[:, 0:tb], func=AF.Ln), reads=['b_sg'], writes=['b_lf'])
                P.op('dve', lambda e: e.tensor_scalar(out=kg[:, 0:tb], in0=sg[:, 0:tb], scalar1=-1.0, scalar2=1.0,
                                                       op0=ALU.mult, op1=ALU.add), reads=['b_sg'], writes=['b_kg'])
                P.op('dve', lambda e: e.memset(E[:, 0:tb], 0.0), writes=['b_E'])
                if not rev:
                    P.op('dve', lambda e: e.tensor_tensor_scan(out=G[:, 0:tb], data0=lf[:, 0:tb], data1=E[:, 0:tb],
                                                               initial=0.0, op0=ALU.add, op1=ALU.add),
                         reads=['b_lf', 'b_E'], writes=['b_G'])
                    ci_ = 0
                else:
                    P.op('dve', lambda e: e.tensor_tensor_scan(out=G[:, 0:tb][:, ::-1], data0=lf[:, 0:tb][:, ::-1],
                                                               data1=E[:, 0:tb], initial=0.0, op0=ALU.add, op1=ALU.add),
                         reads=['b_lf', 'b_E'], writes=['b_G'])
                    ci_ = 31
                G3 = G[:, 0:tb].rearrange("p (c l) -> p c l", l=32)
                lf3 = lf[:, 0:tb].rearrange("p (c l) -> p c l", l=32)
                P.op('dve', lambda e: e.tensor_tensor(out=gref[:, 0:nch], in0=G3[:, :, ci_], in1=lf3[:, :, ci_], op=ALU.subtract),
                     reads=['b_G', 'b_lf'], writes=['gref'])
                P.op('dve', lambda e: e.tensor_tensor(out=br[:, 0:tb].rearrange("p (c l) -> p c l", l=32), in0=G3,
                                                      in1=gref[:, 0:nch].unsqueeze(2).to_broadcast([128, nch, 32]), op=ALU.subtract),
                     reads=['b_G', 'gref'], writes=['b_br'])
                bend = br[:, 0:tb].rearrange("p (c l) -> p c l", l=32)[:, :, (0 if rev else 31)]
                P.op('act', lambda e: e.activation(out=gam[:, 0:nch], in_=bend, func=AF.Exp), reads=['b_br'], writes=['gam'])
                P.op('act', lambda e: e.activation(out=E[:, 0:tb], in_=br[:, 0:tb], func=AF.Exp), reads=['b_br'], writes=['b_E'])
                P.op('act', lambda e: e.activation(out=Ei[:, 0:tb], in_=br[:, 0:tb], func=AF.Exp, scale=-1.0),
                     reads=['b_br'], writes=['b_Ei'])
                P.op('dve', lambda e: e.tensor_tensor(out=qt[:, 0:tb], in0=Fq[:, t0:t0 + tb], in1=E[:, 0:tb], op=ALU.mult),
                     reads=['Fq', 'b_E'], writes=['b_qt'])
                P.op('dve', lambda e: e.tensor_tensor(out=kt[:, 0:tb], in0=kg[:, 0:tb], in1=Ei[:, 0:tb], op=ALU.mult),
                     reads=['b_kg', 'b_Ei'], writes=['b_kt'])
                P.op('dve', lambda e: e.tensor_tensor(out=kh[:, 0:tb].rearrange("p (c l) -> p c l", l=32),
                                                      in0=kt[:, 0:tb].rearrange("p (c l) -> p c l", l=32),
                                                      in1=gam[:, 0:nch].unsqueeze(2).to_broadcast([128, nch, 32]), op=ALU.mult),
                     reads=['b_kt', 'gam'], writes=['b_kh'])
                if debug and debug.get('inner') and h == head_ids[0]:
                    for n_ in ['lf', 'kg', 'br', 'E', 'qt', 'kt', 'kh']:
                        dump("%s_d%d" % (n_, d), bt[n_][:, 0:tb], ['b_' + n_], tb, col0=off + t0)
                pz, pk = nps()
                for j in range(tb // 128):
                    P.op('pe', lambda e: e.transpose(out=pz[:, j * 128:(j + 1) * 128], in_=kh[:, j * 128:(j + 1) * 128],
                                                     identity=ident[:]), reads=['b_kh', 'ident'], writes=[pk])
                P.op('act', lambda e: e.copy(out=khtok[:, 0:tb // 128, :], in_=pz[:, 0:tb].rearrange("p (j c) -> p j c", c=128)),
                     reads=[pk], writes=['khtok'])
                tiles = list(range(tb // 128))
                if rev:
                    tiles = tiles[::-1]
                for j in tiles:
                    tg = t0 // 128 + j
                    pa, pak = nps()
                    P.op('pe', lambda e: e.matmul(pa[:, 0:128], kt[:, j * 128:(j + 1) * 128], qt[:, j * 128:(j + 1) * 128],
                                                  start=True, stop=True), reads=['b_kt', 'b_qt'], writes=[pak])
                    P.op('dve', lambda e: e.tensor_tensor(out=attT[:], in0=pa[:, 0:128], in1=mH[:, d, :], op=ALU.mult),
                         reads=[pak, 'mH'], writes=['attT'])
                    po, pok = nps()
                    P.op('pe', lambda e: e.matmul(po[:, 0:128], attT[:], Fv[:, tg, :], start=True, stop=False),
                         reads=['attT', 'Fv'], writes=[pok])
                    chs = [0, 1, 2, 3]
                    if rev:
                        chs = chs[::-1]
                    pds = []
                    for c in chs:
                        pd, pdk = nps()
                        P.op('pe', lambda e: e.matmul(
                            pd[:, 0:128], khtok[32 * c:32 * c + 32, j, :], Fv[32 * c:32 * c + 32, tg, :],
                            start=True, stop=True, tile_position=(32 * c, 0)),
                            reads=['khtok', 'Fv'], writes=[pdk])
                        pds.append((pd, pdk))
                    for ci, c in enumerate(chs):
                        nxt_ = (cur + 1) % NS_
                        Scur = Sst[cur]
                        Snew = Sst[nxt_]
                        pd, pdk = pds[ci]
                        P.op('pe', lambda e: e.matmul(
                            po[32 * c:32 * c + 32, 0:128], qt[:, j * 128 + 32 * c: j * 128 + 32 * c + 32], Scur[:],
                            start=False, stop=(ci == 3), tile_position=(0, 32 * c)),
                            reads=['b_qt', 'Sst%d' % cur], writes=[pok])
                        gidx = j * 4 + c
                        P.op('dve', lambda e: e.scalar_tensor_tensor(
                            out=Snew[:], in0=Scur[:], scalar=gam[:, gidx:gidx + 1], in1=pd[:, 0:128],
                            op0=ALU.mult, op1=ALU.add),
                            reads=['Sst%d' % cur, 'gam', pdk], writes=['Sst%d' % nxt_])
                        cur = nxt_
                    if d == 0:
                        P.op('act', lambda e: e.copy(out=Fo[:, tg, :], in_=po[:, 0:128]), reads=[pok], writes=[('Fo', tg)])
                    else:
                        P.op('dve', lambda e: e.tensor_tensor(out=Fo[:, tg, :], in0=Fo[:, tg, :], in1=po[:, 0:128], op=ALU.add),
                             reads=[pok, ('Fo', tg)], writes=[('Fo', tg)])
            if not is_sample:
                P.dma(o_hgrn[pidx, d, h], Sst[cur][:], reads=['Sst%d' % cur], q='pool')
        for tg in range(T // 128):
            P.op('act', lambda e: e.activation(out=attT[:], in_=Fo[:, tg, :], func=AF.Square, accum_out=ostat[:, tg, 0:1]),
                 reads=[('Fo', tg)], writes=['attT', ('ostat', tg)])
            P.op('dve', lambda e: e.tensor_scalar(out=ostat[:, tg, 1:2], in0=ostat[:, tg, 0:1], scalar1=1.0 / 128,
                                                  scalar2=1e-6, op0=ALU.mult, op1=ALU.add),
                 reads=[('ostat', tg)], writes=[('ostat', tg)])
            P.op('act', lambda e: e.activation(out=ostat[:, tg, 2:3], in_=ostat[:, tg, 1:2], func=AF.Sqrt),
                 reads=[('ostat', tg)], writes=[('ostat', tg)])
            P.op('dve', lambda e: e.reciprocal(out=ostat[:, tg, 3:4], in_=ostat[:, tg, 2:3]),
                 reads=[('ostat', tg)], writes=[('ostat', tg)])
            P.op('dve', lambda e: e.tensor_scalar(out=Fo[:, tg, :], in0=Fo[:, tg, :], scalar1=ostat[:, tg, 3:4],
                                                  scalar2=None, op0=ALU.mult),
                 reads=[('Fo', tg), ('ostat', tg)], writes=[('Fo', tg)])
        n4 = min(4, T // 128)
        for g4 in range(T // (128 * n4)):
            pz, pk = nps()
            for j in range(n4):
                tg = g4 * n4 + j
                P.op('pe', lambda e: e.transpose(out=pz[:, j * 128:(j + 1) * 128], in_=Fo[:, tg, :], identity=ident[:]),
                     reads=[('Fo', tg), 'ident'], writes=[pk])
            w_ = n4 * 128
            P.op('dve', lambda e: e.scalar_tensor_tensor(
                out=yT[:, h, g4 * w_:(g4 + 1) * w_], in0=pz[:, 0:w_], scalar=hgg_sb[:, h:h + 1],
                in1=Fsz[:, g4 * w_:(g4 + 1) * w_], op0=ALU.mult, op1=ALU.mult),
                reads=[pk, 'hgg', 'Fsz'], writes=[('yT', h)])

    TR = 256
    LWC = -0.6065306597126334
    LR = [sb0("LR%d" % g, [64, TS], BF16) for g in range(4)]
    rb = {n_: BT[i][0:64, :] for i, n_ in enumerate(
          ['lw', 'a', 'kk', 'kq', 'kap', 'kd', 'b', 'rk', 'G', 'br', 'E', 'Ei', 'Em', 'bh', 'kh', 'Kb', 'Bb', 't1'])}
    KR = sb0("r_KR", [64, 2, TR])
    cset = [{n_: sb0("c%d_%s" % (i_, n_), [64, (128 if n_ in ('AB', 'BB') else 64)],
                     (BF16 if n_ in ('XTa', 'XTb', 'Xa', 'Xb', 'Pm0', 'Pm1') else F32))
             for n_ in ['AB', 'BB', 'XT0', 'XTa', 'XTb', 'Xa', 'Xb', 'Pm0', 'Pm1', 'Vt', 'Kt', 'Bt']} for i_ in range(4)]
    rsq = {n_: sb0("rq_" + n_, [64, 64]) for n_ in ['U', 'Z0', 'Z1', 'zt']}
    rsq['Wb'] = sb0("rq_Wb", [64, 64], BF16)
    rgam = sb0("rgam", [64, 8])
    rgref = sb0("rgref", [64, 4])
    mR = sb0("mR", [64, 2, 3, 128])
    P.dma(mR[:], maskR.rearrange("d m s t -> s d m t"), writes=['mR'])
    prm = {}
    for n_, src_, shp in [('mu_rkv', mu_rkv, [64, 2, 4, 16]), ('mu_lr', mu_lr, [64, 2, 4]), ('w0', w0T, [64, 2, 16]),
                          ('a0', a0T, [64, 2, 16]), ('kk', kkT, [64, 16]), ('ka', kaT, [64, 16]), ('rk', rkT, [64, 16]),
                          ('gng', gngT, [64, 16]), ('gnb', gnbT, [64, 16])]:
        prm[n_] = sb0("p_" + n_, shp)
        P.dma(prm[n_][:], src_[:], writes=['p_' + n_])
    c0_rkv = sb0("c0_rkv", [64, 4, 16])
    c0_lr = sb0("c0_lr", [64, 4])
    omka = sb0("omka", [64, 16])
    P.op('dve', lambda e: e.tensor_tensor(out=c0_rkv[:], in0=prm['mu_rkv'][:, 0], in1=prm['mu_rkv'][:, 1], op=ALU.add),
         reads=['p_mu_rkv'], writes=['c0_rkv'])
    P.op('dve', lambda e: e.tensor_scalar(out=c0_rkv[:], in0=c0_rkv[:], scalar1=-1.0, scalar2=1.0, op0=ALU.mult, op1=ALU.add),
         reads=['c0_rkv'], writes=['c0_rkv'])
    P.op('dve', lambda e: e.tensor_tensor(out=c0_lr[:], in0=prm['mu_lr'][:, 0], in1=prm['mu_lr'][:, 1], op=ALU.add),
         reads=['p_mu_lr'], writes=['c0_lr'])
    P.op('dve', lambda e: e.tensor_scalar(out=c0_lr[:], in0=c0_lr[:], scalar1=-1.0, scalar2=1.0, op0=ALU.mult, op1=ALU.add),
         reads=['c0_lr'], writes=['c0_lr'])
    P.op('dve', lambda e: e.tensor_scalar(out=omka[:], in0=prm['ka'][:], scalar1=-1.0, scalar2=1.0, op0=ALU.mult, op1=ALU.add),
         reads=['p_ka'], writes=['omka'])
    w2a2 = sb0("w2a2", [64, 4, D], BF16)
    for g, src_ in enumerate([w2[0], w2[1], a2[0], a2[1]]):
        for c0 in range(0, D, 256):
            P.dma(wst[0:64, 0, 0:256], src_[:, c0:c0 + 256], writes=['wst'])
            P.op('pool', lambda e: e.tensor_copy(out=w2a2[:, g, c0:c0 + 256], in_=wst[0:64, 0, 0:256]), reads=['wst'], writes=['w2a2'])

    def shift_into(dst, dkey, raw, rkey, T, c0ap, m0ap, m1ap, t1tile, eng='dve'):
        for s0 in range(0, T, 512):
            n = min(512, T - s0)
            P.op('dve', lambda e: e.tensor_scalar(out=t1tile[:, 0:n], in0=raw[:, 16 + s0:16 + s0 + n], scalar1=c0ap, scalar2=None,
                                                op0=ALU.mult), reads=[rkey], writes=['shift_t'])
            P.op('dve', lambda e: e.scalar_tensor_tensor(out=t1tile[:, 0:n], in0=raw[:, 15 + s0:15 + s0 + n], scalar=m0ap,
                                                       in1=t1tile[:, 0:n], op0=ALU.mult, op1=ALU.add),
                 reads=[rkey, 'shift_t'], writes=['shift_t'])
            P.op('dve', lambda e: e.scalar_tensor_tensor(out=dst[:, s0:s0 + n], in0=raw[:, 17 + s0:17 + s0 + n], scalar=m1ap,
                                                       in1=t1tile[:, 0:n], op0=ALU.mult, op1=ALU.add),
                 reads=[rkey, 'shift_t'], writes=[dkey])

    shiftt = sb0("shiftt", [64, 512])

    def rwkv_seq_setup(off, T):
        wb0.use(('lr', off), lambda dst, dk: load_w(wLR, 256, dst=dst, dkey=dk))
        pb = min(512, T)
        for g in range(4):
            raw = FT[g]
            P.op('pool', lambda e: e.memset(raw[0:64, 15:16], 0.0), writes=['FT%d' % g])
            P.op('pool', lambda e: e.memset(raw[0:64, T + 16:T + 17], 0.0), writes=['FT%d' % g])
            for b in range(T // pb):
                t0 = b * pb
                proj(64 * g, 64, t0, pb, lambda pz, pk: P.op(
                    'act', lambda e: e.copy(out=raw[0:64, 16 + t0:16 + t0 + pb], in_=pz[0:64, 0:pb]), reads=[pk], writes=['FT%d' % g]))
            shift_into(FT[4][0:64, :], 'FT4', raw[0:64, :], 'FT%d' % g, T, c0_lr[:, g:g + 1], prm['mu_lr'][:, 0, g:g + 1],
                       prm['mu_lr'][:, 1, g:g + 1], shiftt)
            if g < 2:
                P.op('act', lambda e: e.activation(out=LR[g][:, 0:T], in_=FT[4][0:64, 0:T], func=AF.Tanh), reads=['FT4'], writes=['LR%d' % g])
            else:
                P.op('act', lambda e: e.copy(out=LR[g][:, 0:T], in_=FT[4][0:64, 0:T]), reads=['FT4'], writes=['LR%d' % g])

    def rwkv_head(h, slot, off, T, is_sample, pidx, nxt=None):
        P.barrier()
        ld_b = lambda hh: (lambda dst, dk: load_w(wB[hh], 256, dst=dst, dkey=dk))
        wb0.use(('wB', off, h), ld_b(h))
        pb = min(512, T)
        nchT = T // 64
        for g in range(3):
            raw = FT[g]
            P.op('pool', lambda e: e.memset(raw[0:64, 15:16], 0.0), writes=['FT%d' % g])
            P.op('pool', lambda e: e.memset(raw[0:64, T + 16:T + 17], 0.0), writes=['FT%d' % g])
            for b in range(T // pb):
                t0 = b * pb
                proj(64 * g, 64, t0, pb, lambda pz, pk: P.op(
                    'act', lambda e: e.copy(out=raw[0:64, 16 + t0:16 + t0 + pb], in_=pz[0:64, 0:pb]), reads=[pk], writes=['FT%d' % g]))
            shift_into(FT[3 + g][0:64, :], 'FT%d' % (3 + g), raw[0:64, :], 'FT%d' % g, T, c0_rkv[:, g, h:h + 1],
                       prm['mu_rkv'][:, 0, g, h:h + 1], prm['mu_rkv'][:, 1, g, h:h + 1], shiftt, eng=('dve' if g != 1 else 'pool'))
        rS, kS, vS = FT[3][0:64, :], FT[4][0:64, :], FT[5][0:64, :]
        szb, yaccr, bonus = FT[0][0:64, :], FT[1][0:64, 0:T], FT[2][0:64, :]
        yacc = yaccr.rearrange("p (c v) -> p c v", v=64)
        for b in range(T // pb):
            t0 = b * pb
            proj(192, 64, t0, pb, lambda pz, pk: P.op(
                'act', lambda e: e.activation(out=szb[:, t0:t0 + pb], in_=pz[0:64, 0:pb], func=AF.Silu), reads=[pk], writes=['FT0']))
        tb = min(TR, T)
        nblk = T // tb
        if nxt is not None:
            wb0.prefetch(('wB', off, nxt), ld_b(nxt))
        P.barrier()
        for d in range(2):
            rev = (d == 1)
            cur = 0
            Zt = [rsq['Z0'], rsq['Z1']]
            if is_sample:
                P.dma(rsq['zt'][:], s_rwkv[d, h], writes=['rq_zt'])
                pz, pk = nps()
                P.op('pe', lambda e: e.transpose(out=pz[0:64, 0:64], in_=rsq['zt'][:], identity=ident[0:64, 0:64]),
                     reads=['rq_zt', 'ident'], writes=[pk])
                P.op('act', lambda e: e.copy(out=Zt[0][:], in_=pz[0:64, 0:64]), reads=[pk], writes=['rq_Z0'])
            else:
                P.op('pool', lambda e: e.memset(Zt[0][:], 0.0), writes=['rq_Z0'])
            blks = list(range(nblk))
            if rev:
                blks = blks[::-1]
            for b in blks:
                t0 = b * tb
                sl = slice(t0, t0 + tb)
                nch = tb // 64
                R_ = rb
                pz, pk = nps()
                P.op('pe', lambda e: e.matmul(pz[0:64, 0:tb], w2a2[:, d, h * 64:(h + 1) * 64], LR[d][:, sl], start=True, stop=True),
                     reads=['w2a2', 'LR%d' % d], writes=[pk])
                P.op('act', lambda e: e.activation(out=R_['lw'][:, 0:tb], in_=pz[0:64, 0:tb], func=AF.Sigmoid,
                                                   bias=prm['w0'][:, d, h:h + 1], scale=1.0), reads=[pk, 'p_w0'], writes=['r_lw'])
                pz, pk = nps()
                P.op('pe', lambda e: e.matmul(pz[0:64, 0:tb], w2a2[:, 2 + d, h * 64:(h + 1) * 64], LR[2 + d][:, sl], start=True, stop=True),
                     reads=['w2a2', 'LR%d' % (2 + d)], writes=[pk])
                P.op('act', lambda e: e.activation(out=R_['a'][:, 0:tb], in_=pz[0:64, 0:tb], func=AF.Sigmoid,
                                                   bias=prm['a0'][:, d, h:h + 1], scale=1.0), reads=[pk, 'p_a0'], writes=['r_a'])
                P.op('dve', lambda e: e.tensor_scalar(out=R_['kk'][:, 0:tb], in0=kS[:, sl], scalar1=prm['kk'][:, h:h + 1],
                                                      scalar2=None, op0=ALU.mult), reads=['FT4', 'p_kk'], writes=['r_kk'])
                P.op('dve', lambda e: e.tensor_tensor(out=R_['kq'][:, 0:tb], in0=R_['kk'][:, 0:tb], in1=R_['kk'][:, 0:tb], op=ALU.mult),
                     reads=['r_kk'], writes=['r_kq'])
                pz, pk = nps()
                P.op('pe', lambda e: e.matmul(pz[0:64, 0:tb], ones[0:64, 0:64], R_['kq'][:, 0:tb], start=True, stop=True),
                     reads=['ones', 'r_kq'], writes=[pk])
                P.op('dve', lambda e: e.tensor_scalar(out=R_['kq'][:, 0:tb], in0=pz[0:64, 0:tb], scalar1=1e-24, scalar2=None,
                                                      op0=ALU.max), reads=[pk], writes=['r_kq'])
                P.op('act', lambda e: e.activation(out=R_['kq'][:, 0:tb], in_=R_['kq'][:, 0:tb], func=AF.Ln), reads=['r_kq'], writes=['r_kq'])
                P.op('act', lambda e: e.activation(out=R_['kq'][:, 0:tb], in_=R_['kq'][:, 0:tb], func=AF.Exp, scale=-0.5),
                     reads=['r_kq'], writes=['r_kq'])
                P.op('dve', lambda e: e.tensor_tensor(out=R_['kap'][:, 0:tb], in0=R_['kk'][:, 0:tb], in1=R_['kq'][:, 0:tb], op=ALU.mult),
                     reads=['r_kk', 'r_kq'], writes=['r_kap'])
                P.op('dve', lambda e: e.tensor_scalar(out=R_['t1'][:, 0:tb], in0=R_['a'][:, 0:tb], scalar1=prm['ka'][:, h:h + 1],
                                                       scalar2=omka[:, h:h + 1], op0=ALU.mult, op1=ALU.add),
                     reads=['r_a', 'p_ka', 'omka'], writes=['r_t1'])
                P.op('dve', lambda e: e.tensor_tensor(out=R_['kd'][:, 0:tb], in0=kS[:, sl], in1=R_['t1'][:, 0:tb], op=ALU.mult),
                     reads=['FT4', 'r_t1'], writes=['r_kd'])
                P.op('dve', lambda e: e.tensor_tensor(out=R_['b'][:, 0:tb], in0=R_['a'][:, 0:tb], in1=R_['kap'][:, 0:tb], op=ALU.mult),
                     reads=['r_a', 'r_kap'], writes=['r_b'])
                P.op('dve', lambda e: e.scalar_tensor_tensor(out=R_['rk'][:, 0:tb], in0=rS[:, sl], scalar=prm['rk'][:, h:h + 1],
                                                             in1=R_['kd'][:, 0:tb], op0=ALU.mult, op1=ALU.mult),
                     reads=['FT3', 'p_rk', 'r_kd'], writes=['r_rk'])
                pz, pk = nps()
                P.op('pe', lambda e: e.matmul(pz[0:64, 0:tb], ones[0:64, 0:64], R_['rk'][:, 0:tb], start=True, stop=True),
                     reads=['ones', 'r_rk'], writes=[pk])
                if d == 0:
                    P.op('dve', lambda e: e.tensor_tensor(out=bonus[:, sl], in0=pz[0:64, 0:tb], in1=vS[:, sl], op=ALU.mult),
                         reads=[pk, 'FT5'], writes=[('bonus', b)])
                else:
                    P.op('dve', lambda e: e.tensor_tensor(out=R_['rk'][:, 0:tb], in0=pz[0:64, 0:tb], in1=vS[:, sl], op=ALU.mult),
                         reads=[pk, 'FT5'], writes=['r_rk'])
                    P.op('dve', lambda e: e.tensor_tensor(out=bonus[:, sl], in0=bonus[:, sl], in1=R_['rk'][:, 0:tb], op=ALU.add),
                         reads=['r_rk', ('bonus', b)], writes=[('bonus', b)])
                G, br, E, Ei, Em = R_['G'], R_['br'], R_['E'], R_['Ei'], R_['Em']
                P.op('dve', lambda e: e.memset(E[:, 0:tb], 0.0), writes=['r_E'])
                if not rev:
                    P.op('dve', lambda e: e.tensor_tensor_scan(out=G[:, 0:tb], data0=R_['lw'][:, 0:tb], data1=E[:, 0:tb],
                                                               initial=0.0, op0=ALU.add, op1=ALU.add),
                         reads=['r_lw', 'r_E'], writes=['r_G'])
                    ci_ = 0
                else:
                    P.op('dve', lambda e: e.tensor_tensor_scan(out=G[:, 0:tb][:, ::-1], data0=R_['lw'][:, 0:tb][:, ::-1],
                                                               data1=E[:, 0:tb], initial=0.0, op0=ALU.add, op1=ALU.add),
                         reads=['r_lw', 'r_E'], writes=['r_G'])
                    ci_ = 63
                G3 = G[:, 0:tb].rearrange("p (c l) -> p c l", l=64)
                lw3 = R_['lw'][:, 0:tb].rearrange("p (c l) -> p c l", l=64)
                P.op('dve', lambda e: e.tensor_tensor(out=rgref[:, 0:nch], in0=G3[:, :, ci_], in1=lw3[:, :, ci_], op=ALU.subtract),
                     reads=['r_G', 'r_lw'], writes=['rgref'])
                P.op('dve', lambda e: e.tensor_tensor(out=br[:, 0:tb].rearrange("p (c l) -> p c l", l=64), in0=G3,
                                                      in1=rgref[:, 0:nch].unsqueeze(2).to_broadcast([64, nch, 64]), op=ALU.subtract),
                     reads=['r_G', 'rgref'], writes=['r_br'])
                bend = br[:, 0:tb].rearrange("p (c l) -> p c l", l=64)[:, :, (0 if rev else 63)]
                P.op('act', lambda e: e.activation(out=rgam[:, 0:nch], in_=bend, func=AF.Exp, scale=LWC), reads=['r_br'], writes=['rgam'])
                P.op('dve', lambda e: e.tensor_scalar(out=rgam[:, 4:4 + nch], in0=rgam[:, 0:nch], scalar1=-1.0, scalar2=None,
                                                       op0=ALU.mult), reads=['rgam'], writes=['rgam'])
                P.op('act', lambda e: e.activation(out=E[:, 0:tb], in_=br[:, 0:tb], func=AF.Exp, scale=LWC), reads=['r_br'], writes=['r_E'])
                P.op('act', lambda e: e.activation(out=Ei[:, 0:tb], in_=br[:, 0:tb], func=AF.Exp, scale=-LWC),
                     reads=['r_br'], writes=['r_Ei'])
                P.op('dve', lambda e: e.tensor_tensor(out=R_['t1'][:, 0:tb], in0=br[:, 0:tb], in1=R_['lw'][:, 0:tb], op=ALU.subtract),
                     reads=['r_br', 'r_lw'], writes=['r_t1'])
                P.op('act', lambda e: e.activation(out=Em[:, 0:tb], in_=R_['t1'][:, 0:tb], func=AF.Exp, scale=LWC), reads=['r_t1'], writes=['r_Em'])
                P.op('dve', lambda e: e.tensor_tensor(out=KR[:, 0, 0:tb], in0=R_['kap'][:, 0:tb], in1=Em[:, 0:tb], op=ALU.mult),
                     reads=['r_kap', 'r_Em'], writes=['r_KR'])
                P.op('dve', lambda e: e.tensor_tensor(out=KR[:, 1, 0:tb], in0=rS[:, sl], in1=E[:, 0:tb], op=ALU.mult),
                     reads=['FT3', 'r_E', 'r_KR'], writes=['r_KR'])
                P.op('dve', lambda e: e.tensor_tensor(out=R_['bh'][:, 0:tb], in0=R_['b'][:, 0:tb], in1=Ei[:, 0:tb], op=ALU.mult),
                     reads=['r_b', 'r_Ei'], writes=['r_bh'])
                P.op('dve', lambda e: e.tensor_tensor(out=R_['kh'][:, 0:tb], in0=R_['kd'][:, 0:tb], in1=Ei[:, 0:tb], op=ALU.mult),
                     reads=['r_kd', 'r_Ei'], writes=['r_kh'])
                P.op('dve', lambda e: e.tensor_tensor(out=R_['Kb'][:, 0:tb].rearrange("p (c l) -> p c l", l=64),
                                                      in0=R_['kh'][:, 0:tb].rearrange("p (c l) -> p c l", l=64),
                                                      in1=rgam[:, 0:nch].unsqueeze(2).to_broadcast([64, nch, 64]), op=ALU.mult),
                     reads=['r_kh', 'rgam'], writes=['r_Kb'])
                P.op('dve', lambda e: e.tensor_tensor(out=R_['Bb'][:, 0:tb].rearrange("p (c l) -> p c l", l=64),
                                                      in0=R_['bh'][:, 0:tb].rearrange("p (c l) -> p c l", l=64),
                                                      in1=rgam[:, 4:4 + nch].unsqueeze(2).to_broadcast([64, nch, 64]), op=ALU.mult),
                     reads=['r_bh', 'rgam'], writes=['r_Bb'])
                chs = list(range(nch))
                if rev:
                    chs = chs[::-1]
                st = {}
                for c in chs:
                    cs = slice(c * 64, (c + 1) * 64)
                    C_ = cset[c]
                    ck = (lambda n_, c=c: 'c%d_%s' % (c, n_))
                    pA, pAk = nps()
                    P.op('pe', lambda e: e.matmul(pA[0:64, 0:128], R_['bh'][:, cs], KR[:, :, cs], start=True, stop=True),
                         reads=['r_bh', 'r_KR'], writes=[pAk])
                    P.op('pe', lambda e: e.matmul(pA[0:64, 128:256], R_['kh'][:, cs], KR[:, :, cs], start=True, stop=True),
                         reads=['r_kh', 'r_KR'], writes=[pAk])
                    P.op('pe', lambda e: e.matmul(pA[0:64, 256:320], KR[:, 0, cs], R_['bh'][:, cs], start=True, stop=True),
                         reads=['r_bh', 'r_KR'], writes=[pAk])
                    AB, BB = C_['AB'], C_['BB']
                    P.op('dve', lambda e: e.tensor_tensor(out=AB[:], in0=pA[0:64, 0:128], in1=mR[:, d, 0, :], op=ALU.mult),
                         reads=[pAk, 'mR'], writes=[ck('AB')])
                    P.op('dve', lambda e: e.tensor_tensor(out=BB[:], in0=pA[0:64, 128:256], in1=mR[:, d, 1, :], op=ALU.mult),
                         reads=[pAk, 'mR'], writes=[ck('BB')])
                    P.op('dve', lambda e: e.tensor_tensor(out=C_['XT0'][:], in0=pA[0:64, 256:320], in1=mR[:, d, 2, 0:64], op=ALU.mult),
                         reads=[pAk, 'mR'], writes=[ck('XT0')])
                    P.op('dve', lambda e: e.tensor_tensor(out=C_['Pm0'][:], in0=AB[:, 0:64], in1=ident[0:64, 0:64], op=ALU.add),
                         reads=[ck('AB'), 'ident'], writes=[ck('Pm0')])
                    st[c] = dict(X=AB[:, 0:64], Xk=ck('AB'), XT=C_['XT0'], XTk=ck('XT0'), xti=0, pmi=0)
                for lev in range(5):
                    for c in chs:
                        C_ = cset[c]
                        s_ = st[c]
                        ck = (lambda n_, c=c: 'c%d_%s' % (c, n_))
                        X, Xk, XT, XTk = s_['X'], s_['Xk'], s_['XT'], s_['XTk']
                        pq, pqk = nps()
                        nXTn = 'XTa' if lev % 2 == 0 else 'XTb'
                        nXT, nXTk = C_[nXTn], ck(nXTn)
                        P.op('pe', lambda e: e.matmul(pq[0:64, 64:128], X, XT[:], start=True, stop=True), reads=[Xk, XTk], writes=[pqk])
                        if lev < 4:
                            P.op('pe', lambda e: e.matmul(pq[0:64, 0:64], XT[:], X, start=True, stop=True), reads=[Xk, XTk], writes=[pqk])
                        P.op('act', lambda e: e.copy(out=nXT[:], in_=pq[0:64, 64:128]), reads=[pqk], writes=[nXTk])
                        if lev < 4:
                            tn = 'Xa' if lev % 2 == 0 else 'Xb'
                            P.op('act', lambda e: e.copy(out=C_[tn][:], in_=pq[0:64, 0:64]), reads=[pqk], writes=[ck(tn)])
                            s_['X'], s_['Xk'] = C_[tn][:], ck(tn)
                        s_['XT'], s_['XTk'], s_['xti'] = nXT, nXTk, 1 - s_['xti']
                    for c in chs:
                        C_ = cset[c]
                        s_ = st[c]
                        ck = (lambda n_, c=c: 'c%d_%s' % (c, n_))
                        nXT, nXTk = s_['XT'], s_['XTk']
                        pmi = s_['pmi']
                        Pc, Pn = C_['Pm%d' % pmi], C_['Pm%d' % (1 - pmi)]
                        pp, ppk = nps()
                        P.op('pe', lambda e: e.matmul(pp[0:64, 0:64], nXT[:], Pc[:], start=True, stop=True),
                             reads=[nXTk, ck('Pm%d' % pmi)], writes=[ppk])
                        P.op('dve', lambda e: e.tensor_tensor(out=Pn[:], in0=pp[0:64, 0:64], in1=Pc[:], op=ALU.add),
                             reads=[ppk, ck('Pm%d' % pmi)], writes=[ck('Pm%d' % (1 - pmi))])
                        s_['pmi'] = 1 - pmi
                for c in chs:
                    cs = slice(c * 64, (c + 1) * 64)
                    gsl = slice(t0 + c * 64, t0 + (c + 1) * 64)
                    C_ = cset[c]
                    pt, ptk = nps()
                    P.op('pe', lambda e: e.transpose(out=pt[0:64, 0:64], in_=vS[:, gsl], identity=ident[0:64, 0:64]),
                         reads=['FT5', 'ident'], writes=[ptk])
                    P.op('pe', lambda e: e.transpose(out=pt[0:64, 64:128], in_=R_['Kb'][:, cs], identity=ident[0:64, 0:64]),
                         reads=['r_Kb', 'ident'], writes=[ptk])
                    P.op('pe', lambda e: e.transpose(out=pt[0:64, 128:192], in_=R_['Bb'][:, cs], identity=ident[0:64, 0:64]),
                         reads=['r_Bb', 'ident'], writes=[ptk])
                    P.op('act', lambda e: e.copy(out=C_['Vt'][:], in_=pt[0:64, 0:64]), reads=[ptk], writes=['c%d_Vt' % c])
                    P.op('act', lambda e: e.copy(out=C_['Kt'][:], in_=pt[0:64, 64:128]), reads=[ptk], writes=['c%d_Kt' % c])
                    P.op('act', lambda e: e.copy(out=C_['Bt'][:], in_=pt[0:64, 128:192]), reads=[ptk], writes=['c%d_Bt' % c])
                for c in chs:
                    cs = slice(c * 64, (c + 1) * 64)
                    cg = (t0 // 64) + c
                    C_ = cset[c]
                    s_ = st[c]
                    AB, BB = C_['AB'], C_['BB']
                    ABk, BBk, Vtk, Ktk, Btk = ['c%d_%s' % (c, n_) for n_ in ('AB', 'BB', 'Vt', 'Kt', 'Bt')]
                    Pm, Pmk = C_['Pm%d' % s_['pmi']], 'c%d_Pm%d' % (c, s_['pmi'])
                    Zc, Zn = Zt[cur], Zt[1 - cur]
                    zck, znk = 'rq_Z%d' % cur, 'rq_Z%d' % (1 - cur)
                    pw, pwk = nps()
                    P.op('pe', lambda e: e.matmul(pw[0:64, 0:64], KR[:, 0, cs], Zc[:], start=True, stop=False),
                         reads=['r_KR', zck], writes=[pwk])
                    P.op('pe', lambda e: e.matmul(pw[0:64, 0:64], BB[:, 0:64], C_['Vt'][:], start=False, stop=True),
                         reads=[BBk, Vtk], writes=[pwk])
                    P.op('act', lambda e: e.copy(out=rsq['Wb'][:], in_=pw[0:64, 0:64]), reads=[pwk], writes=['rq_Wb'])
                    pu, puk = nps()
                    P.op('pe', lambda e: e.matmul(pu[0:64, 0:64], Pm[:], rsq['Wb'][:], start=True, stop=True),
                         reads=[Pmk, 'rq_Wb'], writes=[puk])
                    P.op('act', lambda e: e.copy(out=rsq['U'][:], in_=pu[0:64, 0:64]), reads=[puk], writes=['rq_U'])
                    pzz, pzk = nps()
                    P.op('pe', lambda e: e.matmul(pzz[0:64, 0:64], C_['Kt'][:], C_['Vt'][:], start=True, stop=False),
                         reads=[Ktk, Vtk], writes=[pzk])
                    P.op('pe', lambda e: e.matmul(pzz[0:64, 0:64], C_['Bt'][:], rsq['U'][:], start=False, stop=True),
                         reads=[Btk, 'rq_U'], writes=[pzk])
                    P.op('dve', lambda e: e.scalar_tensor_tensor(out=Zn[:], in0=Zc[:], scalar=rgam[:, c:c + 1], in1=pzz[0:64, 0:64],
                                                                 op0=ALU.mult, op1=ALU.add), reads=[zck, 'rgam', pzk], writes=[znk])
                    py, pyk = nps()
                    P.op('pe', lambda e: e.matmul(py[0:64, 0:64], KR[:, 1, cs], Zc[:], start=True, stop=False),
                         reads=['r_KR', zck], writes=[pyk])
                    P.op('pe', lambda e: e.matmul(py[0:64, 0:64], BB[:, 64:128], C_['Vt'][:], start=False, stop=False),
                         reads=[BBk, Vtk], writes=[pyk])
                    P.op('pe', lambda e: e.matmul(py[0:64, 0:64], AB[:, 64:128], rsq['U'][:], start=False, stop=True),
                         reads=[ABk, 'rq_U'], writes=[pyk])
                    if d == 0:
                        P.op('act', lambda e: e.copy(out=yacc[:, cg, :], in_=py[0:64, 0:64]), reads=[pyk], writes=[('yacc', cg)])
                    else:
                        P.op('dve', lambda e: e.tensor_tensor(out=yacc[:, cg, :], in0=yacc[:, cg, :], in1=py[0:64, 0:64], op=ALU.add),
                             reads=[pyk, ('yacc', cg)], writes=[('yacc', cg)])
                    cur = 1 - cur
            if not is_sample:
                pz, pk = nps()
                P.op('pe', lambda e: e.transpose(out=pz[0:64, 0:64], in_=Zt[cur][:], identity=ident[0:64, 0:64]),
                     reads=['rq_Z%d' % cur, 'ident'], writes=[pk])
                P.op('act', lambda e: e.copy(out=rsq['zt'][:], in_=pz[0:64, 0:64]), reads=[pk], writes=['rq_zt'])
                P.dma(o_rwkv[pidx, d, h], rsq['zt'][:], reads=['rq_zt'], q='pool')
        ykeys = [('yacc', c) for c in range(nchT)]
        gst = ostat[0:64, :, :].rearrange("p a b -> p (a b)")
        P.op('dve', lambda e: e.tensor_reduce(out=gst[:, 0:nchT], in_=yacc, axis=AX.X, op=ALU.add), reads=ykeys, writes=['gst'])
        P.op('dve', lambda e: e.tensor_scalar(out=gst[:, 0:nchT], in0=gst[:, 0:nchT], scalar1=-1.0 / 64, scalar2=None, op0=ALU.mult),
             reads=['gst'], writes=['gst'])
        P.op('dve', lambda e: e.tensor_tensor(out=yacc, in0=yacc, in1=gst[:, 0:nchT].unsqueeze(2).to_broadcast([64, nchT, 64]), op=ALU.add),
             reads=ykeys + ['gst'], writes=ykeys)
        sq = FT[3][0:64, 0:T].rearrange("p (c v) -> p c v", v=64)
        P.op('dve', lambda e: e.tensor_tensor(out=sq, in0=yacc, in1=yacc, op=ALU.mult), reads=ykeys, writes=['FT3'])
        P.op('dve', lambda e: e.tensor_reduce(out=gst[:, 32:32 + nchT], in_=sq, axis=AX.X, op=ALU.add), reads=['FT3'], writes=['gst'])
        P.op('dve', lambda e: e.tensor_scalar(out=gst[:, 32:32 + nchT], in0=gst[:, 32:32 + nchT], scalar1=1.0 / 64, scalar2=64e-5,
                                              op0=ALU.mult, op1=ALU.add), reads=['gst'], writes=['gst'])
        P.op('act', lambda e: e.activation(out=gst[:, 32:32 + nchT], in_=gst[:, 32:32 + nchT], func=AF.Sqrt), reads=['gst'], writes=['gst'])
        P.op('dve', lambda e: e.reciprocal(out=gst[:, 32:32 + nchT], in_=gst[:, 32:32 + nchT]), reads=['gst'], writes=['gst'])
        P.op('dve', lambda e: e.tensor_tensor(out=yacc, in0=yacc, in1=gst[:, 32:32 + nchT].unsqueeze(2).to_broadcast([64, nchT, 64]),
                                              op=ALU.mult), reads=ykeys + ['gst'], writes=ykeys)
        n8 = min(8, nchT)
        for g8 in range(nchT // n8):
            pz, pk = nps()
            for j in range(n8):
                cg = g8 * n8 + j
                P.op('pe', lambda e: e.transpose(out=pz[0:64, j * 64:(j + 1) * 64], in_=yacc[:, cg, :], identity=ident[0:64, 0:64]),
                     reads=[('yacc', cg), 'ident'], writes=[pk])
            w_ = n8 * 64
            gs = slice(g8 * w_, (g8 + 1) * w_)
            P.op('dve', lambda e: e.tensor_scalar(out=shiftt[:, 0:w_], in0=pz[0:64, 0:w_], scalar1=prm['gng'][:, h:h + 1],
                                                  scalar2=prm['gnb'][:, h:h + 1], op0=ALU.mult, op1=ALU.add),
                 reads=[pk, 'p_gng', 'p_gnb'], writes=['shift_t'])
            P.op('dve', lambda e: e.tensor_tensor(out=shiftt[:, 0:w_], in0=shiftt[:, 0:w_], in1=bonus[:, gs], op=ALU.add),
                 reads=['shift_t'] + [('bonus', b) for b in range(nblk)], writes=['shift_t'])
            P.op('dve', lambda e: e.tensor_tensor(out=yT[0:64, slot, gs], in0=shiftt[:, 0:w_], in1=szb[:, gs], op=ALU.mult),
                 reads=['shift_t', 'FT0'], writes=[('yT', slot)])

    seq_ids = debug.get('seqs', [0, 1, 2]) if debug else [0, 1, 2]
    head_ids = debug.get('heads', list(range(8))) if debug else list(range(8))
    rheads = debug.get('rheads', list(range(16))) if debug else list(range(16))
    for si in seq_ids:
        off, T, cidx, is_sample = SEQS[si]
        P.barrier()
        make_gate(0, cidx)
        make_hT(0, xin, 'xin', off, T, cidx)
        P.barrier()
        if debug and debug.get('inner'):
            dump("mT", mT[:, 0].rearrange("p a b -> p (a b)"), ['mT'], 48)
            dump("sc1", sc1[:, 0].rearrange("p a b -> p (a b)"), ['sc1'], 16)
            dump("scT", scT[:].rearrange("p a b -> p (a b)"), ['scT'], 16)
            dump("xn", xn, ['xn'], 1024)
            dump("xt0", xt[0], ['xt0'], 1024)
            for kc in range(8):
                dump("hT%d" % kc, hT[:, kc, 0:T], hT_keys(0, T), T, col0=off)
        for hi_, h in enumerate(head_ids):
            hgrn_head(h, off, T, is_sample, si - 1, nxt=(head_ids[hi_ + 1] if hi_ + 1 < len(head_ids) else None))
            if debug and debug.get('dump_y'):
                dump("yT%d" % h, yT[:, h, 0:T], [('yT', h)], T, col0=off)
        P.barrier()
        load_wo(w_out_even[0:D, :], 128)
        outproj(0, [(128, s_) for s_ in range(8)], xin, 'xin', x1, 'x1', off, T, cidx)
        P.barrier()
        rwkv_seq_setup(off, T)
        for half in range(2):
            P.barrier()
            for slot in range(8):
                h = half * 8 + slot
                if h in rheads:
                    nh_ = h + 1 if (slot < 7 and (h + 1) in rheads) else None
                    rwkv_head(h, slot, off, T, is_sample, si - 1, nxt=nh_)
                    if debug and debug.get('dump_y'):
                        dump("yR%d" % h, yT[0:64, slot, 0:T], [('yT', slot)], T, col0=off, parts=64)
            P.barrier()
            load_wo(w_out_even[D + half * 512: D + (half + 1) * 512, :], 64)
            outproj(0, [(64, s_) for s_ in range(8)], x1, 'x1', x1, 'x1', off, T, cidx)
    P.barrier()
    L0.close()

    if not (debug and debug.get('l0only')):
        L1 = contextlib.ExitStack()

        def sb1(name, shape, dt=F32):
            return L1.enter_context(nc.sbuf_tensor(name, list(shape), dt))

        make_fng()
        LC = 128
        DH = 512
        qT = yT[:, 4:8, :]
        kT = sb1("kT", [128, 4, TS], BF16)
        vch = sb1("vch", [128, DH], BF16)
        Cst = sb1("Cst", [128, 4, DH])
        Cbf = sb1("Cbf", [128, 4, DH], BF16)
        nst = sb1("nst", [128, 8])
        nbf = sb1("nbf", [128, 4], BF16)
        ktok = sb1("ktok", [128, DH], BF16)
        vw = sb1("vw", [128, DH], BF16)
        sTs = sb1("sTs", [128, 128], BF16)
        onesb = sb1("onesb", [128, 1], BF16)
        identb = sb1("identb", [128, 128], BF16)
        mC = sb1("mC", [128, 2, 128])
        SEL = sb1("SEL", [36, 4, 128])
        XA = sb1("XA", [36, TS])
        XB = sb1("XB", [36, TS])
        zrow = sb1("zrow", [36, 512])
        sm = {n_: sb1("sm_" + n_, [36, 16]) for n_ in ['ac', 'bl', 'M', 'MP', 'mu', 'al', 'gref', 'm0']}
        Wtok = sb1("Wtok", [128, 2, 16, 8])
        Wtokb = sb1("Wtokb", [128, 2, 16, 4], BF16)
        ALb = sb1("ALb", [128, 2, 4, 16])
        dstat = sb1("dstat", [128, 8])
        wGb = sb1("wGb", [128, 8, 16], BF16)
        gbT = sb1("gbT", [36, 4])
        ngbT = sb1("ngbT", [36, 4])
        cw = sb1("cw", [128, 32, 9])
        cb = sb1("cb", [128, 32])
        mng = sb1("mng", [128, 16])
        wb1 = WB([sb1("wbf1A", [128, 8, DH], BF16), sb1("wbf1B", [128, 8, DH], BF16)], ['wbf1A', 'wbf1B'])
        P.dma(mC[:], maskC.rearrange("d s t -> s d t"), writes=['mC'])
        P.dma(SEL[:], sel_d[:], writes=['SEL'])
        P.dma(gbT[:], gbT_d[:], writes=['gbT'])
        P.dma(cw[:], cw_d[:], writes=['cw'])
        P.dma(cb[:], cb_d[:], writes=['cb'])
        P.dma(mng[:], mng_d[:], writes=['mng'])
        P.op('dve', lambda e: e.memset(onesb[:], 1.0), writes=['onesb'])
        P.op('dve', lambda e: e.memset(zrow[:], 0.0), writes=['zrow'])
        P.op('dve', lambda e: e.tensor_copy(out=identb[:], in_=ident[:]), reads=['ident'], writes=['identb'])
        P.op('dve', lambda e: e.tensor_scalar(out=ngbT[:], in0=gbT[:], scalar1=-1.0, scalar2=None, op0=ALU.mult), reads=['gbT'], writes=['ngbT'])
        P.dma(wst[:, :, 0:16], w_in_odd[:, 10240:10256].rearrange("(kc p) n -> p kc n", p=128), writes=['wst'])
        P.op('pool', lambda e: e.tensor_copy(out=wGb[:], in_=wst[:, :, 0:16]), reads=['wst'], writes=['wGb'])
        LNK = float(np.log(DH ** -0.5))

        def load_w1(c0, ncols, dst, dk):
            v = w_in_odd[:, c0:c0 + ncols].rearrange("(kc p) n -> p kc n", p=128)
            for q0 in range(0, ncols, 256):
                w_ = min(256, ncols - q0)
                P.dma(wst[:, :, 0:w_], v[:, :, q0:q0 + w_], writes=['wst'])
                P.op('act', lambda e: e.copy(out=dst[:, :, q0:q0 + w_], in_=wst[:, :, 0:w_]), reads=['wst'], writes=[dk])

        def ld1(c0):
            return lambda dst, dk: load_w1(c0, DH, dst, dk)

        def gates_seq(T, is_sample):
            NC = T // LC
            pbk = min(512, T)
            for d in range(2):
                pb = 32 * d
                rows = slice(pb, pb + 4)
                for b in range(T // pbk):
                    t0 = b * pbk
                    pz, pk = nps()
                    for kc in range(8):
                        P.op('pe', lambda e: e.matmul(pz[pb:pb + 4, 0:pbk], wGb[:, kc, (2 + d) * 4:(3 + d) * 4], hT[:, kc, t0:t0 + pbk],
                                                      start=(kc == 0), stop=(kc == 7)), reads=['wGb'] + hT_keys(t0, t0 + pbk), writes=[pk])
                    P.op('act', lambda e: e.activation(out=XA[rows, t0:t0 + pbk], in_=pz[pb:pb + 4, 0:pbk], func=AF.Exp,
                                                       bias=ngbT[rows, 2 + d:3 + d], scale=-1.0), reads=[pk, 'ngbT'], writes=['XA'])
                P.op('act', lambda e: e.activation(out=XA[rows, 0:T], in_=XA[rows, 0:T], func=AF.Ln, bias=1.0, scale=1.0),
                     reads=['XA'], writes=['XA'])
                for b in range(T // pbk):
                    bs = slice(b * pbk, (b + 1) * pbk)
                    if d == 0:
                        P.op('dve', lambda e: e.tensor_tensor_scan(out=XB[rows, bs], data0=XA[rows, bs], data1=zrow[rows, 0:pbk],
                                                                   initial=0.0, op0=ALU.add, op1=ALU.add), reads=['XA', 'zrow'], writes=['XB'])
                    else:
                        P.op('dve', lambda e: e.tensor_tensor_scan(out=XB[rows, bs][:, ::-1], data0=XA[rows, bs][:, ::-1],
                                                                   data1=zrow[rows, 0:pbk], initial=0.0, op0=ALU.add, op1=ALU.add),
                             reads=['XA', 'zrow'], writes=['XB'])
                if d == 0:
                    ci_, ce_ = 0, LC - 1
                else:
                    ci_, ce_ = LC - 1, 0
                B3 = XB[rows, 0:T].rearrange("p (c l) -> p c l", l=LC)
                A3 = XA[rows, 0:T].rearrange("p (c l) -> p c l", l=LC)
                S = {k_: v_[rows, :] for k_, v_ in sm.items()}
                P.op('dve', lambda e: e.tensor_tensor(out=S['gref'][:, 0:NC], in0=B3[:, :, ci_], in1=A3[:, :, ci_], op=ALU.subtract),
                     reads=['XA', 'XB'], writes=['sm_gref'])
                P.op('dve', lambda e: e.tensor_tensor(out=B3, in0=B3, in1=S['gref'][:, 0:NC].unsqueeze(2).to_broadcast([4, NC, LC]),
                                                      op=ALU.subtract), reads=['XB', 'sm_gref'], writes=['XB'])
                for b in range(T // pbk):
                    t0 = b * pbk
                    pz, pk = nps()
                    for kc in range(8):
                        P.op('pe', lambda e: e.matmul(pz[pb:pb + 4, 0:pbk], wGb[:, kc, d * 4:(d + 1) * 4], hT[:, kc, t0:t0 + pbk],
                                                      start=(kc == 0), stop=(kc == 7)), reads=['wGb'] + hT_keys(t0, t0 + pbk), writes=[pk])
                    P.op('dve', lambda e: e.scalar_tensor_tensor(out=XA[rows, t0:t0 + pbk], in0=pz[pb:pb + 4, 0:pbk], scalar=gbT[rows, d:d + 1],
                                                                 in1=XB[rows, t0:t0 + pbk], op0=ALU.add, op1=ALU.add),
                         reads=[pk, 'gbT', 'XB', 'XA'], writes=['XA'])
                P.op('dve', lambda e: e.tensor_reduce(out=S['ac'][:, 0:NC], in_=A3, axis=AX.X, op=ALU.max), reads=['XA'], writes=['sm_ac'])
                P.op('dve', lambda e: e.tensor_scalar(out=S['bl'][:, 0:NC], in0=B3[:, :, ce_], scalar1=-1.0, scalar2=None, op0=ALU.mult),
                     reads=['XB'], writes=['sm_bl'])
                if is_sample:
                    P.dma(S['m0'][:, 0:1], s_m[d, :].rearrange("(h o) -> h o", o=1), writes=['sm_m0'])
                else:
                    P.op('dve', lambda e: e.memset(S['m0'][:, 0:1], 0.0), writes=['sm_m0'])
                if d == 0:
                    P.op('dve', lambda e: e.tensor_tensor_scan(out=S['M'][:, 0:NC], data0=S['ac'][:, 0:NC], data1=S['bl'][:, 0:NC],
                                                               initial=S['m0'][:, 0:1], op0=ALU.max, op1=ALU.add),
                         reads=['sm_ac', 'sm_bl', 'sm_m0'], writes=['sm_M'])
                    P.op('dve', lambda e: e.tensor_copy(out=S['MP'][:, 0:1], in_=S['m0'][:, 0:1]), reads=['sm_m0'], writes=['sm_MP'])
                    if NC > 1:
                        P.op('dve', lambda e: e.tensor_copy(out=S['MP'][:, 1:NC], in_=S['M'][:, 0:NC - 1]), reads=['sm_M', 'sm_MP'], writes=['sm_MP'])
                else:
                    P.op('dve', lambda e: e.tensor_tensor_scan(out=S['M'][:, 0:NC][:, ::-1], data0=S['ac'][:, 0:NC][:, ::-1],
                                                               data1=S['bl'][:, 0:NC][:, ::-1], initial=S['m0'][:, 0:1],
                                                               op0=ALU.max, op1=ALU.add),
                         reads=['sm_ac', 'sm_bl', 'sm_m0'], writes=['sm_M'])
                    P.op('dve', lambda e: e.tensor_copy(out=S['MP'][:, NC - 1:NC], in_=S['m0'][:, 0:1]), reads=['sm_m0'], writes=['sm_MP'])
                    if NC > 1:
                        P.op('dve', lambda e: e.tensor_copy(out=S['MP'][:, 0:NC - 1], in_=S['M'][:, 1:NC]), reads=['sm_M', 'sm_MP'], writes=['sm_MP'])
                P.op('dve', lambda e: e.tensor_tensor(out=S['mu'][:, 0:NC], in0=S['MP'][:, 0:NC], in1=S['ac'][:, 0:NC], op=ALU.max),
                     reads=['sm_MP', 'sm_ac'], writes=['sm_mu'])
                P.op('dve', lambda e: e.tensor_tensor(out=S['al'][:, 0:NC], in0=S['MP'][:, 0:NC], in1=S['mu'][:, 0:NC], op=ALU.subtract),
                     reads=['sm_MP', 'sm_mu'], writes=['sm_al'])
                P.op('act', lambda e: e.activation(out=S['al'][:, 0:NC], in_=S['al'][:, 0:NC], func=AF.Exp), reads=['sm_al'], writes=['sm_al'])
                mub = S['mu'][:, 0:NC].unsqueeze(2).to_broadcast([4, NC, LC])
                P.op('dve', lambda e: e.tensor_tensor(out=A3, in0=A3, in1=mub, op=ALU.subtract), reads=['XA', 'sm_mu'], writes=['XA'])
                P.op('dve', lambda e: e.tensor_tensor(out=B3, in0=B3, in1=mub, op=ALU.subtract), reads=['XB', 'sm_mu'], writes=['XB'])
                P.op('dve', lambda e: e.tensor_scalar(out=XA[rows, 0:T], in0=XA[rows, 0:T], scalar1=LNK, scalar2=None, op0=ALU.add),
                     reads=['XA'], writes=['XA'])
                P.op('act', lambda e: e.activation(out=XA[rows, 0:T], in_=XA[rows, 0:T], func=AF.Exp), reads=['XA'], writes=['XA'])
                P.op('act', lambda e: e.activation(out=XB[rows, 0:T], in_=XB[rows, 0:T], func=AF.Exp), reads=['XB'], writes=['XB'])
                pz, pk = nps()
                for c in range(NC):
                    P.op('pe', lambda e: e.transpose(out=pz[:, c * 8:c * 8 + 4], in_=XA[rows, c * LC:(c + 1) * LC],
                                                     identity=ident[rows, pb:pb + 4]), reads=['XA', 'ident'], writes=[pk])
                    P.op('pe', lambda e: e.transpose(out=pz[:, c * 8 + 4:c * 8 + 8], in_=XB[rows, c * LC:(c + 1) * LC],
                                                     identity=ident[rows, pb:pb + 4]), reads=['XB', 'ident'], writes=[pk])
                P.op('dve', lambda e: e.tensor_copy(out=Wtok[:, d, 0:NC, :], in_=pz[:, 0:NC * 8].rearrange("p (c k) -> p c k", k=8)),
                     reads=[pk], writes=['Wtok'])
                P.op('dve', lambda e: e.tensor_copy(out=Wtokb[:, d, 0:NC, :], in_=Wtok[:, d, 0:NC, 0:4]), reads=['Wtok'], writes=['Wtokb'])
                pz, pk = nps()
                for hd in range(4):
                    P.op('pe', lambda e: e.matmul(pz[:, hd * 16:hd * 16 + NC], SEL[rows, hd, :], S['al'][:, 0:NC], start=True, stop=True),
                         reads=['SEL', 'sm_al'], writes=[pk])
                P.op('dve', lambda e: e.tensor_copy(out=ALb[:, d, :, 0:NC], in_=pz[:, 0:64].rearrange("p (h c) -> p h c", c=16)[:, :, 0:NC]),
                     reads=[pk], writes=['ALb'])

        def conv_tile(dst, dkey, slot_j, widx, t0src, T, is_sample):
            X = FT[1][:, 0:T]
            A = FT[0][:, 0:T]
            if is_sample:
                R_, Cw = T // 64, 64
                taps = [(dr, dc) for dr in (-1, 0, 1) for dc in (-1, 0, 1)]
            else:
                R_, Cw = 1, T
                taps = [(0, dc) for dc in (-1, 0, 1)]
            X3 = X.rearrange("p (r c) -> p r c", c=Cw)
            A3 = A.rearrange("p (r c) -> p r c", c=Cw)
            P.op('dve', lambda e: e.tensor_scalar(out=A, in0=X, scalar1=cw[:, widx, 4:5], scalar2=None, op0=ALU.mult),
                 reads=['FT1', 'cw'], writes=['FT0'])
            for (dr, dc) in taps:
                if dr == 0 and dc == 0:
                    continue
                r0, r1 = max(0, -dr), R_ - max(0, dr)
                c0, c1 = max(0, -dc), Cw - max(0, dc)
                ti = (dr + 1) * 3 + (dc + 1)
                P.op('dve', lambda e: e.scalar_tensor_tensor(out=A3[:, r0:r1, c0:c1], in0=X3[:, r0 + dr:r1 + dr, c0 + dc:c1 + dc],
                                                             scalar=cw[:, widx, ti:ti + 1], in1=A3[:, r0:r1, c0:c1],
                                                             op0=ALU.mult, op1=ALU.add), reads=['FT1', 'FT0', 'cw'], writes=['FT0'])
            P.op('act', lambda e: e.activation(out=dst[:, slot_j, 0:T], in_=A, func=AF.Silu, bias=cb[:, widx:widx + 1], scale=1.0),
                 reads=['FT0', 'cb'], writes=[dkey])

        hacc = [FT[2 + i][:, 0:TS].rearrange("p (j e) -> p j e", e=DH) for i in range(4)]

        def mlstm_head(hd, off, T, is_sample, pidx):
            NC = T // LC
            NTt = T // 128
            pbk = min(512, T)
            for qk in range(2):
                wb1.use(('qk', off, hd, qk), ld1(qk * 2048 + hd * DH))
                if qk == 0:
                    wb1.prefetch(('qk', off, hd, 1), ld1(2048 + hd * DH))
                else:
                    wb1.prefetch(('v', off, hd), ld1(4096 + hd * DH))
                for j in range(4):
                    for b in range(T // pbk):
                        t0 = b * pbk
                        pz, pk = nps()
                        for kc in range(8):
                            P.op('pe', lambda e: e.matmul(pz[:, 0:pbk], wb1.tile[:, kc, j * 128:(j + 1) * 128], hT[:, kc, t0:t0 + pbk],
                                                          start=(kc == 0), stop=(kc == 7)), reads=[wb1.key] + hT_keys(t0, t0 + pbk), writes=[pk])
                        P.op('act', lambda e: e.copy(out=FT[1][:, t0:t0 + pbk], in_=pz[:, 0:pbk]), reads=[pk], writes=['FT1'])
                    widx = (qk * 4 + hd) * 4 + j
                    if qk == 0:
                        conv_tile(qT, ('yT', 4 + j), j, widx, 0, T, is_sample)
                    else:
                        conv_tile(kT, 'kT', j, widx, 0, T, is_sample)
            wb1.use(('v', off, hd), ld1(4096 + hd * DH))
            wb1.prefetch(('o', off, hd), ld1(6144 + hd * DH))
            qkeys = [('yT', 4 + j) for j in range(4)]
            for d in range(2):
                rev = (d == 1)
                if is_sample:
                    P.dma(Cst[:], s_C[d, hd].rearrange("(j p) e -> p j e", p=128), writes=['Cst'])
                    P.dma(nst[:, 0:4], s_n[d, hd].rearrange("(j p) -> p j", p=128), writes=['nst'], allow_slow_non_contiguous=True)
                else:
                    P.op('pool', lambda e: e.memset(Cst[:], 0.0), writes=['Cst'])
                    P.op('pool', lambda e: e.memset(nst[:, 0:4], 0.0), writes=['nst'])
                chunks = list(range(NC))
                if rev:
                    chunks = chunks[::-1]
                for c in chunks:
                    cs = slice(c * LC, (c + 1) * LC)
                    wcol = Wtok[:, d, c, hd:hd + 1]
                    thcol = Wtok[:, d, c, 4 + hd:5 + hd]
                    alcol = ALb[:, d, hd, c:c + 1]
                    pv, pvk = nps()
                    for kc in range(8):
                        P.op('pe', lambda e: e.matmul(pv[:, 0:DH], hT[:, kc, cs], wb1.tile[:, kc, :], start=(kc == 0), stop=(kc == 7)),
                             reads=[wb1.key, ('hT', c)], writes=[pvk])
                    P.op('act', lambda e: e.copy(out=vch[:], in_=pv[:, 0:DH]), reads=[pvk], writes=['vch'])
                    pt, ptk = nps()
                    ptb = pt[:].bitcast(BF16)
                    for j in range(4):
                        P.op('pe', lambda e: e.transpose(out=ptb[:, j * 128:(j + 1) * 128], in_=kT[:, j, cs], identity=identb[:]),
                             reads=['kT', 'identb'], writes=[ptk])
                    P.op('act', lambda e: e.copy(out=ktok[:], in_=ptb[:, 0:DH]), reads=[ptk], writes=['ktok'])
                    ps_, psk = nps()
                    for j in range(4):
                        P.op('pe', lambda e: e.matmul(ps_[:, 0:128], kT[:, j, cs], qT[:, j, cs], start=(j == 0), stop=(j == 3)),
                             reads=['kT'] + qkeys, writes=[psk])
                    P.op('dve', lambda e: e.scalar_tensor_tensor(out=sTs[:], in0=ps_[:, 0:128], scalar=wcol, in1=mC[:, d, :],
                                                                 op0=ALU.mult, op1=ALU.mult), reads=[psk, 'Wtok', 'mC'], writes=['sTs'])
                    P.op('dve', lambda e: e.tensor_scalar(out=Cst[:], in0=Cst[:], scalar1=alcol, scalar2=None, op0=ALU.mult),
                         reads=['Cst', 'ALb'], writes=['Cst'])
                    P.op('act', lambda e: e.copy(out=Cbf[:], in_=Cst[:]), reads=['Cst'], writes=['Cbf'])
                    P.op('dve', lambda e: e.tensor_scalar(out=nst[:, 0:4], in0=nst[:, 0:4], scalar1=alcol, scalar2=None, op0=ALU.mult),
                         reads=['nst', 'ALb'], writes=['nst'])
                    P.op('dve', lambda e: e.tensor_copy(out=nbf[:], in_=nst[:, 0:4]), reads=['nst'], writes=['nbf'])
                    pn, pnk = nps()
                    for j in range(4):
                        P.op('pe', lambda e: e.matmul(pn[:, 0:DH], qT[:, j, cs], Cbf[:, j, :], start=(j == 0), stop=False),
                             reads=qkeys + ['Cbf'], writes=[pnk])
                    P.op('pe', lambda e: e.matmul(pn[:, 0:DH], sTs[:], vch[:], start=False, stop=True),
                         reads=['sTs', 'vch'], writes=[pnk])
                    pd_, pdk = nps()
                    for j in range(4):
                        P.op('pe', lambda e: e.matmul(pd_[:, 0:1], qT[:, j, cs], nbf[:, j:j + 1], start=(j == 0), stop=False),
                             reads=qkeys + ['nbf'], writes=[pdk])
                    P.op('pe', lambda e: e.matmul(pd_[:, 0:1], sTs[:], onesb[:], start=False, stop=True), reads=['sTs', 'onesb'], writes=[pdk])
                    P.op('act', lambda e: e.activation(out=dstat[:, 2:3], in_=pd_[:, 0:1], func=AF.Abs), reads=[pdk], writes=['dstat'])
                    P.op('dve', lambda e: e.tensor_tensor(out=dstat[:, 0:1], in0=dstat[:, 2:3], in1=thcol, op=ALU.max),
                         reads=['dstat', 'Wtok'], writes=['dstat'])
                    P.op('dve', lambda e: e.reciprocal(out=dstat[:, 1:2], in_=dstat[:, 0:1]), reads=['dstat'], writes=['dstat'])
                    hdst = hacc[c // 4][:, c % 4, :]
                    if d == 0:
                        P.op('act', lambda e: e.activation(out=hdst, in_=pn[:, 0:DH], func=AF.Identity, scale=dstat[:, 1:2]),
                             reads=[pnk, 'dstat'], writes=[('hacc', c)])
                    else:
                        P.op('dve', lambda e: e.scalar_tensor_tensor(out=hdst, in0=pn[:, 0:DH], scalar=dstat[:, 1:2], in1=hdst,
                                                                     op0=ALU.mult, op1=ALU.add), reads=[pnk, 'dstat', ('hacc', c)], writes=[('hacc', c)])
                    P.op('act', lambda e: e.activation(out=vw[:], in_=vch[:], func=AF.Identity, scale=wcol),
                         reads=['vch', 'Wtok'], writes=['vw'])
                    for j in range(4):
                        pc, pck = nps()
                        P.op('pe', lambda e: e.matmul(pc[:, 0:DH], ktok[:, j * 128:(j + 1) * 128], vw[:], start=True, stop=True),
                             reads=['ktok', 'vw'], writes=[pck])
                        P.op('dve', lambda e: e.tensor_tensor(out=Cst[:, j, :], in0=Cst[:, j, :], in1=pc[:, 0:DH], op=ALU.add),
                             reads=[pck, 'Cst'], writes=['Cst'])
                    pq_, pqk = nps()
                    for j in range(4):
                        P.op('pe', lambda e: e.matmul(pq_[:, j:j + 1], ktok[:, j * 128:(j + 1) * 128], Wtokb[:, d, c, hd:hd + 1],
                                                      start=True, stop=True), reads=['ktok', 'Wtokb'], writes=[pqk])
                    P.op('dve', lambda e: e.tensor_tensor(out=nst[:, 0:4], in0=nst[:, 0:4], in1=pq_[:, 0:4], op=ALU.add),
                         reads=[pqk, 'nst'], writes=['nst'])
                if not is_sample:
                    P.dma(o_C[pidx, d, hd].rearrange("(j p) e -> p j e", p=128), Cst[:], reads=['Cst'], q='pool')
                    P.dma(o_n[pidx, d, hd].rearrange("(j p) -> p j", p=128), nst[:, 0:4], reads=['nst'], q='pool', allow_slow_non_contiguous=True)
            wb1.use(('o', off, hd), ld1(6144 + hd * DH))
            wb1.prefetch(('z', off, hd), ld1(8192 + hd * DH))
            for tt in range(NTt):
                hdst = hacc[tt // 4][:, tt % 4, :]
                pz, pk = nps()
                for kc in range(8):
                    P.op('pe', lambda e: e.matmul(pz[:, 0:DH], hT[:, kc, tt * 128:(tt + 1) * 128], wb1.tile[:, kc, :],
                                                  start=(kc == 0), stop=(kc == 7)), reads=[wb1.key, ('hT', tt)], writes=[pk])
                P.op('act', lambda e: e.activation(out=FT[0][:, 0:DH], in_=pz[:, 0:DH], func=AF.Sigmoid), reads=[pk], writes=['FT0'])
                P.op('dve', lambda e: e.tensor_tensor(out=hdst, in0=hdst, in1=FT[0][:, 0:DH], op=ALU.mult),
                     reads=['FT0', ('hacc', tt)], writes=[('hacc', tt)])
                P.op('act', lambda e: e.activation(out=FT[0][:, 0:DH], in_=hdst, func=AF.Square, accum_out=dstat[:, 4:5]),
                     reads=[('hacc', tt), 'FT0'], writes=['FT0', 'dstat'])
                P.op('dve', lambda e: e.tensor_scalar(out=dstat[:, 5:6], in0=dstat[:, 4:5], scalar1=1.0 / DH, scalar2=1e-6,
                                                      op0=ALU.mult, op1=ALU.add), reads=['dstat'], writes=['dstat'])
                P.op('act', lambda e: e.activation(out=dstat[:, 6:7], in_=dstat[:, 5:6], func=AF.Sqrt), reads=['dstat'], writes=['dstat'])
                P.op('dve', lambda e: e.reciprocal(out=dstat[:, 7:8], in_=dstat[:, 6:7]), reads=['dstat'], writes=['dstat'])
                P.op('dve', lambda e: e.tensor_scalar(out=hdst, in0=hdst, scalar1=dstat[:, 7:8], scalar2=None, op0=ALU.mult),
                     reads=[('hacc', tt), 'dstat'], writes=[('hacc', tt)])
            wb1.use(('z', off, hd), ld1(8192 + hd * DH))
            if hd < 3:
                wb1.prefetch(('qk', off, hd + 1, 0), ld1((hd + 1) * DH))
            for tt in range(NTt):
                hdst = hacc[tt // 4][:, tt % 4, :]
                pz, pk = nps()
                for kc in range(8):
                    P.op('pe', lambda e: e.matmul(pz[:, 0:DH], hT[:, kc, tt * 128:(tt + 1) * 128], wb1.tile[:, kc, :],
                                                  start=(kc == 0), stop=(kc == 7)), reads=[wb1.key, ('hT', tt)], writes=[pk])
                P.op('act', lambda e: e.activation(out=FT[0][:, 0:DH], in_=pz[:, 0:DH], func=AF.Silu), reads=[pk], writes=['FT0'])
                P.op('dve', lambda e: e.tensor_tensor(out=hdst, in0=hdst, in1=FT[0][:, 0:DH], op=ALU.mult),
                     reads=['FT0', ('hacc', tt)], writes=[('hacc', tt)])
                pz, pk = nps()
                for j in range(4):
                    P.op('pe', lambda e: e.transpose(out=pz[:, j * 128:(j + 1) * 128], in_=hdst[:, j * 128:(j + 1) * 128], identity=ident[:]),
                         reads=[('hacc', tt), 'ident'], writes=[pk])
                for j in range(4):
                    P.op('act', lambda e: e.activation(out=yT[:, j, tt * 128:(tt + 1) * 128], in_=pz[:, j * 128:(j + 1) * 128],
                                                       func=AF.Identity, scale=mng[:, hd * 4 + j:hd * 4 + j + 1]),
                         reads=[pk, 'mng'], writes=[('yT', j)])

        for si in seq_ids:
            off, T, cidx, is_sample = SEQS[si]
            P.barrier()
            make_gate(1, cidx)
            make_hT(1, x1, 'x1', off, T, cidx)
            P.barrier()
            gates_seq(T, is_sample)
            if not is_sample:
                for d in range(2):
                    lastc = (T // LC - 1) if d == 0 else 0
                    P.dma(o_m[si - 1, d, :].rearrange("(h o) -> h o", o=1), sm['M'][32 * d:32 * d + 4, lastc:lastc + 1],
                          reads=['sm_M'], q='pool')
            for hd in range(4):
                P.barrier()
                mlstm_head(hd, off, T, is_sample, si - 1)
                if debug and debug.get('dump_y'):
                    for j in range(4):
                        dump("yM%d_%d" % (hd, j), yT[:, j, 0:T], [('yT', j)], T, col0=off)
                P.barrier()
                load_wo(w_out_odd[hd * DH:(hd + 1) * DH, :], 128, nk=4)
                last = (hd == 3)
                outproj(1, [(128, s_) for s_ in range(4)], x1, 'x1', (yout if last else x1), ('yout' if last else 'x1'),
                        off, T, cidx, final=last)
        P.barrier()
        L1.close()
    P.finish()
    sems = {s: es.enter_context(nc.semaphore(s)) for s in P.sem_names}
    P.emit(sems)
    es.close()
    global _last_dslot
    _last_dslot = dslot if debug else {}
    return nc, P


def host_inputs(inp, core):
    f = lambda a: np.ascontiguousarray(a, dtype=np.float32)
    b = core % 2
    m = {}
    m["xin"] = f(np.concatenate([inp["x_sample"][b], inp["x_prompt"][2 * core], inp["x_prompt"][2 * core + 1]], axis=0))
    cond = np.stack([inp["c"][b], inp["c_ctx"]], axis=0)
    m["condT"] = f(cond.reshape(2, 8, 128).transpose(2, 1, 0))
    m["s_hgrn"] = f(inp["state_hgrn"][b, 0])
    m["s_rwkv"] = f(inp["state_rwkv"][b, 0])
    m["s_C"] = f(inp["state_mlstm_C"][b, 0])
    m["s_n"] = f(inp["state_mlstm_n"][b, 0])
    m["s_m"] = f(inp["state_mlstm_m"][b, 0])
    m["w_mod"] = f(inp["w_mod"])
    m["b_modT"] = f(inp["b_mod"].reshape(2, 24, 128).transpose(2, 0, 1))
    m["norm_gT"] = f(inp["norm_g"].reshape(2, 8, 128).transpose(2, 0, 1))
    m["fnorm_gT"] = f(inp["final_norm_g"].reshape(8, 128).T)
    w = inp["w_in_even"][0]
    DA = 1024
    wA = np.stack([np.concatenate([w[:, g * DA + h * 128: g * DA + (h + 1) * 128] for g in (0, 1, 4, 2, 3)], axis=1)
                   for h in range(8)], axis=0)
    m["wA"] = f(wA)
    o = 5 * DA
    zb0 = o + 3328
    wB = np.stack([np.concatenate([w[:, o + g * 1024 + h * 64: o + g * 1024 + (h + 1) * 64] for g in (0, 1, 2)]
                                  + [w[:, zb0 + h * 64: zb0 + (h + 1) * 64]], axis=1) for h in range(16)], axis=0)
    m["wB"] = f(wB)
    m["wLR"] = f(w[:, o + 3072: o + 3328])
    m["w_out_even"] = f(inp["w_out_even"][0])
    m["lbT"] = f(inp["hgrn_lb_logits"].reshape(2, 8, 128).transpose(2, 0, 1))
    m["hg_gT"] = f(inp["hgrn_norm_g"][0].reshape(8, 128).T)
    mu = inp["rwkv_shift_mu"][0]
    mr = np.zeros((64, 2, 4, 16), np.float32)
    for g in range(3):
        mr[:, :, g, :] = mu[:, g * 1024:(g + 1) * 1024].reshape(2, 16, 64).transpose(2, 0, 1)
    m["mu_rkv"] = mr
    m["mu_lr"] = f(mu[:, 3072:3328].reshape(2, 4, 64).transpose(2, 0, 1))
    m["w0T"] = f(inp["rwkv_w0"][0].reshape(2, 16, 64).transpose(2, 0, 1))
    m["a0T"] = f(inp["rwkv_a0"][0].reshape(2, 16, 64).transpose(2, 0, 1))
    m["w2"] = f(inp["rwkv_w2"][0])
    m["a2"] = f(inp["rwkv_a2"][0])
    m["kkT"] = f(inp["rwkv_k_k"][0].reshape(16, 64).T)
    m["kaT"] = f(inp["rwkv_k_a"][0].reshape(16, 64).T)
    m["rkT"] = f(inp["rwkv_r_k"][0].T)
    m["gngT"] = f(inp["rwkv_gn_g"][0].reshape(16, 64).T)
    m["gnbT"] = f(inp["rwkv_gn_b"][0].reshape(16, 64).T)
    s = np.arange(128)[:, None]
    t = np.arange(128)[None, :]
    same = (s // 32) == (t // 32)
    m["maskH"] = np.stack([(same & (s <= t)), (same & (s >= t))]).astype(np.float32)
    m["ident_in"] = np.eye(128, dtype=np.float32)
    s6 = np.arange(64)[:, None]
    t6 = np.arange(64)[None, :]
    mr_ = np.zeros((2, 3, 64, 128), np.float32)
    for d_, (st_, inc_) in enumerate([((s6 < t6), (s6 <= t6)), ((s6 > t6), (s6 >= t6))]):
        st_ = st_.astype(np.float32)
        inc_ = inc_.astype(np.float32)
        mr_[d_, 0, :, 0:64] = -st_
        mr_[d_, 0, :, 64:128] = -inc_
        mr_[d_, 1, :, 0:64] = st_
        mr_[d_, 1, :, 64:128] = inc_
        mr_[d_, 2, :, 0:64] = -(st_.T)
    m["maskR"] = mr_
    m["w_in_odd"] = f(inp["w_in_odd"][0])
    m["w_out_odd"] = f(inp["w_out_odd"][0])
    m["maskC"] = np.stack([(s <= t), (s >= t)]).astype(np.float32)
    sel = np.zeros((36, 4, 128), np.float32)
    gbt = np.zeros((36, 4), np.float32)
    for pb_ in (0, 32):
        for k_ in range(4):
            sel[pb_ + k_, k_, :] = 1.0
        gbt[pb_:pb_ + 4, :] = inp["mlstm_gate_b"][0].T
    m["sel_d"] = sel
    m["gbT_d"] = gbt
    m["cw_d"] = f(inp["mlstm_conv_w"][0].reshape(9, 32, 128).transpose(2, 1, 0))
    m["cb_d"] = f(inp["mlstm_conv_b"][0].reshape(32, 128).T)
    m["mng_d"] = f(inp["mlstm_norm_g"][0].reshape(16, 128).T)
    return m


def kernel(**inp):
    inp = {k: np.asarray(v) for k, v in inp.items()}
    nc, P = build()
    in_maps = [host_inputs(inp, c) for c in range(NCORES)]
    res = run_bass_kernel_spmd(nc, in_maps, core_ids=list(range(NCORES)))
    r = res.results
    y_prompt = np.zeros((16, TP, D), np.float32)
    y_sample = np.zeros((2, TS, D), np.float32)
    for c in range(NCORES):
        y_prompt[2 * c] = r[c]["yout"][TS:TS + TP]
        y_prompt[2 * c + 1] = r[c]["yout"][TS + TP:]
    for b in range(2):
        y_sample[b] = r[b]["yout"][0:TS]
    new_hgrn = np.concatenate([r[c]["o_hgrn"] for c in range(NCORES)], axis=0)[:, None]
    new_rwkv = np.concatenate([r[c]["o_rwkv"] for c in range(NCORES)], axis=0)[:, None]
    new_C = np.concatenate([r[c]["o_C"] for c in range(NCORES)], axis=0)[:, None]
    new_n = np.concatenate([r[c]["o_n"] for c in range(NCORES)], axis=0)[:, None]
    new_m = np.concatenate([r[c]["o_m"] for c in range(NCORES)], axis=0)[:, None]
    return (y_prompt, y_sample, new_hgrn.astype(np.float32), new_rwkv.astype(np.float32),
            new_C.astype(np.float32), new_n.astype(np.float32), new_m.astype(np.float32))
```

```python
import contextlib
import numpy as np
import concourse.bass as bass
import concourse.mybir as mybir
from concourse.bass_utils import run_bass_kernel_spmd

F32 = mybir.dt.float32
BF16 = mybir.dt.bfloat16
ALU = mybir.AluOpType
AF = mybir.ActivationFunctionType
AX = mybir.AxisListType

D = 1024
TS = 2048
TP = 256
TT = TS + 2 * TP
NCORES = 8


class _Rec:
    def __init__(self):
        self.calls = []

    def __getattr__(self, name):
        def f(*a, **k):
            self.calls.append((name, a, k))
            return self
        return f


class Prog:
    ENGS = ['pe', 'dve', 'act', 'pool', 'sp']
    NDMA = 16

    def __init__(self, nc):
        self.nc = nc
        self.ops = {e: [] for e in self.ENGS}
        self.cnt = {}
        self.waited = {e: {} for e in self.ENGS}
        self.last_write = {}
        self.readers = {}
        self.dma_rr = 0
        self.sem_names = list(self.ENGS) + ['d%d' % i for i in range(self.NDMA)]
        for s in self.sem_names:
            self.cnt[s] = 0
        self.n_ops = 0

    def _deps(self, eng, reads, writes):
        deps = {}

        def add(p):
            if p is None:
                return
            f, n = p
            if f == 'pe' and eng == 'pe':
                return
            if n > deps.get(f, 0):
                deps[f] = n
        for k in reads:
            add(self.last_write.get(k))
        for k in writes:
            add(self.last_write.get(k))
            for p in self.readers.get(k, ()):
                add(p)
        waits = []
        for f, n in deps.items():
            if n > self.waited[eng].get(f, 0):
                waits.append((f, n))
                self.waited[eng][f] = n
        return waits

    def _commit(self, tag, reads, writes):
        for k in reads:
            lst = self.readers.setdefault(k, [])
            lst[:] = [p for p in lst if p[0] != tag[0]]
            lst.append(tag)
        for k in writes:
            self.last_write[k] = tag
            self.readers[k] = []

    def op(self, eng, fn, reads=(), writes=()):
        rec = _Rec()
        fn(rec)
        name, a, k = rec.calls[0]
        fn = (lambda e, name=name, a=a, k=k: getattr(e, name)(*a, **k))
        waits = self._deps(eng, reads, writes)
        self.cnt[eng] += 1
        tag = (eng, self.cnt[eng])
        self.ops[eng].append((waits, fn, eng, 1))
        self._commit(tag, reads, writes)
        self.n_ops += 1

    def dma(self, out, in_, reads=(), writes=(), q='sp', **kw):
        d = 'd%d' % self.dma_rr
        self.dma_rr = (self.dma_rr + 1) % self.NDMA
        waits = self._deps(q, reads, writes)
        prev = self.cnt[d]
        if prev > self.waited[q].get(d, 0):
            waits.append((d, prev))
            self.waited[q][d] = prev
        self.cnt[d] += 16
        tag = (d, self.cnt[d])
        self.ops[q].append((waits, (lambda e: e.dma_start(out=out, in_=in_, **kw)), d, 16))
        self._commit(tag, reads, writes)
        self.n_ops += 1

    def barrier(self):
        allsems = list(self.sem_names)
        for e in self.ENGS:
            waits = []
            for f in allsems:
                if self.cnt[f] > self.waited[e].get(f, 0):
                    waits.append((f, self.cnt[f]))
                    self.waited[e][f] = self.cnt[f]
            self.ops[e].append((waits, None, None, 0))

    def finish(self, q='sp'):
        waits = []
        for i in range(self.NDMA):
            d = 'd%d' % i
            if self.cnt[d] > self.waited[q].get(d, 0):
                waits.append((d, self.cnt[d]))
                self.waited[q][d] = self.cnt[d]
        self.ops[q].append((waits, None, None, 0))

    def emit(self, sems):
        ops = self.ops

        def run(e, lst):
            for waits, fn, semname, inc in lst:
                for f, n in waits:
                    e.wait_ge(sems[f], n)
                if fn is not None:
                    fn(e).then_inc(sems[semname], inc)
        with self.nc.Block() as block:
            @block.tensor
            def _(e):
                run(e, ops['pe'])

            @block.vector
            def _(e):
                run(e, ops['dve'])

            @block.scalar
            def _(e):
                run(e, ops['act'])

            @block.gpsimd
            def _(e):
                run(e, ops['pool'])

            @block.sync
            def _(e):
                run(e, ops['sp'])


SEQS = [(0, TS, 0, True), (TS, TP, 1, False), (TS + TP, TP, 1, False)]


def build(debug=None):
    nc = bass.Bass('TRN2', target_bir_lowering=False)
    P = Prog(nc)
    es = contextlib.ExitStack()

    def din(name, shape):
        return nc.dram_tensor(name, list(shape), F32, kind="ExternalInput").ap()

    def dout(name, shape):
        return nc.dram_tensor(name, list(shape), F32, kind="ExternalOutput").ap()

    xin = din("xin", [TT, D])
    condT = din("condT", [128, 8, 2])
    s_hgrn = din("s_hgrn", [2, 8, 128, 128])
    s_rwkv = din("s_rwkv", [2, 16, 64, 64])
    s_C = din("s_C", [2, 4, 512, 512])
    s_n = din("s_n", [2, 4, 512])
    s_m = din("s_m", [2, 4])
    w_mod = din("w_mod", [2, D, 3 * D])
    b_modT = din("b_modT", [128, 2, 24])
    norm_gT = din("norm_gT", [128, 2, 8])
    fnorm_gT = din("fnorm_gT", [128, 8])
    wA = din("wA", [8, D, 640])
    wB = din("wB", [16, D, 256])
    wLR = din("wLR", [D, 256])
    w_out_even = din("w_out_even", [2 * D, D])
    lbT = din("lbT", [128, 2, 8])
    hg_gT = din("hg_gT", [128, 8])
    mu_rkv = din("mu_rkv", [64, 2, 4, 16])
    mu_lr = din("mu_lr", [64, 2, 4])
    w0T = din("w0T", [64, 2, 16])
    a0T = din("a0T", [64, 2, 16])
    w2 = din("w2", [2, 64, D])
    a2 = din("a2", [2, 64, D])
    kkT = din("kkT", [64, 16])
    kaT = din("kaT", [64, 16])
    rkT = din("rkT", [64, 16])
    gngT = din("gngT", [64, 16])
    gnbT = din("gnbT", [64, 16])
    maskR = din("maskR", [2, 3, 64, 128])
    maskH = din("maskH", [2, 128, 128])
    ident_d = din("ident_in", [128, 128])
    w_in_odd = din("w_in_odd", [D, 10256])
    w_out_odd = din("w_out_odd", [2 * D, D])
    maskC = din("maskC", [2, 128, 128])
    sel_d = din("sel_d", [36, 4, 128])
    gbT_d = din("gbT_d", [36, 4])
    cw_d = din("cw_d", [128, 32, 9])
    cb_d = din("cb_d", [128, 32])
    mng_d = din("mng_d", [128, 16])

    yout = dout("yout", [TT, D])
    o_hgrn = dout("o_hgrn", [2, 2, 8, 128, 128])
    o_rwkv = dout("o_rwkv", [2, 2, 16, 64, 64])
    o_C = dout("o_C", [2, 2, 4, 512, 512])
    o_n = dout("o_n", [2, 2, 4, 512])
    o_m = dout("o_m", [2, 2, 4])
    dbg = dout("dbg", [40, 128, TT]) if debug else None
    dslot = {}
    dumpt = {}
    x1 = dout("x1", [TT, D]) if debug else nc.dram_tensor("x1", [TT, D], F32, kind="Internal").ap()

    def sb(name, shape, dt=F32):
        return es.enter_context(nc.sbuf_tensor(name, list(shape), dt))

    pstiles = [es.enter_context(nc.psum_tensor("ps%d" % i, [128, 512], F32)) for i in range(8)]
    psrr = [0]

    def nps():
        i = psrr[0]
        psrr[0] = (i + 1) % 8
        return pstiles[i], 'ps%d' % i

    def dump(name, ap, keys, n, col0=0, parts=128):
        if not debug:
            return
        slot = dslot.setdefault(name, len(dslot))
        dt_ = dumpt['tile']
        for c0 in range(0, n, 256):
            w_ = min(256, n - c0)
            P.op('pool', (lambda e, c0=c0, w_=w_: e.tensor_copy(out=dt_[0:parts, 0:w_], in_=ap[:, c0:c0 + w_])),
                 reads=keys, writes=['dumpt'])
            P.dma(dbg[slot, 0:parts, col0 + c0:col0 + c0 + w_], dt_[0:parts, 0:w_], reads=['dumpt'])

    if debug:
        dumpt['tile'] = sb("dumpt", [128, 256])

    ident = sb("ident", [128, 128])
    ones = sb("ones", [128, 128])
    P.dma(ident[:], ident_d[:], writes=['ident'])
    P.op('dve', lambda e: e.memset(ones[:], 1.0), writes=['ones'])

    condT_sb = sb("condT_sb", [128, 8, 2])
    bmod_sb = sb("bmod_sb", [128, 2, 24])
    ng_sb = sb("ng_sb", [128, 2, 8])
    fng_sb = sb("fng_sb", [128, 8])
    lb_sb = sb("lb_sb", [128, 2, 8])
    hgg_sb = sb("hgg_sb", [128, 8])
    for t_, d_, k_ in [(condT_sb, condT, 'condT'), (bmod_sb, b_modT, 'bmod'), (ng_sb, norm_gT, 'ng'),
                       (fng_sb, fnorm_gT, 'fng'), (lb_sb, lbT, 'lb'), (hgg_sb, hg_gT, 'hgg')]:
        P.dma(t_[:], d_[:], writes=[k_])

    scT = sb("scT", [128, 8, 2])
    P.op('act', lambda e: e.activation(out=scT[:], in_=condT_sb[:], func=AF.Silu), reads=['condT'], writes=['scT'])
    mT = sb("mT", [128, 2, 24, 2])
    sc1 = sb("sc1", [128, 2, 8, 2])
    gate_bc = sb("gate_bc", [128, D])
    dg = sb("dg", [128, 128])

    def make_gate(l, c):
        if True:
            for half in range(2):
                pz, pk = nps()
                for kq in range(4):
                    kc = half * 4 + kq
                    P.op('dve', lambda e: e.tensor_scalar(
                        out=dg[:], in0=ident[:], scalar1=mT[:, l, 16 + kc, c:c + 1], scalar2=None, op0=ALU.mult),
                        reads=['ident', 'mT'], writes=['dg'])
                    P.op('pe', lambda e: e.matmul(pz[:, kq * 128:(kq + 1) * 128], ones[:], dg[:], start=True, stop=True),
                         reads=['ones', 'dg'], writes=[pk])
                P.op('act', lambda e: e.copy(out=gate_bc[:, half * 512:(half + 1) * 512], in_=pz[:]),
                     reads=[pk], writes=['gate_bc'])

    with contextlib.ExitStack() as es2:
        wm = [es2.enter_context(nc.sbuf_tensor("wm%d" % i, [128, 8, 512], F32)) for i in range(2)]
        for l in range(2):
            for cbk in range(6):
                i = (l * 6 + cbk) % 2
                P.dma(wm[i][:], w_mod[l].rearrange("(kc p) n -> p kc n", p=128)[:, :, cbk * 512:(cbk + 1) * 512],
                      writes=['wm%d' % i])
                pz, pk = nps()
                for j in range(4):
                    for kc in range(8):
                        P.op('pe', lambda e: e.matmul(pz[:, j * 2:(j + 1) * 2], wm[i][:, kc, j * 128:(j + 1) * 128],
                                                      scT[:, kc, :], start=(kc == 0), stop=(kc == 7)),
                             reads=['wm%d' % i, 'scT'], writes=[pk])
                P.op('dve', lambda e: e.tensor_tensor(out=mT[:, l, cbk * 4:(cbk + 1) * 4, :],
                                                      in0=pz[:, 0:8].rearrange("p (j c) -> p j c", c=2),
                                                      in1=bmod_sb[:, l, cbk * 4:(cbk + 1) * 4].unsqueeze(2).to_broadcast([128, 4, 2]),
                                                      op=ALU.add),
                     reads=[pk, 'bmod'], writes=['mT'])
    P.barrier()
    for l in range(2):
        P.op('dve', lambda e: e.scalar_tensor_tensor(
            out=sc1[:, l], in0=mT[:, l, 8:16, :], scalar=1.0,
            in1=ng_sb[:, l, :].unsqueeze(2).to_broadcast([128, 8, 2]), op0=ALU.add, op1=ALU.mult),
            reads=['mT', 'ng'], writes=['sc1'])
    fng_holder = {}

    def make_fng():
        fng_bc = sb("fng_bc", [128, D])
        fng_holder['t'] = fng_bc
        for half in range(2):
            pz, pk = nps()
            for kq in range(4):
                kc = half * 4 + kq
                P.op('dve', (lambda e, kc=kc: e.tensor_scalar(
                    out=dg[:], in0=ident[:], scalar1=fng_sb[:, kc:kc + 1], scalar2=None, op0=ALU.mult)),
                    reads=['ident', 'fng'], writes=['dg'])
                P.op('pe', (lambda e, pz=pz, kq=kq: e.matmul(pz[:, kq * 128:(kq + 1) * 128], ones[:], dg[:],
                                                             start=True, stop=True)),
                     reads=['ones', 'dg'], writes=[pk])
            P.op('act', (lambda e, half=half, pz=pz: e.copy(out=fng_bc[:, half * 512:(half + 1) * 512], in_=pz[:])),
                 reads=[pk], writes=['fng_bc'])

    lbv = sb("lbv", [128, 8])
    oml = sb("oml", [128, 8])
    P.op('dve', lambda e: e.tensor_tensor(out=lbv[:], in0=lb_sb[:, 0, :], in1=lb_sb[:, 1, :], op=ALU.subtract),
         reads=['lb'], writes=['lbv'])
    P.op('act', lambda e: e.activation(out=lbv[:], in_=lbv[:], func=AF.Sigmoid), reads=['lbv'], writes=['lbv'])
    P.op('act', lambda e: e.activation(out=oml[:], in_=lbv[:], func=AF.Identity, bias=1.0, scale=-1.0),
         reads=['lbv'], writes=['oml'])

    hT = sb("hT", [128, 8, TS], BF16)
    yT = sb("yT", [128, 8, TS], BF16)
    st4 = sb("st4", [128, 4])
    wst = sb("wst", [128, 8, 256])
    FT = [sb("FT%d" % i, [128, TS + 32]) for i in range(6)]
    xt = [FT[0][:, 0:D], FT[0][:, D:2 * D]]
    xn = FT[1][:, 0:D]
    junk = FT[1][:, D:2 * D]
    wo_v = [FT[2][:, 0:TS].bitcast(BF16).rearrange("p (s n) -> p s n", n=D),
            FT[3][:, 0:TS].bitcast(BF16).rearrange("p (s n) -> p s n", n=D)]

    def load_wo(src, parts, nk=8):
        v = src.rearrange("(kc p) n -> p kc n", p=parts)
        for c0 in range(0, D, 256):
            w_ = min(256, D - c0)
            P.dma(wst[0:parts, 0:nk, 0:w_], v[:, :, c0:c0 + w_], writes=['wst'])
            for hf in range(nk // 4):
                P.op('act', lambda e: e.copy(out=wo_v[hf][0:parts, :, c0:c0 + w_], in_=wst[0:parts, hf * 4:hf * 4 + 4, 0:w_]),
                     reads=['wst'], writes=['wo_bf'])

    def make_hT(layer, xsrc, xkey, off, T, cidx):
        for tt in range(T // 128):
            i = tt % 2
            P.dma(xt[i], xsrc[off + tt * 128: off + (tt + 1) * 128, :], reads=[(xkey, off // 128 + tt)], writes=['xt%d' % i])
            P.op('act', lambda e: e.activation(out=junk, in_=xt[i], func=AF.Square, accum_out=st4[:, 0:1]),
                 reads=['xt%d' % i], writes=['junk', 'st4'])
            P.op('dve', lambda e: e.tensor_scalar(out=st4[:, 1:2], in0=st4[:, 0:1], scalar1=1.0 / D, scalar2=1e-6,
                                                  op0=ALU.mult, op1=ALU.add), reads=['st4'], writes=['st4'])
            P.op('act', lambda e: e.activation(out=st4[:, 2:3], in_=st4[:, 1:2], func=AF.Sqrt), reads=['st4'], writes=['st4'])
            P.op('dve', lambda e: e.reciprocal(out=st4[:, 3:4], in_=st4[:, 2:3]), reads=['st4'], writes=['st4'])
            P.op('dve', lambda e: e.tensor_scalar(out=xn, in0=xt[i], scalar1=st4[:, 3:4], scalar2=None, op0=ALU.mult),
                 reads=['xt%d' % i, 'st4'], writes=['xn'])
            for half in range(2):
                pz, pk = nps()
                for kq in range(4):
                    kc = half * 4 + kq
                    P.op('pe', lambda e: e.transpose(out=pz[:, kq * 128:(kq + 1) * 128], in_=xn[:, kc * 128:(kc + 1) * 128],
                                                     identity=ident[:]), reads=['xn', 'ident'], writes=[pk])
                for kq in range(4):
                    kc = half * 4 + kq
                    P.op('act', lambda e: e.activation(
                        out=hT[:, kc, tt * 128:(tt + 1) * 128], in_=pz[:, kq * 128:(kq + 1) * 128], func=AF.Identity,
                        bias=mT[:, layer, kc, cidx:cidx + 1], scale=sc1[:, layer, kc, cidx:cidx + 1]),
                        reads=[pk, 'mT', 'sc1'], writes=[('hT', tt)])

    def hT_keys(t0, t1):
        return [('hT', tt) for tt in range(t0 // 128, (t1 + 127) // 128)]

    class WB:
        def __init__(self, tiles, keys):
            self.t, self.k, self.cur, self.pending = tiles, keys, 0, None

        def prefetch(self, tag, loader):
            if self.pending is not None:
                return
            i = 1 - self.cur
            loader(self.t[i], self.k[i])
            self.pending = tag

        def use(self, tag, loader):
            if self.pending == tag:
                self.cur = 1 - self.cur
                self.pending = None
            else:
                assert self.pending is None, (self.pending, tag)
                i = 1 - self.cur
                loader(self.t[i], self.k[i])
                self.cur = i

        @property
        def tile(self):
            return self.t[self.cur]

        @property
        def key(self):
            return self.k[self.cur]

    def load_w(src, ncols, dst=None, dkey='wbf', parts=128, nk=8):
        v = src.rearrange("(kc p) n -> p kc n", p=parts)
        for c0 in range(0, ncols, 256):
            w_ = min(256, ncols - c0)
            P.dma(wst[0:parts, 0:nk, 0:w_], v[:, :, c0:c0 + w_], writes=['wst'])
            P.op('act', lambda e: e.copy(out=dst[0:parts, 0:nk, c0:c0 + w_], in_=wst[0:parts, 0:nk, 0:w_]),
                 reads=['wst'], writes=[dkey])

    def proj(c0, M, t0, n, evac):
        pz, pk = nps()
        for kc in range(8):
            P.op('pe', lambda e: e.matmul(pz[0:M, 0:n], wb0.tile[:, kc, c0:c0 + M], hT[:, kc, t0:t0 + n],
                                          start=(kc == 0), stop=(kc == 7)),
                 reads=[wb0.key] + hT_keys(t0, t0 + n), writes=[pk])
        evac(pz, pk)

    def outproj(layer, groups, xsrc, skey, xdst, dkey, off, T, cidx, final=False):
        for tt in range(T // 128):
            i = tt % 2
            P.dma(xt[i], xsrc[off + tt * 128: off + (tt + 1) * 128, :], reads=[(skey, off // 128 + tt)], writes=['xt%d' % i])
            for half in range(2):
                pz, pk = nps()
                for gi, (K, slot) in enumerate(groups):
                    P.op('pe', lambda e: e.matmul(pz[:, 0:512], yT[0:K, slot, tt * 128:(tt + 1) * 128],
                                                  wo_v[slot // 4][0:K, slot % 4, half * 512:(half + 1) * 512],
                                                  start=(gi == 0), stop=(gi == len(groups) - 1)),
                         reads=[('yT', slot), 'wo_bf'], writes=[pk])
                P.op('dve', lambda e: e.tensor_tensor(out=xn[:, half * 512:(half + 1) * 512], in0=pz[:, 0:512],
                                                      in1=gate_bc[:, half * 512:(half + 1) * 512], op=ALU.mult),
                     reads=[pk, 'gate_bc'], writes=['xn'])
                P.op('dve', lambda e: e.tensor_tensor(out=xt[i][:, half * 512:(half + 1) * 512],
                                                      in0=xt[i][:, half * 512:(half + 1) * 512],
                                                      in1=xn[:, half * 512:(half + 1) * 512], op=ALU.add),
                     reads=['xn', 'xt%d' % i], writes=['xt%d' % i])
            if final:
                P.op('act', lambda e: e.activation(out=junk, in_=xt[i], func=AF.Square, accum_out=st4[:, 0:1]),
                     reads=['xt%d' % i], writes=['junk', 'st4'])
                P.op('dve', lambda e: e.tensor_scalar(out=st4[:, 1:2], in0=st4[:, 0:1], scalar1=1.0 / D, scalar2=1e-6,
                                                      op0=ALU.mult, op1=ALU.add), reads=['st4'], writes=['st4'])
                P.op('act', lambda e: e.activation(out=st4[:, 2:3], in_=st4[:, 1:2], func=AF.Sqrt), reads=['st4'], writes=['st4'])
                P.op('dve', lambda e: e.reciprocal(out=st4[:, 3:4], in_=st4[:, 2:3]), reads=['st4'], writes=['st4'])
                P.op('dve', lambda e: e.scalar_tensor_tensor(out=xt[i], in0=xt[i], scalar=st4[:, 3:4], in1=fng_holder['t'][:],
                                                             op0=ALU.mult, op1=ALU.mult),
                     reads=['xt%d' % i, 'st4', 'fng_bc'], writes=['xt%d' % i])
            P.dma(xdst[off + tt * 128: off + (tt + 1) * 128, :], xt[i], reads=['xt%d' % i], writes=[(dkey, off // 128 + tt)], q='pool')

    L0 = contextlib.ExitStack()

    def sb0(name, shape, dt=F32):
        return L0.enter_context(nc.sbuf_tensor(name, list(shape), dt))

    wb0 = WB([sb0("wbfA", [128, 8, 384], BF16), sb0("wbfB", [128, 8, 384], BF16)], ['wbfA', 'wbfB'])
    TB = 256
    Fq, Fsz, Fvr, For = FT[0][:, 0:TS], FT[1][:, 0:TS], FT[2][:, 0:TS], FT[3][:, 0:TS]
    Fv = Fvr.rearrange("p (j c) -> p j c", c=128)
    Fo = For.rearrange("p (j c) -> p j c", c=128)
    BT = [sb0("BT%d" % i, [128, 256]) for i in range(18)]
    bt = {n_: BT[i] for i, n_ in enumerate(['sg', 'lf', 'kg', 'G', 'br', 'E', 'Ei', 'qt', 'kt', 'kh', 'vT'])}
    khtok = sb0("khtok", [128, TB // 128, 128])
    gam = sb0("gam", [128, TB // 32])
    gref = sb0("gref", [128, TB // 32])
    NS_ = 3 if debug else 5
    Sst = [sb0("Sst%d" % i, [128, 128]) for i in range(NS_)]
    attT = sb0("attT", [128, 128])
    ostat = sb0("ostat", [128, TS // 128, 4])
    mH = sb0("mH", [128, 2, 128])
    P.dma(mH[:], maskH.rearrange("d s t -> s d t"), writes=['mH'])

    def hgrn_head(h, off, T, is_sample, pidx, nxt=None):
        ld_qvz = lambda hh: (lambda dst, dk: load_w(wA[hh][:, 0:384], 384, dst=dst, dkey=dk))
        ld_ffb = lambda hh: (lambda dst, dk: load_w(wA[hh][:, 384:640], 256, dst=dst, dkey=dk))
        wb0.use(('qvz', off, h), ld_qvz(h))
        wb0.prefetch(('ffb', off, h), ld_ffb(h))
        tb = min(TB, T)
        nblk = T // tb
        for b in range(nblk):
            t0 = b * tb
            proj(0, 128, t0, tb, lambda pz, pk: P.op(
                'act', lambda e: e.copy(out=Fq[:, t0:t0 + tb], in_=pz[:, 0:tb]), reads=[pk], writes=['Fq']))
            proj(256, 128, t0, tb, lambda pz, pk: P.op(
                'act', lambda e: e.activation(out=Fsz[:, t0:t0 + tb], in_=pz[:, 0:tb], func=AF.Silu), reads=[pk], writes=['Fsz']))
            proj(128, 128, t0, tb, lambda pz, pk: P.op(
                'dve', lambda e: e.tensor_copy(out=bt['vT'][:, 0:tb], in_=pz[:, 0:tb]), reads=[pk], writes=['b_vT']))
            pz, pk = nps()
            for j in range(tb // 128):
                P.op('pe', lambda e: e.transpose(out=pz[:, j * 128:(j + 1) * 128], in_=bt['vT'][:, j * 128:(j + 1) * 128],
                                                 identity=ident[:]), reads=['b_vT', 'ident'], writes=[pk])
            P.op('dve', lambda e: e.tensor_copy(out=Fv[:, t0 // 128:(t0 + tb) // 128, :],
                                                in_=pz[:, 0:tb].rearrange("p (j c) -> p j c", c=128)),
                 reads=[pk], writes=['Fv'])
        wb0.use(('ffb', off, h), ld_ffb(h))
        if nxt is not None:
            wb0.prefetch(('qvz', off, nxt), ld_qvz(nxt))
        for d in range(2):
            rev = (d == 1)
            cur = 0
            if is_sample:
                P.dma(Sst[0][:], s_hgrn[d, h], writes=['Sst0'])
            else:
                P.op('pool', lambda e: e.memset(Sst[0][:], 0.0), writes=['Sst0'])
            blks = list(range(nblk))
            if rev:
                blks = blks[::-1]
            for b in blks:
                t0 = b * tb
                nch = tb // 32
                sg, lf, kg, G, br, E, Ei, qt, kt, kh = [bt[n_] for n_ in ['sg', 'lf', 'kg', 'G', 'br', 'E', 'Ei', 'qt', 'kt', 'kh']]
                proj(128 * d, 128, t0, tb, lambda pz, pk: P.op(
                    'act', lambda e: e.activation(out=sg[:, 0:tb], in_=pz[:, 0:tb], func=AF.Sigmoid), reads=[pk], writes=['b_sg']))
                P.op('dve', lambda e: e.tensor_scalar(out=sg[:, 0:tb], in0=sg[:, 0:tb], scalar1=oml[:, h:h + 1],
                                                      scalar2=lbv[:, h:h + 1], op0=ALU.mult, op1=ALU.add),
                     reads=['b_sg', 'oml', 'lbv'], writes=['b_sg'])
                P.op('act', lambda e: e.activation(out=lf[:, 0:tb], in_=sg[:, 0:tb], func=AF.Ln), reads=['b_sg'], writes=['b_lf'])
                P.op('dve', lambda e: e.tensor_scalar(out=kg[:, 0:tb], in0=sg[:, 0:tb], scalar1=-1.0, scalar2=1.0,
                                                       op0=ALU.mult, op1=ALU.add), reads=['b_sg'], writes=['b_kg'])
                P.op('dve', lambda e: e.memset(E[:, 0:tb], 0.0), writes=['b_E'])
                if not rev:
                    P.op('dve', lambda e: e.tensor_tensor_scan(out=G[:, 0:tb], data0=lf[:, 0:tb], data1=E[:, 0:tb],
                                                               initial=0.0, op0=ALU.add, op1=ALU.add),
                         reads=['b_lf', 'b_E'], writes=['b_G'])
                    ci_ = 0
                else:
                    P.op('dve', lambda e: e.tensor_tensor_scan(out=G[:, 0:tb][:, ::-1], data0=lf[:, 0:tb][:, ::-1],
                                                               data1=E[:, 0:tb], initial=0.0, op0=ALU.add, op1=ALU.add),
                         reads=['b_lf', 'b_E'], writes=['b_G'])
                    ci_ = 31
                G3 = G[:, 0:tb].rearrange("p (c l) -> p c l", l=32)
                lf3 = lf[:, 0:tb].rearrange("p (c l) -> p c l", l=32)
                P.op('dve', lambda e: e.tensor_tensor(out=gref[:, 0:nch], in0=G3[:, :, ci_], in1=lf3[:, :, ci_], op=ALU.subtract),
                     reads=['b_G', 'b_lf'], writes=['gref'])
                P.op('dve', lambda e: e.tensor_tensor(out=br[:, 0:tb].rearrange("p (c l) -> p c l", l=32), in0=G3,
                                                      in1=gref[:, 0:nch].unsqueeze(2).to_broadcast([128, nch, 32]), op=ALU.subtract),
                     reads=['b_G', 'gref'], writes=['b_br'])
                bend = br[:, 0:tb].rearrange("p (c l) -> p c l", l=32)[:, :, (0 if rev else 31)]
                P.op('act', lambda e: e.activation(out=gam[:, 0:nch], in_=bend, func=AF.Exp), reads=['b_br'], writes=['gam'])
                P.op('act', lambda e: e.activation(out=E[:, 0:tb], in_=br[:, 0:tb], func=AF.Exp), reads=['b_br'], writes=['b_E'])
                P.op('act', lambda e: e.activation(out=Ei[:, 0:tb], in_=br[:, 0:tb], func=AF.Exp, scale=-1.0),
                     reads=['b_br'], writes=['b_Ei'])
                P.op('dve', lambda e: e.tensor_tensor(out=qt[:, 0:tb], in0=Fq[:, t0:t0 + tb], in1=E[:, 0:tb], op=ALU.mult),
                     reads=['Fq', 'b_E'], writes=['b_qt'])
                P.op('dve', lambda e: e.tensor_tensor(out=kt[:, 0:tb], in0=kg[:, 0:tb], in1=Ei[:, 0:tb], op=ALU.mult),
                     reads=['b_kg', 'b_Ei'], writes=['b_kt'])
                P.op('dve', lambda e: e.tensor_tensor(out=kh[:, 0:tb].rearrange("p (c l) -> p c l", l=32),
                                                      in0=kt[:, 0:tb].rearrange("p (c l) -> p c l", l=32),
                                                      in1=gam[:, 0:nch].unsqueeze(2).to_broadcast([128, nch, 32]), op=ALU.mult),
                     reads=['b_kt', 'gam'], writes=['b_kh'])
                if debug and debug.get('inner') and h == head_ids[0]:
                    for n_ in ['lf', 'kg', 'br', 'E', 'qt', 'kt', 'kh']:
                        dump("%s_d%d" % (n_, d), bt[n_][:, 0:tb], ['b_' + n_], tb, col0=off + t0)
                pz, pk = nps()
                for j in range(tb // 128):
                    P.op('pe', lambda e: e.transpose(out=pz[:, j * 128:(j + 1) * 128], in_=kh[:, j * 128:(j + 1) * 128],
                                                     identity=ident[:]), reads=['b_kh', 'ident'], writes=[pk])
                P.op('act', lambda e: e.copy(out=khtok[:, 0:tb // 128, :], in_=pz[:, 0:tb].rearrange("p (j c) -> p j c", c=128)),
                     reads=[pk], writes=['khtok'])
                tiles = list(range(tb // 128))
                if rev:
                    tiles = tiles[::-1]
                for j in tiles:
                    tg = t0 // 128 + j
                    pa, pak = nps()
                    P.op('pe', lambda e: e.matmul(pa[:, 0:128], kt[:, j * 128:(j + 1) * 128], qt[:, j * 128:(j + 1) * 128],
                                                  start=True, stop=True), reads=['b_kt', 'b_qt'], writes=[pak])
                    P.op('dve', lambda e: e.tensor_tensor(out=attT[:], in0=pa[:, 0:128], in1=mH[:, d, :], op=ALU.mult),
                         reads=[pak, 'mH'], writes=['attT'])
                    po, pok = nps()
                    P.op('pe', lambda e: e.matmul(po[:, 0:128], attT[:], Fv[:, tg, :], start=True, stop=False),
                         reads=['attT', 'Fv'], writes=[pok])
                    chs = [0, 1, 2, 3]
                    if rev:
                        chs = chs[::-1]
                    pds = []
                    for c in chs:
                        pd, pdk = nps()
                        P.op('pe', lambda e: e.matmul(
                            pd[:, 0:128], khtok[32 * c:32 * c + 32, j, :], Fv[32 * c:32 * c + 32, tg, :],
                            start=True, stop=True, tile_position=(32 * c, 0)),
                            reads=['khtok', 'Fv'], writes=[pdk])
                        pds.append((pd, pdk))
                    for ci, c in enumerate(chs):
                        nxt_ = (cur + 1) % NS_
                        Scur = Sst[cur]
                        Snew = Sst[nxt_]
                        pd, pdk = pds[ci]
                        P.op('pe', lambda e: e.matmul(
                            po[32 * c:32 * c + 32, 0:128], qt[:, j * 128 + 32 * c: j * 128 + 32 * c + 32], Scur[:],
                            start=False, stop=(ci == 3), tile_position=(0, 32 * c)),
                            reads=['b_qt', 'Sst%d' % cur], writes=[pok])
                        gidx = j * 4 + c
                        P.op('dve', lambda e: e.scalar_tensor_tensor(
                            out=Snew[:], in0=Scur[:], scalar=gam[:, gidx:gidx + 1], in1=pd[:, 0:128],
                            op0=ALU.mult, op1=ALU.add),
                            reads=['Sst%d' % cur, 'gam', pdk], writes=['Sst%d' % nxt_])
                        cur = nxt_
                    if d == 0:
                        P.op('act', lambda e: e.copy(out=Fo[:, tg, :], in_=po[:, 0:128]), reads=[pok], writes=[('Fo', tg)])
                    else:
                        P.op('dve', lambda e: e.tensor_tensor(out=Fo[:, tg, :], in0=Fo[:, tg, :], in1=po[:, 0:128], op=ALU.add),
                             reads=[pok, ('Fo', tg)], writes=[('Fo', tg)])
            if not is_sample:
                P.dma(o_hgrn[pidx, d, h], Sst[cur][:], reads=['Sst%d' % cur], q='pool')
        for tg in range(T // 128):
            P.op('act', lambda e: e.activation(out=attT[:], in_=Fo[:, tg, :], func=AF.Square, accum_out=ostat[:, tg, 0:1]),
                 reads=[('Fo', tg)], writes=['attT', ('ostat', tg)])
            P.op('dve', lambda e: e.tensor_scalar(out=ostat[:, tg, 1:2], in0=ostat[:, tg, 0:1], scalar1=1.0 / 128,
                                                  scalar2=1e-6, op0=ALU.mult, op1=ALU.add),
                 reads=[('ostat', tg)], writes=[('ostat', tg)])
            P.op('act', lambda e: e.activation(out=ostat[:, tg, 2:3], in_=ostat[:, tg, 1:2], func=AF.Sqrt),
                 reads=[('ostat', tg)], writes=[('ostat', tg)])
            P.op('dve', lambda e: e.reciprocal(out=ostat[:, tg, 3:4], in_=ostat[:, tg, 2:3]),
                 reads=[('ostat', tg)], writes=[('ostat', tg)])
            P.op('dve', lambda e: e.tensor_scalar(out=Fo[:, tg, :], in0=Fo[:, tg, :], scalar1=ostat[:, tg, 3:4],
                                                  scalar2=None, op0=ALU.mult),
                 reads=[('Fo', tg), ('ostat', tg)], writes=[('Fo', tg)])
        n4 = min(4, T // 128)
        for g4 in range(T // (128 * n4)):
            pz, pk = nps()
            for j in range(n4):
                tg = g4 * n4 + j
                P.op('pe', lambda e: e.transpose(out=pz[:, j * 128:(j + 1) * 128], in_=Fo[:, tg, :], identity=ident[:]),
                     reads=[('Fo', tg), 'ident'], writes=[pk])
            w_ = n4 * 128
            P.op('dve', lambda e: e.scalar_tensor_tensor(
                out=yT[:, h, g4 * w_:(g4 + 1) * w_], in0=pz[:, 0:w_], scalar=hgg_sb[:, h:h + 1],
                in1=Fsz[:, g4 * w_:(g4 + 1) * w_], op0=ALU.mult, op1=ALU.mult),
                reads=[pk, 'hgg', 'Fsz'], writes=[('yT', h)])

    TR = 256
    LWC = -0.6065306597126334
    LR = [sb0("LR%d" % g, [64, TS], BF16) for g in range(4)]
    rb = {n_: BT[i][0:64, :] for i, n_ in enumerate(
          ['lw', 'a', 'kk', 'kq', 'kap', 'kd', 'b', 'rk', 'G', 'br', 'E', 'Ei', 'Em', 'bh', 'kh', 'Kb', 'Bb', 't1'])}
    KR = sb0("r_KR", [64, 2, TR])
    cset = [{n_: sb0("c%d_%s" % (i_, n_), [64, (128 if n_ in ('AB', 'BB') else 64)],
                     (BF16 if n_ in ('XTa', 'XTb', 'Xa', 'Xb', 'Pm0', 'Pm1') else F32))
             for n_ in ['AB', 'BB', 'XT0', 'XTa', 'XTb', 'Xa', 'Xb', 'Pm0', 'Pm1', 'Vt', 'Kt', 'Bt']} for i_ in range(4)]
    rsq = {n_: sb0("rq_" + n_, [64, 64]) for n_ in ['U', 'Z0', 'Z1', 'zt']}
    rsq['Wb'] = sb0("rq_Wb", [64, 64], BF16)
    rgam = sb0("rgam", [64, 8])
    rgref = sb0("rgref", [64, 4])
    mR = sb0("mR", [64, 2, 3, 128])
    P.dma(mR[:], maskR.rearrange("d m s t -> s d m t"), writes=['mR'])
    prm = {}
    for n_, src_, shp in [('mu_rkv', mu_rkv, [64, 2, 4, 16]), ('mu_lr', mu_lr, [64, 2, 4]), ('w0', w0T, [64, 2, 16]),
                          ('a0', a0T, [64, 2, 16]), ('kk', kkT, [64, 16]), ('ka', kaT, [64, 16]), ('rk', rkT, [64, 16]),
                          ('gng', gngT, [64, 16]), ('gnb', gnbT, [64, 16])]:
        prm[n_] = sb0("p_" + n_, shp)
        P.dma(prm[n_][:], src_[:], writes=['p_' + n_])
    c0_rkv = sb0("c0_rkv", [64, 4, 16])
    c0_lr = sb0("c0_lr", [64, 4])
    omka = sb0("omka", [64, 16])
    P.op('dve', lambda e: e.tensor_tensor(out=c0_rkv[:], in0=prm['mu_rkv'][:, 0], in1=prm['mu_rkv'][:, 1], op=ALU.add),
         reads=['p_mu_rkv'], writes=['c0_rkv'])
    P.op('dve', lambda e: e.tensor_scalar(out=c0_rkv[:], in0=c0_rkv[:], scalar1=-1.0, scalar2=1.0, op0=ALU.mult, op1=ALU.add),
         reads=['c0_rkv'], writes=['c0_rkv'])
    P.op('dve', lambda e: e.tensor_tensor(out=c0_lr[:], in0=prm['mu_lr'][:, 0], in1=prm['mu_lr'][:, 1], op=ALU.add),
         reads=['p_mu_lr'], writes=['c0_lr'])
    P.op('dve', lambda e: e.tensor_scalar(out=c0_lr[:], in0=c0_lr[:], scalar1=-1.0, scalar2=1.0, op0=ALU.mult, op1=ALU.add),
         reads=['c0_lr'], writes=['c0_lr'])
    P.op('dve', lambda e: e.tensor_scalar(out=omka[:], in0=prm['ka'][:], scalar1=-1.0, scalar2=1.0, op0=ALU.mult, op1=ALU.add),
         reads=['p_ka'], writes=['omka'])
    w2a2 = sb0("w2a2", [64, 4, D], BF16)
    for g, src_ in enumerate([w2[0], w2[1], a2[0], a2[1]]):
        for c0 in range(0, D, 256):
            P.dma(wst[0:64, 0, 0:256], src_[:, c0:c0 + 256], writes=['wst'])
            P.op('pool', lambda e: e.tensor_copy(out=w2a2[:, g, c0:c0 + 256], in_=wst[0:64, 0, 0:256]), reads=['wst'], writes=['w2a2'])

    def shift_into(dst, dkey, raw, rkey, T, c0ap, m0ap, m1ap, t1tile, eng='dve'):
        for s0 in range(0, T, 512):
            n = min(512, T - s0)
            P.op('dve', lambda e: e.tensor_scalar(out=t1tile[:, 0:n], in0=raw[:, 16 + s0:16 + s0 + n], scalar1=c0ap, scalar2=None,
                                                op0=ALU.mult), reads=[rkey], writes=['shift_t'])
            P.op('dve', lambda e: e.scalar_tensor_tensor(out=t1tile[:, 0:n], in0=raw[:, 15 + s0:15 + s0 + n], scalar=m0ap,
                                                       in1=t1tile[:, 0:n], op0=ALU.mult, op1=ALU.add),
                 reads=[rkey, 'shift_t'], writes=['shift_t'])
            P.op('dve', lambda e: e.scalar_tensor_tensor(out=dst[:, s0:s0 + n], in0=raw[:, 17 + s0:17 + s0 + n], scalar=m1ap,
                                                       in1=t1tile[:, 0:n], op0=ALU.mult, op1=ALU.add),
                 reads=[rkey, 'shift_t'], writes=[dkey])

    shiftt = sb0("shiftt", [64, 512])

    def rwkv_seq_setup(off, T):
        wb0.use(('lr', off), lambda dst, dk: load_w(wLR, 256, dst=dst, dkey=dk))
        pb = min(512, T)
        for g in range(4):
            raw = FT[g]
            P.op('pool', lambda e: e.memset(raw[0:64, 15:16], 0.0), writes=['FT%d' % g])
            P.op('pool', lambda e: e.memset(raw[0:64, T + 16:T + 17], 0.0), writes=['FT%d' % g])
            for b in range(T // pb):
                t0 = b * pb
                proj(64 * g, 64, t0, pb, lambda pz, pk: P.op(
                    'act', lambda e: e.copy(out=raw[0:64, 16 + t0:16 + t0 + pb], in_=pz[0:64, 0:pb]), reads=[pk], writes=['FT%d' % g]))
            shift_into(FT[4][0:64, :], 'FT4', raw[0:64, :], 'FT%d' % g, T, c0_lr[:, g:g + 1], prm['mu_lr'][:, 0, g:g + 1],
                       prm['mu_lr'][:, 1, g:g + 1], shiftt)
            if g < 2:
                P.op('act', lambda e: e.activation(out=LR[g][:, 0:T], in_=FT[4][0:64, 0:T], func=AF.Tanh), reads=['FT4'], writes=['LR%d' % g])
            else:
                P.op('act', lambda e: e.copy(out=LR[g][:, 0:T], in_=FT[4][0:64, 0:T]), reads=['FT4'], writes=['LR%d' % g])

    def rwkv_head(h, slot, off, T, is_sample, pidx, nxt=None):
        P.barrier()
        ld_b = lambda hh: (lambda dst, dk: load_w(wB[hh], 256, dst=dst, dkey=dk))
        wb0.use(('wB', off, h), ld_b(h))
        pb = min(512, T)
        nchT = T // 64
        for g in range(3):
            raw = FT[g]
            P.op('pool', lambda e: e.memset(raw[0:64, 15:16], 0.0), writes=['FT%d' % g])
            P.op('pool', lambda e: e.memset(raw[0:64, T + 16:T + 17], 0.0), writes=['FT%d' % g])
            for b in range(T // pb):
                t0 = b * pb
                proj(64 * g, 64, t0, pb, lambda pz, pk: P.op(
                    'act', lambda e: e.copy(out=raw[0:64, 16 + t0:16 + t0 + pb], in_=pz[0:64, 0:pb]), reads=[pk], writes=['FT%d' % g]))
            shift_into(FT[3 + g][0:64, :], 'FT%d' % (3 + g), raw[0:64, :], 'FT%d' % g, T, c0_rkv[:, g, h:h + 1],
                       prm['mu_rkv'][:, 0, g, h:h + 1], prm['mu_rkv'][:, 1, g, h:h + 1], shiftt, eng=('dve' if g != 1 else 'pool'))
        rS, kS, vS = FT[3][0:64, :], FT[4][0:64, :], FT[5][0:64, :]
        szb, yaccr, bonus = FT[0][0:64, :], FT[1][0:64, 0:T], FT[2][0:64, :]
        yacc = yaccr.rearrange("p (c v) -> p c v", v=64)
        for b in range(T // pb):
            t0 = b * pb
            proj(192, 64, t0, pb, lambda pz, pk: P.op(
                'act', lambda e: e.activation(out=szb[:, t0:t0 + pb], in_=pz[0:64, 0:pb], func=AF.Silu), reads=[pk], writes=['FT0']))
        tb = min(TR, T)
        nblk = T // tb
        if nxt is not None:
            wb0.prefetch(('wB', off, nxt), ld_b(nxt))
        P.barrier()
        for d in range(2):
            rev = (d == 1)
            cur = 0
            Zt = [rsq['Z0'], rsq['Z1']]
            if is_sample:
                P.dma(rsq['zt'][:], s_rwkv[d, h], writes=['rq_zt'])
                pz, pk = nps()
                P.op('pe', lambda e: e.transpose(out=pz[0:64, 0:64], in_=rsq['zt'][:], identity=ident[0:64, 0:64]),
                     reads=['rq_zt', 'ident'], writes=[pk])
                P.op('act', lambda e: e.copy(out=Zt[0][:], in_=pz[0:64, 0:64]), reads=[pk], writes=['rq_Z0'])
            else:
                P.op('pool', lambda e: e.memset(Zt[0][:], 0.0), writes=['rq_Z0'])
            blks = list(range(nblk))
            if rev:
                blks = blks[::-1]
            for b in blks:
                t0 = b * tb
                sl = slice(t0, t0 + tb)
                nch = tb // 64
                R_ = rb
                pz, pk = nps()
                P.op('pe', lambda e: e.matmul(pz[0:64, 0:tb], w2a2[:, d, h * 64:(h + 1) * 64], LR[d][:, sl], start=True, stop=True),
                     reads=['w2a2', 'LR%d' % d], writes=[pk])
                P.op('act', lambda e: e.activation(out=R_['lw'][:, 0:tb], in_=pz[0:64, 0:tb], func=AF.Sigmoid,
                                                   bias=prm['w0'][:, d, h:h + 1], scale=1.0), reads=[pk, 'p_w0'], writes=['r_lw'])
                pz, pk = nps()
                P.op('pe', lambda e: e.matmul(pz[0:64, 0:tb], w2a2[:, 2 + d, h * 64:(h + 1) * 64], LR[2 + d][:, sl], start=True, stop=True),
                     reads=['w2a2', 'LR%d' % (2 + d)], writes=[pk])
                P.op('act', lambda e: e.activation(out=R_['a'][:, 0:tb], in_=pz[0:64, 0:tb], func=AF.Sigmoid,
                                                   bias=prm['a0'][:, d, h:h + 1], scale=1.0), reads=[pk, 'p_a0'], writes=['r_a'])
                P.op('dve', lambda e: e.tensor_scalar(out=R_['kk'][:, 0:tb], in0=kS[:, sl], scalar1=prm['kk'][:, h:h + 1],
                                                      scalar2=None, op0=ALU.mult), reads=['FT4', 'p_kk'], writes=['r_kk'])
                P.op('dve', lambda e: e.tensor_tensor(out=R_['kq'][:, 0:tb], in0=R_['kk'][:, 0:tb], in1=R_['kk'][:, 0:tb], op=ALU.mult),
                     reads=['r_kk'], writes=['r_kq'])
                pz, pk = nps()
                P.op('pe', lambda e: e.matmul(pz[0:64, 0:tb], ones[0:64, 0:64], R_['kq'][:, 0:tb], start=True, stop=True),
                     reads=['ones', 'r_kq'], writes=[pk])
                P.op('dve', lambda e: e.tensor_scalar(out=R_['kq'][:, 0:tb], in0=pz[0:64, 0:tb], scalar1=1e-24, scalar2=None,
                                                      op0=ALU.max), reads=[pk], writes=['r_kq'])
                P.op('act', lambda e: e.activation(out=R_['kq'][:, 0:tb], in_=R_['kq'][:, 0:tb], func=AF.Ln), reads=['r_kq'], writes=['r_kq'])
                P.op('act', lambda e: e.activation(out=R_['kq'][:, 0:tb], in_=R_['kq'][:, 0:tb], func=AF.Exp, scale=-0.5),
                     reads=['r_kq'], writes=['r_kq'])
                P.op('dve', lambda e: e.tensor_tensor(out=R_['kap'][:, 0:tb], in0=R_['kk'][:, 0:tb], in1=R_['kq'][:, 0:tb], op=ALU.mult),
                     reads=['r_kk', 'r_kq'], writes=['r_kap'])
                P.op('dve', lambda e: e.tensor_scalar(out=R_['t1'][:, 0:tb], in0=R_['a'][:, 0:tb], scalar1=prm['ka'][:, h:h + 1],
                                                       scalar2=omka[:, h:h + 1], op0=ALU.mult, op1=ALU.add),
                     reads=['r_a', 'p_ka', 'omka'], writes=['r_t1'])
                P.op('dve', lambda e: e.tensor_tensor(out=R_['kd'][:, 0:tb], in0=kS[:, sl], in1=R_['t1'][:, 0:tb], op=ALU.mult),
                     reads=['FT4', 'r_t1'], writes=['r_kd'])
                P.op('dve', lambda e: e.tensor_tensor(out=R_['b'][:, 0:tb], in0=R_['a'][:, 0:tb], in1=R_['kap'][:, 0:tb], op=ALU.mult),
                     reads=['r_a', 'r_kap'], writes=['r_b'])
                P.op('dve', lambda e: e.scalar_tensor_tensor(out=R_['rk'][:, 0:tb], in0=rS[:, sl], scalar=prm['rk'][:, h:h + 1],
                                                             in1=R_['kd'][:, 0:tb], op0=ALU.mult, op1=ALU.mult),
                     reads=['FT3', 'p_rk', 'r_kd'], writes=['r_rk'])
                pz, pk = nps()
                P.op('pe', lambda e: e.matmul(pz[0:64, 0:tb], ones[0:64, 0:64], R_['rk'][:, 0:tb], start=True, stop=True),
                     reads=['ones', 'r_rk'], writes=[pk])
                if d == 0:
                    P.op('dve', lambda e: e.tensor_tensor(out=bonus[:, sl], in0=pz[0:64, 0:tb], in1=vS[:, sl], op=ALU.mult),
                         reads=[pk, 'FT5'], writes=[('bonus', b)])
                else:
                    P.op('dve', lambda e: e.tensor_tensor(out=R_['rk'][:, 0:tb], in0=pz[0:64, 0:tb], in1=vS[:, sl], op=ALU.mult),
                         reads=[pk, 'FT5'], writes=['r_rk'])
                    P.op('dve', lambda e: e.tensor_tensor(out=bonus[:, sl], in0=bonus[:, sl], in1=R_['rk'][:, 0:tb], op=ALU.add),
                         reads=['r_rk', ('bonus', b)], writes=[('bonus', b)])
                G, br, E, Ei, Em = R_['G'], R_['br'], R_['E'], R_['Ei'], R_['Em']
                P.op('dve', lambda e: e.memset(E[:, 0:tb], 0.0), writes=['r_E'])
                if not rev:
                    P.op('dve', lambda e: e.tensor_tensor_scan(out=G[:, 0:tb], data0=R_['lw'][:, 0:tb], data1=E[:, 0:tb],
                                                               initial=0.0, op0=ALU.add, op1=ALU.add),
                         reads=['r_lw', 'r_E'], writes=['r_G'])
                    ci_ = 0
                else:
                    P.op('dve', lambda e: e.tensor_tensor_scan(out=G[:, 0:tb][:, ::-1], data0=R_['lw'][:, 0:tb][:, ::-1],
                                                               data1=E[:, 0:tb], initial=0.0, op0=ALU.add, op1=ALU.add),
                         reads=['r_lw', 'r_E'], writes=['r_G'])
                    ci_ = 63
                G3 = G[:, 0:tb].rearrange("p (c l) -> p c l", l=64)
                lw3 = R_['lw'][:, 0:tb].rearrange("p (c l) -> p c l", l=64)
                P.op('dve', lambda e: e.tensor_tensor(out=rgref[:, 0:nch], in0=G3[:, :, ci_], in1=lw3[:, :, ci_], op=ALU.subtract),
                     reads=['r_G', 'r_lw'], writes=['rgref'])
                P.op('dve', lambda e: e.tensor_tensor(out=br[:, 0:tb].rearrange("p (c l) -> p c l", l=64), in0=G3,
                                                      in1=rgref[:, 0:nch].unsqueeze(2).to_broadcast([64, nch, 64]), op=ALU.subtract),
                     reads=['r_G', 'rgref'], writes=['r_br'])
                bend = br[:, 0:tb].rearrange("p (c l) -> p c l", l=64)[:, :, (0 if rev else 63)]
                P.op('act', lambda e: e.activation(out=rgam[:, 0:nch], in_=bend, func=AF.Exp, scale=LWC), reads=['r_br'], writes=['rgam'])
                P.op('dve', lambda e: e.tensor_scalar(out=rgam[:, 4:4 + nch], in0=rgam[:, 0:nch], scalar1=-1.0, scalar2=None,
                                                       op0=ALU.mult), reads=['rgam'], writes=['rgam'])
                P.op('act', lambda e: e.activation(out=E[:, 0:tb], in_=br[:, 0:tb], func=AF.Exp, scale=LWC), reads=['r_br'], writes=['r_E'])
                P.op('act', lambda e: e.activation(out=Ei[:, 0:tb], in_=br[:, 0:tb], func=AF.Exp, scale=-LWC),
                     reads=['r_br'], writes=['r_Ei'])
                P.op('dve', lambda e: e.tensor_tensor(out=R_['t1'][:, 0:tb], in0=br[:, 0:tb], in1=R_['lw'][:, 0:tb], op=ALU.subtract),
                     reads=['r_br', 'r_lw'], writes=['r_t1'])
                P.op('act', lambda e: e.activation(out=Em[:, 0:tb], in_=R_['t1'][:, 0:tb], func=AF.Exp, scale=LWC), reads=['r_t1'], writes=['r_Em'])
                P.op('dve', lambda e: e.tensor_tensor(out=KR[:, 0, 0:tb], in0=R_['kap'][:, 0:tb], in1=Em[:, 0:tb], op=ALU.mult),
                     reads=['r_kap', 'r_Em'], writes=['r_KR'])
                P.op('dve', lambda e: e.tensor_tensor(out=KR[:, 1, 0:tb], in0=rS[:, sl], in1=E[:, 0:tb], op=ALU.mult),
                     reads=['FT3', 'r_E', 'r_KR'], writes=['r_KR'])
                P.op('dve', lambda e: e.tensor_tensor(out=R_['bh'][:, 0:tb], in0=R_['b'][:, 0:tb], in1=Ei[:, 0:tb], op=ALU.mult),
                     reads=['r_b', 'r_Ei'], writes=['r_bh'])
                P.op('dve', lambda e: e.tensor_tensor(out=R_['kh'][:, 0:tb], in0=R_['kd'][:, 0:tb], in1=Ei[:, 0:tb], op=ALU.mult),
                     reads=['r_kd', 'r_Ei'], writes=['r_kh'])
                P.op('dve', lambda e: e.tensor_tensor(out=R_['Kb'][:, 0:tb].rearrange("p (c l) -> p c l", l=64),
                                                      in0=R_['kh'][:, 0:tb].rearrange("p (c l) -> p c l", l=64),
                                                      in1=rgam[:, 0:nch].unsqueeze(2).to_broadcast([64, nch, 64]), op=ALU.mult),
                     reads=['r_kh', 'rgam'], writes=['r_Kb'])
                P.op('dve', lambda e: e.tensor_tensor(out=R_['Bb'][:, 0:tb].rearrange("p (c l) -> p c l", l=64),
                                                      in0=R_['bh'][:, 0:tb].rearrange("p (c l) -> p c l", l=64),
                                                      in1=rgam[:, 4:4 + nch].unsqueeze(2).to_broadcast([64, nch, 64]), op=ALU.mult),
                     reads=['r_bh', 'rgam'], writes=['r_Bb'])
                chs = list(range(nch))
                if rev:
                    chs = chs[::-1]
                st = {}
                for c in chs:
                    cs = slice(c * 64, (c + 1) * 64)
                    C_ = cset[c]
                    ck = (lambda n_, c=c: 'c%d_%s' % (c, n_))
                    pA, pAk = nps()
                    P.op('pe', lambda e: e.matmul(pA[0:64, 0:128], R_['bh'][:, cs], KR[:, :, cs], start=True, stop=True),
                         reads=['r_bh', 'r_KR'], writes=[pAk])
                    P.op('pe', lambda e: e.matmul(pA[0:64, 128:256], R_['kh'][:, cs], KR[:, :, cs], start=True, stop=True),
                         reads=['r_kh', 'r_KR'], writes=[pAk])
                    P.op('pe', lambda e: e.matmul(pA[0:64, 256:320], KR[:, 0, cs], R_['bh'][:, cs], start=True, stop=True),
                         reads=['r_bh', 'r_KR'], writes=[pAk])
                    AB, BB = C_['AB'], C_['BB']
                    P.op('dve', lambda e: e.tensor_tensor(out=AB[:], in0=pA[0:64, 0:128], in1=mR[:, d, 0, :], op=ALU.mult),
                         reads=[pAk, 'mR'], writes=[ck('AB')])
                    P.op('dve', lambda e: e.tensor_tensor(out=BB[:], in0=pA[0:64, 128:256], in1=mR[:, d, 1, :], op=ALU.mult),
                         reads=[pAk, 'mR'], writes=[ck('BB')])
                    P.op('dve', lambda e: e.tensor_tensor(out=C_['XT0'][:], in0=pA[0:64, 256:320], in1=mR[:, d, 2, 0:64], op=ALU.mult),
                         reads=[pAk, 'mR'], writes=[ck('XT0')])
                    P.op('dve', lambda e: e.tensor_tensor(out=C_['Pm0'][:], in0=AB[:, 0:64], in1=ident[0:64, 0:64], op=ALU.add),
                         reads=[ck('AB'), 'ident'], writes=[ck('Pm0')])
                    st[c] = dict(X=AB[:, 0:64], Xk=ck('AB'), XT=C_['XT0'], XTk=ck('XT0'), xti=0, pmi=0)
                for lev in range(5):
                    for c in chs:
                        C_ = cset[c]
                        s_ = st[c]
                        ck = (lambda n_, c=c: 'c%d_%s' % (c, n_))
                        X, Xk, XT, XTk = s_['X'], s_['Xk'], s_['XT'], s_['XTk']
                        pq, pqk = nps()
                        nXTn = 'XTa' if lev % 2 == 0 else 'XTb'
                        nXT, nXTk = C_[nXTn], ck(nXTn)
                        P.op('pe', lambda e: e.matmul(pq[0:64, 64:128], X, XT[:], start=True, stop=True), reads=[Xk, XTk], writes=[pqk])
                        if lev < 4:
                            P.op('pe', lambda e: e.matmul(pq[0:64, 0:64], XT[:], X, start=True, stop=True), reads=[Xk, XTk], writes=[pqk])
                        P.op('act', lambda e: e.copy(out=nXT[:], in_=pq[0:64, 64:128]), reads=[pqk], writes=[nXTk])
                        if lev < 4:
                            tn = 'Xa' if lev % 2 == 0 else 'Xb'
                            P.op('act', lambda e: e.copy(out=C_[tn][:], in_=pq[0:64, 0:64]), reads=[pqk], writes=[ck(tn)])
                            s_['X'], s_['Xk'] = C_[tn][:], ck(tn)
                        s_['XT'], s_['XTk'], s_['xti'] = nXT, nXTk, 1 - s_['xti']
                    for c in chs:
                        C_ = cset[c]
                        s_ = st[c]
                        ck = (lambda n_, c=c: 'c%d_%s' % (c, n_))
                        nXT, nXTk = s_['XT'], s_['XTk']
                        pmi = s_['pmi']
                        Pc, Pn = C_['Pm%d' % pmi], C_['Pm%d' % (1 - pmi)]
                        pp, ppk = nps()
                        P.op('pe', lambda e: e.matmul(pp[0:64, 0:64], nXT[:], Pc[:], start=True, stop=True),
                             reads=[nXTk, ck('Pm%d' % pmi)], writes=[ppk])
                        P.op('dve', lambda e: e.tensor_tensor(out=Pn[:], in0=pp[0:64, 0:64], in1=Pc[:], op=ALU.add),
                             reads=[ppk, ck('Pm%d' % pmi)], writes=[ck('Pm%d' % (1 - pmi))])
                        s_['pmi'] = 1 - pmi
                for c in chs:
                    cs = slice(c * 64, (c + 1) * 64)
                    gsl = slice(t0 + c * 64, t0 + (c + 1) * 64)
                    C_ = cset[c]
                    pt, ptk = nps()
                    P.op('pe', lambda e: e.transpose(out=pt[0:64, 0:64], in_=vS[:, gsl], identity=ident[0:64, 0:64]),
                         reads=['FT5', 'ident'], writes=[ptk])
                    P.op('pe', lambda e: e.transpose(out=pt[0:64, 64:128], in_=R_['Kb'][:, cs], identity=ident[0:64, 0:64]),
                         reads=['r_Kb', 'ident'], writes=[ptk])
                    P.op('pe', lambda e: e.transpose(out=pt[0:64, 128:192], in_=R_['Bb'][:, cs], identity=ident[0:64, 0:64]),
                         reads=['r_Bb', 'ident'], writes=[ptk])
                    P.op('act', lambda e: e.copy(out=C_['Vt'][:], in_=pt[0:64, 0:64]), reads=[ptk], writes=['c%d_Vt' % c])
                    P.op('act', lambda e: e.copy(out=C_['Kt'][:], in_=pt[0:64, 64:128]), reads=[ptk], writes=['c%d_Kt' % c])
                    P.op('act', lambda e: e.copy(out=C_['Bt'][:], in_=pt[0:64, 128:192]), reads=[ptk], writes=['c%d_Bt' % c])
                for c in chs:
                    cs = slice(c * 64, (c + 1) * 64)
                    cg = (t0 // 64) + c
                    C_ = cset[c]
                    s_ = st[c]
                    AB, BB = C_['AB'], C_['BB']
                    ABk, BBk, Vtk, Ktk, Btk = ['c%d_%s' % (c, n_) for n_ in ('AB', 'BB', 'Vt', 'Kt', 'Bt')]
                    Pm, Pmk = C_['Pm%d' % s_['pmi']], 'c%d_Pm%d' % (c, s_['pmi'])
                    Zc, Zn = Zt[cur], Zt[1 - cur]
                    zck, znk = 'rq_Z%d' % cur, 'rq_Z%d' % (1 - cur)
                    pw, pwk = nps()
                    P.op('pe', lambda e: e.matmul(pw[0:64, 0:64], KR[:, 0, cs], Zc[:], start=True, stop=False),
                         reads=['r_KR', zck], writes=[pwk])
                    P.op('pe', lambda e: e.matmul(pw[0:64, 0:64], BB[:, 0:64], C_['Vt'][:], start=False, stop=True),
                         reads=[BBk, Vtk], writes=[pwk])
                    P.op('act', lambda e: e.copy(out=rsq['Wb'][:], in_=pw[0:64, 0:64]), reads=[pwk], writes=['rq_Wb'])
                    pu, puk = nps()
                    P.op('pe', lambda e: e.matmul(pu[0:64, 0:64], Pm[:], rsq['Wb'][:], start=True, stop=True),
                         reads=[Pmk, 'rq_Wb'], writes=[puk])
                    P.op('act', lambda e: e.copy(out=rsq['U'][:], in_=pu[0:64, 0:64]), reads=[puk], writes=['rq_U'])
                    pzz, pzk = nps()
                    P.op('pe', lambda e: e.matmul(pzz[0:64, 0:64], C_['Kt'][:], C_['Vt'][:], start=True, stop=False),
                         reads=[Ktk, Vtk], writes=[pzk])
                    P.op('pe', lambda e: e.matmul(pzz[0:64, 0:64], C_['Bt'][:], rsq['U'][:], start=False, stop=True),
                         reads=[Btk, 'rq_U'], writes=[pzk])
                    P.op('dve', lambda e: e.scalar_tensor_tensor(out=Zn[:], in0=Zc[:], scalar=rgam[:, c:c + 1], in1=pzz[0:64, 0:64],
                                                                 op0=ALU.mult, op1=ALU.add), reads=[zck, 'rgam', pzk], writes=[znk])
                    py, pyk = nps()
                    P.op('pe', lambda e: e.matmul(py[0:64, 0:64], KR[:, 1, cs], Zc[:], start=True, stop=False),
                         reads=['r_KR', zck], writes=[pyk])
                    P.op('pe', lambda e: e.matmul(py[0:64, 0:64], BB[:, 64:128], C_['Vt'][:], start=False, stop=False),
                         reads=[BBk, Vtk], writes=[pyk])
                    P.op('pe', lambda e: e.matmul(py[0:64, 0:64], AB[:, 64:128], rsq['U'][:], start=False, stop=True),
                         reads=[ABk, 'rq_U'], writes=[pyk])
                    if d == 0:
                        P.op('act', lambda e: e.copy(out=yacc[:, cg, :], in_=py[0:64, 0:64]), reads=[pyk], writes=[('yacc', cg)])
                    else:
                        P.op('dve', lambda e: e.tensor_tensor(out=yacc[:, cg, :], in0=yacc[:, cg, :], in1=py[0:64, 0:64], op=ALU.add),
                             reads=[pyk, ('yacc', cg)], writes=[('yacc', cg)])
                    cur = 1 - cur
            if not is_sample:
                pz, pk = nps()
                P.op('pe', lambda e: e.transpose(out=pz[0:64, 0:64], in_=Zt[cur][:], identity=ident[0:64, 0:64]),
                     reads=['rq_Z%d' % cur, 'ident'], writes=[pk])
                P.op('act', lambda e: e.copy(out=rsq['zt'][:], in_=pz[0:64, 0:64]), reads=[pk], writes=['rq_zt'])
                P.dma(o_rwkv[pidx, d, h], rsq['zt'][:], reads=['rq_zt'], q='pool')
        ykeys = [('yacc', c) for c in range(nchT)]
        gst = ostat[0:64, :, :].rearrange("p a b -> p (a b)")
        P.op('dve', lambda e: e.tensor_reduce(out=gst[:, 0:nchT], in_=yacc, axis=AX.X, op=ALU.add), reads=ykeys, writes=['gst'])
        P.op('dve', lambda e: e.tensor_scalar(out=gst[:, 0:nchT], in0=gst[:, 0:nchT], scalar1=-1.0 / 64, scalar2=None, op0=ALU.mult),
             reads=['gst'], writes=['gst'])
        P.op('dve', lambda e: e.tensor_tensor(out=yacc, in0=yacc, in1=gst[:, 0:nchT].unsqueeze(2).to_broadcast([64, nchT, 64]), op=ALU.add),
             reads=ykeys + ['gst'], writes=ykeys)
        sq = FT[3][0:64, 0:T].rearrange("p (c v) -> p c v", v=64)
        P.op('dve', lambda e: e.tensor_tensor(out=sq, in0=yacc, in1=yacc, op=ALU.mult), reads=ykeys, writes=['FT3'])
        P.op('dve', lambda e: e.tensor_reduce(out=gst[:, 32:32 + nchT], in_=sq, axis=AX.X, op=ALU.add), reads=['FT3'], writes=['gst'])
        P.op('dve', lambda e: e.tensor_scalar(out=gst[:, 32:32 + nchT], in0=gst[:, 32:32 + nchT], scalar1=1.0 / 64, scalar2=64e-5,
                                              op0=ALU.mult, op1=ALU.add), reads=['gst'], writes=['gst'])
        P.op('act', lambda e: e.activation(out=gst[:, 32:32 + nchT], in_=gst[:, 32:32 + nchT], func=AF.Sqrt), reads=['gst'], writes=['gst'])
        P.op('dve', lambda e: e.reciprocal(out=gst[:, 32:32 + nchT], in_=gst[:, 32:32 + nchT]), reads=['gst'], writes=['gst'])
        P.op('dve', lambda e: e.tensor_tensor(out=yacc, in0=yacc, in1=gst[:, 32:32 + nchT].unsqueeze(2).to_broadcast([64, nchT, 64]),
                                              op=ALU.mult), reads=ykeys + ['gst'], writes=ykeys)
        n8 = min(8, nchT)
        for g8 in range(nchT // n8):
            pz, pk = nps()
            for j in range(n8):
                cg = g8 * n8 + j
                P.op('pe', lambda e: e.transpose(out=pz[0:64, j * 64:(j + 1) * 64], in_=yacc[:, cg, :], identity=ident[0:64, 0:64]),
                     reads=[('yacc', cg), 'ident'], writes=[pk])
            w_ = n8 * 64
            gs = slice(g8 * w_, (g8 + 1) * w_)
            P.op('dve', lambda e: e.tensor_scalar(out=shiftt[:, 0:w_], in0=pz[0:64, 0:w_], scalar1=prm['gng'][:, h:h + 1],
                                                  scalar2=prm['gnb'][:, h:h + 1], op0=ALU.mult, op1=ALU.add),
                 reads=[pk, 'p_gng', 'p_gnb'], writes=['shift_t'])
            P.op('dve', lambda e: e.tensor_tensor(out=shiftt[:, 0:w_], in0=shiftt[:, 0:w_], in1=bonus[:, gs], op=ALU.add),
                 reads=['shift_t'] + [('bonus', b) for b in range(nblk)], writes=['shift_t'])
            P.op('dve', lambda e: e.tensor_tensor(out=yT[0:64, slot, gs], in0=shiftt[:, 0:w_], in1=szb[:, gs], op=ALU.mult),
                 reads=['shift_t', 'FT0'], writes=[('yT', slot)])

    seq_ids = debug.get('seqs', [0, 1, 2]) if debug else [0, 1, 2]
    head_ids = debug.get('heads', list(range(8))) if debug else list(range(8))
    rheads = debug.get('rheads', list(range(16))) if debug else list(range(16))
    for si in seq_ids:
        off, T, cidx, is_sample = SEQS[si]
        P.barrier()
        make_gate(0, cidx)
        make_hT(0, xin, 'xin', off, T, cidx)
        P.barrier()
        if debug and debug.get('inner'):
            dump("mT", mT[:, 0].rearrange("p a b -> p (a b)"), ['mT'], 48)
            dump("sc1", sc1[:, 0].rearrange("p a b -> p (a b)"), ['sc1'], 16)
            dump("scT", scT[:].rearrange("p a b -> p (a b)"), ['scT'], 16)
            dump("xn", xn, ['xn'], 1024)
            dump("xt0", xt[0], ['xt0'], 1024)
            for kc in range(8):
                dump("hT%d" % kc, hT[:, kc, 0:T], hT_keys(0, T), T, col0=off)
        for hi_, h in enumerate(head_ids):
            hgrn_head(h, off, T, is_sample, si - 1, nxt=(head_ids[hi_ + 1] if hi_ + 1 < len(head_ids) else None))
            if debug and debug.get('dump_y'):
                dump("yT%d" % h, yT[:, h, 0:T], [('yT', h)], T, col0=off)
        P.barrier()
        load_wo(w_out_even[0:D, :], 128)
        outproj(0, [(128, s_) for s_ in range(8)], xin, 'xin', x1, 'x1', off, T, cidx)
        P.barrier()
        rwkv_seq_setup(off, T)
        for half in range(2):
            P.barrier()
            for slot in range(8):
                h = half * 8 + slot
                if h in rheads:
                    nh_ = h + 1 if (slot < 7 and (h + 1) in rheads) else None
                    rwkv_head(h, slot, off, T, is_sample, si - 1, nxt=nh_)
                    if debug and debug.get('dump_y'):
                        dump("yR%d" % h, yT[0:64, slot, 0:T], [('yT', slot)], T, col0=off, parts=64)
            P.barrier()
            load_wo(w_out_even[D + half * 512: D + (half + 1) * 512, :], 64)
            outproj(0, [(64, s_) for s_ in range(8)], x1, 'x1', x1, 'x1', off, T, cidx)
    P.barrier()
    L0.close()

    if not (debug and debug.get('l0only')):
        L1 = contextlib.ExitStack()

        def sb1(name, shape, dt=F32):
            return L1.enter_context(nc.sbuf_tensor(name, list(shape), dt))

        make_fng()
        LC = 128
        DH = 512
        qT = yT[:, 4:8, :]
        kT = sb1("kT", [128, 4, TS], BF16)
        vch = sb1("vch", [128, DH], BF16)
        Cst = sb1("Cst", [128, 4, DH])
        Cbf = sb1("Cbf", [128, 4, DH], BF16)
        nst = sb1("nst", [128, 8])
        nbf = sb1("nbf", [128, 4], BF16)
        ktok = sb1("ktok", [128, DH], BF16)
        vw = sb1("vw", [128, DH], BF16)
        sTs = sb1("sTs", [128, 128], BF16)
        onesb = sb1("onesb", [128, 1], BF16)
        identb = sb1("identb", [128, 128], BF16)
        mC = sb1("mC", [128, 2, 128])
        SEL = sb1("SEL", [36, 4, 128])
        XA = sb1("XA", [36, TS])
        XB = sb1("XB", [36, TS])
        zrow = sb1("zrow", [36, 512])
        sm = {n_: sb1("sm_" + n_, [36, 16]) for n_ in ['ac', 'bl', 'M', 'MP', 'mu', 'al', 'gref', 'm0']}
        Wtok = sb1("Wtok", [128, 2, 16, 8])
        Wtokb = sb1("Wtokb", [128, 2, 16, 4], BF16)
        ALb = sb1("ALb", [128, 2, 4, 16])
        dstat = sb1("dstat", [128, 8])
        wGb = sb1("wGb", [128, 8, 16], BF16)
        gbT = sb1("gbT", [36, 4])
        ngbT = sb1("ngbT", [36, 4])
        cw = sb1("cw", [128, 32, 9])
        cb = sb1("cb", [128, 32])
        mng = sb1("mng", [128, 16])
        wb1 = WB([sb1("wbf1A", [128, 8, DH], BF16), sb1("wbf1B", [128, 8, DH], BF16)], ['wbf1A', 'wbf1B'])
        P.dma(mC[:], maskC.rearrange("d s t -> s d t"), writes=['mC'])
        P.dma(SEL[:], sel_d[:], writes=['SEL'])
        P.dma(gbT[:], gbT_d[:], writes=['gbT'])
        P.dma(cw[:], cw_d[:], writes=['cw'])
        P.dma(cb[:], cb_d[:], writes=['cb'])
        P.dma(mng[:], mng_d[:], writes=['mng'])
        P.op('dve', lambda e: e.memset(onesb[:], 1.0), writes=['onesb'])
        P.op('dve', lambda e: e.memset(zrow[:], 0.0), writes=['zrow'])
        P.op('dve', lambda e: e.tensor_copy(out=identb[:], in_=ident[:]), reads=['ident'], writes=['identb'])
        P.op('dve', lambda e: e.tensor_scalar(out=ngbT[:], in0=gbT[:], scalar1=-1.0, scalar2=None, op0=ALU.mult), reads=['gbT'], writes=['ngbT'])
        P.dma(wst[:, :, 0:16], w_in_odd[:, 10240:10256].rearrange("(kc p) n -> p kc n", p=128), writes=['wst'])
        P.op('pool', lambda e: e.tensor_copy(out=wGb[:], in_=wst[:, :, 0:16]), reads=['wst'], writes=['wGb'])
        LNK = float(np.log(DH ** -0.5))

        def load_w1(c0, ncols, dst, dk):
            v = w_in_odd[:, c0:c0 + ncols].rearrange("(kc p) n -> p kc n", p=128)
            for q0 in range(0, ncols, 256):
                w_ = min(256, ncols - q0)
                P.dma(wst[:, :, 0:w_], v[:, :, q0:q0 + w_], writes=['wst'])
                P.op('act', lambda e: e.copy(out=dst[:, :, q0:q0 + w_], in_=wst[:, :, 0:w_]), reads=['wst'], writes=[dk])

        def ld1(c0):
            return lambda dst, dk: load_w1(c0, DH, dst, dk)

        def gates_seq(T, is_sample):
            NC = T // LC
            pbk = min(512, T)
            for d in range(2):
                pb = 32 * d
                rows = slice(pb, pb + 4)
                for b in range(T // pbk):
                    t0 = b * pbk
                    pz, pk = nps()
                    for kc in range(8):
                        P.op('pe', lambda e: e.matmul(pz[pb:pb + 4, 0:pbk], wGb[:, kc, (2 + d) * 4:(3 + d) * 4], hT[:, kc, t0:t0 + pbk],
                                                      start=(kc == 0), stop=(kc == 7)), reads=['wGb'] + hT_keys(t0, t0 + pbk), writes=[pk])
                    P.op('act', lambda e: e.activation(out=XA[rows, t0:t0 + pbk], in_=pz[pb:pb + 4, 0:pbk], func=AF.Exp,
                                                       bias=ngbT[rows, 2 + d:3 + d], scale=-1.0), reads=[pk, 'ngbT'], writes=['XA'])
                P.op('act', lambda e: e.activation(out=XA[rows, 0:T], in_=XA[rows, 0:T], func=AF.Ln, bias=1.0, scale=1.0),
                     reads=['XA'], writes=['XA'])
                for b in range(T // pbk):
                    bs = slice(b * pbk, (b + 1) * pbk)
                    if d == 0:
                        P.op('dve', lambda e: e.tensor_tensor_scan(out=XB[rows, bs], data0=XA[rows, bs], data1=zrow[rows, 0:pbk],
                                                                   initial=0.0, op0=ALU.add, op1=ALU.add), reads=['XA', 'zrow'], writes=['XB'])
                    else:
                        P.op('dve', lambda e: e.tensor_tensor_scan(out=XB[rows, bs][:, ::-1], data0=XA[rows, bs][:, ::-1],
                                                                   data1=zrow[rows, 0:pbk], initial=0.0, op0=ALU.add, op1=ALU.add),
                             reads=['XA', 'zrow'], writes=['XB'])
                if d == 0:
                    ci_, ce_ = 0, LC - 1
                else:
                    ci_, ce_ = LC - 1, 0
                B3 = XB[rows, 0:T].rearrange("p (c l) -> p c l", l=LC)
                A3 = XA[rows, 0:T].rearrange("p (c l) -> p c l", l=LC)
                S = {k_: v_[rows, :] for k_, v_ in sm.items()}
                P.op('dve', lambda e: e.tensor_tensor(out=S['gref'][:, 0:NC], in0=B3[:, :, ci_], in1=A3[:, :, ci_], op=ALU.subtract),
                     reads=['XA', 'XB'], writes=['sm_gref'])
                P.op('dve', lambda e: e.tensor_tensor(out=B3, in0=B3, in1=S['gref'][:, 0:NC].unsqueeze(2).to_broadcast([4, NC, LC]),
                                                      op=ALU.subtract), reads=['XB', 'sm_gref'], writes=['XB'])
                for b in range(T // pbk):
                    t0 = b * pbk
                    pz, pk = nps()
                    for kc in range(8):
                        P.op('pe', lambda e: e.matmul(pz[pb:pb + 4, 0:pbk], wGb[:, kc, d * 4:(d + 1) * 4], hT[:, kc, t0:t0 + pbk],
                                                      start=(kc == 0), stop=(kc == 7)), reads=['wGb'] + hT_keys(t0, t0 + pbk), writes=[pk])
                    P.op('dve', lambda e: e.scalar_tensor_tensor(out=XA[rows, t0:t0 + pbk], in0=pz[pb:pb + 4, 0:pbk], scalar=gbT[rows, d:d + 1],
                                                                 in1=XB[rows, t0:t0 + pbk], op0=ALU.add, op1=ALU.add),
                         reads=[pk, 'gbT', 'XB', 'XA'], writes=['XA'])
                P.op('dve', lambda e: e.tensor_reduce(out=S['ac'][:, 0:NC], in_=A3, axis=AX.X, op=ALU.max), reads=['XA'], writes=['sm_ac'])
                P.op('dve', lambda e: e.tensor_scalar(out=S['bl'][:, 0:NC], in0=B3[:, :, ce_], scalar1=-1.0, scalar2=None, op0=ALU.mult),
                     reads=['XB'], writes=['sm_bl'])
                if is_sample:
                    P.dma(S['m0'][:, 0:1], s_m[d, :].rearrange("(h o) -> h o", o=1), writes=['sm_m0'])
                else:
                    P.op('dve', lambda e: e.memset(S['m0'][:, 0:1], 0.0), writes=['sm_m0'])
                if d == 0:
                    P.op('dve', lambda e: e.tensor_tensor_scan(out=S['M'][:, 0:NC], data0=S['ac'][:, 0:NC], data1=S['bl'][:, 0:NC],
                                                               initial=S['m0'][:, 0:1], op0=ALU.max, op1=ALU.add),
                         reads=['sm_ac', 'sm_bl', 'sm_m0'], writes=['sm_M'])
                    P.op('dve', lambda e: e.tensor_copy(out=S['MP'][:, 0:1], in_=S['m0'][:, 0:1]), reads=['sm_m0'], writes=['sm_MP'])
                    if NC > 1:
                        P.op('dve', lambda e: e.tensor_copy(out=S['MP'][:, 1:NC], in_=S['M'][:, 0:NC - 1]), reads=['sm_M', 'sm_MP'], writes=['sm_MP'])
                else:
                    P.op('dve', lambda e: e.tensor_tensor_scan(out=S['M'][:, 0:NC][:, ::-1], data0=S['ac'][:, 0:NC][:, ::-1],
                                                               data1=S['bl'][:, 0:NC][:, ::-1], initial=S['m0'][:, 0:1],
                                                               op0=ALU.max, op1=ALU.add),
                         reads=['sm_ac', 'sm_bl', 'sm_m0'], writes=['sm_M'])
                    P.op('dve', lambda e: e.tensor_copy(out=S['MP'][:, NC - 1:NC], in_=S['m0'][:, 0:1]), reads=['sm_m0'], writes=['sm_MP'])
                    if NC > 1:
                        P.op('dve', lambda e: e.tensor_copy(out=S['MP'][:, 0:NC - 1], in_=S['M'][:, 1:NC]), reads=['sm_M', 'sm_MP'], writes=['sm_MP'])
                P.op('dve', lambda e: e.tensor_tensor(out=S['mu'][:, 0:NC], in0=S['MP'][:, 0:NC], in1=S['ac'][:, 0:NC], op=ALU.max),
                     reads=['sm_MP', 'sm_ac'], writes=['sm_mu'])
                P.op('dve', lambda e: e.tensor_tensor(out=S['al'][:, 0:NC], in0=S['MP'][:, 0:NC], in1=S['mu'][:, 0:NC], op=ALU.subtract),
                     reads=['sm_MP', 'sm_mu'], writes=['sm_al'])
                P.op('act', lambda e: e.activation(out=S['al'][:, 0:NC], in_=S['al'][:, 0:NC], func=AF.Exp), reads=['sm_al'], writes=['sm_al'])
                mub = S['mu'][:, 0:NC].unsqueeze(2).to_broadcast([4, NC, LC])
                P.op('dve', lambda e: e.tensor_tensor(out=A3, in0=A3, in1=mub, op=ALU.subtract), reads=['XA', 'sm_mu'], writes=['XA'])
                P.op('dve', lambda e: e.tensor_tensor(out=B3, in0=B3, in1=mub, op=ALU.subtract), reads=['XB', 'sm_mu'], writes=['XB'])
                P.op('dve', lambda e: e.tensor_scalar(out=XA[rows, 0:T], in0=XA[rows, 0:T], scalar1=LNK, scalar2=None, op0=ALU.add),
                     reads=['XA'], writes=['XA'])
                P.op('act', lambda e: e.activation(out=XA[rows, 0:T], in_=XA[rows, 0:T], func=AF.Exp), reads=['XA'], writes=['XA'])
                P.op('act', lambda e: e.activation(out=XB[rows, 0:T], in_=XB[rows, 0:T], func=AF.Exp), reads=['XB'], writes=['XB'])
                pz, pk = nps()
                for c in range(NC):
                    P.op('pe', lambda e: e.transpose(out=pz[:, c * 8:c * 8 + 4], in_=XA[rows, c * LC:(c + 1) * LC],
                                                     identity=ident[rows, pb:pb + 4]), reads=['XA', 'ident'], writes=[pk])
                    P.op('pe', lambda e: e.transpose(out=pz[:, c * 8 + 4:c * 8 + 8], in_=XB[rows, c * LC:(c + 1) * LC],
                                                     identity=ident[rows, pb:pb + 4]), reads=['XB', 'ident'], writes=[pk])
                P.op('dve', lambda e: e.tensor_copy(out=Wtok[:, d, 0:NC, :], in_=pz[:, 0:NC * 8].rearrange("p (c k) -> p c k", k=8)),
                     reads=[pk], writes=['Wtok'])
                P.op('dve', lambda e: e.tensor_copy(out=Wtokb[:, d, 0:NC, :], in_=Wtok[:, d, 0:NC, 0:4]), reads=['Wtok'], writes=['Wtokb'])
                pz, pk = nps()
                for hd in range(4):
                    P.op('pe', lambda e: e.matmul(pz[:, hd * 16:hd * 16 + NC], SEL[rows, hd, :], S['al'][:, 0:NC], start=True, stop=True),
                         reads=['SEL', 'sm_al'], writes=[pk])
                P.op('dve', lambda e: e.tensor_copy(out=ALb[:, d, :, 0:NC], in_=pz[:, 0:64].rearrange("p (h c) -> p h c", c=16)[:, :, 0:NC]),
                     reads=[pk], writes=['ALb'])

        def conv_tile(dst, dkey, slot_j, widx, t0src, T, is_sample):
            X = FT[1][:, 0:T]
            A = FT[0][:, 0:T]
            if is_sample:
                R_, Cw = T // 64, 64
                taps = [(dr, dc) for dr in (-1, 0, 1) for dc in (-1, 0, 1)]
            else:
                R_, Cw = 1, T
                taps = [(0, dc) for dc in (-1, 0, 1)]
            X3 = X.rearrange("p (r c) -> p r c", c=Cw)
            A3 = A.rearrange("p (r c) -> p r c", c=Cw)
            P.op('dve', lambda e: e.tensor_scalar(out=A, in0=X, scalar1=cw[:, widx, 4:5], scalar2=None, op0=ALU.mult),
                 reads=['FT1', 'cw'], writes=['FT0'])
            for (dr, dc) in taps:
                if dr == 0 and dc == 0:
                    continue
                r0, r1 = max(0, -dr), R_ - max(0, dr)
                c0, c1 = max(0, -dc), Cw - max(0, dc)
                ti = (dr + 1) * 3 + (dc + 1)
                P.op('dve', lambda e: e.scalar_tensor_tensor(out=A3[:, r0:r1, c0:c1], in0=X3[:, r0 + dr:r1 + dr, c0 + dc:c1 + dc],
                                                             scalar=cw[:, widx, ti:ti + 1], in1=A3[:, r0:r1, c0:c1],
                                                             op0=ALU.mult, op1=ALU.add), reads=['FT1', 'FT0', 'cw'], writes=['FT0'])
            P.op('act', lambda e: e.activation(out=dst[:, slot_j, 0:T], in_=A, func=AF.Silu, bias=cb[:, widx:widx + 1], scale=1.0),
                 reads=['FT0', 'cb'], writes=[dkey])

        hacc = [FT[2 + i][:, 0:TS].rearrange("p (j e) -> p j e", e=DH) for i in range(4)]

        def mlstm_head(hd, off, T, is_sample, pidx):
            NC = T // LC
            NTt = T // 128
            pbk = min(512, T)
            for qk in range(2):
                wb1.use(('qk', off, hd, qk), ld1(qk * 2048 + hd * DH))
                if qk == 0:
                    wb1.prefetch(('qk', off, hd, 1), ld1(2048 + hd * DH))
                else:
                    wb1.prefetch(('v', off, hd), ld1(4096 + hd * DH))
                for j in range(4):
                    for b in range(T // pbk):
                        t0 = b * pbk
                        pz, pk = nps()
                        for kc in range(8):
                            P.op('pe', lambda e: e.matmul(pz[:, 0:pbk], wb1.tile[:, kc, j * 128:(j + 1) * 128], hT[:, kc, t0:t0 + pbk],
                                                          start=(kc == 0), stop=(kc == 7)), reads=[wb1.key] + hT_keys(t0, t0 + pbk), writes=[pk])
                        P.op('act', lambda e: e.copy(out=FT[1][:, t0:t0 + pbk], in_=pz[:, 0:pbk]), reads=[pk], writes=['FT1'])
                    widx = (qk * 4 + hd) * 4 + j
                    if qk == 0:
                        conv_tile(qT, ('yT', 4 + j), j, widx, 0, T, is_sample)
                    else:
                        conv_tile(kT, 'kT', j, widx, 0, T, is_sample)
            wb1.use(('v', off, hd), ld1(4096 + hd * DH))
            wb1.prefetch(('o', off, hd), ld1(6144 + hd * DH))
            qkeys = [('yT', 4 + j) for j in range(4)]
            for d in range(2):
                rev = (d == 1)
                if is_sample:
                    P.dma(Cst[:], s_C[d, hd].rearrange("(j p) e -> p j e", p=128), writes=[('Cst', j_) for j_ in range(4)])
                    P.dma(nst[:, 0:4], s_n[d, hd].rearrange("(j p) -> p j", p=128), writes=['nst'], allow_slow_non_contiguous=True)
                else:
                    P.op('pool', lambda e: e.memset(Cst[:], 0.0), writes=[('Cst', j_) for j_ in range(4)])
                    P.op('pool', lambda e: e.memset(nst[:, 0:4], 0.0), writes=['nst'])
                chunks = list(range(NC))
                if rev:
                    chunks = chunks[::-1]
                for c in chunks:
                    cs = slice(c * LC, (c + 1) * LC)
                    wcol = Wtok[:, d, c, hd:hd + 1]
                    thcol = Wtok[:, d, c, 4 + hd:5 + hd]
                    alcol = ALb[:, d, hd, c:c + 1]
                    pv, pvk = nps()
                    for kc in range(8):
                        P.op('pe', lambda e: e.matmul(pv[:, 0:DH], hT[:, kc, cs], wb1.tile[:, kc, :], start=(kc == 0), stop=(kc == 7)),
                             reads=[wb1.key, ('hT', c)], writes=[pvk])
                    P.op('act', lambda e: e.copy(out=vch[:], in_=pv[:, 0:DH]), reads=[pvk], writes=['vch'])
                    P.op('act', lambda e: e.activation(out=vw[:], in_=vch[:], func=AF.Identity, scale=wcol),
                         reads=['vch', 'Wtok'], writes=['vw'])
                    pt, ptk = nps()
                    ptb = pt[:].bitcast(BF16)
                    for j in range(4):
                        P.op('pe', lambda e: e.transpose(out=ptb[:, j * 128:(j + 1) * 128], in_=kT[:, j, cs], identity=identb[:]),
                             reads=['kT', 'identb'], writes=[ptk])
                    P.op('act', lambda e: e.copy(out=ktok[:], in_=ptb[:, 0:DH]), reads=[ptk], writes=['ktok'])
                    for j in range(4):
                        P.op('act', lambda e: e.activation(out=Cbf[:, j, :], in_=Cst[:, j, :], func=AF.Identity, scale=alcol),
                             reads=[('Cst', j), 'ALb'], writes=[('Cbf', j)])
                    P.op('dve', lambda e: e.tensor_scalar(out=nbf[:], in0=nst[:, 0:4], scalar1=alcol, scalar2=None, op0=ALU.mult),
                         reads=['nst', 'ALb'], writes=['nbf'])
                    ps_, psk = nps()
                    for j in range(4):
                        P.op('pe', lambda e: e.matmul(ps_[:, 0:128], kT[:, j, cs], qT[:, j, cs], start=(j == 0), stop=(j == 3)),
                             reads=['kT'] + qkeys, writes=[psk])
                    P.op('dve', lambda e: e.scalar_tensor_tensor(out=sTs[:], in0=ps_[:, 0:128], scalar=wcol, in1=mC[:, d, :],
                                                                 op0=ALU.mult, op1=ALU.mult), reads=[psk, 'Wtok', 'mC'], writes=['sTs'])
                    pcs = []
                    for j in range(4):
                        pc, pck = nps()
                        P.op('pe', lambda e: e.matmul(pc[:, 0:DH], ktok[:, j * 128:(j + 1) * 128], vw[:], start=True, stop=True),
                             reads=['ktok', 'vw'], writes=[pck])
                        pcs.append((pc, pck))
                    pq_, pqk = nps()
                    for j in range(4):
                        P.op('pe', lambda e: e.matmul(pq_[:, j:j + 1], ktok[:, j * 128:(j + 1) * 128], Wtokb[:, d, c, hd:hd + 1],
                                                      start=True, stop=True), reads=['ktok', 'Wtokb'], writes=[pqk])
                    for j in range(4):
                        pc, pck = pcs[j]
                        P.op('dve', lambda e: e.scalar_tensor_tensor(out=Cst[:, j, :], in0=Cst[:, j, :], scalar=alcol, in1=pc[:, 0:DH],
                                                                     op0=ALU.mult, op1=ALU.add),
                             reads=[pck, ('Cst', j), 'ALb'], writes=[('Cst', j)])
                    P.op('dve', lambda e: e.scalar_tensor_tensor(out=nst[:, 0:4], in0=nst[:, 0:4], scalar=alcol, in1=pq_[:, 0:4],
                                                                 op0=ALU.mult, op1=ALU.add), reads=[pqk, 'nst', 'ALb'], writes=['nst'])
                    pn, pnk = nps()
                    for j in range(4):
                        P.op('pe', lambda e: e.matmul(pn[:, 0:DH], qT[:, j, cs], Cbf[:, j, :], start=(j == 0), stop=False),
                             reads=qkeys + [('Cbf', j)], writes=[pnk])
                    P.op('pe', lambda e: e.matmul(pn[:, 0:DH], sTs[:], vch[:], start=False, stop=True),
                         reads=['sTs', 'vch'], writes=[pnk])
                    pd_, pdk = nps()
                    for j in range(4):
                        P.op('pe', lambda e: e.matmul(pd_[:, 0:1], qT[:, j, cs], nbf[:, j:j + 1], start=(j == 0), stop=False),
                             reads=qkeys + ['nbf'], writes=[pdk])
                    P.op('pe', lambda e: e.matmul(pd_[:, 0:1], sTs[:], onesb[:], start=False, stop=True), reads=['sTs', 'onesb'], writes=[pdk])
                    P.op('act', lambda e: e.activation(out=dstat[:, 2:3], in_=pd_[:, 0:1], func=AF.Abs), reads=[pdk], writes=['dstat'])
                    P.op('dve', lambda e: e.tensor_tensor(out=dstat[:, 0:1], in0=dstat[:, 2:3], in1=thcol, op=ALU.max),
                         reads=['dstat', 'Wtok'], writes=['dstat'])
                    P.op('dve', lambda e: e.reciprocal(out=dstat[:, 1:2], in_=dstat[:, 0:1]), reads=['dstat'], writes=['dstat'])
                    hdst = hacc[c // 4][:, c % 4, :]
                    if d == 0:
                        P.op('act', lambda e: e.activation(out=hdst, in_=pn[:, 0:DH], func=AF.Identity, scale=dstat[:, 1:2]),
                             reads=[pnk, 'dstat'], writes=[('hacc', c)])
                    else:
                        P.op('dve', lambda e: e.scalar_tensor_tensor(out=hdst, in0=pn[:, 0:DH], scalar=dstat[:, 1:2], in1=hdst,
                                                                     op0=ALU.mult, op1=ALU.add), reads=[pnk, 'dstat', ('hacc', c)], writes=[('hacc', c)])
                if not is_sample:
                    P.dma(o_C[pidx, d, hd].rearrange("(j p) e -> p j e", p=128), Cst[:], reads=[('Cst', j_) for j_ in range(4)], q='pool')
                    P.dma(o_n[pidx, d, hd].rearrange("(j p) -> p j", p=128), nst[:, 0:4], reads=['nst'], q='pool', allow_slow_non_contiguous=True)
            wb1.use(('o', off, hd), ld1(6144 + hd * DH))
            wb1.prefetch(('z', off, hd), ld1(8192 + hd * DH))
            for tt in range(NTt):
                hdst = hacc[tt // 4][:, tt % 4, :]
                pz, pk = nps()
                for kc in range(8):
                    P.op('pe', lambda e: e.matmul(pz[:, 0:DH], hT[:, kc, tt * 128:(tt + 1) * 128], wb1.tile[:, kc, :],
                                                  start=(kc == 0), stop=(kc == 7)), reads=[wb1.key, ('hT', tt)], writes=[pk])
                P.op('act', lambda e: e.activation(out=FT[0][:, 0:DH], in_=pz[:, 0:DH], func=AF.Sigmoid), reads=[pk], writes=['FT0'])
                P.op('dve', lambda e: e.tensor_tensor(out=hdst, in0=hdst, in1=FT[0][:, 0:DH], op=ALU.mult),
                     reads=['FT0', ('hacc', tt)], writes=[('hacc', tt)])
                P.op('act', lambda e: e.activation(out=FT[0][:, 0:DH], in_=hdst, func=AF.Square, accum_out=dstat[:, 4:5]),
                     reads=[('hacc', tt), 'FT0'], writes=['FT0', 'dstat'])
                P.op('dve', lambda e: e.tensor_scalar(out=dstat[:, 5:6], in0=dstat[:, 4:5], scalar1=1.0 / DH, scalar2=1e-6,
                                                      op0=ALU.mult, op1=ALU.add), reads=['dstat'], writes=['dstat'])
                P.op('act', lambda e: e.activation(out=dstat[:, 6:7], in_=dstat[:, 5:6], func=AF.Sqrt), reads=['dstat'], writes=['dstat'])
                P.op('dve', lambda e: e.reciprocal(out=dstat[:, 7:8], in_=dstat[:, 6:7]), reads=['dstat'], writes=['dstat'])
                P.op('dve', lambda e: e.tensor_scalar(out=hdst, in0=hdst, scalar1=dstat[:, 7:8], scalar2=None, op0=ALU.mult),
                     reads=[('hacc', tt), 'dstat'], writes=[('hacc', tt)])
            wb1.use(('z', off, hd), ld1(8192 + hd * DH))
            if hd < 3:
                wb1.prefetch(('qk', off, hd + 1, 0), ld1((hd + 1) * DH))
            for tt in range(NTt):
                hdst = hacc[tt // 4][:, tt % 4, :]
                pz, pk = nps()
                for kc in range(8):
                    P.op('pe', lambda e: e.matmul(pz[:, 0:DH], hT[:, kc, tt * 128:(tt + 1) * 128], wb1.tile[:, kc, :],
                                                  start=(kc == 0), stop=(kc == 7)), reads=[wb1.key, ('hT', tt)], writes=[pk])
                P.op('act', lambda e: e.activation(out=FT[0][:, 0:DH], in_=pz[:, 0:DH], func=AF.Silu), reads=[pk], writes=['FT0'])
                P.op('dve', lambda e: e.tensor_tensor(out=hdst, in0=hdst, in1=FT[0][:, 0:DH], op=ALU.mult),
                     reads=['FT0', ('hacc', tt)], writes=[('hacc', tt)])
                pz, pk = nps()
                for j in range(4):
                    P.op('pe', lambda e: e.transpose(out=pz[:, j * 128:(j + 1) * 128], in_=hdst[:, j * 128:(j + 1) * 128], identity=ident[:]),
                         reads=[('hacc', tt), 'ident'], writes=[pk])
                for j in range(4):
                    P.op('act', lambda e: e.activation(out=yT[:, j, tt * 128:(tt + 1) * 128], in_=pz[:, j * 128:(j + 1) * 128],
                                                       func=AF.Identity, scale=mng[:, hd * 4 + j:hd * 4 + j + 1]),
                         reads=[pk, 'mng'], writes=[('yT', j)])

        for si in seq_ids:
            off, T, cidx, is_sample = SEQS[si]
            P.barrier()
            make_gate(1, cidx)
            make_hT(1, x1, 'x1', off, T, cidx)
            P.barrier()
            gates_seq(T, is_sample)
            if not is_sample:
                for d in range(2):
                    lastc = (T // LC - 1) if d == 0 else 0
                    P.dma(o_m[si - 1, d, :].rearrange("(h o) -> h o", o=1), sm['M'][32 * d:32 * d + 4, lastc:lastc + 1],
                          reads=['sm_M'], q='pool')
            for hd in range(4):
                P.barrier()
                mlstm_head(hd, off, T, is_sample, si - 1)
                if debug and debug.get('dump_y'):
                    for j in range(4):
                        dump("yM%d_%d" % (hd, j), yT[:, j, 0:T], [('yT', j)], T, col0=off)
                P.barrier()
                load_wo(w_out_odd[hd * DH:(hd + 1) * DH, :], 128, nk=4)
                last = (hd == 3)
                outproj(1, [(128, s_) for s_ in range(4)], x1, 'x1', (yout if last else x1), ('yout' if last else 'x1'),
                        off, T, cidx, final=last)
        P.barrier()
        L1.close()
    P.finish()
    sems = {s: es.enter_context(nc.semaphore(s)) for s in P.sem_names}
    P.emit(sems)
    es.close()
    global _last_dslot
    _last_dslot = dslot if debug else {}
    return nc, P


def host_inputs(inp, core):
    f = lambda a: np.ascontiguousarray(a, dtype=np.float32)
    b = core % 2
    m = {}
    m["xin"] = f(np.concatenate([inp["x_sample"][b], inp["x_prompt"][2 * core], inp["x_prompt"][2 * core + 1]], axis=0))
    cond = np.stack([inp["c"][b], inp["c_ctx"]], axis=0)
    m["condT"] = f(cond.reshape(2, 8, 128).transpose(2, 1, 0))
    m["s_hgrn"] = f(inp["state_hgrn"][b, 0])
    m["s_rwkv"] = f(inp["state_rwkv"][b, 0])
    m["s_C"] = f(inp["state_mlstm_C"][b, 0])
    m["s_n"] = f(inp["state_mlstm_n"][b, 0])
    m["s_m"] = f(inp["state_mlstm_m"][b, 0])
    m["w_mod"] = f(inp["w_mod"])
    m["b_modT"] = f(inp["b_mod"].reshape(2, 24, 128).transpose(2, 0, 1))
    m["norm_gT"] = f(inp["norm_g"].reshape(2, 8, 128).transpose(2, 0, 1))
    m["fnorm_gT"] = f(inp["final_norm_g"].reshape(8, 128).T)
    w = inp["w_in_even"][0]
    DA = 1024
    wA = np.stack([np.concatenate([w[:, g * DA + h * 128: g * DA + (h + 1) * 128] for g in (0, 1, 4, 2, 3)], axis=1)
                   for h in range(8)], axis=0)
    m["wA"] = f(wA)
    o = 5 * DA
    zb0 = o + 3328
    wB = np.stack([np.concatenate([w[:, o + g * 1024 + h * 64: o + g * 1024 + (h + 1) * 64] for g in (0, 1, 2)]
                                  + [w[:, zb0 + h * 64: zb0 + (h + 1) * 64]], axis=1) for h in range(16)], axis=0)
    m["wB"] = f(wB)
    m["wLR"] = f(w[:, o + 3072: o + 3328])
    m["w_out_even"] = f(inp["w_out_even"][0])
    m["lbT"] = f(inp["hgrn_lb_logits"].reshape(2, 8, 128).transpose(2, 0, 1))
    m["hg_gT"] = f(inp["hgrn_norm_g"][0].reshape(8, 128).T)
    mu = inp["rwkv_shift_mu"][0]
    mr = np.zeros((64, 2, 4, 16), np.float32)
    for g in range(3):
        mr[:, :, g, :] = mu[:, g * 1024:(g + 1) * 1024].reshape(2, 16, 64).transpose(2, 0, 1)
    m["mu_rkv"] = mr
    m["mu_lr"] = f(mu[:, 3072:3328].reshape(2, 4, 64).transpose(2, 0, 1))
    m["w0T"] = f(inp["rwkv_w0"][0].reshape(2, 16, 64).transpose(2, 0, 1))
    m["a0T"] = f(inp["rwkv_a0"][0].reshape(2, 16, 64).transpose(2, 0, 1))
    m["w2"] = f(inp["rwkv_w2"][0])
    m["a2"] = f(inp["rwkv_a2"][0])
    m["kkT"] = f(inp["rwkv_k_k"][0].reshape(16, 64).T)
    m["kaT"] = f(inp["rwkv_k_a"][0].reshape(16, 64).T)
    m["rkT"] = f(inp["rwkv_r_k"][0].T)
    m["gngT"] = f(inp["rwkv_gn_g"][0].reshape(16, 64).T)
    m["gnbT"] = f(inp["rwkv_gn_b"][0].reshape(16, 64).T)
    s = np.arange(128)[:, None]
    t = np.arange(128)[None, :]
    same = (s // 32) == (t // 32)
    m["maskH"] = np.stack([(same & (s <= t)), (same & (s >= t))]).astype(np.float32)
    m["ident_in"] = np.eye(128, dtype=np.float32)
    s6 = np.arange(64)[:, None]
    t6 = np.arange(64)[None, :]
    mr_ = np.zeros((2, 3, 64, 128), np.float32)
    for d_, (st_, inc_) in enumerate([((s6 < t6), (s6 <= t6)), ((s6 > t6), (s6 >= t6))]):
        st_ = st_.astype(np.float32)
        inc_ = inc_.astype(np.float32)
        mr_[d_, 0, :, 0:64] = -st_
        mr_[d_, 0, :, 64:128] = -inc_
        mr_[d_, 1, :, 0:64] = st_
        mr_[d_, 1, :, 64:128] = inc_
        mr_[d_, 2, :, 0:64] = -(st_.T)
    m["maskR"] = mr_
    m["w_in_odd"] = f(inp["w_in_odd"][0])
    m["w_out_odd"] = f(inp["w_out_odd"][0])
    m["maskC"] = np.stack([(s <= t), (s >= t)]).astype(np.float32)
    sel = np.zeros((36, 4, 128), np.float32)
    gbt = np.zeros((36, 4), np.float32)
    for pb_ in (0, 32):
        for k_ in range(4):
            sel[pb_ + k_, k_, :] = 1.0
        gbt[pb_:pb_ + 4, :] = inp["mlstm_gate_b"][0].T
    m["sel_d"] = sel
    m["gbT_d"] = gbt
    m["cw_d"] = f(inp["mlstm_conv_w"][0].reshape(9, 32, 128).transpose(2, 1, 0))
    m["cb_d"] = f(inp["mlstm_conv_b"][0].reshape(32, 128).T)
    m["mng_d"] = f(inp["mlstm_norm_g"][0].reshape(16, 128).T)
    return m


def kernel(**inp):
    inp = {k: np.asarray(v) for k, v in inp.items()}
    nc, P = build()
    in_maps = [host_inputs(inp, c) for c in range(NCORES)]
    res = run_bass_kernel_spmd(nc, in_maps, core_ids=list(range(NCORES)))
    r = res.results
    y_prompt = np.zeros((16, TP, D), np.float32)
    y_sample = np.zeros((2, TS, D), np.float32)
    for c in range(NCORES):
        y_prompt[2 * c] = r[c]["yout"][TS:TS + TP]
        y_prompt[2 * c + 1] = r[c]["yout"][TS + TP:]
    for b in range(2):
        y_sample[b] = r[b]["yout"][0:TS]
    new_hgrn = np.concatenate([r[c]["o_hgrn"] for c in range(NCORES)], axis=0)[:, None]
    new_rwkv = np.concatenate([r[c]["o_rwkv"] for c in range(NCORES)], axis=0)[:, None]
    new_C = np.concatenate([r[c]["o_C"] for c in range(NCORES)], axis=0)[:, None]
    new_n = np.concatenate([r[c]["o_n"] for c in range(NCORES)], axis=0)[:, None]
    new_m = np.concatenate([r[c]["o_m"] for c in range(NCORES)], axis=0)[:, None]
    return (y_prompt, y_sample, new_hgrn.astype(np.float32), new_rwkv.astype(np.float32),
            new_C.astype(np.float32), new_n.astype(np.float32), new_m.astype(np.float32))
```
